# Optimizing a Trainium2 kernel written in Bass

```python
import jax, jax.numpy as jnp
from jax import lax
import numpy as np

D_MODEL = 1024
BATCH = 8
SEQ = 2048
DEPTH = 1
DEC_BATCH = 128
DEC_SEQ = 1
PAST_LEN = 16384
PAGE_SIZE = 128

MIX_WIDTH = D_MODEL
GDN_HEADS = 4
GDN_DV = (MIX_WIDTH // 2) // GDN_HEADS
GDN_DK = GDN_DV
GDN_WIDTH = GDN_HEADS * GDN_DV
ML_HEADS = 4
ML_WIDTH = MIX_WIDTH - GDN_WIDTH
ML_DV = ML_WIDTH // ML_HEADS
ML_DK = ML_DV // 2
CONV_W = 4
CONV_CH = 2 * GDN_HEADS * GDN_DK + GDN_WIDTH
CHUNK = 64
D_FF = 4 * D_MODEL
EPS = 1e-6
IN_SPLITS = (GDN_HEADS * GDN_DK, GDN_HEADS * GDN_DK, GDN_WIDTH, GDN_WIDTH, GDN_HEADS, GDN_HEADS,
             ML_HEADS * ML_DK, ML_HEADS * ML_DK, ML_WIDTH, ML_WIDTH, ML_HEADS, ML_HEADS)
IN_COLS = sum(IN_SPLITS)

kernel_name = 'hybrid_gdn_mlstm_decoder_step'


def _rmsnorm(x, g):
    xf = x.astype(jnp.float32)
    y = xf * lax.rsqrt(jnp.mean(xf * xf, axis=-1, keepdims=True) + EPS)
    return (y * g.astype(jnp.float32)).astype(x.dtype)


def _l2norm(x):
    return x * lax.rsqrt(jnp.sum(x * x, axis=-1, keepdims=True) + EPS)


def _split(x, sizes):
    offs = np.cumsum(sizes)[:-1].tolist()
    return jnp.split(x, offs, axis=-1)


def _causal_conv(x, buf, w):
    T = x.shape[1]
    xp = jnp.concatenate([buf, x], axis=1)
    y = xp[:, 0:T] * w[0]
    for j in range(1, CONV_W):
        y = y + xp[:, j:j + T] * w[j]
    return jax.nn.silu(y), xp[:, -(CONV_W - 1):]


def _to_chunks(x, chunk):
    B, T = x.shape[:2]
    x = x.reshape((B, T // chunk, chunk) + x.shape[2:])
    x = jnp.moveaxis(x, 2, 3)
    return jnp.moveaxis(x, 1, 0)


def _from_chunks(o):
    N, B, H, C, E = o.shape
    return o.transpose(1, 0, 3, 2, 4).reshape(B, N * C, H, E)


def _gated_delta_rule(q, k, v, g, beta, S0, chunk):
    qc, kc, vc, gc, bc = (_to_chunks(t, chunk) for t in (q, k, v, g, beta))
    G = jnp.cumsum(gc, axis=-1)
    causal = jnp.tril(jnp.ones((chunk, chunk), dtype=bool))
    strict = jnp.tril(jnp.ones((chunk, chunk), dtype=bool), -1)
    decay = jnp.exp(jnp.where(causal, G[..., :, None] - G[..., None, :], -jnp.inf))
    kk = jnp.einsum('nbhcd,nbhsd->nbhcs', kc, kc)
    A = jnp.where(strict, bc[..., None] * kk * decay, 0.0) + jnp.eye(chunk, dtype=jnp.float32)
    w_v = lax.linalg.triangular_solve(A, bc[..., None] * vc, left_side=True, lower=True, unit_diagonal=True)
    w_k = lax.linalg.triangular_solve(A, (bc * jnp.exp(G))[..., None] * kc, left_side=True, lower=True,
                                      unit_diagonal=True)
    qk = jnp.einsum('nbhcd,nbhsd->nbhcs', qc, kc) * decay
    k_end = kc * jnp.exp(G[..., -1:] - G)[..., None]
    g_end = jnp.exp(G[..., -1])
    eG = jnp.exp(G)

    def step(S, xs):
        q_, wv_, wk_, qk_, ke_, ge_, eg_ = xs
        U = wv_ - jnp.einsum('bhcd,bhde->bhce', wk_, S)
        o = eg_[..., None] * jnp.einsum('bhcd,bhde->bhce', q_, S) + jnp.einsum('bhcs,bhse->bhce', qk_, U)
        S = ge_[..., None, None] * S + jnp.einsum('bhcd,bhce->bhde', ke_, U)
        return S, o

    S, o = lax.scan(step, S0, (qc, w_v, w_k, qk, k_end, g_end, eG))
    return _from_chunks(o), S


def _mlstm(q, k, v, ig, lf, C0, n0, m0, chunk):
    qc, kc, vc, ic, fc = (_to_chunks(t, chunk) for t in (q, k, v, ig, lf))
    F = jnp.cumsum(fc, axis=-1)
    causal = jnp.tril(jnp.ones((chunk, chunk), dtype=bool))
    D = jnp.where(causal, F[..., :, None] - F[..., None, :] + ic[..., None, :], -jnp.inf)
    D_max = jnp.max(D, axis=-1)
    qk = jnp.einsum('nbhcd,nbhsd->nbhcs', qc, kc)

    def step(carry, xs):
        C, n, m = carry
        q_, k_, v_, F_, D_, Dm_, qk_ = xs
        b = F_ + m[..., None]
        mt = jnp.maximum(b, Dm_)
        w_prev = jnp.exp(b - mt)
        P = jnp.exp(D_ - mt[..., None])
        Pqk = P * qk_
        num = w_prev[..., None] * jnp.einsum('bhcd,bhde->bhce', q_, C) + jnp.einsum('bhcs,bhse->bhce', Pqk, v_)
        den = w_prev * jnp.einsum('bhcd,bhd->bhc', q_, n) + jnp.sum(Pqk, axis=-1)
        h = num / jnp.maximum(jnp.abs(den), jnp.exp(-mt))[..., None]
        p_end = P[..., -1, :]
        C = w_prev[..., -1, None, None] * C + jnp.einsum('bhs,bhsd,bhse->bhde', p_end, k_, v_)
        n = w_prev[..., -1, None] * n + jnp.einsum('bhs,bhsd->bhd', p_end, k_)
        return (C, n, mt[..., -1]), h

    (C, n, m), h = lax.scan(step, (C0, n0, m0), (qc, kc, vc, F, D, D_max, qk))
    return _from_chunks(h), C, n, m


def _mixer(h, conv_buf, S0, C0, n0, m0, w_in, conv_w, a_log, dt_bias, gdn_norm_g, b_igate, b_fgate,
           mlstm_norm_g, w_out):
    f32 = jnp.float32
    Bsz, T, _ = h.shape
    chunk = CHUNK if T % CHUNK == 0 else T
    (gq, gk, gv, gz, gb, ga, mq, mk, mv, mo, mi, mf) = _split(h @ w_in, IN_SPLITS)
    qkv, conv_new = _causal_conv(jnp.concatenate([gq, gk, gv], axis=-1), conv_buf.astype(h.dtype), conv_w)
    gq, gk, gv = _split(qkv.astype(f32), (GDN_HEADS * GDN_DK, GDN_HEADS * GDN_DK, GDN_WIDTH))
    gq = _l2norm(gq.reshape(Bsz, T, GDN_HEADS, GDN_DK)) * (GDN_DK ** -0.5)
    gk = _l2norm(gk.reshape(Bsz, T, GDN_HEADS, GDN_DK))
    gv = gv.reshape(Bsz, T, GDN_HEADS, GDN_DV)
    beta = jax.nn.sigmoid(gb.astype(f32))
    g = -jnp.exp(a_log.astype(f32)) * jax.nn.softplus(ga.astype(f32) + dt_bias.astype(f32))
    o_gdn, S_new = _gated_delta_rule(gq, gk, gv, g, beta, S0.astype(f32), chunk)
    o_gdn = _rmsnorm(o_gdn, gdn_norm_g) * jax.nn.silu(gz.astype(f32).reshape(Bsz, T, GDN_HEADS, GDN_DV))
    mq = mq.astype(f32).reshape(Bsz, T, ML_HEADS, ML_DK)
    mk = mk.astype(f32).reshape(Bsz, T, ML_HEADS, ML_DK) * (ML_DK ** -0.5)
    mv = mv.astype(f32).reshape(Bsz, T, ML_HEADS, ML_DV)
    ig = mi.astype(f32) + b_igate.astype(f32)
    lf = jax.nn.log_sigmoid(mf.astype(f32) + b_fgate.astype(f32))
    h_ml, C_new, n_new, m_new = _mlstm(mq, mk, mv, ig, lf, C0.astype(f32), n0.astype(f32), m0.astype(f32), chunk)
    h_ml = jax.nn.sigmoid(mo.astype(f32)).reshape(Bsz, T, ML_HEADS, ML_DV) * _rmsnorm(h_ml, mlstm_norm_g)
    o = jnp.concatenate([o_gdn.reshape(Bsz, T, GDN_WIDTH), h_ml.reshape(Bsz, T, ML_WIDTH)], axis=-1)
    return o.astype(h.dtype) @ w_out, (conv_new, S_new, C_new, n_new, m_new)


def _layer(x, states, params):
    (norm_pre_mix, w_in, conv_w, a_log, dt_bias, gdn_norm_g, b_igate, b_fgate, mlstm_norm_g, w_out,
     norm_post_mix, norm_pre_mlp, w_up, w_down, norm_post_mlp) = params
    conv_buf, S0, C0, n0, m0 = states
    h = _rmsnorm(x, norm_pre_mix)
    mix, new_states = _mixer(h, conv_buf, S0, C0, n0, m0, w_in, conv_w, a_log, dt_bias, gdn_norm_g,
                             b_igate, b_fgate, mlstm_norm_g, w_out)
    x = x + _rmsnorm(mix, norm_post_mix)
    u = jnp.square(jax.nn.relu(_rmsnorm(x, norm_pre_mlp) @ w_up))
    x = x + _rmsnorm(u @ w_down, norm_post_mlp)
    return x, new_states


def _trunk(x, states, params):
    new = [[] for _ in states]
    for l in range(DEPTH):
        x, st = _layer(x, tuple(s[l] for s in states), tuple(p[l] for p in params))
        for lst, s in zip(new, st):
            lst.append(s)
    return x, tuple(jnp.stack(lst) for lst in new)


def setup_inputs(seed: int = 0) -> dict:
    key = jax.random.key(seed)
    ks = jax.random.split(key, 24)
    nrm = jax.random.normal
    f32 = jnp.float32
    dt = jnp.exp(jax.random.uniform(ks[10], (DEPTH, GDN_HEADS), f32) * (np.log(0.1) - np.log(0.001)) + np.log(0.001))
    return {
        'x_prompt': nrm(ks[0], (BATCH, SEQ, D_MODEL), f32),
        'x_sample': nrm(ks[1], (DEC_BATCH, DEC_SEQ, D_MODEL), f32),
        'state_gdn_conv': nrm(ks[2], (DEPTH, DEC_BATCH, CONV_W - 1, CONV_CH), f32),
        'state_gdn_S': 0.05 * nrm(ks[3], (DEPTH, DEC_BATCH, GDN_HEADS, GDN_DK, GDN_DV), f32),
        'state_mlstm_C': 0.05 * nrm(ks[4], (DEPTH, DEC_BATCH, ML_HEADS, ML_DK, ML_DV), f32),
        'state_mlstm_n': 0.05 * nrm(ks[5], (DEPTH, DEC_BATCH, ML_HEADS, ML_DK), f32),
        'state_mlstm_m': nrm(ks[6], (DEPTH, DEC_BATCH, ML_HEADS), f32),
        'norm_pre_mix': 1.0 + 0.02 * nrm(ks[7], (DEPTH, D_MODEL), f32),
        'w_in': nrm(ks[8], (DEPTH, D_MODEL, IN_COLS), f32) * D_MODEL ** -0.5,
        'conv_w': 0.5 * nrm(ks[9], (DEPTH, CONV_W, CONV_CH), f32),
        'a_log': jnp.log(jax.random.uniform(ks[11], (DEPTH, GDN_HEADS), f32, 1.0, 16.0)),
        'dt_bias': dt + jnp.log(-jnp.expm1(-dt)),
        'gdn_norm_g': 1.0 + 0.02 * nrm(ks[12], (DEPTH, GDN_DV), f32),
        'b_igate': 0.1 * nrm(ks[13], (DEPTH, ML_HEADS), f32),
        'b_fgate': jnp.linspace(3.0, 6.0, ML_HEADS, dtype=f32)[None, :] + 0.1 * nrm(ks[14], (DEPTH, ML_HEADS), f32),
        'mlstm_norm_g': 1.0 + 0.02 * nrm(ks[15], (DEPTH, ML_DV), f32),
        'w_out': nrm(ks[16], (DEPTH, MIX_WIDTH, D_MODEL), f32) * MIX_WIDTH ** -0.5,
        'norm_post_mix': 1.0 + 0.02 * nrm(ks[17], (DEPTH, D_MODEL), f32),
        'norm_pre_mlp': 1.0 + 0.02 * nrm(ks[18], (DEPTH, D_MODEL), f32),
        'w_up': nrm(ks[19], (DEPTH, D_MODEL, D_FF), f32) * D_MODEL ** -0.5,
        'w_down': nrm(ks[20], (DEPTH, D_FF, D_MODEL), f32) * D_FF ** -0.5,
        'norm_post_mlp': 1.0 + 0.02 * nrm(ks[21], (DEPTH, D_MODEL), f32),
    }


def reference(x_prompt, x_sample, state_gdn_conv, state_gdn_S, state_mlstm_C, state_mlstm_n, state_mlstm_m,
              norm_pre_mix, w_in, conv_w, a_log, dt_bias, gdn_norm_g, b_igate, b_fgate, mlstm_norm_g, w_out,
              norm_post_mix, norm_pre_mlp, w_up, w_down, norm_post_mlp):
    params = (norm_pre_mix, w_in, conv_w, a_log, dt_bias, gdn_norm_g, b_igate, b_fgate, mlstm_norm_g, w_out,
              norm_post_mix, norm_pre_mlp, w_up, w_down, norm_post_mlp)
    B = x_prompt.shape[0]
    f32 = jnp.float32
    zero_states = (jnp.zeros((DEPTH, B, CONV_W - 1, CONV_CH), x_prompt.dtype),
                   jnp.zeros((DEPTH, B, GDN_HEADS, GDN_DK, GDN_DV), f32),
                   jnp.zeros((DEPTH, B, ML_HEADS, ML_DK, ML_DV), f32),
                   jnp.zeros((DEPTH, B, ML_HEADS, ML_DK), f32),
                   jnp.zeros((DEPTH, B, ML_HEADS), f32))
    y_prompt, (p_conv, p_S, p_C, p_n, p_m) = _trunk(x_prompt, zero_states, params)
    sample_states = (state_gdn_conv, state_gdn_S, state_mlstm_C, state_mlstm_n, state_mlstm_m)
    y_sample, (s_conv, s_S, s_C, s_n, s_m) = _trunk(x_sample, sample_states, params)
    return (y_prompt, y_sample, p_conv, p_S, p_C, p_n, p_m, s_conv, s_S, s_C, s_n, s_m)
```

```python
from contextlib import ExitStack
import numpy as np
import concourse.bass as bass
import concourse.mybir as mybir
from concourse.bass_utils import run_bass_kernel_spmd

F32 = mybir.dt.float32
BF16 = mybir.dt.bfloat16
ALU = mybir.AluOpType
AF = mybir.ActivationFunctionType
AX = mybir.AxisListType

NCORES = 8
T = 2048
NT = T // 128
D = 1024
DFF = 4096
NS = 16
EPS = 1e-6
NEG = -30000.0
import os
KNT = int(os.environ.get('KNT', NT))
KPH2 = int(os.environ.get('KPH2', 1))
KSTAGE = int(os.environ.get('KSTAGE', 99))
KSUB = int(os.environ.get('KSUB', 99))


class Res:
    __slots__ = ("name", "w", "rd")

    def __init__(self, name):
        self.name = name
        self.w = None
        self.rd = []


class Op:
    __slots__ = ("eng", "fn", "reads", "writes", "dma", "deps", "signal", "cnt", "sem", "waits")

    def __init__(self, eng, fn, reads, writes, dma):
        self.eng = eng
        self.fn = fn
        self.reads = reads
        self.writes = writes
        self.dma = dma
        self.deps = []
        self.signal = False
        self.cnt = 0
        self.sem = None
        self.waits = []


def _res(lst):
    out = []
    for x in lst:
        if x is None:
            continue
        if isinstance(x, Res):
            out.append(x)
        elif isinstance(x, (list, tuple)):
            out.extend(_res(x))
        else:
            out.append(x.r)
    return out


class Prog:
    ENGS = ("pe", "act", "dve", "pool", "sp")

    def __init__(self, nc, n_dma_sems=56):
        self.nc = nc
        self.ops = []
        self.n_dma_sems = n_dma_sems
        self.engobj = {"pe": nc.tensor, "act": nc.scalar, "dve": nc.vector,
                       "pool": nc.gpsimd, "sp": nc.sync}

    def op(self, eng, fn, r=(), w=()):
        self.ops.append(Op(eng, fn, _res(r), _res(w), False))

    def dma(self, eng, fn, r=(), w=()):
        self.ops.append(Op(eng, fn, _res(r), _res(w), True))

    def barrier(self, allres):
        for e in ("pe", "act", "dve", "pool", "sp"):
            self.ops.append(Op(e, (lambda en: en.nop(nofuse=True)), [], _res(allres), False))

    def finalize(self, stack):
        nc = self.nc
        ops = self.ops
        dma_slot_last = [None] * self.n_dma_sems
        dma_i = 0
        for i, o in enumerate(ops):
            deps = set()
            for r in o.reads:
                if r.w is not None:
                    deps.add(r.w)
            for r in o.writes:
                if r.w is not None:
                    deps.add(r.w)
                for j in r.rd:
                    deps.add(j)
            if o.dma:
                slot = dma_i % self.n_dma_sems
                dma_i += 1
                o.sem = slot
                if dma_slot_last[slot] is not None:
                    deps.add(dma_slot_last[slot])
                dma_slot_last[slot] = i
            deps.discard(i)
            o.deps = sorted(deps)
            for r in o.reads:
                r.rd.append(i)
            for r in o.writes:
                r.w = i
                r.rd = []
            for j in o.deps:
                pj = ops[j]
                if pj.dma:
                    continue
                if pj.eng == "pe" and o.eng == "pe" and not o.dma:
                    continue
                pj.signal = True
        cnt = {e: 0 for e in self.ENGS}
        dcnt = [0] * self.n_dma_sems
        for o in ops:
            if o.dma:
                dcnt[o.sem] += 16
                o.cnt = dcnt[o.sem]
            elif o.signal:
                cnt[o.eng] += 1
                o.cnt = cnt[o.eng]
        seen = {e: {} for e in self.ENGS}
        for o in ops:
            need = {}
            for j in o.deps:
                pj = ops[j]
                if pj.dma:
                    key = ("d", pj.sem)
                else:
                    if pj.eng == "pe" and o.eng == "pe" and not o.dma:
                        continue
                    key = ("e", pj.eng)
                if pj.cnt > need.get(key, 0):
                    need[key] = pj.cnt
            s = seen[o.eng]
            for key, v in need.items():
                if s.get(key, 0) >= v:
                    continue
                s[key] = v
                o.waits.append((key, v))
        final_waits = [(("d", k), dcnt[k]) for k in range(self.n_dma_sems) if dcnt[k] > 0]
        final_waits += [(("e", e), cnt[e]) for e in self.ENGS if cnt[e] > 0 and e != "sp"]
        esem = {e: stack.enter_context(nc.semaphore("s_" + e)) for e in self.ENGS}
        dsem = [stack.enter_context(nc.semaphore("d_%d" % k)) for k in range(self.n_dma_sems)]

        def semof(key):
            return dsem[key[1]] if key[0] == "d" else esem[key[1]]

        n_ins = 0
        for o in ops:
            e = self.engobj[o.eng]
            for key, v in o.waits:
                e.wait_ge(semof(key), v)
                n_ins += 1
            ins = o.fn(e)
            n_ins += 1
            if o.dma:
                ins.then_inc(dsem[o.sem], 16)
            elif o.signal:
                ins.then_inc(esem[o.eng], 1)
        sp = self.engobj["sp"]
        for key, v in final_waits:
            if seen["sp"].get(key, 0) >= v:
                continue
            sp.wait_ge(semof(key), v)
        return n_ins


class View:
    __slots__ = ("t", "r")

    def __init__(self, ap, r):
        self.t = ap
        self.r = r

    def __getitem__(self, k):
        return self.t[k]


class Tl:
    __slots__ = ("t", "r")

    def __init__(self, t, name):
        self.t = t
        self.r = Res(name)

    def __getitem__(self, k):
        return self.t[k]


C_IDF, C_TRI, C_ONES, C_SEL127, C_MST, C_MIT, C_MI, C_SELR, C_HALF = (
    0, 128, 256, 384, 512, 1024, 1536, 2048, 2560)
NCONST = 2560


def make_consts():
    c = np.zeros((128, NCONST), np.float32)
    s = np.arange(128)[:, None]
    f = np.arange(128)[None, :]
    c[:, C_IDF:C_IDF + 128] = (s == f)
    c[:, C_TRI:C_TRI + 128] = (s <= f)
    c[:, C_ONES:C_ONES + 128] = 1.0
    c[:, C_SEL127:C_SEL127 + 128] = (s == 127)
    mst = np.where(s < f, 0.0, NEG)
    mit = np.where(s <= f, 0.0, NEG)
    mi = np.where(f <= s, 0.0, NEG)
    for h in range(4):
        c[:, C_MST + h * 128:C_MST + (h + 1) * 128] = mst
        c[:, C_MIT + h * 128:C_MIT + (h + 1) * 128] = mit
        c[:, C_MI + h * 128:C_MI + (h + 1) * 128] = mi
        c[h, C_SELR + h * 128:C_SELR + (h + 1) * 128] = 1.0
    return c


W_QKV, W_MQK, W_GZ, W_MV, W_MO, W_GATE = 0, 1536, 2048, 2560, 3072, 3584
WIN_MOVES = [(0, 0, 1536), (1536, 2056, 512), (2048, 1536, 512), (2560, 2568, 512),
             (3072, 3080, 512), (3584, 2048, 8), (3592, 3596, 4), (3596, 3592, 4)]


def build_program():
    nc = bass.Bass("TRN2", target_bir_lowering=False)
    P = Prog(nc)

    def din(name, shape):
        return nc.dram_tensor(name, list(shape), F32, kind="ExternalInput").ap()

    def dout(name, shape):
        return nc.dram_tensor(name, list(shape), F32, kind="ExternalOutput").ap()

    x_d = din("x", [T, D])
    xs_d = din("xs", [NS, D])
    sconv_d = din("sconv", [NS, 3, 1536])
    sS_d = din("sS", [NS, 4, 128, 128])
    sC_d = din("sC", [NS, 4, 64, 128])
    sn_d = din("sn", [NS, 256])
    sm_d = din("sm", [NS, 4])
    win_d = din("w_in", [D, 3600])
    wout_d = din("w_out", [D, D])
    wup_d = din("w_up", [D, DFF])
    wdn_d = din("w_down", [DFF, D])
    consts_d = din("consts", [128, NCONST])
    gpre_d = din("gpre_fm", [128, 8])
    gpremlp_d = din("gpremlp_fm", [128, 8])
    cw_d = din("cw_fm", [128, 48])
    gpm_d = din("gpostmix", [1, D])
    gpl_d = din("gpostmlp", [1, D])
    small_d = din("small", [1, 16])
    gng_d = din("gdn_norm_g", [1, 128])
    mng_d = din("mlstm_norm_g", [1, 128])

    y_d = dout("y", [T, D])
    ys_d = dout("ys", [NS, D])
    pconv_d = dout("pconv", [3, 1536])
    pS_d = dout("pS", [4, 128, 128])
    pC_d = dout("pC", [4, 64, 128])
    pn_d = dout("pn", [4, 64])
    pm_d = dout("pm", [1, 4])
    oconv_d = dout("oconv", [NS, 3, 1536])
    oS_d = dout("oS", [NS, 4, 128, 128])
    oC_d = dout("oC", [NS, 4, 64, 128])
    on_d = dout("on", [NS, 256])
    om_d = dout("om", [NS, 4])

    def mm(out, lhsT, rhs, start, stop, r, w):
        P.op("pe", lambda e, o=out, l=lhsT, rr=rhs, s=start, t=stop:
             e.matmul(o, lhsT=l, rhs=rr, start=s, stop=t), r, w)

    def tr(out, in_, ident, r, w):
        P.op("pe", lambda e, o=out, i=in_, d=ident: e.transpose(o, i, d), r, w)

    def tt(eng, out, in0, in1, op, r, w):
        P.op(eng, lambda e, o=out, a=in0, b=in1, p=op: e.tensor_tensor(out=o, in0=a, in1=b, op=p), r, w)

    def ts(eng, out, in0, s1, s2, op0, op1, r, w, accum=None):
        if op1 is None:
            P.op(eng, lambda e, o=out, a=in0, x=s1, p0=op0:
                 e.tensor_scalar(out=o, in0=a, scalar1=x, scalar2=None, op0=p0), r, w)
        elif accum is None:
            P.op(eng, lambda e, o=out, a=in0, x=s1, y=s2, p0=op0, p1=op1:
                 e.tensor_scalar(out=o, in0=a, scalar1=x, scalar2=y, op0=p0, op1=p1), r, w)
        else:
            P.op(eng, lambda e, o=out, a=in0, x=s1, y=s2, p0=op0, p1=op1, ac=accum:
                 e.tensor_scalar(out=o, in0=a, scalar1=x, scalar2=y, op0=p0, op1=p1, accum_out=ac), r, w)

    def stt(eng, out, in0, scalar, in1, op0, op1, r, w):
        P.op(eng, lambda e, o=out, a=in0, s=scalar, b=in1, p0=op0, p1=op1:
             e.scalar_tensor_tensor(out=o, in0=a, scalar=s, in1=b, op0=p0, op1=p1), r, w)

    def act(out, in_, func, r, w, bias=None, scale=1.0, accum=None):
        kw = {}
        if bias is not None:
            kw["bias"] = bias
        if accum is not None:
            kw["accum_out"] = accum
        P.op("act", lambda e, o=out, i=in_, f=func, s=scale, k=kw:
             e.activation(out=o, in_=i, func=f, scale=s, **k), r, w)

    def cp(eng, out, in_, r, w):
        if eng == "act":
            act(out, in_, AF.Copy, r, w)
        else:
            P.op(eng, lambda e, o=out, i=in_: e.tensor_copy(out=o, in_=i), r, w)

    def red(eng, out, in_, op, r, w):
        P.op(eng, lambda e, o=out, i=in_, p=op: e.tensor_reduce(out=o, in_=i, axis=AX.X, op=p), r, w)

    def memset(eng, ap, val, w):
        P.op(eng, lambda e, a=ap, v=val: e.memset(a, v), [], w)

    def dma(q, out, in_, r, w, slow=False):
        if slow:
            P.dma(q, lambda e, o=out, i=in_: e.dma_start(out=o, in_=i, allow_slow_non_contiguous=True), r, w)
        else:
            P.dma(q, lambda e, o=out, i=in_: e.dma_start(out=o, in_=i), r, w)

    def rsqrt_small(out, in_, scale, r_, w_, tmp):
        ts("dve", tmp, in_, scale, EPS, ALU.mult, ALU.add, r_, [w_[0]])
        act(tmp, tmp, AF.Ln, [w_[0]], [w_[0]])
        act(out, tmp, AF.Exp, [w_[0]], w_, scale=-0.5)

    def bc3(ap, n_mid, n_in):
        return ap.unsqueeze(2).to_broadcast([ap.shape[0], n_mid, n_in])

    def bcm(ap, n_mid):
        return ap.unsqueeze(1).to_broadcast([ap.shape[0], n_mid, ap.shape[1]])

    with ExitStack() as top:
        def sb(stack, name, shape, dt=F32):
            return Tl(stack.enter_context(nc.sbuf_tensor("sb_" + name, list(shape), dt)), name)

        def ps(stack, name, shape, dt=F32):
            return Tl(stack.enter_context(nc.psum_tensor("ps_" + name, list(shape), dt)), name)

        Xt = top.enter_context(nc.sbuf_tensor("sb_X", [128, NT, D], F32))
        RX = [Res("X%d" % t) for t in range(NT)]
        XS1T = sb(top, "XS1T", [128, 8, NS])
        HNS = sb(top, "HNS", [128, 8, NS], BF16)
        IDF2 = sb(top, "IDF2", [128, 128])
        IDB = sb(top, "IDB", [128, 128], BF16)
        GPREMLP = sb(top, "GPREMLP", [128, 8])
        BK = [ps(top, "B%d" % i, [128, 512]) for i in range(7)]
        TPB = ps(top, "TPB", [128, 1024], BF16)

        dma("sp", GPREMLP[:], gpremlp_d, [], [GPREMLP])

        all_res_p1 = []

        with ExitStack() as p1:
            cur = [p1]

            def s1(name, shape, dt=F32):
                tl = sb(cur[0], name, shape, dt)
                all_res_p1.append(tl.r)
                return tl

            WIN = p1.enter_context(nc.sbuf_tensor("sb_WIN", [128, 8, 3600], BF16))
            RWIN = [Res("WIN%d" % i) for i in range(len(WIN_MOVES))]
            WOUT = s1("WOUT", [128, 8, D], BF16)
            CONST = s1("CONST", [128, NCONST])
            GPRE = s1("GPRE", [128, 8])
            CW = s1("CW", [128, 4, 12])
            GPM = s1("GPM", [128, D])
            SMB = s1("SMB", [128, 16])
            GNG = s1("GNG", [128, 128])
            MNG = s1("MNG", [128, 128])
            all_res_p1.extend(RWIN)

            win_v = win_d.rearrange("(k p) c -> p k c", p=128)
            for i, (dst, src, n) in enumerate(WIN_MOVES):
                dma("pool", WIN[:, :, dst:dst + n], win_v[:, :, src:src + n], [], [RWIN[i]])
            dma("pool", WOUT[:], wout_d.rearrange("(k p) c -> p k c", p=128), [], [WOUT])
            dma("sp", CONST[:], consts_d, [], [CONST])
            dma("sp", GPRE[:], gpre_d, [], [GPRE])
            dma("sp", CW[:], cw_d.rearrange("p (j c) -> p j c", j=4), [], [CW])
            dma("sp", GPM[:], gpm_d.partition_broadcast(128), [], [GPM])
            dma("sp", SMB[:], small_d.partition_broadcast(128), [], [SMB])
            dma("sp", GNG[:], gng_d.partition_broadcast(128), [], [GNG])
            dma("sp", MNG[:], mng_d.partition_broadcast(128), [], [MNG])

            IDF = CONST[:, C_IDF:C_IDF + 128]
            TRI = CONST[:, C_TRI:C_TRI + 128]
            ONES = CONST[:, C_ONES:C_ONES + 128]
            SEL127 = CONST[:, C_SEL127:C_SEL127 + 128]
            MST4 = CONST[:, C_MST:C_MST + 512]
            MIT4 = CONST[:, C_MIT:C_MIT + 512]
            MI4 = CONST[:, C_MI:C_MI + 512]
            SELR = CONST[0:4, C_SELR:C_SELR + 512]
            ONES4 = CONST[0:4, C_ONES:C_ONES + 128]

            def rwin(c0, c1):
                out = []
                for i, (dst, src, n) in enumerate(WIN_MOVES):
                    if dst < c1 and c0 < dst + n:
                        out.append(RWIN[i])
                return out

            cp("dve", IDB[:], IDF, [CONST], [IDB])
            cp("pool", IDF2[:], IDF, [CONST], [IDF2])

            NEGC = s1("NEGC", [128, 12])
            SGN = s1("SGN", [128, 12])
            BIA = s1("BIA", [128, 12])
            memset("pool", NEGC[:], -1.0, [NEGC])
            act(NEGC[:, 4:8], SMB[:, 0:4], AF.Exp, [SMB, NEGC], [NEGC])
            ts("dve", NEGC[:, 4:8], NEGC[:, 4:8], -1.0, None, ALU.mult, None, [NEGC], [NEGC])
            memset("pool", SGN[:], -1.0, [SGN])
            memset("pool", SGN[:, 4:8], 1.0, [SGN])
            memset("pool", BIA[:], 0.0, [BIA])
            cp("dve", BIA[:, 4:8], SMB[:, 4:8], [SMB, BIA], [BIA])
            ts("dve", BIA[:, 8:12], SMB[:, 12:16], -1.0, None, ALU.mult, None, [SMB, BIA], [BIA])

            p1a = ExitStack()
            cur[0] = p1a
            XN = s1("XN", [128, D], BF16)
            HT = s1("HT", [128, 8, 128], BF16)
            PCC = [s1("PCC%d" % i, [128, 131]) for i in range(2)]
            ACC = [s1("ACC%d" % i, [128, 128]) for i in range(2)]
            TAIL = s1("TAIL", [128, 12, 3])
            QKF = s1("QKF", [128, 8, 128])
            GQT = s1("GQT", [128, 4, 128], BF16)
            GKT = s1("GKT", [128, 4, 128], BF16)
            GVT = s1("GVT", [128, 4, 128], BF16)
            MQT = s1("MQT", [64, 4, 128], BF16)
            MKT = s1("MKT", [64, 4, 128], BF16)
            ZSG = s1("ZSG", [128, 512])
            MOSG = s1("MOSG", [128, 512])
            MVt = s1("MVt", [128, 4, 128], BF16)
            GT = s1("GT", [128, 16])
            ARG = s1("ARG", [128, 12])
            GL = s1("GL", [128, 12])
            BETA = s1("BETA", [128, 4])
            IG = s1("IG", [128, 4])
            GF = s1("GF", [128, 8])
            COLS = s1("COLS", [128, 20])
            RT = s1("RT", [4, 5, 128])
            BD = s1("BD", [4, 4, 128])
            QKD = s1("QKD", [128, 4, 128], BF16)
            MB = [s1("MB%d" % i, [128, 512]) for i in range(2)]
            LB = [s1("LB%d" % i, [128, 512]) for i in range(2)]
            RR = s1("RR", [128, 512])
            BINV = s1("BINV", [128, 4, 128], BF16)
            GKt = s1("GKt", [128, 4, 128], BF16)
            XK = s1("XK", [128, 4, 128], BF16)
            KEND = s1("KEND", [128, 4, 128], BF16)
            GVb = s1("GVb", [128, 4, 128], BF16)
            SC4 = [s1("SC4_%d" % i, [128, 8]) for i in range(6)]
            LASTG = s1("LASTG", [128, 8])
            WKT = s1("WKT", [128, 4, 128], BF16)
            S32 = s1("S32", [128, 4, 128])
            SBh = s1("SBh", [128, 4, 128], BF16)
            U = s1("U", [128, 4, 128], BF16)
            TMP = [s1("TMP%d" % i, [128, 512]) for i in range(2)]
            WV = TMP[1]
            PK = View(GVb[:, :, 0:64], GVb.r)
            EXPQ = TMP[1]
            EXPA = MB[0]
            MIX = XN
            MIXT = HT
            MKt = s1("MKt", [128, 4, 64], BF16)
            PQKb = XK
            PQKT = KEND
            C32 = s1("C32", [64, 4, 128])
            CBh = s1("CBh", [64, 4, 128], BF16)
            N32 = s1("N32", [64, 4])
            NBh = s1("NBh", [64, 4, 2], BF16)
            MBC = s1("MBC", [128, 4])
            DMAX = s1("DMAX", [128, 4])
            T12 = s1("T12", [128, 12])
            WLC = s1("WLC", [64, 4])
            ONEB = s1("ONEB", [128, 2], BF16)
            for b in BK:
                all_res_p1.append(b.r)
            all_res_p1.append(TPB.r)

            memset("pool", TAIL[:], 0.0, [TAIL])
            memset("pool", S32[:], 0.0, [S32])
            memset("pool", SBh[:], 0.0, [SBh])
            memset("pool", C32[:], 0.0, [C32])
            memset("pool", CBh[:], 0.0, [CBh])
            memset("pool", N32[:], 0.0, [N32])
            memset("pool", NBh[:], 0.0, [NBh])
            memset("pool", MBC[:], 0.0, [MBC])
            memset("pool", ONEB[:], 1.0, [ONEB])

            def headnorm_gate(src, gate, dst, nm):
                t0, ss, rs = TMP[1], SC4[4], SC4[5]
                tt("pool", t0[:], src[:], src[:], ALU.mult, [src], [t0])
                red("dve", ss[:, 0:4], t0[:].rearrange("p (h e) -> p h e", h=4), ALU.add, [t0], [ss])
                rsqrt_small(rs[:, 0:4], ss[:, 0:4], 1.0 / 128, [ss], [rs], rs[:, 4:8])
                tt("dve", t0[:].rearrange("p (h e) -> p h e", h=4), src[:].rearrange("p (h e) -> p h e", h=4),
                   bc3(rs[:, 0:4], 4, 128), ALU.mult, [src, rs], [t0])
                tt("dve", dst, t0[:], gate[:], ALU.mult, [t0, gate], [MIX])

            for t in range(KNT):
                Xv = Xt[:, t, :]
                dma("sp", Xv, x_d[t * 128:(t + 1) * 128, :], [], [RX[t]])
                ss, rs = SC4[0], SC4[1]
                act(XN[:], Xv, AF.Square, [RX[t]], [XN, ss], accum=ss[:, 0:1])
                rsqrt_small(rs[:, 0:1], ss[:, 0:1], 1.0 / D, [ss], [rs], rs[:, 1:2])
                ts("dve", XN[:], Xv, rs[:, 0:1], None, ALU.mult, None, [RX[t], rs], [XN])
                for k in range(8):
                    tr(TPB[:, k * 128:(k + 1) * 128], XN[:, k * 128:(k + 1) * 128], IDB[:], [XN, IDB], [TPB])
                tt("dve", HT[:], TPB[:].rearrange("p (k t) -> p k t", k=8), bc3(GPRE[:], 8, 128), ALU.mult,
                   [TPB, GPRE], [HT])

                if KSTAGE < 1:
                    continue
                for ch in range(12):
                    bk = BK[ch % 2]
                    c0 = ch * 128
                    for k in range(8):
                        mm(bk[:, 0:128], WIN[:, k, c0:c0 + 128], HT[:, k, :], k == 0, k == 7,
                           [HT] + rwin(c0, c0 + 128), [bk])
                    if ch < 12:
                        pc = PCC[ch % 2]
                        ac = ACC[ch % 2]
                        ce = "dve"
                        cp("act", pc[:, 3:131], bk[:, 0:128], [bk], [pc])
                        cp("pool", pc[:, 0:3], TAIL[:, ch, :], [TAIL], [pc])
                        ts(ce, ac[:], pc[:, 0:128], CW[:, 0, ch:ch + 1], None, ALU.mult, None, [pc, CW], [ac])
                        for j in range(1, 4):
                            stt(ce, ac[:], pc[:, j:j + 128], CW[:, j, ch:ch + 1], ac[:], ALU.mult, ALU.add,
                                [pc, CW, ac], [ac])
                        cp("pool", TAIL[:, ch, :], pc[:, 128:131], [pc], [TAIL])
                        if ch < 8:
                            act(QKF[:, ch, :], ac[:], AF.Silu, [ac], [QKF])
                        else:
                            act(GVT[:, ch - 8, :], ac[:], AF.Silu, [ac], [GVT])
                for hh in range(8):
                    bk = BK[hh % 2]
                    c0 = W_MQK + hh * 64
                    for k in range(8):
                        mm(bk[0:64, 0:128], WIN[:, k, c0:c0 + 64], HT[:, k, :], k == 0, k == 7,
                           [HT] + rwin(c0, c0 + 64), [bk])
                    if hh < 4:
                        cp("act", MQT[:, hh, :], bk[0:64, 0:128], [bk], [MQT])
                    else:
                        act(MKT[:, hh - 4, :], bk[0:64, 0:128], AF.Copy, [bk], [MKT], scale=0.125)
                for hf in range(2):
                    act(TMP[hf][:], QKF[:, hf * 4:(hf + 1) * 4, :].rearrange("p a b -> p (a b)"), AF.Square,
                        [QKF], [TMP[hf]])
                    mm(BK[2 + hf][:], ONES, TMP[hf][:], True, True, [CONST, TMP[hf]], [BK[2 + hf]])
                for hf in range(2):
                    ts("dve", TMP[hf][:], BK[2 + hf][:], 1.0, EPS, ALU.mult, ALU.add, [BK[2 + hf]], [TMP[hf]])
                    act(TMP[hf][:], TMP[hf][:], AF.Ln, [TMP[hf]], [TMP[hf]])
                    act(TMP[hf][:], TMP[hf][:], AF.Exp, [TMP[hf]], [TMP[hf]], scale=-0.5)
                stt("dve", GQT[:].rearrange("p a b -> p (a b)"), QKF[:, 0:4, :].rearrange("p a b -> p (a b)"),
                    128.0 ** -0.5, TMP[0][:], ALU.mult, ALU.mult, [QKF, TMP[0]], [GQT])
                tt("pool", GKT[:].rearrange("p a b -> p (a b)"), QKF[:, 4:8, :].rearrange("p a b -> p (a b)"),
                   TMP[1][:], ALU.mult, [QKF, TMP[1]], [GKT])

                if KSTAGE < 2:
                    continue
                def tok_proj(bk, c0, n):
                    for k in range(8):
                        mm(bk[:, 0:n], HT[:, k, :], WIN[:, k, c0:c0 + n], k == 0, k == 7,
                           [HT] + rwin(c0, c0 + n), [bk])
                tok_proj(BK[0], W_GZ, 512)
                act(TMP[0][:], BK[0][:], AF.Silu, [BK[0]], [TMP[0]])
                tt("pool", ZSG[:].rearrange("p (h e) -> p h e", h=4), TMP[0][:].rearrange("p (h e) -> p h e", h=4),
                   bcm(GNG[:], 4), ALU.mult, [TMP[0], GNG], [ZSG])
                tok_proj(BK[1], W_MV, 512)
                cp("act", MVt[:].rearrange("p h e -> p (h e)"), BK[1][:], [BK[1]], [MVt])
                tok_proj(BK[0], W_MO, 512)
                act(TMP[0][:], BK[0][:], AF.Exp, [BK[0]], [TMP[0]], scale=-1.0)
                ts("dve", TMP[0][:], TMP[0][:], 1.0, None, ALU.add, None, [TMP[0]], [TMP[0]])
                P.op("dve", lambda e: e.reciprocal(out=TMP[0][:], in_=TMP[0][:]), [TMP[0]], [TMP[0]])
                tt("pool", MOSG[:].rearrange("p (h e) -> p h e", h=4), TMP[0][:].rearrange("p (h e) -> p h e", h=4),
                   bcm(MNG[:], 4), ALU.mult, [TMP[0], MNG], [MOSG])
                tok_proj(BK[1], W_GATE, 16)
                cp("dve", GT[:], BK[1][:, 0:16], [BK[1]], [GT])
                tt("dve", ARG[:], GT[:, 0:12], SGN[:], ALU.mult, [GT, SGN], [ARG])
                tt("dve", ARG[:], ARG[:], BIA[:], ALU.add, [ARG, BIA], [ARG])
                act(ARG[:], ARG[:], AF.Exp, [ARG], [ARG])
                act(ARG[:], ARG[:], AF.Ln, [ARG], [ARG], bias=1.0)
                tt("dve", GL[:], ARG[:], NEGC[:], ALU.mult, [ARG, NEGC], [GL])
                act(BETA[:], GL[:, 0:4], AF.Exp, [GL], [BETA])
                tt("dve", IG[:], GT[:, 12:16], SMB[:, 8:12], ALU.add, [GT, SMB], [IG])

                if KSTAGE < 3:
                    continue
                mm(BK[2][:, 0:8], TRI, GL[:, 4:12], True, True, [CONST, GL], [BK[2]])
                cp("dve", GF[:], BK[2][:, 0:8], [BK[2]], [GF])
                cp("pool", COLS[:, 0:4], GF[:, 0:4], [GF], [COLS])
                tt("dve", COLS[:, 4:8], GF[:, 0:4], GL[:, 0:4], ALU.add, [GF, GL], [COLS])
                cp("pool", COLS[:, 8:12], GF[:, 4:8], [GF], [COLS])
                tt("dve", COLS[:, 12:16], IG[:], GF[:, 4:8], ALU.subtract, [IG, GF], [COLS])
                ts("dve", COLS[:, 16:20], GF[:, 0:4], -1.0, None, ALU.mult, None, [GF], [COLS])
                for j in range(4):
                    tr(BK[3][0:4, j * 128:(j + 1) * 128], COLS[:, 4 * j:4 * j + 4], IDF, [COLS, CONST], [BK[3]])
                tr(BK[4][0:4, 0:128], COLS[:, 16:20], IDF, [COLS, CONST], [BK[4]])
                cp("dve", RT[:, 0:4, :].rearrange("p a b -> p (a b)"), BK[3][0:4, :], [BK[3]], [RT])
                cp("dve", RT[:, 4, :], BK[4][0:4, 0:128], [BK[4]], [RT])
                SELR3 = SELR.rearrange("p (a b) -> p a b", a=4)
                tt("dve", BD[:], bcm(RT[:, 1, :], 4), SELR3, ALU.mult, [RT, CONST], [BD])
                mm(BK[2][:], ONES4, BD[:].rearrange("p a b -> p (a b)"), True, False, [CONST, BD], [BK[2]])
                mm(BK[2][:], RT[:, 4, :], SELR, False, False, [RT, CONST], [BK[2]])
                mm(BK[2][:], IDF, MST4, False, True, [CONST], [BK[2]])
                act(EXPA[:], BK[2][:], AF.Exp, [BK[2]], [EXPA])
                tt("dve", BD[:], bcm(RT[:, 0, :], 4), SELR3, ALU.mult, [RT, CONST], [BD])
                mm(BK[3][:], ONES4, BD[:].rearrange("p a b -> p (a b)"), True, False, [CONST, BD], [BK[3]])
                mm(BK[3][:], RT[:, 4, :], SELR, False, False, [RT, CONST], [BK[3]])
                mm(BK[3][:], IDF, MIT4, False, True, [CONST], [BK[3]])
                act(EXPQ[:], BK[3][:], AF.Exp, [BK[3]], [EXPQ])
                for h in range(4):
                    mm(BK[4][:, h * 128:(h + 1) * 128], GKT[:, h, :], GKT[:, h, :], True, True, [GKT], [BK[4]])
                for h in range(4):
                    mm(BK[5][:, h * 128:(h + 1) * 128], GKT[:, h, :], GQT[:, h, :], True, True, [GKT, GQT], [BK[5]])
                tt("dve", MB[0][:], BK[4][:], EXPA[:], ALU.mult, [BK[4], EXPA], [MB[0]])
                tt("dve", QKD[:].rearrange("p h c -> p (h c)"), BK[5][:], EXPQ[:], ALU.mult, [BK[5], EXPQ], [QKD])

                if KSTAGE < 4:
                    continue
                for h in range(4):
                    tr(BK[2][:, h * 128:(h + 1) * 128], MB[0][:, h * 128:(h + 1) * 128], IDF, [MB[0], CONST], [BK[2]])
                cp("act", LB[0][:], BK[2][:], [BK[2]], [LB[0]])
                tt("dve", RR[:].rearrange("p (h c) -> p h c", h=4), bcm(IDF, 4),
                   MB[0][:].rearrange("p (h c) -> p h c", h=4), ALU.subtract, [CONST, MB[0]], [RR])
                NLEV = 6
                for k in range(NLEV):
                    a, b = k % 2, (k + 1) % 2
                    for h in range(4):
                        sl = slice(h * 128, (h + 1) * 128)
                        mm(BK[3][:, sl], MB[a][:, sl], LB[a][:, sl], True, True, [MB[a], LB[a]], [BK[3]])
                    if k < NLEV - 1:
                        for h in range(4):
                            sl = slice(h * 128, (h + 1) * 128)
                            mm(BK[4][:, sl], LB[a][:, sl], MB[a][:, sl], True, True, [MB[a], LB[a]], [BK[4]])
                    cp("act", LB[b][:], BK[3][:], [BK[3]], [LB[b]])
                    if k < NLEV - 1:
                        cp("dve", MB[b][:], BK[4][:], [BK[4]], [MB[b]])
                    for h in range(4):
                        sl = slice(h * 128, (h + 1) * 128)
                        mm(BK[5][:, sl], LB[b][:, sl], RR[:, sl], True, True, [LB[b], RR], [BK[5]])
                    tt("dve", RR[:], RR[:], BK[5][:], ALU.add, [RR, BK[5]], [RR])
                cp("act", BINV[:].rearrange("p h c -> p (h c)"), RR[:], [RR], [BINV])

                if KSTAGE < 5:
                    continue
                for h in range(4):
                    tr(TPB[:, h * 128:(h + 1) * 128], GKT[:, h, :], IDB[:], [GKT, IDB], [TPB])
                    tr(TPB[:, 512 + h * 128:512 + (h + 1) * 128], GVT[:, h, :], IDB[:], [GVT, IDB], [TPB])
                cp("act", GKt[:].rearrange("p h d -> p (h d)"), TPB[:, 0:512], [TPB], [GKt])
                EG, BEG, EGL, GEND = SC4[0], SC4[1], SC4[2], SC4[3]
                act(EG[:, 0:4], GF[:, 0:4], AF.Exp, [GF], [EG])
                tt("dve", BEG[:, 0:4], EG[:, 0:4], BETA[:], ALU.mult, [EG, BETA], [BEG])
                mm(BK[2][:, 0:8], SEL127, GF[:], True, True, [CONST, GF], [BK[2]])
                cp("dve", LASTG[:], BK[2][:, 0:8], [BK[2]], [LASTG])
                tt("dve", EGL[:, 0:4], LASTG[:, 0:4], GF[:, 0:4], ALU.subtract, [LASTG, GF], [EGL])
                act(EGL[:, 0:4], EGL[:, 0:4], AF.Exp, [EGL], [EGL])
                act(GEND[:, 0:4], LASTG[:, 0:4], AF.Exp, [LASTG], [GEND])
                tt("dve", XK[:], GKt[:], bc3(BEG[:, 0:4], 4, 128), ALU.mult, [GKt, BEG], [XK])
                tt("pool", KEND[:], GKt[:], bc3(EGL[:, 0:4], 4, 128), ALU.mult, [GKt, EGL], [KEND])
                tt("dve", GVb[:], TPB[:, 512:1024].rearrange("p (h e) -> p h e", h=4), bc3(BETA[:], 4, 128),
                   ALU.mult, [TPB, BETA], [GVb])
                for h in range(4):
                    sl = slice(h * 128, (h + 1) * 128)
                    mm(BK[2][:, sl], BINV[:, h, :], GVb[:, h, :], True, True, [BINV, GVb], [BK[2]])
                    mm(BK[3][:, sl], XK[:, h, :], BINV[:, h, :], True, True, [BINV, XK], [BK[3]])
                cp("act", WV[:], BK[2][:], [BK[2]], [WV])
                cp("dve", WKT[:].rearrange("p h c -> p (h c)"), BK[3][:], [BK[3]], [WKT])
                for h in range(4):
                    sl = slice(h * 128, (h + 1) * 128)
                    mm(BK[4][:, sl], WKT[:, h, :], SBh[:, h, :], True, True, [WKT, SBh], [BK[4]])
                    mm(BK[5][:, sl], GQT[:, h, :], SBh[:, h, :], True, True, [GQT, SBh], [BK[5]])
                tt("dve", U[:].rearrange("p h e -> p (h e)"), WV[:], BK[4][:], ALU.subtract, [WV, BK[4]], [U])
                for h in range(4):
                    sl = slice(h * 128, (h + 1) * 128)
                    mm(BK[2][:, sl], QKD[:, h, :], U[:, h, :], True, True, [QKD, U], [BK[2]])
                    mm(BK[3][:, sl], KEND[:, h, :], U[:, h, :], True, True, [KEND, U], [BK[3]])
                OG = TMP[0]
                tt("dve", OG[:].rearrange("p (h e) -> p h e", h=4), BK[5][:].rearrange("p (h e) -> p h e", h=4),
                   bc3(EG[:, 0:4], 4, 128), ALU.mult, [BK[5], EG], [OG])
                tt("dve", OG[:], OG[:], BK[2][:], ALU.add, [OG, BK[2]], [OG])
                for h in range(4):
                    stt("dve", S32[:, h, :], S32[:, h, :], GEND[:, h:h + 1], BK[3][:, h * 128:(h + 1) * 128],
                        ALU.mult, ALU.add, [S32, GEND, BK[3]], [S32])
                cp("act", SBh[:], S32[:], [S32], [SBh])
                headnorm_gate(OG, ZSG, MIX[:, 0:512], "g")

                if KSTAGE < 6:
                    continue
                tt("dve", BD[:], bcm(RT[:, 3, :], 4), SELR3, ALU.mult, [RT, CONST], [BD])
                mm(BK[6][:], ONES4, BD[:].rearrange("p a b -> p (a b)"), True, False, [CONST, BD], [BK[6]])
                mm(BK[6][:], RT[:, 2, :], SELR, False, False, [RT, CONST], [BK[6]])
                mm(BK[6][:], IDF, MI4, False, True, [CONST], [BK[6]])
                red("dve", DMAX[:], BK[6][:].rearrange("p (h s) -> p h s", h=4), ALU.max, [BK[6]], [DMAX])
                if KSUB < 1:
                    continue
                for h in range(4):
                    mm(BK[4][:, h * 128:(h + 1) * 128], MQT[:, h, :], MKT[:, h, :], True, True, [MQT, MKT], [BK[4]])
                if KSUB < 2:
                    continue
                for h in range(4):
                    tr(TPB[:, h * 64:(h + 1) * 64], MKT[:, h, :], IDB[0:64, 0:64], [MKT, IDB], [TPB])
                cp("act", MKt[:].rearrange("p h d -> p (h d)"), TPB[:, 0:256], [TPB], [MKt])
                if KSUB < 3:
                    continue
                Bv, MT, WP, EMT = SC4[0], SC4[1], SC4[2], SC4[3]
                tt("dve", Bv[:, 0:4], GF[:, 4:8], MBC[:], ALU.add, [GF, MBC], [Bv])
                tt("dve", MT[:, 0:4], Bv[:, 0:4], DMAX[:], ALU.max, [Bv, DMAX], [MT])
                tt("dve", WP[:, 0:4], Bv[:, 0:4], MT[:, 0:4], ALU.subtract, [Bv, MT], [WP])
                act(WP[:, 0:4], WP[:, 0:4], AF.Exp, [WP], [WP])
                if KSUB < 4:
                    continue
                PD = TMP[0]
                tt("dve", PD[:].rearrange("p (h s) -> p h s", h=4), BK[6][:].rearrange("p (h s) -> p h s", h=4),
                   bc3(MT[:, 0:4], 4, 128), ALU.subtract, [BK[6], MT], [PD])
                act(PD[:], PD[:], AF.Exp, [PD], [PD])
                tt("dve", PD[:], PD[:], BK[4][:], ALU.mult, [PD, BK[4]], [PD])
                RS = SC4[4]
                red("dve", RS[:, 0:4], PD[:].rearrange("p (h s) -> p h s", h=4), ALU.add, [PD], [RS])
                if KSUB < 5:
                    continue
                cp("act", PQKb[:].rearrange("p h s -> p (h s)"), PD[:], [PD], [PQKb])
                for h in range(4):
                    tr(TPB[:, 512 + h * 128:512 + (h + 1) * 128], PQKb[:, h, :], IDB[:], [PQKb, IDB], [TPB])
                cp("act", PQKT[:].rearrange("p h s -> p (h s)"), TPB[:, 512:1024], [TPB], [PQKT])
                for h in range(4):
                    sl = slice(h * 128, (h + 1) * 128)
                    mm(BK[2][:, sl], PQKT[:, h, :], MVt[:, h, :], True, True, [PQKT, MVt], [BK[2]])
                    mm(BK[3][:, sl], MQT[:, h, :], CBh[:, h, :], True, True, [MQT, CBh], [BK[3]])
                    mm(BK[5][:, 2 * h:2 * h + 2], MQT[:, h, :], NBh[:, h, :], True, True, [MQT, NBh], [BK[5]])
                if KSUB < 6:
                    continue
                NUM = TMP[0]
                tt("dve", NUM[:].rearrange("p (h e) -> p h e", h=4), BK[3][:].rearrange("p (h e) -> p h e", h=4),
                   bc3(WP[:, 0:4], 4, 128), ALU.mult, [BK[3], WP], [NUM])
                tt("dve", NUM[:], NUM[:], BK[2][:], ALU.add, [NUM, BK[2]], [NUM])
                DEN = SC4[5]
                tt("dve", DEN[:, 0:4], BK[5][:, 0:8].rearrange("p (h two) -> p h two", two=2)[:, :, 0], WP[:, 0:4], ALU.mult, [BK[5], WP], [DEN])
                tt("dve", DEN[:, 0:4], DEN[:, 0:4], RS[:, 0:4], ALU.add, [DEN, RS], [DEN])
                act(EMT[:, 0:4], MT[:, 0:4], AF.Exp, [MT], [EMT], scale=-1.0)
                ts("dve", DEN[:, 4:8], DEN[:, 0:4], -1.0, None, ALU.mult, None, [DEN], [DEN])
                tt("dve", DEN[:, 0:4], DEN[:, 0:4], DEN[:, 4:8], ALU.max, [DEN], [DEN])
                tt("dve", DEN[:, 0:4], DEN[:, 0:4], EMT[:, 0:4], ALU.max, [DEN, EMT], [DEN])
                P.op("dve", lambda e, d=DEN: e.reciprocal(out=d[:, 0:4], in_=d[:, 0:4]), [DEN], [DEN])
                tt("dve", NUM[:].rearrange("p (h e) -> p h e", h=4), NUM[:].rearrange("p (h e) -> p h e", h=4),
                   bc3(DEN[:, 0:4], 4, 128), ALU.mult, [NUM, DEN], [NUM])
                if KSUB < 7:
                    continue
                cp("pool", T12[:, 0:4], MT[:, 0:4], [MT], [T12])
                tt("dve", T12[:, 4:8], GF[:, 4:8], MT[:, 0:4], ALU.subtract, [GF, MT], [T12])
                cp("pool", T12[:, 8:12], WP[:, 0:4], [WP], [T12])
                mm(BK[5][:, 16:28], SEL127, T12[:], True, True, [CONST, T12], [BK[5]])
                cp("dve", MBC[:], BK[5][:, 16:20], [BK[5]], [MBC])
                PEND = SC4[4]
                tt("dve", PEND[:, 4:8], COLS[:, 12:16], BK[5][:, 20:24], ALU.add, [COLS, BK[5]], [PEND])
                act(PEND[:, 4:8], PEND[:, 4:8], AF.Exp, [PEND], [PEND])
                cp("dve", WLC[:], BK[5][0:64, 24:28], [BK[5]], [WLC])
                if KSUB < 8:
                    continue
                tt("dve", PK[:], MKt[:], bc3(PEND[:, 4:8], 4, 64), ALU.mult, [MKt, PEND], [PK])
                if KSUB < 9:
                    continue
                for h in range(4):
                    mm(BK[3][0:64, h * 128:(h + 1) * 128], PK[:, h, :], MVt[:, h, :], True, True, [PK, MVt], [BK[3]])
                for h in range(4):
                    mm(BK[5][0:64, 32 + 2 * h:34 + 2 * h], PK[:, h, :], ONEB[:], True, True, [PK, ONEB], [BK[5]])
                if KSUB < 10:
                    continue
                for h in range(4):
                    stt("dve", C32[:, h, :], C32[:, h, :], WLC[:, h:h + 1], BK[3][0:64, h * 128:(h + 1) * 128],
                        ALU.mult, ALU.add, [C32, WLC, BK[3]], [C32])
                tt("dve", N32[:], N32[:], WLC[:], ALU.mult, [N32, WLC], [N32])
                tt("dve", N32[:], N32[:], BK[5][0:64, 32:40].rearrange("p (a two) -> p a two", two=2)[:, :, 0],
                   ALU.add, [N32, BK[5]], [N32])
                cp("act", CBh[:], C32[:], [C32], [CBh])
                cp("act", NBh[:, :, 0], N32[:], [N32], [NBh])
                if KSUB < 11:
                    continue
                headnorm_gate(NUM, MOSG, MIX[:, 512:1024], "m")

                if KSTAGE < 7:
                    continue
                for k in range(8):
                    tr(TPB[:, k * 128:(k + 1) * 128], MIX[:, k * 128:(k + 1) * 128], IDB[:], [MIX, IDB], [TPB])
                cp("act", MIXT[:].rearrange("p k t -> p (k t)"), TPB[:], [TPB], [MIXT])
                for eh in range(2):
                    for k in range(8):
                        mm(BK[eh][:], MIXT[:, k, :], WOUT[:, k, eh * 512:(eh + 1) * 512], k == 0, k == 7,
                           [MIXT, WOUT], [BK[eh]])
                ss, rs = SC4[0], SC4[1]
                for eh in range(2):
                    act(XN[:, eh * 512:(eh + 1) * 512], BK[eh][:], AF.Square, [BK[eh]], [XN, ss],
                        accum=ss[:, eh:eh + 1])
                tt("dve", ss[:, 2:3], ss[:, 0:1], ss[:, 1:2], ALU.add, [ss], [ss])
                rsqrt_small(rs[:, 0:1], ss[:, 2:3], 1.0 / D, [ss], [rs], rs[:, 1:2])
                for eh in range(2):
                    sl = slice(eh * 512, (eh + 1) * 512)
                    stt("dve", TMP[eh][:], BK[eh][:], rs[:, 0:1], GPM[:, sl], ALU.mult, ALU.mult,
                        [BK[eh], rs, GPM], [TMP[eh]])
                    tt("pool", Xt[:, t, sl], Xt[:, t, sl], TMP[eh][:], ALU.add, [RX[t], TMP[eh]], [RX[t]])

            if KSTAGE >= 99:
              pass
            dma("sp", pS_d.rearrange("h d e -> d h e"), S32[:], [S32], [])
            dma("sp", pC_d.rearrange("h d e -> d h e"), C32[:], [C32], [])
            dma("sp", pn_d.rearrange("h d -> d h"), N32[:], [N32], [], slow=True)
            dma("sp", pm_d, MBC[0:1, :], [MBC], [])
            for ch in range(12):
                tr(BK[2 + ch // 4][0:3, (ch % 4) * 128:(ch % 4 + 1) * 128], TAIL[:, ch, :], IDF, [TAIL, CONST],
                   [BK[2 + ch // 4]])
            for g in range(3):
                cp("dve", TMP[g % 2][0:3, :], BK[2 + g][0:3, :], [BK[2 + g]], [TMP[g % 2]])
                dma("sp", pconv_d[:, g * 512:(g + 1) * 512], TMP[g % 2][0:3, :], [TMP[g % 2]], [])


            P.barrier(all_res_p1 + RX + [XS1T.r, HNS.r, IDF2.r, IDB.r, GPREMLP.r])
            p1a.close()
            p1b = ExitStack()
            cur[0] = p1b
            R_ = NS
            GRP = 1
            XS = s1("XS", [R_, D])
            XNs = s1("XNs", [R_, D], BF16)
            HTs = s1("HTs", [128, 8, R_], BF16)
            SCV = s1("SCV", [R_, 512])
            XPT = s1("XPT", [128, 12, 4, R_])
            ACs = s1("ACs", [128, 12, R_])
            TM12 = s1("TM12", [128, 12, R_])
            SQs = s1("SQs", [128, 8, R_])
            GQs = s1("GQs", [128, 4, R_])
            GKs = s1("GKs", [128, 4, R_])
            GVs = s1("GVs", [128, 4, R_])
            MQs = s1("MQs", [64, 4, R_])
            MKs = s1("MKs", [64, 4, R_])
            ZSGs = s1("ZSGs", [R_, 512])
            MOSGs = s1("MOSGs", [R_, 512])
            MVs = s1("MVs", [R_, 4, 128])
            GTs = s1("GTs", [R_, 16])
            ARGs = s1("ARGs", [R_, 12])
            GLs = s1("GLs", [R_, 12])
            BETAs = s1("BETAs", [R_, 4])
            IGs = s1("IGs", [R_, 4])
            EGs = s1("EGs", [R_, 4])
            Qt = s1("Qt", [R_, 4, 128])
            Kt = s1("Kt", [R_, 4, 128])
            Vt = s1("Vt", [R_, 4, 128])
            MQt = s1("MQt", [R_, 4, 64])
            MKtt = s1("MKtt", [R_, 4, 64])
            PKt = s1("PKt", [R_, 4, 64])
            DIAGI = s1("DIAGI", [R_, 16, 16])
            M16 = s1("M16", [128, 16, 16])
            KD = s1("KD", [128, 4, 16])
            QD = s1("QD", [128, 4, 16])
            QDm = s1("QDm", [64, 4, 16])
            DG4 = s1("DG4", [R_, 16, 4])
            EGBC = s1("EGBC", [128, 64])
            WPBC = s1("WPBC", [128, 64])
            S0g = s1("S0g", [128, GRP, 4, 128])
            C0g = s1("C0g", [64, GRP, 4, 128])
            KS = s1("KS", [R_, 512])
            QS = s1("QS", [R_, 512])
            QC = s1("QC", [R_, 512])
            Ug = s1("Ug", [R_, 4, 128])
            KROW = s1("KROW", [R_, 512])
            PKROW = View(DIAGI[:].rearrange("p a b -> p (a b)"), DIAGI.r)
            N0 = s1("N0", [R_, 4, 64])
            SMs = [s1("SMs%d" % i, [R_, 8]) for i in range(10)]
            T1 = s1("T1s", [R_, 512])
            T2 = KROW

            def hn_gate16(src, gate, dst, wres):
                ss, rs = SMs[8], SMs[9]
                tt("pool", T2[:], src[:], src[:], ALU.mult, [src], [T2])
                red("dve", ss[:, 0:4], T2[:].rearrange("p (h e) -> p h e", h=4), ALU.add, [T2], [ss])
                rsqrt_small(rs[:, 0:4], ss[:, 0:4], 1.0 / 128, [ss], [rs], rs[:, 4:8])
                tt("dve", T2[:].rearrange("p (h e) -> p h e", h=4), src[:].rearrange("p (h e) -> p h e", h=4),
                   bc3(rs[:, 0:4], 4, 128), ALU.mult, [src, rs], [T2])
                tt("dve", dst, T2[:], gate[:], ALU.mult, [T2, gate], [wres])

            IDF16 = CONST[0:R_, C_IDF:C_IDF + R_]
            ONES16 = CONST[0:R_, C_ONES:C_ONES + 128]

            dma("sp", XS[:], xs_d, [], [XS])
            dma("sp", N0[:].rearrange("p h d -> p (h d)"), sn_d, [], [N0])
            dma("sp", SMs[0][:, 0:4], sm_d, [], [SMs[0]])
            dma("sp", oconv_d[:, 0:2, :], sconv_d[:, 1:3, :], [], [])
            ss, rs = SMs[8], SMs[9]
            act(XNs[:], XS[:], AF.Square, [XS], [XNs, ss], accum=ss[:, 0:1])
            rsqrt_small(rs[:, 0:1], ss[:, 0:1], 1.0 / D, [ss], [rs], rs[:, 1:2])
            ts("dve", XNs[:], XS[:], rs[:, 0:1], None, ALU.mult, None, [XS, rs], [XNs])
            for k in range(8):
                tr(TPB[:, k * 128:k * 128 + R_], XNs[:, k * 128:(k + 1) * 128], IDB[0:R_, 0:R_], [XNs, IDB], [TPB])
            tt("dve", HTs[:], TPB[:].rearrange("p (k t) -> p k t", k=8)[:, :, 0:R_], bc3(GPRE[:], 8, R_), ALU.mult,
               [TPB, GPRE], [HTs])

            for ch in range(12):
                bk = BK[ch % 2]
                c0 = ch * 128
                for k in range(8):
                    mm(bk[:, 0:R_], WIN[:, k, c0:c0 + 128], HTs[:, k, :], k == 0, k == 7, [HTs] + rwin(c0, c0 + 128), [bk])
                cp("act", XPT[:, ch, 3, :], bk[:, 0:R_], [bk], [XPT])
            for hh in range(8):
                bk = BK[hh % 2]
                c0 = W_MQK + hh * 64
                for k in range(8):
                    mm(bk[0:64, 0:R_], WIN[:, k, c0:c0 + 64], HTs[:, k, :], k == 0, k == 7, [HTs] + rwin(c0, c0 + 64), [bk])
                if hh < 4:
                    cp("act", MQs[:, hh, :], bk[0:64, 0:R_], [bk], [MQs])
                else:
                    act(MKs[:, hh - 4, :], bk[0:64, 0:R_], AF.Copy, [bk], [MKs], scale=0.125)
            for g in range(3):
                for j in range(3):
                    dma("sp", SCV[:], sconv_d[:, j, g * 512:(g + 1) * 512], [], [SCV])
                    for c4 in range(4):
                        o0 = (j * 4 + c4) * R_
                        tr(BK[2][:, o0:o0 + R_], SCV[:, c4 * 128:(c4 + 1) * 128], IDF16, [SCV, CONST], [BK[2]])
                cp("dve", XPT[:, 4 * g:4 * g + 4, 0:3, :],
                   BK[2][:, 0:12 * R_].rearrange("p (j c r) -> p c j r", j=3, c=4), [BK[2]], [XPT])
            for g in range(3):
                bk = BK[3 + g % 2]
                for k in range(8):
                    mm(bk[0:R_, :], HTs[:, k, :], WIN[:, k, g * 512:(g + 1) * 512], k == 0, k == 7,
                       [HTs] + rwin(g * 512, (g + 1) * 512), [bk])
                cp("act", SCV[:], bk[0:R_, :], [bk], [SCV])
                dma("sp", oconv_d[:, 2, g * 512:(g + 1) * 512], SCV[:], [SCV], [])
            tt("dve", ACs[:], XPT[:, :, 0, :], bc3(CW[:, 0, :], 12, R_), ALU.mult, [XPT, CW], [ACs])
            for j in range(1, 4):
                tt("dve", TM12[:], XPT[:, :, j, :], bc3(CW[:, j, :], 12, R_), ALU.mult, [XPT, CW], [TM12])
                tt("dve", ACs[:], ACs[:], TM12[:], ALU.add, [ACs, TM12], [ACs])
            QKFs = TM12
            act(QKFs[:, 0:8, :], ACs[:, 0:8, :], AF.Silu, [ACs], [QKFs])
            act(GVs[:], ACs[:, 8:12, :], AF.Silu, [ACs], [GVs])
            act(SQs[:], QKFs[:, 0:8, :], AF.Square, [QKFs], [SQs])
            mm(BK[2][:, 0:8 * R_], ONES, SQs[:].rearrange("p a b -> p (a b)"), True, True, [CONST, SQs], [BK[2]])
            ts("dve", SQs[:].rearrange("p a b -> p (a b)"), BK[2][:, 0:8 * R_], 1.0, EPS, ALU.mult, ALU.add, [BK[2]], [SQs])
            act(SQs[:], SQs[:], AF.Ln, [SQs], [SQs])
            act(SQs[:], SQs[:], AF.Exp, [SQs], [SQs], scale=-0.5)
            stt("dve", GQs[:], QKFs[:, 0:4, :], 128.0 ** -0.5, SQs[:, 0:4, :], ALU.mult, ALU.mult, [QKFs, SQs], [GQs])
            tt("dve", GKs[:], QKFs[:, 4:8, :], SQs[:, 4:8, :], ALU.mult, [QKFs, SQs], [GKs])

            def tok16(bk, c0, n):
                for k in range(8):
                    mm(bk[0:R_, 0:n], HTs[:, k, :], WIN[:, k, c0:c0 + n], k == 0, k == 7, [HTs] + rwin(c0, c0 + n), [bk])
            tok16(BK[0], W_GZ, 512)
            act(T1[:], BK[0][0:R_, :], AF.Silu, [BK[0]], [T1])
            tt("dve", ZSGs[:].rearrange("p (h e) -> p h e", h=4), T1[:].rearrange("p (h e) -> p h e", h=4),
               bcm(GNG[0:R_, :], 4), ALU.mult, [T1, GNG], [ZSGs])
            tok16(BK[1], W_MV, 512)
            cp("act", MVs[:].rearrange("p h e -> p (h e)"), BK[1][0:R_, :], [BK[1]], [MVs])
            tok16(BK[0], W_MO, 512)
            act(T1[:], BK[0][0:R_, :], AF.Exp, [BK[0]], [T1], scale=-1.0)
            ts("dve", T1[:], T1[:], 1.0, None, ALU.add, None, [T1], [T1])
            P.op("dve", lambda e: e.reciprocal(out=T1[:], in_=T1[:]), [T1], [T1])
            tt("dve", MOSGs[:].rearrange("p (h e) -> p h e", h=4), T1[:].rearrange("p (h e) -> p h e", h=4),
               bcm(MNG[0:R_, :], 4), ALU.mult, [T1, MNG], [MOSGs])
            tok16(BK[1], W_GATE, 16)
            cp("dve", GTs[:], BK[1][0:R_, 0:16], [BK[1]], [GTs])
            tt("dve", ARGs[:], GTs[:, 0:12], SGN[0:R_, :], ALU.mult, [GTs, SGN], [ARGs])
            tt("dve", ARGs[:], ARGs[:], BIA[0:R_, :], ALU.add, [ARGs, BIA], [ARGs])
            act(ARGs[:], ARGs[:], AF.Exp, [ARGs], [ARGs])
            act(ARGs[:], ARGs[:], AF.Ln, [ARGs], [ARGs], bias=1.0)
            tt("dve", GLs[:], ARGs[:], NEGC[0:R_, :], ALU.mult, [ARGs, NEGC], [GLs])
            act(BETAs[:], GLs[:, 0:4], AF.Exp, [GLs], [BETAs])
            tt("dve", IGs[:], GTs[:, 12:16], SMB[0:R_, 8:12], ALU.add, [GTs, SMB], [IGs])
            act(EGs[:], GLs[:, 4:8], AF.Exp, [GLs], [EGs])
            M0s, Bs, MTs, WPs, Ps, QKG, QKMs, QNs = SMs[0], SMs[1], SMs[2], SMs[3], SMs[4], SMs[5], SMs[6], SMs[7]
            tt("dve", Bs[:, 0:4], GLs[:, 8:12], M0s[:, 0:4], ALU.add, [GLs, M0s], [Bs])
            tt("dve", MTs[:, 0:4], Bs[:, 0:4], IGs[:], ALU.max, [Bs, IGs], [MTs])
            tt("dve", WPs[:, 0:4], Bs[:, 0:4], MTs[:, 0:4], ALU.subtract, [Bs, MTs], [WPs])
            act(WPs[:, 0:4], WPs[:, 0:4], AF.Exp, [WPs], [WPs])
            tt("dve", Ps[:, 0:4], IGs[:], MTs[:, 0:4], ALU.subtract, [IGs, MTs], [Ps])
            act(Ps[:, 0:4], Ps[:, 0:4], AF.Exp, [Ps], [Ps])
            dma("sp", om_d, MTs[:, 0:4], [MTs], [])

            for h in range(4):
                tr(BK[2][0:R_, h * 128:(h + 1) * 128], GQs[:, h, :], IDF, [GQs, CONST], [BK[2]])
                tr(BK[3][0:R_, h * 128:(h + 1) * 128], GKs[:, h, :], IDF, [GKs, CONST], [BK[3]])
                tr(BK[4][0:R_, h * 128:(h + 1) * 128], GVs[:, h, :], IDF, [GVs, CONST], [BK[4]])
                tr(BK[5][0:R_, h * 64:(h + 1) * 64], MQs[:, h, :], CONST[0:64, C_IDF:C_IDF + 64], [MQs, CONST], [BK[5]])
                tr(BK[5][0:R_, 256 + h * 64:256 + (h + 1) * 64], MKs[:, h, :], CONST[0:64, C_IDF:C_IDF + 64],
                   [MKs, CONST], [BK[5]])
            cp("act", Qt[:].rearrange("p h d -> p (h d)"), BK[2][0:R_, :], [BK[2]], [Qt])
            cp("dve", Kt[:].rearrange("p h d -> p (h d)"), BK[3][0:R_, :], [BK[3]], [Kt])
            cp("act", Vt[:].rearrange("p h d -> p (h d)"), BK[4][0:R_, :], [BK[4]], [Vt])
            cp("dve", MQt[:].rearrange("p h d -> p (h d)"), BK[5][0:R_, 0:256], [BK[5]], [MQt])
            cp("act", MKtt[:].rearrange("p h d -> p (h d)"), BK[5][0:R_, 256:512], [BK[5]], [MKtt])
            tt("dve", T1[:], Qt[:].rearrange("p h d -> p (h d)"), Kt[:].rearrange("p h d -> p (h d)"), ALU.mult,
               [Qt, Kt], [T1])
            red("dve", QKG[:, 0:4], T1[:].rearrange("p (h d) -> p h d", h=4), ALU.add, [T1], [QKG])
            tt("dve", T1[:, 0:256], MQt[:].rearrange("p h d -> p (h d)"), MKtt[:].rearrange("p h d -> p (h d)"),
               ALU.mult, [MQt, MKtt], [T1])
            red("dve", QKMs[:, 0:4], T1[:, 0:256].rearrange("p (h d) -> p h d", h=4), ALU.add, [T1], [QKMs])
            tt("dve", T1[:, 0:256], MQt[:].rearrange("p h d -> p (h d)"), N0[:].rearrange("p h d -> p (h d)"),
               ALU.mult, [MQt, N0], [T1])
            red("dve", QNs[:, 0:4], T1[:, 0:256].rearrange("p (h d) -> p h d", h=4), ALU.add, [T1], [QNs])
            tt("dve", PKt[:], MKtt[:], bc3(Ps[:, 0:4], 4, 64), ALU.mult, [MKtt, Ps], [PKt])
            tt("dve", N0[:], N0[:], bc3(WPs[:, 0:4], 4, 64), ALU.mult, [N0, WPs], [N0])
            tt("dve", N0[:], N0[:], PKt[:], ALU.add, [N0, PKt], [N0])
            dma("sp", on_d, N0[:].rearrange("p h d -> p (h d)"), [N0], [])

            tt("dve", DIAGI[:], bc3(IDF16, R_, R_), bcm(IDF16, R_), ALU.mult, [CONST], [DIAGI])
            mm(BK[2][:, 0:R_ * R_], ONES16, DIAGI[:].rearrange("p a b -> p (a b)"), True, True, [CONST, DIAGI], [BK[2]])
            cp("dve", M16[:].rearrange("p a b -> p (a b)"), BK[2][:, 0:R_ * R_], [BK[2]], [M16])
            tt("dve", DG4[:], bcm(EGs[:], R_), bc3(IDF16, R_, 4), ALU.mult, [EGs, CONST], [DG4])
            mm(BK[3][:, 0:64], ONES16, DG4[:].rearrange("p a b -> p (a b)"), True, True, [CONST, DG4], [BK[3]])
            cp("dve", EGBC[:], BK[3][:, 0:64], [BK[3]], [EGBC])
            tt("dve", DG4[:], bcm(WPs[:, 0:4], R_), bc3(IDF16, R_, 4), ALU.mult, [WPs, CONST], [DG4])
            mm(BK[3][:, 0:64], ONES16, DG4[:].rearrange("p a b -> p (a b)"), True, True, [CONST, DG4], [BK[3]])
            cp("dve", WPBC[:], BK[3][:, 0:64], [BK[3]], [WPBC])

            for g in range(R_ // GRP):
                r0 = g * GRP
                dma("sp", S0g[:], sS_d[r0:r0 + GRP].rearrange("r h d e -> d r h e"), [], [S0g])
                dma("sp", C0g[:], sC_d[r0:r0 + GRP].rearrange("r h d e -> d r h e"), [], [C0g])
                tt("dve", KD[:], GKs[:], bcm(M16[:, r0, :], 4), ALU.mult, [GKs, M16], [KD])
                tt("pool", QD[:], GQs[:], bcm(M16[:, r0, :], 4), ALU.mult, [GQs, M16], [QD])
                tt("dve", QDm[:], MQs[:], bcm(M16[0:64, r0, :], 4), ALU.mult, [MQs, M16], [QDm])
                for h in range(4):
                    sl = slice(h * 128, (h + 1) * 128)
                    mm(BK[2][0:R_, sl], KD[:, h, :], S0g[:, 0, h, :], True, True, [KD, S0g], [BK[2]])
                    mm(BK[3][0:R_, sl], QD[:, h, :], S0g[:, 0, h, :], True, True, [QD, S0g], [BK[3]])
                    mm(BK[5][0:R_, sl], QDm[:, h, :], C0g[:, 0, h, :], True, True, [QDm, C0g], [BK[5]])
                if g == 0:
                    cp("dve", KS[:], BK[2][0:R_, :], [BK[2]], [KS])
                    cp("act", QS[:], BK[3][0:R_, :], [BK[3]], [QS])
                    cp("act", QC[:], BK[5][0:R_, :], [BK[5]], [QC])
                else:
                    tt("dve", KS[:], KS[:], BK[2][0:R_, :], ALU.add, [KS, BK[2]], [KS])
                    tt("dve", QS[:], QS[:], BK[3][0:R_, :], ALU.add, [QS, BK[3]], [QS])
                    tt("dve", QC[:], QC[:], BK[5][0:R_, :], ALU.add, [QC, BK[5]], [QC])
                tt("dve", Ug[:], BK[2][0:R_, :].rearrange("p (h e) -> p h e", h=4), bc3(EGs[:], 4, 128), ALU.mult,
                   [BK[2], EGs], [Ug])
                tt("dve", Ug[:], Vt[:], Ug[:], ALU.subtract, [Vt, Ug], [Ug])
                tt("dve", Ug[:], Ug[:], bc3(BETAs[:], 4, 128), ALU.mult, [Ug, BETAs], [Ug])
                for rl in range(GRP):
                    r = r0 + rl
                    ts("dve", KROW[:], Kt[:].rearrange("p h d -> p (h d)"), IDF16[:, r:r + 1], None, ALU.mult, None,
                       [Kt, CONST], [KROW])
                    ts("dve", PKROW[:], PKt[:].rearrange("p h d -> p (h d)"), IDF16[:, r:r + 1], None, ALU.mult, None,
                       [PKt, CONST], [PKROW])
                    for h in range(4):
                        mm(BK[4][:, h * 128:(h + 1) * 128], KROW[:, h * 128:(h + 1) * 128], Ug[:, h, :], True, True,
                           [KROW, Ug], [BK[4]])
                    for h in range(4):
                        mm(BK[6][0:64, h * 128:(h + 1) * 128], PKROW[:, h * 64:(h + 1) * 64], MVs[:, h, :], True, True,
                           [PKROW, MVs], [BK[6]])
                    for h in range(4):
                        stt("dve", S0g[:, rl, h, :], S0g[:, rl, h, :], EGBC[:, r * 4 + h:r * 4 + h + 1],
                            BK[4][:, h * 128:(h + 1) * 128], ALU.mult, ALU.add, [S0g, EGBC, BK[4]], [S0g])
                        stt("dve", C0g[:, rl, h, :], C0g[:, rl, h, :], WPBC[0:64, r * 4 + h:r * 4 + h + 1],
                            BK[6][0:64, h * 128:(h + 1) * 128], ALU.mult, ALU.add, [C0g, WPBC, BK[6]], [C0g])
                dma("sp", oS_d[r0:r0 + GRP].rearrange("r h d e -> d r h e"), S0g[:], [S0g], [])
                dma("sp", oC_d[r0:r0 + GRP].rearrange("r h d e -> d r h e"), C0g[:], [C0g], [])

            MIXs = XNs
            tt("dve", Ug[:], KS[:].rearrange("p (h e) -> p h e", h=4), bc3(EGs[:], 4, 128), ALU.mult, [KS, EGs], [Ug])
            tt("dve", Ug[:], Vt[:], Ug[:], ALU.subtract, [Vt, Ug], [Ug])
            tt("dve", Ug[:], Ug[:], bc3(BETAs[:], 4, 128), ALU.mult, [Ug, BETAs], [Ug])
            tt("dve", T1[:].rearrange("p (h e) -> p h e", h=4), QS[:].rearrange("p (h e) -> p h e", h=4),
               bc3(EGs[:], 4, 128), ALU.mult, [QS, EGs], [T1])
            tt("dve", Ug[:], Ug[:], bc3(QKG[:, 0:4], 4, 128), ALU.mult, [Ug, QKG], [Ug])
            tt("dve", T1[:], T1[:], Ug[:].rearrange("p h e -> p (h e)"), ALU.add, [T1, Ug], [T1])
            hn_gate16(T1, ZSGs, MIXs[:, 0:512], MIXs)
            PQ = SMs[5]
            tt("dve", PQ[:, 4:8], Ps[:, 0:4], QKMs[:, 0:4], ALU.mult, [Ps, QKMs], [PQ])
            tt("dve", T1[:].rearrange("p (h e) -> p h e", h=4), QC[:].rearrange("p (h e) -> p h e", h=4),
               bc3(WPs[:, 0:4], 4, 128), ALU.mult, [QC, WPs], [T1])
            tt("dve", Ug[:], MVs[:], bc3(PQ[:, 4:8], 4, 128), ALU.mult, [MVs, PQ], [Ug])
            tt("dve", T1[:], T1[:], Ug[:].rearrange("p h e -> p (h e)"), ALU.add, [T1, Ug], [T1])
            DENs, EMTs = SMs[6], SMs[7]
            tt("dve", DENs[:, 4:8], WPs[:, 0:4], QNs[:, 0:4], ALU.mult, [WPs, QNs], [DENs])
            tt("dve", DENs[:, 4:8], DENs[:, 4:8], PQ[:, 4:8], ALU.add, [DENs, PQ], [DENs])
            ts("dve", DENs[:, 0:4], DENs[:, 4:8], -1.0, None, ALU.mult, None, [DENs], [DENs])
            tt("dve", DENs[:, 4:8], DENs[:, 4:8], DENs[:, 0:4], ALU.max, [DENs], [DENs])
            act(EMTs[:, 4:8], MTs[:, 0:4], AF.Exp, [MTs], [EMTs], scale=-1.0)
            tt("dve", DENs[:, 4:8], DENs[:, 4:8], EMTs[:, 4:8], ALU.max, [DENs, EMTs], [DENs])
            P.op("dve", lambda e, d=DENs: e.reciprocal(out=d[:, 4:8], in_=d[:, 4:8]), [DENs], [DENs])
            tt("dve", T1[:].rearrange("p (h e) -> p h e", h=4), T1[:].rearrange("p (h e) -> p h e", h=4),
               bc3(DENs[:, 4:8], 4, 128), ALU.mult, [T1, DENs], [T1])
            hn_gate16(T1, MOSGs, MIXs[:, 512:1024], MIXs)
            for k in range(8):
                tr(TPB[:, k * 128:k * 128 + R_], MIXs[:, k * 128:(k + 1) * 128], IDB[0:R_, 0:R_], [MIXs, IDB], [TPB])
            cp("act", HTs[:], TPB[:].rearrange("p (k t) -> p k t", k=8)[:, :, 0:R_], [TPB], [HTs])
            for eh in range(2):
                for k in range(8):
                    mm(BK[eh][0:R_, :], HTs[:, k, :], WOUT[:, k, eh * 512:(eh + 1) * 512], k == 0, k == 7,
                       [HTs, WOUT], [BK[eh]])
            ss, rs = SMs[8], SMs[9]
            for eh in range(2):
                act(XNs[:, eh * 512:(eh + 1) * 512], BK[eh][0:R_, :], AF.Square, [BK[eh]], [XNs, ss],
                    accum=ss[:, eh:eh + 1])
            tt("dve", ss[:, 2:3], ss[:, 0:1], ss[:, 1:2], ALU.add, [ss], [ss])
            rsqrt_small(rs[:, 0:1], ss[:, 2:3], 1.0 / D, [ss], [rs], rs[:, 1:2])
            for eh in range(2):
                sl = slice(eh * 512, (eh + 1) * 512)
                stt("dve", T1[:], BK[eh][0:R_, :], rs[:, 0:1], GPM[0:R_, sl], ALU.mult, ALU.mult, [BK[eh], rs, GPM], [T1])
                tt("dve", XS[:, sl], XS[:, sl], T1[:], ALU.add, [XS, T1], [XS])
            for k in range(8):
                tr(BK[2][:, k * R_:(k + 1) * R_], XS[:, k * 128:(k + 1) * 128], IDF16, [XS, CONST], [BK[2]])
            cp("dve", XS1T[:].rearrange("p k r -> p (k r)"), BK[2][:, 0:8 * R_], [BK[2]], [XS1T])
            act(XNs[:], XS[:], AF.Square, [XS], [XNs, ss], accum=ss[:, 4:5])
            rsqrt_small(rs[:, 4:5], ss[:, 4:5], 1.0 / D, [ss], [rs], rs[:, 5:6])
            ts("dve", XNs[:], XS[:], rs[:, 4:5], None, ALU.mult, None, [XS, rs], [XNs])
            for k in range(8):
                tr(TPB[:, k * 128:k * 128 + R_], XNs[:, k * 128:(k + 1) * 128], IDB[0:R_, 0:R_], [XNs, IDB], [TPB])
            tt("dve", HNS[:], TPB[:].rearrange("p (k t) -> p k t", k=8)[:, :, 0:R_], bc3(GPREMLP[:], 8, R_), ALU.mult,
               [TPB, GPREMLP], [HNS])

            P.barrier(all_res_p1 + RX + [XS1T.r, HNS.r, IDF2.r, IDB.r, GPREMLP.r])
            p1b.close()

        with ExitStack() as p2:
            WUP = p2.enter_context(nc.sbuf_tensor("sb_WUP", [128, 8, DFF], BF16))
            WDN = p2.enter_context(nc.sbuf_tensor("sb_WDN", [128, 32, D], BF16))
            NWC = 8
            RWU = [Res("WUP%d" % i) for i in range(NWC)]
            RWD = [Res("WDN%d" % i) for i in range(NWC)]
            wup_v = wup_d.rearrange("(k p) c -> p k c", p=128)
            wdn_v = wdn_d.rearrange("(k p) c -> p k c", p=128)
            for i in range(NWC if KPH2 else 0):
                dma("pool", WUP[:, :, i * 512:(i + 1) * 512], wup_v[:, :, i * 512:(i + 1) * 512], [], [RWU[i]])
                dma("pool", WDN[:, i * 4:(i + 1) * 4, :], wdn_v[:, i * 4:(i + 1) * 4, :], [], [RWD[i]])
            XN2 = sb(p2, "XN2", [128, D], BF16)
            GPL = sb(p2, "GPL", [128, D])
            dma("sp", GPL[:], gpl_d.partition_broadcast(128), [], [GPL])
            HN = sb(p2, "HN", [128, 8, 256], BF16)
            UT = [sb(p2, "UT%d" % i, [128, 256], BF16) for i in range(2)]
            RL = [sb(p2, "RL0", [128, 256])] * 2
            SS2 = sb(p2, "SS2", [128, 8])

            def mlp_block(rows, ntl, xview, rxs, out_ap_fn):
                n = rows * ntl
                for j in range(ntl):
                    xv = xview(j)
                    act(XN2[0:rows, :], xv, AF.Square, [rxs[j]], [XN2, SS2], accum=SS2[0:rows, 0:1])
                    rsqrt_small(SS2[0:rows, 1:2], SS2[0:rows, 0:1], 1.0 / D, [SS2], [SS2], SS2[0:rows, 2:3])
                    ts("dve", XN2[0:rows, :], xv, SS2[0:rows, 1:2], None, ALU.mult, None, [rxs[j], SS2], [XN2])
                    for k in range(8):
                        tr(TPB[:, k * 128:k * 128 + rows], XN2[0:rows, k * 128:(k + 1) * 128], IDB[0:rows, 0:rows],
                           [XN2, IDB], [TPB])
                    tt("dve", HN[:, :, j * rows:(j + 1) * rows],
                       TPB[:].rearrange("p (k t) -> p k t", k=8)[:, :, 0:rows],
                       bc3(GPREMLP[:], 8, rows), ALU.mult, [TPB, GPREMLP], [HN])
                acc = [[BK[2 + 2 * j + eh] for eh in range(2)] for j in range(ntl)]
                for f in range(32):
                    bk = BK[f % 2]
                    for k in range(8):
                        mm(bk[:, 0:n], WUP[:, k, f * 128:(f + 1) * 128], HN[:, k, 0:n], k == 0, k == 7,
                           [HN, RWU[f // 4]], [bk])
                    rl, ut = RL[f % 2], UT[f % 2]
                    act(rl[:, 0:n], bk[:, 0:n], AF.Relu, [bk], [rl])
                    tt("dve" if f % 2 == 0 else "pool", ut[:, 0:n], rl[:, 0:n], rl[:, 0:n], ALU.mult, [rl], [ut])
                    for j in range(ntl):
                        for eh in range(2):
                            mm(acc[j][eh][0:rows, :], ut[:, j * rows:(j + 1) * rows],
                               WDN[:, f, eh * 512:(eh + 1) * 512], f == 0, f == 31, [ut, RWD[f // 4]], [acc[j][eh]])
                for j in range(ntl):
                    xv = xview(j)
                    for eh in range(2):
                        act(XN2[0:rows, eh * 512:(eh + 1) * 512], acc[j][eh][0:rows, :], AF.Square, [acc[j][eh]],
                            [XN2, SS2], accum=SS2[0:rows, 3 + eh:4 + eh])
                    tt("dve", SS2[0:rows, 5:6], SS2[0:rows, 3:4], SS2[0:rows, 4:5], ALU.add, [SS2], [SS2])
                    rsqrt_small(SS2[0:rows, 6:7], SS2[0:rows, 5:6], 1.0 / D, [SS2], [SS2], SS2[0:rows, 7:8])
                    for eh in range(2):
                        sl = slice(eh * 512, (eh + 1) * 512)
                        stt("dve", acc[j][eh][0:rows, :], acc[j][eh][0:rows, :], SS2[0:rows, 6:7], GPL[0:rows, sl],
                            ALU.mult, ALU.mult, [acc[j][eh], SS2, GPL], [acc[j][eh]])
                        tt("dve", xv[:, sl], xv[:, sl], acc[j][eh][0:rows, :], ALU.add, [rxs[j], acc[j][eh]], [rxs[j]])
                    dma("sp", out_ap_fn(j), xv, [rxs[j]], [])

            for blk in range(NT // 2 if KPH2 else 0):
                mlp_block(128, 2, lambda j, b=blk: Xt[:, 2 * b + j, :], [RX[2 * blk], RX[2 * blk + 1]],
                          lambda j, b=blk: y_d[(2 * b + j) * 128:(2 * b + j + 1) * 128, :])


            R_ = NS
            for f in range(32 if KPH2 else 0):
                bk = BK[f % 2]
                for k in range(8):
                    mm(bk[:, 0:R_], WUP[:, k, f * 128:(f + 1) * 128], HNS[:, k, :], k == 0, k == 7, [HNS, RWU[f // 4]], [bk])
                rl, ut = RL[f % 2], UT[f % 2]
                act(rl[:, 0:R_], bk[:, 0:R_], AF.Relu, [bk], [rl])
                tt("dve", ut[:, 0:R_], rl[:, 0:R_], rl[:, 0:R_], ALU.mult, [rl], [ut])
                for eh in range(2):
                    mm(BK[2 + eh][0:R_, :], ut[:, 0:R_], WDN[:, f, eh * 512:(eh + 1) * 512], f == 0, f == 31,
                       [ut, RWD[f // 4]], [BK[2 + eh]])
            if KPH2:
                for eh in range(2):
                    act(XN2[0:R_, eh * 512:(eh + 1) * 512], BK[2 + eh][0:R_, :], AF.Square, [BK[2 + eh]], [XN2, SS2],
                        accum=SS2[0:R_, 3 + eh:4 + eh])
                tt("dve", SS2[0:R_, 5:6], SS2[0:R_, 3:4], SS2[0:R_, 4:5], ALU.add, [SS2], [SS2])
                rsqrt_small(SS2[0:R_, 6:7], SS2[0:R_, 5:6], 1.0 / D, [SS2], [SS2], SS2[0:R_, 7:8])
                YSB = XN2.t.bitcast(F32)
                for eh in range(2):
                    sl = slice(eh * 512, (eh + 1) * 512)
                    stt("dve", BK[2 + eh][0:R_, :], BK[2 + eh][0:R_, :], SS2[0:R_, 6:7], GPL[0:R_, sl], ALU.mult, ALU.mult,
                        [BK[2 + eh], SS2, GPL], [BK[2 + eh]])
                    for j in range(4):
                        tr(BK[4 + eh][0:R_, j * 128:(j + 1) * 128], XS1T[:, 4 * eh + j, :], IDF2[:], [XS1T, IDF2],
                           [BK[4 + eh]])
                    cp("act", YSB[0:R_, :], BK[4 + eh][0:R_, :], [BK[4 + eh]], [XN2])
                    tt("dve", YSB[0:R_, :], YSB[0:R_, :], BK[2 + eh][0:R_, :], ALU.add, [XN2, BK[2 + eh]], [XN2])
                    dma("sp", ys_d[:, sl], YSB[0:R_, :], [XN2], [])

        n_ins = P.finalize(top)
    return nc, n_ins


_CACHE = {}


def kernel(x_prompt, x_sample, state_gdn_conv, state_gdn_S, state_mlstm_C, state_mlstm_n, state_mlstm_m,
           norm_pre_mix, w_in, conv_w, a_log, dt_bias, gdn_norm_g, b_igate, b_fgate, mlstm_norm_g, w_out,
           norm_post_mix, norm_pre_mlp, w_up, w_down, norm_post_mlp):
    f = lambda a: np.ascontiguousarray(np.asarray(a, dtype=np.float32))
    if "nc" not in _CACHE:
        _CACHE["nc"] = build_program()
    nc, _ = _CACHE["nc"]
    consts = make_consts()
    small = np.concatenate([f(a_log)[0], f(dt_bias)[0], f(b_igate)[0], f(b_fgate)[0]])[None, :]
    shared = {
        "w_in": f(w_in)[0], "w_out": f(w_out)[0], "w_up": f(w_up)[0], "w_down": f(w_down)[0],
        "consts": consts,
        "gpre_fm": f(f(norm_pre_mix)[0].reshape(8, 128).T),
        "gpremlp_fm": f(f(norm_pre_mlp)[0].reshape(8, 128).T),
        "cw_fm": f(f(conv_w)[0].reshape(4, 12, 128).transpose(2, 0, 1).reshape(128, 48)),
        "gpostmix": f(norm_post_mix)[0][None, :], "gpostmlp": f(norm_post_mlp)[0][None, :],
        "small": f(small), "gdn_norm_g": f(gdn_norm_g)[0][None, :], "mlstm_norm_g": f(mlstm_norm_g)[0][None, :],
    }
    xp, xs = f(x_prompt), f(x_sample)
    in_maps = []
    for c in range(NCORES):
        r = slice(c * NS, (c + 1) * NS)
        m = dict(shared)
        m.update({
            "x": xp[c], "xs": xs[r, 0, :],
            "sconv": f(state_gdn_conv)[0, r], "sS": f(state_gdn_S)[0, r], "sC": f(state_mlstm_C)[0, r],
            "sn": f(state_mlstm_n)[0, r].reshape(NS, 256), "sm": f(state_mlstm_m)[0, r],
        })
        in_maps.append(m)
    res = run_bass_kernel_spmd(nc, in_maps, core_ids=list(range(NCORES)))
    R = res.results
    g = lambda k: np.stack([np.asarray(R[c][k], dtype=np.float32) for c in range(NCORES)])
    gc = lambda k: np.concatenate([np.asarray(R[c][k], dtype=np.float32) for c in range(NCORES)], axis=0)
    y_prompt = g("y")
    y_sample = gc("ys")[:, None, :]
    p_conv = g("pconv")[None]
    p_S = g("pS")[None]
    p_C = g("pC")[None]
    p_n = g("pn")[None]
    p_m = g("pm").reshape(NCORES, 4)[None]
    s_conv = gc("oconv")[None]
    s_S = gc("oS")[None]
    s_C = gc("oC")[None]
    s_n = gc("on").reshape(NCORES * NS, 4, 64)[None]
    s_m = gc("om")[None]
    return (y_prompt, y_sample, p_conv, p_S, p_C, p_n, p_m, s_conv, s_S, s_C, s_n, s_m)
```

```python
from contextlib import ExitStack
import numpy as np
import concourse.bass as bass
import concourse.mybir as mybir
from concourse.bass_utils import run_bass_kernel_spmd

F32 = mybir.dt.float32
BF16 = mybir.dt.bfloat16
ALU = mybir.AluOpType
AF = mybir.ActivationFunctionType
AX = mybir.AxisListType

NCORES = 8
T = 2048
NT = T // 128
D = 1024
DFF = 4096
NS = 16
EPS = 1e-6
NEG = -30000.0
import os
KNT = int(os.environ.get('KNT', NT))
KPH2 = int(os.environ.get('KPH2', 1))
KSTAGE = int(os.environ.get('KSTAGE', 99))
KSUB = int(os.environ.get('KSUB', 99))


class Res:
    __slots__ = ("name", "w", "rd")

    def __init__(self, name):
        self.name = name
        self.w = None
        self.rd = []


class Op:
    __slots__ = ("eng", "fn", "reads", "writes", "dma", "deps", "signal", "cnt", "sem", "waits")

    def __init__(self, eng, fn, reads, writes, dma):
        self.eng = eng
        self.fn = fn
        self.reads = reads
        self.writes = writes
        self.dma = dma
        self.deps = []
        self.signal = False
        self.cnt = 0
        self.sem = None
        self.waits = []


def _res(lst):
    out = []
    for x in lst:
        if x is None:
            continue
        if isinstance(x, Res):
            out.append(x)
        elif isinstance(x, (list, tuple)):
            out.extend(_res(x))
        else:
            out.append(x.r)
    return out


class Prog:
    ENGS = ("pe", "act", "dve", "pool", "sp")

    def __init__(self, nc, n_dma_sems=56):
        self.nc = nc
        self.ops = []
        self.n_dma_sems = n_dma_sems
        self.engobj = {"pe": nc.tensor, "act": nc.scalar, "dve": nc.vector,
                       "pool": nc.gpsimd, "sp": nc.sync}

    def op(self, eng, fn, r=(), w=()):
        self.ops.append(Op(eng, fn, _res(r), _res(w), False))

    def dma(self, eng, fn, r=(), w=()):
        self.ops.append(Op(eng, fn, _res(r), _res(w), True))

    def barrier(self, allres):
        for e in ("pe", "act", "dve", "pool", "sp"):
            self.ops.append(Op(e, (lambda en: en.nop(nofuse=True)), [], _res(allres), False))

    def finalize(self, stack):
        nc = self.nc
        ops = self.ops
        dma_slot_last = [None] * self.n_dma_sems
        dma_i = 0
        for i, o in enumerate(ops):
            deps = set()
            for r in o.reads:
                if r.w is not None:
                    deps.add(r.w)
            for r in o.writes:
                if r.w is not None:
                    deps.add(r.w)
                for j in r.rd:
                    deps.add(j)
            if o.dma:
                slot = dma_i % self.n_dma_sems
                dma_i += 1
                o.sem = slot
                if dma_slot_last[slot] is not None:
                    deps.add(dma_slot_last[slot])
                dma_slot_last[slot] = i
            deps.discard(i)
            o.deps = sorted(deps)
            for r in o.reads:
                r.rd.append(i)
            for r in o.writes:
                r.w = i
                r.rd = []
            for j in o.deps:
                pj = ops[j]
                if pj.dma:
                    continue
                if pj.eng == "pe" and o.eng == "pe" and not o.dma:
                    continue
                pj.signal = True
        cnt = {e: 0 for e in self.ENGS}
        dcnt = [0] * self.n_dma_sems
        for o in ops:
            if o.dma:
                dcnt[o.sem] += 16
                o.cnt = dcnt[o.sem]
            elif o.signal:
                cnt[o.eng] += 1
                o.cnt = cnt[o.eng]
        seen = {e: {} for e in self.ENGS}
        for o in ops:
            need = {}
            for j in o.deps:
                pj = ops[j]
                if pj.dma:
                    key = ("d", pj.sem)
                else:
                    if pj.eng == "pe" and o.eng == "pe" and not o.dma:
                        continue
                    key = ("e", pj.eng)
                if pj.cnt > need.get(key, 0):
                    need[key] = pj.cnt
            s = seen[o.eng]
            for key, v in need.items():
                if s.get(key, 0) >= v:
                    continue
                s[key] = v
                o.waits.append((key, v))
        final_waits = [(("d", k), dcnt[k]) for k in range(self.n_dma_sems) if dcnt[k] > 0]
        final_waits += [(("e", e), cnt[e]) for e in self.ENGS if cnt[e] > 0 and e != "sp"]
        esem = {e: stack.enter_context(nc.semaphore("s_" + e)) for e in self.ENGS}
        dsem = [stack.enter_context(nc.semaphore("d_%d" % k)) for k in range(self.n_dma_sems)]

        def semof(key):
            return dsem[key[1]] if key[0] == "d" else esem[key[1]]

        n_ins = 0
        for o in ops:
            e = self.engobj[o.eng]
            for key, v in o.waits:
                e.wait_ge(semof(key), v)
                n_ins += 1
            ins = o.fn(e)
            n_ins += 1
            if o.dma:
                ins.then_inc(dsem[o.sem], 16)
            elif o.signal:
                ins.then_inc(esem[o.eng], 1)
        sp = self.engobj["sp"]
        for key, v in final_waits:
            if seen["sp"].get(key, 0) >= v:
                continue
            sp.wait_ge(semof(key), v)
        return n_ins


class View:
    __slots__ = ("t", "r")

    def __init__(self, ap, r):
        self.t = ap
        self.r = r

    def __getitem__(self, k):
        return self.t[k]


class Tl:
    __slots__ = ("t", "r")

    def __init__(self, t, name):
        self.t = t
        self.r = Res(name)

    def __getitem__(self, k):
        return self.t[k]


C_IDF, C_TRI, C_ONES, C_SEL127, C_MST, C_MIT, C_MI, C_SELR = (
    0, 128, 256, 384, 512, 640, 768, 896)
NCONST = 1408


def make_consts():
    c = np.zeros((128, NCONST), np.float32)
    s = np.arange(128)[:, None]
    f = np.arange(128)[None, :]
    c[:, C_IDF:C_IDF + 128] = (s == f)
    c[:, C_TRI:C_TRI + 128] = (s <= f)
    c[:, C_ONES:C_ONES + 128] = 1.0
    c[:, C_SEL127:C_SEL127 + 128] = (s == 127)
    mst = np.where(s < f, 0.0, NEG)
    mit = np.where(s <= f, 0.0, NEG)
    mi = np.where(f <= s, 0.0, NEG)
    c[:, C_MST:C_MST + 128] = mst
    c[:, C_MIT:C_MIT + 128] = mit
    c[:, C_MI:C_MI + 128] = mi
    for h in range(4):
        c[h, C_SELR + h * 128:C_SELR + (h + 1) * 128] = 1.0
    return c


W_QKV, W_MQK, W_GZ, W_MV, W_MO, W_GATE = 0, 1536, 2048, 2560, 3072, 3584
WIN_MOVES = [(0, 0, 1536), (1536, 2056, 512), (2048, 1536, 512), (2560, 2568, 512),
             (3072, 3080, 512), (3584, 2048, 8), (3592, 3596, 4), (3596, 3592, 4)]


def build_program():
    nc = bass.Bass("TRN2", target_bir_lowering=False)
    P = Prog(nc)

    def din(name, shape):
        return nc.dram_tensor(name, list(shape), F32, kind="ExternalInput").ap()

    def dout(name, shape):
        return nc.dram_tensor(name, list(shape), F32, kind="ExternalOutput").ap()

    x_d = din("x", [T, D])
    xs_d = din("xs", [NS, D])
    sconv_d = din("sconv", [NS, 3, 1536])
    sS_d = din("sS", [NS, 4, 128, 128])
    sC_d = din("sC", [NS, 4, 64, 128])
    sn_d = din("sn", [NS, 256])
    sm_d = din("sm", [NS, 4])
    win_d = din("w_in", [D, 3600])
    wout_d = din("w_out", [D, D])
    wup_d = din("w_up", [D, DFF])
    wdn_d = din("w_down", [DFF, D])
    consts_d = din("consts", [128, NCONST])
    gpre_d = din("gpre_fm", [128, 8])
    gpremlp_d = din("gpremlp_fm", [128, 8])
    cw_d = din("cw_fm", [128, 48])
    gpm_d = din("gpostmix", [1, D])
    gpl_d = din("gpostmlp", [1, D])
    small_d = din("small", [1, 16])
    gng_d = din("gdn_norm_g", [1, 128])
    mng_d = din("mlstm_norm_g", [1, 128])

    y_d = dout("y", [T, D])
    ys_d = dout("ys", [NS, D])
    pconv_d = dout("pconv", [3, 1536])
    pS_d = dout("pS", [4, 128, 128])
    pC_d = dout("pC", [4, 64, 128])
    pn_d = dout("pn", [4, 64])
    pm_d = dout("pm", [1, 4])
    oconv_d = dout("oconv", [NS, 3, 1536])
    oS_d = dout("oS", [NS, 4, 128, 128])
    oC_d = dout("oC", [NS, 4, 64, 128])
    on_d = dout("on", [NS, 256])
    om_d = dout("om", [NS, 4])

    def mm(out, lhsT, rhs, start, stop, r, w):
        P.op("pe", lambda e, o=out, l=lhsT, rr=rhs, s=start, t=stop:
             e.matmul(o, lhsT=l, rhs=rr, start=s, stop=t), r, w)

    def tr(out, in_, ident, r, w):
        P.op("pe", lambda e, o=out, i=in_, d=ident: e.transpose(o, i, d), r, w)

    def tt(eng, out, in0, in1, op, r, w):
        P.op(eng, lambda e, o=out, a=in0, b=in1, p=op: e.tensor_tensor(out=o, in0=a, in1=b, op=p), r, w)

    def ts(eng, out, in0, s1, s2, op0, op1, r, w, accum=None):
        if op1 is None:
            P.op(eng, lambda e, o=out, a=in0, x=s1, p0=op0:
                 e.tensor_scalar(out=o, in0=a, scalar1=x, scalar2=None, op0=p0), r, w)
        elif accum is None:
            P.op(eng, lambda e, o=out, a=in0, x=s1, y=s2, p0=op0, p1=op1:
                 e.tensor_scalar(out=o, in0=a, scalar1=x, scalar2=y, op0=p0, op1=p1), r, w)
        else:
            P.op(eng, lambda e, o=out, a=in0, x=s1, y=s2, p0=op0, p1=op1, ac=accum:
                 e.tensor_scalar(out=o, in0=a, scalar1=x, scalar2=y, op0=p0, op1=p1, accum_out=ac), r, w)

    def stt(eng, out, in0, scalar, in1, op0, op1, r, w):
        P.op(eng, lambda e, o=out, a=in0, s=scalar, b=in1, p0=op0, p1=op1:
             e.scalar_tensor_tensor(out=o, in0=a, scalar=s, in1=b, op0=p0, op1=p1), r, w)

    def act(out, in_, func, r, w, bias=None, scale=1.0, accum=None):
        kw = {}
        if bias is not None:
            kw["bias"] = bias
        if accum is not None:
            kw["accum_out"] = accum
        P.op("act", lambda e, o=out, i=in_, f=func, s=scale, k=kw:
             e.activation(out=o, in_=i, func=f, scale=s, **k), r, w)

    def cp(eng, out, in_, r, w):
        if eng == "act":
            act(out, in_, AF.Copy, r, w)
        else:
            P.op(eng, lambda e, o=out, i=in_: e.tensor_copy(out=o, in_=i), r, w)

    def red(eng, out, in_, op, r, w):
        P.op(eng, lambda e, o=out, i=in_, p=op: e.tensor_reduce(out=o, in_=i, axis=AX.X, op=p), r, w)

    def memset(eng, ap, val, w):
        P.op(eng, lambda e, a=ap, v=val: e.memset(a, v), [], w)

    def dma(q, out, in_, r, w, slow=False):
        if slow:
            P.dma(q, lambda e, o=out, i=in_: e.dma_start(out=o, in_=i, allow_slow_non_contiguous=True), r, w)
        else:
            P.dma(q, lambda e, o=out, i=in_: e.dma_start(out=o, in_=i), r, w)

    def rsqrt_small(out, in_, scale, r_, w_, tmp):
        ts("dve", tmp, in_, scale, EPS, ALU.mult, ALU.add, r_, [w_[0]])
        act(tmp, tmp, AF.Ln, [w_[0]], [w_[0]])
        act(out, tmp, AF.Exp, [w_[0]], w_, scale=-0.5)

    def bc3(ap, n_mid, n_in):
        return ap.unsqueeze(2).to_broadcast([ap.shape[0], n_mid, n_in])

    def bcm(ap, n_mid):
        return ap.unsqueeze(1).to_broadcast([ap.shape[0], n_mid, ap.shape[1]])

    with ExitStack() as top:
        def sb(stack, name, shape, dt=F32):
            return Tl(stack.enter_context(nc.sbuf_tensor("sb_" + name, list(shape), dt)), name)

        def ps(stack, name, shape, dt=F32):
            return Tl(stack.enter_context(nc.psum_tensor("ps_" + name, list(shape), dt)), name)

        Xt = top.enter_context(nc.sbuf_tensor("sb_X", [128, NT, D], F32))
        RX = [Res("X%d" % t) for t in range(NT)]
        XS1T = sb(top, "XS1T", [128, 8, NS])
        HNS = sb(top, "HNS", [128, 8, NS], BF16)
        IDF2 = sb(top, "IDF2", [128, 128])
        IDB = sb(top, "IDB", [128, 128], BF16)
        GPREMLP = sb(top, "GPREMLP", [128, 8])
        BK = [ps(top, "B%d" % i, [128, 512]) for i in range(7)]
        TPB = ps(top, "TPB", [128, 1024], BF16)

        dma("sp", GPREMLP[:], gpremlp_d, [], [GPREMLP])

        all_res_p1 = []

        with ExitStack() as p1:
            cur = [p1]

            def s1(name, shape, dt=F32):
                tl = sb(cur[0], name, shape, dt)
                all_res_p1.append(tl.r)
                return tl

            WIN = p1.enter_context(nc.sbuf_tensor("sb_WIN", [128, 8, 3600], BF16))
            RWIN = [Res("WIN%d" % i) for i in range(len(WIN_MOVES))]
            WOUT = s1("WOUT", [128, 8, D], BF16)
            CONST = s1("CONST", [128, NCONST])
            GPRE = s1("GPRE", [128, 8])
            CW = s1("CW", [128, 4, 12])
            GPM = s1("GPM", [128, D])
            SMB = s1("SMB", [128, 16])
            GNG = s1("GNG", [128, 128])
            MNG = s1("MNG", [128, 128])
            all_res_p1.extend(RWIN)

            win_v = win_d.rearrange("(k p) c -> p k c", p=128)
            for i, (dst, src, n) in enumerate(WIN_MOVES):
                dma("pool", WIN[:, :, dst:dst + n], win_v[:, :, src:src + n], [], [RWIN[i]])
            dma("pool", WOUT[:], wout_d.rearrange("(k p) c -> p k c", p=128), [], [WOUT])
            dma("sp", CONST[:], consts_d, [], [CONST])
            dma("sp", GPRE[:], gpre_d, [], [GPRE])
            dma("sp", CW[:], cw_d.rearrange("p (j c) -> p j c", j=4), [], [CW])
            dma("sp", GPM[:], gpm_d.partition_broadcast(128), [], [GPM])
            dma("sp", SMB[:], small_d.partition_broadcast(128), [], [SMB])
            dma("sp", GNG[:], gng_d.partition_broadcast(128), [], [GNG])
            dma("sp", MNG[:], mng_d.partition_broadcast(128), [], [MNG])

            IDF = CONST[:, C_IDF:C_IDF + 128]
            TRI = CONST[:, C_TRI:C_TRI + 128]
            ONES = CONST[:, C_ONES:C_ONES + 128]
            SEL127 = CONST[:, C_SEL127:C_SEL127 + 128]
            MST = CONST[:, C_MST:C_MST + 128]
            MIT = CONST[:, C_MIT:C_MIT + 128]
            MI = CONST[:, C_MI:C_MI + 128]
            SELR = CONST[0:4, C_SELR:C_SELR + 512]
            ONES4 = CONST[0:4, C_ONES:C_ONES + 128]

            def rwin(c0, c1):
                out = []
                for i, (dst, src, n) in enumerate(WIN_MOVES):
                    if dst < c1 and c0 < dst + n:
                        out.append(RWIN[i])
                return out

            cp("dve", IDB[:], IDF, [CONST], [IDB])
            cp("pool", IDF2[:], IDF, [CONST], [IDF2])

            NEGC = s1("NEGC", [128, 12])
            SGN = s1("SGN", [128, 12])
            BIA = s1("BIA", [128, 12])
            memset("pool", NEGC[:], -1.0, [NEGC])
            act(NEGC[:, 4:8], SMB[:, 0:4], AF.Exp, [SMB, NEGC], [NEGC])
            ts("dve", NEGC[:, 4:8], NEGC[:, 4:8], -1.0, None, ALU.mult, None, [NEGC], [NEGC])
            memset("pool", SGN[:], -1.0, [SGN])
            memset("pool", SGN[:, 4:8], 1.0, [SGN])
            memset("pool", BIA[:], 0.0, [BIA])
            cp("dve", BIA[:, 4:8], SMB[:, 4:8], [SMB, BIA], [BIA])
            ts("dve", BIA[:, 8:12], SMB[:, 12:16], -1.0, None, ALU.mult, None, [SMB, BIA], [BIA])

            p1a = ExitStack()
            cur[0] = p1a
            XN = s1("XN", [128, D], BF16)
            HT = s1("HT", [128, 8, 128], BF16)
            PCC = [s1("PCC%d" % i, [128, 131]) for i in range(2)]
            ACC = [s1("ACC%d" % i, [128, 128]) for i in range(2)]
            TAIL = s1("TAIL", [128, 12, 3])
            QKF = s1("QKF", [128, 8, 128])
            GQT = s1("GQT", [128, 4, 128], BF16)
            GKT = s1("GKT", [128, 4, 128], BF16)
            GVT = s1("GVT", [128, 4, 128], BF16)
            MQT = s1("MQT", [64, 4, 128], BF16)
            MKT = s1("MKT", [64, 4, 128], BF16)
            ZSG = s1("ZSG", [128, 512])
            MOSG = s1("MOSG", [128, 512])
            MVt = s1("MVt", [128, 4, 128], BF16)
            GT = s1("GT", [128, 16])
            ARG = s1("ARG", [128, 12])
            GL = s1("GL", [128, 12])
            BETA = s1("BETA", [128, 4])
            IG = s1("IG", [128, 4])
            GF = s1("GF", [128, 8])
            COLS = s1("COLS", [128, 20])
            RT = s1("RT", [4, 5, 128])
            BD = s1("BD", [4, 4, 128])
            QKD = s1("QKD", [128, 4, 128], BF16)
            MB = [s1("MB%d" % i, [128, 512]) for i in range(2)]
            LB = [s1("LB%d" % i, [128, 512]) for i in range(2)]
            RR = s1("RR", [128, 512])
            BINV = s1("BINV", [128, 4, 128], BF16)
            GKt = s1("GKt", [128, 4, 128], BF16)
            XK = s1("XK", [128, 4, 128], BF16)
            KEND = s1("KEND", [128, 4, 128], BF16)
            GVb = s1("GVb", [128, 4, 128], BF16)
            SC4 = [s1("SC4_%d" % i, [128, 8]) for i in range(6)]
            LASTG = s1("LASTG", [128, 8])
            WKT = s1("WKT", [128, 4, 128], BF16)
            S32 = s1("S32", [128, 4, 128])
            SBh = s1("SBh", [128, 4, 128], BF16)
            U = s1("U", [128, 4, 128], BF16)
            TMP = [s1("TMP%d" % i, [128, 512]) for i in range(2)]
            WV = LB[1]
            PK = s1("PK", [128, 4, 64], BF16)
            EXPQ = TMP[1]
            EXPA = MB[0]
            MIX = XN
            MIXT = HT
            MKt = s1("MKt", [128, 4, 64], BF16)
            PQKb = s1("PQKb", [128, 4, 128], BF16)
            PQKT = s1("PQKT", [128, 4, 128], BF16)
            SG4 = [s1("SG4_%d" % i, [128, 8]) for i in range(6)]
            C32 = s1("C32", [64, 4, 128])
            CBh = s1("CBh", [64, 4, 128], BF16)
            N32 = s1("N32", [64, 4])
            NBh = s1("NBh", [64, 4, 2], BF16)
            MBC = s1("MBC", [128, 4])
            DMAX = s1("DMAX", [128, 4])
            T12 = s1("T12", [128, 12])
            WLC = s1("WLC", [64, 4])
            ONEB = s1("ONEB", [128, 2], BF16)
            for b in BK:
                all_res_p1.append(b.r)
            all_res_p1.append(TPB.r)

            memset("pool", TAIL[:], 0.0, [TAIL])
            memset("pool", S32[:], 0.0, [S32])
            memset("pool", SBh[:], 0.0, [SBh])
            memset("pool", C32[:], 0.0, [C32])
            memset("pool", CBh[:], 0.0, [CBh])
            memset("pool", N32[:], 0.0, [N32])
            memset("pool", NBh[:], 0.0, [NBh])
            memset("pool", MBC[:], 0.0, [MBC])
            memset("pool", ONEB[:], 1.0, [ONEB])

            def headnorm_gate(src, gate, dst, t0, ss, rs):
                tt("pool", t0[:], src[:], src[:], ALU.mult, [src], [t0])
                red("dve", ss[:, 0:4], t0[:].rearrange("p (h e) -> p h e", h=4), ALU.add, [t0], [ss])
                rsqrt_small(rs[:, 0:4], ss[:, 0:4], 1.0 / 128, [ss], [rs], rs[:, 4:8])
                tt("dve", t0[:].rearrange("p (h e) -> p h e", h=4), src[:].rearrange("p (h e) -> p h e", h=4),
                   bc3(rs[:, 0:4], 4, 128), ALU.mult, [src, rs], [t0])
                tt("dve", dst, t0[:], gate[:], ALU.mult, [t0, gate], [MIX])

            for t in range(KNT):
                Xv = Xt[:, t, :]
                dma("sp", Xv, x_d[t * 128:(t + 1) * 128, :], [], [RX[t]])
                ss, rs = SC4[0], SC4[1]
                act(XN[:], Xv, AF.Square, [RX[t]], [XN, ss], accum=ss[:, 0:1])
                rsqrt_small(rs[:, 0:1], ss[:, 0:1], 1.0 / D, [ss], [rs], rs[:, 1:2])
                ts("dve", XN[:], Xv, rs[:, 0:1], None, ALU.mult, None, [RX[t], rs], [XN])
                for k in range(8):
                    tr(TPB[:, k * 128:(k + 1) * 128], XN[:, k * 128:(k + 1) * 128], IDB[:], [XN, IDB], [TPB])
                tt("dve", HT[:], TPB[:].rearrange("p (k t) -> p k t", k=8), bc3(GPRE[:], 8, 128), ALU.mult,
                   [TPB, GPRE], [HT])

                for ch in range(12):
                    bk = BK[ch % 2]
                    c0 = ch * 128
                    for k in range(8):
                        mm(bk[:, 0:128], WIN[:, k, c0:c0 + 128], HT[:, k, :], k == 0, k == 7,
                           [HT] + rwin(c0, c0 + 128), [bk])
                    if ch < 12:
                        pc = PCC[ch % 2]
                        ac = ACC[ch % 2]
                        ce = "dve"
                        cp("act", pc[:, 3:131], bk[:, 0:128], [bk], [pc])
                        cp("pool", pc[:, 0:3], TAIL[:, ch, :], [TAIL], [pc])
                        ts(ce, ac[:], pc[:, 0:128], CW[:, 0, ch:ch + 1], None, ALU.mult, None, [pc, CW], [ac])
                        for j in range(1, 4):
                            stt(ce, ac[:], pc[:, j:j + 128], CW[:, j, ch:ch + 1], ac[:], ALU.mult, ALU.add,
                                [pc, CW, ac], [ac])
                        cp("pool", TAIL[:, ch, :], pc[:, 128:131], [pc], [TAIL])
                        if ch < 8:
                            act(QKF[:, ch, :], ac[:], AF.Silu, [ac], [QKF])
                        else:
                            act(GVT[:, ch - 8, :], ac[:], AF.Silu, [ac], [GVT])
                for hh in range(8):
                    bk = BK[hh % 2]
                    c0 = W_MQK + hh * 64
                    for k in range(8):
                        mm(bk[0:64, 0:128], WIN[:, k, c0:c0 + 64], HT[:, k, :], k == 0, k == 7,
                           [HT] + rwin(c0, c0 + 64), [bk])
                    if hh < 4:
                        cp("act", MQT[:, hh, :], bk[0:64, 0:128], [bk], [MQT])
                    else:
                        act(MKT[:, hh - 4, :], bk[0:64, 0:128], AF.Copy, [bk], [MKT], scale=0.125)
                for hf in range(2):
                    act(TMP[hf][:], QKF[:, hf * 4:(hf + 1) * 4, :].rearrange("p a b -> p (a b)"), AF.Square,
                        [QKF], [TMP[hf]])
                    mm(BK[2 + hf][:], ONES, TMP[hf][:], True, True, [CONST, TMP[hf]], [BK[2 + hf]])
                for hf in range(2):
                    ts("dve", TMP[hf][:], BK[2 + hf][:], 1.0, EPS, ALU.mult, ALU.add, [BK[2 + hf]], [TMP[hf]])
                    act(TMP[hf][:], TMP[hf][:], AF.Ln, [TMP[hf]], [TMP[hf]])
                    act(TMP[hf][:], TMP[hf][:], AF.Exp, [TMP[hf]], [TMP[hf]], scale=-0.5)
                stt("dve", GQT[:].rearrange("p a b -> p (a b)"), QKF[:, 0:4, :].rearrange("p a b -> p (a b)"),
                    128.0 ** -0.5, TMP[0][:], ALU.mult, ALU.mult, [QKF, TMP[0]], [GQT])
                tt("pool", GKT[:].rearrange("p a b -> p (a b)"), QKF[:, 4:8, :].rearrange("p a b -> p (a b)"),
                   TMP[1][:], ALU.mult, [QKF, TMP[1]], [GKT])

                def tok_proj(bk, c0, n):
                    for k in range(8):
                        mm(bk[:, 0:n], HT[:, k, :], WIN[:, k, c0:c0 + n], k == 0, k == 7,
                           [HT] + rwin(c0, c0 + n), [bk])
                tok_proj(BK[0], W_GZ, 512)
                act(TMP[0][:], BK[0][:], AF.Silu, [BK[0]], [TMP[0]])
                tt("pool", ZSG[:].rearrange("p (h e) -> p h e", h=4), TMP[0][:].rearrange("p (h e) -> p h e", h=4),
                   bcm(GNG[:], 4), ALU.mult, [TMP[0], GNG], [ZSG])
                tok_proj(BK[1], W_MV, 512)
                cp("act", MVt[:].rearrange("p h e -> p (h e)"), BK[1][:], [BK[1]], [MVt])
                tok_proj(BK[0], W_MO, 512)
                act(TMP[0][:], BK[0][:], AF.Exp, [BK[0]], [TMP[0]], scale=-1.0)
                ts("dve", TMP[0][:], TMP[0][:], 1.0, None, ALU.add, None, [TMP[0]], [TMP[0]])
                P.op("dve", lambda e: e.reciprocal(out=TMP[0][:], in_=TMP[0][:]), [TMP[0]], [TMP[0]])
                tt("pool", MOSG[:].rearrange("p (h e) -> p h e", h=4), TMP[0][:].rearrange("p (h e) -> p h e", h=4),
                   bcm(MNG[:], 4), ALU.mult, [TMP[0], MNG], [MOSG])
                tok_proj(BK[1], W_GATE, 16)
                cp("dve", GT[:], BK[1][:, 0:16], [BK[1]], [GT])
                tt("dve", ARG[:], GT[:, 0:12], SGN[:], ALU.mult, [GT, SGN], [ARG])
                tt("dve", ARG[:], ARG[:], BIA[:], ALU.add, [ARG, BIA], [ARG])
                act(ARG[:], ARG[:], AF.Exp, [ARG], [ARG])
                act(ARG[:], ARG[:], AF.Ln, [ARG], [ARG], bias=1.0)
                tt("dve", GL[:], ARG[:], NEGC[:], ALU.mult, [ARG, NEGC], [GL])
                act(BETA[:], GL[:, 0:4], AF.Exp, [GL], [BETA])
                tt("dve", IG[:], GT[:, 12:16], SMB[:, 8:12], ALU.add, [GT, SMB], [IG])

                mm(BK[2][:, 0:8], TRI, GL[:, 4:12], True, True, [CONST, GL], [BK[2]])
                cp("dve", GF[:], BK[2][:, 0:8], [BK[2]], [GF])
                cp("pool", COLS[:, 0:4], GF[:, 0:4], [GF], [COLS])
                tt("dve", COLS[:, 4:8], GF[:, 0:4], GL[:, 0:4], ALU.add, [GF, GL], [COLS])
                cp("pool", COLS[:, 8:12], GF[:, 4:8], [GF], [COLS])
                tt("dve", COLS[:, 12:16], IG[:], GF[:, 4:8], ALU.subtract, [IG, GF], [COLS])
                ts("dve", COLS[:, 16:20], GF[:, 0:4], -1.0, None, ALU.mult, None, [GF], [COLS])
                for j in range(4):
                    tr(BK[3][0:4, j * 128:(j + 1) * 128], COLS[:, 4 * j:4 * j + 4], IDF, [COLS, CONST], [BK[3]])
                tr(BK[4][0:4, 0:128], COLS[:, 16:20], IDF, [COLS, CONST], [BK[4]])
                cp("dve", RT[:, 0:4, :].rearrange("p a b -> p (a b)"), BK[3][0:4, :], [BK[3]], [RT])
                cp("dve", RT[:, 4, :], BK[4][0:4, 0:128], [BK[4]], [RT])
                SELR3 = SELR.rearrange("p (a b) -> p a b", a=4)
                tt("dve", BD[:], bcm(RT[:, 1, :], 4), SELR3, ALU.mult, [RT, CONST], [BD])
                mm(BK[2][:], ONES4, BD[:].rearrange("p a b -> p (a b)"), True, False, [CONST, BD], [BK[2]])
                mm(BK[2][:], RT[:, 4, :], SELR, False, True, [RT, CONST], [BK[2]])
                tt("dve", EXPA[:].rearrange("p (h c) -> p h c", h=4), BK[2][:].rearrange("p (h c) -> p h c", h=4),
                   bcm(MST, 4), ALU.add, [BK[2], CONST], [EXPA])
                act(EXPA[:], EXPA[:], AF.Exp, [EXPA], [EXPA])
                tt("dve", BD[:], bcm(RT[:, 0, :], 4), SELR3, ALU.mult, [RT, CONST], [BD])
                mm(BK[3][:], ONES4, BD[:].rearrange("p a b -> p (a b)"), True, False, [CONST, BD], [BK[3]])
                mm(BK[3][:], RT[:, 4, :], SELR, False, True, [RT, CONST], [BK[3]])
                tt("dve", EXPQ[:].rearrange("p (h c) -> p h c", h=4), BK[3][:].rearrange("p (h c) -> p h c", h=4),
                   bcm(MIT, 4), ALU.add, [BK[3], CONST], [EXPQ])
                act(EXPQ[:], EXPQ[:], AF.Exp, [EXPQ], [EXPQ])
                for h in range(4):
                    mm(BK[4][:, h * 128:(h + 1) * 128], GKT[:, h, :], GKT[:, h, :], True, True, [GKT], [BK[4]])
                for h in range(4):
                    mm(BK[5][:, h * 128:(h + 1) * 128], GKT[:, h, :], GQT[:, h, :], True, True, [GKT, GQT], [BK[5]])
                tt("dve", MB[0][:], BK[4][:], EXPA[:], ALU.mult, [BK[4], EXPA], [MB[0]])
                tt("dve", QKD[:].rearrange("p h c -> p (h c)"), BK[5][:], EXPQ[:], ALU.mult, [BK[5], EXPQ], [QKD])

                for h in range(4):
                    tr(TPB[:, h * 128:(h + 1) * 128], GKT[:, h, :], IDB[:], [GKT, IDB], [TPB])
                    tr(TPB[:, 512 + h * 128:512 + (h + 1) * 128], GVT[:, h, :], IDB[:], [GVT, IDB], [TPB])
                cp("act", GKt[:].rearrange("p h d -> p (h d)"), TPB[:, 0:512], [TPB], [GKt])
                EG, BEG, EGL, GEND = SC4[0], SC4[1], SC4[2], SC4[3]
                act(EG[:, 0:4], GF[:, 0:4], AF.Exp, [GF], [EG])
                tt("dve", BEG[:, 0:4], EG[:, 0:4], BETA[:], ALU.mult, [EG, BETA], [BEG])
                mm(BK[2][:, 0:8], SEL127, GF[:], True, True, [CONST, GF], [BK[2]])
                cp("dve", LASTG[:], BK[2][:, 0:8], [BK[2]], [LASTG])
                tt("dve", EGL[:, 0:4], LASTG[:, 0:4], GF[:, 0:4], ALU.subtract, [LASTG, GF], [EGL])
                act(EGL[:, 0:4], EGL[:, 0:4], AF.Exp, [EGL], [EGL])
                act(GEND[:, 0:4], LASTG[:, 0:4], AF.Exp, [LASTG], [GEND])
                tt("dve", XK[:], GKt[:], bc3(BEG[:, 0:4], 4, 128), ALU.mult, [GKt, BEG], [XK])
                tt("pool", KEND[:], GKt[:], bc3(EGL[:, 0:4], 4, 128), ALU.mult, [GKt, EGL], [KEND])
                tt("dve", GVb[:], TPB[:, 512:1024].rearrange("p (h e) -> p h e", h=4), bc3(BETA[:], 4, 128),
                   ALU.mult, [TPB, BETA], [GVb])
                def gen_EF():
                    for h in range(4):
                        tr(BK[2][:, h * 128:(h + 1) * 128], MB[0][:, h * 128:(h + 1) * 128], IDF, [MB[0], CONST], [BK[2]])
                    yield
                    cp("act", LB[0][:], BK[2][:], [BK[2]], [LB[0]])
                    yield
                    tt("dve", RR[:].rearrange("p (h c) -> p h c", h=4), bcm(IDF, 4),
                       MB[0][:].rearrange("p (h c) -> p h c", h=4), ALU.subtract, [CONST, MB[0]], [RR])
                    yield
                    NLEV = 6
                    yield
                    for k in range(NLEV):
                        a, b = k % 2, (k + 1) % 2
                        for h in range(4):
                            sl = slice(h * 128, (h + 1) * 128)
                            mm(BK[3][:, sl], MB[a][:, sl], LB[a][:, sl], True, True, [MB[a], LB[a]], [BK[3]])
                        yield
                        if k < NLEV - 1:
                            for h in range(4):
                                sl = slice(h * 128, (h + 1) * 128)
                                mm(BK[4][:, sl], LB[a][:, sl], MB[a][:, sl], True, True, [MB[a], LB[a]], [BK[4]])
                            yield
                        cp("act", LB[b][:], BK[3][:], [BK[3]], [LB[b]])
                        yield
                        if k < NLEV - 1:
                            cp("dve", MB[b][:], BK[4][:], [BK[4]], [MB[b]])
                            yield
                        for h in range(4):
                            sl = slice(h * 128, (h + 1) * 128)
                            mm(BK[5][:, sl], LB[b][:, sl], RR[:, sl], True, True, [LB[b], RR], [BK[5]])
                        yield
                        tt("dve", RR[:], RR[:], BK[5][:], ALU.add, [RR, BK[5]], [RR])
                        yield
                    cp("act", BINV[:].rearrange("p h c -> p (h c)"), RR[:], [RR], [BINV])
                    yield
                    for h in range(4):
                        sl = slice(h * 128, (h + 1) * 128)
                        mm(BK[2][:, sl], BINV[:, h, :], GVb[:, h, :], True, True, [BINV, GVb], [BK[2]])
                        mm(BK[3][:, sl], XK[:, h, :], BINV[:, h, :], True, True, [BINV, XK], [BK[3]])
                    yield
                    cp("act", WV[:], BK[2][:], [BK[2]], [WV])
                    yield
                    cp("dve", WKT[:].rearrange("p h c -> p (h c)"), BK[3][:], [BK[3]], [WKT])
                    yield
                    for h in range(4):
                        sl = slice(h * 128, (h + 1) * 128)
                        mm(BK[4][:, sl], WKT[:, h, :], SBh[:, h, :], True, True, [WKT, SBh], [BK[4]])
                        mm(BK[5][:, sl], GQT[:, h, :], SBh[:, h, :], True, True, [GQT, SBh], [BK[5]])
                    yield
                    tt("dve", U[:].rearrange("p h e -> p (h e)"), WV[:], BK[4][:], ALU.subtract, [WV, BK[4]], [U])
                    yield
                    for h in range(4):
                        sl = slice(h * 128, (h + 1) * 128)
                        mm(BK[2][:, sl], QKD[:, h, :], U[:, h, :], True, True, [QKD, U], [BK[2]])
                        mm(BK[3][:, sl], KEND[:, h, :], U[:, h, :], True, True, [KEND, U], [BK[3]])
                    yield
                    OG = MB[1]
                    yield
                    tt("dve", OG[:].rearrange("p (h e) -> p h e", h=4), BK[5][:].rearrange("p (h e) -> p h e", h=4),
                       bc3(EG[:, 0:4], 4, 128), ALU.mult, [BK[5], EG], [OG])
                    yield
                    tt("dve", OG[:], OG[:], BK[2][:], ALU.add, [OG, BK[2]], [OG])
                    yield
                    for h in range(4):
                        stt("dve", S32[:, h, :], S32[:, h, :], GEND[:, h:h + 1], BK[3][:, h * 128:(h + 1) * 128],
                            ALU.mult, ALU.add, [S32, GEND, BK[3]], [S32])
                    yield
                    cp("act", SBh[:], S32[:], [S32], [SBh])
                    yield
                    headnorm_gate(OG, ZSG, MIX[:, 0:512], LB[0], SC4[4], SC4[5])
                    yield
                def gen_G():
                    tt("dve", BD[:], bcm(RT[:, 3, :], 4), SELR3, ALU.mult, [RT, CONST], [BD])
                    yield
                    mm(BK[6][:], ONES4, BD[:].rearrange("p a b -> p (a b)"), True, False, [CONST, BD], [BK[6]])
                    yield
                    mm(BK[6][:], RT[:, 2, :], SELR, False, True, [RT, CONST], [BK[6]])
                    yield
                    PD = TMP[0]
                    yield
                    tt("dve", PD[:].rearrange("p (h s) -> p h s", h=4), BK[6][:].rearrange("p (h s) -> p h s", h=4),
                       bcm(MI, 4), ALU.add, [BK[6], CONST], [PD])
                    yield
                    red("dve", DMAX[:], PD[:].rearrange("p (h s) -> p h s", h=4), ALU.max, [PD], [DMAX])
                    yield
                    for h in range(4):
                        mm(BK[0][:, h * 128:(h + 1) * 128], MQT[:, h, :], MKT[:, h, :], True, True, [MQT, MKT], [BK[0]])
                    yield
                    for h in range(4):
                        tr(TPB[:, h * 64:(h + 1) * 64], MKT[:, h, :], IDB[0:64, 0:64], [MKT, IDB], [TPB])
                    yield
                    cp("act", MKt[:].rearrange("p h d -> p (h d)"), TPB[:, 0:256], [TPB], [MKt])
                    yield
                    Bv, MT, WP, EMT = SG4[0], SG4[1], SG4[2], SG4[3]
                    yield
                    tt("dve", Bv[:, 0:4], GF[:, 4:8], MBC[:], ALU.add, [GF, MBC], [Bv])
                    yield
                    tt("dve", MT[:, 0:4], Bv[:, 0:4], DMAX[:], ALU.max, [Bv, DMAX], [MT])
                    yield
                    tt("dve", WP[:, 0:4], Bv[:, 0:4], MT[:, 0:4], ALU.subtract, [Bv, MT], [WP])
                    yield
                    act(WP[:, 0:4], WP[:, 0:4], AF.Exp, [WP], [WP])
                    yield
                    tt("dve", PD[:].rearrange("p (h s) -> p h s", h=4), PD[:].rearrange("p (h s) -> p h s", h=4),
                       bc3(MT[:, 0:4], 4, 128), ALU.subtract, [PD, MT], [PD])
                    yield
                    act(PD[:], PD[:], AF.Exp, [PD], [PD])
                    yield
                    tt("dve", PD[:], PD[:], BK[0][:], ALU.mult, [PD, BK[0]], [PD])
                    yield
                    RS = SG4[4]
                    yield
                    red("dve", RS[:, 0:4], PD[:].rearrange("p (h s) -> p h s", h=4), ALU.add, [PD], [RS])
                    yield
                    cp("act", PQKb[:].rearrange("p h s -> p (h s)"), PD[:], [PD], [PQKb])
                    yield
                    for h in range(4):
                        tr(TPB[:, 512 + h * 128:512 + (h + 1) * 128], PQKb[:, h, :], IDB[:], [PQKb, IDB], [TPB])
                    yield
                    cp("act", PQKT[:].rearrange("p h s -> p (h s)"), TPB[:, 512:1024], [TPB], [PQKT])
                    yield
                    for h in range(4):
                        sl = slice(h * 128, (h + 1) * 128)
                        mm(BK[1][:, sl], PQKT[:, h, :], MVt[:, h, :], True, True, [PQKT, MVt], [BK[1]])
                        mm(BK[0][:, sl], MQT[:, h, :], CBh[:, h, :], True, True, [MQT, CBh], [BK[0]])
                        mm(BK[6][:, 2 * h:2 * h + 2], MQT[:, h, :], NBh[:, h, :], True, True, [MQT, NBh], [BK[6]])
                    yield
                    NUM = TMP[0]
                    yield
                    tt("dve", NUM[:].rearrange("p (h e) -> p h e", h=4), BK[0][:].rearrange("p (h e) -> p h e", h=4),
                       bc3(WP[:, 0:4], 4, 128), ALU.mult, [BK[0], WP], [NUM])
                    yield
                    tt("dve", NUM[:], NUM[:], BK[1][:], ALU.add, [NUM, BK[1]], [NUM])
                    yield
                    DEN = SG4[5]
                    yield
                    tt("dve", DEN[:, 0:4], BK[6][:, 0:8].rearrange("p (h two) -> p h two", two=2)[:, :, 0], WP[:, 0:4], ALU.mult, [BK[6], WP], [DEN])
                    yield
                    tt("dve", DEN[:, 0:4], DEN[:, 0:4], RS[:, 0:4], ALU.add, [DEN, RS], [DEN])
                    yield
                    act(EMT[:, 0:4], MT[:, 0:4], AF.Exp, [MT], [EMT], scale=-1.0)
                    yield
                    ts("dve", DEN[:, 4:8], DEN[:, 0:4], -1.0, None, ALU.mult, None, [DEN], [DEN])
                    yield
                    tt("dve", DEN[:, 0:4], DEN[:, 0:4], DEN[:, 4:8], ALU.max, [DEN], [DEN])
                    yield
                    tt("dve", DEN[:, 0:4], DEN[:, 0:4], EMT[:, 0:4], ALU.max, [DEN, EMT], [DEN])
                    yield
                    P.op("dve", lambda e, d=DEN: e.reciprocal(out=d[:, 0:4], in_=d[:, 0:4]), [DEN], [DEN])
                    yield
                    tt("dve", NUM[:].rearrange("p (h e) -> p h e", h=4), NUM[:].rearrange("p (h e) -> p h e", h=4),
                       bc3(DEN[:, 0:4], 4, 128), ALU.mult, [NUM, DEN], [NUM])
                    yield
                    cp("pool", T12[:, 0:4], MT[:, 0:4], [MT], [T12])
                    yield
                    tt("dve", T12[:, 4:8], GF[:, 4:8], MT[:, 0:4], ALU.subtract, [GF, MT], [T12])
                    yield
                    cp("pool", T12[:, 8:12], WP[:, 0:4], [WP], [T12])
                    yield
                    mm(BK[6][:, 16:28], SEL127, T12[:], True, True, [CONST, T12], [BK[6]])
                    yield
                    cp("dve", MBC[:], BK[6][:, 16:20], [BK[6]], [MBC])
                    yield
                    PEND = SG4[4]
                    yield
                    tt("dve", PEND[:, 4:8], COLS[:, 12:16], BK[6][:, 20:24], ALU.add, [COLS, BK[6]], [PEND])
                    yield
                    act(PEND[:, 4:8], PEND[:, 4:8], AF.Exp, [PEND], [PEND])
                    yield
                    cp("dve", WLC[:], BK[6][0:64, 24:28], [BK[6]], [WLC])
                    yield
                    tt("dve", PK[:], MKt[:], bc3(PEND[:, 4:8], 4, 64), ALU.mult, [MKt, PEND], [PK])
                    yield
                    for h in range(4):
                        mm(BK[1][0:64, h * 128:(h + 1) * 128], PK[:, h, :], MVt[:, h, :], True, True, [PK, MVt], [BK[1]])
                    yield
                    for h in range(4):
                        mm(BK[6][0:64, 32 + 2 * h:34 + 2 * h], PK[:, h, :], ONEB[:], True, True, [PK, ONEB], [BK[6]])
                    yield
                    for h in range(4):
                        stt("dve", C32[:, h, :], C32[:, h, :], WLC[:, h:h + 1], BK[1][0:64, h * 128:(h + 1) * 128],
                            ALU.mult, ALU.add, [C32, WLC, BK[1]], [C32])
                    yield
                    tt("dve", N32[:], N32[:], WLC[:], ALU.mult, [N32, WLC], [N32])
                    yield
                    tt("dve", N32[:], N32[:], BK[6][0:64, 32:40].rearrange("p (a two) -> p a two", two=2)[:, :, 0],
                       ALU.add, [N32, BK[6]], [N32])
                    yield
                    cp("act", CBh[:], C32[:], [C32], [CBh])
                    yield
                    cp("act", NBh[:, :, 0], N32[:], [N32], [NBh])
                    yield
                    headnorm_gate(NUM, MOSG, MIX[:, 512:1024], TMP[1], SG4[4], SG4[5])
                    yield

                def interleave(ga, gb, na, nb):
                    a = b = True
                    while a or b:
                        for _ in range(na):
                            if a:
                                try:
                                    next(ga)
                                except StopIteration:
                                    a = False
                        for _ in range(nb):
                            if b:
                                try:
                                    next(gb)
                                except StopIteration:
                                    b = False
                interleave(gen_EF(), gen_G(), 2, 1)

                for k in range(8):
                    tr(TPB[:, k * 128:(k + 1) * 128], MIX[:, k * 128:(k + 1) * 128], IDB[:], [MIX, IDB], [TPB])
                cp("act", MIXT[:].rearrange("p k t -> p (k t)"), TPB[:], [TPB], [MIXT])
                for eh in range(2):
                    for k in range(8):
                        mm(BK[eh][:], MIXT[:, k, :], WOUT[:, k, eh * 512:(eh + 1) * 512], k == 0, k == 7,
                           [MIXT, WOUT], [BK[eh]])
                ss, rs = SC4[0], SC4[1]
                for eh in range(2):
                    act(XN[:, eh * 512:(eh + 1) * 512], BK[eh][:], AF.Square, [BK[eh]], [XN, ss],
                        accum=ss[:, eh:eh + 1])
                tt("dve", ss[:, 2:3], ss[:, 0:1], ss[:, 1:2], ALU.add, [ss], [ss])
                rsqrt_small(rs[:, 0:1], ss[:, 2:3], 1.0 / D, [ss], [rs], rs[:, 1:2])
                for eh in range(2):
                    sl = slice(eh * 512, (eh + 1) * 512)
                    stt("dve", TMP[eh][:], BK[eh][:], rs[:, 0:1], GPM[:, sl], ALU.mult, ALU.mult,
                        [BK[eh], rs, GPM], [TMP[eh]])
                    tt("pool", Xt[:, t, sl], Xt[:, t, sl], TMP[eh][:], ALU.add, [RX[t], TMP[eh]], [RX[t]])

            dma("sp", pS_d.rearrange("h d e -> d h e"), S32[:], [S32], [])
            dma("sp", pC_d.rearrange("h d e -> d h e"), C32[:], [C32], [])
            dma("sp", pn_d.rearrange("h d -> d h"), N32[:], [N32], [], slow=True)
            dma("sp", pm_d, MBC[0:1, :], [MBC], [])
            for ch in range(12):
                tr(BK[2 + ch // 4][0:3, (ch % 4) * 128:(ch % 4 + 1) * 128], TAIL[:, ch, :], IDF, [TAIL, CONST],
                   [BK[2 + ch // 4]])
            for g in range(3):
                cp("dve", TMP[g % 2][0:3, :], BK[2 + g][0:3, :], [BK[2 + g]], [TMP[g % 2]])
                dma("sp", pconv_d[:, g * 512:(g + 1) * 512], TMP[g % 2][0:3, :], [TMP[g % 2]], [])


            P.barrier(all_res_p1 + RX + [XS1T.r, HNS.r, IDF2.r, IDB.r, GPREMLP.r])
            p1a.close()
            p1b = ExitStack()
            cur[0] = p1b
            R_ = NS
            GRP = 1
            XS = s1("XS", [R_, D])
            XNs = s1("XNs", [R_, D], BF16)
            HTs = s1("HTs", [128, 8, R_], BF16)
            SCV = s1("SCV", [R_, 512])
            XPT = s1("XPT", [128, 12, 4, R_])
            ACs = s1("ACs", [128, 12, R_])
            TM12 = s1("TM12", [128, 12, R_])
            SQs = s1("SQs", [128, 8, R_])
            GQs = s1("GQs", [128, 4, R_])
            GKs = s1("GKs", [128, 4, R_])
            GVs = s1("GVs", [128, 4, R_])
            MQs = s1("MQs", [64, 4, R_])
            MKs = s1("MKs", [64, 4, R_])
            ZSGs = s1("ZSGs", [R_, 512])
            MOSGs = s1("MOSGs", [R_, 512])
            MVs = s1("MVs", [R_, 4, 128])
            GTs = s1("GTs", [R_, 16])
            ARGs = s1("ARGs", [R_, 12])
            GLs = s1("GLs", [R_, 12])
            BETAs = s1("BETAs", [R_, 4])
            IGs = s1("IGs", [R_, 4])
            EGs = s1("EGs", [R_, 4])
            Qt = s1("Qt", [R_, 4, 128])
            Kt = s1("Kt", [R_, 4, 128])
            Vt = s1("Vt", [R_, 4, 128])
            MQt = s1("MQt", [R_, 4, 64])
            MKtt = s1("MKtt", [R_, 4, 64])
            PKt = s1("PKt", [R_, 4, 64])
            DIAGI = s1("DIAGI", [R_, 16, 16])
            M16 = s1("M16", [128, 16, 16])
            KD = s1("KD", [128, 4, 16])
            QD = s1("QD", [128, 4, 16])
            QDm = s1("QDm", [64, 4, 16])
            DG4 = s1("DG4", [R_, 16, 4])
            EGBC = s1("EGBC", [128, 64])
            WPBC = s1("WPBC", [128, 64])
            S0g = s1("S0g", [128, GRP, 4, 128])
            C0g = s1("C0g", [64, GRP, 4, 128])
            KS = s1("KS", [R_, 512])
            QS = s1("QS", [R_, 512])
            QC = s1("QC", [R_, 512])
            Ug = s1("Ug", [R_, 4, 128])
            KROW = s1("KROW", [R_, 512])
            PKROW = View(DIAGI[:].rearrange("p a b -> p (a b)"), DIAGI.r)
            N0 = s1("N0", [R_, 4, 64])
            SMs = [s1("SMs%d" % i, [R_, 8]) for i in range(10)]
            T1 = s1("T1s", [R_, 512])
            T2 = KROW

            def hn_gate16(src, gate, dst, wres):
                ss, rs = SMs[8], SMs[9]
                tt("pool", T2[:], src[:], src[:], ALU.mult, [src], [T2])
                red("dve", ss[:, 0:4], T2[:].rearrange("p (h e) -> p h e", h=4), ALU.add, [T2], [ss])
                rsqrt_small(rs[:, 0:4], ss[:, 0:4], 1.0 / 128, [ss], [rs], rs[:, 4:8])
                tt("dve", T2[:].rearrange("p (h e) -> p h e", h=4), src[:].rearrange("p (h e) -> p h e", h=4),
                   bc3(rs[:, 0:4], 4, 128), ALU.mult, [src, rs], [T2])
                tt("dve", dst, T2[:], gate[:], ALU.mult, [T2, gate], [wres])

            IDF16 = CONST[0:R_, C_IDF:C_IDF + R_]
            ONES16 = CONST[0:R_, C_ONES:C_ONES + 128]

            dma("sp", XS[:], xs_d, [], [XS])
            dma("sp", N0[:].rearrange("p h d -> p (h d)"), sn_d, [], [N0])
            dma("sp", SMs[0][:, 0:4], sm_d, [], [SMs[0]])
            dma("sp", oconv_d[:, 0:2, :], sconv_d[:, 1:3, :], [], [])
            ss, rs = SMs[8], SMs[9]
            act(XNs[:], XS[:], AF.Square, [XS], [XNs, ss], accum=ss[:, 0:1])
            rsqrt_small(rs[:, 0:1], ss[:, 0:1], 1.0 / D, [ss], [rs], rs[:, 1:2])
            ts("dve", XNs[:], XS[:], rs[:, 0:1], None, ALU.mult, None, [XS, rs], [XNs])
            for k in range(8):
                tr(TPB[:, k * 128:k * 128 + R_], XNs[:, k * 128:(k + 1) * 128], IDB[0:R_, 0:R_], [XNs, IDB], [TPB])
            tt("dve", HTs[:], TPB[:].rearrange("p (k t) -> p k t", k=8)[:, :, 0:R_], bc3(GPRE[:], 8, R_), ALU.mult,
               [TPB, GPRE], [HTs])

            for ch in range(12):
                bk = BK[ch % 2]
                c0 = ch * 128
                for k in range(8):
                    mm(bk[:, 0:R_], WIN[:, k, c0:c0 + 128], HTs[:, k, :], k == 0, k == 7, [HTs] + rwin(c0, c0 + 128), [bk])
                cp("act", XPT[:, ch, 3, :], bk[:, 0:R_], [bk], [XPT])
            for hh in range(8):
                bk = BK[hh % 2]
                c0 = W_MQK + hh * 64
                for k in range(8):
                    mm(bk[0:64, 0:R_], WIN[:, k, c0:c0 + 64], HTs[:, k, :], k == 0, k == 7, [HTs] + rwin(c0, c0 + 64), [bk])
                if hh < 4:
                    cp("act", MQs[:, hh, :], bk[0:64, 0:R_], [bk], [MQs])
                else:
                    act(MKs[:, hh - 4, :], bk[0:64, 0:R_], AF.Copy, [bk], [MKs], scale=0.125)
            for g in range(3):
                for j in range(3):
                    dma("sp", SCV[:], sconv_d[:, j, g * 512:(g + 1) * 512], [], [SCV])
                    for c4 in range(4):
                        o0 = (j * 4 + c4) * R_
                        tr(BK[2][:, o0:o0 + R_], SCV[:, c4 * 128:(c4 + 1) * 128], IDF16, [SCV, CONST], [BK[2]])
                cp("dve", XPT[:, 4 * g:4 * g + 4, 0:3, :],
                   BK[2][:, 0:12 * R_].rearrange("p (j c r) -> p c j r", j=3, c=4), [BK[2]], [XPT])
            for g in range(3):
                bk = BK[3 + g % 2]
                for k in range(8):
                    mm(bk[0:R_, :], HTs[:, k, :], WIN[:, k, g * 512:(g + 1) * 512], k == 0, k == 7,
                       [HTs] + rwin(g * 512, (g + 1) * 512), [bk])
                cp("act", SCV[:], bk[0:R_, :], [bk], [SCV])
                dma("sp", oconv_d[:, 2, g * 512:(g + 1) * 512], SCV[:], [SCV], [])
            tt("dve", ACs[:], XPT[:, :, 0, :], bc3(CW[:, 0, :], 12, R_), ALU.mult, [XPT, CW], [ACs])
            for j in range(1, 4):
                tt("dve", TM12[:], XPT[:, :, j, :], bc3(CW[:, j, :], 12, R_), ALU.mult, [XPT, CW], [TM12])
                tt("dve", ACs[:], ACs[:], TM12[:], ALU.add, [ACs, TM12], [ACs])
            QKFs = TM12
            act(QKFs[:, 0:8, :], ACs[:, 0:8, :], AF.Silu, [ACs], [QKFs])
            act(GVs[:], ACs[:, 8:12, :], AF.Silu, [ACs], [GVs])
            act(SQs[:], QKFs[:, 0:8, :], AF.Square, [QKFs], [SQs])
            mm(BK[2][:, 0:8 * R_], ONES, SQs[:].rearrange("p a b -> p (a b)"), True, True, [CONST, SQs], [BK[2]])
            ts("dve", SQs[:].rearrange("p a b -> p (a b)"), BK[2][:, 0:8 * R_], 1.0, EPS, ALU.mult, ALU.add, [BK[2]], [SQs])
            act(SQs[:], SQs[:], AF.Ln, [SQs], [SQs])
            act(SQs[:], SQs[:], AF.Exp, [SQs], [SQs], scale=-0.5)
            stt("dve", GQs[:], QKFs[:, 0:4, :], 128.0 ** -0.5, SQs[:, 0:4, :], ALU.mult, ALU.mult, [QKFs, SQs], [GQs])
            tt("dve", GKs[:], QKFs[:, 4:8, :], SQs[:, 4:8, :], ALU.mult, [QKFs, SQs], [GKs])

            def tok16(bk, c0, n):
                for k in range(8):
                    mm(bk[0:R_, 0:n], HTs[:, k, :], WIN[:, k, c0:c0 + n], k == 0, k == 7, [HTs] + rwin(c0, c0 + n), [bk])
            tok16(BK[0], W_GZ, 512)
            act(T1[:], BK[0][0:R_, :], AF.Silu, [BK[0]], [T1])
            tt("dve", ZSGs[:].rearrange("p (h e) -> p h e", h=4), T1[:].rearrange("p (h e) -> p h e", h=4),
               bcm(GNG[0:R_, :], 4), ALU.mult, [T1, GNG], [ZSGs])
            tok16(BK[1], W_MV, 512)
            cp("act", MVs[:].rearrange("p h e -> p (h e)"), BK[1][0:R_, :], [BK[1]], [MVs])
            tok16(BK[0], W_MO, 512)
            act(T1[:], BK[0][0:R_, :], AF.Exp, [BK[0]], [T1], scale=-1.0)
            ts("dve", T1[:], T1[:], 1.0, None, ALU.add, None, [T1], [T1])
            P.op("dve", lambda e: e.reciprocal(out=T1[:], in_=T1[:]), [T1], [T1])
            tt("dve", MOSGs[:].rearrange("p (h e) -> p h e", h=4), T1[:].rearrange("p (h e) -> p h e", h=4),
               bcm(MNG[0:R_, :], 4), ALU.mult, [T1, MNG], [MOSGs])
            tok16(BK[1], W_GATE, 16)
            cp("dve", GTs[:], BK[1][0:R_, 0:16], [BK[1]], [GTs])
            tt("dve", ARGs[:], GTs[:, 0:12], SGN[0:R_, :], ALU.mult, [GTs, SGN], [ARGs])
            tt("dve", ARGs[:], ARGs[:], BIA[0:R_, :], ALU.add, [ARGs, BIA], [ARGs])
            act(ARGs[:], ARGs[:], AF.Exp, [ARGs], [ARGs])
            act(ARGs[:], ARGs[:], AF.Ln, [ARGs], [ARGs], bias=1.0)
            tt("dve", GLs[:], ARGs[:], NEGC[0:R_, :], ALU.mult, [ARGs, NEGC], [GLs])
            act(BETAs[:], GLs[:, 0:4], AF.Exp, [GLs], [BETAs])
            tt("dve", IGs[:], GTs[:, 12:16], SMB[0:R_, 8:12], ALU.add, [GTs, SMB], [IGs])
            act(EGs[:], GLs[:, 4:8], AF.Exp, [GLs], [EGs])
            M0s, Bs, MTs, WPs, Ps, QKG, QKMs, QNs = SMs[0], SMs[1], SMs[2], SMs[3], SMs[4], SMs[5], SMs[6], SMs[7]
            tt("dve", Bs[:, 0:4], GLs[:, 8:12], M0s[:, 0:4], ALU.add, [GLs, M0s], [Bs])
            tt("dve", MTs[:, 0:4], Bs[:, 0:4], IGs[:], ALU.max, [Bs, IGs], [MTs])
            tt("dve", WPs[:, 0:4], Bs[:, 0:4], MTs[:, 0:4], ALU.subtract, [Bs, MTs], [WPs])
            act(WPs[:, 0:4], WPs[:, 0:4], AF.Exp, [WPs], [WPs])
            tt("dve", Ps[:, 0:4], IGs[:], MTs[:, 0:4], ALU.subtract, [IGs, MTs], [Ps])
            act(Ps[:, 0:4], Ps[:, 0:4], AF.Exp, [Ps], [Ps])
            dma("sp", om_d, MTs[:, 0:4], [MTs], [])

            for h in range(4):
                tr(BK[2][0:R_, h * 128:(h + 1) * 128], GQs[:, h, :], IDF, [GQs, CONST], [BK[2]])
                tr(BK[3][0:R_, h * 128:(h + 1) * 128], GKs[:, h, :], IDF, [GKs, CONST], [BK[3]])
                tr(BK[4][0:R_, h * 128:(h + 1) * 128], GVs[:, h, :], IDF, [GVs, CONST], [BK[4]])
                tr(BK[5][0:R_, h * 64:(h + 1) * 64], MQs[:, h, :], CONST[0:64, C_IDF:C_IDF + 64], [MQs, CONST], [BK[5]])
                tr(BK[5][0:R_, 256 + h * 64:256 + (h + 1) * 64], MKs[:, h, :], CONST[0:64, C_IDF:C_IDF + 64],
                   [MKs, CONST], [BK[5]])
            cp("act", Qt[:].rearrange("p h d -> p (h d)"), BK[2][0:R_, :], [BK[2]], [Qt])
            cp("dve", Kt[:].rearrange("p h d -> p (h d)"), BK[3][0:R_, :], [BK[3]], [Kt])
            cp("act", Vt[:].rearrange("p h d -> p (h d)"), BK[4][0:R_, :], [BK[4]], [Vt])
            cp("dve", MQt[:].rearrange("p h d -> p (h d)"), BK[5][0:R_, 0:256], [BK[5]], [MQt])
            cp("act", MKtt[:].rearrange("p h d -> p (h d)"), BK[5][0:R_, 256:512], [BK[5]], [MKtt])
            tt("dve", T1[:], Qt[:].rearrange("p h d -> p (h d)"), Kt[:].rearrange("p h d -> p (h d)"), ALU.mult,
               [Qt, Kt], [T1])
            red("dve", QKG[:, 0:4], T1[:].rearrange("p (h d) -> p h d", h=4), ALU.add, [T1], [QKG])
            tt("dve", T1[:, 0:256], MQt[:].rearrange("p h d -> p (h d)"), MKtt[:].rearrange("p h d -> p (h d)"),
               ALU.mult, [MQt, MKtt], [T1])
            red("dve", QKMs[:, 0:4], T1[:, 0:256].rearrange("p (h d) -> p h d", h=4), ALU.add, [T1], [QKMs])
            tt("dve", T1[:, 0:256], MQt[:].rearrange("p h d -> p (h d)"), N0[:].rearrange("p h d -> p (h d)"),
               ALU.mult, [MQt, N0], [T1])
            red("dve", QNs[:, 0:4], T1[:, 0:256].rearrange("p (h d) -> p h d", h=4), ALU.add, [T1], [QNs])
            tt("dve", PKt[:], MKtt[:], bc3(Ps[:, 0:4], 4, 64), ALU.mult, [MKtt, Ps], [PKt])
            tt("dve", N0[:], N0[:], bc3(WPs[:, 0:4], 4, 64), ALU.mult, [N0, WPs], [N0])
            tt("dve", N0[:], N0[:], PKt[:], ALU.add, [N0, PKt], [N0])
            dma("sp", on_d, N0[:].rearrange("p h d -> p (h d)"), [N0], [])

            tt("dve", DIAGI[:], bc3(IDF16, R_, R_), bcm(IDF16, R_), ALU.mult, [CONST], [DIAGI])
            mm(BK[2][:, 0:R_ * R_], ONES16, DIAGI[:].rearrange("p a b -> p (a b)"), True, True, [CONST, DIAGI], [BK[2]])
            cp("dve", M16[:].rearrange("p a b -> p (a b)"), BK[2][:, 0:R_ * R_], [BK[2]], [M16])
            tt("dve", DG4[:], bcm(EGs[:], R_), bc3(IDF16, R_, 4), ALU.mult, [EGs, CONST], [DG4])
            mm(BK[3][:, 0:64], ONES16, DG4[:].rearrange("p a b -> p (a b)"), True, True, [CONST, DG4], [BK[3]])
            cp("dve", EGBC[:], BK[3][:, 0:64], [BK[3]], [EGBC])
            tt("dve", DG4[:], bcm(WPs[:, 0:4], R_), bc3(IDF16, R_, 4), ALU.mult, [WPs, CONST], [DG4])
            mm(BK[3][:, 0:64], ONES16, DG4[:].rearrange("p a b -> p (a b)"), True, True, [CONST, DG4], [BK[3]])
            cp("dve", WPBC[:], BK[3][:, 0:64], [BK[3]], [WPBC])

            for g in range(R_ // GRP):
                r0 = g * GRP
                dma("sp", S0g[:], sS_d[r0:r0 + GRP].rearrange("r h d e -> d r h e"), [], [S0g])
                dma("sp", C0g[:], sC_d[r0:r0 + GRP].rearrange("r h d e -> d r h e"), [], [C0g])
                tt("dve", KD[:], GKs[:], bcm(M16[:, r0, :], 4), ALU.mult, [GKs, M16], [KD])
                tt("pool", QD[:], GQs[:], bcm(M16[:, r0, :], 4), ALU.mult, [GQs, M16], [QD])
                tt("dve", QDm[:], MQs[:], bcm(M16[0:64, r0, :], 4), ALU.mult, [MQs, M16], [QDm])
                for h in range(4):
                    sl = slice(h * 128, (h + 1) * 128)
                    mm(BK[2][0:R_, sl], KD[:, h, :], S0g[:, 0, h, :], True, True, [KD, S0g], [BK[2]])
                    mm(BK[3][0:R_, sl], QD[:, h, :], S0g[:, 0, h, :], True, True, [QD, S0g], [BK[3]])
                    mm(BK[5][0:R_, sl], QDm[:, h, :], C0g[:, 0, h, :], True, True, [QDm, C0g], [BK[5]])
                if g == 0:
                    cp("dve", KS[:], BK[2][0:R_, :], [BK[2]], [KS])
                    cp("act", QS[:], BK[3][0:R_, :], [BK[3]], [QS])
                    cp("act", QC[:], BK[5][0:R_, :], [BK[5]], [QC])
                else:
                    tt("dve", KS[:], KS[:], BK[2][0:R_, :], ALU.add, [KS, BK[2]], [KS])
                    tt("dve", QS[:], QS[:], BK[3][0:R_, :], ALU.add, [QS, BK[3]], [QS])
                    tt("dve", QC[:], QC[:], BK[5][0:R_, :], ALU.add, [QC, BK[5]], [QC])
                tt("dve", Ug[:], BK[2][0:R_, :].rearrange("p (h e) -> p h e", h=4), bc3(EGs[:], 4, 128), ALU.mult,
                   [BK[2], EGs], [Ug])
                tt("dve", Ug[:], Vt[:], Ug[:], ALU.subtract, [Vt, Ug], [Ug])
                tt("dve", Ug[:], Ug[:], bc3(BETAs[:], 4, 128), ALU.mult, [Ug, BETAs], [Ug])
                for rl in range(GRP):
                    r = r0 + rl
                    ts("dve", KROW[:], Kt[:].rearrange("p h d -> p (h d)"), IDF16[:, r:r + 1], None, ALU.mult, None,
                       [Kt, CONST], [KROW])
                    ts("dve", PKROW[:], PKt[:].rearrange("p h d -> p (h d)"), IDF16[:, r:r + 1], None, ALU.mult, None,
                       [PKt, CONST], [PKROW])
                    for h in range(4):
                        mm(BK[4][:, h * 128:(h + 1) * 128], KROW[:, h * 128:(h + 1) * 128], Ug[:, h, :], True, True,
                           [KROW, Ug], [BK[4]])
                    for h in range(4):
                        mm(BK[6][0:64, h * 128:(h + 1) * 128], PKROW[:, h * 64:(h + 1) * 64], MVs[:, h, :], True, True,
                           [PKROW, MVs], [BK[6]])
                    for h in range(4):
                        stt("dve", S0g[:, rl, h, :], S0g[:, rl, h, :], EGBC[:, r * 4 + h:r * 4 + h + 1],
                            BK[4][:, h * 128:(h + 1) * 128], ALU.mult, ALU.add, [S0g, EGBC, BK[4]], [S0g])
                        stt("dve", C0g[:, rl, h, :], C0g[:, rl, h, :], WPBC[0:64, r * 4 + h:r * 4 + h + 1],
                            BK[6][0:64, h * 128:(h + 1) * 128], ALU.mult, ALU.add, [C0g, WPBC, BK[6]], [C0g])
                dma("sp", oS_d[r0:r0 + GRP].rearrange("r h d e -> d r h e"), S0g[:], [S0g], [])
                dma("sp", oC_d[r0:r0 + GRP].rearrange("r h d e -> d r h e"), C0g[:], [C0g], [])

            MIXs = XNs
            tt("dve", Ug[:], KS[:].rearrange("p (h e) -> p h e", h=4), bc3(EGs[:], 4, 128), ALU.mult, [KS, EGs], [Ug])
            tt("dve", Ug[:], Vt[:], Ug[:], ALU.subtract, [Vt, Ug], [Ug])
            tt("dve", Ug[:], Ug[:], bc3(BETAs[:], 4, 128), ALU.mult, [Ug, BETAs], [Ug])
            tt("dve", T1[:].rearrange("p (h e) -> p h e", h=4), QS[:].rearrange("p (h e) -> p h e", h=4),
               bc3(EGs[:], 4, 128), ALU.mult, [QS, EGs], [T1])
            tt("dve", Ug[:], Ug[:], bc3(QKG[:, 0:4], 4, 128), ALU.mult, [Ug, QKG], [Ug])
            tt("dve", T1[:], T1[:], Ug[:].rearrange("p h e -> p (h e)"), ALU.add, [T1, Ug], [T1])
            hn_gate16(T1, ZSGs, MIXs[:, 0:512], MIXs)
            PQ = SMs[5]
            tt("dve", PQ[:, 4:8], Ps[:, 0:4], QKMs[:, 0:4], ALU.mult, [Ps, QKMs], [PQ])
            tt("dve", T1[:].rearrange("p (h e) -> p h e", h=4), QC[:].rearrange("p (h e) -> p h e", h=4),
               bc3(WPs[:, 0:4], 4, 128), ALU.mult, [QC, WPs], [T1])
            tt("dve", Ug[:], MVs[:], bc3(PQ[:, 4:8], 4, 128), ALU.mult, [MVs, PQ], [Ug])
            tt("dve", T1[:], T1[:], Ug[:].rearrange("p h e -> p (h e)"), ALU.add, [T1, Ug], [T1])
            DENs, EMTs = SMs[6], SMs[7]
            tt("dve", DENs[:, 4:8], WPs[:, 0:4], QNs[:, 0:4], ALU.mult, [WPs, QNs], [DENs])
            tt("dve", DENs[:, 4:8], DENs[:, 4:8], PQ[:, 4:8], ALU.add, [DENs, PQ], [DENs])
            ts("dve", DENs[:, 0:4], DENs[:, 4:8], -1.0, None, ALU.mult, None, [DENs], [DENs])
            tt("dve", DENs[:, 4:8], DENs[:, 4:8], DENs[:, 0:4], ALU.max, [DENs], [DENs])
            act(EMTs[:, 4:8], MTs[:, 0:4], AF.Exp, [MTs], [EMTs], scale=-1.0)
            tt("dve", DENs[:, 4:8], DENs[:, 4:8], EMTs[:, 4:8], ALU.max, [DENs, EMTs], [DENs])
            P.op("dve", lambda e, d=DENs: e.reciprocal(out=d[:, 4:8], in_=d[:, 4:8]), [DENs], [DENs])
            tt("dve", T1[:].rearrange("p (h e) -> p h e", h=4), T1[:].rearrange("p (h e) -> p h e", h=4),
               bc3(DENs[:, 4:8], 4, 128), ALU.mult, [T1, DENs], [T1])
            hn_gate16(T1, MOSGs, MIXs[:, 512:1024], MIXs)
            for k in range(8):
                tr(TPB[:, k * 128:k * 128 + R_], MIXs[:, k * 128:(k + 1) * 128], IDB[0:R_, 0:R_], [MIXs, IDB], [TPB])
            cp("act", HTs[:], TPB[:].rearrange("p (k t) -> p k t", k=8)[:, :, 0:R_], [TPB], [HTs])
            for eh in range(2):
                for k in range(8):
                    mm(BK[eh][0:R_, :], HTs[:, k, :], WOUT[:, k, eh * 512:(eh + 1) * 512], k == 0, k == 7,
                       [HTs, WOUT], [BK[eh]])
            ss, rs = SMs[8], SMs[9]
            for eh in range(2):
                act(XNs[:, eh * 512:(eh + 1) * 512], BK[eh][0:R_, :], AF.Square, [BK[eh]], [XNs, ss],
                    accum=ss[:, eh:eh + 1])
            tt("dve", ss[:, 2:3], ss[:, 0:1], ss[:, 1:2], ALU.add, [ss], [ss])
            rsqrt_small(rs[:, 0:1], ss[:, 2:3], 1.0 / D, [ss], [rs], rs[:, 1:2])
            for eh in range(2):
                sl = slice(eh * 512, (eh + 1) * 512)
                stt("dve", T1[:], BK[eh][0:R_, :], rs[:, 0:1], GPM[0:R_, sl], ALU.mult, ALU.mult, [BK[eh], rs, GPM], [T1])
                tt("dve", XS[:, sl], XS[:, sl], T1[:], ALU.add, [XS, T1], [XS])
            for k in range(8):
                tr(BK[2][:, k * R_:(k + 1) * R_], XS[:, k * 128:(k + 1) * 128], IDF16, [XS, CONST], [BK[2]])
            cp("dve", XS1T[:].rearrange("p k r -> p (k r)"), BK[2][:, 0:8 * R_], [BK[2]], [XS1T])
            act(XNs[:], XS[:], AF.Square, [XS], [XNs, ss], accum=ss[:, 4:5])
            rsqrt_small(rs[:, 4:5], ss[:, 4:5], 1.0 / D, [ss], [rs], rs[:, 5:6])
            ts("dve", XNs[:], XS[:], rs[:, 4:5], None, ALU.mult, None, [XS, rs], [XNs])
            for k in range(8):
                tr(TPB[:, k * 128:k * 128 + R_], XNs[:, k * 128:(k + 1) * 128], IDB[0:R_, 0:R_], [XNs, IDB], [TPB])
            tt("dve", HNS[:], TPB[:].rearrange("p (k t) -> p k t", k=8)[:, :, 0:R_], bc3(GPREMLP[:], 8, R_), ALU.mult,
               [TPB, GPREMLP], [HNS])

            P.barrier(all_res_p1 + RX + [XS1T.r, HNS.r, IDF2.r, IDB.r, GPREMLP.r])
            p1b.close()

        with ExitStack() as p2:
            WUP = p2.enter_context(nc.sbuf_tensor("sb_WUP", [128, 8, DFF], BF16))
            WDN = p2.enter_context(nc.sbuf_tensor("sb_WDN", [128, 32, D], BF16))
            NWC = 8
            RWU = [Res("WUP%d" % i) for i in range(NWC)]
            RWD = [Res("WDN%d" % i) for i in range(NWC)]
            wup_v = wup_d.rearrange("(k p) c -> p k c", p=128)
            wdn_v = wdn_d.rearrange("(k p) c -> p k c", p=128)
            for i in range(NWC if KPH2 else 0):
                dma("pool", WUP[:, :, i * 512:(i + 1) * 512], wup_v[:, :, i * 512:(i + 1) * 512], [], [RWU[i]])
                dma("pool", WDN[:, i * 4:(i + 1) * 4, :], wdn_v[:, i * 4:(i + 1) * 4, :], [], [RWD[i]])
            XN2 = sb(p2, "XN2", [128, D], BF16)
            GPL = sb(p2, "GPL", [128, D])
            dma("sp", GPL[:], gpl_d.partition_broadcast(128), [], [GPL])
            HN = sb(p2, "HN", [128, 8, 256], BF16)
            UT = [sb(p2, "UT%d" % i, [128, 256], BF16) for i in range(2)]
            RL = [sb(p2, "RL%d" % i, [128, 256]) for i in range(2)]
            SS2 = sb(p2, "SS2", [128, 8])

            NB2 = NT // 2 if KPH2 else 0

            def xv_of(blk, j):
                return Xt[:, 2 * blk + j, :], RX[2 * blk + j]

            def prep(blk):
                for j in range(2):
                    xv, rx = xv_of(blk, j)
                    act(XN2[:], xv, AF.Square, [rx], [XN2, SS2], accum=SS2[:, 0:1])
                    rsqrt_small(SS2[:, 1:2], SS2[:, 0:1], 1.0 / D, [SS2], [SS2], SS2[:, 2:3])
                    ts("dve", XN2[:], xv, SS2[:, 1:2], None, ALU.mult, None, [rx, SS2], [XN2])
                    for k in range(8):
                        tr(TPB[:, k * 128:(k + 1) * 128], XN2[:, k * 128:(k + 1) * 128], IDB[:], [XN2, IDB], [TPB])
                    tt("dve", HN[:, :, j * 128:(j + 1) * 128], TPB[:].rearrange("p (k t) -> p k t", k=8),
                       bc3(GPREMLP[:], 8, 128), ALU.mult, [TPB, GPREMLP], [HN])

            def up(f, hn, n):
                bk = BK[f % 2]
                for k in range(8):
                    mm(bk[:, 0:n], WUP[:, k, f * 128:(f + 1) * 128], hn[:, k, 0:n], k == 0, k == 7,
                       [hn, RWU[f // 4]], [bk])
                rl, ut = RL[f % 2], UT[f % 2]
                act(rl[:, 0:n], bk[:, 0:n], AF.Relu, [bk], [rl])
                tt("dve" if f % 2 == 0 else "pool", ut[:, 0:n], rl[:, 0:n], rl[:, 0:n], ALU.mult, [rl], [ut])

            def down(f, rows, ntl):
                ut = UT[f % 2]
                for j in range(ntl):
                    for eh in range(2):
                        ab = BK[2 + 2 * j + eh]
                        mm(ab[0:rows, :], ut[:, j * rows:(j + 1) * rows], WDN[:, f, eh * 512:(eh + 1) * 512],
                           f == 0, f == 31, [ut, RWD[f // 4]], [ab])

            def fin(blk):
                for j in range(2):
                    xv, rx = xv_of(blk, j)
                    for eh in range(2):
                        ab = BK[2 + 2 * j + eh]
                        act(XN2[:, eh * 512:(eh + 1) * 512], ab[:], AF.Square, [ab], [XN2, SS2],
                            accum=SS2[:, 3 + eh:4 + eh])
                    tt("dve", SS2[:, 5:6], SS2[:, 3:4], SS2[:, 4:5], ALU.add, [SS2], [SS2])
                    rsqrt_small(SS2[:, 6:7], SS2[:, 5:6], 1.0 / D, [SS2], [SS2], SS2[:, 7:8])
                    for eh in range(2):
                        ab = BK[2 + 2 * j + eh]
                        sl = slice(eh * 512, (eh + 1) * 512)
                        stt("dve", ab[:], ab[:], SS2[:, 6:7], GPL[:, sl], ALU.mult, ALU.mult, [ab, SS2, GPL], [ab])
                        tt("dve", xv[:, sl], xv[:, sl], ab[:], ALU.add, [rx, ab], [rx])
                    r0 = (2 * blk + j) * 128
                    dma("sp", y_d[r0:r0 + 128, :], xv, [rx], [])

            if NB2:
                prep(0)
            for blk in range(NB2):
                up(0, HN, 256)
                for f in range(32):
                    if f + 1 < 32:
                        up(f + 1, HN, 256)
                    elif blk + 1 < NB2:
                        prep(blk + 1)
                    down(f, 128, 2)
                fin(blk)

            R_ = NS
            if KPH2:
                up(0, HNS, R_)
                for f in range(32):
                    if f + 1 < 32:
                        up(f + 1, HNS, R_)
                    down(f, R_, 1)
            if KPH2:
                for eh in range(2):
                    act(XN2[0:R_, eh * 512:(eh + 1) * 512], BK[2 + eh][0:R_, :], AF.Square, [BK[2 + eh]], [XN2, SS2],
                        accum=SS2[0:R_, 3 + eh:4 + eh])
                tt("dve", SS2[0:R_, 5:6], SS2[0:R_, 3:4], SS2[0:R_, 4:5], ALU.add, [SS2], [SS2])
                rsqrt_small(SS2[0:R_, 6:7], SS2[0:R_, 5:6], 1.0 / D, [SS2], [SS2], SS2[0:R_, 7:8])
                YSB = XN2.t.bitcast(F32)
                for eh in range(2):
                    sl = slice(eh * 512, (eh + 1) * 512)
                    stt("dve", BK[2 + eh][0:R_, :], BK[2 + eh][0:R_, :], SS2[0:R_, 6:7], GPL[0:R_, sl], ALU.mult, ALU.mult,
                        [BK[2 + eh], SS2, GPL], [BK[2 + eh]])
                    for j in range(4):
                        tr(BK[4 + eh][0:R_, j * 128:(j + 1) * 128], XS1T[:, 4 * eh + j, :], IDF2[:], [XS1T, IDF2],
                           [BK[4 + eh]])
                    cp("act", YSB[0:R_, :], BK[4 + eh][0:R_, :], [BK[4 + eh]], [XN2])
                    tt("dve", YSB[0:R_, :], YSB[0:R_, :], BK[2 + eh][0:R_, :], ALU.add, [XN2, BK[2 + eh]], [XN2])
                    dma("sp", ys_d[:, sl], YSB[0:R_, :], [XN2], [])

        n_ins = P.finalize(top)
    return nc, n_ins


_CACHE = {}


def kernel(x_prompt, x_sample, state_gdn_conv, state_gdn_S, state_mlstm_C, state_mlstm_n, state_mlstm_m,
           norm_pre_mix, w_in, conv_w, a_log, dt_bias, gdn_norm_g, b_igate, b_fgate, mlstm_norm_g, w_out,
           norm_post_mix, norm_pre_mlp, w_up, w_down, norm_post_mlp):
    f = lambda a: np.ascontiguousarray(np.asarray(a, dtype=np.float32))
    if "nc" not in _CACHE:
        _CACHE["nc"] = build_program()
    nc, _ = _CACHE["nc"]
    consts = make_consts()
    small = np.concatenate([f(a_log)[0], f(dt_bias)[0], f(b_igate)[0], f(b_fgate)[0]])[None, :]
    shared = {
        "w_in": f(w_in)[0], "w_out": f(w_out)[0], "w_up": f(w_up)[0], "w_down": f(w_down)[0],
        "consts": consts,
        "gpre_fm": f(f(norm_pre_mix)[0].reshape(8, 128).T),
        "gpremlp_fm": f(f(norm_pre_mlp)[0].reshape(8, 128).T),
        "cw_fm": f(f(conv_w)[0].reshape(4, 12, 128).transpose(2, 0, 1).reshape(128, 48)),
        "gpostmix": f(norm_post_mix)[0][None, :], "gpostmlp": f(norm_post_mlp)[0][None, :],
        "small": f(small), "gdn_norm_g": f(gdn_norm_g)[0][None, :], "mlstm_norm_g": f(mlstm_norm_g)[0][None, :],
    }
    xp, xs = f(x_prompt), f(x_sample)
    in_maps = []
    for c in range(NCORES):
        r = slice(c * NS, (c + 1) * NS)
        m = dict(shared)
        m.update({
            "x": xp[c], "xs": xs[r, 0, :],
            "sconv": f(state_gdn_conv)[0, r], "sS": f(state_gdn_S)[0, r], "sC": f(state_mlstm_C)[0, r],
            "sn": f(state_mlstm_n)[0, r].reshape(NS, 256), "sm": f(state_mlstm_m)[0, r],
        })
        in_maps.append(m)
    res = run_bass_kernel_spmd(nc, in_maps, core_ids=list(range(NCORES)))
    R = res.results
    g = lambda k: np.stack([np.asarray(R[c][k], dtype=np.float32) for c in range(NCORES)])
    gc = lambda k: np.concatenate([np.asarray(R[c][k], dtype=np.float32) for c in range(NCORES)], axis=0)
    y_prompt = g("y")
    y_sample = gc("ys")[:, None, :]
    p_conv = g("pconv")[None]
    p_S = g("pS")[None]
    p_C = g("pC")[None]
    p_n = g("pn")[None]
    p_m = g("pm").reshape(NCORES, 4)[None]
    s_conv = gc("oconv")[None]
    s_S = gc("oS")[None]
    s_C = gc("oC")[None]
    s_n = gc("on").reshape(NCORES * NS, 4, 64)[None]
    s_m = gc("om")[None]
    return (y_prompt, y_sample, p_conv, p_S, p_C, p_n, p_m, s_conv, s_S, s_C, s_n, s_m)
```

```python
from contextlib import ExitStack
import numpy as np
import concourse.bass as bass
import concourse.mybir as mybir
from concourse.bass_utils import run_bass_kernel_spmd

F32 = mybir.dt.float32
BF16 = mybir.dt.bfloat16
ALU = mybir.AluOpType
AF = mybir.ActivationFunctionType
AX = mybir.AxisListType

NCORES = 8
T = 2048
NT = T // 128
D = 1024
DFF = 4096
NS = 16
EPS = 1e-6
NEG = -30000.0
import os
KNT = int(os.environ.get('KNT', NT))
KPH2 = int(os.environ.get('KPH2', 1))
KSTAGE = int(os.environ.get('KSTAGE', 99))
KSUB = int(os.environ.get('KSUB', 99))


class Res:
    __slots__ = ("name", "w", "rd")

    def __init__(self, name):
        self.name = name
        self.w = None
        self.rd = []


class Op:
    __slots__ = ("eng", "fn", "reads", "writes", "dma", "deps", "signal", "cnt", "sem", "waits")

    def __init__(self, eng, fn, reads, writes, dma):
        self.eng = eng
        self.fn = fn
        self.reads = reads
        self.writes = writes
        self.dma = dma
        self.deps = []
        self.signal = False
        self.cnt = 0
        self.sem = None
        self.waits = []


def _res(lst):
    out = []
    for x in lst:
        if x is None:
            continue
        if isinstance(x, Res):
            out.append(x)
        elif isinstance(x, (list, tuple)):
            out.extend(_res(x))
        else:
            out.append(x.r)
    return out


class Prog:
    ENGS = ("pe", "act", "dve", "pool", "sp")

    def __init__(self, nc, n_dma_sems=56):
        self.nc = nc
        self.ops = []
        self.n_dma_sems = n_dma_sems
        self.engobj = {"pe": nc.tensor, "act": nc.scalar, "dve": nc.vector,
                       "pool": nc.gpsimd, "sp": nc.sync}

    def op(self, eng, fn, r=(), w=()):
        self.ops.append(Op(eng, fn, _res(r), _res(w), False))

    def dma(self, eng, fn, r=(), w=()):
        self.ops.append(Op(eng, fn, _res(r), _res(w), True))

    def barrier(self, allres):
        for e in ("pe", "act", "dve", "pool", "sp"):
            self.ops.append(Op(e, (lambda en: en.nop(nofuse=True)), [], _res(allres), False))

    def finalize(self, stack):
        nc = self.nc
        ops = self.ops
        dma_slot_last = [None] * self.n_dma_sems
        dma_i = 0
        for i, o in enumerate(ops):
            deps = set()
            for r in o.reads:
                if r.w is not None:
                    deps.add(r.w)
            for r in o.writes:
                if r.w is not None:
                    deps.add(r.w)
                for j in r.rd:
                    deps.add(j)
            if o.dma:
                slot = dma_i % self.n_dma_sems
                dma_i += 1
                o.sem = slot
                if dma_slot_last[slot] is not None:
                    deps.add(dma_slot_last[slot])
                dma_slot_last[slot] = i
            deps.discard(i)
            o.deps = sorted(deps)
            for r in o.reads:
                r.rd.append(i)
            for r in o.writes:
                r.w = i
                r.rd = []
            for j in o.deps:
                pj = ops[j]
                if pj.dma:
                    continue
                if pj.eng == "pe" and o.eng == "pe" and not o.dma:
                    continue
                pj.signal = True
        cnt = {e: 0 for e in self.ENGS}
        dcnt = [0] * self.n_dma_sems
        for o in ops:
            if o.dma:
                dcnt[o.sem] += 16
                o.cnt = dcnt[o.sem]
            elif o.signal:
                cnt[o.eng] += 1
                o.cnt = cnt[o.eng]
        seen = {e: {} for e in self.ENGS}
        for o in ops:
            need = {}
            for j in o.deps:
                pj = ops[j]
                if pj.dma:
                    key = ("d", pj.sem)
                else:
                    if pj.eng == "pe" and o.eng == "pe" and not o.dma:
                        continue
                    key = ("e", pj.eng)
                if pj.cnt > need.get(key, 0):
                    need[key] = pj.cnt
            s = seen[o.eng]
            for key, v in need.items():
                if s.get(key, 0) >= v:
                    continue
                s[key] = v
                o.waits.append((key, v))
        final_waits = [(("d", k), dcnt[k]) for k in range(self.n_dma_sems) if dcnt[k] > 0]
        final_waits += [(("e", e), cnt[e]) for e in self.ENGS if cnt[e] > 0 and e != "sp"]
        esem = {e: stack.enter_context(nc.semaphore("s_" + e)) for e in self.ENGS}
        dsem = [stack.enter_context(nc.semaphore("d_%d" % k)) for k in range(self.n_dma_sems)]

        def semof(key):
            return dsem[key[1]] if key[0] == "d" else esem[key[1]]

        n_ins = 0
        for o in ops:
            e = self.engobj[o.eng]
            for key, v in o.waits:
                e.wait_ge(semof(key), v)
                n_ins += 1
            ins = o.fn(e)
            n_ins += 1
            if o.dma:
                ins.then_inc(dsem[o.sem], 16)
            elif o.signal:
                ins.then_inc(esem[o.eng], 1)
        sp = self.engobj["sp"]
        for key, v in final_waits:
            if seen["sp"].get(key, 0) >= v:
                continue
            sp.wait_ge(semof(key), v)
        return n_ins


class View:
    __slots__ = ("t", "r")

    def __init__(self, ap, r):
        self.t = ap
        self.r = r

    def __getitem__(self, k):
        return self.t[k]


class Tl:
    __slots__ = ("t", "r")

    def __init__(self, t, name):
        self.t = t
        self.r = Res(name)

    def __getitem__(self, k):
        return self.t[k]


C_IDF, C_TRI, C_ONES, C_SEL127, C_MST, C_MIT, C_MI, C_SELR = (
    0, 128, 256, 384, 512, 640, 768, 896)
NCONST = 1408


def make_consts():
    c = np.zeros((128, NCONST), np.float32)
    s = np.arange(128)[:, None]
    f = np.arange(128)[None, :]
    c[:, C_IDF:C_IDF + 128] = (s == f)
    c[:, C_TRI:C_TRI + 128] = (s <= f)
    c[:, C_ONES:C_ONES + 128] = 1.0
    c[:, C_SEL127:C_SEL127 + 128] = (s == 127)
    mst = np.where(s < f, 0.0, NEG)
    mit = np.where(s <= f, 0.0, NEG)
    mi = np.where(f <= s, 0.0, NEG)
    c[:, C_MST:C_MST + 128] = mst
    c[:, C_MIT:C_MIT + 128] = mit
    c[:, C_MI:C_MI + 128] = mi
    for h in range(4):
        c[h, C_SELR + h * 128:C_SELR + (h + 1) * 128] = 1.0
    return c


W_QKV, W_MQK, W_GZ, W_MV, W_MO, W_GATE = 0, 1536, 2048, 2560, 3072, 3584
WIN_MOVES = [(0, 0, 1536), (1536, 2056, 512), (2048, 1536, 512), (2560, 2568, 512),
             (3072, 3080, 512), (3584, 2048, 8), (3592, 3596, 4), (3596, 3592, 4)]


def build_program():
    nc = bass.Bass("TRN2", target_bir_lowering=False)
    P = Prog(nc)

    def din(name, shape):
        return nc.dram_tensor(name, list(shape), F32, kind="ExternalInput").ap()

    def dout(name, shape):
        return nc.dram_tensor(name, list(shape), F32, kind="ExternalOutput").ap()

    x_d = din("x", [T, D])
    xs_d = din("xs", [NS, D])
    sconv_d = din("sconv", [NS, 3, 1536])
    sS_d = din("sS", [NS, 4, 128, 128])
    sC_d = din("sC", [NS, 4, 64, 128])
    sn_d = din("sn", [NS, 256])
    sm_d = din("sm", [NS, 4])
    win_d = din("w_in", [D, 3600])
    wout_d = din("w_out", [D, D])
    wup_d = din("w_up", [D, DFF])
    wdn_d = din("w_down", [DFF, D])
    consts_d = din("consts", [128, NCONST])
    gpre_d = din("gpre_fm", [128, 8])
    gpremlp_d = din("gpremlp_fm", [128, 8])
    cw_d = din("cw_fm", [128, 48])
    gpm_d = din("gpostmix", [1, D])
    gpl_d = din("gpostmlp", [1, D])
    small_d = din("small", [1, 16])
    gng_d = din("gdn_norm_g", [1, 128])
    mng_d = din("mlstm_norm_g", [1, 128])

    y_d = dout("y", [T, D])
    ys_d = dout("ys", [NS, D])
    pconv_d = dout("pconv", [3, 1536])
    pS_d = dout("pS", [4, 128, 128])
    pC_d = dout("pC", [4, 64, 128])
    pn_d = dout("pn", [4, 64])
    pm_d = dout("pm", [1, 4])
    oconv_d = dout("oconv", [NS, 3, 1536])
    oS_d = dout("oS", [NS, 4, 128, 128])
    oC_d = dout("oC", [NS, 4, 64, 128])
    on_d = dout("on", [NS, 256])
    om_d = dout("om", [NS, 4])

    def mm(out, lhsT, rhs, start, stop, r, w, skip=False):
        if skip:
            P.op("pe", lambda e, o=out, l=lhsT, rr=rhs, s=start, t=stop:
                 e.matmul(o, lhsT=l, rhs=rr, start=s, stop=t, skip_group_check=True), r, w)
        else:
            P.op("pe", lambda e, o=out, l=lhsT, rr=rhs, s=start, t=stop:
                 e.matmul(o, lhsT=l, rhs=rr, start=s, stop=t), r, w)

    def tr(out, in_, ident, r, w):
        P.op("pe", lambda e, o=out, i=in_, d=ident: e.transpose(o, i, d), r, w)

    def tt(eng, out, in0, in1, op, r, w):
        P.op(eng, lambda e, o=out, a=in0, b=in1, p=op: e.tensor_tensor(out=o, in0=a, in1=b, op=p), r, w)

    def ts(eng, out, in0, s1, s2, op0, op1, r, w, accum=None):
        if op1 is None:
            P.op(eng, lambda e, o=out, a=in0, x=s1, p0=op0:
                 e.tensor_scalar(out=o, in0=a, scalar1=x, scalar2=None, op0=p0), r, w)
        elif accum is None:
            P.op(eng, lambda e, o=out, a=in0, x=s1, y=s2, p0=op0, p1=op1:
                 e.tensor_scalar(out=o, in0=a, scalar1=x, scalar2=y, op0=p0, op1=p1), r, w)
        else:
            P.op(eng, lambda e, o=out, a=in0, x=s1, y=s2, p0=op0, p1=op1, ac=accum:
                 e.tensor_scalar(out=o, in0=a, scalar1=x, scalar2=y, op0=p0, op1=p1, accum_out=ac), r, w)

    def stt(eng, out, in0, scalar, in1, op0, op1, r, w):
        P.op(eng, lambda e, o=out, a=in0, s=scalar, b=in1, p0=op0, p1=op1:
             e.scalar_tensor_tensor(out=o, in0=a, scalar=s, in1=b, op0=p0, op1=p1), r, w)

    def act(out, in_, func, r, w, bias=None, scale=1.0, accum=None):
        kw = {}
        if bias is not None:
            kw["bias"] = bias
        if accum is not None:
            kw["accum_out"] = accum
        P.op("act", lambda e, o=out, i=in_, f=func, s=scale, k=kw:
             e.activation(out=o, in_=i, func=f, scale=s, **k), r, w)

    def cp(eng, out, in_, r, w):
        if eng == "act":
            act(out, in_, AF.Copy, r, w)
        else:
            P.op(eng, lambda e, o=out, i=in_: e.tensor_copy(out=o, in_=i), r, w)

    def red(eng, out, in_, op, r, w):
        P.op(eng, lambda e, o=out, i=in_, p=op: e.tensor_reduce(out=o, in_=i, axis=AX.X, op=p), r, w)

    def memset(eng, ap, val, w):
        P.op(eng, lambda e, a=ap, v=val: e.memset(a, v), [], w)

    def dma(q, out, in_, r, w, slow=False):
        if slow:
            P.dma(q, lambda e, o=out, i=in_: e.dma_start(out=o, in_=i, allow_slow_non_contiguous=True), r, w)
        else:
            P.dma(q, lambda e, o=out, i=in_: e.dma_start(out=o, in_=i), r, w)

    def rsqrt_small(out, in_, scale, r_, w_, tmp):
        ts("dve", tmp, in_, scale, EPS, ALU.mult, ALU.add, r_, [w_[0]])
        act(tmp, tmp, AF.Ln, [w_[0]], [w_[0]])
        act(out, tmp, AF.Exp, [w_[0]], w_, scale=-0.5)

    def bc3(ap, n_mid, n_in):
        return ap.unsqueeze(2).to_broadcast([ap.shape[0], n_mid, n_in])

    def bcm(ap, n_mid):
        return ap.unsqueeze(1).to_broadcast([ap.shape[0], n_mid, ap.shape[1]])

    with ExitStack() as top:
        def sb(stack, name, shape, dt=F32):
            return Tl(stack.enter_context(nc.sbuf_tensor("sb_" + name, list(shape), dt)), name)

        def ps(stack, name, shape, dt=F32):
            return Tl(stack.enter_context(nc.psum_tensor("ps_" + name, list(shape), dt)), name)

        Xt = top.enter_context(nc.sbuf_tensor("sb_X", [128, NT, D], F32))
        RX = [Res("X%d" % t) for t in range(NT)]
        XS1T = sb(top, "XS1T", [128, 8, NS])
        HNS = sb(top, "HNS", [128, 8, NS], BF16)
        IDF2 = sb(top, "IDF2", [128, 128])
        IDB = sb(top, "IDB", [128, 128], BF16)
        GPREMLP = sb(top, "GPREMLP", [128, 8])
        BK = [ps(top, "B%d" % i, [128, 512]) for i in range(7)]
        TPB = ps(top, "TPB", [128, 1024], BF16)

        dma("sp", GPREMLP[:], gpremlp_d, [], [GPREMLP])

        all_res_p1 = []

        with ExitStack() as p1:
            cur = [p1]

            def s1(name, shape, dt=F32):
                tl = sb(cur[0], name, shape, dt)
                all_res_p1.append(tl.r)
                return tl

            WIN = p1.enter_context(nc.sbuf_tensor("sb_WIN", [128, 8, 3600], BF16))
            RWIN = [Res("WIN%d" % i) for i in range(len(WIN_MOVES))]
            WOUT = s1("WOUT", [128, 8, D], BF16)
            CONST = s1("CONST", [128, NCONST])
            GPRE = s1("GPRE", [128, 8])
            CW = s1("CW", [128, 4, 12])
            GPM = s1("GPM", [128, D])
            SMB = s1("SMB", [128, 16])
            GNG = s1("GNG", [128, 128])
            MNG = s1("MNG", [128, 128])
            all_res_p1.extend(RWIN)

            win_v = win_d.rearrange("(k p) c -> p k c", p=128)
            for i, (dst, src, n) in enumerate(WIN_MOVES):
                dma("pool", WIN[:, :, dst:dst + n], win_v[:, :, src:src + n], [], [RWIN[i]])
            dma("pool", WOUT[:], wout_d.rearrange("(k p) c -> p k c", p=128), [], [WOUT])
            dma("sp", CONST[:], consts_d, [], [CONST])
            dma("sp", GPRE[:], gpre_d, [], [GPRE])
            dma("sp", CW[:], cw_d.rearrange("p (j c) -> p j c", j=4), [], [CW])
            dma("sp", GPM[:], gpm_d.partition_broadcast(128), [], [GPM])
            dma("sp", SMB[:], small_d.partition_broadcast(128), [], [SMB])
            dma("sp", GNG[:], gng_d.partition_broadcast(128), [], [GNG])
            dma("sp", MNG[:], mng_d.partition_broadcast(128), [], [MNG])

            IDF = CONST[:, C_IDF:C_IDF + 128]
            TRI = CONST[:, C_TRI:C_TRI + 128]
            ONES = CONST[:, C_ONES:C_ONES + 128]
            SEL127 = CONST[:, C_SEL127:C_SEL127 + 128]
            MST = CONST[:, C_MST:C_MST + 128]
            MIT = CONST[:, C_MIT:C_MIT + 128]
            MI = CONST[:, C_MI:C_MI + 128]
            SELR = CONST[0:4, C_SELR:C_SELR + 512]
            ONES4 = CONST[0:4, C_ONES:C_ONES + 128]

            def rwin(c0, c1):
                out = []
                for i, (dst, src, n) in enumerate(WIN_MOVES):
                    if dst < c1 and c0 < dst + n:
                        out.append(RWIN[i])
                return out

            cp("dve", IDB[:], IDF, [CONST], [IDB])
            cp("pool", IDF2[:], IDF, [CONST], [IDF2])

            NEGC = s1("NEGC", [128, 12])
            SGN = s1("SGN", [128, 12])
            BIA = s1("BIA", [128, 12])
            memset("pool", NEGC[:], -1.0, [NEGC])
            act(NEGC[:, 4:8], SMB[:, 0:4], AF.Exp, [SMB, NEGC], [NEGC])
            ts("dve", NEGC[:, 4:8], NEGC[:, 4:8], -1.0, None, ALU.mult, None, [NEGC], [NEGC])
            memset("pool", SGN[:], -1.0, [SGN])
            memset("pool", SGN[:, 4:8], 1.0, [SGN])
            memset("pool", BIA[:], 0.0, [BIA])
            cp("dve", BIA[:, 4:8], SMB[:, 4:8], [SMB, BIA], [BIA])
            ts("dve", BIA[:, 8:12], SMB[:, 12:16], -1.0, None, ALU.mult, None, [SMB, BIA], [BIA])

            p1a = ExitStack()
            cur[0] = p1a
            XN = s1("XN", [128, D], BF16)
            HT = s1("HT", [128, 8, 128], BF16)
            PCC = [s1("PCC%d" % i, [128, 131]) for i in range(3)]
            ACC = [s1("ACC%d" % i, [128, 128]) for i in range(3)]
            PTMP = s1("PTMP", [128, 128])
            TAIL = s1("TAIL", [128, 12, 3])
            QKF = s1("QKF", [128, 8, 128])
            GQT = s1("GQT", [128, 4, 128], BF16)
            GKT = s1("GKT", [128, 4, 128], BF16)
            GVT = s1("GVT", [128, 4, 128], BF16)
            MQT = s1("MQT", [64, 4, 128], BF16)
            MKT = s1("MKT", [64, 4, 128], BF16)
            ZSG = s1("ZSG", [128, 512])
            MOSG = s1("MOSG", [128, 512])
            MVt = s1("MVt", [128, 4, 128], BF16)
            GT = s1("GT", [128, 16])
            ARG = s1("ARG", [128, 12])
            GL = s1("GL", [128, 12])
            BETA = s1("BETA", [128, 4])
            IG = s1("IG", [128, 4])
            GF = s1("GF", [128, 8])
            COLS = s1("COLS", [128, 20])
            RT = s1("RT", [4, 5, 128])
            BD = s1("BD", [4, 4, 128])
            QKD = s1("QKD", [128, 4, 128], BF16)
            MB = [s1("MB%d" % i, [128, 512]) for i in range(2)]
            LB = [s1("LB%d" % i, [128, 512]) for i in range(2)]
            RR = s1("RR", [128, 512])
            BINV = s1("BINV", [128, 4, 128], BF16)
            GKt = s1("GKt", [128, 4, 128], BF16)
            XK = s1("XK", [128, 4, 128], BF16)
            KEND = s1("KEND", [128, 4, 128], BF16)
            GVb = s1("GVb", [128, 4, 128], BF16)
            SC4 = [s1("SC4_%d" % i, [128, 8]) for i in range(6)]
            LASTG = s1("LASTG", [128, 8])
            WKT = s1("WKT", [128, 4, 128], BF16)
            S32 = s1("S32", [128, 4, 128])
            SBh = s1("SBh", [128, 4, 128], BF16)
            U = s1("U", [128, 4, 128], BF16)
            TMP = [s1("TMP%d" % i, [128, 512]) for i in range(2)]
            WV = LB[1]
            PK = s1("PK", [128, 4, 64], BF16)
            EXPQ = TMP[1]
            EXPA = MB[0]
            MIX = XN
            MIXT = HT
            MKt = s1("MKt", [128, 4, 64], BF16)
            PQKb = s1("PQKb", [128, 4, 128], BF16)
            PQKT = s1("PQKT", [128, 4, 128], BF16)
            SG4 = [s1("SG4_%d" % i, [128, 8]) for i in range(6)]
            C32 = s1("C32", [64, 4, 128])
            CBh = s1("CBh", [64, 4, 128], BF16)
            N32 = s1("N32", [64, 4])
            NBh = s1("NBh", [64, 4, 2], BF16)
            MBC = s1("MBC", [128, 4])
            DMAX = s1("DMAX", [128, 4])
            T12 = s1("T12", [128, 12])
            WLC = s1("WLC", [64, 4])
            ONEB = s1("ONEB", [128, 2], BF16)
            for b in BK:
                all_res_p1.append(b.r)
            all_res_p1.append(TPB.r)

            memset("pool", TAIL[:], 0.0, [TAIL])
            memset("pool", S32[:], 0.0, [S32])
            memset("pool", SBh[:], 0.0, [SBh])
            memset("pool", C32[:], 0.0, [C32])
            memset("pool", CBh[:], 0.0, [CBh])
            memset("pool", N32[:], 0.0, [N32])
            memset("pool", NBh[:], 0.0, [NBh])
            memset("pool", MBC[:], 0.0, [MBC])
            memset("pool", ONEB[:], 1.0, [ONEB])

            def headnorm_gate(src, gate, dst, t0, ss, rs):
                tt("pool", t0[:], src[:], src[:], ALU.mult, [src], [t0])
                red("dve", ss[:, 0:4], t0[:].rearrange("p (h e) -> p h e", h=4), ALU.add, [t0], [ss])
                rsqrt_small(rs[:, 0:4], ss[:, 0:4], 1.0 / 128, [ss], [rs], rs[:, 4:8])
                tt("dve", t0[:].rearrange("p (h e) -> p h e", h=4), src[:].rearrange("p (h e) -> p h e", h=4),
                   bc3(rs[:, 0:4], 4, 128), ALU.mult, [src, rs], [t0])
                tt("dve", dst, t0[:], gate[:], ALU.mult, [t0, gate], [MIX])

            for t in range(KNT):
                Xv = Xt[:, t, :]
                dma("sp", Xv, x_d[t * 128:(t + 1) * 128, :], [], [RX[t]])
                ss, rs = SC4[0], SC4[1]
                act(XN[:], Xv, AF.Square, [RX[t]], [XN, ss], accum=ss[:, 0:1])
                rsqrt_small(rs[:, 0:1], ss[:, 0:1], 1.0 / D, [ss], [rs], rs[:, 1:2])
                ts("dve", XN[:], Xv, rs[:, 0:1], None, ALU.mult, None, [RX[t], rs], [XN])
                for k in range(8):
                    tr(TPB[:, k * 128:(k + 1) * 128], XN[:, k * 128:(k + 1) * 128], IDB[:], [XN, IDB], [TPB])
                tt("dve", HT[:], TPB[:].rearrange("p (k t) -> p k t", k=8), bc3(GPRE[:], 8, 128), ALU.mult,
                   [TPB, GPRE], [HT])

                def b_mm(ch):
                    bk = BK[ch % 2]
                    c0 = ch * 128
                    for k in range(8):
                        mm(bk[:, 0:128], WIN[:, k, c0:c0 + 128], HT[:, k, :], k == 0, k == 7,
                           [HT] + rwin(c0, c0 + 128), [bk])

                def b_copy(ch):
                    bk, pc = BK[ch % 2], PCC[ch % 3]
                    cp("act", pc[:, 3:131], bk[:, 0:128], [bk], [pc])
                    cp("pool", pc[:, 0:3], TAIL[:, ch, :], [TAIL], [pc])

                def b_conv(ch):
                    pc, ac = PCC[ch % 3], ACC[ch % 3]
                    if ch % 3 == 2:
                        ts("pool", ac[:], pc[:, 0:128], CW[:, 0, ch:ch + 1], None, ALU.mult, None, [pc, CW], [ac])
                        for j in range(1, 4):
                            ts("pool", PTMP[:], pc[:, j:j + 128], CW[:, j, ch:ch + 1], None, ALU.mult, None,
                               [pc, CW], [PTMP])
                            tt("pool", ac[:], ac[:], PTMP[:], ALU.add, [ac, PTMP], [ac])
                    else:
                        ts("dve", ac[:], pc[:, 0:128], CW[:, 0, ch:ch + 1], None, ALU.mult, None, [pc, CW], [ac])
                        for j in range(1, 4):
                            stt("dve", ac[:], pc[:, j:j + 128], CW[:, j, ch:ch + 1], ac[:], ALU.mult, ALU.add,
                                [pc, CW, ac], [ac])
                    cp("pool", TAIL[:, ch, :], pc[:, 128:131], [pc], [TAIL])

                def b_silu(ch):
                    ac = ACC[ch % 3]
                    if ch < 8:
                        act(QKF[:, ch, :], ac[:], AF.Silu, [ac], [QKF])
                    else:
                        act(GVT[:, ch - 8, :], ac[:], AF.Silu, [ac], [GVT])

                for i in range(12 + 3):
                    if i < 12:
                        b_mm(i)
                    if 0 <= i - 1 < 12:
                        b_copy(i - 1)
                    if 0 <= i - 2 < 12:
                        b_conv(i - 2)
                    if 0 <= i - 3 < 12:
                        b_silu(i - 3)

                def tok_proj(bk, c0, n):
                    for k in range(8):
                        mm(bk[:, 0:n], HT[:, k, :], WIN[:, k, c0:c0 + n], k == 0, k == 7,
                           [HT] + rwin(c0, c0 + n), [bk])
                tok_proj(BK[0], W_GZ, 512)
                act(TMP[0][:], BK[0][:], AF.Silu, [BK[0]], [TMP[0]])
                tt("pool", ZSG[:].rearrange("p (h e) -> p h e", h=4), TMP[0][:].rearrange("p (h e) -> p h e", h=4),
                   bcm(GNG[:], 4), ALU.mult, [TMP[0], GNG], [ZSG])
                tok_proj(BK[1], W_MO, 512)
                act(TMP[1][:], BK[1][:], AF.Sigmoid, [BK[1]], [TMP[1]])
                tt("pool", MOSG[:].rearrange("p (h e) -> p h e", h=4), TMP[1][:].rearrange("p (h e) -> p h e", h=4),
                   bcm(MNG[:], 4), ALU.mult, [TMP[1], MNG], [MOSG])
                for hh in range(8):
                    bk = BK[hh % 2]
                    c0 = W_MQK + hh * 64
                    for k in range(8):
                        mm(bk[0:64, 0:128], WIN[:, k, c0:c0 + 64], HT[:, k, :], k == 0, k == 7,
                           [HT] + rwin(c0, c0 + 64), [bk])
                    if hh < 4:
                        cp("act", MQT[:, hh, :], bk[0:64, 0:128], [bk], [MQT])
                    else:
                        act(MKT[:, hh - 4, :], bk[0:64, 0:128], AF.Copy, [bk], [MKT], scale=0.125)
                tok_proj(BK[0], W_MV, 512)
                cp("act", MVt[:].rearrange("p h e -> p (h e)"), BK[0][:], [BK[0]], [MVt])
                tok_proj(BK[1], W_GATE, 16)
                cp("dve", GT[:], BK[1][:, 0:16], [BK[1]], [GT])
                tt("dve", ARG[:], GT[:, 0:12], SGN[:], ALU.mult, [GT, SGN], [ARG])
                tt("dve", ARG[:], ARG[:], BIA[:], ALU.add, [ARG, BIA], [ARG])
                for hf in range(2):
                    act(TMP[hf][:], QKF[:, hf * 4:(hf + 1) * 4, :].rearrange("p a b -> p (a b)"), AF.Square,
                        [QKF], [TMP[hf]])
                    mm(BK[2 + hf][:], ONES, TMP[hf][:], True, True, [CONST, TMP[hf]], [BK[2 + hf]])
                for hf in range(2):
                    ts("dve", TMP[hf][:], BK[2 + hf][:], 1.0, EPS, ALU.mult, ALU.add, [BK[2 + hf]], [TMP[hf]])
                    act(TMP[hf][:], TMP[hf][:], AF.Ln, [TMP[hf]], [TMP[hf]])
                    act(TMP[hf][:], TMP[hf][:], AF.Exp, [TMP[hf]], [TMP[hf]], scale=-0.5)
                stt("dve", GQT[:].rearrange("p a b -> p (a b)"), QKF[:, 0:4, :].rearrange("p a b -> p (a b)"),
                    128.0 ** -0.5, TMP[0][:], ALU.mult, ALU.mult, [QKF, TMP[0]], [GQT])
                tt("pool", GKT[:].rearrange("p a b -> p (a b)"), QKF[:, 4:8, :].rearrange("p a b -> p (a b)"),
                   TMP[1][:], ALU.mult, [QKF, TMP[1]], [GKT])
                act(ARG[:], ARG[:], AF.Exp, [ARG], [ARG])
                act(ARG[:], ARG[:], AF.Ln, [ARG], [ARG], bias=1.0)
                tt("dve", GL[:], ARG[:], NEGC[:], ALU.mult, [ARG, NEGC], [GL])
                act(BETA[:], GL[:, 0:4], AF.Exp, [GL], [BETA])
                tt("dve", IG[:], GT[:, 12:16], SMB[:, 8:12], ALU.add, [GT, SMB], [IG])

                mm(BK[2][:, 0:8], TRI, GL[:, 4:12], True, True, [CONST, GL], [BK[2]])
                cp("dve", GF[:], BK[2][:, 0:8], [BK[2]], [GF])
                cp("pool", COLS[:, 0:4], GF[:, 0:4], [GF], [COLS])
                tt("dve", COLS[:, 4:8], GF[:, 0:4], GL[:, 0:4], ALU.add, [GF, GL], [COLS])
                cp("pool", COLS[:, 8:12], GF[:, 4:8], [GF], [COLS])
                tt("dve", COLS[:, 12:16], IG[:], GF[:, 4:8], ALU.subtract, [IG, GF], [COLS])
                ts("dve", COLS[:, 16:20], GF[:, 0:4], -1.0, None, ALU.mult, None, [GF], [COLS])
                for j in range(4):
                    tr(BK[3][0:4, j * 128:(j + 1) * 128], COLS[:, 4 * j:4 * j + 4], IDF, [COLS, CONST], [BK[3]])
                tr(BK[4][0:4, 0:128], COLS[:, 16:20], IDF, [COLS, CONST], [BK[4]])
                cp("dve", RT[:, 0:4, :].rearrange("p a b -> p (a b)"), BK[3][0:4, :], [BK[3]], [RT])
                cp("dve", RT[:, 4, :], BK[4][0:4, 0:128], [BK[4]], [RT])
                SELR3 = SELR.rearrange("p (a b) -> p a b", a=4)
                tt("dve", BD[:], bcm(RT[:, 1, :], 4), SELR3, ALU.mult, [RT, CONST], [BD])
                mm(BK[2][:], ONES4, BD[:].rearrange("p a b -> p (a b)"), True, False, [CONST, BD], [BK[2]])
                mm(BK[2][:], RT[:, 4, :], SELR, False, True, [RT, CONST], [BK[2]])
                tt("dve", EXPA[:].rearrange("p (h c) -> p h c", h=4), BK[2][:].rearrange("p (h c) -> p h c", h=4),
                   bcm(MST, 4), ALU.add, [BK[2], CONST], [EXPA])
                act(EXPA[:], EXPA[:], AF.Exp, [EXPA], [EXPA])
                tt("dve", BD[:], bcm(RT[:, 0, :], 4), SELR3, ALU.mult, [RT, CONST], [BD])
                mm(BK[3][:], ONES4, BD[:].rearrange("p a b -> p (a b)"), True, False, [CONST, BD], [BK[3]])
                mm(BK[3][:], RT[:, 4, :], SELR, False, True, [RT, CONST], [BK[3]])
                tt("dve", EXPQ[:].rearrange("p (h c) -> p h c", h=4), BK[3][:].rearrange("p (h c) -> p h c", h=4),
                   bcm(MIT, 4), ALU.add, [BK[3], CONST], [EXPQ])
                act(EXPQ[:], EXPQ[:], AF.Exp, [EXPQ], [EXPQ])
                for h in range(4):
                    mm(BK[4][:, h * 128:(h + 1) * 128], GKT[:, h, :], GKT[:, h, :], True, True, [GKT], [BK[4]])
                for h in range(4):
                    mm(BK[5][:, h * 128:(h + 1) * 128], GKT[:, h, :], GQT[:, h, :], True, True, [GKT, GQT], [BK[5]])
                tt("dve", MB[0][:], BK[4][:], EXPA[:], ALU.mult, [BK[4], EXPA], [MB[0]])
                tt("dve", QKD[:].rearrange("p h c -> p (h c)"), BK[5][:], EXPQ[:], ALU.mult, [BK[5], EXPQ], [QKD])

                for h in range(4):
                    tr(TPB[:, h * 128:(h + 1) * 128], GKT[:, h, :], IDB[:], [GKT, IDB], [TPB])
                    tr(TPB[:, 512 + h * 128:512 + (h + 1) * 128], GVT[:, h, :], IDB[:], [GVT, IDB], [TPB])
                cp("act", GKt[:].rearrange("p h d -> p (h d)"), TPB[:, 0:512], [TPB], [GKt])
                EG, BEG, EGL, GEND = SC4[0], SC4[1], SC4[2], SC4[3]
                act(EG[:, 0:4], GF[:, 0:4], AF.Exp, [GF], [EG])
                tt("dve", BEG[:, 0:4], EG[:, 0:4], BETA[:], ALU.mult, [EG, BETA], [BEG])
                mm(BK[2][:, 0:8], SEL127, GF[:], True, True, [CONST, GF], [BK[2]])
                cp("dve", LASTG[:], BK[2][:, 0:8], [BK[2]], [LASTG])
                tt("dve", EGL[:, 0:4], LASTG[:, 0:4], GF[:, 0:4], ALU.subtract, [LASTG, GF], [EGL])
                act(EGL[:, 0:4], EGL[:, 0:4], AF.Exp, [EGL], [EGL])
                act(GEND[:, 0:4], LASTG[:, 0:4], AF.Exp, [LASTG], [GEND])
                tt("dve", XK[:], GKt[:], bc3(BEG[:, 0:4], 4, 128), ALU.mult, [GKt, BEG], [XK])
                tt("pool", KEND[:], GKt[:], bc3(EGL[:, 0:4], 4, 128), ALU.mult, [GKt, EGL], [KEND])
                tt("dve", GVb[:], TPB[:, 512:1024].rearrange("p (h e) -> p h e", h=4), bc3(BETA[:], 4, 128),
                   ALU.mult, [TPB, BETA], [GVb])
                def gen_EF():
                    for h in range(4):
                        tr(BK[2][:, h * 128:(h + 1) * 128], MB[0][:, h * 128:(h + 1) * 128], IDF, [MB[0], CONST], [BK[2]])
                    yield
                    cp("act", LB[0][:], BK[2][:], [BK[2]], [LB[0]])
                    yield
                    tt("dve", RR[:].rearrange("p (h c) -> p h c", h=4), bcm(IDF, 4),
                       MB[0][:].rearrange("p (h c) -> p h c", h=4), ALU.subtract, [CONST, MB[0]], [RR])
                    yield
                    NLEV = 6
                    yield
                    for k in range(NLEV):
                        a, b = k % 2, (k + 1) % 2
                        for h in range(4):
                            sl = slice(h * 128, (h + 1) * 128)
                            mm(BK[3][:, sl], MB[a][:, sl], LB[a][:, sl], True, True, [MB[a], LB[a]], [BK[3]])
                        yield
                        if k < NLEV - 1:
                            for h in range(4):
                                sl = slice(h * 128, (h + 1) * 128)
                                mm(BK[4][:, sl], LB[a][:, sl], MB[a][:, sl], True, True, [MB[a], LB[a]], [BK[4]])
                            yield
                        cp("act", LB[b][:], BK[3][:], [BK[3]], [LB[b]])
                        yield
                        if k < NLEV - 1:
                            cp("dve", MB[b][:], BK[4][:], [BK[4]], [MB[b]])
                            yield
                        for h in range(4):
                            sl = slice(h * 128, (h + 1) * 128)
                            mm(BK[5][:, sl], LB[b][:, sl], RR[:, sl], True, True, [LB[b], RR], [BK[5]])
                        yield
                        tt("dve", RR[:], RR[:], BK[5][:], ALU.add, [RR, BK[5]], [RR])
                        yield
                    cp("act", BINV[:].rearrange("p h c -> p (h c)"), RR[:], [RR], [BINV])
                    yield
                    for h in range(4):
                        sl = slice(h * 128, (h + 1) * 128)
                        mm(BK[2][:, sl], BINV[:, h, :], GVb[:, h, :], True, True, [BINV, GVb], [BK[2]])
                        mm(BK[3][:, sl], XK[:, h, :], BINV[:, h, :], True, True, [BINV, XK], [BK[3]])
                    yield
                    cp("act", WV[:], BK[2][:], [BK[2]], [WV])
                    yield
                    cp("dve", WKT[:].rearrange("p h c -> p (h c)"), BK[3][:], [BK[3]], [WKT])
                    yield
                    for h in range(4):
                        sl = slice(h * 128, (h + 1) * 128)
                        mm(BK[4][:, sl], WKT[:, h, :], SBh[:, h, :], True, True, [WKT, SBh], [BK[4]])
                        mm(BK[5][:, sl], GQT[:, h, :], SBh[:, h, :], True, True, [GQT, SBh], [BK[5]])
                    yield
                    tt("dve", U[:].rearrange("p h e -> p (h e)"), WV[:], BK[4][:], ALU.subtract, [WV, BK[4]], [U])
                    yield
                    for h in range(4):
                        sl = slice(h * 128, (h + 1) * 128)
                        mm(BK[2][:, sl], QKD[:, h, :], U[:, h, :], True, True, [QKD, U], [BK[2]])
                        mm(BK[3][:, sl], KEND[:, h, :], U[:, h, :], True, True, [KEND, U], [BK[3]])
                    yield
                    OG = MB[1]
                    yield
                    tt("dve", OG[:].rearrange("p (h e) -> p h e", h=4), BK[5][:].rearrange("p (h e) -> p h e", h=4),
                       bc3(EG[:, 0:4], 4, 128), ALU.mult, [BK[5], EG], [OG])
                    yield
                    tt("dve", OG[:], OG[:], BK[2][:], ALU.add, [OG, BK[2]], [OG])
                    yield
                    for h in range(4):
                        stt("dve", S32[:, h, :], S32[:, h, :], GEND[:, h:h + 1], BK[3][:, h * 128:(h + 1) * 128],
                            ALU.mult, ALU.add, [S32, GEND, BK[3]], [S32])
                    yield
                    cp("act", SBh[:], S32[:], [S32], [SBh])
                    yield
                    headnorm_gate(OG, ZSG, MIX[:, 0:512], LB[0], SC4[4], SC4[5])
                    yield
                def gen_G():
                    tt("dve", BD[:], bcm(RT[:, 3, :], 4), SELR3, ALU.mult, [RT, CONST], [BD])
                    yield
                    mm(BK[6][:], ONES4, BD[:].rearrange("p a b -> p (a b)"), True, False, [CONST, BD], [BK[6]])
                    yield
                    mm(BK[6][:], RT[:, 2, :], SELR, False, True, [RT, CONST], [BK[6]])
                    yield
                    PD = TMP[0]
                    yield
                    tt("dve", PD[:].rearrange("p (h s) -> p h s", h=4), BK[6][:].rearrange("p (h s) -> p h s", h=4),
                       bcm(MI, 4), ALU.add, [BK[6], CONST], [PD])
                    yield
                    red("dve", DMAX[:], PD[:].rearrange("p (h s) -> p h s", h=4), ALU.max, [PD], [DMAX])
                    yield
                    for h in range(4):
                        mm(BK[0][:, h * 128:(h + 1) * 128], MQT[:, h, :], MKT[:, h, :], True, True, [MQT, MKT], [BK[0]])
                    yield
                    for h in range(4):
                        tr(TPB[:, h * 64:(h + 1) * 64], MKT[:, h, :], IDB[0:64, 0:64], [MKT, IDB], [TPB])
                    yield
                    cp("act", MKt[:].rearrange("p h d -> p (h d)"), TPB[:, 0:256], [TPB], [MKt])
                    yield
                    Bv, MT, WP, EMT = SG4[0], SG4[1], SG4[2], SG4[3]
                    yield
                    tt("dve", Bv[:, 0:4], GF[:, 4:8], MBC[:], ALU.add, [GF, MBC], [Bv])
                    yield
                    tt("dve", MT[:, 0:4], Bv[:, 0:4], DMAX[:], ALU.max, [Bv, DMAX], [MT])
                    yield
                    tt("dve", WP[:, 0:4], Bv[:, 0:4], MT[:, 0:4], ALU.subtract, [Bv, MT], [WP])
                    yield
                    act(WP[:, 0:4], WP[:, 0:4], AF.Exp, [WP], [WP])
                    yield
                    tt("dve", PD[:].rearrange("p (h s) -> p h s", h=4), PD[:].rearrange("p (h s) -> p h s", h=4),
                       bc3(MT[:, 0:4], 4, 128), ALU.subtract, [PD, MT], [PD])
                    yield
                    act(PD[:], PD[:], AF.Exp, [PD], [PD])
                    yield
                    tt("dve", PD[:], PD[:], BK[0][:], ALU.mult, [PD, BK[0]], [PD])
                    yield
                    RS = SG4[4]
                    yield
                    red("dve", RS[:, 0:4], PD[:].rearrange("p (h s) -> p h s", h=4), ALU.add, [PD], [RS])
                    yield
                    cp("act", PQKb[:].rearrange("p h s -> p (h s)"), PD[:], [PD], [PQKb])
                    yield
                    for h in range(4):
                        tr(TPB[:, 512 + h * 128:512 + (h + 1) * 128], PQKb[:, h, :], IDB[:], [PQKb, IDB], [TPB])
                    yield
                    cp("act", PQKT[:].rearrange("p h s -> p (h s)"), TPB[:, 512:1024], [TPB], [PQKT])
                    yield
                    for h in range(4):
                        sl = slice(h * 128, (h + 1) * 128)
                        mm(BK[1][:, sl], PQKT[:, h, :], MVt[:, h, :], True, True, [PQKT, MVt], [BK[1]])
                        mm(BK[0][:, sl], MQT[:, h, :], CBh[:, h, :], True, True, [MQT, CBh], [BK[0]])
                        mm(BK[6][:, 2 * h:2 * h + 2], MQT[:, h, :], NBh[:, h, :], True, True, [MQT, NBh], [BK[6]])
                    yield
                    NUM = TMP[0]
                    yield
                    tt("dve", NUM[:].rearrange("p (h e) -> p h e", h=4), BK[0][:].rearrange("p (h e) -> p h e", h=4),
                       bc3(WP[:, 0:4], 4, 128), ALU.mult, [BK[0], WP], [NUM])
                    yield
                    tt("dve", NUM[:], NUM[:], BK[1][:], ALU.add, [NUM, BK[1]], [NUM])
                    yield
                    DEN = SG4[5]
                    yield
                    tt("dve", DEN[:, 0:4], BK[6][:, 0:8].rearrange("p (h two) -> p h two", two=2)[:, :, 0], WP[:, 0:4], ALU.mult, [BK[6], WP], [DEN])
                    yield
                    tt("dve", DEN[:, 0:4], DEN[:, 0:4], RS[:, 0:4], ALU.add, [DEN, RS], [DEN])
                    yield
                    act(EMT[:, 0:4], MT[:, 0:4], AF.Exp, [MT], [EMT], scale=-1.0)
                    yield
                    ts("dve", DEN[:, 4:8], DEN[:, 0:4], -1.0, None, ALU.mult, None, [DEN], [DEN])
                    yield
                    tt("dve", DEN[:, 0:4], DEN[:, 0:4], DEN[:, 4:8], ALU.max, [DEN], [DEN])
                    yield
                    tt("dve", DEN[:, 0:4], DEN[:, 0:4], EMT[:, 0:4], ALU.max, [DEN, EMT], [DEN])
                    yield
                    P.op("dve", lambda e, d=DEN: e.reciprocal(out=d[:, 0:4], in_=d[:, 0:4]), [DEN], [DEN])
                    yield
                    tt("dve", NUM[:].rearrange("p (h e) -> p h e", h=4), NUM[:].rearrange("p (h e) -> p h e", h=4),
                       bc3(DEN[:, 0:4], 4, 128), ALU.mult, [NUM, DEN], [NUM])
                    yield
                    cp("pool", T12[:, 0:4], MT[:, 0:4], [MT], [T12])
                    yield
                    tt("dve", T12[:, 4:8], GF[:, 4:8], MT[:, 0:4], ALU.subtract, [GF, MT], [T12])
                    yield
                    cp("pool", T12[:, 8:12], WP[:, 0:4], [WP], [T12])
                    yield
                    mm(BK[6][:, 16:28], SEL127, T12[:], True, True, [CONST, T12], [BK[6]])
                    yield
                    cp("dve", MBC[:], BK[6][:, 16:20], [BK[6]], [MBC])
                    yield
                    PEND = SG4[4]
                    yield
                    tt("dve", PEND[:, 4:8], COLS[:, 12:16], BK[6][:, 20:24], ALU.add, [COLS, BK[6]], [PEND])
                    yield
                    act(PEND[:, 4:8], PEND[:, 4:8], AF.Exp, [PEND], [PEND])
                    yield
                    cp("dve", WLC[:], BK[6][0:64, 24:28], [BK[6]], [WLC])
                    yield
                    tt("dve", PK[:], MKt[:], bc3(PEND[:, 4:8], 4, 64), ALU.mult, [MKt, PEND], [PK])
                    yield
                    for h in range(4):
                        mm(BK[1][0:64, h * 128:(h + 1) * 128], PK[:, h, :], MVt[:, h, :], True, True, [PK, MVt], [BK[1]])
                    yield
                    for h in range(4):
                        mm(BK[6][0:64, 32 + 2 * h:34 + 2 * h], PK[:, h, :], ONEB[:], True, True, [PK, ONEB], [BK[6]])
                    yield
                    for h in range(4):
                        stt("dve", C32[:, h, :], C32[:, h, :], WLC[:, h:h + 1], BK[1][0:64, h * 128:(h + 1) * 128],
                            ALU.mult, ALU.add, [C32, WLC, BK[1]], [C32])
                    yield
                    tt("dve", N32[:], N32[:], WLC[:], ALU.mult, [N32, WLC], [N32])
                    yield
                    tt("dve", N32[:], N32[:], BK[6][0:64, 32:40].rearrange("p (a two) -> p a two", two=2)[:, :, 0],
                       ALU.add, [N32, BK[6]], [N32])
                    yield
                    cp("act", CBh[:], C32[:], [C32], [CBh])
                    yield
                    cp("act", NBh[:, :, 0], N32[:], [N32], [NBh])
                    yield
                    headnorm_gate(NUM, MOSG, MIX[:, 512:1024], TMP[1], SG4[4], SG4[5])
                    yield

                def interleave(ga, gb, na, nb):
                    a = b = True
                    while a or b:
                        for _ in range(na):
                            if a:
                                try:
                                    next(ga)
                                except StopIteration:
                                    a = False
                        for _ in range(nb):
                            if b:
                                try:
                                    next(gb)
                                except StopIteration:
                                    b = False
                interleave(gen_EF(), gen_G(), 2, 1)

                for k in range(8):
                    tr(TPB[:, k * 128:(k + 1) * 128], MIX[:, k * 128:(k + 1) * 128], IDB[:], [MIX, IDB], [TPB])
                cp("act", MIXT[:].rearrange("p k t -> p (k t)"), TPB[:], [TPB], [MIXT])
                for eh in range(2):
                    for k in range(8):
                        mm(BK[eh][:], MIXT[:, k, :], WOUT[:, k, eh * 512:(eh + 1) * 512], k == 0, k == 7,
                           [MIXT, WOUT], [BK[eh]])
                ss, rs = SC4[0], SC4[1]
                for eh in range(2):
                    act(XN[:, eh * 512:(eh + 1) * 512], BK[eh][:], AF.Square, [BK[eh]], [XN, ss],
                        accum=ss[:, eh:eh + 1])
                tt("dve", ss[:, 2:3], ss[:, 0:1], ss[:, 1:2], ALU.add, [ss], [ss])
                rsqrt_small(rs[:, 0:1], ss[:, 2:3], 1.0 / D, [ss], [rs], rs[:, 1:2])
                for eh in range(2):
                    sl = slice(eh * 512, (eh + 1) * 512)
                    stt("dve", TMP[eh][:], BK[eh][:], rs[:, 0:1], GPM[:, sl], ALU.mult, ALU.mult,
                        [BK[eh], rs, GPM], [TMP[eh]])
                    tt("pool", Xt[:, t, sl], Xt[:, t, sl], TMP[eh][:], ALU.add, [RX[t], TMP[eh]], [RX[t]])

            dma("sp", pS_d.rearrange("h d e -> d h e"), S32[:], [S32], [])
            dma("sp", pC_d.rearrange("h d e -> d h e"), C32[:], [C32], [])
            dma("sp", pn_d.rearrange("h d -> d h"), N32[:], [N32], [], slow=True)
            dma("sp", pm_d, MBC[0:1, :], [MBC], [])
            for ch in range(12):
                tr(BK[2 + ch // 4][0:3, (ch % 4) * 128:(ch % 4 + 1) * 128], TAIL[:, ch, :], IDF, [TAIL, CONST],
                   [BK[2 + ch // 4]])
            for g in range(3):
                cp("dve", TMP[g % 2][0:3, :], BK[2 + g][0:3, :], [BK[2 + g]], [TMP[g % 2]])
                dma("sp", pconv_d[:, g * 512:(g + 1) * 512], TMP[g % 2][0:3, :], [TMP[g % 2]], [])


            P.barrier(all_res_p1 + RX + [XS1T.r, HNS.r, IDF2.r, IDB.r, GPREMLP.r])
            p1a.close()
            p1b = ExitStack()
            cur[0] = p1b
            R_ = NS
            GRP = 1
            XS = s1("XS", [R_, D])
            XNs = s1("XNs", [R_, D], BF16)
            HTs = s1("HTs", [128, 8, R_], BF16)
            SCV = s1("SCV", [R_, 512])
            XPT = s1("XPT", [128, 12, 4, R_])
            ACs = s1("ACs", [128, 12, R_])
            TM12 = s1("TM12", [128, 12, R_])
            SQs = s1("SQs", [128, 8, R_])
            GQs = s1("GQs", [128, 4, R_])
            GKs = s1("GKs", [128, 4, R_])
            GVs = s1("GVs", [128, 4, R_])
            MQs = s1("MQs", [64, 4, R_])
            MKs = s1("MKs", [64, 4, R_])
            ZSGs = s1("ZSGs", [R_, 512])
            MOSGs = s1("MOSGs", [R_, 512])
            MVs = s1("MVs", [R_, 4, 128])
            GTs = s1("GTs", [R_, 16])
            ARGs = s1("ARGs", [R_, 12])
            GLs = s1("GLs", [R_, 12])
            BETAs = s1("BETAs", [R_, 4])
            IGs = s1("IGs", [R_, 4])
            EGs = s1("EGs", [R_, 4])
            Qt = s1("Qt", [R_, 4, 128])
            Kt = s1("Kt", [R_, 4, 128])
            Vt = s1("Vt", [R_, 4, 128])
            MQt = s1("MQt", [R_, 4, 64])
            MKtt = s1("MKtt", [R_, 4, 64])
            PKt = s1("PKt", [R_, 4, 64])
            DIAGI = s1("DIAGI", [R_, 16, 16])
            M16 = s1("M16", [128, 16, 16])
            KD = [s1("KD%d" % i, [128, 4, 16]) for i in range(2)]
            QD = [s1("QD%d" % i, [128, 4, 16]) for i in range(2)]
            QDm = [s1("QDm%d" % i, [64, 4, 16]) for i in range(2)]
            DG4 = s1("DG4", [R_, 16, 4])
            EGBC = s1("EGBC", [128, 64])
            WPBC = s1("WPBC", [128, 64])
            S0b = [s1("S0b%d" % i, [128, 4, 128]) for i in range(2)]
            C0b = [s1("C0b%d" % i, [64, 4, 128]) for i in range(2)]
            Ug = s1("Ug", [R_, 4, 128])
            KROW = s1("KROW", [R_, 512])
            PKROW = View(DIAGI[:].rearrange("p a b -> p (a b)"), DIAGI.r)
            N0 = s1("N0", [R_, 4, 64])
            SMs = [s1("SMs%d" % i, [R_, 8]) for i in range(10)]
            T1 = s1("T1s", [R_, 512])
            T2 = KROW

            def hn_gate16(src, gate, dst, wres):
                ss, rs = SMs[8], SMs[9]
                tt("pool", T2[:], src[:], src[:], ALU.mult, [src], [T2])
                red("dve", ss[:, 0:4], T2[:].rearrange("p (h e) -> p h e", h=4), ALU.add, [T2], [ss])
                rsqrt_small(rs[:, 0:4], ss[:, 0:4], 1.0 / 128, [ss], [rs], rs[:, 4:8])
                tt("dve", T2[:].rearrange("p (h e) -> p h e", h=4), src[:].rearrange("p (h e) -> p h e", h=4),
                   bc3(rs[:, 0:4], 4, 128), ALU.mult, [src, rs], [T2])
                tt("dve", dst, T2[:], gate[:], ALU.mult, [T2, gate], [wres])

            IDF16 = CONST[0:R_, C_IDF:C_IDF + R_]
            ONES16 = CONST[0:R_, C_ONES:C_ONES + 128]

            dma("sp", XS[:], xs_d, [], [XS])
            dma("sp", N0[:].rearrange("p h d -> p (h d)"), sn_d, [], [N0])
            dma("sp", SMs[0][:, 0:4], sm_d, [], [SMs[0]])
            dma("sp", oconv_d[:, 0:2, :], sconv_d[:, 1:3, :], [], [])
            ss, rs = SMs[8], SMs[9]
            act(XNs[:], XS[:], AF.Square, [XS], [XNs, ss], accum=ss[:, 0:1])
            rsqrt_small(rs[:, 0:1], ss[:, 0:1], 1.0 / D, [ss], [rs], rs[:, 1:2])
            ts("dve", XNs[:], XS[:], rs[:, 0:1], None, ALU.mult, None, [XS, rs], [XNs])
            for k in range(8):
                tr(TPB[:, k * 128:k * 128 + R_], XNs[:, k * 128:(k + 1) * 128], IDB[0:R_, 0:R_], [XNs, IDB], [TPB])
            tt("dve", HTs[:], TPB[:].rearrange("p (k t) -> p k t", k=8)[:, :, 0:R_], bc3(GPRE[:], 8, R_), ALU.mult,
               [TPB, GPRE], [HTs])

            for ch in range(12):
                bk = BK[ch % 2]
                c0 = ch * 128
                for k in range(8):
                    mm(bk[:, 0:R_], WIN[:, k, c0:c0 + 128], HTs[:, k, :], k == 0, k == 7, [HTs] + rwin(c0, c0 + 128), [bk])
                cp("act", XPT[:, ch, 3, :], bk[:, 0:R_], [bk], [XPT])
            for hh in range(8):
                bk = BK[hh % 2]
                c0 = W_MQK + hh * 64
                for k in range(8):
                    mm(bk[0:64, 0:R_], WIN[:, k, c0:c0 + 64], HTs[:, k, :], k == 0, k == 7, [HTs] + rwin(c0, c0 + 64), [bk])
                if hh < 4:
                    cp("act", MQs[:, hh, :], bk[0:64, 0:R_], [bk], [MQs])
                else:
                    act(MKs[:, hh - 4, :], bk[0:64, 0:R_], AF.Copy, [bk], [MKs], scale=0.125)
            for g in range(3):
                for j in range(3):
                    dma("sp", SCV[:], sconv_d[:, j, g * 512:(g + 1) * 512], [], [SCV])
                    for c4 in range(4):
                        o0 = (j * 4 + c4) * R_
                        tr(BK[2][:, o0:o0 + R_], SCV[:, c4 * 128:(c4 + 1) * 128], IDF16, [SCV, CONST], [BK[2]])
                cp("dve", XPT[:, 4 * g:4 * g + 4, 0:3, :],
                   BK[2][:, 0:12 * R_].rearrange("p (j c r) -> p c j r", j=3, c=4), [BK[2]], [XPT])
            for g in range(3):
                bk = BK[3 + g % 2]
                for k in range(8):
                    mm(bk[0:R_, :], HTs[:, k, :], WIN[:, k, g * 512:(g + 1) * 512], k == 0, k == 7,
                       [HTs] + rwin(g * 512, (g + 1) * 512), [bk])
                cp("act", SCV[:], bk[0:R_, :], [bk], [SCV])
                dma("sp", oconv_d[:, 2, g * 512:(g + 1) * 512], SCV[:], [SCV], [])
            tt("dve", ACs[:], XPT[:, :, 0, :], bc3(CW[:, 0, :], 12, R_), ALU.mult, [XPT, CW], [ACs])
            for j in range(1, 4):
                tt("dve", TM12[:], XPT[:, :, j, :], bc3(CW[:, j, :], 12, R_), ALU.mult, [XPT, CW], [TM12])
                tt("dve", ACs[:], ACs[:], TM12[:], ALU.add, [ACs, TM12], [ACs])
            QKFs = TM12
            act(QKFs[:, 0:8, :], ACs[:, 0:8, :], AF.Silu, [ACs], [QKFs])
            act(GVs[:], ACs[:, 8:12, :], AF.Silu, [ACs], [GVs])
            act(SQs[:], QKFs[:, 0:8, :], AF.Square, [QKFs], [SQs])
            mm(BK[2][:, 0:8 * R_], ONES, SQs[:].rearrange("p a b -> p (a b)"), True, True, [CONST, SQs], [BK[2]])
            ts("dve", SQs[:].rearrange("p a b -> p (a b)"), BK[2][:, 0:8 * R_], 1.0, EPS, ALU.mult, ALU.add, [BK[2]], [SQs])
            act(SQs[:], SQs[:], AF.Ln, [SQs], [SQs])
            act(SQs[:], SQs[:], AF.Exp, [SQs], [SQs], scale=-0.5)
            stt("dve", GQs[:], QKFs[:, 0:4, :], 128.0 ** -0.5, SQs[:, 0:4, :], ALU.mult, ALU.mult, [QKFs, SQs], [GQs])
            tt("dve", GKs[:], QKFs[:, 4:8, :], SQs[:, 4:8, :], ALU.mult, [QKFs, SQs], [GKs])

            def tok16(bk, c0, n):
                for k in range(8):
                    mm(bk[0:R_, 0:n], HTs[:, k, :], WIN[:, k, c0:c0 + n], k == 0, k == 7, [HTs] + rwin(c0, c0 + n), [bk])
            tok16(BK[0], W_GZ, 512)
            act(T1[:], BK[0][0:R_, :], AF.Silu, [BK[0]], [T1])
            tt("dve", ZSGs[:].rearrange("p (h e) -> p h e", h=4), T1[:].rearrange("p (h e) -> p h e", h=4),
               bcm(GNG[0:R_, :], 4), ALU.mult, [T1, GNG], [ZSGs])
            tok16(BK[1], W_MV, 512)
            cp("act", MVs[:].rearrange("p h e -> p (h e)"), BK[1][0:R_, :], [BK[1]], [MVs])
            tok16(BK[0], W_MO, 512)
            act(T1[:], BK[0][0:R_, :], AF.Exp, [BK[0]], [T1], scale=-1.0)
            ts("dve", T1[:], T1[:], 1.0, None, ALU.add, None, [T1], [T1])
            P.op("dve", lambda e: e.reciprocal(out=T1[:], in_=T1[:]), [T1], [T1])
            tt("dve", MOSGs[:].rearrange("p (h e) -> p h e", h=4), T1[:].rearrange("p (h e) -> p h e", h=4),
               bcm(MNG[0:R_, :], 4), ALU.mult, [T1, MNG], [MOSGs])
            tok16(BK[1], W_GATE, 16)
            cp("dve", GTs[:], BK[1][0:R_, 0:16], [BK[1]], [GTs])
            tt("dve", ARGs[:], GTs[:, 0:12], SGN[0:R_, :], ALU.mult, [GTs, SGN], [ARGs])
            tt("dve", ARGs[:], ARGs[:], BIA[0:R_, :], ALU.add, [ARGs, BIA], [ARGs])
            act(ARGs[:], ARGs[:], AF.Exp, [ARGs], [ARGs])
            act(ARGs[:], ARGs[:], AF.Ln, [ARGs], [ARGs], bias=1.0)
            tt("dve", GLs[:], ARGs[:], NEGC[0:R_, :], ALU.mult, [ARGs, NEGC], [GLs])
            act(BETAs[:], GLs[:, 0:4], AF.Exp, [GLs], [BETAs])
            tt("dve", IGs[:], GTs[:, 12:16], SMB[0:R_, 8:12], ALU.add, [GTs, SMB], [IGs])
            act(EGs[:], GLs[:, 4:8], AF.Exp, [GLs], [EGs])
            M0s, Bs, MTs, WPs, Ps, QKG, QKMs, QNs = SMs[0], SMs[1], SMs[2], SMs[3], SMs[4], SMs[5], SMs[6], SMs[7]
            tt("dve", Bs[:, 0:4], GLs[:, 8:12], M0s[:, 0:4], ALU.add, [GLs, M0s], [Bs])
            tt("dve", MTs[:, 0:4], Bs[:, 0:4], IGs[:], ALU.max, [Bs, IGs], [MTs])
            tt("dve", WPs[:, 0:4], Bs[:, 0:4], MTs[:, 0:4], ALU.subtract, [Bs, MTs], [WPs])
            act(WPs[:, 0:4], WPs[:, 0:4], AF.Exp, [WPs], [WPs])
            tt("dve", Ps[:, 0:4], IGs[:], MTs[:, 0:4], ALU.subtract, [IGs, MTs], [Ps])
            act(Ps[:, 0:4], Ps[:, 0:4], AF.Exp, [Ps], [Ps])
            dma("sp", om_d, MTs[:, 0:4], [MTs], [])

            for h in range(4):
                tr(BK[2][0:R_, h * 128:(h + 1) * 128], GQs[:, h, :], IDF, [GQs, CONST], [BK[2]])
                tr(BK[3][0:R_, h * 128:(h + 1) * 128], GKs[:, h, :], IDF, [GKs, CONST], [BK[3]])
                tr(BK[4][0:R_, h * 128:(h + 1) * 128], GVs[:, h, :], IDF, [GVs, CONST], [BK[4]])
                tr(BK[5][0:R_, h * 64:(h + 1) * 64], MQs[:, h, :], CONST[0:64, C_IDF:C_IDF + 64], [MQs, CONST], [BK[5]])
                tr(BK[5][0:R_, 256 + h * 64:256 + (h + 1) * 64], MKs[:, h, :], CONST[0:64, C_IDF:C_IDF + 64],
                   [MKs, CONST], [BK[5]])
            cp("act", Qt[:].rearrange("p h d -> p (h d)"), BK[2][0:R_, :], [BK[2]], [Qt])
            cp("dve", Kt[:].rearrange("p h d -> p (h d)"), BK[3][0:R_, :], [BK[3]], [Kt])
            cp("act", Vt[:].rearrange("p h d -> p (h d)"), BK[4][0:R_, :], [BK[4]], [Vt])
            cp("dve", MQt[:].rearrange("p h d -> p (h d)"), BK[5][0:R_, 0:256], [BK[5]], [MQt])
            cp("act", MKtt[:].rearrange("p h d -> p (h d)"), BK[5][0:R_, 256:512], [BK[5]], [MKtt])
            tt("dve", T1[:], Qt[:].rearrange("p h d -> p (h d)"), Kt[:].rearrange("p h d -> p (h d)"), ALU.mult,
               [Qt, Kt], [T1])
            red("dve", QKG[:, 0:4], T1[:].rearrange("p (h d) -> p h d", h=4), ALU.add, [T1], [QKG])
            tt("dve", T1[:, 0:256], MQt[:].rearrange("p h d -> p (h d)"), MKtt[:].rearrange("p h d -> p (h d)"),
               ALU.mult, [MQt, MKtt], [T1])
            red("dve", QKMs[:, 0:4], T1[:, 0:256].rearrange("p (h d) -> p h d", h=4), ALU.add, [T1], [QKMs])
            tt("dve", T1[:, 0:256], MQt[:].rearrange("p h d -> p (h d)"), N0[:].rearrange("p h d -> p (h d)"),
               ALU.mult, [MQt, N0], [T1])
            red("dve", QNs[:, 0:4], T1[:, 0:256].rearrange("p (h d) -> p h d", h=4), ALU.add, [T1], [QNs])
            tt("dve", PKt[:], MKtt[:], bc3(Ps[:, 0:4], 4, 64), ALU.mult, [MKtt, Ps], [PKt])
            tt("dve", N0[:], N0[:], bc3(WPs[:, 0:4], 4, 64), ALU.mult, [N0, WPs], [N0])
            tt("dve", N0[:], N0[:], PKt[:], ALU.add, [N0, PKt], [N0])
            dma("sp", on_d, N0[:].rearrange("p h d -> p (h d)"), [N0], [])

            tt("dve", DIAGI[:], bc3(IDF16, R_, R_), bcm(IDF16, R_), ALU.mult, [CONST], [DIAGI])
            mm(BK[2][:, 0:R_ * R_], ONES16, DIAGI[:].rearrange("p a b -> p (a b)"), True, True, [CONST, DIAGI], [BK[2]])
            cp("dve", M16[:].rearrange("p a b -> p (a b)"), BK[2][:, 0:R_ * R_], [BK[2]], [M16])
            tt("dve", DG4[:], bcm(EGs[:], R_), bc3(IDF16, R_, 4), ALU.mult, [EGs, CONST], [DG4])
            mm(BK[3][:, 0:64], ONES16, DG4[:].rearrange("p a b -> p (a b)"), True, True, [CONST, DG4], [BK[3]])
            cp("dve", EGBC[:], BK[3][:, 0:64], [BK[3]], [EGBC])
            tt("dve", DG4[:], bcm(WPs[:, 0:4], R_), bc3(IDF16, R_, 4), ALU.mult, [WPs, CONST], [DG4])
            mm(BK[3][:, 0:64], ONES16, DG4[:].rearrange("p a b -> p (a b)"), True, True, [CONST, DG4], [BK[3]])
            cp("dve", WPBC[:], BK[3][:, 0:64], [BK[3]], [WPBC])

            def ld(r, i):
                dma("sp", S0b[i][:], sS_d[r].rearrange("h d e -> d h e"), [], [S0b[i]])
                dma("sp", C0b[i][:], sC_d[r].rearrange("h d e -> d h e"), [], [C0b[i]])
            ld(0, 0)
            for r in range(R_):
                i = r % 2
                if r + 1 < R_:
                    ld(r + 1, 1 - i)
                tt("dve", KD[i][:], GKs[:], bcm(M16[:, r, :], 4), ALU.mult, [GKs, M16], [KD[i]])
                tt("pool", QD[i][:], GQs[:], bcm(M16[:, r, :], 4), ALU.mult, [GQs, M16], [QD[i]])
                tt("dve", QDm[i][:], MQs[:], bcm(M16[0:64, r, :], 4), ALU.mult, [MQs, M16], [QDm[i]])
                for h in range(4):
                    sl = slice(h * 128, (h + 1) * 128)
                    st_, sp_ = (r == 0 and h == 0), (r == R_ - 1 and h == 3)
                    mm(BK[2][0:R_, sl], KD[i][:, h, :], S0b[i][:, h, :], st_, sp_, [KD[i], S0b[i]], [BK[2]], skip=True)
                    mm(BK[3][0:R_, sl], QD[i][:, h, :], S0b[i][:, h, :], st_, sp_, [QD[i], S0b[i]], [BK[3]], skip=True)
                    mm(BK[5][0:R_, sl], QDm[i][:, h, :], C0b[i][:, h, :], st_, sp_, [QDm[i], C0b[i]], [BK[5]], skip=True)
            KSp, QSp, QCp = BK[2], BK[3], BK[5]
            tt("dve", Ug[:], KSp[0:R_, :].rearrange("p (h e) -> p h e", h=4), bc3(EGs[:], 4, 128), ALU.mult,
               [KSp, EGs], [Ug])
            tt("dve", Ug[:], Vt[:], Ug[:], ALU.subtract, [Vt, Ug], [Ug])
            tt("dve", Ug[:], Ug[:], bc3(BETAs[:], 4, 128), ALU.mult, [Ug, BETAs], [Ug])
            ld(0, 0)
            for r in range(R_):
                i = r % 2
                if r + 1 < R_:
                    ld(r + 1, 1 - i)
                ts("dve", KROW[:], Kt[:].rearrange("p h d -> p (h d)"), IDF16[:, r:r + 1], None, ALU.mult, None,
                   [Kt, CONST], [KROW])
                ts("dve", PKROW[:], PKt[:].rearrange("p h d -> p (h d)"), IDF16[:, r:r + 1], None, ALU.mult, None,
                   [PKt, CONST], [PKROW])
                for h in range(4):
                    mm(BK[4][:, h * 128:(h + 1) * 128], KROW[:, h * 128:(h + 1) * 128], Ug[:, h, :], True, True,
                       [KROW, Ug], [BK[4]])
                for h in range(4):
                    mm(BK[6][0:64, h * 128:(h + 1) * 128], PKROW[:, h * 64:(h + 1) * 64], MVs[:, h, :], True, True,
                       [PKROW, MVs], [BK[6]])
                for h in range(4):
                    stt("dve", S0b[i][:, h, :], S0b[i][:, h, :], EGBC[:, r * 4 + h:r * 4 + h + 1],
                        BK[4][:, h * 128:(h + 1) * 128], ALU.mult, ALU.add, [S0b[i], EGBC, BK[4]], [S0b[i]])
                    stt("dve", C0b[i][:, h, :], C0b[i][:, h, :], WPBC[0:64, r * 4 + h:r * 4 + h + 1],
                        BK[6][0:64, h * 128:(h + 1) * 128], ALU.mult, ALU.add, [C0b[i], WPBC, BK[6]], [C0b[i]])
                dma("sp", oS_d[r].rearrange("h d e -> d h e"), S0b[i][:], [S0b[i]], [])
                dma("sp", oC_d[r].rearrange("h d e -> d h e"), C0b[i][:], [C0b[i]], [])

            MIXs = XNs
            tt("dve", T1[:].rearrange("p (h e) -> p h e", h=4), QSp[0:R_, :].rearrange("p (h e) -> p h e", h=4),
               bc3(EGs[:], 4, 128), ALU.mult, [QSp, EGs], [T1])
            tt("dve", Ug[:], Ug[:], bc3(QKG[:, 0:4], 4, 128), ALU.mult, [Ug, QKG], [Ug])
            tt("dve", T1[:], T1[:], Ug[:].rearrange("p h e -> p (h e)"), ALU.add, [T1, Ug], [T1])
            hn_gate16(T1, ZSGs, MIXs[:, 0:512], MIXs)
            PQ = SMs[5]
            tt("dve", PQ[:, 4:8], Ps[:, 0:4], QKMs[:, 0:4], ALU.mult, [Ps, QKMs], [PQ])
            tt("dve", T1[:].rearrange("p (h e) -> p h e", h=4), QCp[0:R_, :].rearrange("p (h e) -> p h e", h=4),
               bc3(WPs[:, 0:4], 4, 128), ALU.mult, [QCp, WPs], [T1])
            tt("dve", Ug[:], MVs[:], bc3(PQ[:, 4:8], 4, 128), ALU.mult, [MVs, PQ], [Ug])
            tt("dve", T1[:], T1[:], Ug[:].rearrange("p h e -> p (h e)"), ALU.add, [T1, Ug], [T1])
            DENs, EMTs = SMs[6], SMs[7]
            tt("dve", DENs[:, 4:8], WPs[:, 0:4], QNs[:, 0:4], ALU.mult, [WPs, QNs], [DENs])
            tt("dve", DENs[:, 4:8], DENs[:, 4:8], PQ[:, 4:8], ALU.add, [DENs, PQ], [DENs])
            ts("dve", DENs[:, 0:4], DENs[:, 4:8], -1.0, None, ALU.mult, None, [DENs], [DENs])
            tt("dve", DENs[:, 4:8], DENs[:, 4:8], DENs[:, 0:4], ALU.max, [DENs], [DENs])
            act(EMTs[:, 4:8], MTs[:, 0:4], AF.Exp, [MTs], [EMTs], scale=-1.0)
            tt("dve", DENs[:, 4:8], DENs[:, 4:8], EMTs[:, 4:8], ALU.max, [DENs, EMTs], [DENs])
            P.op("dve", lambda e, d=DENs: e.reciprocal(out=d[:, 4:8], in_=d[:, 4:8]), [DENs], [DENs])
            tt("dve", T1[:].rearrange("p (h e) -> p h e", h=4), T1[:].rearrange("p (h e) -> p h e", h=4),
               bc3(DENs[:, 4:8], 4, 128), ALU.mult, [T1, DENs], [T1])
            hn_gate16(T1, MOSGs, MIXs[:, 512:1024], MIXs)
            for k in range(8):
                tr(TPB[:, k * 128:k * 128 + R_], MIXs[:, k * 128:(k + 1) * 128], IDB[0:R_, 0:R_], [MIXs, IDB], [TPB])
            cp("act", HTs[:], TPB[:].rearrange("p (k t) -> p k t", k=8)[:, :, 0:R_], [TPB], [HTs])
            for eh in range(2):
                for k in range(8):
                    mm(BK[eh][0:R_, :], HTs[:, k, :], WOUT[:, k, eh * 512:(eh + 1) * 512], k == 0, k == 7,
                       [HTs, WOUT], [BK[eh]])
            ss, rs = SMs[8], SMs[9]
            for eh in range(2):
                act(XNs[:, eh * 512:(eh + 1) * 512], BK[eh][0:R_, :], AF.Square, [BK[eh]], [XNs, ss],
                    accum=ss[:, eh:eh + 1])
            tt("dve", ss[:, 2:3], ss[:, 0:1], ss[:, 1:2], ALU.add, [ss], [ss])
            rsqrt_small(rs[:, 0:1], ss[:, 2:3], 1.0 / D, [ss], [rs], rs[:, 1:2])
            for eh in range(2):
                sl = slice(eh * 512, (eh + 1) * 512)
                stt("dve", T1[:], BK[eh][0:R_, :], rs[:, 0:1], GPM[0:R_, sl], ALU.mult, ALU.mult, [BK[eh], rs, GPM], [T1])
                tt("dve", XS[:, sl], XS[:, sl], T1[:], ALU.add, [XS, T1], [XS])
            for k in range(8):
                tr(BK[2][:, k * R_:(k + 1) * R_], XS[:, k * 128:(k + 1) * 128], IDF16, [XS, CONST], [BK[2]])
            cp("dve", XS1T[:].rearrange("p k r -> p (k r)"), BK[2][:, 0:8 * R_], [BK[2]], [XS1T])
            act(XNs[:], XS[:], AF.Square, [XS], [XNs, ss], accum=ss[:, 4:5])
            rsqrt_small(rs[:, 4:5], ss[:, 4:5], 1.0 / D, [ss], [rs], rs[:, 5:6])
            ts("dve", XNs[:], XS[:], rs[:, 4:5], None, ALU.mult, None, [XS, rs], [XNs])
            for k in range(8):
                tr(TPB[:, k * 128:k * 128 + R_], XNs[:, k * 128:(k + 1) * 128], IDB[0:R_, 0:R_], [XNs, IDB], [TPB])
            tt("dve", HNS[:], TPB[:].rearrange("p (k t) -> p k t", k=8)[:, :, 0:R_], bc3(GPREMLP[:], 8, R_), ALU.mult,
               [TPB, GPREMLP], [HNS])

            P.barrier(all_res_p1 + RX + [XS1T.r, HNS.r, IDF2.r, IDB.r, GPREMLP.r])
            p1b.close()

        with ExitStack() as p2:
            WUP = p2.enter_context(nc.sbuf_tensor("sb_WUP", [128, 8, DFF], BF16))
            WDN = p2.enter_context(nc.sbuf_tensor("sb_WDN", [128, 32, D], BF16))
            NWC = 8
            RWU = [Res("WUP%d" % i) for i in range(NWC)]
            RWD = [Res("WDN%d" % i) for i in range(NWC)]
            wup_v = wup_d.rearrange("(k p) c -> p k c", p=128)
            wdn_v = wdn_d.rearrange("(k p) c -> p k c", p=128)
            for i in range(NWC if KPH2 else 0):
                dma("pool", WUP[:, :, i * 512:(i + 1) * 512], wup_v[:, :, i * 512:(i + 1) * 512], [], [RWU[i]])
                dma("pool", WDN[:, i * 4:(i + 1) * 4, :], wdn_v[:, i * 4:(i + 1) * 4, :], [], [RWD[i]])
            XN2 = sb(p2, "XN2", [128, D], BF16)
            GPL = sb(p2, "GPL", [128, D])
            dma("sp", GPL[:], gpl_d.partition_broadcast(128), [], [GPL])
            HN = sb(p2, "HN", [128, 8, 256], BF16)
            UT = [sb(p2, "UT%d" % i, [128, 256], BF16) for i in range(2)]
            RL = [sb(p2, "RL%d" % i, [128, 256]) for i in range(2)]
            SS2 = sb(p2, "SS2", [128, 8])

            NB2 = NT // 2 if KPH2 else 0

            def xv_of(blk, j):
                return Xt[:, 2 * blk + j, :], RX[2 * blk + j]

            def prep(blk):
                for j in range(2):
                    xv, rx = xv_of(blk, j)
                    act(XN2[:], xv, AF.Square, [rx], [XN2, SS2], accum=SS2[:, 0:1])
                    rsqrt_small(SS2[:, 1:2], SS2[:, 0:1], 1.0 / D, [SS2], [SS2], SS2[:, 2:3])
                    ts("dve", XN2[:], xv, SS2[:, 1:2], None, ALU.mult, None, [rx, SS2], [XN2])
                    for k in range(8):
                        tr(TPB[:, k * 128:(k + 1) * 128], XN2[:, k * 128:(k + 1) * 128], IDB[:], [XN2, IDB], [TPB])
                    tt("dve", HN[:, :, j * 128:(j + 1) * 128], TPB[:].rearrange("p (k t) -> p k t", k=8),
                       bc3(GPREMLP[:], 8, 128), ALU.mult, [TPB, GPREMLP], [HN])

            def up(f, hn, n):
                bk = BK[f % 2]
                for k in range(8):
                    mm(bk[:, 0:n], WUP[:, k, f * 128:(f + 1) * 128], hn[:, k, 0:n], k == 0, k == 7,
                       [hn, RWU[f // 4]], [bk])
                rl, ut = RL[f % 2], UT[f % 2]
                act(rl[:, 0:n], bk[:, 0:n], AF.Relu, [bk], [rl])
                tt("dve" if f % 2 == 0 else "pool", ut[:, 0:n], rl[:, 0:n], rl[:, 0:n], ALU.mult, [rl], [ut])

            def down(f, rows, ntl):
                ut = UT[f % 2]
                for j in range(ntl):
                    for eh in range(2):
                        ab = BK[2 + 2 * j + eh]
                        mm(ab[0:rows, :], ut[:, j * rows:(j + 1) * rows], WDN[:, f, eh * 512:(eh + 1) * 512],
                           f == 0, f == 31, [ut, RWD[f // 4]], [ab])

            def fin(blk):
                for j in range(2):
                    xv, rx = xv_of(blk, j)
                    for eh in range(2):
                        ab = BK[2 + 2 * j + eh]
                        act(XN2[:, eh * 512:(eh + 1) * 512], ab[:], AF.Square, [ab], [XN2, SS2],
                            accum=SS2[:, 3 + eh:4 + eh])
                    tt("dve", SS2[:, 5:6], SS2[:, 3:4], SS2[:, 4:5], ALU.add, [SS2], [SS2])
                    rsqrt_small(SS2[:, 6:7], SS2[:, 5:6], 1.0 / D, [SS2], [SS2], SS2[:, 7:8])
                    for eh in range(2):
                        ab = BK[2 + 2 * j + eh]
                        sl = slice(eh * 512, (eh + 1) * 512)
                        stt("dve", ab[:], ab[:], SS2[:, 6:7], GPL[:, sl], ALU.mult, ALU.mult, [ab, SS2, GPL], [ab])
                        tt("dve", xv[:, sl], xv[:, sl], ab[:], ALU.add, [rx, ab], [rx])
                    r0 = (2 * blk + j) * 128
                    dma("sp", y_d[r0:r0 + 128, :], xv, [rx], [])

            if NB2:
                prep(0)
            for blk in range(NB2):
                up(0, HN, 256)
                for f in range(32):
                    if f + 1 < 32:
                        up(f + 1, HN, 256)
                    elif blk + 1 < NB2:
                        prep(blk + 1)
                    down(f, 128, 2)
                fin(blk)

            R_ = NS
            if KPH2:
                up(0, HNS, R_)
                for f in range(32):
                    if f + 1 < 32:
                        up(f + 1, HNS, R_)
                    down(f, R_, 1)
            if KPH2:
                for eh in range(2):
                    act(XN2[0:R_, eh * 512:(eh + 1) * 512], BK[2 + eh][0:R_, :], AF.Square, [BK[2 + eh]], [XN2, SS2],
                        accum=SS2[0:R_, 3 + eh:4 + eh])
                tt("dve", SS2[0:R_, 5:6], SS2[0:R_, 3:4], SS2[0:R_, 4:5], ALU.add, [SS2], [SS2])
                rsqrt_small(SS2[0:R_, 6:7], SS2[0:R_, 5:6], 1.0 / D, [SS2], [SS2], SS2[0:R_, 7:8])
                YSB = XN2.t.bitcast(F32)
                for eh in range(2):
                    sl = slice(eh * 512, (eh + 1) * 512)
                    stt("dve", BK[2 + eh][0:R_, :], BK[2 + eh][0:R_, :], SS2[0:R_, 6:7], GPL[0:R_, sl], ALU.mult, ALU.mult,
                        [BK[2 + eh], SS2, GPL], [BK[2 + eh]])
                    for j in range(4):
                        tr(BK[4 + eh][0:R_, j * 128:(j + 1) * 128], XS1T[:, 4 * eh + j, :], IDF2[:], [XS1T, IDF2],
                           [BK[4 + eh]])
                    cp("act", YSB[0:R_, :], BK[4 + eh][0:R_, :], [BK[4 + eh]], [XN2])
                    tt("dve", YSB[0:R_, :], YSB[0:R_, :], BK[2 + eh][0:R_, :], ALU.add, [XN2, BK[2 + eh]], [XN2])
                    dma("sp", ys_d[:, sl], YSB[0:R_, :], [XN2], [])

        n_ins = P.finalize(top)
    return nc, n_ins


_CACHE = {}


def kernel(x_prompt, x_sample, state_gdn_conv, state_gdn_S, state_mlstm_C, state_mlstm_n, state_mlstm_m,
           norm_pre_mix, w_in, conv_w, a_log, dt_bias, gdn_norm_g, b_igate, b_fgate, mlstm_norm_g, w_out,
           norm_post_mix, norm_pre_mlp, w_up, w_down, norm_post_mlp):
    f = lambda a: np.ascontiguousarray(np.asarray(a, dtype=np.float32))
    if "nc" not in _CACHE:
        _CACHE["nc"] = build_program()
    nc, _ = _CACHE["nc"]
    consts = make_consts()
    small = np.concatenate([f(a_log)[0], f(dt_bias)[0], f(b_igate)[0], f(b_fgate)[0]])[None, :]
    shared = {
        "w_in": f(w_in)[0], "w_out": f(w_out)[0], "w_up": f(w_up)[0], "w_down": f(w_down)[0],
        "consts": consts,
        "gpre_fm": f(f(norm_pre_mix)[0].reshape(8, 128).T),
        "gpremlp_fm": f(f(norm_pre_mlp)[0].reshape(8, 128).T),
        "cw_fm": f(f(conv_w)[0].reshape(4, 12, 128).transpose(2, 0, 1).reshape(128, 48)),
        "gpostmix": f(norm_post_mix)[0][None, :], "gpostmlp": f(norm_post_mlp)[0][None, :],
        "small": f(small), "gdn_norm_g": f(gdn_norm_g)[0][None, :], "mlstm_norm_g": f(mlstm_norm_g)[0][None, :],
    }
    xp, xs = f(x_prompt), f(x_sample)
    in_maps = []
    for c in range(NCORES):
        r = slice(c * NS, (c + 1) * NS)
        m = dict(shared)
        m.update({
            "x": xp[c], "xs": xs[r, 0, :],
            "sconv": f(state_gdn_conv)[0, r], "sS": f(state_gdn_S)[0, r], "sC": f(state_mlstm_C)[0, r],
            "sn": f(state_mlstm_n)[0, r].reshape(NS, 256), "sm": f(state_mlstm_m)[0, r],
        })
        in_maps.append(m)
    res = run_bass_kernel_spmd(nc, in_maps, core_ids=list(range(NCORES)))
    R = res.results
    g = lambda k: np.stack([np.asarray(R[c][k], dtype=np.float32) for c in range(NCORES)])
    gc = lambda k: np.concatenate([np.asarray(R[c][k], dtype=np.float32) for c in range(NCORES)], axis=0)
    y_prompt = g("y")
    y_sample = gc("ys")[:, None, :]
    p_conv = g("pconv")[None]
    p_S = g("pS")[None]
    p_C = g("pC")[None]
    p_n = g("pn")[None]
    p_m = g("pm").reshape(NCORES, 4)[None]
    s_conv = gc("oconv")[None]
    s_S = gc("oS")[None]
    s_C = gc("oC")[None]
    s_n = gc("on").reshape(NCORES * NS, 4, 64)[None]
    s_m = gc("om")[None]
    return (y_prompt, y_sample, p_conv, p_S, p_C, p_n, p_m, s_conv, s_S, s_C, s_n, s_m)
```

```python
from contextlib import ExitStack
import numpy as np
import concourse.bass as bass
import concourse.mybir as mybir
from concourse.bass_utils import run_bass_kernel_spmd

F32 = mybir.dt.float32
BF16 = mybir.dt.bfloat16
ALU = mybir.AluOpType
AF = mybir.ActivationFunctionType
AX = mybir.AxisListType

NCORES = 8
T = 2048
NT = T // 128
D = 1024
DFF = 4096
NS = 16
EPS = 1e-6
NEG = -30000.0
import os
KNT = int(os.environ.get('KNT', NT))
KPH2 = int(os.environ.get('KPH2', 1))
KSTAGE = int(os.environ.get('KSTAGE', 99))
KSUB = int(os.environ.get('KSUB', 99))


class Res:
    __slots__ = ("name", "w", "rd")

    def __init__(self, name):
        self.name = name
        self.w = None
        self.rd = []


class Op:
    __slots__ = ("eng", "fn", "reads", "writes", "dma", "deps", "signal", "cnt", "sem", "waits")

    def __init__(self, eng, fn, reads, writes, dma):
        self.eng = eng
        self.fn = fn
        self.reads = reads
        self.writes = writes
        self.dma = dma
        self.deps = []
        self.signal = False
        self.cnt = 0
        self.sem = None
        self.waits = []


def _res(lst):
    out = []
    for x in lst:
        if x is None:
            continue
        if isinstance(x, Res):
            out.append(x)
        elif isinstance(x, (list, tuple)):
            out.extend(_res(x))
        else:
            out.append(x.r)
    return out


class Prog:
    ENGS = ("pe", "act", "dve", "pool", "sp")

    def __init__(self, nc, n_dma_sems=56):
        self.nc = nc
        self.ops = []
        self.n_dma_sems = n_dma_sems
        self.engobj = {"pe": nc.tensor, "act": nc.scalar, "dve": nc.vector,
                       "pool": nc.gpsimd, "sp": nc.sync}

    def op(self, eng, fn, r=(), w=()):
        self.ops.append(Op(eng, fn, _res(r), _res(w), False))

    def dma(self, eng, fn, r=(), w=()):
        self.ops.append(Op(eng, fn, _res(r), _res(w), True))

    def barrier(self, allres):
        for e in ("pe", "act", "dve", "pool", "sp"):
            self.ops.append(Op(e, (lambda en: en.nop(nofuse=True)), [], _res(allres), False))

    def finalize(self, stack):
        nc = self.nc
        ops = self.ops
        dma_slot_last = [None] * self.n_dma_sems
        dma_i = 0
        for i, o in enumerate(ops):
            deps = set()
            for r in o.reads:
                if r.w is not None:
                    deps.add(r.w)
            for r in o.writes:
                if r.w is not None:
                    deps.add(r.w)
                for j in r.rd:
                    deps.add(j)
            if o.dma:
                slot = dma_i % self.n_dma_sems
                dma_i += 1
                o.sem = slot
                if dma_slot_last[slot] is not None:
                    deps.add(dma_slot_last[slot])
                dma_slot_last[slot] = i
            deps.discard(i)
            o.deps = sorted(deps)
            for r in o.reads:
                r.rd.append(i)
            for r in o.writes:
                r.w = i
                r.rd = []
            for j in o.deps:
                pj = ops[j]
                if pj.dma:
                    continue
                if pj.eng == "pe" and o.eng == "pe" and not o.dma:
                    continue
                pj.signal = True
        cnt = {e: 0 for e in self.ENGS}
        dcnt = [0] * self.n_dma_sems
        for o in ops:
            if o.dma:
                dcnt[o.sem] += 16
                o.cnt = dcnt[o.sem]
            elif o.signal:
                cnt[o.eng] += 1
                o.cnt = cnt[o.eng]
        seen = {e: {} for e in self.ENGS}
        for o in ops:
            need = {}
            for j in o.deps:
                pj = ops[j]
                if pj.dma:
                    key = ("d", pj.sem)
                else:
                    if pj.eng == "pe" and o.eng == "pe" and not o.dma:
                        continue
                    key = ("e", pj.eng)
                if pj.cnt > need.get(key, 0):
                    need[key] = pj.cnt
            s = seen[o.eng]
            for key, v in need.items():
                if s.get(key, 0) >= v:
                    continue
                s[key] = v
                o.waits.append((key, v))
        final_waits = [(("d", k), dcnt[k]) for k in range(self.n_dma_sems) if dcnt[k] > 0]
        final_waits += [(("e", e), cnt[e]) for e in self.ENGS if cnt[e] > 0 and e != "sp"]
        esem = {e: stack.enter_context(nc.semaphore("s_" + e)) for e in self.ENGS}
        dsem = [stack.enter_context(nc.semaphore("d_%d" % k)) for k in range(self.n_dma_sems)]

        def semof(key):
            return dsem[key[1]] if key[0] == "d" else esem[key[1]]

        n_ins = 0
        for o in ops:
            e = self.engobj[o.eng]
            for key, v in o.waits:
                e.wait_ge(semof(key), v)
                n_ins += 1
            ins = o.fn(e)
            n_ins += 1
            if o.dma:
                ins.then_inc(dsem[o.sem], 16)
            elif o.signal:
                ins.then_inc(esem[o.eng], 1)
        sp = self.engobj["sp"]
        for key, v in final_waits:
            if seen["sp"].get(key, 0) >= v:
                continue
            sp.wait_ge(semof(key), v)
        return n_ins


class View:
    __slots__ = ("t", "r")

    def __init__(self, ap, r):
        self.t = ap
        self.r = r

    def __getitem__(self, k):
        return self.t[k]


class Tl:
    __slots__ = ("t", "r")

    def __init__(self, t, name):
        self.t = t
        self.r = Res(name)

    def __getitem__(self, k):
        return self.t[k]


C_IDF, C_TRI, C_ONES, C_SEL127, C_MST, C_MIT, C_MI, C_SELR = (
    0, 128, 256, 384, 512, 640, 768, 896)
NCONST = 1408


def make_consts():
    c = np.zeros((128, NCONST), np.float32)
    s = np.arange(128)[:, None]
    f = np.arange(128)[None, :]
    c[:, C_IDF:C_IDF + 128] = (s == f)
    c[:, C_TRI:C_TRI + 128] = (s <= f)
    c[:, C_ONES:C_ONES + 128] = 1.0
    c[:, C_SEL127:C_SEL127 + 128] = (s == 127)
    mst = np.where(s < f, 0.0, NEG)
    mit = np.where(s <= f, 0.0, NEG)
    mi = np.where(f <= s, 0.0, NEG)
    c[:, C_MST:C_MST + 128] = mst
    c[:, C_MIT:C_MIT + 128] = mit
    c[:, C_MI:C_MI + 128] = mi
    for h in range(4):
        c[h, C_SELR + h * 128:C_SELR + (h + 1) * 128] = 1.0
    return c


W_QKV, W_MQK, W_GZ, W_MV, W_MO, W_GATE = 0, 1536, 2048, 2560, 3072, 3584
WIN_MOVES = [(0, 0, 1536), (1536, 2056, 512), (2048, 1536, 512), (2560, 2568, 512),
             (3072, 3080, 512), (3584, 2048, 8), (3592, 3596, 4), (3596, 3592, 4)]


def build_program():
    nc = bass.Bass("TRN2", target_bir_lowering=False)
    P = Prog(nc)

    def din(name, shape):
        return nc.dram_tensor(name, list(shape), F32, kind="ExternalInput").ap()

    def dout(name, shape):
        return nc.dram_tensor(name, list(shape), F32, kind="ExternalOutput").ap()

    x_d = din("x", [T, D])
    xs_d = din("xs", [NS, D])
    sconv_d = din("sconv", [NS, 3, 1536])
    sS_d = din("sS", [NS, 4, 128, 128])
    sC_d = din("sC", [NS, 4, 64, 128])
    sn_d = din("sn", [NS, 256])
    sm_d = din("sm", [NS, 4])
    win_d = din("w_in", [D, 3600])
    wout_d = din("w_out", [D, D])
    wup_d = din("w_up", [D, DFF])
    wdn_d = din("w_down", [DFF, D])
    consts_d = din("consts", [128, NCONST])
    gpre_d = din("gpre_fm", [128, 8])
    gpremlp_d = din("gpremlp_fm", [128, 8])
    cw_d = din("cw_fm", [128, 48])
    gpm_d = din("gpostmix", [1, D])
    gpl_d = din("gpostmlp", [1, D])
    small_d = din("small", [1, 16])
    gng_d = din("gdn_norm_g", [1, 128])
    mng_d = din("mlstm_norm_g", [1, 128])

    y_d = dout("y", [T, D])
    ys_d = dout("ys", [NS, D])
    pconv_d = dout("pconv", [3, 1536])
    pS_d = dout("pS", [4, 128, 128])
    pC_d = dout("pC", [4, 64, 128])
    pn_d = dout("pn", [4, 64])
    pm_d = dout("pm", [1, 4])
    oconv_d = dout("oconv", [NS, 3, 1536])
    oS_d = dout("oS", [NS, 4, 128, 128])
    oC_d = dout("oC", [NS, 4, 64, 128])
    on_d = dout("on", [NS, 256])
    om_d = dout("om", [NS, 4])

    def mm(out, lhsT, rhs, start, stop, r, w, skip=False):
        if skip:
            P.op("pe", lambda e, o=out, l=lhsT, rr=rhs, s=start, t=stop:
                 e.matmul(o, lhsT=l, rhs=rr, start=s, stop=t, skip_group_check=True), r, w)
        else:
            P.op("pe", lambda e, o=out, l=lhsT, rr=rhs, s=start, t=stop:
                 e.matmul(o, lhsT=l, rhs=rr, start=s, stop=t), r, w)

    def tr(out, in_, ident, r, w):
        P.op("pe", lambda e, o=out, i=in_, d=ident: e.transpose(o, i, d), r, w)

    def tt(eng, out, in0, in1, op, r, w):
        P.op(eng, lambda e, o=out, a=in0, b=in1, p=op: e.tensor_tensor(out=o, in0=a, in1=b, op=p), r, w)

    def ts(eng, out, in0, s1, s2, op0, op1, r, w, accum=None):
        if op1 is None:
            P.op(eng, lambda e, o=out, a=in0, x=s1, p0=op0:
                 e.tensor_scalar(out=o, in0=a, scalar1=x, scalar2=None, op0=p0), r, w)
        elif accum is None:
            P.op(eng, lambda e, o=out, a=in0, x=s1, y=s2, p0=op0, p1=op1:
                 e.tensor_scalar(out=o, in0=a, scalar1=x, scalar2=y, op0=p0, op1=p1), r, w)
        else:
            P.op(eng, lambda e, o=out, a=in0, x=s1, y=s2, p0=op0, p1=op1, ac=accum:
                 e.tensor_scalar(out=o, in0=a, scalar1=x, scalar2=y, op0=p0, op1=p1, accum_out=ac), r, w)

    def stt(eng, out, in0, scalar, in1, op0, op1, r, w):
        P.op(eng, lambda e, o=out, a=in0, s=scalar, b=in1, p0=op0, p1=op1:
             e.scalar_tensor_tensor(out=o, in0=a, scalar=s, in1=b, op0=p0, op1=p1), r, w)

    def act(out, in_, func, r, w, bias=None, scale=1.0, accum=None):
        kw = {}
        if bias is not None:
            kw["bias"] = bias
        if accum is not None:
            kw["accum_out"] = accum
        P.op("act", lambda e, o=out, i=in_, f=func, s=scale, k=kw:
             e.activation(out=o, in_=i, func=f, scale=s, **k), r, w)

    def cp(eng, out, in_, r, w):
        if eng == "act":
            act(out, in_, AF.Copy, r, w)
        else:
            P.op(eng, lambda e, o=out, i=in_: e.tensor_copy(out=o, in_=i), r, w)

    def red(eng, out, in_, op, r, w):
        P.op(eng, lambda e, o=out, i=in_, p=op: e.tensor_reduce(out=o, in_=i, axis=AX.X, op=p), r, w)

    def memset(eng, ap, val, w):
        P.op(eng, lambda e, a=ap, v=val: e.memset(a, v), [], w)

    def dma(q, out, in_, r, w, slow=False):
        if slow:
            P.dma(q, lambda e, o=out, i=in_: e.dma_start(out=o, in_=i, allow_slow_non_contiguous=True), r, w)
        else:
            P.dma(q, lambda e, o=out, i=in_: e.dma_start(out=o, in_=i), r, w)

    def rsqrt_small(out, in_, scale, r_, w_, tmp):
        ts("dve", tmp, in_, scale, EPS, ALU.mult, ALU.add, r_, [w_[0]])
        act(tmp, tmp, AF.Ln, [w_[0]], [w_[0]])
        act(out, tmp, AF.Exp, [w_[0]], w_, scale=-0.5)

    def bc3(ap, n_mid, n_in):
        return ap.unsqueeze(2).to_broadcast([ap.shape[0], n_mid, n_in])

    def bcm(ap, n_mid):
        return ap.unsqueeze(1).to_broadcast([ap.shape[0], n_mid, ap.shape[1]])

    with ExitStack() as top:
        def sb(stack, name, shape, dt=F32):
            return Tl(stack.enter_context(nc.sbuf_tensor("sb_" + name, list(shape), dt)), name)

        def ps(stack, name, shape, dt=F32):
            return Tl(stack.enter_context(nc.psum_tensor("ps_" + name, list(shape), dt)), name)

        Xt = top.enter_context(nc.sbuf_tensor("sb_X", [128, NT, D], F32))
        RX = [Res("X%d" % t) for t in range(NT)]
        XS1T = sb(top, "XS1T", [128, 8, NS])
        HNS = sb(top, "HNS", [128, 8, NS], BF16)
        IDF2 = sb(top, "IDF2", [128, 128])
        IDB = sb(top, "IDB", [128, 128], BF16)
        GPREMLP = sb(top, "GPREMLP", [128, 8])
        BK = [ps(top, "B%d" % i, [128, 512]) for i in range(7)]
        TPB = ps(top, "TPB", [128, 1024], BF16)

        dma("sp", GPREMLP[:], gpremlp_d, [], [GPREMLP])

        all_res_p1 = []

        with ExitStack() as p1:
            cur = [p1]

            def s1(name, shape, dt=F32):
                tl = sb(cur[0], name, shape, dt)
                all_res_p1.append(tl.r)
                return tl

            WIN = p1.enter_context(nc.sbuf_tensor("sb_WIN", [128, 8, 3600], BF16))
            RWIN = [Res("WIN%d" % i) for i in range(len(WIN_MOVES))]
            WOUT = s1("WOUT", [128, 8, D], BF16)
            CONST = s1("CONST", [128, NCONST])
            GPRE = s1("GPRE", [128, 8])
            CW = s1("CW", [128, 4, 12])
            GPM = s1("GPM", [128, D])
            SMB = s1("SMB", [128, 16])
            GNG = s1("GNG", [128, 128])
            MNG = s1("MNG", [128, 128])
            all_res_p1.extend(RWIN)

            win_v = win_d.rearrange("(k p) c -> p k c", p=128)
            for i, (dst, src, n) in enumerate(WIN_MOVES):
                dma("pool", WIN[:, :, dst:dst + n], win_v[:, :, src:src + n], [], [RWIN[i]])
            dma("pool", WOUT[:], wout_d.rearrange("(k p) c -> p k c", p=128), [], [WOUT])
            dma("sp", CONST[:], consts_d, [], [CONST])
            dma("sp", GPRE[:], gpre_d, [], [GPRE])
            dma("sp", CW[:], cw_d.rearrange("p (j c) -> p j c", j=4), [], [CW])
            dma("sp", GPM[:], gpm_d.partition_broadcast(128), [], [GPM])
            dma("sp", SMB[:], small_d.partition_broadcast(128), [], [SMB])
            dma("sp", GNG[:], gng_d.partition_broadcast(128), [], [GNG])
            dma("sp", MNG[:], mng_d.partition_broadcast(128), [], [MNG])

            IDF = CONST[:, C_IDF:C_IDF + 128]
            TRI = CONST[:, C_TRI:C_TRI + 128]
            ONES = CONST[:, C_ONES:C_ONES + 128]
            SEL127 = CONST[:, C_SEL127:C_SEL127 + 128]
            MST = CONST[:, C_MST:C_MST + 128]
            MIT = CONST[:, C_MIT:C_MIT + 128]
            MI = CONST[:, C_MI:C_MI + 128]
            SELR = CONST[0:4, C_SELR:C_SELR + 512]
            ONES4 = CONST[0:4, C_ONES:C_ONES + 128]

            def rwin(c0, c1):
                out = []
                for i, (dst, src, n) in enumerate(WIN_MOVES):
                    if dst < c1 and c0 < dst + n:
                        out.append(RWIN[i])
                return out

            cp("dve", IDB[:], IDF, [CONST], [IDB])
            cp("pool", IDF2[:], IDF, [CONST], [IDF2])

            NEGC = s1("NEGC", [128, 12])
            SGN = s1("SGN", [128, 12])
            BIA = s1("BIA", [128, 12])
            memset("pool", NEGC[:], -1.0, [NEGC])
            act(NEGC[:, 4:8], SMB[:, 0:4], AF.Exp, [SMB, NEGC], [NEGC])
            ts("dve", NEGC[:, 4:8], NEGC[:, 4:8], -1.0, None, ALU.mult, None, [NEGC], [NEGC])
            memset("pool", SGN[:], -1.0, [SGN])
            memset("pool", SGN[:, 4:8], 1.0, [SGN])
            memset("pool", BIA[:], 0.0, [BIA])
            cp("dve", BIA[:, 4:8], SMB[:, 4:8], [SMB, BIA], [BIA])
            ts("dve", BIA[:, 8:12], SMB[:, 12:16], -1.0, None, ALU.mult, None, [SMB, BIA], [BIA])

            p1a = ExitStack()
            cur[0] = p1a
            XN = s1("XN", [128, D], BF16)
            HT = s1("HT", [128, 8, 128], BF16)
            PCC = [s1("PCC%d" % i, [128, 131]) for i in range(3)]
            ACC = [s1("ACC%d" % i, [128, 128]) for i in range(3)]
            SA4 = [s1("SA4_%d" % i, [128, 8]) for i in range(2)]
            TAIL = s1("TAIL", [128, 12, 3])
            QKF = s1("QKF", [128, 8, 128])
            GQT = s1("GQT", [128, 4, 128], BF16)
            GKT = s1("GKT", [128, 4, 128], BF16)
            GVT = s1("GVT", [128, 4, 128], BF16)
            MQT = s1("MQT", [64, 4, 128], BF16)
            MKT = s1("MKT", [64, 4, 128], BF16)
            ZSG = s1("ZSG", [128, 512])
            MOSG = s1("MOSG", [128, 512])
            MVt = s1("MVt", [128, 4, 128], BF16)
            GT = s1("GT", [128, 16])
            ARG = s1("ARG", [128, 12])
            GL = s1("GL", [128, 12])
            BETA = s1("BETA", [128, 4])
            IG = s1("IG", [128, 4])
            GF = s1("GF", [128, 8])
            COLS = s1("COLS", [128, 20])
            RT = s1("RT", [4, 5, 128])
            BD = s1("BD", [4, 4, 128])
            QKD = s1("QKD", [128, 4, 128], BF16)
            MB = [s1("MB%d" % i, [128, 512]) for i in range(2)]
            LB = [s1("LB%d" % i, [128, 512]) for i in range(2)]
            RR = s1("RR", [128, 512])
            BINV = s1("BINV", [128, 4, 128], BF16)
            GKt = s1("GKt", [128, 4, 128], BF16)
            XK = s1("XK", [128, 4, 128], BF16)
            KEND = s1("KEND", [128, 4, 128], BF16)
            GVb = s1("GVb", [128, 4, 128], BF16)
            SC4 = [s1("SC4_%d" % i, [128, 8]) for i in range(6)]
            LASTG = s1("LASTG", [128, 8])
            WKT = s1("WKT", [128, 4, 128], BF16)
            S32 = s1("S32", [128, 4, 128])
            SBh = s1("SBh", [128, 4, 128], BF16)
            U = s1("U", [128, 4, 128], BF16)
            TMP = [s1("TMP%d" % i, [128, 512]) for i in range(2)]
            WV = LB[1]
            MIX = View(MB[0].t.bitcast(BF16)[:, 0:1024], MB[0].r)
            MIXT = View(RR.t.bitcast(BF16)[:, 0:1024].rearrange("p (k t) -> p k t", k=8), RR.r)
            PK = s1("PK", [128, 4, 64], BF16)
            EXPQ = TMP[1]
            EXPA = MB[0]
            MKt = s1("MKt", [128, 4, 64], BF16)
            PQKb = s1("PQKb", [128, 4, 128], BF16)
            PQKT = s1("PQKT", [128, 4, 128], BF16)
            SG4 = [s1("SG4_%d" % i, [128, 8]) for i in range(6)]
            C32 = s1("C32", [64, 4, 128])
            CBh = s1("CBh", [64, 4, 128], BF16)
            N32 = s1("N32", [64, 4])
            NBh = s1("NBh", [64, 4, 2], BF16)
            MBC = s1("MBC", [128, 4])
            DMAX = s1("DMAX", [128, 4])
            T12 = s1("T12", [128, 12])
            WLC = s1("WLC", [64, 4])
            ONEB = s1("ONEB", [128, 2], BF16)
            for b in BK:
                all_res_p1.append(b.r)
            all_res_p1.append(TPB.r)

            memset("pool", TAIL[:], 0.0, [TAIL])
            memset("pool", S32[:], 0.0, [S32])
            memset("pool", SBh[:], 0.0, [SBh])
            memset("pool", C32[:], 0.0, [C32])
            memset("pool", CBh[:], 0.0, [CBh])
            memset("pool", N32[:], 0.0, [N32])
            memset("pool", NBh[:], 0.0, [NBh])
            memset("pool", MBC[:], 0.0, [MBC])
            memset("pool", ONEB[:], 1.0, [ONEB])

            def headnorm_gate(src, gate, dst, t0, ss, rs):
                tt("pool", t0[:], src[:], src[:], ALU.mult, [src], [t0])
                red("dve", ss[:, 0:4], t0[:].rearrange("p (h e) -> p h e", h=4), ALU.add, [t0], [ss])
                rsqrt_small(rs[:, 0:4], ss[:, 0:4], 1.0 / 128, [ss], [rs], rs[:, 4:8])
                tt("dve", t0[:].rearrange("p (h e) -> p h e", h=4), src[:].rearrange("p (h e) -> p h e", h=4),
                   bc3(rs[:, 0:4], 4, 128), ALU.mult, [src, rs], [t0])
                tt("dve", dst, t0[:], gate[:], ALU.mult, [t0, gate], [MIX])

            def gen_H(th):
                for k in range(8):
                    tr(TPB[:, k * 128:(k + 1) * 128], MIX[:, k * 128:(k + 1) * 128], IDB[:], [MIX, IDB], [TPB])
                yield
                cp("act", MIXT[:].rearrange("p k t -> p (k t)"), TPB[:], [TPB], [MIXT])
                yield
                for eh in range(2):
                    for k in range(8):
                        mm(BK[4 + eh][:], MIXT[:, k, :], WOUT[:, k, eh * 512:(eh + 1) * 512], k == 0, k == 7,
                           [MIXT, WOUT], [BK[4 + eh]])
                    yield
                ss, rs = SC4[0], SC4[1]
                for eh in range(2):
                    act(MIX[:, eh * 512:(eh + 1) * 512], BK[4 + eh][:], AF.Square, [BK[4 + eh]], [MIX, ss],
                        accum=ss[:, eh:eh + 1])
                    yield
                tt("dve", ss[:, 2:3], ss[:, 0:1], ss[:, 1:2], ALU.add, [ss], [ss])
                rsqrt_small(rs[:, 0:1], ss[:, 2:3], 1.0 / D, [ss], [rs], rs[:, 1:2])
                yield
                for eh in range(2):
                    sl = slice(eh * 512, (eh + 1) * 512)
                    stt("dve", BK[4 + eh][:], BK[4 + eh][:], rs[:, 0:1], GPM[:, sl], ALU.mult, ALU.mult,
                        [BK[4 + eh], rs, GPM], [BK[4 + eh]])
                    yield
                    tt("dve", Xt[:, th, sl], Xt[:, th, sl], BK[4 + eh][:], ALU.add, [RX[th], BK[4 + eh]], [RX[th]])
                    yield

            def interleave(ga, gb, na, nb):
                a = b = True
                while a or b:
                    for _ in range(na):
                        if a:
                            try:
                                next(ga)
                            except StopIteration:
                                a = False
                    for _ in range(nb):
                        if b:
                            try:
                                next(gb)
                            except StopIteration:
                                b = False

            def stage_A(t):
                Xv = Xt[:, t, :]
                ss, rs = SA4[0], SA4[1]
                act(XN[:], Xv, AF.Square, [RX[t]], [XN, ss], accum=ss[:, 0:1])
                rsqrt_small(rs[:, 0:1], ss[:, 0:1], 1.0 / D, [ss], [rs], rs[:, 1:2])
                ts("dve", XN[:], Xv, rs[:, 0:1], None, ALU.mult, None, [RX[t], rs], [XN])
                for k in range(8):
                    tr(TPB[:, k * 128:(k + 1) * 128], XN[:, k * 128:(k + 1) * 128], IDB[:], [XN, IDB], [TPB])
                tt("dve", HT[:], TPB[:].rearrange("p (k t) -> p k t", k=8), bc3(GPRE[:], 8, 128), ALU.mult,
                   [TPB, GPRE], [HT])

            for t in range(KNT):
                dma("sp", Xt[:, t, :], x_d[t * 128:(t + 1) * 128, :], [], [RX[t]])
            stage_A(0)

            for t in range(KNT):
                Xv = Xt[:, t, :]
                def b_mm(ch):
                    bk = BK[ch % 2]
                    c0 = ch * 128
                    for k in range(8):
                        mm(bk[:, 0:128], WIN[:, k, c0:c0 + 128], HT[:, k, :], k == 0, k == 7,
                           [HT] + rwin(c0, c0 + 128), [bk])

                def b_copy(ch):
                    bk, pc = BK[ch % 2], PCC[ch % 3]
                    cp("act", pc[:, 3:131], bk[:, 0:128], [bk], [pc])
                    cp("pool", pc[:, 0:3], TAIL[:, ch, :], [TAIL], [pc])

                def b_conv(ch):
                    pc, ac = PCC[ch % 3], ACC[ch % 3]
                    ts("dve", ac[:], pc[:, 0:128], CW[:, 0, ch:ch + 1], None, ALU.mult, None, [pc, CW], [ac])
                    for j in range(1, 4):
                        stt("dve", ac[:], pc[:, j:j + 128], CW[:, j, ch:ch + 1], ac[:], ALU.mult, ALU.add,
                            [pc, CW, ac], [ac])
                    cp("pool", TAIL[:, ch, :], pc[:, 128:131], [pc], [TAIL])

                def b_silu(ch):
                    ac = ACC[ch % 3]
                    if ch < 8:
                        act(QKF[:, ch, :], ac[:], AF.Silu, [ac], [QKF])
                    else:
                        act(GVT[:, ch - 8, :], ac[:], AF.Silu, [ac], [GVT])

                def gen_B():
                    for i in range(12 + 3):
                        if i < 12:
                            b_mm(i)
                        if 0 <= i - 1 < 12:
                            b_copy(i - 1)
                        if 0 <= i - 2 < 12:
                            b_conv(i - 2)
                        if 0 <= i - 3 < 12:
                            b_silu(i - 3)
                        yield
                if t > 0:
                    interleave(gen_B(), gen_H(t - 1), 1, 1)
                else:
                    for _ in gen_B():
                        pass

                def tok_proj(bk, c0, n):
                    for k in range(8):
                        mm(bk[:, 0:n], HT[:, k, :], WIN[:, k, c0:c0 + n], k == 0, k == 7,
                           [HT] + rwin(c0, c0 + n), [bk])
                tok_proj(BK[0], W_GZ, 512)
                act(TMP[0][:], BK[0][:], AF.Silu, [BK[0]], [TMP[0]])
                tt("pool", ZSG[:].rearrange("p (h e) -> p h e", h=4), TMP[0][:].rearrange("p (h e) -> p h e", h=4),
                   bcm(GNG[:], 4), ALU.mult, [TMP[0], GNG], [ZSG])
                tok_proj(BK[1], W_MO, 512)
                act(TMP[1][:], BK[1][:], AF.Sigmoid, [BK[1]], [TMP[1]])
                tt("pool", MOSG[:].rearrange("p (h e) -> p h e", h=4), TMP[1][:].rearrange("p (h e) -> p h e", h=4),
                   bcm(MNG[:], 4), ALU.mult, [TMP[1], MNG], [MOSG])
                for hh in range(8):
                    bk = BK[hh % 2]
                    c0 = W_MQK + hh * 64
                    for k in range(8):
                        mm(bk[0:64, 0:128], WIN[:, k, c0:c0 + 64], HT[:, k, :], k == 0, k == 7,
                           [HT] + rwin(c0, c0 + 64), [bk])
                    if hh < 4:
                        cp("act", MQT[:, hh, :], bk[0:64, 0:128], [bk], [MQT])
                    else:
                        act(MKT[:, hh - 4, :], bk[0:64, 0:128], AF.Copy, [bk], [MKT], scale=0.125)
                tok_proj(BK[0], W_MV, 512)
                cp("act", MVt[:].rearrange("p h e -> p (h e)"), BK[0][:], [BK[0]], [MVt])
                tok_proj(BK[1], W_GATE, 16)
                cp("dve", GT[:], BK[1][:, 0:16], [BK[1]], [GT])
                tt("dve", ARG[:], GT[:, 0:12], SGN[:], ALU.mult, [GT, SGN], [ARG])
                tt("dve", ARG[:], ARG[:], BIA[:], ALU.add, [ARG, BIA], [ARG])
                for hf in range(2):
                    act(TMP[hf][:], QKF[:, hf * 4:(hf + 1) * 4, :].rearrange("p a b -> p (a b)"), AF.Square,
                        [QKF], [TMP[hf]])
                    mm(BK[2 + hf][:], ONES, TMP[hf][:], True, True, [CONST, TMP[hf]], [BK[2 + hf]])
                for hf in range(2):
                    ts("dve", TMP[hf][:], BK[2 + hf][:], 1.0, EPS, ALU.mult, ALU.add, [BK[2 + hf]], [TMP[hf]])
                    act(TMP[hf][:], TMP[hf][:], AF.Ln, [TMP[hf]], [TMP[hf]])
                    act(TMP[hf][:], TMP[hf][:], AF.Exp, [TMP[hf]], [TMP[hf]], scale=-0.5)
                stt("dve", GQT[:].rearrange("p a b -> p (a b)"), QKF[:, 0:4, :].rearrange("p a b -> p (a b)"),
                    128.0 ** -0.5, TMP[0][:], ALU.mult, ALU.mult, [QKF, TMP[0]], [GQT])
                tt("pool", GKT[:].rearrange("p a b -> p (a b)"), QKF[:, 4:8, :].rearrange("p a b -> p (a b)"),
                   TMP[1][:], ALU.mult, [QKF, TMP[1]], [GKT])
                act(ARG[:], ARG[:], AF.Exp, [ARG], [ARG])
                act(ARG[:], ARG[:], AF.Ln, [ARG], [ARG], bias=1.0)
                tt("dve", GL[:], ARG[:], NEGC[:], ALU.mult, [ARG, NEGC], [GL])
                act(BETA[:], GL[:, 0:4], AF.Exp, [GL], [BETA])
                tt("dve", IG[:], GT[:, 12:16], SMB[:, 8:12], ALU.add, [GT, SMB], [IG])

                if t + 1 < KNT:
                    stage_A(t + 1)

                mm(BK[2][:, 0:8], TRI, GL[:, 4:12], True, True, [CONST, GL], [BK[2]])
                cp("dve", GF[:], BK[2][:, 0:8], [BK[2]], [GF])
                cp("pool", COLS[:, 0:4], GF[:, 0:4], [GF], [COLS])
                tt("dve", COLS[:, 4:8], GF[:, 0:4], GL[:, 0:4], ALU.add, [GF, GL], [COLS])
                cp("pool", COLS[:, 8:12], GF[:, 4:8], [GF], [COLS])
                tt("dve", COLS[:, 12:16], IG[:], GF[:, 4:8], ALU.subtract, [IG, GF], [COLS])
                ts("dve", COLS[:, 16:20], GF[:, 0:4], -1.0, None, ALU.mult, None, [GF], [COLS])
                for j in range(4):
                    tr(BK[3][0:4, j * 128:(j + 1) * 128], COLS[:, 4 * j:4 * j + 4], IDF, [COLS, CONST], [BK[3]])
                tr(BK[4][0:4, 0:128], COLS[:, 16:20], IDF, [COLS, CONST], [BK[4]])
                cp("dve", RT[:, 0:4, :].rearrange("p a b -> p (a b)"), BK[3][0:4, :], [BK[3]], [RT])
                cp("dve", RT[:, 4, :], BK[4][0:4, 0:128], [BK[4]], [RT])
                SELR3 = SELR.rearrange("p (a b) -> p a b", a=4)
                tt("dve", BD[:], bcm(RT[:, 1, :], 4), SELR3, ALU.mult, [RT, CONST], [BD])
                mm(BK[2][:], ONES4, BD[:].rearrange("p a b -> p (a b)"), True, False, [CONST, BD], [BK[2]])
                mm(BK[2][:], RT[:, 4, :], SELR, False, True, [RT, CONST], [BK[2]])
                tt("dve", EXPA[:].rearrange("p (h c) -> p h c", h=4), BK[2][:].rearrange("p (h c) -> p h c", h=4),
                   bcm(MST, 4), ALU.add, [BK[2], CONST], [EXPA])
                act(EXPA[:], EXPA[:], AF.Exp, [EXPA], [EXPA])
                tt("dve", BD[:], bcm(RT[:, 0, :], 4), SELR3, ALU.mult, [RT, CONST], [BD])
                mm(BK[3][:], ONES4, BD[:].rearrange("p a b -> p (a b)"), True, False, [CONST, BD], [BK[3]])
                mm(BK[3][:], RT[:, 4, :], SELR, False, True, [RT, CONST], [BK[3]])
                tt("dve", EXPQ[:].rearrange("p (h c) -> p h c", h=4), BK[3][:].rearrange("p (h c) -> p h c", h=4),
                   bcm(MIT, 4), ALU.add, [BK[3], CONST], [EXPQ])
                act(EXPQ[:], EXPQ[:], AF.Exp, [EXPQ], [EXPQ])
                for h in range(4):
                    mm(BK[4][:, h * 128:(h + 1) * 128], GKT[:, h, :], GKT[:, h, :], True, True, [GKT], [BK[4]])
                for h in range(4):
                    mm(BK[5][:, h * 128:(h + 1) * 128], GKT[:, h, :], GQT[:, h, :], True, True, [GKT, GQT], [BK[5]])
                tt("dve", MB[0][:], BK[4][:], EXPA[:], ALU.mult, [BK[4], EXPA], [MB[0]])
                tt("dve", QKD[:].rearrange("p h c -> p (h c)"), BK[5][:], EXPQ[:], ALU.mult, [BK[5], EXPQ], [QKD])

                for h in range(4):
                    tr(TPB[:, h * 128:(h + 1) * 128], GKT[:, h, :], IDB[:], [GKT, IDB], [TPB])
                    tr(TPB[:, 512 + h * 128:512 + (h + 1) * 128], GVT[:, h, :], IDB[:], [GVT, IDB], [TPB])
                cp("act", GKt[:].rearrange("p h d -> p (h d)"), TPB[:, 0:512], [TPB], [GKt])
                EG, BEG, EGL, GEND = SC4[0], SC4[1], SC4[2], SC4[3]
                act(EG[:, 0:4], GF[:, 0:4], AF.Exp, [GF], [EG])
                tt("dve", BEG[:, 0:4], EG[:, 0:4], BETA[:], ALU.mult, [EG, BETA], [BEG])
                mm(BK[2][:, 0:8], SEL127, GF[:], True, True, [CONST, GF], [BK[2]])
                cp("dve", LASTG[:], BK[2][:, 0:8], [BK[2]], [LASTG])
                tt("dve", EGL[:, 0:4], LASTG[:, 0:4], GF[:, 0:4], ALU.subtract, [LASTG, GF], [EGL])
                act(EGL[:, 0:4], EGL[:, 0:4], AF.Exp, [EGL], [EGL])
                act(GEND[:, 0:4], LASTG[:, 0:4], AF.Exp, [LASTG], [GEND])
                tt("dve", XK[:], GKt[:], bc3(BEG[:, 0:4], 4, 128), ALU.mult, [GKt, BEG], [XK])
                tt("pool", KEND[:], GKt[:], bc3(EGL[:, 0:4], 4, 128), ALU.mult, [GKt, EGL], [KEND])
                tt("dve", GVb[:], TPB[:, 512:1024].rearrange("p (h e) -> p h e", h=4), bc3(BETA[:], 4, 128),
                   ALU.mult, [TPB, BETA], [GVb])
                def gen_EF():
                    for h in range(4):
                        tr(BK[2][:, h * 128:(h + 1) * 128], MB[0][:, h * 128:(h + 1) * 128], IDF, [MB[0], CONST], [BK[2]])
                    yield
                    cp("act", LB[0][:], BK[2][:], [BK[2]], [LB[0]])
                    yield
                    tt("dve", RR[:].rearrange("p (h c) -> p h c", h=4), bcm(IDF, 4),
                       MB[0][:].rearrange("p (h c) -> p h c", h=4), ALU.subtract, [CONST, MB[0]], [RR])
                    yield
                    NLEV = 6
                    yield
                    for k in range(NLEV):
                        a, b = k % 2, (k + 1) % 2
                        for h in range(4):
                            sl = slice(h * 128, (h + 1) * 128)
                            mm(BK[3][:, sl], MB[a][:, sl], LB[a][:, sl], True, True, [MB[a], LB[a]], [BK[3]])
                        yield
                        if k < NLEV - 1:
                            for h in range(4):
                                sl = slice(h * 128, (h + 1) * 128)
                                mm(BK[4][:, sl], LB[a][:, sl], MB[a][:, sl], True, True, [MB[a], LB[a]], [BK[4]])
                            yield
                        cp("act", LB[b][:], BK[3][:], [BK[3]], [LB[b]])
                        yield
                        if k < NLEV - 1:
                            cp("dve", MB[b][:], BK[4][:], [BK[4]], [MB[b]])
                            yield
                        for h in range(4):
                            sl = slice(h * 128, (h + 1) * 128)
                            mm(BK[5][:, sl], LB[b][:, sl], RR[:, sl], True, True, [LB[b], RR], [BK[5]])
                        yield
                        tt("dve", RR[:], RR[:], BK[5][:], ALU.add, [RR, BK[5]], [RR])
                        yield
                    cp("act", BINV[:].rearrange("p h c -> p (h c)"), RR[:], [RR], [BINV])
                    yield
                    for h in range(4):
                        sl = slice(h * 128, (h + 1) * 128)
                        mm(BK[2][:, sl], BINV[:, h, :], GVb[:, h, :], True, True, [BINV, GVb], [BK[2]])
                        mm(BK[3][:, sl], XK[:, h, :], BINV[:, h, :], True, True, [BINV, XK], [BK[3]])
                    yield
                    cp("act", WV[:], BK[2][:], [BK[2]], [WV])
                    yield
                    cp("dve", WKT[:].rearrange("p h c -> p (h c)"), BK[3][:], [BK[3]], [WKT])
                    yield
                    for h in range(4):
                        sl = slice(h * 128, (h + 1) * 128)
                        mm(BK[4][:, sl], WKT[:, h, :], SBh[:, h, :], True, True, [WKT, SBh], [BK[4]])
                        mm(BK[5][:, sl], GQT[:, h, :], SBh[:, h, :], True, True, [GQT, SBh], [BK[5]])
                    yield
                    tt("dve", U[:].rearrange("p h e -> p (h e)"), WV[:], BK[4][:], ALU.subtract, [WV, BK[4]], [U])
                    yield
                    for h in range(4):
                        sl = slice(h * 128, (h + 1) * 128)
                        mm(BK[2][:, sl], QKD[:, h, :], U[:, h, :], True, True, [QKD, U], [BK[2]])
                        mm(BK[3][:, sl], KEND[:, h, :], U[:, h, :], True, True, [KEND, U], [BK[3]])
                    yield
                    OG = MB[1]
                    yield
                    tt("dve", OG[:].rearrange("p (h e) -> p h e", h=4), BK[5][:].rearrange("p (h e) -> p h e", h=4),
                       bc3(EG[:, 0:4], 4, 128), ALU.mult, [BK[5], EG], [OG])
                    yield
                    tt("dve", OG[:], OG[:], BK[2][:], ALU.add, [OG, BK[2]], [OG])
                    yield
                    for h in range(4):
                        stt("dve", S32[:, h, :], S32[:, h, :], GEND[:, h:h + 1], BK[3][:, h * 128:(h + 1) * 128],
                            ALU.mult, ALU.add, [S32, GEND, BK[3]], [S32])
                    yield
                    cp("act", SBh[:], S32[:], [S32], [SBh])
                    yield
                    headnorm_gate(OG, ZSG, MIX[:, 0:512], LB[0], SC4[4], SC4[5])
                    yield
                def gen_G():
                    tt("dve", BD[:], bcm(RT[:, 3, :], 4), SELR3, ALU.mult, [RT, CONST], [BD])
                    yield
                    mm(BK[6][:], ONES4, BD[:].rearrange("p a b -> p (a b)"), True, False, [CONST, BD], [BK[6]])
                    yield
                    mm(BK[6][:], RT[:, 2, :], SELR, False, True, [RT, CONST], [BK[6]])
                    yield
                    PD = TMP[0]
                    yield
                    tt("dve", PD[:].rearrange("p (h s) -> p h s", h=4), BK[6][:].rearrange("p (h s) -> p h s", h=4),
                       bcm(MI, 4), ALU.add, [BK[6], CONST], [PD])
                    yield
                    red("dve", DMAX[:], PD[:].rearrange("p (h s) -> p h s", h=4), ALU.max, [PD], [DMAX])
                    yield
                    for h in range(4):
                        mm(BK[0][:, h * 128:(h + 1) * 128], MQT[:, h, :], MKT[:, h, :], True, True, [MQT, MKT], [BK[0]])
                    yield
                    for h in range(4):
                        tr(TPB[:, h * 64:(h + 1) * 64], MKT[:, h, :], IDB[0:64, 0:64], [MKT, IDB], [TPB])
                    yield
                    cp("act", MKt[:].rearrange("p h d -> p (h d)"), TPB[:, 0:256], [TPB], [MKt])
                    yield
                    Bv, MT, WP, EMT = SG4[0], SG4[1], SG4[2], SG4[3]
                    yield
                    tt("dve", Bv[:, 0:4], GF[:, 4:8], MBC[:], ALU.add, [GF, MBC], [Bv])
                    yield
                    tt("dve", MT[:, 0:4], Bv[:, 0:4], DMAX[:], ALU.max, [Bv, DMAX], [MT])
                    yield
                    tt("dve", WP[:, 0:4], Bv[:, 0:4], MT[:, 0:4], ALU.subtract, [Bv, MT], [WP])
                    yield
                    act(WP[:, 0:4], WP[:, 0:4], AF.Exp, [WP], [WP])
                    yield
                    tt("dve", PD[:].rearrange("p (h s) -> p h s", h=4), PD[:].rearrange("p (h s) -> p h s", h=4),
                       bc3(MT[:, 0:4], 4, 128), ALU.subtract, [PD, MT], [PD])
                    yield
                    act(PD[:], PD[:], AF.Exp, [PD], [PD])
                    yield
                    tt("dve", PD[:], PD[:], BK[0][:], ALU.mult, [PD, BK[0]], [PD])
                    yield
                    RS = SG4[4]
                    yield
                    red("dve", RS[:, 0:4], PD[:].rearrange("p (h s) -> p h s", h=4), ALU.add, [PD], [RS])
                    yield
                    cp("act", PQKb[:].rearrange("p h s -> p (h s)"), PD[:], [PD], [PQKb])
                    yield
                    for h in range(4):
                        tr(TPB[:, 512 + h * 128:512 + (h + 1) * 128], PQKb[:, h, :], IDB[:], [PQKb, IDB], [TPB])
                    yield
                    cp("act", PQKT[:].rearrange("p h s -> p (h s)"), TPB[:, 512:1024], [TPB], [PQKT])
                    yield
                    for h in range(4):
                        sl = slice(h * 128, (h + 1) * 128)
                        mm(BK[1][:, sl], PQKT[:, h, :], MVt[:, h, :], True, True, [PQKT, MVt], [BK[1]])
                        mm(BK[0][:, sl], MQT[:, h, :], CBh[:, h, :], True, True, [MQT, CBh], [BK[0]])
                        mm(BK[6][:, 2 * h:2 * h + 2], MQT[:, h, :], NBh[:, h, :], True, True, [MQT, NBh], [BK[6]])
                    yield
                    NUM = TMP[0]
                    yield
                    tt("dve", NUM[:].rearrange("p (h e) -> p h e", h=4), BK[0][:].rearrange("p (h e) -> p h e", h=4),
                       bc3(WP[:, 0:4], 4, 128), ALU.mult, [BK[0], WP], [NUM])
                    yield
                    tt("dve", NUM[:], NUM[:], BK[1][:], ALU.add, [NUM, BK[1]], [NUM])
                    yield
                    DEN = SG4[5]
                    yield
                    tt("dve", DEN[:, 0:4], BK[6][:, 0:8].rearrange("p (h two) -> p h two", two=2)[:, :, 0], WP[:, 0:4], ALU.mult, [BK[6], WP], [DEN])
                    yield
                    tt("dve", DEN[:, 0:4], DEN[:, 0:4], RS[:, 0:4], ALU.add, [DEN, RS], [DEN])
                    yield
                    act(EMT[:, 0:4], MT[:, 0:4], AF.Exp, [MT], [EMT], scale=-1.0)
                    yield
                    ts("dve", DEN[:, 4:8], DEN[:, 0:4], -1.0, None, ALU.mult, None, [DEN], [DEN])
                    yield
                    tt("dve", DEN[:, 0:4], DEN[:, 0:4], DEN[:, 4:8], ALU.max, [DEN], [DEN])
                    yield
                    tt("dve", DEN[:, 0:4], DEN[:, 0:4], EMT[:, 0:4], ALU.max, [DEN, EMT], [DEN])
                    yield
                    P.op("dve", lambda e, d=DEN: e.reciprocal(out=d[:, 0:4], in_=d[:, 0:4]), [DEN], [DEN])
                    yield
                    tt("dve", NUM[:].rearrange("p (h e) -> p h e", h=4), NUM[:].rearrange("p (h e) -> p h e", h=4),
                       bc3(DEN[:, 0:4], 4, 128), ALU.mult, [NUM, DEN], [NUM])
                    yield
                    cp("pool", T12[:, 0:4], MT[:, 0:4], [MT], [T12])
                    yield
                    tt("dve", T12[:, 4:8], GF[:, 4:8], MT[:, 0:4], ALU.subtract, [GF, MT], [T12])
                    yield
                    cp("pool", T12[:, 8:12], WP[:, 0:4], [WP], [T12])
                    yield
                    mm(BK[6][:, 16:28], SEL127, T12[:], True, True, [CONST, T12], [BK[6]])
                    yield
                    cp("dve", MBC[:], BK[6][:, 16:20], [BK[6]], [MBC])
                    yield
                    PEND = SG4[4]
                    yield
                    tt("dve", PEND[:, 4:8], COLS[:, 12:16], BK[6][:, 20:24], ALU.add, [COLS, BK[6]], [PEND])
                    yield
                    act(PEND[:, 4:8], PEND[:, 4:8], AF.Exp, [PEND], [PEND])
                    yield
                    cp("dve", WLC[:], BK[6][0:64, 24:28], [BK[6]], [WLC])
                    yield
                    tt("dve", PK[:], MKt[:], bc3(PEND[:, 4:8], 4, 64), ALU.mult, [MKt, PEND], [PK])
                    yield
                    for h in range(4):
                        mm(BK[1][0:64, h * 128:(h + 1) * 128], PK[:, h, :], MVt[:, h, :], True, True, [PK, MVt], [BK[1]])
                    yield
                    for h in range(4):
                        mm(BK[6][0:64, 32 + 2 * h:34 + 2 * h], PK[:, h, :], ONEB[:], True, True, [PK, ONEB], [BK[6]])
                    yield
                    for h in range(4):
                        stt("dve", C32[:, h, :], C32[:, h, :], WLC[:, h:h + 1], BK[1][0:64, h * 128:(h + 1) * 128],
                            ALU.mult, ALU.add, [C32, WLC, BK[1]], [C32])
                    yield
                    tt("dve", N32[:], N32[:], WLC[:], ALU.mult, [N32, WLC], [N32])
                    yield
                    tt("dve", N32[:], N32[:], BK[6][0:64, 32:40].rearrange("p (a two) -> p a two", two=2)[:, :, 0],
                       ALU.add, [N32, BK[6]], [N32])
                    yield
                    cp("act", CBh[:], C32[:], [C32], [CBh])
                    yield
                    cp("act", NBh[:, :, 0], N32[:], [N32], [NBh])
                    yield

                interleave(gen_EF(), gen_G(), 2, 1)
                headnorm_gate(TMP[0], MOSG, MIX[:, 512:1024], TMP[1], SG4[4], SG4[5])

            for _ in gen_H(KNT - 1):
                pass

            dma("sp", pS_d.rearrange("h d e -> d h e"), S32[:], [S32], [])
            dma("sp", pC_d.rearrange("h d e -> d h e"), C32[:], [C32], [])
            dma("sp", pn_d.rearrange("h d -> d h"), N32[:], [N32], [], slow=True)
            dma("sp", pm_d, MBC[0:1, :], [MBC], [])
            for ch in range(12):
                tr(BK[2 + ch // 4][0:3, (ch % 4) * 128:(ch % 4 + 1) * 128], TAIL[:, ch, :], IDF, [TAIL, CONST],
                   [BK[2 + ch // 4]])
            for g in range(3):
                cp("dve", TMP[g % 2][0:3, :], BK[2 + g][0:3, :], [BK[2 + g]], [TMP[g % 2]])
                dma("sp", pconv_d[:, g * 512:(g + 1) * 512], TMP[g % 2][0:3, :], [TMP[g % 2]], [])


            P.barrier(all_res_p1 + RX + [XS1T.r, HNS.r, IDF2.r, IDB.r, GPREMLP.r])
            p1a.close()
            p1b = ExitStack()
            cur[0] = p1b
            R_ = NS
            GRP = 1
            XS = s1("XS", [R_, D])
            XNs = s1("XNs", [R_, D], BF16)
            HTs = s1("HTs", [128, 8, R_], BF16)
            SCV = s1("SCV", [R_, 512])
            XPT = s1("XPT", [128, 12, 4, R_])
            ACs = s1("ACs", [128, 12, R_])
            TM12 = s1("TM12", [128, 12, R_])
            SQs = s1("SQs", [128, 8, R_])
            GQs = s1("GQs", [128, 4, R_])
            GKs = s1("GKs", [128, 4, R_])
            GVs = s1("GVs", [128, 4, R_])
            MQs = s1("MQs", [64, 4, R_])
            MKs = s1("MKs", [64, 4, R_])
            ZSGs = s1("ZSGs", [R_, 512])
            MOSGs = s1("MOSGs", [R_, 512])
            MVs = s1("MVs", [R_, 4, 128])
            GTs = s1("GTs", [R_, 16])
            ARGs = s1("ARGs", [R_, 12])
            GLs = s1("GLs", [R_, 12])
            BETAs = s1("BETAs", [R_, 4])
            IGs = s1("IGs", [R_, 4])
            EGs = s1("EGs", [R_, 4])
            Qt = s1("Qt", [R_, 4, 128])
            Kt = s1("Kt", [R_, 4, 128])
            Vt = s1("Vt", [R_, 4, 128])
            MQt = s1("MQt", [R_, 4, 64])
            MKtt = s1("MKtt", [R_, 4, 64])
            PKt = s1("PKt", [R_, 4, 64])
            DIAGI = s1("DIAGI", [R_, 16, 16])
            M16 = s1("M16", [128, 16, 16])
            KD = [s1("KD%d" % i, [128, 4, 16]) for i in range(2)]
            QD = [s1("QD%d" % i, [128, 4, 16]) for i in range(2)]
            QDm = [s1("QDm%d" % i, [64, 4, 16]) for i in range(2)]
            DG4 = s1("DG4", [R_, 16, 4])
            EGBC = s1("EGBC", [128, 64])
            WPBC = s1("WPBC", [128, 64])
            S0b = [s1("S0b%d" % i, [128, 4, 128]) for i in range(2)]
            C0b = [s1("C0b%d" % i, [64, 4, 128]) for i in range(2)]
            Ug = s1("Ug", [R_, 4, 128])
            KROW = s1("KROW", [R_, 512])
            PKROW = View(DIAGI[:].rearrange("p a b -> p (a b)"), DIAGI.r)
            N0 = s1("N0", [R_, 4, 64])
            SMs = [s1("SMs%d" % i, [R_, 8]) for i in range(10)]
            T1 = s1("T1s", [R_, 512])
            T2 = KROW

            def hn_gate16(src, gate, dst, wres):
                ss, rs = SMs[8], SMs[9]
                tt("pool", T2[:], src[:], src[:], ALU.mult, [src], [T2])
                red("dve", ss[:, 0:4], T2[:].rearrange("p (h e) -> p h e", h=4), ALU.add, [T2], [ss])
                rsqrt_small(rs[:, 0:4], ss[:, 0:4], 1.0 / 128, [ss], [rs], rs[:, 4:8])
                tt("dve", T2[:].rearrange("p (h e) -> p h e", h=4), src[:].rearrange("p (h e) -> p h e", h=4),
                   bc3(rs[:, 0:4], 4, 128), ALU.mult, [src, rs], [T2])
                tt("dve", dst, T2[:], gate[:], ALU.mult, [T2, gate], [wres])

            IDF16 = CONST[0:R_, C_IDF:C_IDF + R_]
            ONES16 = CONST[0:R_, C_ONES:C_ONES + 128]

            dma("sp", XS[:], xs_d, [], [XS])
            dma("sp", N0[:].rearrange("p h d -> p (h d)"), sn_d, [], [N0])
            dma("sp", SMs[0][:, 0:4], sm_d, [], [SMs[0]])
            dma("sp", oconv_d[:, 0:2, :], sconv_d[:, 1:3, :], [], [])
            ss, rs = SMs[8], SMs[9]
            act(XNs[:], XS[:], AF.Square, [XS], [XNs, ss], accum=ss[:, 0:1])
            rsqrt_small(rs[:, 0:1], ss[:, 0:1], 1.0 / D, [ss], [rs], rs[:, 1:2])
            ts("dve", XNs[:], XS[:], rs[:, 0:1], None, ALU.mult, None, [XS, rs], [XNs])
            for k in range(8):
                tr(TPB[:, k * 128:k * 128 + R_], XNs[:, k * 128:(k + 1) * 128], IDB[0:R_, 0:R_], [XNs, IDB], [TPB])
            tt("dve", HTs[:], TPB[:].rearrange("p (k t) -> p k t", k=8)[:, :, 0:R_], bc3(GPRE[:], 8, R_), ALU.mult,
               [TPB, GPRE], [HTs])

            for ch in range(12):
                bk = BK[ch % 2]
                c0 = ch * 128
                for k in range(8):
                    mm(bk[:, 0:R_], WIN[:, k, c0:c0 + 128], HTs[:, k, :], k == 0, k == 7, [HTs] + rwin(c0, c0 + 128), [bk])
                cp("act", XPT[:, ch, 3, :], bk[:, 0:R_], [bk], [XPT])
            for hh in range(8):
                bk = BK[hh % 2]
                c0 = W_MQK + hh * 64
                for k in range(8):
                    mm(bk[0:64, 0:R_], WIN[:, k, c0:c0 + 64], HTs[:, k, :], k == 0, k == 7, [HTs] + rwin(c0, c0 + 64), [bk])
                if hh < 4:
                    cp("act", MQs[:, hh, :], bk[0:64, 0:R_], [bk], [MQs])
                else:
                    act(MKs[:, hh - 4, :], bk[0:64, 0:R_], AF.Copy, [bk], [MKs], scale=0.125)
            for g in range(3):
                for j in range(3):
                    dma("sp", SCV[:], sconv_d[:, j, g * 512:(g + 1) * 512], [], [SCV])
                    for c4 in range(4):
                        o0 = (j * 4 + c4) * R_
                        tr(BK[2][:, o0:o0 + R_], SCV[:, c4 * 128:(c4 + 1) * 128], IDF16, [SCV, CONST], [BK[2]])
                cp("dve", XPT[:, 4 * g:4 * g + 4, 0:3, :],
                   BK[2][:, 0:12 * R_].rearrange("p (j c r) -> p c j r", j=3, c=4), [BK[2]], [XPT])
            for g in range(3):
                bk = BK[3 + g % 2]
                for k in range(8):
                    mm(bk[0:R_, :], HTs[:, k, :], WIN[:, k, g * 512:(g + 1) * 512], k == 0, k == 7,
                       [HTs] + rwin(g * 512, (g + 1) * 512), [bk])
                cp("act", SCV[:], bk[0:R_, :], [bk], [SCV])
                dma("sp", oconv_d[:, 2, g * 512:(g + 1) * 512], SCV[:], [SCV], [])
            tt("dve", ACs[:], XPT[:, :, 0, :], bc3(CW[:, 0, :], 12, R_), ALU.mult, [XPT, CW], [ACs])
            for j in range(1, 4):
                tt("dve", TM12[:], XPT[:, :, j, :], bc3(CW[:, j, :], 12, R_), ALU.mult, [XPT, CW], [TM12])
                tt("dve", ACs[:], ACs[:], TM12[:], ALU.add, [ACs, TM12], [ACs])
            QKFs = TM12
            act(QKFs[:, 0:8, :], ACs[:, 0:8, :], AF.Silu, [ACs], [QKFs])
            act(GVs[:], ACs[:, 8:12, :], AF.Silu, [ACs], [GVs])
            act(SQs[:], QKFs[:, 0:8, :], AF.Square, [QKFs], [SQs])
            mm(BK[2][:, 0:8 * R_], ONES, SQs[:].rearrange("p a b -> p (a b)"), True, True, [CONST, SQs], [BK[2]])
            ts("dve", SQs[:].rearrange("p a b -> p (a b)"), BK[2][:, 0:8 * R_], 1.0, EPS, ALU.mult, ALU.add, [BK[2]], [SQs])
            act(SQs[:], SQs[:], AF.Ln, [SQs], [SQs])
            act(SQs[:], SQs[:], AF.Exp, [SQs], [SQs], scale=-0.5)
            stt("dve", GQs[:], QKFs[:, 0:4, :], 128.0 ** -0.5, SQs[:, 0:4, :], ALU.mult, ALU.mult, [QKFs, SQs], [GQs])
            tt("dve", GKs[:], QKFs[:, 4:8, :], SQs[:, 4:8, :], ALU.mult, [QKFs, SQs], [GKs])

            def tok16(bk, c0, n):
                for k in range(8):
                    mm(bk[0:R_, 0:n], HTs[:, k, :], WIN[:, k, c0:c0 + n], k == 0, k == 7, [HTs] + rwin(c0, c0 + n), [bk])
            tok16(BK[0], W_GZ, 512)
            act(T1[:], BK[0][0:R_, :], AF.Silu, [BK[0]], [T1])
            tt("dve", ZSGs[:].rearrange("p (h e) -> p h e", h=4), T1[:].rearrange("p (h e) -> p h e", h=4),
               bcm(GNG[0:R_, :], 4), ALU.mult, [T1, GNG], [ZSGs])
            tok16(BK[1], W_MV, 512)
            cp("act", MVs[:].rearrange("p h e -> p (h e)"), BK[1][0:R_, :], [BK[1]], [MVs])
            tok16(BK[0], W_MO, 512)
            act(T1[:], BK[0][0:R_, :], AF.Exp, [BK[0]], [T1], scale=-1.0)
            ts("dve", T1[:], T1[:], 1.0, None, ALU.add, None, [T1], [T1])
            P.op("dve", lambda e: e.reciprocal(out=T1[:], in_=T1[:]), [T1], [T1])
            tt("dve", MOSGs[:].rearrange("p (h e) -> p h e", h=4), T1[:].rearrange("p (h e) -> p h e", h=4),
               bcm(MNG[0:R_, :], 4), ALU.mult, [T1, MNG], [MOSGs])
            tok16(BK[1], W_GATE, 16)
            cp("dve", GTs[:], BK[1][0:R_, 0:16], [BK[1]], [GTs])
            tt("dve", ARGs[:], GTs[:, 0:12], SGN[0:R_, :], ALU.mult, [GTs, SGN], [ARGs])
            tt("dve", ARGs[:], ARGs[:], BIA[0:R_, :], ALU.add, [ARGs, BIA], [ARGs])
            act(ARGs[:], ARGs[:], AF.Exp, [ARGs], [ARGs])
            act(ARGs[:], ARGs[:], AF.Ln, [ARGs], [ARGs], bias=1.0)
            tt("dve", GLs[:], ARGs[:], NEGC[0:R_, :], ALU.mult, [ARGs, NEGC], [GLs])
            act(BETAs[:], GLs[:, 0:4], AF.Exp, [GLs], [BETAs])
            tt("dve", IGs[:], GTs[:, 12:16], SMB[0:R_, 8:12], ALU.add, [GTs, SMB], [IGs])
            act(EGs[:], GLs[:, 4:8], AF.Exp, [GLs], [EGs])
            M0s, Bs, MTs, WPs, Ps, QKG, QKMs, QNs = SMs[0], SMs[1], SMs[2], SMs[3], SMs[4], SMs[5], SMs[6], SMs[7]
            tt("dve", Bs[:, 0:4], GLs[:, 8:12], M0s[:, 0:4], ALU.add, [GLs, M0s], [Bs])
            tt("dve", MTs[:, 0:4], Bs[:, 0:4], IGs[:], ALU.max, [Bs, IGs], [MTs])
            tt("dve", WPs[:, 0:4], Bs[:, 0:4], MTs[:, 0:4], ALU.subtract, [Bs, MTs], [WPs])
            act(WPs[:, 0:4], WPs[:, 0:4], AF.Exp, [WPs], [WPs])
            tt("dve", Ps[:, 0:4], IGs[:], MTs[:, 0:4], ALU.subtract, [IGs, MTs], [Ps])
            act(Ps[:, 0:4], Ps[:, 0:4], AF.Exp, [Ps], [Ps])
            dma("sp", om_d, MTs[:, 0:4], [MTs], [])

            for h in range(4):
                tr(BK[2][0:R_, h * 128:(h + 1) * 128], GQs[:, h, :], IDF, [GQs, CONST], [BK[2]])
                tr(BK[3][0:R_, h * 128:(h + 1) * 128], GKs[:, h, :], IDF, [GKs, CONST], [BK[3]])
                tr(BK[4][0:R_, h * 128:(h + 1) * 128], GVs[:, h, :], IDF, [GVs, CONST], [BK[4]])
                tr(BK[5][0:R_, h * 64:(h + 1) * 64], MQs[:, h, :], CONST[0:64, C_IDF:C_IDF + 64], [MQs, CONST], [BK[5]])
                tr(BK[5][0:R_, 256 + h * 64:256 + (h + 1) * 64], MKs[:, h, :], CONST[0:64, C_IDF:C_IDF + 64],
                   [MKs, CONST], [BK[5]])
            cp("act", Qt[:].rearrange("p h d -> p (h d)"), BK[2][0:R_, :], [BK[2]], [Qt])
            cp("dve", Kt[:].rearrange("p h d -> p (h d)"), BK[3][0:R_, :], [BK[3]], [Kt])
            cp("act", Vt[:].rearrange("p h d -> p (h d)"), BK[4][0:R_, :], [BK[4]], [Vt])
            cp("dve", MQt[:].rearrange("p h d -> p (h d)"), BK[5][0:R_, 0:256], [BK[5]], [MQt])
            cp("act", MKtt[:].rearrange("p h d -> p (h d)"), BK[5][0:R_, 256:512], [BK[5]], [MKtt])
            tt("dve", T1[:], Qt[:].rearrange("p h d -> p (h d)"), Kt[:].rearrange("p h d -> p (h d)"), ALU.mult,
               [Qt, Kt], [T1])
            red("dve", QKG[:, 0:4], T1[:].rearrange("p (h d) -> p h d", h=4), ALU.add, [T1], [QKG])
            tt("dve", T1[:, 0:256], MQt[:].rearrange("p h d -> p (h d)"), MKtt[:].rearrange("p h d -> p (h d)"),
               ALU.mult, [MQt, MKtt], [T1])
            red("dve", QKMs[:, 0:4], T1[:, 0:256].rearrange("p (h d) -> p h d", h=4), ALU.add, [T1], [QKMs])
            tt("dve", T1[:, 0:256], MQt[:].rearrange("p h d -> p (h d)"), N0[:].rearrange("p h d -> p (h d)"),
               ALU.mult, [MQt, N0], [T1])
            red("dve", QNs[:, 0:4], T1[:, 0:256].rearrange("p (h d) -> p h d", h=4), ALU.add, [T1], [QNs])
            tt("dve", PKt[:], MKtt[:], bc3(Ps[:, 0:4], 4, 64), ALU.mult, [MKtt, Ps], [PKt])
            tt("dve", N0[:], N0[:], bc3(WPs[:, 0:4], 4, 64), ALU.mult, [N0, WPs], [N0])
            tt("dve", N0[:], N0[:], PKt[:], ALU.add, [N0, PKt], [N0])
            dma("sp", on_d, N0[:].rearrange("p h d -> p (h d)"), [N0], [])

            tt("dve", DIAGI[:], bc3(IDF16, R_, R_), bcm(IDF16, R_), ALU.mult, [CONST], [DIAGI])
            mm(BK[2][:, 0:R_ * R_], ONES16, DIAGI[:].rearrange("p a b -> p (a b)"), True, True, [CONST, DIAGI], [BK[2]])
            cp("dve", M16[:].rearrange("p a b -> p (a b)"), BK[2][:, 0:R_ * R_], [BK[2]], [M16])
            tt("dve", DG4[:], bcm(EGs[:], R_), bc3(IDF16, R_, 4), ALU.mult, [EGs, CONST], [DG4])
            mm(BK[3][:, 0:64], ONES16, DG4[:].rearrange("p a b -> p (a b)"), True, True, [CONST, DG4], [BK[3]])
            cp("dve", EGBC[:], BK[3][:, 0:64], [BK[3]], [EGBC])
            tt("dve", DG4[:], bcm(WPs[:, 0:4], R_), bc3(IDF16, R_, 4), ALU.mult, [WPs, CONST], [DG4])
            mm(BK[3][:, 0:64], ONES16, DG4[:].rearrange("p a b -> p (a b)"), True, True, [CONST, DG4], [BK[3]])
            cp("dve", WPBC[:], BK[3][:, 0:64], [BK[3]], [WPBC])

            def ld(r, i):
                dma("sp", S0b[i][:], sS_d[r].rearrange("h d e -> d h e"), [], [S0b[i]])
                dma("sp", C0b[i][:], sC_d[r].rearrange("h d e -> d h e"), [], [C0b[i]])
            ld(0, 0)
            for r in range(R_):
                i = r % 2
                if r + 1 < R_:
                    ld(r + 1, 1 - i)
                tt("dve", KD[i][:], GKs[:], bcm(M16[:, r, :], 4), ALU.mult, [GKs, M16], [KD[i]])
                tt("pool", QD[i][:], GQs[:], bcm(M16[:, r, :], 4), ALU.mult, [GQs, M16], [QD[i]])
                tt("dve", QDm[i][:], MQs[:], bcm(M16[0:64, r, :], 4), ALU.mult, [MQs, M16], [QDm[i]])
                for h in range(4):
                    sl = slice(h * 128, (h + 1) * 128)
                    st_, sp_ = (r == 0 and h == 0), (r == R_ - 1 and h == 3)
                    mm(BK[2][0:R_, sl], KD[i][:, h, :], S0b[i][:, h, :], st_, sp_, [KD[i], S0b[i]], [BK[2]], skip=True)
                    mm(BK[3][0:R_, sl], QD[i][:, h, :], S0b[i][:, h, :], st_, sp_, [QD[i], S0b[i]], [BK[3]], skip=True)
                    mm(BK[5][0:R_, sl], QDm[i][:, h, :], C0b[i][:, h, :], st_, sp_, [QDm[i], C0b[i]], [BK[5]], skip=True)
            KSp, QSp, QCp = BK[2], BK[3], BK[5]
            tt("dve", Ug[:], KSp[0:R_, :].rearrange("p (h e) -> p h e", h=4), bc3(EGs[:], 4, 128), ALU.mult,
               [KSp, EGs], [Ug])
            tt("dve", Ug[:], Vt[:], Ug[:], ALU.subtract, [Vt, Ug], [Ug])
            tt("dve", Ug[:], Ug[:], bc3(BETAs[:], 4, 128), ALU.mult, [Ug, BETAs], [Ug])
            ld(0, 0)
            for r in range(R_):
                i = r % 2
                if r + 1 < R_:
                    ld(r + 1, 1 - i)
                ts("dve", KROW[:], Kt[:].rearrange("p h d -> p (h d)"), IDF16[:, r:r + 1], None, ALU.mult, None,
                   [Kt, CONST], [KROW])
                ts("dve", PKROW[:], PKt[:].rearrange("p h d -> p (h d)"), IDF16[:, r:r + 1], None, ALU.mult, None,
                   [PKt, CONST], [PKROW])
                for h in range(4):
                    mm(BK[4][:, h * 128:(h + 1) * 128], KROW[:, h * 128:(h + 1) * 128], Ug[:, h, :], True, True,
                       [KROW, Ug], [BK[4]])
                for h in range(4):
                    mm(BK[6][0:64, h * 128:(h + 1) * 128], PKROW[:, h * 64:(h + 1) * 64], MVs[:, h, :], True, True,
                       [PKROW, MVs], [BK[6]])
                for h in range(4):
                    stt("dve", S0b[i][:, h, :], S0b[i][:, h, :], EGBC[:, r * 4 + h:r * 4 + h + 1],
                        BK[4][:, h * 128:(h + 1) * 128], ALU.mult, ALU.add, [S0b[i], EGBC, BK[4]], [S0b[i]])
                    stt("dve", C0b[i][:, h, :], C0b[i][:, h, :], WPBC[0:64, r * 4 + h:r * 4 + h + 1],
                        BK[6][0:64, h * 128:(h + 1) * 128], ALU.mult, ALU.add, [C0b[i], WPBC, BK[6]], [C0b[i]])
                dma("sp", oS_d[r].rearrange("h d e -> d h e"), S0b[i][:], [S0b[i]], [])
                dma("sp", oC_d[r].rearrange("h d e -> d h e"), C0b[i][:], [C0b[i]], [])

            MIXs = XNs
            tt("dve", T1[:].rearrange("p (h e) -> p h e", h=4), QSp[0:R_, :].rearrange("p (h e) -> p h e", h=4),
               bc3(EGs[:], 4, 128), ALU.mult, [QSp, EGs], [T1])
            tt("dve", Ug[:], Ug[:], bc3(QKG[:, 0:4], 4, 128), ALU.mult, [Ug, QKG], [Ug])
            tt("dve", T1[:], T1[:], Ug[:].rearrange("p h e -> p (h e)"), ALU.add, [T1, Ug], [T1])
            hn_gate16(T1, ZSGs, MIXs[:, 0:512], MIXs)
            PQ = SMs[5]
            tt("dve", PQ[:, 4:8], Ps[:, 0:4], QKMs[:, 0:4], ALU.mult, [Ps, QKMs], [PQ])
            tt("dve", T1[:].rearrange("p (h e) -> p h e", h=4), QCp[0:R_, :].rearrange("p (h e) -> p h e", h=4),
               bc3(WPs[:, 0:4], 4, 128), ALU.mult, [QCp, WPs], [T1])
            tt("dve", Ug[:], MVs[:], bc3(PQ[:, 4:8], 4, 128), ALU.mult, [MVs, PQ], [Ug])
            tt("dve", T1[:], T1[:], Ug[:].rearrange("p h e -> p (h e)"), ALU.add, [T1, Ug], [T1])
            DENs, EMTs = SMs[6], SMs[7]
            tt("dve", DENs[:, 4:8], WPs[:, 0:4], QNs[:, 0:4], ALU.mult, [WPs, QNs], [DENs])
            tt("dve", DENs[:, 4:8], DENs[:, 4:8], PQ[:, 4:8], ALU.add, [DENs, PQ], [DENs])
            ts("dve", DENs[:, 0:4], DENs[:, 4:8], -1.0, None, ALU.mult, None, [DENs], [DENs])
            tt("dve", DENs[:, 4:8], DENs[:, 4:8], DENs[:, 0:4], ALU.max, [DENs], [DENs])
            act(EMTs[:, 4:8], MTs[:, 0:4], AF.Exp, [MTs], [EMTs], scale=-1.0)
            tt("dve", DENs[:, 4:8], DENs[:, 4:8], EMTs[:, 4:8], ALU.max, [DENs, EMTs], [DENs])
            P.op("dve", lambda e, d=DENs: e.reciprocal(out=d[:, 4:8], in_=d[:, 4:8]), [DENs], [DENs])
            tt("dve", T1[:].rearrange("p (h e) -> p h e", h=4), T1[:].rearrange("p (h e) -> p h e", h=4),
               bc3(DENs[:, 4:8], 4, 128), ALU.mult, [T1, DENs], [T1])
            hn_gate16(T1, MOSGs, MIXs[:, 512:1024], MIXs)
            for k in range(8):
                tr(TPB[:, k * 128:k * 128 + R_], MIXs[:, k * 128:(k + 1) * 128], IDB[0:R_, 0:R_], [MIXs, IDB], [TPB])
            cp("act", HTs[:], TPB[:].rearrange("p (k t) -> p k t", k=8)[:, :, 0:R_], [TPB], [HTs])
            for eh in range(2):
                for k in range(8):
                    mm(BK[eh][0:R_, :], HTs[:, k, :], WOUT[:, k, eh * 512:(eh + 1) * 512], k == 0, k == 7,
                       [HTs, WOUT], [BK[eh]])
            ss, rs = SMs[8], SMs[9]
            for eh in range(2):
                act(XNs[:, eh * 512:(eh + 1) * 512], BK[eh][0:R_, :], AF.Square, [BK[eh]], [XNs, ss],
                    accum=ss[:, eh:eh + 1])
            tt("dve", ss[:, 2:3], ss[:, 0:1], ss[:, 1:2], ALU.add, [ss], [ss])
            rsqrt_small(rs[:, 0:1], ss[:, 2:3], 1.0 / D, [ss], [rs], rs[:, 1:2])
            for eh in range(2):
                sl = slice(eh * 512, (eh + 1) * 512)
                stt("dve", T1[:], BK[eh][0:R_, :], rs[:, 0:1], GPM[0:R_, sl], ALU.mult, ALU.mult, [BK[eh], rs, GPM], [T1])
                tt("dve", XS[:, sl], XS[:, sl], T1[:], ALU.add, [XS, T1], [XS])
            for k in range(8):
                tr(BK[2][:, k * R_:(k + 1) * R_], XS[:, k * 128:(k + 1) * 128], IDF16, [XS, CONST], [BK[2]])
            cp("dve", XS1T[:].rearrange("p k r -> p (k r)"), BK[2][:, 0:8 * R_], [BK[2]], [XS1T])
            act(XNs[:], XS[:], AF.Square, [XS], [XNs, ss], accum=ss[:, 4:5])
            rsqrt_small(rs[:, 4:5], ss[:, 4:5], 1.0 / D, [ss], [rs], rs[:, 5:6])
            ts("dve", XNs[:], XS[:], rs[:, 4:5], None, ALU.mult, None, [XS, rs], [XNs])
            for k in range(8):
                tr(TPB[:, k * 128:k * 128 + R_], XNs[:, k * 128:(k + 1) * 128], IDB[0:R_, 0:R_], [XNs, IDB], [TPB])
            tt("dve", HNS[:], TPB[:].rearrange("p (k t) -> p k t", k=8)[:, :, 0:R_], bc3(GPREMLP[:], 8, R_), ALU.mult,
               [TPB, GPREMLP], [HNS])

            P.barrier(all_res_p1 + RX + [XS1T.r, HNS.r, IDF2.r, IDB.r, GPREMLP.r])
            p1b.close()

        with ExitStack() as p2:
            WUP = p2.enter_context(nc.sbuf_tensor("sb_WUP", [128, 8, DFF], BF16))
            WDN = p2.enter_context(nc.sbuf_tensor("sb_WDN", [128, 32, D], BF16))
            NWC = 8
            RWU = [Res("WUP%d" % i) for i in range(NWC)]
            RWD = [Res("WDN%d" % i) for i in range(NWC)]
            wup_v = wup_d.rearrange("(k p) c -> p k c", p=128)
            wdn_v = wdn_d.rearrange("(k p) c -> p k c", p=128)
            for i in range(NWC if KPH2 else 0):
                dma("pool", WUP[:, :, i * 512:(i + 1) * 512], wup_v[:, :, i * 512:(i + 1) * 512], [], [RWU[i]])
                dma("pool", WDN[:, i * 4:(i + 1) * 4, :], wdn_v[:, i * 4:(i + 1) * 4, :], [], [RWD[i]])
            XN2 = sb(p2, "XN2", [128, D], BF16)
            GPL = sb(p2, "GPL", [128, D])
            dma("sp", GPL[:], gpl_d.partition_broadcast(128), [], [GPL])
            HN = sb(p2, "HN", [128, 8, 256], BF16)
            UT = [sb(p2, "UT%d" % i, [128, 256], BF16) for i in range(2)]
            RL = [sb(p2, "RL%d" % i, [128, 256]) for i in range(2)]
            SS2 = sb(p2, "SS2", [128, 8])

            NB2 = NT // 2 if KPH2 else 0

            def xv_of(blk, j):
                return Xt[:, 2 * blk + j, :], RX[2 * blk + j]

            def prep(blk):
                for j in range(2):
                    xv, rx = xv_of(blk, j)
                    act(XN2[:], xv, AF.Square, [rx], [XN2, SS2], accum=SS2[:, 0:1])
                    rsqrt_small(SS2[:, 1:2], SS2[:, 0:1], 1.0 / D, [SS2], [SS2], SS2[:, 2:3])
                    ts("dve", XN2[:], xv, SS2[:, 1:2], None, ALU.mult, None, [rx, SS2], [XN2])
                    for k in range(8):
                        tr(TPB[:, k * 128:(k + 1) * 128], XN2[:, k * 128:(k + 1) * 128], IDB[:], [XN2, IDB], [TPB])
                    tt("dve", HN[:, :, j * 128:(j + 1) * 128], TPB[:].rearrange("p (k t) -> p k t", k=8),
                       bc3(GPREMLP[:], 8, 128), ALU.mult, [TPB, GPREMLP], [HN])

            def up(f, hn, n):
                bk = BK[f % 2]
                for k in range(8):
                    mm(bk[:, 0:n], WUP[:, k, f * 128:(f + 1) * 128], hn[:, k, 0:n], k == 0, k == 7,
                       [hn, RWU[f // 4]], [bk])
                rl, ut = RL[f % 2], UT[f % 2]
                act(rl[:, 0:n], bk[:, 0:n], AF.Relu, [bk], [rl])
                tt("dve" if f % 2 == 0 else "pool", ut[:, 0:n], rl[:, 0:n], rl[:, 0:n], ALU.mult, [rl], [ut])

            def down(f, rows, ntl):
                ut = UT[f % 2]
                for j in range(ntl):
                    for eh in range(2):
                        ab = BK[2 + 2 * j + eh]
                        mm(ab[0:rows, :], ut[:, j * rows:(j + 1) * rows], WDN[:, f, eh * 512:(eh + 1) * 512],
                           f == 0, f == 31, [ut, RWD[f // 4]], [ab])

            def fin(blk):
                for j in range(2):
                    xv, rx = xv_of(blk, j)
                    for eh in range(2):
                        ab = BK[2 + 2 * j + eh]
                        act(XN2[:, eh * 512:(eh + 1) * 512], ab[:], AF.Square, [ab], [XN2, SS2],
                            accum=SS2[:, 3 + eh:4 + eh])
                    tt("dve", SS2[:, 5:6], SS2[:, 3:4], SS2[:, 4:5], ALU.add, [SS2], [SS2])
                    rsqrt_small(SS2[:, 6:7], SS2[:, 5:6], 1.0 / D, [SS2], [SS2], SS2[:, 7:8])
                    for eh in range(2):
                        ab = BK[2 + 2 * j + eh]
                        sl = slice(eh * 512, (eh + 1) * 512)
                        stt("dve", ab[:], ab[:], SS2[:, 6:7], GPL[:, sl], ALU.mult, ALU.mult, [ab, SS2, GPL], [ab])
                        tt("dve", xv[:, sl], xv[:, sl], ab[:], ALU.add, [rx, ab], [rx])
                    r0 = (2 * blk + j) * 128
                    dma("sp", y_d[r0:r0 + 128, :], xv, [rx], [])

            if NB2:
                prep(0)
            for blk in range(NB2):
                up(0, HN, 256)
                for f in range(32):
                    if f + 1 < 32:
                        up(f + 1, HN, 256)
                    elif blk + 1 < NB2:
                        prep(blk + 1)
                    down(f, 128, 2)
                fin(blk)

            R_ = NS
            if KPH2:
                up(0, HNS, R_)
                for f in range(32):
                    if f + 1 < 32:
                        up(f + 1, HNS, R_)
                    down(f, R_, 1)
            if KPH2:
                for eh in range(2):
                    act(XN2[0:R_, eh * 512:(eh + 1) * 512], BK[2 + eh][0:R_, :], AF.Square, [BK[2 + eh]], [XN2, SS2],
                        accum=SS2[0:R_, 3 + eh:4 + eh])
                tt("dve", SS2[0:R_, 5:6], SS2[0:R_, 3:4], SS2[0:R_, 4:5], ALU.add, [SS2], [SS2])
                rsqrt_small(SS2[0:R_, 6:7], SS2[0:R_, 5:6], 1.0 / D, [SS2], [SS2], SS2[0:R_, 7:8])
                YSB = XN2.t.bitcast(F32)
                for eh in range(2):
                    sl = slice(eh * 512, (eh + 1) * 512)
                    stt("dve", BK[2 + eh][0:R_, :], BK[2 + eh][0:R_, :], SS2[0:R_, 6:7], GPL[0:R_, sl], ALU.mult, ALU.mult,
                        [BK[2 + eh], SS2, GPL], [BK[2 + eh]])
                    for j in range(4):
                        tr(BK[4 + eh][0:R_, j * 128:(j + 1) * 128], XS1T[:, 4 * eh + j, :], IDF2[:], [XS1T, IDF2],
                           [BK[4 + eh]])
                    cp("act", YSB[0:R_, :], BK[4 + eh][0:R_, :], [BK[4 + eh]], [XN2])
                    tt("dve", YSB[0:R_, :], YSB[0:R_, :], BK[2 + eh][0:R_, :], ALU.add, [XN2, BK[2 + eh]], [XN2])
                    dma("sp", ys_d[:, sl], YSB[0:R_, :], [XN2], [])

        n_ins = P.finalize(top)
    return nc, n_ins


_CACHE = {}


def kernel(x_prompt, x_sample, state_gdn_conv, state_gdn_S, state_mlstm_C, state_mlstm_n, state_mlstm_m,
           norm_pre_mix, w_in, conv_w, a_log, dt_bias, gdn_norm_g, b_igate, b_fgate, mlstm_norm_g, w_out,
           norm_post_mix, norm_pre_mlp, w_up, w_down, norm_post_mlp):
    f = lambda a: np.ascontiguousarray(np.asarray(a, dtype=np.float32))
    if "nc" not in _CACHE:
        _CACHE["nc"] = build_program()
    nc, _ = _CACHE["nc"]
    consts = make_consts()
    small = np.concatenate([f(a_log)[0], f(dt_bias)[0], f(b_igate)[0], f(b_fgate)[0]])[None, :]
    shared = {
        "w_in": f(w_in)[0], "w_out": f(w_out)[0], "w_up": f(w_up)[0], "w_down": f(w_down)[0],
        "consts": consts,
        "gpre_fm": f(f(norm_pre_mix)[0].reshape(8, 128).T),
        "gpremlp_fm": f(f(norm_pre_mlp)[0].reshape(8, 128).T),
        "cw_fm": f(f(conv_w)[0].reshape(4, 12, 128).transpose(2, 0, 1).reshape(128, 48)),
        "gpostmix": f(norm_post_mix)[0][None, :], "gpostmlp": f(norm_post_mlp)[0][None, :],
        "small": f(small), "gdn_norm_g": f(gdn_norm_g)[0][None, :], "mlstm_norm_g": f(mlstm_norm_g)[0][None, :],
    }
    xp, xs = f(x_prompt), f(x_sample)
    in_maps = []
    for c in range(NCORES):
        r = slice(c * NS, (c + 1) * NS)
        m = dict(shared)
        m.update({
            "x": xp[c], "xs": xs[r, 0, :],
            "sconv": f(state_gdn_conv)[0, r], "sS": f(state_gdn_S)[0, r], "sC": f(state_mlstm_C)[0, r],
            "sn": f(state_mlstm_n)[0, r].reshape(NS, 256), "sm": f(state_mlstm_m)[0, r],
        })
        in_maps.append(m)
    res = run_bass_kernel_spmd(nc, in_maps, core_ids=list(range(NCORES)))
    R = res.results
    g = lambda k: np.stack([np.asarray(R[c][k], dtype=np.float32) for c in range(NCORES)])
    gc = lambda k: np.concatenate([np.asarray(R[c][k], dtype=np.float32) for c in range(NCORES)], axis=0)
    y_prompt = g("y")
    y_sample = gc("ys")[:, None, :]
    p_conv = g("pconv")[None]
    p_S = g("pS")[None]
    p_C = g("pC")[None]
    p_n = g("pn")[None]
    p_m = g("pm").reshape(NCORES, 4)[None]
    s_conv = gc("oconv")[None]
    s_S = gc("oS")[None]
    s_C = gc("oC")[None]
    s_n = gc("on").reshape(NCORES * NS, 4, 64)[None]
    s_m = gc("om")[None]
    return (y_prompt, y_sample, p_conv, p_S, p_C, p_n, p_m, s_conv, s_S, s_C, s_n, s_m)
```

```python
from contextlib import ExitStack
import numpy as np
import concourse.bass as bass
import concourse.mybir as mybir
from concourse.bass_utils import run_bass_kernel_spmd

F32 = mybir.dt.float32
BF16 = mybir.dt.bfloat16
ALU = mybir.AluOpType
AF = mybir.ActivationFunctionType
AX = mybir.AxisListType

NCORES = 8
T = 2048
NT = T // 128
D = 1024
DFF = 4096
NS = 16
EPS = 1e-6
NEG = -30000.0
import os
KNT = int(os.environ.get('KNT', NT))
KPH2 = int(os.environ.get('KPH2', 1))
KSTAGE = int(os.environ.get('KSTAGE', 99))
KSUB = int(os.environ.get('KSUB', 99))


class Res:
    __slots__ = ("name", "w", "rd")

    def __init__(self, name):
        self.name = name
        self.w = None
        self.rd = []


class Op:
    __slots__ = ("eng", "fn", "reads", "writes", "dma", "deps", "signal", "cnt", "sem", "waits")

    def __init__(self, eng, fn, reads, writes, dma):
        self.eng = eng
        self.fn = fn
        self.reads = reads
        self.writes = writes
        self.dma = dma
        self.deps = []
        self.signal = False
        self.cnt = 0
        self.sem = None
        self.waits = []


def _res(lst):
    out = []
    for x in lst:
        if x is None:
            continue
        if isinstance(x, Res):
            out.append(x)
        elif isinstance(x, (list, tuple)):
            out.extend(_res(x))
        else:
            out.append(x.r)
    return out


class Prog:
    ENGS = ("pe", "act", "dve", "pool", "sp")

    def __init__(self, nc, n_dma_sems=56):
        self.nc = nc
        self.ops = []
        self.n_dma_sems = n_dma_sems
        self.n_sw_sems = 8
        self.engobj = {"pe": nc.tensor, "act": nc.scalar, "dve": nc.vector,
                       "pool": nc.gpsimd, "sp": nc.sync}

    def op(self, eng, fn, r=(), w=()):
        self.ops.append(Op(eng, fn, _res(r), _res(w), False))

    def dma(self, eng, fn, r=(), w=()):
        self.ops.append(Op(eng, fn, _res(r), _res(w), True))

    def barrier(self, allres):
        for e in ("pe", "act", "dve", "pool", "sp"):
            self.ops.append(Op(e, (lambda en: en.nop(nofuse=True)), [], _res(allres), False))

    def finalize(self, stack):
        nc = self.nc
        ops = self.ops
        dma_slot_last = [None] * self.n_dma_sems
        dma_i = 0
        sw_i = 0
        for i, o in enumerate(ops):
            deps = set()
            for r in o.reads:
                if r.w is not None:
                    deps.add(r.w)
            for r in o.writes:
                if r.w is not None:
                    deps.add(r.w)
                for j in r.rd:
                    deps.add(j)
            if o.dma:
                if o.eng == "pool":
                    slot = sw_i % self.n_sw_sems
                    sw_i += 1
                else:
                    slot = self.n_sw_sems + dma_i % (self.n_dma_sems - self.n_sw_sems)
                    dma_i += 1
                o.sem = slot
                if dma_slot_last[slot] is not None:
                    deps.add(dma_slot_last[slot])
                dma_slot_last[slot] = i
            deps.discard(i)
            o.deps = sorted(deps)
            for r in o.reads:
                r.rd.append(i)
            for r in o.writes:
                r.w = i
                r.rd = []
            for j in o.deps:
                pj = ops[j]
                if pj.dma:
                    continue
                if pj.eng == "pe" and o.eng == "pe" and not o.dma:
                    continue
                pj.signal = True
        cnt = {e: 0 for e in self.ENGS}
        dcnt = [0] * self.n_dma_sems
        for o in ops:
            if o.dma:
                dcnt[o.sem] += 16
                o.cnt = dcnt[o.sem]
            elif o.signal:
                cnt[o.eng] += 1
                o.cnt = cnt[o.eng]
        seen = {e: {} for e in self.ENGS}
        for o in ops:
            need = {}
            for j in o.deps:
                pj = ops[j]
                if pj.dma:
                    key = ("d", pj.sem)
                else:
                    if pj.eng == "pe" and o.eng == "pe" and not o.dma:
                        continue
                    key = ("e", pj.eng)
                if pj.cnt > need.get(key, 0):
                    need[key] = pj.cnt
            s = seen[o.eng]
            for key, v in need.items():
                if s.get(key, 0) >= v:
                    continue
                s[key] = v
                o.waits.append((key, v))
        final_waits = [(("d", k), dcnt[k]) for k in range(self.n_dma_sems) if dcnt[k] > 0]
        final_waits += [(("e", e), cnt[e]) for e in self.ENGS if cnt[e] > 0 and e != "sp"]
        esem = {e: stack.enter_context(nc.semaphore("s_" + e)) for e in self.ENGS}
        dsem = [stack.enter_context(nc.semaphore("d_%d" % k)) for k in range(self.n_dma_sems)]

        def semof(key):
            return dsem[key[1]] if key[0] == "d" else esem[key[1]]

        n_ins = 0
        for o in ops:
            e = self.engobj[o.eng]
            for key, v in o.waits:
                e.wait_ge(semof(key), v)
                n_ins += 1
            ins = o.fn(e)
            n_ins += 1
            if o.dma:
                ins.then_inc(dsem[o.sem], 16)
            elif o.signal:
                ins.then_inc(esem[o.eng], 1)
        sp = self.engobj["sp"]
        for key, v in final_waits:
            if seen["sp"].get(key, 0) >= v:
                continue
            sp.wait_ge(semof(key), v)
        return n_ins


class View:
    __slots__ = ("t", "r")

    def __init__(self, ap, r):
        self.t = ap
        self.r = r

    def __getitem__(self, k):
        return self.t[k]


class Tl:
    __slots__ = ("t", "r")

    def __init__(self, t, name):
        self.t = t
        self.r = Res(name)

    def __getitem__(self, k):
        return self.t[k]


C_IDF, C_TRI, C_ONES, C_SEL127, C_MST, C_MIT, C_MI, C_SELR = (
    0, 128, 256, 384, 512, 640, 768, 896)
NCONST = 1408


def make_consts():
    c = np.zeros((128, NCONST), np.float32)
    s = np.arange(128)[:, None]
    f = np.arange(128)[None, :]
    c[:, C_IDF:C_IDF + 128] = (s == f)
    c[:, C_TRI:C_TRI + 128] = (s <= f)
    c[:, C_ONES:C_ONES + 128] = 1.0
    c[:, C_SEL127:C_SEL127 + 128] = (s == 127)
    mst = np.where(s < f, 0.0, NEG)
    mit = np.where(s <= f, 0.0, NEG)
    mi = np.where(f <= s, 0.0, NEG)
    c[:, C_MST:C_MST + 128] = mst
    c[:, C_MIT:C_MIT + 128] = mit
    c[:, C_MI:C_MI + 128] = mi
    for h in range(4):
        c[h, C_SELR + h * 128:C_SELR + (h + 1) * 128] = 1.0
    return c


W_QKV, W_MQK, W_GZ, W_MV, W_MO, W_GATE = 0, 1536, 2048, 2560, 3072, 3584
WIN_MOVES = [(0, 0, 1536), (1536, 2056, 512), (2048, 1536, 512), (2560, 2568, 512),
             (3072, 3080, 512), (3584, 2048, 8), (3592, 3596, 4), (3596, 3592, 4)]


def build_program():
    nc = bass.Bass("TRN2", target_bir_lowering=False)
    P = Prog(nc)

    def din(name, shape):
        return nc.dram_tensor(name, list(shape), F32, kind="ExternalInput").ap()

    def dout(name, shape):
        return nc.dram_tensor(name, list(shape), F32, kind="ExternalOutput").ap()

    x_d = din("x", [T, D])
    xs_d = din("xs", [NS, D])
    sconv_d = din("sconv", [NS, 3, 1536])
    sS_d = din("sS", [NS, 4, 128, 128])
    sC_d = din("sC", [NS, 4, 64, 128])
    sn_d = din("sn", [NS, 256])
    sm_d = din("sm", [NS, 4])
    win_d = din("w_in", [D, 3600])
    wout_d = din("w_out", [D, D])
    wup_d = din("w_up", [D, DFF])
    wdn_d = din("w_down", [DFF, D])
    consts_d = din("consts", [128, NCONST])
    gpre_d = din("gpre_fm", [128, 8])
    gpremlp_d = din("gpremlp_fm", [128, 8])
    cw_d = din("cw_fm", [128, 48])
    gpm_d = din("gpostmix", [1, D])
    gpl_d = din("gpostmlp", [1, D])
    small_d = din("small", [1, 16])
    gng_d = din("gdn_norm_g", [1, 128])
    mng_d = din("mlstm_norm_g", [1, 128])

    y_d = dout("y", [T, D])
    ys_d = dout("ys", [NS, D])
    pconv_d = dout("pconv", [3, 1536])
    pS_d = dout("pS", [4, 128, 128])
    pC_d = dout("pC", [4, 64, 128])
    pn_d = dout("pn", [4, 64])
    pm_d = dout("pm", [1, 4])
    oconv_d = dout("oconv", [NS, 3, 1536])
    oS_d = dout("oS", [NS, 4, 128, 128])
    oC_d = dout("oC", [NS, 4, 64, 128])
    on_d = dout("on", [NS, 256])
    om_d = dout("om", [NS, 4])

    def mm(out, lhsT, rhs, start, stop, r, w, skip=False):
        if skip:
            P.op("pe", lambda e, o=out, l=lhsT, rr=rhs, s=start, t=stop:
                 e.matmul(o, lhsT=l, rhs=rr, start=s, stop=t, skip_group_check=True), r, w)
        else:
            P.op("pe", lambda e, o=out, l=lhsT, rr=rhs, s=start, t=stop:
                 e.matmul(o, lhsT=l, rhs=rr, start=s, stop=t), r, w)

    def tr(out, in_, ident, r, w):
        P.op("pe", lambda e, o=out, i=in_, d=ident: e.transpose(o, i, d), r, w)

    def tt(eng, out, in0, in1, op, r, w):
        P.op(eng, lambda e, o=out, a=in0, b=in1, p=op: e.tensor_tensor(out=o, in0=a, in1=b, op=p), r, w)

    def ts(eng, out, in0, s1, s2, op0, op1, r, w, accum=None):
        if op1 is None:
            P.op(eng, lambda e, o=out, a=in0, x=s1, p0=op0:
                 e.tensor_scalar(out=o, in0=a, scalar1=x, scalar2=None, op0=p0), r, w)
        elif accum is None:
            P.op(eng, lambda e, o=out, a=in0, x=s1, y=s2, p0=op0, p1=op1:
                 e.tensor_scalar(out=o, in0=a, scalar1=x, scalar2=y, op0=p0, op1=p1), r, w)
        else:
            P.op(eng, lambda e, o=out, a=in0, x=s1, y=s2, p0=op0, p1=op1, ac=accum:
                 e.tensor_scalar(out=o, in0=a, scalar1=x, scalar2=y, op0=p0, op1=p1, accum_out=ac), r, w)

    def stt(eng, out, in0, scalar, in1, op0, op1, r, w):
        P.op(eng, lambda e, o=out, a=in0, s=scalar, b=in1, p0=op0, p1=op1:
             e.scalar_tensor_tensor(out=o, in0=a, scalar=s, in1=b, op0=p0, op1=p1), r, w)

    def act(out, in_, func, r, w, bias=None, scale=1.0, accum=None):
        kw = {}
        if bias is not None:
            kw["bias"] = bias
        if accum is not None:
            kw["accum_out"] = accum
        P.op("act", lambda e, o=out, i=in_, f=func, s=scale, k=kw:
             e.activation(out=o, in_=i, func=f, scale=s, **k), r, w)

    def cp(eng, out, in_, r, w):
        if eng == "act":
            act(out, in_, AF.Copy, r, w)
        else:
            P.op(eng, lambda e, o=out, i=in_: e.tensor_copy(out=o, in_=i), r, w)

    def red(eng, out, in_, op, r, w):
        P.op(eng, lambda e, o=out, i=in_, p=op: e.tensor_reduce(out=o, in_=i, axis=AX.X, op=p), r, w)

    def memset(eng, ap, val, w):
        P.op(eng, lambda e, a=ap, v=val: e.memset(a, v), [], w)

    def dma(q, out, in_, r, w, slow=False):
        if slow:
            P.dma(q, lambda e, o=out, i=in_: e.dma_start(out=o, in_=i, allow_slow_non_contiguous=True), r, w)
        else:
            P.dma(q, lambda e, o=out, i=in_: e.dma_start(out=o, in_=i), r, w)

    def rsqrt_small(out, in_, scale, r_, w_, tmp):
        ts("dve", tmp, in_, scale, EPS, ALU.mult, ALU.add, r_, [w_[0]])
        act(tmp, tmp, AF.Ln, [w_[0]], [w_[0]])
        act(out, tmp, AF.Exp, [w_[0]], w_, scale=-0.5)

    def bc3(ap, n_mid, n_in):
        return ap.unsqueeze(2).to_broadcast([ap.shape[0], n_mid, n_in])

    def bcm(ap, n_mid):
        return ap.unsqueeze(1).to_broadcast([ap.shape[0], n_mid, ap.shape[1]])

    with ExitStack() as top:
        def sb(stack, name, shape, dt=F32):
            return Tl(stack.enter_context(nc.sbuf_tensor("sb_" + name, list(shape), dt)), name)

        def ps(stack, name, shape, dt=F32):
            return Tl(stack.enter_context(nc.psum_tensor("ps_" + name, list(shape), dt)), name)

        Xt = top.enter_context(nc.sbuf_tensor("sb_X", [128, NT, D], F32))
        RX = [Res("X%d" % t) for t in range(NT)]
        XS1T = sb(top, "XS1T", [128, 8, NS])
        HNS = sb(top, "HNS", [128, 8, NS], BF16)
        IDF2 = sb(top, "IDF2", [128, 128])
        IDB = sb(top, "IDB", [128, 128], BF16)
        GPREMLP = sb(top, "GPREMLP", [128, 8])
        BK = [ps(top, "B%d" % i, [128, 512]) for i in range(7)]
        TPB = ps(top, "TPB", [128, 1024], BF16)

        dma("sp", GPREMLP[:], gpremlp_d, [], [GPREMLP])

        all_res_p1 = []

        with ExitStack() as p1:
            cur = [p1]

            def s1(name, shape, dt=F32):
                tl = sb(cur[0], name, shape, dt)
                all_res_p1.append(tl.r)
                return tl

            WIN = p1.enter_context(nc.sbuf_tensor("sb_WIN", [128, 8, 3600], BF16))
            RWIN = [Res("WIN%d" % i) for i in range(len(WIN_MOVES))]
            WOUT = s1("WOUT", [128, 8, D], BF16)
            CONST = s1("CONST", [128, NCONST])
            GPRE = s1("GPRE", [128, 8])
            CW = s1("CW", [128, 4, 12])
            GPM = s1("GPM", [128, D])
            SMB = s1("SMB", [128, 16])
            GNG = s1("GNG", [128, 128])
            MNG = s1("MNG", [128, 128])
            all_res_p1.extend(RWIN)

            win_v = win_d.rearrange("(k p) c -> p k c", p=128)
            for i, (dst, src, n) in enumerate(WIN_MOVES):
                dma("pool", WIN[:, :, dst:dst + n], win_v[:, :, src:src + n], [], [RWIN[i]])
            dma("pool", WOUT[:], wout_d.rearrange("(k p) c -> p k c", p=128), [], [WOUT])
            dma("sp", CONST[:], consts_d, [], [CONST])
            dma("sp", GPRE[:], gpre_d, [], [GPRE])
            dma("sp", CW[:], cw_d.rearrange("p (j c) -> p j c", j=4), [], [CW])
            dma("sp", GPM[:], gpm_d.partition_broadcast(128), [], [GPM])
            dma("sp", SMB[:], small_d.partition_broadcast(128), [], [SMB])
            dma("sp", GNG[:], gng_d.partition_broadcast(128), [], [GNG])
            dma("sp", MNG[:], mng_d.partition_broadcast(128), [], [MNG])

            IDF = CONST[:, C_IDF:C_IDF + 128]
            TRI = CONST[:, C_TRI:C_TRI + 128]
            ONES = CONST[:, C_ONES:C_ONES + 128]
            SEL127 = CONST[:, C_SEL127:C_SEL127 + 128]
            MST = CONST[:, C_MST:C_MST + 128]
            MIT = CONST[:, C_MIT:C_MIT + 128]
            MI = CONST[:, C_MI:C_MI + 128]
            SELR = CONST[0:4, C_SELR:C_SELR + 512]
            ONES4 = CONST[0:4, C_ONES:C_ONES + 128]

            def rwin(c0, c1):
                out = []
                for i, (dst, src, n) in enumerate(WIN_MOVES):
                    if dst < c1 and c0 < dst + n:
                        out.append(RWIN[i])
                return out

            cp("dve", IDB[:], IDF, [CONST], [IDB])
            cp("pool", IDF2[:], IDF, [CONST], [IDF2])

            NEGC = s1("NEGC", [128, 12])
            SGN = s1("SGN", [128, 12])
            BIA = s1("BIA", [128, 12])
            memset("pool", NEGC[:], -1.0, [NEGC])
            act(NEGC[:, 4:8], SMB[:, 0:4], AF.Exp, [SMB, NEGC], [NEGC])
            ts("dve", NEGC[:, 4:8], NEGC[:, 4:8], -1.0, None, ALU.mult, None, [NEGC], [NEGC])
            memset("pool", SGN[:], -1.0, [SGN])
            memset("pool", SGN[:, 4:8], 1.0, [SGN])
            memset("pool", BIA[:], 0.0, [BIA])
            cp("dve", BIA[:, 4:8], SMB[:, 4:8], [SMB, BIA], [BIA])
            ts("dve", BIA[:, 8:12], SMB[:, 12:16], -1.0, None, ALU.mult, None, [SMB, BIA], [BIA])

            p1a = ExitStack()
            cur[0] = p1a
            XN = s1("XN", [128, D], BF16)
            HT = s1("HT", [128, 8, 128], BF16)
            PCC = [s1("PCC%d" % i, [128, 131]) for i in range(3)]
            ACC = [s1("ACC%d" % i, [128, 128]) for i in range(3)]
            SA4 = [s1("SA4_%d" % i, [128, 8]) for i in range(2)]
            TAIL = s1("TAIL", [128, 12, 3])
            QKF = s1("QKF", [128, 8, 128])
            GQT = s1("GQT", [128, 4, 128], BF16)
            GKT = s1("GKT", [128, 4, 128], BF16)
            GVT = s1("GVT", [128, 4, 128], BF16)
            MQT = s1("MQT", [64, 4, 128], BF16)
            MKT = s1("MKT", [64, 4, 128], BF16)
            ZSG = s1("ZSG", [128, 512])
            MOSG = s1("MOSG", [128, 512])
            MVt = s1("MVt", [128, 4, 128], BF16)
            GT = s1("GT", [128, 16])
            ARG = s1("ARG", [128, 12])
            GL = s1("GL", [128, 12])
            BETA = s1("BETA", [128, 4])
            IG = s1("IG", [128, 4])
            GF = s1("GF", [128, 8])
            COLS = s1("COLS", [128, 20])
            RT = s1("RT", [4, 5, 128])
            BD = s1("BD", [4, 4, 128])
            QKD = s1("QKD", [128, 4, 128], BF16)
            MB = [s1("MB%d" % i, [128, 512]) for i in range(2)]
            LB = [s1("LB%d" % i, [128, 512]) for i in range(2)]
            RR = s1("RR", [128, 512])
            BINV = s1("BINV", [128, 4, 128], BF16)
            GKt = s1("GKt", [128, 4, 128], BF16)
            XK = s1("XK", [128, 4, 128], BF16)
            KEND = s1("KEND", [128, 4, 128], BF16)
            GVb = s1("GVb", [128, 4, 128], BF16)
            SC4 = [s1("SC4_%d" % i, [128, 8]) for i in range(6)]
            LASTG = s1("LASTG", [128, 8])
            WKT = s1("WKT", [128, 4, 128], BF16)
            S32 = s1("S32", [128, 4, 128])
            SBh = s1("SBh", [128, 4, 128], BF16)
            U = s1("U", [128, 4, 128], BF16)
            TMP = [s1("TMP%d" % i, [128, 512]) for i in range(2)]
            WV = LB[1]
            MIX = View(MB[0].t.bitcast(BF16)[:, 0:1024], MB[0].r)
            MIXT = View(RR.t.bitcast(BF16)[:, 0:1024].rearrange("p (k t) -> p k t", k=8), RR.r)
            PK = s1("PK", [128, 4, 64], BF16)
            EXPQ = TMP[1]
            EXPA = MB[0]
            MKt = s1("MKt", [128, 4, 64], BF16)
            PQKb = s1("PQKb", [128, 4, 128], BF16)
            PQKT = s1("PQKT", [128, 4, 128], BF16)
            SG4 = [s1("SG4_%d" % i, [128, 8]) for i in range(6)]
            C32 = s1("C32", [64, 4, 128])
            CBh = s1("CBh", [64, 4, 128], BF16)
            N32 = s1("N32", [64, 4])
            NBh = s1("NBh", [64, 4, 2], BF16)
            MBC = s1("MBC", [128, 4])
            DMAX = s1("DMAX", [128, 4])
            T12 = s1("T12", [128, 12])
            WLC = s1("WLC", [64, 4])
            ONEB = s1("ONEB", [128, 2], BF16)
            for b in BK:
                all_res_p1.append(b.r)
            all_res_p1.append(TPB.r)

            memset("pool", TAIL[:], 0.0, [TAIL])
            memset("pool", S32[:], 0.0, [S32])
            memset("pool", SBh[:], 0.0, [SBh])
            memset("pool", C32[:], 0.0, [C32])
            memset("pool", CBh[:], 0.0, [CBh])
            memset("pool", N32[:], 0.0, [N32])
            memset("pool", NBh[:], 0.0, [NBh])
            memset("pool", MBC[:], 0.0, [MBC])
            memset("pool", ONEB[:], 1.0, [ONEB])

            def headnorm_gate(src, gate, dst, t0, ss, rs):
                tt("pool", t0[:], src[:], src[:], ALU.mult, [src], [t0])
                red("dve", ss[:, 0:4], t0[:].rearrange("p (h e) -> p h e", h=4), ALU.add, [t0], [ss])
                rsqrt_small(rs[:, 0:4], ss[:, 0:4], 1.0 / 128, [ss], [rs], rs[:, 4:8])
                tt("dve", t0[:].rearrange("p (h e) -> p h e", h=4), src[:].rearrange("p (h e) -> p h e", h=4),
                   bc3(rs[:, 0:4], 4, 128), ALU.mult, [src, rs], [t0])
                tt("dve", dst, t0[:], gate[:], ALU.mult, [t0, gate], [MIX])

            def gen_H(th):
                for k in range(8):
                    tr(TPB[:, k * 128:(k + 1) * 128], MIX[:, k * 128:(k + 1) * 128], IDB[:], [MIX, IDB], [TPB])
                yield
                cp("act", MIXT[:].rearrange("p k t -> p (k t)"), TPB[:], [TPB], [MIXT])
                yield
                for eh in range(2):
                    for k in range(8):
                        mm(BK[4 + eh][:], MIXT[:, k, :], WOUT[:, k, eh * 512:(eh + 1) * 512], k == 0, k == 7,
                           [MIXT, WOUT], [BK[4 + eh]])
                    yield
                ss, rs = SC4[0], SC4[1]
                for eh in range(2):
                    act(MIX[:, eh * 512:(eh + 1) * 512], BK[4 + eh][:], AF.Square, [BK[4 + eh]], [MIX, ss],
                        accum=ss[:, eh:eh + 1])
                    yield
                tt("dve", ss[:, 2:3], ss[:, 0:1], ss[:, 1:2], ALU.add, [ss], [ss])
                rsqrt_small(rs[:, 0:1], ss[:, 2:3], 1.0 / D, [ss], [rs], rs[:, 1:2])
                yield
                for eh in range(2):
                    sl = slice(eh * 512, (eh + 1) * 512)
                    stt("dve", BK[4 + eh][:], BK[4 + eh][:], rs[:, 0:1], GPM[:, sl], ALU.mult, ALU.mult,
                        [BK[4 + eh], rs, GPM], [BK[4 + eh]])
                    yield
                    tt("dve", Xt[:, th, sl], Xt[:, th, sl], BK[4 + eh][:], ALU.add, [RX[th], BK[4 + eh]], [RX[th]])
                    yield

            def interleave(ga, gb, na, nb):
                a = b = True
                while a or b:
                    for _ in range(na):
                        if a:
                            try:
                                next(ga)
                            except StopIteration:
                                a = False
                    for _ in range(nb):
                        if b:
                            try:
                                next(gb)
                            except StopIteration:
                                b = False

            RB1 = [Res("B1s%d" % i) for i in range(4)]
            ORDER = [8, 9, 10, 11, 0, 1, 2, 3, 4, 5, 6, 7]

            def gen_Bconv():
                def b_mm(i):
                    ch = ORDER[i]
                    c0 = ch * 128
                    bk = BK[1 + i % 2]
                    for k in range(8):
                        mm(bk[:, 0:128], WIN[:, k, c0:c0 + 128], HT[:, k, :], k == 0, k == 7,
                           [HT] + rwin(c0, c0 + 128), [bk])

                def b_copy(i):
                    ch = ORDER[i]
                    pc = PCC[i % 3]
                    bk = BK[1 + i % 2]
                    cp("act", pc[:, 3:131], bk[:, 0:128], [bk], [pc])
                    cp("pool", pc[:, 0:3], TAIL[:, ch, :], [TAIL], [pc])

                def b_conv(i):
                    ch = ORDER[i]
                    pc, ac = PCC[i % 3], ACC[i % 3]
                    ts("dve", ac[:], pc[:, 0:128], CW[:, 0, ch:ch + 1], None, ALU.mult, None, [pc, CW], [ac])
                    for j in range(1, 3):
                        stt("dve", ac[:], pc[:, j:j + 128], CW[:, j, ch:ch + 1], ac[:], ALU.mult, ALU.add,
                            [pc, CW, ac], [ac])
                    if ch < 8:
                        stt("dve", QKF[:, ch, :], pc[:, 3:131], CW[:, 3, ch:ch + 1], ac[:], ALU.mult, ALU.add,
                            [pc, CW, ac], [QKF])
                    else:
                        stt("dve", ac[:], pc[:, 3:131], CW[:, 3, ch:ch + 1], ac[:], ALU.mult, ALU.add,
                            [pc, CW, ac], [ac])
                    cp("pool", TAIL[:, ch, :], pc[:, 128:131], [pc], [TAIL])

                def b_silu(i):
                    ch = ORDER[i]
                    if ch >= 8:
                        act(GVT[:, ch - 8, :], ACC[i % 3][:], AF.Silu, [ACC[i % 3]], [GVT])

                for i in range(12 + 3):
                    if i < 12:
                        b_mm(i)
                    if 0 <= i - 1 < 12:
                        b_copy(i - 1)
                    if 0 <= i - 2 < 12:
                        b_conv(i - 2)
                    if 0 <= i - 3 < 12:
                        b_silu(i - 3)
                    yield
                for hf in range(2):
                    act(QKF[:, hf * 4:(hf + 1) * 4, :], QKF[:, hf * 4:(hf + 1) * 4, :], AF.Silu, [QKF], [QKF])
                    yield

            def interleave3(gens, steps):
                alive = [True] * len(gens)
                while any(alive):
                    for gi, g in enumerate(gens):
                        for _ in range(steps[gi]):
                            if alive[gi]:
                                try:
                                    next(g)
                                except StopIteration:
                                    alive[gi] = False

            def stage_A(t):
                Xv = Xt[:, t, :]
                ss, rs = SA4[0], SA4[1]
                act(XN[:], Xv, AF.Square, [RX[t]], [XN, ss], accum=ss[:, 0:1])
                rsqrt_small(rs[:, 0:1], ss[:, 0:1], 1.0 / D, [ss], [rs], rs[:, 1:2])
                ts("dve", XN[:], Xv, rs[:, 0:1], None, ALU.mult, None, [RX[t], rs], [XN])
                for k in range(8):
                    tr(TPB[:, k * 128:(k + 1) * 128], XN[:, k * 128:(k + 1) * 128], IDB[:], [XN, IDB], [TPB])
                tt("dve", HT[:], TPB[:].rearrange("p (k t) -> p k t", k=8), bc3(GPRE[:], 8, 128), ALU.mult,
                   [TPB, GPRE], [HT])

            for t in range(KNT):
                dma("sp", Xt[:, t, :], x_d[t * 128:(t + 1) * 128, :], [], [RX[t]])
            stage_A(0)
            for _ in gen_Bconv():
                pass

            for t in range(KNT):
                Xv = Xt[:, t, :]
                hgen = gen_H(t - 1) if t > 0 else None

                def hstep(n=2):
                    if hgen is not None:
                        for _ in range(n):
                            try:
                                next(hgen)
                            except StopIteration:
                                break

                def tok_proj(bk, c0, n):
                    for k in range(8):
                        mm(bk[:, 0:n], HT[:, k, :], WIN[:, k, c0:c0 + n], k == 0, k == 7,
                           [HT] + rwin(c0, c0 + n), [bk])
                hstep(2)
                tok_proj(BK[0], W_GZ, 512)
                hstep(1)
                act(TMP[0][:], BK[0][:], AF.Silu, [BK[0]], [TMP[0]])
                tt("pool", ZSG[:].rearrange("p (h e) -> p h e", h=4), TMP[0][:].rearrange("p (h e) -> p h e", h=4),
                   bcm(GNG[:], 4), ALU.mult, [TMP[0], GNG], [ZSG])
                hstep(1)
                tok_proj(BK[1], W_MO, 512)
                hstep(1)
                act(TMP[1][:], BK[1][:], AF.Sigmoid, [BK[1]], [TMP[1]])
                tt("pool", MOSG[:].rearrange("p (h e) -> p h e", h=4), TMP[1][:].rearrange("p (h e) -> p h e", h=4),
                   bcm(MNG[:], 4), ALU.mult, [TMP[1], MNG], [MOSG])
                for hh in range(8):
                    bk = BK[hh % 2]
                    c0 = W_MQK + hh * 64
                    for k in range(8):
                        mm(bk[0:64, 0:128], WIN[:, k, c0:c0 + 64], HT[:, k, :], k == 0, k == 7,
                           [HT] + rwin(c0, c0 + 64), [bk])
                    if hh < 4:
                        cp("act", MQT[:, hh, :], bk[0:64, 0:128], [bk], [MQT])
                    else:
                        act(MKT[:, hh - 4, :], bk[0:64, 0:128], AF.Copy, [bk], [MKT], scale=0.125)
                    hstep(1)
                hstep(20)
                tok_proj(BK[0], W_MV, 512)
                cp("act", MVt[:].rearrange("p h e -> p (h e)"), BK[0][:], [BK[0]], [MVt])
                tok_proj(BK[1], W_GATE, 16)
                cp("dve", GT[:], BK[1][:, 0:16], [BK[1]], [GT])
                tt("dve", ARG[:], GT[:, 0:12], SGN[:], ALU.mult, [GT, SGN], [ARG])
                tt("dve", ARG[:], ARG[:], BIA[:], ALU.add, [ARG, BIA], [ARG])
                for hf in range(2):
                    act(TMP[hf][:], QKF[:, hf * 4:(hf + 1) * 4, :].rearrange("p a b -> p (a b)"), AF.Square,
                        [QKF], [TMP[hf]])
                    mm(BK[2 + hf][:], ONES, TMP[hf][:], True, True, [CONST, TMP[hf]], [BK[2 + hf]])
                for hf in range(2):
                    ts("dve", TMP[hf][:], BK[2 + hf][:], 1.0, EPS, ALU.mult, ALU.add, [BK[2 + hf]], [TMP[hf]])
                    act(TMP[hf][:], TMP[hf][:], AF.Ln, [TMP[hf]], [TMP[hf]])
                    act(TMP[hf][:], TMP[hf][:], AF.Exp, [TMP[hf]], [TMP[hf]], scale=-0.5)
                stt("dve", GQT[:].rearrange("p a b -> p (a b)"), QKF[:, 0:4, :].rearrange("p a b -> p (a b)"),
                    128.0 ** -0.5, TMP[0][:], ALU.mult, ALU.mult, [QKF, TMP[0]], [GQT])
                tt("pool", GKT[:].rearrange("p a b -> p (a b)"), QKF[:, 4:8, :].rearrange("p a b -> p (a b)"),
                   TMP[1][:], ALU.mult, [QKF, TMP[1]], [GKT])
                act(ARG[:], ARG[:], AF.Exp, [ARG], [ARG])
                act(ARG[:], ARG[:], AF.Ln, [ARG], [ARG], bias=1.0)
                tt("dve", GL[:], ARG[:], NEGC[:], ALU.mult, [ARG, NEGC], [GL])
                act(BETA[:], GL[:, 0:4], AF.Exp, [GL], [BETA])
                tt("dve", IG[:], GT[:, 12:16], SMB[:, 8:12], ALU.add, [GT, SMB], [IG])

                if t + 1 < KNT:
                    stage_A(t + 1)

                mm(BK[2][:, 0:8], TRI, GL[:, 4:12], True, True, [CONST, GL], [BK[2]])
                cp("dve", GF[:], BK[2][:, 0:8], [BK[2]], [GF])
                cp("pool", COLS[:, 0:4], GF[:, 0:4], [GF], [COLS])
                tt("dve", COLS[:, 4:8], GF[:, 0:4], GL[:, 0:4], ALU.add, [GF, GL], [COLS])
                cp("pool", COLS[:, 8:12], GF[:, 4:8], [GF], [COLS])
                tt("dve", COLS[:, 12:16], IG[:], GF[:, 4:8], ALU.subtract, [IG, GF], [COLS])
                ts("dve", COLS[:, 16:20], GF[:, 0:4], -1.0, None, ALU.mult, None, [GF], [COLS])
                for j in range(4):
                    tr(BK[3][0:4, j * 128:(j + 1) * 128], COLS[:, 4 * j:4 * j + 4], IDF, [COLS, CONST], [BK[3]])
                tr(BK[4][0:4, 0:128], COLS[:, 16:20], IDF, [COLS, CONST], [BK[4]])
                cp("dve", RT[:, 0:4, :].rearrange("p a b -> p (a b)"), BK[3][0:4, :], [BK[3]], [RT])
                cp("dve", RT[:, 4, :], BK[4][0:4, 0:128], [BK[4]], [RT])
                SELR3 = SELR.rearrange("p (a b) -> p a b", a=4)
                tt("dve", BD[:], bcm(RT[:, 1, :], 4), SELR3, ALU.mult, [RT, CONST], [BD])
                mm(BK[2][:], ONES4, BD[:].rearrange("p a b -> p (a b)"), True, False, [CONST, BD], [BK[2]])
                mm(BK[2][:], RT[:, 4, :], SELR, False, True, [RT, CONST], [BK[2]])
                tt("dve", EXPA[:].rearrange("p (h c) -> p h c", h=4), BK[2][:].rearrange("p (h c) -> p h c", h=4),
                   bcm(MST, 4), ALU.add, [BK[2], CONST], [EXPA])
                act(EXPA[:], EXPA[:], AF.Exp, [EXPA], [EXPA])
                tt("dve", BD[:], bcm(RT[:, 0, :], 4), SELR3, ALU.mult, [RT, CONST], [BD])
                mm(BK[3][:], ONES4, BD[:].rearrange("p a b -> p (a b)"), True, False, [CONST, BD], [BK[3]])
                mm(BK[3][:], RT[:, 4, :], SELR, False, True, [RT, CONST], [BK[3]])
                tt("dve", EXPQ[:].rearrange("p (h c) -> p h c", h=4), BK[3][:].rearrange("p (h c) -> p h c", h=4),
                   bcm(MIT, 4), ALU.add, [BK[3], CONST], [EXPQ])
                act(EXPQ[:], EXPQ[:], AF.Exp, [EXPQ], [EXPQ])
                for h in range(4):
                    mm(BK[4][:, h * 128:(h + 1) * 128], GKT[:, h, :], GKT[:, h, :], True, True, [GKT], [BK[4]])
                for h in range(4):
                    mm(BK[5][:, h * 128:(h + 1) * 128], GKT[:, h, :], GQT[:, h, :], True, True, [GKT, GQT], [BK[5]])
                tt("dve", MB[0][:], BK[4][:], EXPA[:], ALU.mult, [BK[4], EXPA], [MB[0]])
                tt("dve", QKD[:].rearrange("p h c -> p (h c)"), BK[5][:], EXPQ[:], ALU.mult, [BK[5], EXPQ], [QKD])

                for h in range(4):
                    tr(TPB[:, h * 128:(h + 1) * 128], GKT[:, h, :], IDB[:], [GKT, IDB], [TPB])
                    tr(TPB[:, 512 + h * 128:512 + (h + 1) * 128], GVT[:, h, :], IDB[:], [GVT, IDB], [TPB])
                cp("act", GKt[:].rearrange("p h d -> p (h d)"), TPB[:, 0:512], [TPB], [GKt])
                EG, BEG, EGL, GEND = SC4[0], SC4[1], SC4[2], SC4[3]
                act(EG[:, 0:4], GF[:, 0:4], AF.Exp, [GF], [EG])
                tt("dve", BEG[:, 0:4], EG[:, 0:4], BETA[:], ALU.mult, [EG, BETA], [BEG])
                mm(BK[2][:, 0:8], SEL127, GF[:], True, True, [CONST, GF], [BK[2]])
                cp("dve", LASTG[:], BK[2][:, 0:8], [BK[2]], [LASTG])
                tt("dve", EGL[:, 0:4], LASTG[:, 0:4], GF[:, 0:4], ALU.subtract, [LASTG, GF], [EGL])
                act(EGL[:, 0:4], EGL[:, 0:4], AF.Exp, [EGL], [EGL])
                act(GEND[:, 0:4], LASTG[:, 0:4], AF.Exp, [LASTG], [GEND])
                tt("dve", XK[:], GKt[:], bc3(BEG[:, 0:4], 4, 128), ALU.mult, [GKt, BEG], [XK])
                tt("pool", KEND[:], GKt[:], bc3(EGL[:, 0:4], 4, 128), ALU.mult, [GKt, EGL], [KEND])
                tt("dve", GVb[:], TPB[:, 512:1024].rearrange("p (h e) -> p h e", h=4), bc3(BETA[:], 4, 128),
                   ALU.mult, [TPB, BETA], [GVb])
                def gen_EF():
                    for h in range(4):
                        tr(BK[5][:, h * 128:(h + 1) * 128], MB[0][:, h * 128:(h + 1) * 128], IDF, [MB[0], CONST], [BK[5]])
                    yield
                    cp("act", LB[0][:], BK[5][:], [BK[5]], [LB[0]])
                    yield
                    tt("dve", RR[:].rearrange("p (h c) -> p h c", h=4), bcm(IDF, 4),
                       MB[0][:].rearrange("p (h c) -> p h c", h=4), ALU.subtract, [CONST, MB[0]], [RR])
                    yield
                    NLEV = 6
                    yield
                    for k in range(NLEV):
                        a, b = k % 2, (k + 1) % 2
                        for h in range(4):
                            sl = slice(h * 128, (h + 1) * 128)
                            mm(BK[3][:, sl], MB[a][:, sl], LB[a][:, sl], True, True, [MB[a], LB[a]], [BK[3]])
                        yield
                        if k < NLEV - 1:
                            for h in range(4):
                                sl = slice(h * 128, (h + 1) * 128)
                                mm(BK[4][:, sl], LB[a][:, sl], MB[a][:, sl], True, True, [MB[a], LB[a]], [BK[4]])
                            yield
                        cp("act", LB[b][:], BK[3][:], [BK[3]], [LB[b]])
                        yield
                        if k < NLEV - 1:
                            cp("dve", MB[b][:], BK[4][:], [BK[4]], [MB[b]])
                            yield
                        for h in range(4):
                            sl = slice(h * 128, (h + 1) * 128)
                            mm(BK[5][:, sl], LB[b][:, sl], RR[:, sl], True, True, [LB[b], RR], [BK[5]])
                        yield
                        tt("dve", RR[:], RR[:], BK[5][:], ALU.add, [RR, BK[5]], [RR])
                        yield
                    cp("act", BINV[:].rearrange("p h c -> p (h c)"), RR[:], [RR], [BINV])
                    yield
                    for h in range(4):
                        sl = slice(h * 128, (h + 1) * 128)
                        mm(BK[3][:, sl], BINV[:, h, :], GVb[:, h, :], True, True, [BINV, GVb], [BK[3]])
                        mm(BK[4][:, sl], XK[:, h, :], BINV[:, h, :], True, True, [BINV, XK], [BK[4]])
                    yield
                    cp("act", WV[:], BK[3][:], [BK[3]], [WV])
                    yield
                    cp("dve", WKT[:].rearrange("p h c -> p (h c)"), BK[4][:], [BK[4]], [WKT])
                    yield
                    for h in range(4):
                        sl = slice(h * 128, (h + 1) * 128)
                        mm(BK[3][:, sl], WKT[:, h, :], SBh[:, h, :], True, True, [WKT, SBh], [BK[3]])
                        mm(BK[5][:, sl], GQT[:, h, :], SBh[:, h, :], True, True, [GQT, SBh], [BK[5]])
                    yield
                    tt("dve", U[:].rearrange("p h e -> p (h e)"), WV[:], BK[3][:], ALU.subtract, [WV, BK[3]], [U])
                    yield
                    for h in range(4):
                        sl = slice(h * 128, (h + 1) * 128)
                        mm(BK[4][:, sl], QKD[:, h, :], U[:, h, :], True, True, [QKD, U], [BK[4]])
                        mm(BK[3][:, sl], KEND[:, h, :], U[:, h, :], True, True, [KEND, U], [BK[3]])
                    yield
                    OG = MB[1]
                    yield
                    tt("dve", OG[:].rearrange("p (h e) -> p h e", h=4), BK[5][:].rearrange("p (h e) -> p h e", h=4),
                       bc3(EG[:, 0:4], 4, 128), ALU.mult, [BK[5], EG], [OG])
                    yield
                    tt("dve", OG[:], OG[:], BK[4][:], ALU.add, [OG, BK[4]], [OG])
                    yield
                    for h in range(4):
                        stt("dve", S32[:, h, :], S32[:, h, :], GEND[:, h:h + 1], BK[3][:, h * 128:(h + 1) * 128],
                            ALU.mult, ALU.add, [S32, GEND, BK[3]], [S32])
                    yield
                    cp("act", SBh[:], S32[:], [S32], [SBh])
                    yield
                    headnorm_gate(OG, ZSG, MIX[:, 0:512], LB[0], SC4[4], SC4[5])
                    yield
                def gen_G():
                    tt("dve", BD[:], bcm(RT[:, 3, :], 4), SELR3, ALU.mult, [RT, CONST], [BD])
                    yield
                    mm(BK[6][:], ONES4, BD[:].rearrange("p a b -> p (a b)"), True, False, [CONST, BD], [BK[6]])
                    yield
                    mm(BK[6][:], RT[:, 2, :], SELR, False, True, [RT, CONST], [BK[6]])
                    yield
                    PD = TMP[0]
                    yield
                    tt("dve", PD[:].rearrange("p (h s) -> p h s", h=4), BK[6][:].rearrange("p (h s) -> p h s", h=4),
                       bcm(MI, 4), ALU.add, [BK[6], CONST], [PD])
                    yield
                    red("dve", DMAX[:], PD[:].rearrange("p (h s) -> p h s", h=4), ALU.max, [PD], [DMAX])
                    yield
                    for h in range(4):
                        mm(BK[0][:, h * 128:(h + 1) * 128], MQT[:, h, :], MKT[:, h, :], True, True, [MQT, MKT], [BK[0]])
                    yield
                    for h in range(4):
                        tr(TPB[:, h * 64:(h + 1) * 64], MKT[:, h, :], IDB[0:64, 0:64], [MKT, IDB], [TPB])
                    yield
                    cp("act", MKt[:].rearrange("p h d -> p (h d)"), TPB[:, 0:256], [TPB], [MKt])
                    yield
                    Bv, MT, WP, EMT = SG4[0], SG4[1], SG4[2], SG4[3]
                    yield
                    tt("dve", Bv[:, 0:4], GF[:, 4:8], MBC[:], ALU.add, [GF, MBC], [Bv])
                    yield
                    tt("dve", MT[:, 0:4], Bv[:, 0:4], DMAX[:], ALU.max, [Bv, DMAX], [MT])
                    yield
                    tt("dve", WP[:, 0:4], Bv[:, 0:4], MT[:, 0:4], ALU.subtract, [Bv, MT], [WP])
                    yield
                    act(WP[:, 0:4], WP[:, 0:4], AF.Exp, [WP], [WP])
                    yield
                    tt("dve", PD[:].rearrange("p (h s) -> p h s", h=4), PD[:].rearrange("p (h s) -> p h s", h=4),
                       bc3(MT[:, 0:4], 4, 128), ALU.subtract, [PD, MT], [PD])
                    yield
                    act(PD[:], PD[:], AF.Exp, [PD], [PD])
                    yield
                    tt("dve", PD[:], PD[:], BK[0][:], ALU.mult, [PD, BK[0]], [PD])
                    yield
                    RS = SG4[4]
                    yield
                    red("dve", RS[:, 0:4], PD[:].rearrange("p (h s) -> p h s", h=4), ALU.add, [PD], [RS])
                    yield
                    cp("act", PQKb[:].rearrange("p h s -> p (h s)"), PD[:], [PD], [PQKb])
                    yield
                    for h in range(4):
                        tr(TPB[:, 512 + h * 128:512 + (h + 1) * 128], PQKb[:, h, :], IDB[:], [PQKb, IDB], [TPB])
                    yield
                    cp("act", PQKT[:].rearrange("p h s -> p (h s)"), TPB[:, 512:1024], [TPB], [PQKT])
                    yield
                    for h in range(4):
                        sl = slice(h * 128, (h + 1) * 128)
                        mm(BK[0][:, sl], MQT[:, h, :], CBh[:, h, :], True, True, [MQT, CBh], [BK[0]])
                        mm(BK[6][:, 2 * h:2 * h + 2], MQT[:, h, :], NBh[:, h, :], True, True, [MQT, NBh], [BK[6]])
                    yield
                    NUM = TMP[0]
                    yield
                    tt("dve", NUM[:].rearrange("p (h e) -> p h e", h=4), BK[0][:].rearrange("p (h e) -> p h e", h=4),
                       bc3(WP[:, 0:4], 4, 128), ALU.mult, [BK[0], WP], [NUM])
                    yield
                    for h in range(4):
                        sl = slice(h * 128, (h + 1) * 128)
                        mm(BK[0][:, sl], PQKT[:, h, :], MVt[:, h, :], True, True, [PQKT, MVt], [BK[0]])
                    yield
                    tt("dve", NUM[:], NUM[:], BK[0][:], ALU.add, [NUM, BK[0]], [NUM])
                    yield
                    DEN = SG4[5]
                    yield
                    tt("dve", DEN[:, 0:4], BK[6][:, 0:8].rearrange("p (h two) -> p h two", two=2)[:, :, 0], WP[:, 0:4], ALU.mult, [BK[6], WP], [DEN])
                    yield
                    tt("dve", DEN[:, 0:4], DEN[:, 0:4], RS[:, 0:4], ALU.add, [DEN, RS], [DEN])
                    yield
                    act(EMT[:, 0:4], MT[:, 0:4], AF.Exp, [MT], [EMT], scale=-1.0)
                    yield
                    ts("dve", DEN[:, 4:8], DEN[:, 0:4], -1.0, None, ALU.mult, None, [DEN], [DEN])
                    yield
                    tt("dve", DEN[:, 0:4], DEN[:, 0:4], DEN[:, 4:8], ALU.max, [DEN], [DEN])
                    yield
                    tt("dve", DEN[:, 0:4], DEN[:, 0:4], EMT[:, 0:4], ALU.max, [DEN, EMT], [DEN])
                    yield
                    P.op("dve", lambda e, d=DEN: e.reciprocal(out=d[:, 0:4], in_=d[:, 0:4]), [DEN], [DEN])
                    yield
                    tt("dve", NUM[:].rearrange("p (h e) -> p h e", h=4), NUM[:].rearrange("p (h e) -> p h e", h=4),
                       bc3(DEN[:, 0:4], 4, 128), ALU.mult, [NUM, DEN], [NUM])
                    yield
                    cp("pool", T12[:, 0:4], MT[:, 0:4], [MT], [T12])
                    yield
                    tt("dve", T12[:, 4:8], GF[:, 4:8], MT[:, 0:4], ALU.subtract, [GF, MT], [T12])
                    yield
                    cp("pool", T12[:, 8:12], WP[:, 0:4], [WP], [T12])
                    yield
                    mm(BK[6][:, 16:28], SEL127, T12[:], True, True, [CONST, T12], [BK[6]])
                    yield
                    cp("dve", MBC[:], BK[6][:, 16:20], [BK[6]], [MBC])
                    yield
                    PEND = SG4[4]
                    yield
                    tt("dve", PEND[:, 4:8], COLS[:, 12:16], BK[6][:, 20:24], ALU.add, [COLS, BK[6]], [PEND])
                    yield
                    act(PEND[:, 4:8], PEND[:, 4:8], AF.Exp, [PEND], [PEND])
                    yield
                    cp("dve", WLC[:], BK[6][0:64, 24:28], [BK[6]], [WLC])
                    yield
                    tt("dve", PK[:], MKt[:], bc3(PEND[:, 4:8], 4, 64), ALU.mult, [MKt, PEND], [PK])
                    yield
                    for h in range(4):
                        mm(BK[0][0:64, h * 128:(h + 1) * 128], PK[:, h, :], MVt[:, h, :], True, True, [PK, MVt], [BK[0]])
                    yield
                    for h in range(4):
                        mm(BK[6][0:64, 32 + 2 * h:34 + 2 * h], PK[:, h, :], ONEB[:], True, True, [PK, ONEB], [BK[6]])
                    yield
                    for h in range(4):
                        stt("dve", C32[:, h, :], C32[:, h, :], WLC[:, h:h + 1], BK[0][0:64, h * 128:(h + 1) * 128],
                            ALU.mult, ALU.add, [C32, WLC, BK[0]], [C32])
                    yield
                    tt("dve", N32[:], N32[:], WLC[:], ALU.mult, [N32, WLC], [N32])
                    yield
                    tt("dve", N32[:], N32[:], BK[6][0:64, 32:40].rearrange("p (a two) -> p a two", two=2)[:, :, 0],
                       ALU.add, [N32, BK[6]], [N32])
                    yield
                    cp("act", CBh[:], C32[:], [C32], [CBh])
                    yield
                    cp("act", NBh[:, :, 0], N32[:], [N32], [NBh])
                    yield

                if t + 1 < KNT:
                    interleave3([gen_EF(), gen_G(), gen_Bconv()], [2, 1, 1])
                else:
                    interleave3([gen_EF(), gen_G()], [2, 1])
                headnorm_gate(TMP[0], MOSG, MIX[:, 512:1024], TMP[1], SG4[4], SG4[5])

            for _ in gen_H(KNT - 1):
                pass

            dma("sp", pS_d.rearrange("h d e -> d h e"), S32[:], [S32], [])
            dma("sp", pC_d.rearrange("h d e -> d h e"), C32[:], [C32], [])
            dma("sp", pn_d.rearrange("h d -> d h"), N32[:], [N32], [], slow=True)
            dma("sp", pm_d, MBC[0:1, :], [MBC], [])
            for ch in range(12):
                tr(BK[2 + ch // 4][0:3, (ch % 4) * 128:(ch % 4 + 1) * 128], TAIL[:, ch, :], IDF, [TAIL, CONST],
                   [BK[2 + ch // 4]])
            for g in range(3):
                cp("dve", TMP[g % 2][0:3, :], BK[2 + g][0:3, :], [BK[2 + g]], [TMP[g % 2]])
                dma("sp", pconv_d[:, g * 512:(g + 1) * 512], TMP[g % 2][0:3, :], [TMP[g % 2]], [])


            P.barrier(all_res_p1 + RX + [XS1T.r, HNS.r, IDF2.r, IDB.r, GPREMLP.r])
            p1a.close()
            p1b = ExitStack()
            cur[0] = p1b
            R_ = NS
            GRP = 1
            XS = s1("XS", [R_, D])
            XNs = s1("XNs", [R_, D], BF16)
            HTs = s1("HTs", [128, 8, R_], BF16)
            SCV = s1("SCV", [R_, 512])
            XPT = s1("XPT", [128, 12, 4, R_])
            ACs = s1("ACs", [128, 12, R_])
            TM12 = s1("TM12", [128, 12, R_])
            SQs = s1("SQs", [128, 8, R_])
            GQs = s1("GQs", [128, 4, R_])
            GKs = s1("GKs", [128, 4, R_])
            GVs = s1("GVs", [128, 4, R_])
            MQs = s1("MQs", [64, 4, R_])
            MKs = s1("MKs", [64, 4, R_])
            ZSGs = s1("ZSGs", [R_, 512])
            MOSGs = s1("MOSGs", [R_, 512])
            MVs = s1("MVs", [R_, 4, 128])
            GTs = s1("GTs", [R_, 16])
            ARGs = s1("ARGs", [R_, 12])
            GLs = s1("GLs", [R_, 12])
            BETAs = s1("BETAs", [R_, 4])
            IGs = s1("IGs", [R_, 4])
            EGs = s1("EGs", [R_, 4])
            Qt = s1("Qt", [R_, 4, 128])
            Kt = s1("Kt", [R_, 4, 128])
            Vt = s1("Vt", [R_, 4, 128])
            MQt = s1("MQt", [R_, 4, 64])
            MKtt = s1("MKtt", [R_, 4, 64])
            PKt = s1("PKt", [R_, 4, 64])
            DIAGI = s1("DIAGI", [R_, 16, 16])
            M16 = s1("M16", [128, 16, 16])
            KD = [s1("KD%d" % i, [128, 4, 16]) for i in range(2)]
            QD = [s1("QD%d" % i, [128, 4, 16]) for i in range(2)]
            QDm = [s1("QDm%d" % i, [64, 4, 16]) for i in range(2)]
            DG4 = s1("DG4", [R_, 16, 4])
            EGBC = s1("EGBC", [128, 64])
            WPBC = s1("WPBC", [128, 64])
            S0b = [s1("S0b%d" % i, [128, 4, 128]) for i in range(2)]
            C0b = [s1("C0b%d" % i, [64, 4, 128]) for i in range(2)]
            Ug = s1("Ug", [R_, 4, 128])
            KROW = s1("KROW", [R_, 512])
            PKROW = View(DIAGI[:].rearrange("p a b -> p (a b)"), DIAGI.r)
            N0 = s1("N0", [R_, 4, 64])
            SMs = [s1("SMs%d" % i, [R_, 8]) for i in range(10)]
            T1 = s1("T1s", [R_, 512])
            T2 = KROW

            def hn_gate16(src, gate, dst, wres):
                ss, rs = SMs[8], SMs[9]
                tt("pool", T2[:], src[:], src[:], ALU.mult, [src], [T2])
                red("dve", ss[:, 0:4], T2[:].rearrange("p (h e) -> p h e", h=4), ALU.add, [T2], [ss])
                rsqrt_small(rs[:, 0:4], ss[:, 0:4], 1.0 / 128, [ss], [rs], rs[:, 4:8])
                tt("dve", T2[:].rearrange("p (h e) -> p h e", h=4), src[:].rearrange("p (h e) -> p h e", h=4),
                   bc3(rs[:, 0:4], 4, 128), ALU.mult, [src, rs], [T2])
                tt("dve", dst, T2[:], gate[:], ALU.mult, [T2, gate], [wres])

            IDF16 = CONST[0:R_, C_IDF:C_IDF + R_]
            ONES16 = CONST[0:R_, C_ONES:C_ONES + 128]

            dma("sp", XS[:], xs_d, [], [XS])
            dma("sp", N0[:].rearrange("p h d -> p (h d)"), sn_d, [], [N0])
            dma("sp", SMs[0][:, 0:4], sm_d, [], [SMs[0]])
            dma("sp", oconv_d[:, 0:2, :], sconv_d[:, 1:3, :], [], [])
            ss, rs = SMs[8], SMs[9]
            act(XNs[:], XS[:], AF.Square, [XS], [XNs, ss], accum=ss[:, 0:1])
            rsqrt_small(rs[:, 0:1], ss[:, 0:1], 1.0 / D, [ss], [rs], rs[:, 1:2])
            ts("dve", XNs[:], XS[:], rs[:, 0:1], None, ALU.mult, None, [XS, rs], [XNs])
            for k in range(8):
                tr(TPB[:, k * 128:k * 128 + R_], XNs[:, k * 128:(k + 1) * 128], IDB[0:R_, 0:R_], [XNs, IDB], [TPB])
            tt("dve", HTs[:], TPB[:].rearrange("p (k t) -> p k t", k=8)[:, :, 0:R_], bc3(GPRE[:], 8, R_), ALU.mult,
               [TPB, GPRE], [HTs])

            for ch in range(12):
                bk = BK[ch % 2]
                c0 = ch * 128
                for k in range(8):
                    mm(bk[:, 0:R_], WIN[:, k, c0:c0 + 128], HTs[:, k, :], k == 0, k == 7, [HTs] + rwin(c0, c0 + 128), [bk])
                cp("act", XPT[:, ch, 3, :], bk[:, 0:R_], [bk], [XPT])
            for hh in range(8):
                bk = BK[hh % 2]
                c0 = W_MQK + hh * 64
                for k in range(8):
                    mm(bk[0:64, 0:R_], WIN[:, k, c0:c0 + 64], HTs[:, k, :], k == 0, k == 7, [HTs] + rwin(c0, c0 + 64), [bk])
                if hh < 4:
                    cp("act", MQs[:, hh, :], bk[0:64, 0:R_], [bk], [MQs])
                else:
                    act(MKs[:, hh - 4, :], bk[0:64, 0:R_], AF.Copy, [bk], [MKs], scale=0.125)
            for g in range(3):
                for j in range(3):
                    dma("sp", SCV[:], sconv_d[:, j, g * 512:(g + 1) * 512], [], [SCV])
                    for c4 in range(4):
                        o0 = (j * 4 + c4) * R_
                        tr(BK[2][:, o0:o0 + R_], SCV[:, c4 * 128:(c4 + 1) * 128], IDF16, [SCV, CONST], [BK[2]])
                cp("dve", XPT[:, 4 * g:4 * g + 4, 0:3, :],
                   BK[2][:, 0:12 * R_].rearrange("p (j c r) -> p c j r", j=3, c=4), [BK[2]], [XPT])
            for g in range(3):
                bk = BK[3 + g % 2]
                for k in range(8):
                    mm(bk[0:R_, :], HTs[:, k, :], WIN[:, k, g * 512:(g + 1) * 512], k == 0, k == 7,
                       [HTs] + rwin(g * 512, (g + 1) * 512), [bk])
                cp("act", SCV[:], bk[0:R_, :], [bk], [SCV])
                dma("sp", oconv_d[:, 2, g * 512:(g + 1) * 512], SCV[:], [SCV], [])
            tt("dve", ACs[:], XPT[:, :, 0, :], bc3(CW[:, 0, :], 12, R_), ALU.mult, [XPT, CW], [ACs])
            for j in range(1, 4):
                tt("dve", TM12[:], XPT[:, :, j, :], bc3(CW[:, j, :], 12, R_), ALU.mult, [XPT, CW], [TM12])
                tt("dve", ACs[:], ACs[:], TM12[:], ALU.add, [ACs, TM12], [ACs])
            QKFs = TM12
            act(QKFs[:, 0:8, :], ACs[:, 0:8, :], AF.Silu, [ACs], [QKFs])
            act(GVs[:], ACs[:, 8:12, :], AF.Silu, [ACs], [GVs])
            act(SQs[:], QKFs[:, 0:8, :], AF.Square, [QKFs], [SQs])
            mm(BK[2][:, 0:8 * R_], ONES, SQs[:].rearrange("p a b -> p (a b)"), True, True, [CONST, SQs], [BK[2]])
            ts("dve", SQs[:].rearrange("p a b -> p (a b)"), BK[2][:, 0:8 * R_], 1.0, EPS, ALU.mult, ALU.add, [BK[2]], [SQs])
            act(SQs[:], SQs[:], AF.Ln, [SQs], [SQs])
            act(SQs[:], SQs[:], AF.Exp, [SQs], [SQs], scale=-0.5)
            stt("dve", GQs[:], QKFs[:, 0:4, :], 128.0 ** -0.5, SQs[:, 0:4, :], ALU.mult, ALU.mult, [QKFs, SQs], [GQs])
            tt("dve", GKs[:], QKFs[:, 4:8, :], SQs[:, 4:8, :], ALU.mult, [QKFs, SQs], [GKs])

            def tok16(bk, c0, n):
                for k in range(8):
                    mm(bk[0:R_, 0:n], HTs[:, k, :], WIN[:, k, c0:c0 + n], k == 0, k == 7, [HTs] + rwin(c0, c0 + n), [bk])
            tok16(BK[0], W_GZ, 512)
            act(T1[:], BK[0][0:R_, :], AF.Silu, [BK[0]], [T1])
            tt("dve", ZSGs[:].rearrange("p (h e) -> p h e", h=4), T1[:].rearrange("p (h e) -> p h e", h=4),
               bcm(GNG[0:R_, :], 4), ALU.mult, [T1, GNG], [ZSGs])
            tok16(BK[1], W_MV, 512)
            cp("act", MVs[:].rearrange("p h e -> p (h e)"), BK[1][0:R_, :], [BK[1]], [MVs])
            tok16(BK[0], W_MO, 512)
            act(T1[:], BK[0][0:R_, :], AF.Exp, [BK[0]], [T1], scale=-1.0)
            ts("dve", T1[:], T1[:], 1.0, None, ALU.add, None, [T1], [T1])
            P.op("dve", lambda e: e.reciprocal(out=T1[:], in_=T1[:]), [T1], [T1])
            tt("dve", MOSGs[:].rearrange("p (h e) -> p h e", h=4), T1[:].rearrange("p (h e) -> p h e", h=4),
               bcm(MNG[0:R_, :], 4), ALU.mult, [T1, MNG], [MOSGs])
            tok16(BK[1], W_GATE, 16)
            cp("dve", GTs[:], BK[1][0:R_, 0:16], [BK[1]], [GTs])
            tt("dve", ARGs[:], GTs[:, 0:12], SGN[0:R_, :], ALU.mult, [GTs, SGN], [ARGs])
            tt("dve", ARGs[:], ARGs[:], BIA[0:R_, :], ALU.add, [ARGs, BIA], [ARGs])
            act(ARGs[:], ARGs[:], AF.Exp, [ARGs], [ARGs])
            act(ARGs[:], ARGs[:], AF.Ln, [ARGs], [ARGs], bias=1.0)
            tt("dve", GLs[:], ARGs[:], NEGC[0:R_, :], ALU.mult, [ARGs, NEGC], [GLs])
            act(BETAs[:], GLs[:, 0:4], AF.Exp, [GLs], [BETAs])
            tt("dve", IGs[:], GTs[:, 12:16], SMB[0:R_, 8:12], ALU.add, [GTs, SMB], [IGs])
            act(EGs[:], GLs[:, 4:8], AF.Exp, [GLs], [EGs])
            M0s, Bs, MTs, WPs, Ps, QKG, QKMs, QNs = SMs[0], SMs[1], SMs[2], SMs[3], SMs[4], SMs[5], SMs[6], SMs[7]
            tt("dve", Bs[:, 0:4], GLs[:, 8:12], M0s[:, 0:4], ALU.add, [GLs, M0s], [Bs])
            tt("dve", MTs[:, 0:4], Bs[:, 0:4], IGs[:], ALU.max, [Bs, IGs], [MTs])
            tt("dve", WPs[:, 0:4], Bs[:, 0:4], MTs[:, 0:4], ALU.subtract, [Bs, MTs], [WPs])
            act(WPs[:, 0:4], WPs[:, 0:4], AF.Exp, [WPs], [WPs])
            tt("dve", Ps[:, 0:4], IGs[:], MTs[:, 0:4], ALU.subtract, [IGs, MTs], [Ps])
            act(Ps[:, 0:4], Ps[:, 0:4], AF.Exp, [Ps], [Ps])
            dma("sp", om_d, MTs[:, 0:4], [MTs], [])

            for h in range(4):
                tr(BK[2][0:R_, h * 128:(h + 1) * 128], GQs[:, h, :], IDF, [GQs, CONST], [BK[2]])
                tr(BK[3][0:R_, h * 128:(h + 1) * 128], GKs[:, h, :], IDF, [GKs, CONST], [BK[3]])
                tr(BK[4][0:R_, h * 128:(h + 1) * 128], GVs[:, h, :], IDF, [GVs, CONST], [BK[4]])
                tr(BK[5][0:R_, h * 64:(h + 1) * 64], MQs[:, h, :], CONST[0:64, C_IDF:C_IDF + 64], [MQs, CONST], [BK[5]])
                tr(BK[5][0:R_, 256 + h * 64:256 + (h + 1) * 64], MKs[:, h, :], CONST[0:64, C_IDF:C_IDF + 64],
                   [MKs, CONST], [BK[5]])
            cp("act", Qt[:].rearrange("p h d -> p (h d)"), BK[2][0:R_, :], [BK[2]], [Qt])
            cp("dve", Kt[:].rearrange("p h d -> p (h d)"), BK[3][0:R_, :], [BK[3]], [Kt])
            cp("act", Vt[:].rearrange("p h d -> p (h d)"), BK[4][0:R_, :], [BK[4]], [Vt])
            cp("dve", MQt[:].rearrange("p h d -> p (h d)"), BK[5][0:R_, 0:256], [BK[5]], [MQt])
            cp("act", MKtt[:].rearrange("p h d -> p (h d)"), BK[5][0:R_, 256:512], [BK[5]], [MKtt])
            tt("dve", T1[:], Qt[:].rearrange("p h d -> p (h d)"), Kt[:].rearrange("p h d -> p (h d)"), ALU.mult,
               [Qt, Kt], [T1])
            red("dve", QKG[:, 0:4], T1[:].rearrange("p (h d) -> p h d", h=4), ALU.add, [T1], [QKG])
            tt("dve", T1[:, 0:256], MQt[:].rearrange("p h d -> p (h d)"), MKtt[:].rearrange("p h d -> p (h d)"),
               ALU.mult, [MQt, MKtt], [T1])
            red("dve", QKMs[:, 0:4], T1[:, 0:256].rearrange("p (h d) -> p h d", h=4), ALU.add, [T1], [QKMs])
            tt("dve", T1[:, 0:256], MQt[:].rearrange("p h d -> p (h d)"), N0[:].rearrange("p h d -> p (h d)"),
               ALU.mult, [MQt, N0], [T1])
            red("dve", QNs[:, 0:4], T1[:, 0:256].rearrange("p (h d) -> p h d", h=4), ALU.add, [T1], [QNs])
            tt("dve", PKt[:], MKtt[:], bc3(Ps[:, 0:4], 4, 64), ALU.mult, [MKtt, Ps], [PKt])
            tt("dve", N0[:], N0[:], bc3(WPs[:, 0:4], 4, 64), ALU.mult, [N0, WPs], [N0])
            tt("dve", N0[:], N0[:], PKt[:], ALU.add, [N0, PKt], [N0])
            dma("sp", on_d, N0[:].rearrange("p h d -> p (h d)"), [N0], [])

            tt("dve", DIAGI[:], bc3(IDF16, R_, R_), bcm(IDF16, R_), ALU.mult, [CONST], [DIAGI])
            mm(BK[2][:, 0:R_ * R_], ONES16, DIAGI[:].rearrange("p a b -> p (a b)"), True, True, [CONST, DIAGI], [BK[2]])
            cp("dve", M16[:].rearrange("p a b -> p (a b)"), BK[2][:, 0:R_ * R_], [BK[2]], [M16])
            tt("dve", DG4[:], bcm(EGs[:], R_), bc3(IDF16, R_, 4), ALU.mult, [EGs, CONST], [DG4])
            mm(BK[3][:, 0:64], ONES16, DG4[:].rearrange("p a b -> p (a b)"), True, True, [CONST, DG4], [BK[3]])
            cp("dve", EGBC[:], BK[3][:, 0:64], [BK[3]], [EGBC])
            tt("dve", DG4[:], bcm(WPs[:, 0:4], R_), bc3(IDF16, R_, 4), ALU.mult, [WPs, CONST], [DG4])
            mm(BK[3][:, 0:64], ONES16, DG4[:].rearrange("p a b -> p (a b)"), True, True, [CONST, DG4], [BK[3]])
            cp("dve", WPBC[:], BK[3][:, 0:64], [BK[3]], [WPBC])

            def ld(r, i):
                dma("sp", S0b[i][:], sS_d[r].rearrange("h d e -> d h e"), [], [S0b[i]])
                dma("sp", C0b[i][:], sC_d[r].rearrange("h d e -> d h e"), [], [C0b[i]])
            ld(0, 0)
            for r in range(R_):
                i = r % 2
                if r + 1 < R_:
                    ld(r + 1, 1 - i)
                tt("dve", KD[i][:], GKs[:], bcm(M16[:, r, :], 4), ALU.mult, [GKs, M16], [KD[i]])
                tt("pool", QD[i][:], GQs[:], bcm(M16[:, r, :], 4), ALU.mult, [GQs, M16], [QD[i]])
                tt("dve", QDm[i][:], MQs[:], bcm(M16[0:64, r, :], 4), ALU.mult, [MQs, M16], [QDm[i]])
                for h in range(4):
                    sl = slice(h * 128, (h + 1) * 128)
                    st_, sp_ = (r == 0 and h == 0), (r == R_ - 1 and h == 3)
                    mm(BK[2][0:R_, sl], KD[i][:, h, :], S0b[i][:, h, :], st_, sp_, [KD[i], S0b[i]], [BK[2]], skip=True)
                    mm(BK[3][0:R_, sl], QD[i][:, h, :], S0b[i][:, h, :], st_, sp_, [QD[i], S0b[i]], [BK[3]], skip=True)
                    mm(BK[5][0:R_, sl], QDm[i][:, h, :], C0b[i][:, h, :], st_, sp_, [QDm[i], C0b[i]], [BK[5]], skip=True)
            KSp, QSp, QCp = BK[2], BK[3], BK[5]
            tt("dve", Ug[:], KSp[0:R_, :].rearrange("p (h e) -> p h e", h=4), bc3(EGs[:], 4, 128), ALU.mult,
               [KSp, EGs], [Ug])
            tt("dve", Ug[:], Vt[:], Ug[:], ALU.subtract, [Vt, Ug], [Ug])
            tt("dve", Ug[:], Ug[:], bc3(BETAs[:], 4, 128), ALU.mult, [Ug, BETAs], [Ug])
            ld(0, 0)
            for r in range(R_):
                i = r % 2
                if r + 1 < R_:
                    ld(r + 1, 1 - i)
                ts("dve", KROW[:], Kt[:].rearrange("p h d -> p (h d)"), IDF16[:, r:r + 1], None, ALU.mult, None,
                   [Kt, CONST], [KROW])
                ts("dve", PKROW[:], PKt[:].rearrange("p h d -> p (h d)"), IDF16[:, r:r + 1], None, ALU.mult, None,
                   [PKt, CONST], [PKROW])
                for h in range(4):
                    mm(BK[4][:, h * 128:(h + 1) * 128], KROW[:, h * 128:(h + 1) * 128], Ug[:, h, :], True, True,
                       [KROW, Ug], [BK[4]])
                for h in range(4):
                    mm(BK[6][0:64, h * 128:(h + 1) * 128], PKROW[:, h * 64:(h + 1) * 64], MVs[:, h, :], True, True,
                       [PKROW, MVs], [BK[6]])
                for h in range(4):
                    stt("dve", S0b[i][:, h, :], S0b[i][:, h, :], EGBC[:, r * 4 + h:r * 4 + h + 1],
                        BK[4][:, h * 128:(h + 1) * 128], ALU.mult, ALU.add, [S0b[i], EGBC, BK[4]], [S0b[i]])
                    stt("dve", C0b[i][:, h, :], C0b[i][:, h, :], WPBC[0:64, r * 4 + h:r * 4 + h + 1],
                        BK[6][0:64, h * 128:(h + 1) * 128], ALU.mult, ALU.add, [C0b[i], WPBC, BK[6]], [C0b[i]])
                dma("sp", oS_d[r].rearrange("h d e -> d h e"), S0b[i][:], [S0b[i]], [])
                dma("sp", oC_d[r].rearrange("h d e -> d h e"), C0b[i][:], [C0b[i]], [])

            MIXs = XNs
            tt("dve", T1[:].rearrange("p (h e) -> p h e", h=4), QSp[0:R_, :].rearrange("p (h e) -> p h e", h=4),
               bc3(EGs[:], 4, 128), ALU.mult, [QSp, EGs], [T1])
            tt("dve", Ug[:], Ug[:], bc3(QKG[:, 0:4], 4, 128), ALU.mult, [Ug, QKG], [Ug])
            tt("dve", T1[:], T1[:], Ug[:].rearrange("p h e -> p (h e)"), ALU.add, [T1, Ug], [T1])
            hn_gate16(T1, ZSGs, MIXs[:, 0:512], MIXs)
            PQ = SMs[5]
            tt("dve", PQ[:, 4:8], Ps[:, 0:4], QKMs[:, 0:4], ALU.mult, [Ps, QKMs], [PQ])
            tt("dve", T1[:].rearrange("p (h e) -> p h e", h=4), QCp[0:R_, :].rearrange("p (h e) -> p h e", h=4),
               bc3(WPs[:, 0:4], 4, 128), ALU.mult, [QCp, WPs], [T1])
            tt("dve", Ug[:], MVs[:], bc3(PQ[:, 4:8], 4, 128), ALU.mult, [MVs, PQ], [Ug])
            tt("dve", T1[:], T1[:], Ug[:].rearrange("p h e -> p (h e)"), ALU.add, [T1, Ug], [T1])
            DENs, EMTs = SMs[6], SMs[7]
            tt("dve", DENs[:, 4:8], WPs[:, 0:4], QNs[:, 0:4], ALU.mult, [WPs, QNs], [DENs])
            tt("dve", DENs[:, 4:8], DENs[:, 4:8], PQ[:, 4:8], ALU.add, [DENs, PQ], [DENs])
            ts("dve", DENs[:, 0:4], DENs[:, 4:8], -1.0, None, ALU.mult, None, [DENs], [DENs])
            tt("dve", DENs[:, 4:8], DENs[:, 4:8], DENs[:, 0:4], ALU.max, [DENs], [DENs])
            act(EMTs[:, 4:8], MTs[:, 0:4], AF.Exp, [MTs], [EMTs], scale=-1.0)
            tt("dve", DENs[:, 4:8], DENs[:, 4:8], EMTs[:, 4:8], ALU.max, [DENs, EMTs], [DENs])
            P.op("dve", lambda e, d=DENs: e.reciprocal(out=d[:, 4:8], in_=d[:, 4:8]), [DENs], [DENs])
            tt("dve", T1[:].rearrange("p (h e) -> p h e", h=4), T1[:].rearrange("p (h e) -> p h e", h=4),
               bc3(DENs[:, 4:8], 4, 128), ALU.mult, [T1, DENs], [T1])
            hn_gate16(T1, MOSGs, MIXs[:, 512:1024], MIXs)
            for k in range(8):
                tr(TPB[:, k * 128:k * 128 + R_], MIXs[:, k * 128:(k + 1) * 128], IDB[0:R_, 0:R_], [MIXs, IDB], [TPB])
            cp("act", HTs[:], TPB[:].rearrange("p (k t) -> p k t", k=8)[:, :, 0:R_], [TPB], [HTs])
            for eh in range(2):
                for k in range(8):
                    mm(BK[eh][0:R_, :], HTs[:, k, :], WOUT[:, k, eh * 512:(eh + 1) * 512], k == 0, k == 7,
                       [HTs, WOUT], [BK[eh]])
            ss, rs = SMs[8], SMs[9]
            for eh in range(2):
                act(XNs[:, eh * 512:(eh + 1) * 512], BK[eh][0:R_, :], AF.Square, [BK[eh]], [XNs, ss],
                    accum=ss[:, eh:eh + 1])
            tt("dve", ss[:, 2:3], ss[:, 0:1], ss[:, 1:2], ALU.add, [ss], [ss])
            rsqrt_small(rs[:, 0:1], ss[:, 2:3], 1.0 / D, [ss], [rs], rs[:, 1:2])
            for eh in range(2):
                sl = slice(eh * 512, (eh + 1) * 512)
                stt("dve", T1[:], BK[eh][0:R_, :], rs[:, 0:1], GPM[0:R_, sl], ALU.mult, ALU.mult, [BK[eh], rs, GPM], [T1])
                tt("dve", XS[:, sl], XS[:, sl], T1[:], ALU.add, [XS, T1], [XS])
            for k in range(8):
                tr(BK[2][:, k * R_:(k + 1) * R_], XS[:, k * 128:(k + 1) * 128], IDF16, [XS, CONST], [BK[2]])
            cp("dve", XS1T[:].rearrange("p k r -> p (k r)"), BK[2][:, 0:8 * R_], [BK[2]], [XS1T])
            act(XNs[:], XS[:], AF.Square, [XS], [XNs, ss], accum=ss[:, 4:5])
            rsqrt_small(rs[:, 4:5], ss[:, 4:5], 1.0 / D, [ss], [rs], rs[:, 5:6])
            ts("dve", XNs[:], XS[:], rs[:, 4:5], None, ALU.mult, None, [XS, rs], [XNs])
            for k in range(8):
                tr(TPB[:, k * 128:k * 128 + R_], XNs[:, k * 128:(k + 1) * 128], IDB[0:R_, 0:R_], [XNs, IDB], [TPB])
            tt("dve", HNS[:], TPB[:].rearrange("p (k t) -> p k t", k=8)[:, :, 0:R_], bc3(GPREMLP[:], 8, R_), ALU.mult,
               [TPB, GPREMLP], [HNS])

            P.barrier(all_res_p1 + RX + [XS1T.r, HNS.r, IDF2.r, IDB.r, GPREMLP.r])
            p1b.close()

        with ExitStack() as p2:
            WUP = p2.enter_context(nc.sbuf_tensor("sb_WUP", [128, 8, DFF], BF16))
            WDN = p2.enter_context(nc.sbuf_tensor("sb_WDN", [128, 32, D], BF16))
            NWC = 8
            RWU = [Res("WUP%d" % i) for i in range(NWC)]
            RWD = [Res("WDN%d" % i) for i in range(NWC)]
            wup_v = wup_d.rearrange("(k p) c -> p k c", p=128)
            wdn_v = wdn_d.rearrange("(k p) c -> p k c", p=128)
            for i in range(NWC if KPH2 else 0):
                dma("pool", WUP[:, :, i * 512:(i + 1) * 512], wup_v[:, :, i * 512:(i + 1) * 512], [], [RWU[i]])
                dma("pool", WDN[:, i * 4:(i + 1) * 4, :], wdn_v[:, i * 4:(i + 1) * 4, :], [], [RWD[i]])
            XN2 = sb(p2, "XN2", [128, D], BF16)
            GPL = sb(p2, "GPL", [128, D])
            dma("sp", GPL[:], gpl_d.partition_broadcast(128), [], [GPL])
            HN = sb(p2, "HN", [128, 8, 256], BF16)
            UT = [sb(p2, "UT%d" % i, [128, 256], BF16) for i in range(2)]
            RL = [sb(p2, "RL%d" % i, [128, 256]) for i in range(2)]
            SS2 = sb(p2, "SS2", [128, 8])

            NB2 = NT // 2 if KPH2 else 0

            def xv_of(blk, j):
                return Xt[:, 2 * blk + j, :], RX[2 * blk + j]

            def prep(blk):
                for j in range(2):
                    xv, rx = xv_of(blk, j)
                    act(XN2[:], xv, AF.Square, [rx], [XN2, SS2], accum=SS2[:, 0:1])
                    rsqrt_small(SS2[:, 1:2], SS2[:, 0:1], 1.0 / D, [SS2], [SS2], SS2[:, 2:3])
                    ts("dve", XN2[:], xv, SS2[:, 1:2], None, ALU.mult, None, [rx, SS2], [XN2])
                    for k in range(8):
                        tr(TPB[:, k * 128:(k + 1) * 128], XN2[:, k * 128:(k + 1) * 128], IDB[:], [XN2, IDB], [TPB])
                    tt("dve", HN[:, :, j * 128:(j + 1) * 128], TPB[:].rearrange("p (k t) -> p k t", k=8),
                       bc3(GPREMLP[:], 8, 128), ALU.mult, [TPB, GPREMLP], [HN])

            def up(f, hn, n):
                bk = BK[f % 2]
                for k in range(8):
                    mm(bk[:, 0:n], WUP[:, k, f * 128:(f + 1) * 128], hn[:, k, 0:n], k == 0, k == 7,
                       [hn, RWU[f // 4]], [bk])
                rl, ut = RL[f % 2], UT[f % 2]
                act(rl[:, 0:n], bk[:, 0:n], AF.Relu, [bk], [rl])
                tt("dve", ut[:, 0:n], rl[:, 0:n], rl[:, 0:n], ALU.mult, [rl], [ut])

            def down(f, rows, ntl):
                ut = UT[f % 2]
                for j in range(ntl):
                    for eh in range(2):
                        ab = BK[2 + 2 * j + eh]
                        mm(ab[0:rows, :], ut[:, j * rows:(j + 1) * rows], WDN[:, f, eh * 512:(eh + 1) * 512],
                           f == 0, f == 31, [ut, RWD[f // 4]], [ab])

            def fin(blk):
                for j in range(2):
                    xv, rx = xv_of(blk, j)
                    for eh in range(2):
                        ab = BK[2 + 2 * j + eh]
                        act(XN2[:, eh * 512:(eh + 1) * 512], ab[:], AF.Square, [ab], [XN2, SS2],
                            accum=SS2[:, 3 + eh:4 + eh])
                    tt("dve", SS2[:, 5:6], SS2[:, 3:4], SS2[:, 4:5], ALU.add, [SS2], [SS2])
                    rsqrt_small(SS2[:, 6:7], SS2[:, 5:6], 1.0 / D, [SS2], [SS2], SS2[:, 7:8])
                    for eh in range(2):
                        ab = BK[2 + 2 * j + eh]
                        sl = slice(eh * 512, (eh + 1) * 512)
                        stt("dve", ab[:], ab[:], SS2[:, 6:7], GPL[:, sl], ALU.mult, ALU.mult, [ab, SS2, GPL], [ab])
                        tt("dve", xv[:, sl], xv[:, sl], ab[:], ALU.add, [rx, ab], [rx])
                    r0 = (2 * blk + j) * 128
                    dma("sp", y_d[r0:r0 + 128, :], xv, [rx], [])

            if NB2:
                prep(0)
            for blk in range(NB2):
                up(0, HN, 256)
                for f in range(32):
                    if f + 1 < 32:
                        up(f + 1, HN, 256)
                    elif blk + 1 < NB2:
                        prep(blk + 1)
                    down(f, 128, 2)
                fin(blk)

            R_ = NS
            if KPH2:
                up(0, HNS, R_)
                for f in range(32):
                    if f + 1 < 32:
                        up(f + 1, HNS, R_)
                    down(f, R_, 1)
            if KPH2:
                for eh in range(2):
                    act(XN2[0:R_, eh * 512:(eh + 1) * 512], BK[2 + eh][0:R_, :], AF.Square, [BK[2 + eh]], [XN2, SS2],
                        accum=SS2[0:R_, 3 + eh:4 + eh])
                tt("dve", SS2[0:R_, 5:6], SS2[0:R_, 3:4], SS2[0:R_, 4:5], ALU.add, [SS2], [SS2])
                rsqrt_small(SS2[0:R_, 6:7], SS2[0:R_, 5:6], 1.0 / D, [SS2], [SS2], SS2[0:R_, 7:8])
                YSB = XN2.t.bitcast(F32)
                for eh in range(2):
                    sl = slice(eh * 512, (eh + 1) * 512)
                    stt("dve", BK[2 + eh][0:R_, :], BK[2 + eh][0:R_, :], SS2[0:R_, 6:7], GPL[0:R_, sl], ALU.mult, ALU.mult,
                        [BK[2 + eh], SS2, GPL], [BK[2 + eh]])
                    for j in range(4):
                        tr(BK[4 + eh][0:R_, j * 128:(j + 1) * 128], XS1T[:, 4 * eh + j, :], IDF2[:], [XS1T, IDF2],
                           [BK[4 + eh]])
                    cp("act", YSB[0:R_, :], BK[4 + eh][0:R_, :], [BK[4 + eh]], [XN2])
                    tt("dve", YSB[0:R_, :], YSB[0:R_, :], BK[2 + eh][0:R_, :], ALU.add, [XN2, BK[2 + eh]], [XN2])
                    dma("sp", ys_d[:, sl], YSB[0:R_, :], [XN2], [])

        n_ins = P.finalize(top)
    return nc, n_ins


_CACHE = {}


def kernel(x_prompt, x_sample, state_gdn_conv, state_gdn_S, state_mlstm_C, state_mlstm_n, state_mlstm_m,
           norm_pre_mix, w_in, conv_w, a_log, dt_bias, gdn_norm_g, b_igate, b_fgate, mlstm_norm_g, w_out,
           norm_post_mix, norm_pre_mlp, w_up, w_down, norm_post_mlp):
    f = lambda a: np.ascontiguousarray(np.asarray(a, dtype=np.float32))
    if "nc" not in _CACHE:
        _CACHE["nc"] = build_program()
    nc, _ = _CACHE["nc"]
    consts = make_consts()
    small = np.concatenate([f(a_log)[0], f(dt_bias)[0], f(b_igate)[0], f(b_fgate)[0]])[None, :]
    shared = {
        "w_in": f(w_in)[0], "w_out": f(w_out)[0], "w_up": f(w_up)[0], "w_down": f(w_down)[0],
        "consts": consts,
        "gpre_fm": f(f(norm_pre_mix)[0].reshape(8, 128).T),
        "gpremlp_fm": f(f(norm_pre_mlp)[0].reshape(8, 128).T),
        "cw_fm": f(f(conv_w)[0].reshape(4, 12, 128).transpose(2, 0, 1).reshape(128, 48)),
        "gpostmix": f(norm_post_mix)[0][None, :], "gpostmlp": f(norm_post_mlp)[0][None, :],
        "small": f(small), "gdn_norm_g": f(gdn_norm_g)[0][None, :], "mlstm_norm_g": f(mlstm_norm_g)[0][None, :],
    }
    xp, xs = f(x_prompt), f(x_sample)
    in_maps = []
    for c in range(NCORES):
        r = slice(c * NS, (c + 1) * NS)
        m = dict(shared)
        m.update({
            "x": xp[c], "xs": xs[r, 0, :],
            "sconv": f(state_gdn_conv)[0, r], "sS": f(state_gdn_S)[0, r], "sC": f(state_mlstm_C)[0, r],
            "sn": f(state_mlstm_n)[0, r].reshape(NS, 256), "sm": f(state_mlstm_m)[0, r],
        })
        in_maps.append(m)
    res = run_bass_kernel_spmd(nc, in_maps, core_ids=list(range(NCORES)))
    R = res.results
    g = lambda k: np.stack([np.asarray(R[c][k], dtype=np.float32) for c in range(NCORES)])
    gc = lambda k: np.concatenate([np.asarray(R[c][k], dtype=np.float32) for c in range(NCORES)], axis=0)
    y_prompt = g("y")
    y_sample = gc("ys")[:, None, :]
    p_conv = g("pconv")[None]
    p_S = g("pS")[None]
    p_C = g("pC")[None]
    p_n = g("pn")[None]
    p_m = g("pm").reshape(NCORES, 4)[None]
    s_conv = gc("oconv")[None]
    s_S = gc("oS")[None]
    s_C = gc("oC")[None]
    s_n = gc("on").reshape(NCORES * NS, 4, 64)[None]
    s_m = gc("om")[None]
    return (y_prompt, y_sample, p_conv, p_S, p_C, p_n, p_m, s_conv, s_S, s_C, s_n, s_m)
```

```python
from contextlib import ExitStack
import numpy as np
import concourse.bass as bass
import concourse.mybir as mybir
from concourse.bass_utils import run_bass_kernel_spmd

F32 = mybir.dt.float32
BF16 = mybir.dt.bfloat16
ALU = mybir.AluOpType
AF = mybir.ActivationFunctionType
AX = mybir.AxisListType

NCORES = 8
T = 2048
NT = T // 128
D = 1024
DFF = 4096
NS = 16
EPS = 1e-6
NEG = -30000.0
import os
KNT = int(os.environ.get('KNT', NT))
KPH2 = int(os.environ.get('KPH2', 1))
KSTAGE = int(os.environ.get('KSTAGE', 99))
KSUB = int(os.environ.get('KSUB', 99))
KRATIO = [int(v) for v in os.environ.get('KRATIO', '3,2,1').split(',')]
KSCHED = int(os.environ.get('KSCHED', 0))
KSEG = int(os.environ.get('KSEG', 7))
KLO = int(os.environ.get('KLO', 0))
KHI = int(os.environ.get('KHI', 99))


class Res:
    __slots__ = ("name", "w", "rd")

    def __init__(self, name):
        self.name = name
        self.w = None
        self.rd = []


class Op:
    __slots__ = ("eng", "fn", "reads", "writes", "dma", "deps", "signal", "cnt", "sem", "waits", "cost", "tab")

    def __init__(self, eng, fn, reads, writes, dma, cost=0.4, tab=None):
        self.cost = cost
        self.tab = tab
        self.eng = eng
        self.fn = fn
        self.reads = reads
        self.writes = writes
        self.dma = dma
        self.deps = []
        self.signal = False
        self.cnt = 0
        self.sem = None
        self.waits = []


def _res(lst):
    out = []
    for x in lst:
        if x is None:
            continue
        if isinstance(x, Res):
            out.append(x)
        elif isinstance(x, (list, tuple)):
            out.extend(_res(x))
        else:
            out.append(x.r)
    return out


class Prog:
    ENGS = ("pe", "act", "dve", "pool", "sp")

    def __init__(self, nc, n_dma_sems=56):
        self.nc = nc
        self.ops = []
        self.n_dma_sems = n_dma_sems
        self.n_sw_sems = 8
        self.fence = Res("fence")
        self.marks = []
        self.engobj = {"pe": nc.tensor, "act": nc.scalar, "dve": nc.vector,
                       "pool": nc.gpsimd, "sp": nc.sync}

    def op(self, eng, fn, r=(), w=(), cost=0.4, tab=None):
        self.ops.append(Op(eng, fn, _res(r) + ([self.fence] if KSCHED else []), _res(w), False, cost, tab))

    def dma(self, eng, fn, r=(), w=(), cost=2.5):
        self.ops.append(Op(eng, fn, _res(r) + ([self.fence] if KSCHED else []), _res(w), True, cost))

    def barrier(self, allres):
        for e in ("pe", "act", "dve", "pool", "sp"):
            self.ops.append(Op(e, (lambda en: en.nop(nofuse=True)), [], _res(allres) + [self.fence], False, 0.1))

    def _schedule(self):
        ops = self.ops
        n = len(ops)
        LAT = 0.2
        succ = [[] for _ in range(n)]
        for i, o in enumerate(ops):
            for j in o.deps:
                succ[j].append(i)
        prio = [0.0] * n
        for i in range(n - 1, -1, -1):
            m = 0.0
            for k in succ[i]:
                if prio[k] + LAT > m:
                    m = prio[k] + LAT
            prio[i] = ops[i].cost + m
        if KSCHED == 2:
            prio = [float(n - i) for i in range(n)]
        ndep = [len(o.deps) for o in ops]
        ready = [0.0] * n
        finish = [0.0] * n
        avail = {e: [] for e in self.ENGS}
        for i, o in enumerate(ops):
            if ndep[i] == 0:
                avail[o.eng].append(i)
        free = {e: 0.0 for e in self.ENGS}
        lasttab = None
        order = []
        start = [0.0] * n
        done = 0
        while done < n:
            best_e, best_i, best_t = None, None, None
            for e in self.ENGS:
                av = avail[e]
                if not av:
                    continue
                fe = free[e]
                cand = None
                cs = None
                tmin_i, tmin = None, None
                for i in av:
                    r = ready[i]
                    if tmin is None or r < tmin or (r == tmin and i < tmin_i):
                        tmin, tmin_i = r, i
                    if r <= fe + 1e-9:
                        sc = prio[i]
                        if e == "act" and ops[i].tab is not None and lasttab is not None and ops[i].tab != lasttab:
                            sc -= 4.0
                        if cs is None or sc > cs or (sc == cs and i < cand):
                            cand, cs = i, sc
                if cand is None:
                    cand = tmin_i
                t0 = max(fe, ready[cand])
                if best_t is None or t0 < best_t:
                    best_e, best_i, best_t = e, cand, t0
            i = best_i
            o = ops[i]
            avail[best_e].remove(i)
            c = o.cost
            if best_e == "act" and o.tab is not None:
                if lasttab is not None and o.tab != lasttab:
                    c += 1.3
                lasttab = o.tab
            start[i] = best_t
            if o.dma:
                free[best_e] = best_t + 0.1
                finish[i] = best_t + c
            else:
                free[best_e] = best_t + c
                finish[i] = best_t + c
            order.append(i)
            done += 1
            for k in succ[i]:
                ndep[k] -= 1
                if finish[i] + LAT > ready[k]:
                    ready[k] = finish[i] + LAT
                if ndep[k] == 0:
                    avail[ops[k].eng].append(k)
        newpos = {old: new for new, old in enumerate(order)}
        newops = [ops[i] for i in order]
        for o in newops:
            o.deps = sorted(newpos[j] for j in o.deps)
        self.ops = newops
        self.est_us = max(finish) if n else 0.0

    def finalize(self, stack):
        nc = self.nc
        ops = self.ops
        for i, o in enumerate(ops):
            deps = set()
            for r in o.reads:
                if r.w is not None:
                    deps.add(r.w)
            for r in o.writes:
                if r.w is not None:
                    deps.add(r.w)
                for j in r.rd:
                    deps.add(j)
            deps.discard(i)
            o.deps = sorted(deps)
            for r in o.reads:
                r.rd.append(i)
            for r in o.writes:
                r.w = i
                r.rd = []
        if KSCHED:
            seg = 0
            last = {}
            nbar = 0
            for i, o in enumerate(ops):
                if o.cost == 0.1 and not o.dma and o.writes and o.writes[-1] is self.fence:
                    nbar += 1
                    if nbar % 5 == 1:
                        seg += 1
                        last = {}
                free_ = (KSEG >> min(seg, 2)) & 1
                if seg == 0 and self.marks:
                    lo = self.marks[min(KLO, len(self.marks) - 1)]
                    hi = self.marks[min(KHI, len(self.marks) - 1)] if KHI < len(self.marks) else 10 ** 9
                    free_ = free_ and (lo <= i < hi)
                if not free_:
                    if o.eng in last and last[o.eng] not in o.deps:
                        o.deps = sorted(o.deps + [last[o.eng]])
                    last[o.eng] = i
            self._schedule()
            ops = self.ops
        dma_slot_last = [None] * self.n_dma_sems
        dma_i = 0
        sw_i = 0
        for i, o in enumerate(ops):
            if o.dma:
                if o.eng == "pool":
                    slot = sw_i % self.n_sw_sems
                    sw_i += 1
                else:
                    slot = self.n_sw_sems + dma_i % (self.n_dma_sems - self.n_sw_sems)
                    dma_i += 1
                o.sem = slot
                if dma_slot_last[slot] is not None and dma_slot_last[slot] not in o.deps:
                    o.deps = sorted(o.deps + [dma_slot_last[slot]])
                dma_slot_last[slot] = i
            for j in o.deps:
                pj = ops[j]
                if pj.dma:
                    continue
                if pj.eng == "pe" and o.eng == "pe" and not o.dma:
                    continue
                pj.signal = True
        cnt = {e: 0 for e in self.ENGS}
        dcnt = [0] * self.n_dma_sems
        for o in ops:
            if o.dma:
                dcnt[o.sem] += 16
                o.cnt = dcnt[o.sem]
            elif o.signal:
                cnt[o.eng] += 1
                o.cnt = cnt[o.eng]
        seen = {e: {} for e in self.ENGS}
        for o in ops:
            need = {}
            for j in o.deps:
                pj = ops[j]
                if pj.dma:
                    key = ("d", pj.sem)
                else:
                    if pj.eng == "pe" and o.eng == "pe" and not o.dma:
                        continue
                    key = ("e", pj.eng)
                if pj.cnt > need.get(key, 0):
                    need[key] = pj.cnt
            s = seen[o.eng]
            for key, v in need.items():
                if s.get(key, 0) >= v:
                    continue
                s[key] = v
                o.waits.append((key, v))
        final_waits = [(("d", k), dcnt[k]) for k in range(self.n_dma_sems) if dcnt[k] > 0]
        final_waits += [(("e", e), cnt[e]) for e in self.ENGS if cnt[e] > 0 and e != "sp"]
        esem = {e: stack.enter_context(nc.semaphore("s_" + e)) for e in self.ENGS}
        dsem = [stack.enter_context(nc.semaphore("d_%d" % k)) for k in range(self.n_dma_sems)]

        def semof(key):
            return dsem[key[1]] if key[0] == "d" else esem[key[1]]

        n_ins = 0
        for o in ops:
            e = self.engobj[o.eng]
            for key, v in o.waits:
                e.wait_ge(semof(key), v)
                n_ins += 1
            ins = o.fn(e)
            n_ins += 1
            if o.dma:
                ins.then_inc(dsem[o.sem], 16)
            elif o.signal:
                ins.then_inc(esem[o.eng], 1)
        sp = self.engobj["sp"]
        for key, v in final_waits:
            if seen["sp"].get(key, 0) >= v:
                continue
            sp.wait_ge(semof(key), v)
        return n_ins


class View:
    __slots__ = ("t", "r")

    def __init__(self, ap, r):
        self.t = ap
        self.r = r

    def __getitem__(self, k):
        return self.t[k]


class Tl:
    __slots__ = ("t", "r")

    def __init__(self, t, name):
        self.t = t
        self.r = Res(name)

    def __getitem__(self, k):
        return self.t[k]


C_IDF, C_TRI, C_ONES, C_SEL127, C_MST, C_MIT, C_MI, C_SELR = (
    0, 128, 256, 384, 512, 640, 768, 896)
NCONST = 1408


def make_consts():
    c = np.zeros((128, NCONST), np.float32)
    s = np.arange(128)[:, None]
    f = np.arange(128)[None, :]
    c[:, C_IDF:C_IDF + 128] = (s == f)
    c[:, C_TRI:C_TRI + 128] = (s <= f)
    c[:, C_ONES:C_ONES + 128] = 1.0
    c[:, C_SEL127:C_SEL127 + 128] = (s == 127)
    mst = np.where(s < f, 0.0, NEG)
    mit = np.where(s <= f, 0.0, NEG)
    mi = np.where(f <= s, 0.0, NEG)
    c[:, C_MST:C_MST + 128] = mst
    c[:, C_MIT:C_MIT + 128] = mit
    c[:, C_MI:C_MI + 128] = mi
    for h in range(4):
        c[h, C_SELR + h * 128:C_SELR + (h + 1) * 128] = 1.0
    return c


W_QKV, W_MQK, W_GZ, W_MV, W_MO, W_GATE = 0, 1536, 2048, 2560, 3072, 3584
WIN_MOVES = [(0, 0, 1536), (1536, 2056, 512), (2048, 1536, 512), (2560, 2568, 512),
             (3072, 3080, 512), (3584, 2048, 8), (3592, 3596, 4), (3596, 3592, 4)]


def build_program():
    nc = bass.Bass("TRN2", target_bir_lowering=False)
    P = Prog(nc)

    def din(name, shape):
        return nc.dram_tensor(name, list(shape), F32, kind="ExternalInput").ap()

    def dout(name, shape):
        return nc.dram_tensor(name, list(shape), F32, kind="ExternalOutput").ap()

    x_d = din("x", [T, D])
    xs_d = din("xs", [NS, D])
    sconv_d = din("sconv", [NS, 3, 1536])
    sS_d = din("sS", [NS, 4, 128, 128])
    sC_d = din("sC", [NS, 4, 64, 128])
    sn_d = din("sn", [NS, 256])
    sm_d = din("sm", [NS, 4])
    win_d = din("w_in", [D, 3600])
    wout_d = din("w_out", [D, D])
    wup_d = din("w_up", [D, DFF])
    wdn_d = din("w_down", [DFF, D])
    consts_d = din("consts", [128, NCONST])
    gpre_d = din("gpre_fm", [128, 8])
    gpremlp_d = din("gpremlp_fm", [128, 8])
    cw_d = din("cw_fm", [128, 48])
    gpm_d = din("gpostmix", [1, D])
    gpl_d = din("gpostmlp", [1, D])
    small_d = din("small", [1, 16])
    gng_d = din("gdn_norm_g", [1, 128])
    mng_d = din("mlstm_norm_g", [1, 128])

    y_d = dout("y", [T, D])
    ys_d = dout("ys", [NS, D])
    pconv_d = dout("pconv", [3, 1536])
    pS_d = dout("pS", [4, 128, 128])
    pC_d = dout("pC", [4, 64, 128])
    pn_d = dout("pn", [4, 64])
    pm_d = dout("pm", [1, 4])
    oconv_d = dout("oconv", [NS, 3, 1536])
    oS_d = dout("oS", [NS, 4, 128, 128])
    oC_d = dout("oC", [NS, 4, 64, 128])
    on_d = dout("on", [NS, 256])
    om_d = dout("om", [NS, 4])

    def fsz(ap):
        try:
            return int(ap.free_size())
        except Exception:
            return 128

    def ecost(eng, ap):
        n = fsz(ap)
        if eng == "act":
            return 0.23 + n / 1200.0
        if eng == "pool":
            return 0.12 + n / 480.0
        return 0.08 + n / 960.0

    def mm(out, lhsT, rhs, start, stop, r, w, skip=False):
        passes = 4 if lhsT.dtype == F32 else 1
        c = 0.035 + passes * fsz(out) / 2400.0
        if skip:
            P.op("pe", lambda e, o=out, l=lhsT, rr=rhs, s=start, t=stop:
                 e.matmul(o, lhsT=l, rhs=rr, start=s, stop=t, skip_group_check=True), r, w, cost=c)
        else:
            P.op("pe", lambda e, o=out, l=lhsT, rr=rhs, s=start, t=stop:
                 e.matmul(o, lhsT=l, rhs=rr, start=s, stop=t), r, w, cost=c)

    def tr(out, in_, ident, r, w):
        P.op("pe", lambda e, o=out, i=in_, d=ident: e.transpose(o, i, d), r, w, cost=0.04 + fsz(out) / 2400.0)

    def tt(eng, out, in0, in1, op, r, w):
        P.op(eng, lambda e, o=out, a=in0, b=in1, p=op: e.tensor_tensor(out=o, in0=a, in1=b, op=p), r, w,
             cost=ecost(eng, out))

    def ts(eng, out, in0, s1, s2, op0, op1, r, w, accum=None):
        c = ecost(eng, out)
        if op1 is None:
            P.op(eng, lambda e, o=out, a=in0, x=s1, p0=op0:
                 e.tensor_scalar(out=o, in0=a, scalar1=x, scalar2=None, op0=p0), r, w, cost=c)
        elif accum is None:
            P.op(eng, lambda e, o=out, a=in0, x=s1, y=s2, p0=op0, p1=op1:
                 e.tensor_scalar(out=o, in0=a, scalar1=x, scalar2=y, op0=p0, op1=p1), r, w, cost=c)
        else:
            P.op(eng, lambda e, o=out, a=in0, x=s1, y=s2, p0=op0, p1=op1, ac=accum:
                 e.tensor_scalar(out=o, in0=a, scalar1=x, scalar2=y, op0=p0, op1=p1, accum_out=ac), r, w, cost=c)

    def stt(eng, out, in0, scalar, in1, op0, op1, r, w):
        P.op(eng, lambda e, o=out, a=in0, s=scalar, b=in1, p0=op0, p1=op1:
             e.scalar_tensor_tensor(out=o, in0=a, scalar=s, in1=b, op0=p0, op1=p1), r, w, cost=ecost(eng, out))

    def act(out, in_, func, r, w, bias=None, scale=1.0, accum=None):
        kw = {}
        if bias is not None:
            kw["bias"] = bias
        if accum is not None:
            kw["accum_out"] = accum
        tab = None
        if func in (AF.Silu, AF.Sigmoid):
            tab = "S"
        elif func in (AF.Exp, AF.Ln):
            tab = "E"
        P.op("act", lambda e, o=out, i=in_, f=func, s=scale, k=kw:
             e.activation(out=o, in_=i, func=f, scale=s, **k), r, w, cost=ecost("act", out), tab=tab)

    def cp(eng, out, in_, r, w):
        if eng == "act":
            act(out, in_, AF.Copy, r, w)
        else:
            P.op(eng, lambda e, o=out, i=in_: e.tensor_copy(out=o, in_=i), r, w, cost=ecost(eng, out))

    def red(eng, out, in_, op, r, w):
        P.op(eng, lambda e, o=out, i=in_, p=op: e.tensor_reduce(out=o, in_=i, axis=AX.X, op=p), r, w,
             cost=ecost(eng, in_))

    def memset(eng, ap, val, w):
        P.op(eng, lambda e, a=ap, v=val: e.memset(a, v), [], w, cost=ecost(eng, ap))

    def dma(q, out, in_, r, w, slow=False):
        if slow:
            P.dma(q, lambda e, o=out, i=in_: e.dma_start(out=o, in_=i, allow_slow_non_contiguous=True), r, w)
        else:
            P.dma(q, lambda e, o=out, i=in_: e.dma_start(out=o, in_=i), r, w)

    def rsqrt_small(out, in_, scale, r_, w_, tmp):
        ts("dve", tmp, in_, scale, EPS, ALU.mult, ALU.add, r_, [w_[0]])
        act(tmp, tmp, AF.Ln, [w_[0]], [w_[0]])
        act(out, tmp, AF.Exp, [w_[0]], w_, scale=-0.5)

    def bc3(ap, n_mid, n_in):
        return ap.unsqueeze(2).to_broadcast([ap.shape[0], n_mid, n_in])

    def bcm(ap, n_mid):
        return ap.unsqueeze(1).to_broadcast([ap.shape[0], n_mid, ap.shape[1]])

    with ExitStack() as top:
        def sb(stack, name, shape, dt=F32):
            return Tl(stack.enter_context(nc.sbuf_tensor("sb_" + name, list(shape), dt)), name)

        def ps(stack, name, shape, dt=F32):
            return Tl(stack.enter_context(nc.psum_tensor("ps_" + name, list(shape), dt)), name)

        Xt = top.enter_context(nc.sbuf_tensor("sb_X", [128, NT, D], F32))
        RX = [Res("X%d" % t) for t in range(NT)]
        XS1T = sb(top, "XS1T", [128, 8, NS])
        HNS = sb(top, "HNS", [128, 8, NS], BF16)
        IDF2 = sb(top, "IDF2", [128, 128])
        IDB = sb(top, "IDB", [128, 128], BF16)
        GPREMLP = sb(top, "GPREMLP", [128, 8])
        BK = [ps(top, "B%d" % i, [128, 512]) for i in range(7)]
        TPB = ps(top, "TPB", [128, 1024], BF16)

        dma("sp", GPREMLP[:], gpremlp_d, [], [GPREMLP])

        all_res_p1 = []

        with ExitStack() as p1:
            cur = [p1]

            def s1(name, shape, dt=F32):
                tl = sb(cur[0], name, shape, dt)
                all_res_p1.append(tl.r)
                return tl

            WIN = p1.enter_context(nc.sbuf_tensor("sb_WIN", [128, 8, 3600], BF16))
            RWIN = [Res("WIN%d" % i) for i in range(len(WIN_MOVES))]
            WOUT = s1("WOUT", [128, 8, D], BF16)
            CONST = s1("CONST", [128, NCONST])
            GPRE = s1("GPRE", [128, 8])
            CW = s1("CW", [128, 4, 12])
            GPM = s1("GPM", [128, D])
            SMB = s1("SMB", [128, 16])
            GNG = s1("GNG", [128, 128])
            MNG = s1("MNG", [128, 128])
            all_res_p1.extend(RWIN)

            win_v = win_d.rearrange("(k p) c -> p k c", p=128)
            for i, (dst, src, n) in enumerate(WIN_MOVES):
                dma("pool", WIN[:, :, dst:dst + n], win_v[:, :, src:src + n], [], [RWIN[i]])
            dma("pool", WOUT[:], wout_d.rearrange("(k p) c -> p k c", p=128), [], [WOUT])
            dma("sp", CONST[:], consts_d, [], [CONST])
            dma("sp", GPRE[:], gpre_d, [], [GPRE])
            dma("sp", CW[:], cw_d.rearrange("p (j c) -> p j c", j=4), [], [CW])
            dma("sp", GPM[:], gpm_d.partition_broadcast(128), [], [GPM])
            dma("sp", SMB[:], small_d.partition_broadcast(128), [], [SMB])
            dma("sp", GNG[:], gng_d.partition_broadcast(128), [], [GNG])
            dma("sp", MNG[:], mng_d.partition_broadcast(128), [], [MNG])

            IDF = CONST[:, C_IDF:C_IDF + 128]
            TRI = CONST[:, C_TRI:C_TRI + 128]
            ONES = CONST[:, C_ONES:C_ONES + 128]
            SEL127 = CONST[:, C_SEL127:C_SEL127 + 128]
            MST = CONST[:, C_MST:C_MST + 128]
            MIT = CONST[:, C_MIT:C_MIT + 128]
            MI = CONST[:, C_MI:C_MI + 128]
            SELR = CONST[0:4, C_SELR:C_SELR + 512]
            ONES4 = CONST[0:4, C_ONES:C_ONES + 128]

            def rwin(c0, c1):
                out = []
                for i, (dst, src, n) in enumerate(WIN_MOVES):
                    if dst < c1 and c0 < dst + n:
                        out.append(RWIN[i])
                return out

            cp("dve", IDB[:], IDF, [CONST], [IDB])
            cp("pool", IDF2[:], IDF, [CONST], [IDF2])

            NEGC = s1("NEGC", [128, 12])
            SGN = s1("SGN", [128, 12])
            BIA = s1("BIA", [128, 12])
            memset("pool", NEGC[:], -1.0, [NEGC])
            act(NEGC[:, 4:8], SMB[:, 0:4], AF.Exp, [SMB, NEGC], [NEGC])
            ts("dve", NEGC[:, 4:8], NEGC[:, 4:8], -1.0, None, ALU.mult, None, [NEGC], [NEGC])
            memset("pool", SGN[:], -1.0, [SGN])
            memset("pool", SGN[:, 4:8], 1.0, [SGN])
            memset("pool", BIA[:], 0.0, [BIA])
            cp("dve", BIA[:, 4:8], SMB[:, 4:8], [SMB, BIA], [BIA])
            ts("dve", BIA[:, 8:12], SMB[:, 12:16], -1.0, None, ALU.mult, None, [SMB, BIA], [BIA])

            p1a = ExitStack()
            cur[0] = p1a
            XN = s1("XN", [128, D], BF16)
            HT = s1("HT", [128, 8, 128], BF16)
            PCC = [s1("PCC%d" % i, [128, 131]) for i in range(3)]
            ACC = [s1("ACC%d" % i, [128, 128]) for i in range(3)]
            SA4 = [s1("SA4_%d" % i, [128, 8]) for i in range(2)]
            TAIL = s1("TAIL", [128, 12, 3])
            QKF = s1("QKF", [128, 8, 128])
            GQT = s1("GQT", [128, 4, 128], BF16)
            GKT = s1("GKT", [128, 4, 128], BF16)
            GVT = s1("GVT", [128, 4, 128], BF16)
            MQT = s1("MQT", [64, 4, 128], BF16)
            MKT = s1("MKT", [64, 4, 128], BF16)
            ZSG = s1("ZSG", [128, 512])
            MOSG = s1("MOSG", [128, 512])
            MVt = s1("MVt", [128, 4, 128], BF16)
            GT = s1("GT", [128, 16])
            ARG = s1("ARG", [128, 12])
            GL = s1("GL", [128, 12])
            BETA = s1("BETA", [128, 4])
            IG = s1("IG", [128, 4])
            GF = s1("GF", [128, 8])
            COLS = s1("COLS", [128, 20])
            RT = s1("RT", [4, 5, 128])
            BD = s1("BD", [4, 4, 128])
            QKD = s1("QKD", [128, 4, 128], BF16)
            MB = [s1("MB%d" % i, [128, 512]) for i in range(2)]
            LB = [s1("LB%d" % i, [128, 512]) for i in range(2)]
            RR = s1("RR", [128, 512])
            BINV = s1("BINV", [128, 4, 128], BF16)
            GKt = s1("GKt", [128, 4, 128], BF16)
            XK = s1("XK", [128, 4, 128], BF16)
            KEND = s1("KEND", [128, 4, 128], BF16)
            GVb = s1("GVb", [128, 4, 128], BF16)
            SC4 = [s1("SC4_%d" % i, [128, 8]) for i in range(6)]
            LASTG = s1("LASTG", [128, 8])
            WKT = s1("WKT", [128, 4, 128], BF16)
            S32 = s1("S32", [128, 4, 128])
            SBh = s1("SBh", [128, 4, 128], BF16)
            U = s1("U", [128, 4, 128], BF16)
            TMP = [s1("TMP%d" % i, [128, 512]) for i in range(2)]
            WV = LB[1]
            MIX = View(MB[0].t.bitcast(BF16)[:, 0:1024], MB[0].r)
            MIXT = View(RR.t.bitcast(BF16)[:, 0:1024].rearrange("p (k t) -> p k t", k=8), RR.r)
            PK = s1("PK", [128, 4, 64], BF16)
            EXPQ = TMP[1]
            EXPA = MB[0]
            MKt = s1("MKt", [128, 4, 64], BF16)
            PQKb = s1("PQKb", [128, 4, 128], BF16)
            PQKT = s1("PQKT", [128, 4, 128], BF16)
            SG4 = [s1("SG4_%d" % i, [128, 8]) for i in range(6)]
            C32 = s1("C32", [64, 4, 128])
            CBh = s1("CBh", [64, 4, 128], BF16)
            N32 = s1("N32", [64, 4])
            NBh = s1("NBh", [64, 4, 2], BF16)
            MBC = s1("MBC", [128, 4])
            DMAX = s1("DMAX", [128, 4])
            T12 = s1("T12", [128, 12])
            WLC = s1("WLC", [64, 4])
            ONEB = s1("ONEB", [128, 2], BF16)
            for b in BK:
                all_res_p1.append(b.r)
            all_res_p1.append(TPB.r)

            memset("pool", TAIL[:], 0.0, [TAIL])
            memset("pool", S32[:], 0.0, [S32])
            memset("pool", SBh[:], 0.0, [SBh])
            memset("pool", C32[:], 0.0, [C32])
            memset("pool", CBh[:], 0.0, [CBh])
            memset("pool", N32[:], 0.0, [N32])
            memset("pool", NBh[:], 0.0, [NBh])
            memset("pool", MBC[:], 0.0, [MBC])
            memset("pool", ONEB[:], 1.0, [ONEB])

            def headnorm_gate(src, gate, dst, t0, ss, rs):
                tt("pool", t0[:], src[:], src[:], ALU.mult, [src], [t0])
                red("dve", ss[:, 0:4], t0[:].rearrange("p (h e) -> p h e", h=4), ALU.add, [t0], [ss])
                rsqrt_small(rs[:, 0:4], ss[:, 0:4], 1.0 / 128, [ss], [rs], rs[:, 4:8])
                tt("dve", t0[:].rearrange("p (h e) -> p h e", h=4), src[:].rearrange("p (h e) -> p h e", h=4),
                   bc3(rs[:, 0:4], 4, 128), ALU.mult, [src, rs], [t0])
                tt("dve", dst, t0[:], gate[:], ALU.mult, [t0, gate], [MIX])

            def gen_H(th):
                for k in range(8):
                    tr(TPB[:, k * 128:(k + 1) * 128], MIX[:, k * 128:(k + 1) * 128], IDB[:], [MIX, IDB], [TPB])
                yield
                cp("act", MIXT[:].rearrange("p k t -> p (k t)"), TPB[:], [TPB], [MIXT])
                yield
                for eh in range(2):
                    for k in range(8):
                        mm(BK[4 + eh][:], MIXT[:, k, :], WOUT[:, k, eh * 512:(eh + 1) * 512], k == 0, k == 7,
                           [MIXT, WOUT], [BK[4 + eh]])
                    yield
                ss, rs = SC4[0], SC4[1]
                for eh in range(2):
                    act(MIX[:, eh * 512:(eh + 1) * 512], BK[4 + eh][:], AF.Square, [BK[4 + eh]], [MIX, ss],
                        accum=ss[:, eh:eh + 1])
                    yield
                tt("dve", ss[:, 2:3], ss[:, 0:1], ss[:, 1:2], ALU.add, [ss], [ss])
                rsqrt_small(rs[:, 0:1], ss[:, 2:3], 1.0 / D, [ss], [rs], rs[:, 1:2])
                yield
                for eh in range(2):
                    sl = slice(eh * 512, (eh + 1) * 512)
                    stt("dve", BK[4 + eh][:], BK[4 + eh][:], rs[:, 0:1], GPM[:, sl], ALU.mult, ALU.mult,
                        [BK[4 + eh], rs, GPM], [BK[4 + eh]])
                    yield
                    tt("dve", Xt[:, th, sl], Xt[:, th, sl], BK[4 + eh][:], ALU.add, [RX[th], BK[4 + eh]], [RX[th]])
                    yield

            def interleave(ga, gb, na, nb):
                a = b = True
                while a or b:
                    for _ in range(na):
                        if a:
                            try:
                                next(ga)
                            except StopIteration:
                                a = False
                    for _ in range(nb):
                        if b:
                            try:
                                next(gb)
                            except StopIteration:
                                b = False

            RB1 = [Res("B1s%d" % i) for i in range(4)]
            ORDER = [8, 9, 10, 11, 0, 1, 2, 3, 4, 5, 6, 7]

            def gen_Bconv():
                def b_mm(i):
                    ch = ORDER[i]
                    c0 = ch * 128
                    bk = BK[1 + i % 2]
                    for k in range(8):
                        mm(bk[:, 0:128], WIN[:, k, c0:c0 + 128], HT[:, k, :], k == 0, k == 7,
                           [HT] + rwin(c0, c0 + 128), [bk])

                def b_copy(i):
                    ch = ORDER[i]
                    pc = PCC[i % 3]
                    bk = BK[1 + i % 2]
                    cp("act", pc[:, 3:131], bk[:, 0:128], [bk], [pc])
                    cp("pool", pc[:, 0:3], TAIL[:, ch, :], [TAIL], [pc])

                def b_conv(i):
                    ch = ORDER[i]
                    pc, ac = PCC[i % 3], ACC[i % 3]
                    ts("dve", ac[:], pc[:, 0:128], CW[:, 0, ch:ch + 1], None, ALU.mult, None, [pc, CW], [ac])
                    for j in range(1, 3):
                        stt("dve", ac[:], pc[:, j:j + 128], CW[:, j, ch:ch + 1], ac[:], ALU.mult, ALU.add,
                            [pc, CW, ac], [ac])
                    if ch < 8:
                        stt("dve", QKF[:, ch, :], pc[:, 3:131], CW[:, 3, ch:ch + 1], ac[:], ALU.mult, ALU.add,
                            [pc, CW, ac], [QKF])
                    else:
                        stt("dve", ac[:], pc[:, 3:131], CW[:, 3, ch:ch + 1], ac[:], ALU.mult, ALU.add,
                            [pc, CW, ac], [ac])
                    cp("pool", TAIL[:, ch, :], pc[:, 128:131], [pc], [TAIL])

                def b_silu(i):
                    ch = ORDER[i]
                    if ch >= 8:
                        act(GVT[:, ch - 8, :], ACC[i % 3][:], AF.Silu, [ACC[i % 3]], [GVT])

                for i in range(12 + 3):
                    if i < 12:
                        b_mm(i)
                    if 0 <= i - 1 < 12:
                        b_copy(i - 1)
                    if 0 <= i - 2 < 12:
                        b_conv(i - 2)
                    if 0 <= i - 3 < 12:
                        b_silu(i - 3)
                    yield
                for hf in range(2):
                    act(QKF[:, hf * 4:(hf + 1) * 4, :], QKF[:, hf * 4:(hf + 1) * 4, :], AF.Silu, [QKF], [QKF])
                    yield

            def interleave3(gens, steps):
                alive = [True] * len(gens)
                while any(alive):
                    for gi, g in enumerate(gens):
                        for _ in range(steps[gi]):
                            if alive[gi]:
                                try:
                                    next(g)
                                except StopIteration:
                                    alive[gi] = False

            def stage_A(t):
                Xv = Xt[:, t, :]
                ss, rs = SA4[0], SA4[1]
                act(XN[:], Xv, AF.Square, [RX[t]], [XN, ss], accum=ss[:, 0:1])
                rsqrt_small(rs[:, 0:1], ss[:, 0:1], 1.0 / D, [ss], [rs], rs[:, 1:2])
                ts("dve", XN[:], Xv, rs[:, 0:1], None, ALU.mult, None, [RX[t], rs], [XN])
                for k in range(8):
                    tr(TPB[:, k * 128:(k + 1) * 128], XN[:, k * 128:(k + 1) * 128], IDB[:], [XN, IDB], [TPB])
                tt("dve", HT[:], TPB[:].rearrange("p (k t) -> p k t", k=8), bc3(GPRE[:], 8, 128), ALU.mult,
                   [TPB, GPRE], [HT])

            P.marks.append(len(P.ops))
            for t in range(KNT):
                dma("sp", Xt[:, t, :], x_d[t * 128:(t + 1) * 128, :], [], [RX[t]])
            stage_A(0)
            for _ in gen_Bconv():
                pass

            for t in range(KNT):
                P.marks.append(len(P.ops))
                Xv = Xt[:, t, :]
                hgen = gen_H(t - 1) if t > 0 else None

                def hstep(n=2):
                    if hgen is not None:
                        for _ in range(n):
                            try:
                                next(hgen)
                            except StopIteration:
                                break

                def tok_proj(bk, c0, n):
                    for k in range(8):
                        mm(bk[:, 0:n], HT[:, k, :], WIN[:, k, c0:c0 + n], k == 0, k == 7,
                           [HT] + rwin(c0, c0 + n), [bk])
                hstep(2)
                tok_proj(BK[0], W_GZ, 512)
                hstep(1)
                act(TMP[0][:], BK[0][:], AF.Silu, [BK[0]], [TMP[0]])
                tt("pool", ZSG[:].rearrange("p (h e) -> p h e", h=4), TMP[0][:].rearrange("p (h e) -> p h e", h=4),
                   bcm(GNG[:], 4), ALU.mult, [TMP[0], GNG], [ZSG])
                hstep(1)
                tok_proj(BK[1], W_MO, 512)
                hstep(1)
                act(TMP[1][:], BK[1][:], AF.Sigmoid, [BK[1]], [TMP[1]])
                tt("pool", MOSG[:].rearrange("p (h e) -> p h e", h=4), TMP[1][:].rearrange("p (h e) -> p h e", h=4),
                   bcm(MNG[:], 4), ALU.mult, [TMP[1], MNG], [MOSG])
                for hh in range(8):
                    bk = BK[hh % 2]
                    c0 = W_MQK + hh * 64
                    for k in range(8):
                        mm(bk[0:64, 0:128], WIN[:, k, c0:c0 + 64], HT[:, k, :], k == 0, k == 7,
                           [HT] + rwin(c0, c0 + 64), [bk])
                    if hh < 4:
                        cp("act", MQT[:, hh, :], bk[0:64, 0:128], [bk], [MQT])
                    else:
                        act(MKT[:, hh - 4, :], bk[0:64, 0:128], AF.Copy, [bk], [MKT], scale=0.125)
                    hstep(1)
                hstep(20)
                tok_proj(BK[0], W_MV, 512)
                cp("act", MVt[:].rearrange("p h e -> p (h e)"), BK[0][:], [BK[0]], [MVt])
                tok_proj(BK[1], W_GATE, 16)
                cp("dve", GT[:], BK[1][:, 0:16], [BK[1]], [GT])
                tt("dve", ARG[:], GT[:, 0:12], SGN[:], ALU.mult, [GT, SGN], [ARG])
                tt("dve", ARG[:], ARG[:], BIA[:], ALU.add, [ARG, BIA], [ARG])
                for hf in range(2):
                    act(TMP[hf][:], QKF[:, hf * 4:(hf + 1) * 4, :].rearrange("p a b -> p (a b)"), AF.Square,
                        [QKF], [TMP[hf]])
                    mm(BK[2 + hf][:], ONES, TMP[hf][:], True, True, [CONST, TMP[hf]], [BK[2 + hf]])
                for hf in range(2):
                    ts("dve", TMP[hf][:], BK[2 + hf][:], 1.0, EPS, ALU.mult, ALU.add, [BK[2 + hf]], [TMP[hf]])
                    act(TMP[hf][:], TMP[hf][:], AF.Ln, [TMP[hf]], [TMP[hf]])
                    act(TMP[hf][:], TMP[hf][:], AF.Exp, [TMP[hf]], [TMP[hf]], scale=-0.5)
                stt("dve", GQT[:].rearrange("p a b -> p (a b)"), QKF[:, 0:4, :].rearrange("p a b -> p (a b)"),
                    128.0 ** -0.5, TMP[0][:], ALU.mult, ALU.mult, [QKF, TMP[0]], [GQT])
                tt("pool", GKT[:].rearrange("p a b -> p (a b)"), QKF[:, 4:8, :].rearrange("p a b -> p (a b)"),
                   TMP[1][:], ALU.mult, [QKF, TMP[1]], [GKT])
                act(ARG[:], ARG[:], AF.Exp, [ARG], [ARG])
                act(ARG[:], ARG[:], AF.Ln, [ARG], [ARG], bias=1.0)
                tt("dve", GL[:], ARG[:], NEGC[:], ALU.mult, [ARG, NEGC], [GL])
                act(BETA[:], GL[:, 0:4], AF.Exp, [GL], [BETA])
                tt("dve", IG[:], GT[:, 12:16], SMB[:, 8:12], ALU.add, [GT, SMB], [IG])

                if t + 1 < KNT:
                    stage_A(t + 1)

                mm(BK[2][:, 0:8], TRI, GL[:, 4:12], True, True, [CONST, GL], [BK[2]])
                cp("dve", GF[:], BK[2][:, 0:8], [BK[2]], [GF])
                cp("pool", COLS[:, 0:4], GF[:, 0:4], [GF], [COLS])
                tt("dve", COLS[:, 4:8], GF[:, 0:4], GL[:, 0:4], ALU.add, [GF, GL], [COLS])
                cp("pool", COLS[:, 8:12], GF[:, 4:8], [GF], [COLS])
                tt("dve", COLS[:, 12:16], IG[:], GF[:, 4:8], ALU.subtract, [IG, GF], [COLS])
                ts("dve", COLS[:, 16:20], GF[:, 0:4], -1.0, None, ALU.mult, None, [GF], [COLS])
                for j in range(4):
                    tr(BK[3][0:4, j * 128:(j + 1) * 128], COLS[:, 4 * j:4 * j + 4], IDF, [COLS, CONST], [BK[3]])
                tr(BK[4][0:4, 0:128], COLS[:, 16:20], IDF, [COLS, CONST], [BK[4]])
                cp("dve", RT[:, 0:4, :].rearrange("p a b -> p (a b)"), BK[3][0:4, :], [BK[3]], [RT])
                cp("dve", RT[:, 4, :], BK[4][0:4, 0:128], [BK[4]], [RT])
                SELR3 = SELR.rearrange("p (a b) -> p a b", a=4)
                tt("dve", BD[:], bcm(RT[:, 1, :], 4), SELR3, ALU.mult, [RT, CONST], [BD])
                mm(BK[2][:], ONES4, BD[:].rearrange("p a b -> p (a b)"), True, False, [CONST, BD], [BK[2]])
                mm(BK[2][:], RT[:, 4, :], SELR, False, True, [RT, CONST], [BK[2]])
                tt("dve", EXPA[:].rearrange("p (h c) -> p h c", h=4), BK[2][:].rearrange("p (h c) -> p h c", h=4),
                   bcm(MST, 4), ALU.add, [BK[2], CONST], [EXPA])
                act(EXPA[:], EXPA[:], AF.Exp, [EXPA], [EXPA])
                tt("dve", BD[:], bcm(RT[:, 0, :], 4), SELR3, ALU.mult, [RT, CONST], [BD])
                mm(BK[3][:], ONES4, BD[:].rearrange("p a b -> p (a b)"), True, False, [CONST, BD], [BK[3]])
                mm(BK[3][:], RT[:, 4, :], SELR, False, True, [RT, CONST], [BK[3]])
                tt("dve", EXPQ[:].rearrange("p (h c) -> p h c", h=4), BK[3][:].rearrange("p (h c) -> p h c", h=4),
                   bcm(MIT, 4), ALU.add, [BK[3], CONST], [EXPQ])
                act(EXPQ[:], EXPQ[:], AF.Exp, [EXPQ], [EXPQ])
                for h in range(4):
                    mm(BK[4][:, h * 128:(h + 1) * 128], GKT[:, h, :], GKT[:, h, :], True, True, [GKT], [BK[4]])
                for h in range(4):
                    mm(BK[5][:, h * 128:(h + 1) * 128], GKT[:, h, :], GQT[:, h, :], True, True, [GKT, GQT], [BK[5]])
                tt("dve", MB[0][:], BK[4][:], EXPA[:], ALU.mult, [BK[4], EXPA], [MB[0]])
                tt("dve", QKD[:].rearrange("p h c -> p (h c)"), BK[5][:], EXPQ[:], ALU.mult, [BK[5], EXPQ], [QKD])

                for h in range(4):
                    tr(TPB[:, h * 128:(h + 1) * 128], GKT[:, h, :], IDB[:], [GKT, IDB], [TPB])
                    tr(TPB[:, 512 + h * 128:512 + (h + 1) * 128], GVT[:, h, :], IDB[:], [GVT, IDB], [TPB])
                cp("act", GKt[:].rearrange("p h d -> p (h d)"), TPB[:, 0:512], [TPB], [GKt])
                EG, BEG, EGL, GEND = SC4[0], SC4[1], SC4[2], SC4[3]
                act(EG[:, 0:4], GF[:, 0:4], AF.Exp, [GF], [EG])
                tt("dve", BEG[:, 0:4], EG[:, 0:4], BETA[:], ALU.mult, [EG, BETA], [BEG])
                mm(BK[2][:, 0:8], SEL127, GF[:], True, True, [CONST, GF], [BK[2]])
                cp("dve", LASTG[:], BK[2][:, 0:8], [BK[2]], [LASTG])
                tt("dve", EGL[:, 0:4], LASTG[:, 0:4], GF[:, 0:4], ALU.subtract, [LASTG, GF], [EGL])
                act(EGL[:, 0:4], EGL[:, 0:4], AF.Exp, [EGL], [EGL])
                act(GEND[:, 0:4], LASTG[:, 0:4], AF.Exp, [LASTG], [GEND])
                tt("dve", XK[:], GKt[:], bc3(BEG[:, 0:4], 4, 128), ALU.mult, [GKt, BEG], [XK])
                tt("pool", KEND[:], GKt[:], bc3(EGL[:, 0:4], 4, 128), ALU.mult, [GKt, EGL], [KEND])
                tt("dve", GVb[:], TPB[:, 512:1024].rearrange("p (h e) -> p h e", h=4), bc3(BETA[:], 4, 128),
                   ALU.mult, [TPB, BETA], [GVb])
                def gen_EF():
                    for h in range(4):
                        tr(BK[5][:, h * 128:(h + 1) * 128], MB[0][:, h * 128:(h + 1) * 128], IDF, [MB[0], CONST], [BK[5]])
                    yield
                    cp("act", LB[0][:], BK[5][:], [BK[5]], [LB[0]])
                    yield
                    tt("dve", RR[:].rearrange("p (h c) -> p h c", h=4), bcm(IDF, 4),
                       MB[0][:].rearrange("p (h c) -> p h c", h=4), ALU.subtract, [CONST, MB[0]], [RR])
                    yield
                    NLEV = 6
                    yield
                    for k in range(NLEV):
                        a, b = k % 2, (k + 1) % 2
                        for h in range(4):
                            sl = slice(h * 128, (h + 1) * 128)
                            mm(BK[3][:, sl], MB[a][:, sl], LB[a][:, sl], True, True, [MB[a], LB[a]], [BK[3]])
                        yield
                        if k < NLEV - 1:
                            for h in range(4):
                                sl = slice(h * 128, (h + 1) * 128)
                                mm(BK[4][:, sl], LB[a][:, sl], MB[a][:, sl], True, True, [MB[a], LB[a]], [BK[4]])
                            yield
                        cp("act", LB[b][:], BK[3][:], [BK[3]], [LB[b]])
                        yield
                        if k < NLEV - 1:
                            cp("dve", MB[b][:], BK[4][:], [BK[4]], [MB[b]])
                            yield
                        for h in range(4):
                            sl = slice(h * 128, (h + 1) * 128)
                            mm(BK[5][:, sl], LB[b][:, sl], RR[:, sl], True, True, [LB[b], RR], [BK[5]])
                        yield
                        tt("dve", RR[:], RR[:], BK[5][:], ALU.add, [RR, BK[5]], [RR])
                        yield
                    cp("act", BINV[:].rearrange("p h c -> p (h c)"), RR[:], [RR], [BINV])
                    yield
                    for h in range(4):
                        sl = slice(h * 128, (h + 1) * 128)
                        mm(BK[3][:, sl], BINV[:, h, :], GVb[:, h, :], True, True, [BINV, GVb], [BK[3]])
                        mm(BK[4][:, sl], XK[:, h, :], BINV[:, h, :], True, True, [BINV, XK], [BK[4]])
                    yield
                    cp("act", WV[:], BK[3][:], [BK[3]], [WV])
                    yield
                    cp("dve", WKT[:].rearrange("p h c -> p (h c)"), BK[4][:], [BK[4]], [WKT])
                    yield
                    for h in range(4):
                        sl = slice(h * 128, (h + 1) * 128)
                        mm(BK[3][:, sl], WKT[:, h, :], SBh[:, h, :], True, True, [WKT, SBh], [BK[3]])
                        mm(BK[5][:, sl], GQT[:, h, :], SBh[:, h, :], True, True, [GQT, SBh], [BK[5]])
                    yield
                    tt("dve", U[:].rearrange("p h e -> p (h e)"), WV[:], BK[3][:], ALU.subtract, [WV, BK[3]], [U])
                    yield
                    for h in range(4):
                        sl = slice(h * 128, (h + 1) * 128)
                        mm(BK[4][:, sl], QKD[:, h, :], U[:, h, :], True, True, [QKD, U], [BK[4]])
                        mm(BK[3][:, sl], KEND[:, h, :], U[:, h, :], True, True, [KEND, U], [BK[3]])
                    yield
                    OG = MB[1]
                    yield
                    tt("dve", OG[:].rearrange("p (h e) -> p h e", h=4), BK[5][:].rearrange("p (h e) -> p h e", h=4),
                       bc3(EG[:, 0:4], 4, 128), ALU.mult, [BK[5], EG], [OG])
                    yield
                    tt("dve", OG[:], OG[:], BK[4][:], ALU.add, [OG, BK[4]], [OG])
                    yield
                    for h in range(4):
                        stt("dve", S32[:, h, :], S32[:, h, :], GEND[:, h:h + 1], BK[3][:, h * 128:(h + 1) * 128],
                            ALU.mult, ALU.add, [S32, GEND, BK[3]], [S32])
                    yield
                    cp("act", SBh[:], S32[:], [S32], [SBh])
                    yield
                    headnorm_gate(OG, ZSG, MIX[:, 0:512], LB[0], SC4[4], SC4[5])
                    yield
                def gen_G():
                    tt("dve", BD[:], bcm(RT[:, 3, :], 4), SELR3, ALU.mult, [RT, CONST], [BD])
                    yield
                    mm(BK[6][:], ONES4, BD[:].rearrange("p a b -> p (a b)"), True, False, [CONST, BD], [BK[6]])
                    yield
                    mm(BK[6][:], RT[:, 2, :], SELR, False, True, [RT, CONST], [BK[6]])
                    yield
                    PD = TMP[0]
                    yield
                    tt("dve", PD[:].rearrange("p (h s) -> p h s", h=4), BK[6][:].rearrange("p (h s) -> p h s", h=4),
                       bcm(MI, 4), ALU.add, [BK[6], CONST], [PD])
                    yield
                    red("dve", DMAX[:], PD[:].rearrange("p (h s) -> p h s", h=4), ALU.max, [PD], [DMAX])
                    yield
                    for h in range(4):
                        mm(BK[0][:, h * 128:(h + 1) * 128], MQT[:, h, :], MKT[:, h, :], True, True, [MQT, MKT], [BK[0]])
                    yield
                    for h in range(4):
                        tr(TPB[:, h * 64:(h + 1) * 64], MKT[:, h, :], IDB[0:64, 0:64], [MKT, IDB], [TPB])
                    yield
                    cp("act", MKt[:].rearrange("p h d -> p (h d)"), TPB[:, 0:256], [TPB], [MKt])
                    yield
                    Bv, MT, WP, EMT = SG4[0], SG4[1], SG4[2], SG4[3]
                    yield
                    tt("dve", Bv[:, 0:4], GF[:, 4:8], MBC[:], ALU.add, [GF, MBC], [Bv])
                    yield
                    tt("dve", MT[:, 0:4], Bv[:, 0:4], DMAX[:], ALU.max, [Bv, DMAX], [MT])
                    yield
                    tt("dve", WP[:, 0:4], Bv[:, 0:4], MT[:, 0:4], ALU.subtract, [Bv, MT], [WP])
                    yield
                    act(WP[:, 0:4], WP[:, 0:4], AF.Exp, [WP], [WP])
                    yield
                    tt("dve", PD[:].rearrange("p (h s) -> p h s", h=4), PD[:].rearrange("p (h s) -> p h s", h=4),
                       bc3(MT[:, 0:4], 4, 128), ALU.subtract, [PD, MT], [PD])
                    yield
                    act(PD[:], PD[:], AF.Exp, [PD], [PD])
                    yield
                    tt("dve", PD[:], PD[:], BK[0][:], ALU.mult, [PD, BK[0]], [PD])
                    yield
                    RS = SG4[4]
                    yield
                    red("dve", RS[:, 0:4], PD[:].rearrange("p (h s) -> p h s", h=4), ALU.add, [PD], [RS])
                    yield
                    cp("act", PQKb[:].rearrange("p h s -> p (h s)"), PD[:], [PD], [PQKb])
                    yield
                    for h in range(4):
                        tr(TPB[:, 512 + h * 128:512 + (h + 1) * 128], PQKb[:, h, :], IDB[:], [PQKb, IDB], [TPB])
                    yield
                    cp("act", PQKT[:].rearrange("p h s -> p (h s)"), TPB[:, 512:1024], [TPB], [PQKT])
                    yield
                    for h in range(4):
                        sl = slice(h * 128, (h + 1) * 128)
                        mm(BK[0][:, sl], MQT[:, h, :], CBh[:, h, :], True, True, [MQT, CBh], [BK[0]])
                        mm(BK[6][:, 2 * h:2 * h + 2], MQT[:, h, :], NBh[:, h, :], True, True, [MQT, NBh], [BK[6]])
                    yield
                    NUM = TMP[0]
                    yield
                    tt("dve", NUM[:].rearrange("p (h e) -> p h e", h=4), BK[0][:].rearrange("p (h e) -> p h e", h=4),
                       bc3(WP[:, 0:4], 4, 128), ALU.mult, [BK[0], WP], [NUM])
                    yield
                    for h in range(4):
                        sl = slice(h * 128, (h + 1) * 128)
                        mm(BK[0][:, sl], PQKT[:, h, :], MVt[:, h, :], True, True, [PQKT, MVt], [BK[0]])
                    yield
                    tt("dve", NUM[:], NUM[:], BK[0][:], ALU.add, [NUM, BK[0]], [NUM])
                    yield
                    DEN = SG4[5]
                    yield
                    tt("dve", DEN[:, 0:4], BK[6][:, 0:8].rearrange("p (h two) -> p h two", two=2)[:, :, 0], WP[:, 0:4], ALU.mult, [BK[6], WP], [DEN])
                    yield
                    tt("dve", DEN[:, 0:4], DEN[:, 0:4], RS[:, 0:4], ALU.add, [DEN, RS], [DEN])
                    yield
                    act(EMT[:, 0:4], MT[:, 0:4], AF.Exp, [MT], [EMT], scale=-1.0)
                    yield
                    ts("dve", DEN[:, 4:8], DEN[:, 0:4], -1.0, None, ALU.mult, None, [DEN], [DEN])
                    yield
                    tt("dve", DEN[:, 0:4], DEN[:, 0:4], DEN[:, 4:8], ALU.max, [DEN], [DEN])
                    yield
                    tt("dve", DEN[:, 0:4], DEN[:, 0:4], EMT[:, 0:4], ALU.max, [DEN, EMT], [DEN])
                    yield
                    P.op("dve", lambda e, d=DEN: e.reciprocal(out=d[:, 0:4], in_=d[:, 0:4]), [DEN], [DEN])
                    yield
                    tt("dve", NUM[:].rearrange("p (h e) -> p h e", h=4), NUM[:].rearrange("p (h e) -> p h e", h=4),
                       bc3(DEN[:, 0:4], 4, 128), ALU.mult, [NUM, DEN], [NUM])
                    yield
                    cp("pool", T12[:, 0:4], MT[:, 0:4], [MT], [T12])
                    yield
                    tt("dve", T12[:, 4:8], GF[:, 4:8], MT[:, 0:4], ALU.subtract, [GF, MT], [T12])
                    yield
                    cp("pool", T12[:, 8:12], WP[:, 0:4], [WP], [T12])
                    yield
                    mm(BK[6][:, 16:28], SEL127, T12[:], True, True, [CONST, T12], [BK[6]])
                    yield
                    cp("dve", MBC[:], BK[6][:, 16:20], [BK[6]], [MBC])
                    yield
                    PEND = SG4[4]
                    yield
                    tt("dve", PEND[:, 4:8], COLS[:, 12:16], BK[6][:, 20:24], ALU.add, [COLS, BK[6]], [PEND])
                    yield
                    act(PEND[:, 4:8], PEND[:, 4:8], AF.Exp, [PEND], [PEND])
                    yield
                    cp("dve", WLC[:], BK[6][0:64, 24:28], [BK[6]], [WLC])
                    yield
                    tt("dve", PK[:], MKt[:], bc3(PEND[:, 4:8], 4, 64), ALU.mult, [MKt, PEND], [PK])
                    yield
                    for h in range(4):
                        mm(BK[0][0:64, h * 128:(h + 1) * 128], PK[:, h, :], MVt[:, h, :], True, True, [PK, MVt], [BK[0]])
                    yield
                    for h in range(4):
                        mm(BK[6][0:64, 32 + 2 * h:34 + 2 * h], PK[:, h, :], ONEB[:], True, True, [PK, ONEB], [BK[6]])
                    yield
                    for h in range(4):
                        stt("dve", C32[:, h, :], C32[:, h, :], WLC[:, h:h + 1], BK[0][0:64, h * 128:(h + 1) * 128],
                            ALU.mult, ALU.add, [C32, WLC, BK[0]], [C32])
                    yield
                    tt("dve", N32[:], N32[:], WLC[:], ALU.mult, [N32, WLC], [N32])
                    yield
                    tt("dve", N32[:], N32[:], BK[6][0:64, 32:40].rearrange("p (a two) -> p a two", two=2)[:, :, 0],
                       ALU.add, [N32, BK[6]], [N32])
                    yield
                    cp("act", CBh[:], C32[:], [C32], [CBh])
                    yield
                    cp("act", NBh[:, :, 0], N32[:], [N32], [NBh])
                    yield

                if t + 1 < KNT:
                    interleave3([gen_EF(), gen_G(), gen_Bconv()], KRATIO)
                else:
                    interleave3([gen_EF(), gen_G()], [2, 1])
                headnorm_gate(TMP[0], MOSG, MIX[:, 512:1024], TMP[1], SG4[4], SG4[5])

            for _ in gen_H(KNT - 1):
                pass

            dma("sp", pS_d.rearrange("h d e -> d h e"), S32[:], [S32], [])
            dma("sp", pC_d.rearrange("h d e -> d h e"), C32[:], [C32], [])
            dma("sp", pn_d.rearrange("h d -> d h"), N32[:], [N32], [], slow=True)
            dma("sp", pm_d, MBC[0:1, :], [MBC], [])
            for ch in range(12):
                tr(BK[2 + ch // 4][0:3, (ch % 4) * 128:(ch % 4 + 1) * 128], TAIL[:, ch, :], IDF, [TAIL, CONST],
                   [BK[2 + ch // 4]])
            for g in range(3):
                cp("dve", TMP[g % 2][0:3, :], BK[2 + g][0:3, :], [BK[2 + g]], [TMP[g % 2]])
                dma("sp", pconv_d[:, g * 512:(g + 1) * 512], TMP[g % 2][0:3, :], [TMP[g % 2]], [])


            P.barrier(all_res_p1 + RX + [XS1T.r, HNS.r, IDF2.r, IDB.r, GPREMLP.r])
            p1a.close()
            p1b = ExitStack()
            cur[0] = p1b
            R_ = NS
            GRP = 1
            XS = s1("XS", [R_, D])
            XNs = s1("XNs", [R_, D], BF16)
            HTs = s1("HTs", [128, 8, R_], BF16)
            SCV = s1("SCV", [R_, 512])
            XPT = s1("XPT", [128, 12, 4, R_])
            ACs = s1("ACs", [128, 12, R_])
            TM12 = s1("TM12", [128, 12, R_])
            SQs = s1("SQs", [128, 8, R_])
            GQs = s1("GQs", [128, 4, R_])
            GKs = s1("GKs", [128, 4, R_])
            GVs = s1("GVs", [128, 4, R_])
            MQs = s1("MQs", [64, 4, R_])
            MKs = s1("MKs", [64, 4, R_])
            ZSGs = s1("ZSGs", [R_, 512])
            MOSGs = s1("MOSGs", [R_, 512])
            MVs = s1("MVs", [R_, 4, 128])
            GTs = s1("GTs", [R_, 16])
            ARGs = s1("ARGs", [R_, 12])
            GLs = s1("GLs", [R_, 12])
            BETAs = s1("BETAs", [R_, 4])
            IGs = s1("IGs", [R_, 4])
            EGs = s1("EGs", [R_, 4])
            Qt = s1("Qt", [R_, 4, 128])
            Kt = s1("Kt", [R_, 4, 128])
            Vt = s1("Vt", [R_, 4, 128])
            MQt = s1("MQt", [R_, 4, 64])
            MKtt = s1("MKtt", [R_, 4, 64])
            PKt = s1("PKt", [R_, 4, 64])
            DIAGI = s1("DIAGI", [R_, 16, 16])
            M16 = s1("M16", [128, 16, 16])
            KD = [s1("KD%d" % i, [128, 4, 16]) for i in range(2)]
            QD = [s1("QD%d" % i, [128, 4, 16]) for i in range(2)]
            QDm = [s1("QDm%d" % i, [64, 4, 16]) for i in range(2)]
            DG4 = s1("DG4", [R_, 16, 4])
            EGBC = s1("EGBC", [128, 64])
            WPBC = s1("WPBC", [128, 64])
            S0b = [s1("S0b%d" % i, [128, 4, 128]) for i in range(2)]
            C0b = [s1("C0b%d" % i, [64, 4, 128]) for i in range(2)]
            Ug = s1("Ug", [R_, 4, 128])
            KROW = s1("KROW", [R_, 512])
            PKROW = View(DIAGI[:].rearrange("p a b -> p (a b)"), DIAGI.r)
            N0 = s1("N0", [R_, 4, 64])
            SMs = [s1("SMs%d" % i, [R_, 8]) for i in range(10)]
            T1 = s1("T1s", [R_, 512])
            T2 = KROW

            def hn_gate16(src, gate, dst, wres):
                ss, rs = SMs[8], SMs[9]
                tt("pool", T2[:], src[:], src[:], ALU.mult, [src], [T2])
                red("dve", ss[:, 0:4], T2[:].rearrange("p (h e) -> p h e", h=4), ALU.add, [T2], [ss])
                rsqrt_small(rs[:, 0:4], ss[:, 0:4], 1.0 / 128, [ss], [rs], rs[:, 4:8])
                tt("dve", T2[:].rearrange("p (h e) -> p h e", h=4), src[:].rearrange("p (h e) -> p h e", h=4),
                   bc3(rs[:, 0:4], 4, 128), ALU.mult, [src, rs], [T2])
                tt("dve", dst, T2[:], gate[:], ALU.mult, [T2, gate], [wres])

            IDF16 = CONST[0:R_, C_IDF:C_IDF + R_]
            ONES16 = CONST[0:R_, C_ONES:C_ONES + 128]

            dma("sp", XS[:], xs_d, [], [XS])
            dma("sp", N0[:].rearrange("p h d -> p (h d)"), sn_d, [], [N0])
            dma("sp", SMs[0][:, 0:4], sm_d, [], [SMs[0]])
            dma("sp", oconv_d[:, 0:2, :], sconv_d[:, 1:3, :], [], [])
            ss, rs = SMs[8], SMs[9]
            act(XNs[:], XS[:], AF.Square, [XS], [XNs, ss], accum=ss[:, 0:1])
            rsqrt_small(rs[:, 0:1], ss[:, 0:1], 1.0 / D, [ss], [rs], rs[:, 1:2])
            ts("dve", XNs[:], XS[:], rs[:, 0:1], None, ALU.mult, None, [XS, rs], [XNs])
            for k in range(8):
                tr(TPB[:, k * 128:k * 128 + R_], XNs[:, k * 128:(k + 1) * 128], IDB[0:R_, 0:R_], [XNs, IDB], [TPB])
            tt("dve", HTs[:], TPB[:].rearrange("p (k t) -> p k t", k=8)[:, :, 0:R_], bc3(GPRE[:], 8, R_), ALU.mult,
               [TPB, GPRE], [HTs])

            for ch in range(12):
                bk = BK[ch % 2]
                c0 = ch * 128
                for k in range(8):
                    mm(bk[:, 0:R_], WIN[:, k, c0:c0 + 128], HTs[:, k, :], k == 0, k == 7, [HTs] + rwin(c0, c0 + 128), [bk])
                cp("act", XPT[:, ch, 3, :], bk[:, 0:R_], [bk], [XPT])
            for hh in range(8):
                bk = BK[hh % 2]
                c0 = W_MQK + hh * 64
                for k in range(8):
                    mm(bk[0:64, 0:R_], WIN[:, k, c0:c0 + 64], HTs[:, k, :], k == 0, k == 7, [HTs] + rwin(c0, c0 + 64), [bk])
                if hh < 4:
                    cp("act", MQs[:, hh, :], bk[0:64, 0:R_], [bk], [MQs])
                else:
                    act(MKs[:, hh - 4, :], bk[0:64, 0:R_], AF.Copy, [bk], [MKs], scale=0.125)
            for g in range(3):
                for j in range(3):
                    dma("sp", SCV[:], sconv_d[:, j, g * 512:(g + 1) * 512], [], [SCV])
                    for c4 in range(4):
                        o0 = (j * 4 + c4) * R_
                        tr(BK[2][:, o0:o0 + R_], SCV[:, c4 * 128:(c4 + 1) * 128], IDF16, [SCV, CONST], [BK[2]])
                cp("dve", XPT[:, 4 * g:4 * g + 4, 0:3, :],
                   BK[2][:, 0:12 * R_].rearrange("p (j c r) -> p c j r", j=3, c=4), [BK[2]], [XPT])
            for g in range(3):
                bk = BK[3 + g % 2]
                for k in range(8):
                    mm(bk[0:R_, :], HTs[:, k, :], WIN[:, k, g * 512:(g + 1) * 512], k == 0, k == 7,
                       [HTs] + rwin(g * 512, (g + 1) * 512), [bk])
                cp("act", SCV[:], bk[0:R_, :], [bk], [SCV])
                dma("sp", oconv_d[:, 2, g * 512:(g + 1) * 512], SCV[:], [SCV], [])
            tt("dve", ACs[:], XPT[:, :, 0, :], bc3(CW[:, 0, :], 12, R_), ALU.mult, [XPT, CW], [ACs])
            for j in range(1, 4):
                tt("dve", TM12[:], XPT[:, :, j, :], bc3(CW[:, j, :], 12, R_), ALU.mult, [XPT, CW], [TM12])
                tt("dve", ACs[:], ACs[:], TM12[:], ALU.add, [ACs, TM12], [ACs])
            QKFs = TM12
            act(QKFs[:, 0:8, :], ACs[:, 0:8, :], AF.Silu, [ACs], [QKFs])
            act(GVs[:], ACs[:, 8:12, :], AF.Silu, [ACs], [GVs])
            act(SQs[:], QKFs[:, 0:8, :], AF.Square, [QKFs], [SQs])
            mm(BK[2][:, 0:8 * R_], ONES, SQs[:].rearrange("p a b -> p (a b)"), True, True, [CONST, SQs], [BK[2]])
            ts("dve", SQs[:].rearrange("p a b -> p (a b)"), BK[2][:, 0:8 * R_], 1.0, EPS, ALU.mult, ALU.add, [BK[2]], [SQs])
            act(SQs[:], SQs[:], AF.Ln, [SQs], [SQs])
            act(SQs[:], SQs[:], AF.Exp, [SQs], [SQs], scale=-0.5)
            stt("dve", GQs[:], QKFs[:, 0:4, :], 128.0 ** -0.5, SQs[:, 0:4, :], ALU.mult, ALU.mult, [QKFs, SQs], [GQs])
            tt("dve", GKs[:], QKFs[:, 4:8, :], SQs[:, 4:8, :], ALU.mult, [QKFs, SQs], [GKs])

            def tok16(bk, c0, n):
                for k in range(8):
                    mm(bk[0:R_, 0:n], HTs[:, k, :], WIN[:, k, c0:c0 + n], k == 0, k == 7, [HTs] + rwin(c0, c0 + n), [bk])
            tok16(BK[0], W_GZ, 512)
            act(T1[:], BK[0][0:R_, :], AF.Silu, [BK[0]], [T1])
            tt("dve", ZSGs[:].rearrange("p (h e) -> p h e", h=4), T1[:].rearrange("p (h e) -> p h e", h=4),
               bcm(GNG[0:R_, :], 4), ALU.mult, [T1, GNG], [ZSGs])
            tok16(BK[1], W_MV, 512)
            cp("act", MVs[:].rearrange("p h e -> p (h e)"), BK[1][0:R_, :], [BK[1]], [MVs])
            tok16(BK[0], W_MO, 512)
            act(T1[:], BK[0][0:R_, :], AF.Exp, [BK[0]], [T1], scale=-1.0)
            ts("dve", T1[:], T1[:], 1.0, None, ALU.add, None, [T1], [T1])
            P.op("dve", lambda e: e.reciprocal(out=T1[:], in_=T1[:]), [T1], [T1])
            tt("dve", MOSGs[:].rearrange("p (h e) -> p h e", h=4), T1[:].rearrange("p (h e) -> p h e", h=4),
               bcm(MNG[0:R_, :], 4), ALU.mult, [T1, MNG], [MOSGs])
            tok16(BK[1], W_GATE, 16)
            cp("dve", GTs[:], BK[1][0:R_, 0:16], [BK[1]], [GTs])
            tt("dve", ARGs[:], GTs[:, 0:12], SGN[0:R_, :], ALU.mult, [GTs, SGN], [ARGs])
            tt("dve", ARGs[:], ARGs[:], BIA[0:R_, :], ALU.add, [ARGs, BIA], [ARGs])
            act(ARGs[:], ARGs[:], AF.Exp, [ARGs], [ARGs])
            act(ARGs[:], ARGs[:], AF.Ln, [ARGs], [ARGs], bias=1.0)
            tt("dve", GLs[:], ARGs[:], NEGC[0:R_, :], ALU.mult, [ARGs, NEGC], [GLs])
            act(BETAs[:], GLs[:, 0:4], AF.Exp, [GLs], [BETAs])
            tt("dve", IGs[:], GTs[:, 12:16], SMB[0:R_, 8:12], ALU.add, [GTs, SMB], [IGs])
            act(EGs[:], GLs[:, 4:8], AF.Exp, [GLs], [EGs])
            M0s, Bs, MTs, WPs, Ps, QKG, QKMs, QNs = SMs[0], SMs[1], SMs[2], SMs[3], SMs[4], SMs[5], SMs[6], SMs[7]
            tt("dve", Bs[:, 0:4], GLs[:, 8:12], M0s[:, 0:4], ALU.add, [GLs, M0s], [Bs])
            tt("dve", MTs[:, 0:4], Bs[:, 0:4], IGs[:], ALU.max, [Bs, IGs], [MTs])
            tt("dve", WPs[:, 0:4], Bs[:, 0:4], MTs[:, 0:4], ALU.subtract, [Bs, MTs], [WPs])
            act(WPs[:, 0:4], WPs[:, 0:4], AF.Exp, [WPs], [WPs])
            tt("dve", Ps[:, 0:4], IGs[:], MTs[:, 0:4], ALU.subtract, [IGs, MTs], [Ps])
            act(Ps[:, 0:4], Ps[:, 0:4], AF.Exp, [Ps], [Ps])
            dma("sp", om_d, MTs[:, 0:4], [MTs], [])

            for h in range(4):
                tr(BK[2][0:R_, h * 128:(h + 1) * 128], GQs[:, h, :], IDF, [GQs, CONST], [BK[2]])
                tr(BK[3][0:R_, h * 128:(h + 1) * 128], GKs[:, h, :], IDF, [GKs, CONST], [BK[3]])
                tr(BK[4][0:R_, h * 128:(h + 1) * 128], GVs[:, h, :], IDF, [GVs, CONST], [BK[4]])
                tr(BK[5][0:R_, h * 64:(h + 1) * 64], MQs[:, h, :], CONST[0:64, C_IDF:C_IDF + 64], [MQs, CONST], [BK[5]])
                tr(BK[5][0:R_, 256 + h * 64:256 + (h + 1) * 64], MKs[:, h, :], CONST[0:64, C_IDF:C_IDF + 64],
                   [MKs, CONST], [BK[5]])
            cp("act", Qt[:].rearrange("p h d -> p (h d)"), BK[2][0:R_, :], [BK[2]], [Qt])
            cp("dve", Kt[:].rearrange("p h d -> p (h d)"), BK[3][0:R_, :], [BK[3]], [Kt])
            cp("act", Vt[:].rearrange("p h d -> p (h d)"), BK[4][0:R_, :], [BK[4]], [Vt])
            cp("dve", MQt[:].rearrange("p h d -> p (h d)"), BK[5][0:R_, 0:256], [BK[5]], [MQt])
            cp("act", MKtt[:].rearrange("p h d -> p (h d)"), BK[5][0:R_, 256:512], [BK[5]], [MKtt])
            tt("dve", T1[:], Qt[:].rearrange("p h d -> p (h d)"), Kt[:].rearrange("p h d -> p (h d)"), ALU.mult,
               [Qt, Kt], [T1])
            red("dve", QKG[:, 0:4], T1[:].rearrange("p (h d) -> p h d", h=4), ALU.add, [T1], [QKG])
            tt("dve", T1[:, 0:256], MQt[:].rearrange("p h d -> p (h d)"), MKtt[:].rearrange("p h d -> p (h d)"),
               ALU.mult, [MQt, MKtt], [T1])
            red("dve", QKMs[:, 0:4], T1[:, 0:256].rearrange("p (h d) -> p h d", h=4), ALU.add, [T1], [QKMs])
            tt("dve", T1[:, 0:256], MQt[:].rearrange("p h d -> p (h d)"), N0[:].rearrange("p h d -> p (h d)"),
               ALU.mult, [MQt, N0], [T1])
            red("dve", QNs[:, 0:4], T1[:, 0:256].rearrange("p (h d) -> p h d", h=4), ALU.add, [T1], [QNs])
            tt("dve", PKt[:], MKtt[:], bc3(Ps[:, 0:4], 4, 64), ALU.mult, [MKtt, Ps], [PKt])
            tt("dve", N0[:], N0[:], bc3(WPs[:, 0:4], 4, 64), ALU.mult, [N0, WPs], [N0])
            tt("dve", N0[:], N0[:], PKt[:], ALU.add, [N0, PKt], [N0])
            dma("sp", on_d, N0[:].rearrange("p h d -> p (h d)"), [N0], [])

            tt("dve", DIAGI[:], bc3(IDF16, R_, R_), bcm(IDF16, R_), ALU.mult, [CONST], [DIAGI])
            mm(BK[2][:, 0:R_ * R_], ONES16, DIAGI[:].rearrange("p a b -> p (a b)"), True, True, [CONST, DIAGI], [BK[2]])
            cp("dve", M16[:].rearrange("p a b -> p (a b)"), BK[2][:, 0:R_ * R_], [BK[2]], [M16])
            tt("dve", DG4[:], bcm(EGs[:], R_), bc3(IDF16, R_, 4), ALU.mult, [EGs, CONST], [DG4])
            mm(BK[3][:, 0:64], ONES16, DG4[:].rearrange("p a b -> p (a b)"), True, True, [CONST, DG4], [BK[3]])
            cp("dve", EGBC[:], BK[3][:, 0:64], [BK[3]], [EGBC])
            tt("dve", DG4[:], bcm(WPs[:, 0:4], R_), bc3(IDF16, R_, 4), ALU.mult, [WPs, CONST], [DG4])
            mm(BK[3][:, 0:64], ONES16, DG4[:].rearrange("p a b -> p (a b)"), True, True, [CONST, DG4], [BK[3]])
            cp("dve", WPBC[:], BK[3][:, 0:64], [BK[3]], [WPBC])

            def ld(r, i):
                dma("sp", S0b[i][:], sS_d[r].rearrange("h d e -> d h e"), [], [S0b[i]])
                dma("sp", C0b[i][:], sC_d[r].rearrange("h d e -> d h e"), [], [C0b[i]])
            ld(0, 0)
            for r in range(R_):
                i = r % 2
                if r + 1 < R_:
                    ld(r + 1, 1 - i)
                tt("dve", KD[i][:], GKs[:], bcm(M16[:, r, :], 4), ALU.mult, [GKs, M16], [KD[i]])
                tt("pool", QD[i][:], GQs[:], bcm(M16[:, r, :], 4), ALU.mult, [GQs, M16], [QD[i]])
                tt("dve", QDm[i][:], MQs[:], bcm(M16[0:64, r, :], 4), ALU.mult, [MQs, M16], [QDm[i]])
                for h in range(4):
                    sl = slice(h * 128, (h + 1) * 128)
                    st_, sp_ = (r == 0 and h == 0), (r == R_ - 1 and h == 3)
                    mm(BK[2][0:R_, sl], KD[i][:, h, :], S0b[i][:, h, :], st_, sp_, [KD[i], S0b[i]], [BK[2]], skip=True)
                    mm(BK[3][0:R_, sl], QD[i][:, h, :], S0b[i][:, h, :], st_, sp_, [QD[i], S0b[i]], [BK[3]], skip=True)
                    mm(BK[5][0:R_, sl], QDm[i][:, h, :], C0b[i][:, h, :], st_, sp_, [QDm[i], C0b[i]], [BK[5]], skip=True)
            KSp, QSp, QCp = BK[2], BK[3], BK[5]
            tt("dve", Ug[:], KSp[0:R_, :].rearrange("p (h e) -> p h e", h=4), bc3(EGs[:], 4, 128), ALU.mult,
               [KSp, EGs], [Ug])
            tt("dve", Ug[:], Vt[:], Ug[:], ALU.subtract, [Vt, Ug], [Ug])
            tt("dve", Ug[:], Ug[:], bc3(BETAs[:], 4, 128), ALU.mult, [Ug, BETAs], [Ug])
            ld(0, 0)
            for r in range(R_):
                i = r % 2
                if r + 1 < R_:
                    ld(r + 1, 1 - i)
                ts("dve", KROW[:], Kt[:].rearrange("p h d -> p (h d)"), IDF16[:, r:r + 1], None, ALU.mult, None,
                   [Kt, CONST], [KROW])
                ts("dve", PKROW[:], PKt[:].rearrange("p h d -> p (h d)"), IDF16[:, r:r + 1], None, ALU.mult, None,
                   [PKt, CONST], [PKROW])
                for h in range(4):
                    mm(BK[4][:, h * 128:(h + 1) * 128], KROW[:, h * 128:(h + 1) * 128], Ug[:, h, :], True, True,
                       [KROW, Ug], [BK[4]])
                for h in range(4):
                    mm(BK[6][0:64, h * 128:(h + 1) * 128], PKROW[:, h * 64:(h + 1) * 64], MVs[:, h, :], True, True,
                       [PKROW, MVs], [BK[6]])
                for h in range(4):
                    stt("dve", S0b[i][:, h, :], S0b[i][:, h, :], EGBC[:, r * 4 + h:r * 4 + h + 1],
                        BK[4][:, h * 128:(h + 1) * 128], ALU.mult, ALU.add, [S0b[i], EGBC, BK[4]], [S0b[i]])
                    stt("dve", C0b[i][:, h, :], C0b[i][:, h, :], WPBC[0:64, r * 4 + h:r * 4 + h + 1],
                        BK[6][0:64, h * 128:(h + 1) * 128], ALU.mult, ALU.add, [C0b[i], WPBC, BK[6]], [C0b[i]])
                dma("sp", oS_d[r].rearrange("h d e -> d h e"), S0b[i][:], [S0b[i]], [])
                dma("sp", oC_d[r].rearrange("h d e -> d h e"), C0b[i][:], [C0b[i]], [])

            MIXs = XNs
            tt("dve", T1[:].rearrange("p (h e) -> p h e", h=4), QSp[0:R_, :].rearrange("p (h e) -> p h e", h=4),
               bc3(EGs[:], 4, 128), ALU.mult, [QSp, EGs], [T1])
            tt("dve", Ug[:], Ug[:], bc3(QKG[:, 0:4], 4, 128), ALU.mult, [Ug, QKG], [Ug])
            tt("dve", T1[:], T1[:], Ug[:].rearrange("p h e -> p (h e)"), ALU.add, [T1, Ug], [T1])
            hn_gate16(T1, ZSGs, MIXs[:, 0:512], MIXs)
            PQ = SMs[5]
            tt("dve", PQ[:, 4:8], Ps[:, 0:4], QKMs[:, 0:4], ALU.mult, [Ps, QKMs], [PQ])
            tt("dve", T1[:].rearrange("p (h e) -> p h e", h=4), QCp[0:R_, :].rearrange("p (h e) -> p h e", h=4),
               bc3(WPs[:, 0:4], 4, 128), ALU.mult, [QCp, WPs], [T1])
            tt("dve", Ug[:], MVs[:], bc3(PQ[:, 4:8], 4, 128), ALU.mult, [MVs, PQ], [Ug])
            tt("dve", T1[:], T1[:], Ug[:].rearrange("p h e -> p (h e)"), ALU.add, [T1, Ug], [T1])
            DENs, EMTs = SMs[6], SMs[7]
            tt("dve", DENs[:, 4:8], WPs[:, 0:4], QNs[:, 0:4], ALU.mult, [WPs, QNs], [DENs])
            tt("dve", DENs[:, 4:8], DENs[:, 4:8], PQ[:, 4:8], ALU.add, [DENs, PQ], [DENs])
            ts("dve", DENs[:, 0:4], DENs[:, 4:8], -1.0, None, ALU.mult, None, [DENs], [DENs])
            tt("dve", DENs[:, 4:8], DENs[:, 4:8], DENs[:, 0:4], ALU.max, [DENs], [DENs])
            act(EMTs[:, 4:8], MTs[:, 0:4], AF.Exp, [MTs], [EMTs], scale=-1.0)
            tt("dve", DENs[:, 4:8], DENs[:, 4:8], EMTs[:, 4:8], ALU.max, [DENs, EMTs], [DENs])
            P.op("dve", lambda e, d=DENs: e.reciprocal(out=d[:, 4:8], in_=d[:, 4:8]), [DENs], [DENs])
            tt("dve", T1[:].rearrange("p (h e) -> p h e", h=4), T1[:].rearrange("p (h e) -> p h e", h=4),
               bc3(DENs[:, 4:8], 4, 128), ALU.mult, [T1, DENs], [T1])
            hn_gate16(T1, MOSGs, MIXs[:, 512:1024], MIXs)
            for k in range(8):
                tr(TPB[:, k * 128:k * 128 + R_], MIXs[:, k * 128:(k + 1) * 128], IDB[0:R_, 0:R_], [MIXs, IDB], [TPB])
            cp("act", HTs[:], TPB[:].rearrange("p (k t) -> p k t", k=8)[:, :, 0:R_], [TPB], [HTs])
            for eh in range(2):
                for k in range(8):
                    mm(BK[eh][0:R_, :], HTs[:, k, :], WOUT[:, k, eh * 512:(eh + 1) * 512], k == 0, k == 7,
                       [HTs, WOUT], [BK[eh]])
            ss, rs = SMs[8], SMs[9]
            for eh in range(2):
                act(XNs[:, eh * 512:(eh + 1) * 512], BK[eh][0:R_, :], AF.Square, [BK[eh]], [XNs, ss],
                    accum=ss[:, eh:eh + 1])
            tt("dve", ss[:, 2:3], ss[:, 0:1], ss[:, 1:2], ALU.add, [ss], [ss])
            rsqrt_small(rs[:, 0:1], ss[:, 2:3], 1.0 / D, [ss], [rs], rs[:, 1:2])
            for eh in range(2):
                sl = slice(eh * 512, (eh + 1) * 512)
                stt("dve", T1[:], BK[eh][0:R_, :], rs[:, 0:1], GPM[0:R_, sl], ALU.mult, ALU.mult, [BK[eh], rs, GPM], [T1])
                tt("dve", XS[:, sl], XS[:, sl], T1[:], ALU.add, [XS, T1], [XS])
            for k in range(8):
                tr(BK[2][:, k * R_:(k + 1) * R_], XS[:, k * 128:(k + 1) * 128], IDF16, [XS, CONST], [BK[2]])
            cp("dve", XS1T[:].rearrange("p k r -> p (k r)"), BK[2][:, 0:8 * R_], [BK[2]], [XS1T])
            act(XNs[:], XS[:], AF.Square, [XS], [XNs, ss], accum=ss[:, 4:5])
            rsqrt_small(rs[:, 4:5], ss[:, 4:5], 1.0 / D, [ss], [rs], rs[:, 5:6])
            ts("dve", XNs[:], XS[:], rs[:, 4:5], None, ALU.mult, None, [XS, rs], [XNs])
            for k in range(8):
                tr(TPB[:, k * 128:k * 128 + R_], XNs[:, k * 128:(k + 1) * 128], IDB[0:R_, 0:R_], [XNs, IDB], [TPB])
            tt("dve", HNS[:], TPB[:].rearrange("p (k t) -> p k t", k=8)[:, :, 0:R_], bc3(GPREMLP[:], 8, R_), ALU.mult,
               [TPB, GPREMLP], [HNS])

            P.barrier(all_res_p1 + RX + [XS1T.r, HNS.r, IDF2.r, IDB.r, GPREMLP.r])
            p1b.close()

        with ExitStack() as p2:
            WUP = p2.enter_context(nc.sbuf_tensor("sb_WUP", [128, 8, DFF], BF16))
            WDN = p2.enter_context(nc.sbuf_tensor("sb_WDN", [128, 32, D], BF16))
            NWC = 8
            RWU = [Res("WUP%d" % i) for i in range(NWC)]
            RWD = [Res("WDN%d" % i) for i in range(NWC)]
            wup_v = wup_d.rearrange("(k p) c -> p k c", p=128)
            wdn_v = wdn_d.rearrange("(k p) c -> p k c", p=128)
            for i in range(NWC if KPH2 else 0):
                dma("pool", WUP[:, :, i * 512:(i + 1) * 512], wup_v[:, :, i * 512:(i + 1) * 512], [], [RWU[i]])
                dma("pool", WDN[:, i * 4:(i + 1) * 4, :], wdn_v[:, i * 4:(i + 1) * 4, :], [], [RWD[i]])
            XN2 = sb(p2, "XN2", [128, D], BF16)
            GPL = sb(p2, "GPL", [128, D])
            dma("sp", GPL[:], gpl_d.partition_broadcast(128), [], [GPL])
            HN = sb(p2, "HN", [128, 8, 256], BF16)
            UT = [sb(p2, "UT%d" % i, [128, 256], BF16) for i in range(2)]
            RL = [sb(p2, "RL%d" % i, [128, 256]) for i in range(2)]
            SS2 = sb(p2, "SS2", [128, 8])

            NB2 = NT // 2 if KPH2 else 0

            def xv_of(blk, j):
                return Xt[:, 2 * blk + j, :], RX[2 * blk + j]

            def prep(blk):
                for j in range(2):
                    xv, rx = xv_of(blk, j)
                    act(XN2[:], xv, AF.Square, [rx], [XN2, SS2], accum=SS2[:, 0:1])
                    rsqrt_small(SS2[:, 1:2], SS2[:, 0:1], 1.0 / D, [SS2], [SS2], SS2[:, 2:3])
                    ts("dve", XN2[:], xv, SS2[:, 1:2], None, ALU.mult, None, [rx, SS2], [XN2])
                    for k in range(8):
                        tr(TPB[:, k * 128:(k + 1) * 128], XN2[:, k * 128:(k + 1) * 128], IDB[:], [XN2, IDB], [TPB])
                    tt("dve", HN[:, :, j * 128:(j + 1) * 128], TPB[:].rearrange("p (k t) -> p k t", k=8),
                       bc3(GPREMLP[:], 8, 128), ALU.mult, [TPB, GPREMLP], [HN])

            def up(f, hn, n):
                bk = BK[f % 2]
                for k in range(8):
                    mm(bk[:, 0:n], WUP[:, k, f * 128:(f + 1) * 128], hn[:, k, 0:n], k == 0, k == 7,
                       [hn, RWU[f // 4]], [bk])
                rl, ut = RL[f % 2], UT[f % 2]
                act(rl[:, 0:n], bk[:, 0:n], AF.Relu, [bk], [rl])
                tt("dve", ut[:, 0:n], rl[:, 0:n], rl[:, 0:n], ALU.mult, [rl], [ut])

            def down(f, rows, ntl):
                ut = UT[f % 2]
                for j in range(ntl):
                    for eh in range(2):
                        ab = BK[2 + 2 * j + eh]
                        mm(ab[0:rows, :], ut[:, j * rows:(j + 1) * rows], WDN[:, f, eh * 512:(eh + 1) * 512],
                           f == 0, f == 31, [ut, RWD[f // 4]], [ab])

            def fin(blk):
                for j in range(2):
                    xv, rx = xv_of(blk, j)
                    for eh in range(2):
                        ab = BK[2 + 2 * j + eh]
                        act(XN2[:, eh * 512:(eh + 1) * 512], ab[:], AF.Square, [ab], [XN2, SS2],
                            accum=SS2[:, 3 + eh:4 + eh])
                    tt("dve", SS2[:, 5:6], SS2[:, 3:4], SS2[:, 4:5], ALU.add, [SS2], [SS2])
                    rsqrt_small(SS2[:, 6:7], SS2[:, 5:6], 1.0 / D, [SS2], [SS2], SS2[:, 7:8])
                    for eh in range(2):
                        ab = BK[2 + 2 * j + eh]
                        sl = slice(eh * 512, (eh + 1) * 512)
                        stt("dve", ab[:], ab[:], SS2[:, 6:7], GPL[:, sl], ALU.mult, ALU.mult, [ab, SS2, GPL], [ab])
                        tt("dve", xv[:, sl], xv[:, sl], ab[:], ALU.add, [rx, ab], [rx])
                    r0 = (2 * blk + j) * 128
                    dma("sp", y_d[r0:r0 + 128, :], xv, [rx], [])

            if NB2:
                prep(0)
            for blk in range(NB2):
                up(0, HN, 256)
                for f in range(32):
                    if f + 1 < 32:
                        up(f + 1, HN, 256)
                    elif blk + 1 < NB2:
                        prep(blk + 1)
                    down(f, 128, 2)
                fin(blk)

            R_ = NS
            if KPH2:
                up(0, HNS, R_)
                for f in range(32):
                    if f + 1 < 32:
                        up(f + 1, HNS, R_)
                    down(f, R_, 1)
            if KPH2:
                for eh in range(2):
                    act(XN2[0:R_, eh * 512:(eh + 1) * 512], BK[2 + eh][0:R_, :], AF.Square, [BK[2 + eh]], [XN2, SS2],
                        accum=SS2[0:R_, 3 + eh:4 + eh])
                tt("dve", SS2[0:R_, 5:6], SS2[0:R_, 3:4], SS2[0:R_, 4:5], ALU.add, [SS2], [SS2])
                rsqrt_small(SS2[0:R_, 6:7], SS2[0:R_, 5:6], 1.0 / D, [SS2], [SS2], SS2[0:R_, 7:8])
                YSB = XN2.t.bitcast(F32)
                for eh in range(2):
                    sl = slice(eh * 512, (eh + 1) * 512)
                    stt("dve", BK[2 + eh][0:R_, :], BK[2 + eh][0:R_, :], SS2[0:R_, 6:7], GPL[0:R_, sl], ALU.mult, ALU.mult,
                        [BK[2 + eh], SS2, GPL], [BK[2 + eh]])
                    for j in range(4):
                        tr(BK[4 + eh][0:R_, j * 128:(j + 1) * 128], XS1T[:, 4 * eh + j, :], IDF2[:], [XS1T, IDF2],
                           [BK[4 + eh]])
                    cp("act", YSB[0:R_, :], BK[4 + eh][0:R_, :], [BK[4 + eh]], [XN2])
                    tt("dve", YSB[0:R_, :], YSB[0:R_, :], BK[2 + eh][0:R_, :], ALU.add, [XN2, BK[2 + eh]], [XN2])
                    dma("sp", ys_d[:, sl], YSB[0:R_, :], [XN2], [])

        n_ins = P.finalize(top)
    return nc, n_ins


_CACHE = {}


def kernel(x_prompt, x_sample, state_gdn_conv, state_gdn_S, state_mlstm_C, state_mlstm_n, state_mlstm_m,
           norm_pre_mix, w_in, conv_w, a_log, dt_bias, gdn_norm_g, b_igate, b_fgate, mlstm_norm_g, w_out,
           norm_post_mix, norm_pre_mlp, w_up, w_down, norm_post_mlp):
    f = lambda a: np.ascontiguousarray(np.asarray(a, dtype=np.float32))
    if "nc" not in _CACHE:
        _CACHE["nc"] = build_program()
    nc, _ = _CACHE["nc"]
    consts = make_consts()
    small = np.concatenate([f(a_log)[0], f(dt_bias)[0], f(b_igate)[0], f(b_fgate)[0]])[None, :]
    shared = {
        "w_in": f(w_in)[0], "w_out": f(w_out)[0], "w_up": f(w_up)[0], "w_down": f(w_down)[0],
        "consts": consts,
        "gpre_fm": f(f(norm_pre_mix)[0].reshape(8, 128).T),
        "gpremlp_fm": f(f(norm_pre_mlp)[0].reshape(8, 128).T),
        "cw_fm": f(f(conv_w)[0].reshape(4, 12, 128).transpose(2, 0, 1).reshape(128, 48)),
        "gpostmix": f(norm_post_mix)[0][None, :], "gpostmlp": f(norm_post_mlp)[0][None, :],
        "small": f(small), "gdn_norm_g": f(gdn_norm_g)[0][None, :], "mlstm_norm_g": f(mlstm_norm_g)[0][None, :],
    }
    xp, xs = f(x_prompt), f(x_sample)
    in_maps = []
    for c in range(NCORES):
        r = slice(c * NS, (c + 1) * NS)
        m = dict(shared)
        m.update({
            "x": xp[c], "xs": xs[r, 0, :],
            "sconv": f(state_gdn_conv)[0, r], "sS": f(state_gdn_S)[0, r], "sC": f(state_mlstm_C)[0, r],
            "sn": f(state_mlstm_n)[0, r].reshape(NS, 256), "sm": f(state_mlstm_m)[0, r],
        })
        in_maps.append(m)
    res = run_bass_kernel_spmd(nc, in_maps, core_ids=list(range(NCORES)))
    R = res.results
    g = lambda k: np.stack([np.asarray(R[c][k], dtype=np.float32) for c in range(NCORES)])
    gc = lambda k: np.concatenate([np.asarray(R[c][k], dtype=np.float32) for c in range(NCORES)], axis=0)
    y_prompt = g("y")
    y_sample = gc("ys")[:, None, :]
    p_conv = g("pconv")[None]
    p_S = g("pS")[None]
    p_C = g("pC")[None]
    p_n = g("pn")[None]
    p_m = g("pm").reshape(NCORES, 4)[None]
    s_conv = gc("oconv")[None]
    s_S = gc("oS")[None]
    s_C = gc("oC")[None]
    s_n = gc("on").reshape(NCORES * NS, 4, 64)[None]
    s_m = gc("om")[None]
    return (y_prompt, y_sample, p_conv, p_S, p_C, p_n, p_m, s_conv, s_S, s_C, s_n, s_m)
```

```python
from contextlib import ExitStack
import numpy as np
import concourse.bass as bass
import concourse.mybir as mybir
from concourse.bass_utils import run_bass_kernel_spmd

F32 = mybir.dt.float32
BF16 = mybir.dt.bfloat16
ALU = mybir.AluOpType
AF = mybir.ActivationFunctionType
AX = mybir.AxisListType

NCORES = 8
T = 2048
NT = T // 128
D = 1024
DFF = 4096
NS = 16
EPS = 1e-6
NEG = -30000.0
import os
KNT = int(os.environ.get('KNT', NT))
KPH2 = int(os.environ.get('KPH2', 1))
KSTAGE = int(os.environ.get('KSTAGE', 99))
KSUB = int(os.environ.get('KSUB', 99))
KRATIO = [int(v) for v in os.environ.get('KRATIO', '3,2,1').split(',')]
KSCHED = int(os.environ.get('KSCHED', 0))
KSEG = int(os.environ.get('KSEG', 7))
KLO = int(os.environ.get('KLO', 0))
KHI = int(os.environ.get('KHI', 99))


class Res:
    __slots__ = ("name", "w", "rd")

    def __init__(self, name):
        self.name = name
        self.w = None
        self.rd = []


class Op:
    __slots__ = ("eng", "fn", "reads", "writes", "dma", "deps", "signal", "cnt", "sem", "waits", "cost", "tab")

    def __init__(self, eng, fn, reads, writes, dma, cost=0.4, tab=None):
        self.cost = cost
        self.tab = tab
        self.eng = eng
        self.fn = fn
        self.reads = reads
        self.writes = writes
        self.dma = dma
        self.deps = []
        self.signal = False
        self.cnt = 0
        self.sem = None
        self.waits = []


def _res(lst):
    out = []
    for x in lst:
        if x is None:
            continue
        if isinstance(x, Res):
            out.append(x)
        elif isinstance(x, (list, tuple)):
            out.extend(_res(x))
        else:
            out.append(x.r)
    return out


class Prog:
    ENGS = ("pe", "act", "dve", "pool", "sp")

    def __init__(self, nc, n_dma_sems=56):
        self.nc = nc
        self.ops = []
        self.n_dma_sems = n_dma_sems
        self.n_sw_sems = 8
        self.fence = Res("fence")
        self.marks = []
        self.engobj = {"pe": nc.tensor, "act": nc.scalar, "dve": nc.vector,
                       "pool": nc.gpsimd, "sp": nc.sync}

    def op(self, eng, fn, r=(), w=(), cost=0.4, tab=None):
        self.ops.append(Op(eng, fn, _res(r) + ([self.fence] if KSCHED else []), _res(w), False, cost, tab))

    def dma(self, eng, fn, r=(), w=(), cost=2.5):
        self.ops.append(Op(eng, fn, _res(r) + ([self.fence] if KSCHED else []), _res(w), True, cost))

    def barrier(self, allres):
        for e in ("pe", "act", "dve", "pool", "sp"):
            self.ops.append(Op(e, (lambda en: en.nop(nofuse=True)), [], _res(allres) + [self.fence], False, 0.1))

    def _schedule(self):
        ops = self.ops
        n = len(ops)
        LAT = 0.2
        succ = [[] for _ in range(n)]
        for i, o in enumerate(ops):
            for j in o.deps:
                succ[j].append(i)
        prio = [0.0] * n
        for i in range(n - 1, -1, -1):
            m = 0.0
            for k in succ[i]:
                if prio[k] + LAT > m:
                    m = prio[k] + LAT
            prio[i] = ops[i].cost + m
        if KSCHED == 2:
            prio = [float(n - i) for i in range(n)]
        ndep = [len(o.deps) for o in ops]
        ready = [0.0] * n
        finish = [0.0] * n
        avail = {e: [] for e in self.ENGS}
        for i, o in enumerate(ops):
            if ndep[i] == 0:
                avail[o.eng].append(i)
        free = {e: 0.0 for e in self.ENGS}
        lasttab = None
        order = []
        start = [0.0] * n
        done = 0
        while done < n:
            best_e, best_i, best_t = None, None, None
            for e in self.ENGS:
                av = avail[e]
                if not av:
                    continue
                fe = free[e]
                cand = None
                cs = None
                tmin_i, tmin = None, None
                for i in av:
                    r = ready[i]
                    if tmin is None or r < tmin or (r == tmin and i < tmin_i):
                        tmin, tmin_i = r, i
                    if r <= fe + 1e-9:
                        sc = prio[i]
                        if e == "act" and ops[i].tab is not None and lasttab is not None and ops[i].tab != lasttab:
                            sc -= 4.0
                        if cs is None or sc > cs or (sc == cs and i < cand):
                            cand, cs = i, sc
                if cand is None:
                    cand = tmin_i
                t0 = max(fe, ready[cand])
                if best_t is None or t0 < best_t:
                    best_e, best_i, best_t = e, cand, t0
            i = best_i
            o = ops[i]
            avail[best_e].remove(i)
            c = o.cost
            if best_e == "act" and o.tab is not None:
                if lasttab is not None and o.tab != lasttab:
                    c += 1.3
                lasttab = o.tab
            start[i] = best_t
            if o.dma:
                free[best_e] = best_t + 0.1
                finish[i] = best_t + c
            else:
                free[best_e] = best_t + c
                finish[i] = best_t + c
            order.append(i)
            done += 1
            for k in succ[i]:
                ndep[k] -= 1
                if finish[i] + LAT > ready[k]:
                    ready[k] = finish[i] + LAT
                if ndep[k] == 0:
                    avail[ops[k].eng].append(k)
        newpos = {old: new for new, old in enumerate(order)}
        newops = [ops[i] for i in order]
        for o in newops:
            o.deps = sorted(newpos[j] for j in o.deps)
        self.ops = newops
        self.est_us = max(finish) if n else 0.0

    def finalize(self, stack):
        nc = self.nc
        ops = self.ops
        for i, o in enumerate(ops):
            deps = set()
            for r in o.reads:
                if r.w is not None:
                    deps.add(r.w)
            for r in o.writes:
                if r.w is not None:
                    deps.add(r.w)
                for j in r.rd:
                    deps.add(j)
            deps.discard(i)
            o.deps = sorted(deps)
            for r in o.reads:
                r.rd.append(i)
            for r in o.writes:
                r.w = i
                r.rd = []
        if KSCHED:
            seg = 0
            last = {}
            nbar = 0
            for i, o in enumerate(ops):
                if o.cost == 0.1 and not o.dma and o.writes and o.writes[-1] is self.fence:
                    nbar += 1
                    if nbar % 5 == 1:
                        seg += 1
                        last = {}
                free_ = (KSEG >> min(seg, 2)) & 1
                if seg == 0 and self.marks:
                    lo = self.marks[min(KLO, len(self.marks) - 1)]
                    hi = self.marks[min(KHI, len(self.marks) - 1)] if KHI < len(self.marks) else 10 ** 9
                    free_ = free_ and (lo <= i < hi)
                if not free_:
                    if o.eng in last and last[o.eng] not in o.deps:
                        o.deps = sorted(o.deps + [last[o.eng]])
                    last[o.eng] = i
            self._schedule()
            ops = self.ops
        dma_slot_last = [None] * self.n_dma_sems
        dma_i = 0
        sw_i = 0
        for i, o in enumerate(ops):
            if o.dma:
                if o.eng == "pool":
                    slot = sw_i % self.n_sw_sems
                    sw_i += 1
                else:
                    slot = self.n_sw_sems + dma_i % (self.n_dma_sems - self.n_sw_sems)
                    dma_i += 1
                o.sem = slot
                if dma_slot_last[slot] is not None and dma_slot_last[slot] not in o.deps:
                    o.deps = sorted(o.deps + [dma_slot_last[slot]])
                dma_slot_last[slot] = i
            for j in o.deps:
                pj = ops[j]
                if pj.dma:
                    continue
                if pj.eng == "pe" and o.eng == "pe" and not o.dma:
                    continue
                pj.signal = True
        cnt = {e: 0 for e in self.ENGS}
        dcnt = [0] * self.n_dma_sems
        for o in ops:
            if o.dma:
                dcnt[o.sem] += 16
                o.cnt = dcnt[o.sem]
            elif o.signal:
                cnt[o.eng] += 1
                o.cnt = cnt[o.eng]
        seen = {e: {} for e in self.ENGS}
        for o in ops:
            need = {}
            for j in o.deps:
                pj = ops[j]
                if pj.dma:
                    key = ("d", pj.sem)
                else:
                    if pj.eng == "pe" and o.eng == "pe" and not o.dma:
                        continue
                    key = ("e", pj.eng)
                if pj.cnt > need.get(key, 0):
                    need[key] = pj.cnt
            s = seen[o.eng]
            for key, v in need.items():
                if s.get(key, 0) >= v:
                    continue
                s[key] = v
                o.waits.append((key, v))
        final_waits = [(("d", k), dcnt[k]) for k in range(self.n_dma_sems) if dcnt[k] > 0]
        final_waits += [(("e", e), cnt[e]) for e in self.ENGS if cnt[e] > 0 and e != "sp"]
        esem = {e: stack.enter_context(nc.semaphore("s_" + e)) for e in self.ENGS}
        dsem = [stack.enter_context(nc.semaphore("d_%d" % k)) for k in range(self.n_dma_sems)]

        def semof(key):
            return dsem[key[1]] if key[0] == "d" else esem[key[1]]

        n_ins = 0
        for o in ops:
            e = self.engobj[o.eng]
            for key, v in o.waits:
                e.wait_ge(semof(key), v)
                n_ins += 1
            ins = o.fn(e)
            n_ins += 1
            if o.dma:
                ins.then_inc(dsem[o.sem], 16)
            elif o.signal:
                ins.then_inc(esem[o.eng], 1)
        sp = self.engobj["sp"]
        for key, v in final_waits:
            if seen["sp"].get(key, 0) >= v:
                continue
            sp.wait_ge(semof(key), v)
        return n_ins


class View:
    __slots__ = ("t", "r")

    def __init__(self, ap, r):
        self.t = ap
        self.r = r

    def __getitem__(self, k):
        return self.t[k]


class Tl:
    __slots__ = ("t", "r")

    def __init__(self, t, name):
        self.t = t
        self.r = Res(name)

    def __getitem__(self, k):
        return self.t[k]


C_IDF, C_TRI, C_ONES, C_SEL127, C_MST, C_MIT, C_MI, C_SELR = (
    0, 128, 256, 384, 512, 640, 768, 896)
NCONST = 1408


def make_consts():
    c = np.zeros((128, NCONST), np.float32)
    s = np.arange(128)[:, None]
    f = np.arange(128)[None, :]
    c[:, C_IDF:C_IDF + 128] = (s == f)
    c[:, C_TRI:C_TRI + 128] = (s <= f)
    c[:, C_ONES:C_ONES + 128] = 1.0
    c[:, C_SEL127:C_SEL127 + 128] = (s == 127)
    mst = np.where(s < f, 0.0, NEG)
    mit = np.where(s <= f, 0.0, NEG)
    mi = np.where(f <= s, 0.0, NEG)
    c[:, C_MST:C_MST + 128] = mst
    c[:, C_MIT:C_MIT + 128] = mit
    c[:, C_MI:C_MI + 128] = mi
    for h in range(4):
        c[h, C_SELR + h * 128:C_SELR + (h + 1) * 128] = 1.0
    return c


W_QKV, W_MQK, W_GZ, W_MV, W_MO, W_GATE = 0, 1536, 2048, 2560, 3072, 3584
WIN_MOVES = [(0, 0, 1536), (1536, 2056, 512), (2048, 1536, 512), (2560, 2568, 512),
             (3072, 3080, 512), (3584, 2048, 8), (3592, 3596, 4), (3596, 3592, 4)]


def build_program():
    nc = bass.Bass("TRN2", target_bir_lowering=False)
    P = Prog(nc)

    def din(name, shape):
        return nc.dram_tensor(name, list(shape), F32, kind="ExternalInput").ap()

    def dout(name, shape):
        return nc.dram_tensor(name, list(shape), F32, kind="ExternalOutput").ap()

    x_d = din("x", [T, D])
    xs_d = din("xs", [NS, D])
    sconv_d = din("sconv", [NS, 3, 1536])
    sS_d = din("sS", [NS, 4, 128, 128])
    sC_d = din("sC", [NS, 4, 64, 128])
    sn_d = din("sn", [NS, 256])
    sm_d = din("sm", [NS, 4])
    win_d = din("w_in", [D, 3600])
    wout_d = din("w_out", [D, D])
    wup_d = din("w_up", [D, DFF])
    wdn_d = din("w_down", [DFF, D])
    consts_d = din("consts", [128, NCONST])
    gpre_d = din("gpre_fm", [128, 8])
    gpremlp_d = din("gpremlp_fm", [128, 8])
    cw_d = din("cw_fm", [128, 48])
    gpm_d = din("gpostmix", [1, D])
    gpl_d = din("gpostmlp", [1, D])
    small_d = din("small", [1, 16])
    gng_d = din("gdn_norm_g", [1, 128])
    mng_d = din("mlstm_norm_g", [1, 128])

    y_d = dout("y", [T, D])
    ys_d = dout("ys", [NS, D])
    pconv_d = dout("pconv", [3, 1536])
    pS_d = dout("pS", [4, 128, 128])
    pC_d = dout("pC", [4, 64, 128])
    pn_d = dout("pn", [4, 64])
    pm_d = dout("pm", [1, 4])
    oconv_d = dout("oconv", [NS, 3, 1536])
    oS_d = dout("oS", [NS, 4, 128, 128])
    oC_d = dout("oC", [NS, 4, 64, 128])
    on_d = dout("on", [NS, 256])
    om_d = dout("om", [NS, 4])

    def fsz(ap):
        try:
            return int(ap.free_size())
        except Exception:
            return 128

    def ecost(eng, ap):
        n = fsz(ap)
        if eng == "act":
            return 0.23 + n / 1200.0
        if eng == "pool":
            return 0.12 + n / 480.0
        return 0.08 + n / 960.0

    def mm(out, lhsT, rhs, start, stop, r, w, skip=False):
        passes = 4 if lhsT.dtype == F32 else 1
        c = 0.035 + passes * fsz(out) / 2400.0
        if skip:
            P.op("pe", lambda e, o=out, l=lhsT, rr=rhs, s=start, t=stop:
                 e.matmul(o, lhsT=l, rhs=rr, start=s, stop=t, skip_group_check=True), r, w, cost=c)
        else:
            P.op("pe", lambda e, o=out, l=lhsT, rr=rhs, s=start, t=stop:
                 e.matmul(o, lhsT=l, rhs=rr, start=s, stop=t), r, w, cost=c)

    def tr(out, in_, ident, r, w):
        P.op("pe", lambda e, o=out, i=in_, d=ident: e.transpose(o, i, d), r, w, cost=0.04 + fsz(out) / 2400.0)

    def tt(eng, out, in0, in1, op, r, w):
        P.op(eng, lambda e, o=out, a=in0, b=in1, p=op: e.tensor_tensor(out=o, in0=a, in1=b, op=p), r, w,
             cost=ecost(eng, out))

    def ts(eng, out, in0, s1, s2, op0, op1, r, w, accum=None):
        c = ecost(eng, out)
        if op1 is None:
            P.op(eng, lambda e, o=out, a=in0, x=s1, p0=op0:
                 e.tensor_scalar(out=o, in0=a, scalar1=x, scalar2=None, op0=p0), r, w, cost=c)
        elif accum is None:
            P.op(eng, lambda e, o=out, a=in0, x=s1, y=s2, p0=op0, p1=op1:
                 e.tensor_scalar(out=o, in0=a, scalar1=x, scalar2=y, op0=p0, op1=p1), r, w, cost=c)
        else:
            P.op(eng, lambda e, o=out, a=in0, x=s1, y=s2, p0=op0, p1=op1, ac=accum:
                 e.tensor_scalar(out=o, in0=a, scalar1=x, scalar2=y, op0=p0, op1=p1, accum_out=ac), r, w, cost=c)

    def stt(eng, out, in0, scalar, in1, op0, op1, r, w):
        P.op(eng, lambda e, o=out, a=in0, s=scalar, b=in1, p0=op0, p1=op1:
             e.scalar_tensor_tensor(out=o, in0=a, scalar=s, in1=b, op0=p0, op1=p1), r, w, cost=ecost(eng, out))

    def act(out, in_, func, r, w, bias=None, scale=1.0, accum=None):
        kw = {}
        if bias is not None:
            kw["bias"] = bias
        if accum is not None:
            kw["accum_out"] = accum
        tab = None
        if func in (AF.Silu, AF.Sigmoid):
            tab = "S"
        elif func in (AF.Exp, AF.Ln):
            tab = "E"
        P.op("act", lambda e, o=out, i=in_, f=func, s=scale, k=kw:
             e.activation(out=o, in_=i, func=f, scale=s, **k), r, w, cost=ecost("act", out), tab=tab)

    def cp(eng, out, in_, r, w):
        if eng == "act":
            act(out, in_, AF.Copy, r, w)
        else:
            P.op(eng, lambda e, o=out, i=in_: e.tensor_copy(out=o, in_=i), r, w, cost=ecost(eng, out))

    def red(eng, out, in_, op, r, w):
        P.op(eng, lambda e, o=out, i=in_, p=op: e.tensor_reduce(out=o, in_=i, axis=AX.X, op=p), r, w,
             cost=ecost(eng, in_))

    def memset(eng, ap, val, w):
        P.op(eng, lambda e, a=ap, v=val: e.memset(a, v), [], w, cost=ecost(eng, ap))

    def dma(q, out, in_, r, w, slow=False):
        if slow:
            P.dma(q, lambda e, o=out, i=in_: e.dma_start(out=o, in_=i, allow_slow_non_contiguous=True), r, w)
        else:
            P.dma(q, lambda e, o=out, i=in_: e.dma_start(out=o, in_=i), r, w)

    def rsqrt_small(out, in_, scale, r_, w_, tmp):
        ts("dve", tmp, in_, scale, EPS, ALU.mult, ALU.add, r_, [w_[0]])
        act(tmp, tmp, AF.Ln, [w_[0]], [w_[0]])
        act(out, tmp, AF.Exp, [w_[0]], w_, scale=-0.5)

    def bc3(ap, n_mid, n_in):
        return ap.unsqueeze(2).to_broadcast([ap.shape[0], n_mid, n_in])

    def bcm(ap, n_mid):
        return ap.unsqueeze(1).to_broadcast([ap.shape[0], n_mid, ap.shape[1]])

    with ExitStack() as top:
        def sb(stack, name, shape, dt=F32):
            return Tl(stack.enter_context(nc.sbuf_tensor("sb_" + name, list(shape), dt)), name)

        def ps(stack, name, shape, dt=F32):
            return Tl(stack.enter_context(nc.psum_tensor("ps_" + name, list(shape), dt)), name)

        Xt = top.enter_context(nc.sbuf_tensor("sb_X", [128, NT, D], F32))
        RX = [Res("X%d" % t) for t in range(NT)]
        XS1T = sb(top, "XS1T", [128, 8, NS])
        HNS = sb(top, "HNS", [128, 8, NS], BF16)
        IDF2 = sb(top, "IDF2", [128, 128])
        IDB = sb(top, "IDB", [128, 128], BF16)
        GPREMLP = sb(top, "GPREMLP", [128, 8])
        BK = [ps(top, "B%d" % i, [128, 512]) for i in range(7)]
        TPB = ps(top, "TPB", [128, 1024], BF16)

        dma("sp", GPREMLP[:], gpremlp_d, [], [GPREMLP])

        all_res_p1 = []

        with ExitStack() as p1:
            cur = [p1]

            def s1(name, shape, dt=F32):
                tl = sb(cur[0], name, shape, dt)
                all_res_p1.append(tl.r)
                return tl

            WIN = p1.enter_context(nc.sbuf_tensor("sb_WIN", [128, 8, 3600], BF16))
            RWIN = [Res("WIN%d" % i) for i in range(len(WIN_MOVES))]
            WOUT = s1("WOUT", [128, 8, D], BF16)
            CONST = s1("CONST", [128, NCONST])
            GPRE = s1("GPRE", [128, 8])
            CW = s1("CW", [128, 4, 12])
            GPM = s1("GPM", [128, D])
            SMB = s1("SMB", [128, 16])
            GNG = s1("GNG", [128, 128])
            MNG = s1("MNG", [128, 128])
            all_res_p1.extend(RWIN)

            win_v = win_d.rearrange("(k p) c -> p k c", p=128)
            for i, (dst, src, n) in enumerate(WIN_MOVES):
                dma("pool", WIN[:, :, dst:dst + n], win_v[:, :, src:src + n], [], [RWIN[i]])
            dma("pool", WOUT[:], wout_d.rearrange("(k p) c -> p k c", p=128), [], [WOUT])
            dma("sp", CONST[:], consts_d, [], [CONST])
            dma("sp", GPRE[:], gpre_d, [], [GPRE])
            dma("sp", CW[:], cw_d.rearrange("p (j c) -> p j c", j=4), [], [CW])
            dma("sp", GPM[:], gpm_d.partition_broadcast(128), [], [GPM])
            dma("sp", SMB[:], small_d.partition_broadcast(128), [], [SMB])
            dma("sp", GNG[:], gng_d.partition_broadcast(128), [], [GNG])
            dma("sp", MNG[:], mng_d.partition_broadcast(128), [], [MNG])

            IDF = CONST[:, C_IDF:C_IDF + 128]
            TRI = CONST[:, C_TRI:C_TRI + 128]
            ONES = CONST[:, C_ONES:C_ONES + 128]
            SEL127 = CONST[:, C_SEL127:C_SEL127 + 128]
            MST = CONST[:, C_MST:C_MST + 128]
            MIT = CONST[:, C_MIT:C_MIT + 128]
            MI = CONST[:, C_MI:C_MI + 128]
            SELR = CONST[0:4, C_SELR:C_SELR + 512]
            ONES4 = CONST[0:4, C_ONES:C_ONES + 128]

            def rwin(c0, c1):
                out = []
                for i, (dst, src, n) in enumerate(WIN_MOVES):
                    if dst < c1 and c0 < dst + n:
                        out.append(RWIN[i])
                return out

            cp("dve", IDB[:], IDF, [CONST], [IDB])
            cp("pool", IDF2[:], IDF, [CONST], [IDF2])

            NEGC = s1("NEGC", [128, 12])
            SGN = s1("SGN", [128, 12])
            BIA = s1("BIA", [128, 12])
            memset("pool", NEGC[:], -1.0, [NEGC])
            act(NEGC[:, 4:8], SMB[:, 0:4], AF.Exp, [SMB, NEGC], [NEGC])
            ts("dve", NEGC[:, 4:8], NEGC[:, 4:8], -1.0, None, ALU.mult, None, [NEGC], [NEGC])
            memset("pool", SGN[:], -1.0, [SGN])
            memset("pool", SGN[:, 4:8], 1.0, [SGN])
            memset("pool", BIA[:], 0.0, [BIA])
            cp("dve", BIA[:, 4:8], SMB[:, 4:8], [SMB, BIA], [BIA])
            ts("dve", BIA[:, 8:12], SMB[:, 12:16], -1.0, None, ALU.mult, None, [SMB, BIA], [BIA])

            p1a = ExitStack()
            cur[0] = p1a
            XN = s1("XN", [128, D], BF16)
            HT = s1("HT", [128, 8, 128], BF16)
            PCC = [s1("PCC%d" % i, [128, 131]) for i in range(3)]
            ACC = [s1("ACC%d" % i, [128, 128]) for i in range(3)]
            SA4 = [s1("SA4_%d" % i, [128, 8]) for i in range(2)]
            TAIL = s1("TAIL", [128, 12, 3])
            QKF = s1("QKF", [128, 8, 128])
            GQT = s1("GQT", [128, 4, 128], BF16)
            GKT = s1("GKT", [128, 4, 128], BF16)
            GVT = s1("GVT", [128, 4, 128], BF16)
            MQT = s1("MQT", [64, 4, 128], BF16)
            MKT = s1("MKT", [64, 4, 128], BF16)
            ZSG = s1("ZSG", [128, 512])
            MOSG = s1("MOSG", [128, 512])
            MVt = s1("MVt", [128, 4, 128], BF16)
            GT = s1("GT", [128, 16])
            ARG = s1("ARG", [128, 12])
            GL = s1("GL", [128, 12])
            BETA = s1("BETA", [128, 4])
            IG = s1("IG", [128, 4])
            GF = s1("GF", [128, 8])
            COLS = s1("COLS", [128, 20])
            RT = s1("RT", [4, 5, 128])
            BD = s1("BD", [4, 4, 128])
            QKD = s1("QKD", [128, 4, 128], BF16)
            MB = [s1("MB%d" % i, [128, 512]) for i in range(2)]
            LB = [s1("LB%d" % i, [128, 512]) for i in range(2)]
            RR = s1("RR", [128, 512])
            BINV = s1("BINV", [128, 4, 128], BF16)
            GKt = s1("GKt", [128, 4, 128], BF16)
            XK = s1("XK", [128, 4, 128], BF16)
            KEND = s1("KEND", [128, 4, 128], BF16)
            GVb = s1("GVb", [128, 4, 128], BF16)
            SC4 = [s1("SC4_%d" % i, [128, 8]) for i in range(6)]
            LASTG = s1("LASTG", [128, 8])
            WKT = s1("WKT", [128, 4, 128], BF16)
            S32 = s1("S32", [128, 4, 128])
            SBh = s1("SBh", [128, 4, 128], BF16)
            U = s1("U", [128, 4, 128], BF16)
            TMP = [s1("TMP%d" % i, [128, 512]) for i in range(2)]
            WV = LB[1]
            MIX = View(MB[0].t.bitcast(BF16)[:, 0:1024], MB[0].r)
            MIXT = View(RR.t.bitcast(BF16)[:, 0:1024].rearrange("p (k t) -> p k t", k=8), RR.r)
            PK = s1("PK", [128, 4, 64], BF16)
            EXPQ = TMP[1]
            EXPA = MB[0]
            MKt = s1("MKt", [128, 4, 64], BF16)
            PQKb = s1("PQKb", [128, 4, 128], BF16)
            PQKT = s1("PQKT", [128, 4, 128], BF16)
            SG4 = [s1("SG4_%d" % i, [128, 8]) for i in range(6)]
            C32 = s1("C32", [64, 4, 128])
            CBh = s1("CBh", [64, 4, 128], BF16)
            N32 = s1("N32", [64, 4])
            NBh = s1("NBh", [64, 4, 2], BF16)
            MBC = s1("MBC", [128, 4])
            DMAX = s1("DMAX", [128, 4])
            T12 = s1("T12", [128, 12])
            WLC = s1("WLC", [64, 4])
            ONEB = s1("ONEB", [128, 2], BF16)
            for b in BK:
                all_res_p1.append(b.r)
            all_res_p1.append(TPB.r)

            memset("pool", TAIL[:], 0.0, [TAIL])
            memset("pool", S32[:], 0.0, [S32])
            memset("pool", SBh[:], 0.0, [SBh])
            memset("pool", C32[:], 0.0, [C32])
            memset("pool", CBh[:], 0.0, [CBh])
            memset("pool", N32[:], 0.0, [N32])
            memset("pool", NBh[:], 0.0, [NBh])
            memset("pool", MBC[:], 0.0, [MBC])
            memset("pool", ONEB[:], 1.0, [ONEB])

            def headnorm_gate(src, gate, dst, t0, ss, rs):
                tt("pool", t0[:], src[:], src[:], ALU.mult, [src], [t0])
                red("dve", ss[:, 0:4], t0[:].rearrange("p (h e) -> p h e", h=4), ALU.add, [t0], [ss])
                rsqrt_small(rs[:, 0:4], ss[:, 0:4], 1.0 / 128, [ss], [rs], rs[:, 4:8])
                tt("dve", t0[:].rearrange("p (h e) -> p h e", h=4), src[:].rearrange("p (h e) -> p h e", h=4),
                   bc3(rs[:, 0:4], 4, 128), ALU.mult, [src, rs], [t0])
                tt("dve", dst, t0[:], gate[:], ALU.mult, [t0, gate], [MIX])

            def gen_H(th):
                for k in range(8):
                    tr(TPB[:, k * 128:(k + 1) * 128], MIX[:, k * 128:(k + 1) * 128], IDB[:], [MIX, IDB], [TPB])
                yield
                cp("act", MIXT[:].rearrange("p k t -> p (k t)"), TPB[:], [TPB], [MIXT])
                yield
                for eh in range(2):
                    for k in range(8):
                        mm(BK[4 + eh][:], MIXT[:, k, :], WOUT[:, k, eh * 512:(eh + 1) * 512], k == 0, k == 7,
                           [MIXT, WOUT], [BK[4 + eh]])
                    yield
                ss, rs = SC4[0], SC4[1]
                for eh in range(2):
                    act(MIX[:, eh * 512:(eh + 1) * 512], BK[4 + eh][:], AF.Square, [BK[4 + eh]], [MIX, ss],
                        accum=ss[:, eh:eh + 1])
                    yield
                tt("dve", ss[:, 2:3], ss[:, 0:1], ss[:, 1:2], ALU.add, [ss], [ss])
                rsqrt_small(rs[:, 0:1], ss[:, 2:3], 1.0 / D, [ss], [rs], rs[:, 1:2])
                yield
                for eh in range(2):
                    sl = slice(eh * 512, (eh + 1) * 512)
                    stt("dve", BK[4 + eh][:], BK[4 + eh][:], rs[:, 0:1], GPM[:, sl], ALU.mult, ALU.mult,
                        [BK[4 + eh], rs, GPM], [BK[4 + eh]])
                    yield
                    tt("dve", Xt[:, th, sl], Xt[:, th, sl], BK[4 + eh][:], ALU.add, [RX[th], BK[4 + eh]], [RX[th]])
                    yield

            def interleave(ga, gb, na, nb):
                a = b = True
                while a or b:
                    for _ in range(na):
                        if a:
                            try:
                                next(ga)
                            except StopIteration:
                                a = False
                    for _ in range(nb):
                        if b:
                            try:
                                next(gb)
                            except StopIteration:
                                b = False

            RB1 = [Res("B1s%d" % i) for i in range(4)]
            ORDER = [8, 9, 10, 11, 0, 1, 2, 3, 4, 5, 6, 7]

            def gen_Bconv():
                def b_mm(i):
                    ch = ORDER[i]
                    c0 = ch * 128
                    bk = BK[1 + i % 2]
                    for k in range(8):
                        mm(bk[:, 0:128], WIN[:, k, c0:c0 + 128], HT[:, k, :], k == 0, k == 7,
                           [HT] + rwin(c0, c0 + 128), [bk])

                def b_copy(i):
                    ch = ORDER[i]
                    pc = PCC[i % 3]
                    bk = BK[1 + i % 2]
                    cp("act", pc[:, 3:131], bk[:, 0:128], [bk], [pc])
                    cp("pool", pc[:, 0:3], TAIL[:, ch, :], [TAIL], [pc])

                def b_conv(i):
                    ch = ORDER[i]
                    pc, ac = PCC[i % 3], ACC[i % 3]
                    ts("dve", ac[:], pc[:, 0:128], CW[:, 0, ch:ch + 1], None, ALU.mult, None, [pc, CW], [ac])
                    for j in range(1, 3):
                        stt("dve", ac[:], pc[:, j:j + 128], CW[:, j, ch:ch + 1], ac[:], ALU.mult, ALU.add,
                            [pc, CW, ac], [ac])
                    if ch < 8:
                        stt("dve", QKF[:, ch, :], pc[:, 3:131], CW[:, 3, ch:ch + 1], ac[:], ALU.mult, ALU.add,
                            [pc, CW, ac], [QKF])
                    else:
                        stt("dve", ac[:], pc[:, 3:131], CW[:, 3, ch:ch + 1], ac[:], ALU.mult, ALU.add,
                            [pc, CW, ac], [ac])
                    cp("pool", TAIL[:, ch, :], pc[:, 128:131], [pc], [TAIL])

                def b_silu(i):
                    ch = ORDER[i]
                    if ch >= 8:
                        act(GVT[:, ch - 8, :], ACC[i % 3][:], AF.Silu, [ACC[i % 3]], [GVT])

                for i in range(12 + 3):
                    if i < 12:
                        b_mm(i)
                    if 0 <= i - 1 < 12:
                        b_copy(i - 1)
                    if 0 <= i - 2 < 12:
                        b_conv(i - 2)
                    if 0 <= i - 3 < 12:
                        b_silu(i - 3)
                    yield
                for hf in range(2):
                    act(QKF[:, hf * 4:(hf + 1) * 4, :], QKF[:, hf * 4:(hf + 1) * 4, :], AF.Silu, [QKF], [QKF])
                    yield

            def interleave3(gens, steps):
                alive = [True] * len(gens)
                while any(alive):
                    for gi, g in enumerate(gens):
                        for _ in range(steps[gi]):
                            if alive[gi]:
                                try:
                                    next(g)
                                except StopIteration:
                                    alive[gi] = False

            def stage_A(t):
                Xv = Xt[:, t, :]
                ss, rs = SA4[0], SA4[1]
                act(XN[:], Xv, AF.Square, [RX[t]], [XN, ss], accum=ss[:, 0:1])
                rsqrt_small(rs[:, 0:1], ss[:, 0:1], 1.0 / D, [ss], [rs], rs[:, 1:2])
                ts("dve", XN[:], Xv, rs[:, 0:1], None, ALU.mult, None, [RX[t], rs], [XN])
                for k in range(8):
                    tr(TPB[:, k * 128:(k + 1) * 128], XN[:, k * 128:(k + 1) * 128], IDB[:], [XN, IDB], [TPB])
                tt("dve", HT[:], TPB[:].rearrange("p (k t) -> p k t", k=8), bc3(GPRE[:], 8, 128), ALU.mult,
                   [TPB, GPRE], [HT])

            P.marks.append(len(P.ops))
            for t in range(KNT):
                dma("sp", Xt[:, t, :], x_d[t * 128:(t + 1) * 128, :], [], [RX[t]])
            stage_A(0)
            for _ in gen_Bconv():
                pass

            for t in range(KNT):
                P.marks.append(len(P.ops))
                Xv = Xt[:, t, :]
                hgen = gen_H(t - 1) if t > 0 else None

                def hstep(n=2):
                    if hgen is not None:
                        for _ in range(n):
                            try:
                                next(hgen)
                            except StopIteration:
                                break

                def tok_proj(bk, c0, n):
                    for k in range(8):
                        mm(bk[:, 0:n], HT[:, k, :], WIN[:, k, c0:c0 + n], k == 0, k == 7,
                           [HT] + rwin(c0, c0 + n), [bk])
                hstep(2)
                tok_proj(BK[0], W_GZ, 512)
                hstep(1)
                act(TMP[0][:], BK[0][:], AF.Silu, [BK[0]], [TMP[0]])
                tt("pool", ZSG[:].rearrange("p (h e) -> p h e", h=4), TMP[0][:].rearrange("p (h e) -> p h e", h=4),
                   bcm(GNG[:], 4), ALU.mult, [TMP[0], GNG], [ZSG])
                hstep(1)
                tok_proj(BK[1], W_MO, 512)
                hstep(1)
                act(TMP[1][:], BK[1][:], AF.Sigmoid, [BK[1]], [TMP[1]])
                tt("pool", MOSG[:].rearrange("p (h e) -> p h e", h=4), TMP[1][:].rearrange("p (h e) -> p h e", h=4),
                   bcm(MNG[:], 4), ALU.mult, [TMP[1], MNG], [MOSG])
                for hh in range(8):
                    bk = BK[hh % 2]
                    c0 = W_MQK + hh * 64
                    for k in range(8):
                        mm(bk[0:64, 0:128], WIN[:, k, c0:c0 + 64], HT[:, k, :], k == 0, k == 7,
                           [HT] + rwin(c0, c0 + 64), [bk])
                    if hh < 4:
                        cp("act", MQT[:, hh, :], bk[0:64, 0:128], [bk], [MQT])
                    else:
                        act(MKT[:, hh - 4, :], bk[0:64, 0:128], AF.Copy, [bk], [MKT], scale=0.125)
                    hstep(1)
                hstep(20)
                tok_proj(BK[0], W_MV, 512)
                cp("act", MVt[:].rearrange("p h e -> p (h e)"), BK[0][:], [BK[0]], [MVt])
                tok_proj(BK[1], W_GATE, 16)
                cp("dve", GT[:], BK[1][:, 0:16], [BK[1]], [GT])
                tt("dve", ARG[:], GT[:, 0:12], SGN[:], ALU.mult, [GT, SGN], [ARG])
                tt("dve", ARG[:], ARG[:], BIA[:], ALU.add, [ARG, BIA], [ARG])
                for hf in range(2):
                    act(TMP[hf][:], QKF[:, hf * 4:(hf + 1) * 4, :].rearrange("p a b -> p (a b)"), AF.Square,
                        [QKF], [TMP[hf]])
                    mm(BK[2 + hf][:], ONES, TMP[hf][:], True, True, [CONST, TMP[hf]], [BK[2 + hf]])
                for hf in range(2):
                    ts("dve", TMP[hf][:], BK[2 + hf][:], 1.0, EPS, ALU.mult, ALU.add, [BK[2 + hf]], [TMP[hf]])
                    act(TMP[hf][:], TMP[hf][:], AF.Ln, [TMP[hf]], [TMP[hf]])
                    act(TMP[hf][:], TMP[hf][:], AF.Exp, [TMP[hf]], [TMP[hf]], scale=-0.5)
                stt("dve", GQT[:].rearrange("p a b -> p (a b)"), QKF[:, 0:4, :].rearrange("p a b -> p (a b)"),
                    128.0 ** -0.5, TMP[0][:], ALU.mult, ALU.mult, [QKF, TMP[0]], [GQT])
                tt("pool", GKT[:].rearrange("p a b -> p (a b)"), QKF[:, 4:8, :].rearrange("p a b -> p (a b)"),
                   TMP[1][:], ALU.mult, [QKF, TMP[1]], [GKT])
                act(ARG[:], ARG[:], AF.Exp, [ARG], [ARG])
                act(ARG[:], ARG[:], AF.Ln, [ARG], [ARG], bias=1.0)
                tt("dve", GL[:], ARG[:], NEGC[:], ALU.mult, [ARG, NEGC], [GL])
                act(BETA[:], GL[:, 0:4], AF.Exp, [GL], [BETA])
                tt("dve", IG[:], GT[:, 12:16], SMB[:, 8:12], ALU.add, [GT, SMB], [IG])

                if t + 1 < KNT:
                    stage_A(t + 1)

                mm(BK[2][:, 0:8], TRI, GL[:, 4:12], True, True, [CONST, GL], [BK[2]])
                cp("dve", GF[:], BK[2][:, 0:8], [BK[2]], [GF])
                cp("pool", COLS[:, 0:4], GF[:, 0:4], [GF], [COLS])
                tt("dve", COLS[:, 4:8], GF[:, 0:4], GL[:, 0:4], ALU.add, [GF, GL], [COLS])
                cp("pool", COLS[:, 8:12], GF[:, 4:8], [GF], [COLS])
                tt("dve", COLS[:, 12:16], IG[:], GF[:, 4:8], ALU.subtract, [IG, GF], [COLS])
                ts("dve", COLS[:, 16:20], GF[:, 0:4], -1.0, None, ALU.mult, None, [GF], [COLS])
                for j in range(4):
                    tr(BK[3][0:4, j * 128:(j + 1) * 128], COLS[:, 4 * j:4 * j + 4], IDF, [COLS, CONST], [BK[3]])
                tr(BK[4][0:4, 0:128], COLS[:, 16:20], IDF, [COLS, CONST], [BK[4]])
                cp("dve", RT[:, 0:4, :].rearrange("p a b -> p (a b)"), BK[3][0:4, :], [BK[3]], [RT])
                cp("dve", RT[:, 4, :], BK[4][0:4, 0:128], [BK[4]], [RT])
                SELR3 = SELR.rearrange("p (a b) -> p a b", a=4)
                tt("dve", BD[:], bcm(RT[:, 1, :], 4), SELR3, ALU.mult, [RT, CONST], [BD])
                mm(BK[2][:], ONES4, BD[:].rearrange("p a b -> p (a b)"), True, False, [CONST, BD], [BK[2]])
                mm(BK[2][:], RT[:, 4, :], SELR, False, True, [RT, CONST], [BK[2]])
                tt("dve", EXPA[:].rearrange("p (h c) -> p h c", h=4), BK[2][:].rearrange("p (h c) -> p h c", h=4),
                   bcm(MST, 4), ALU.add, [BK[2], CONST], [EXPA])
                act(EXPA[:], EXPA[:], AF.Exp, [EXPA], [EXPA])
                tt("dve", BD[:], bcm(RT[:, 0, :], 4), SELR3, ALU.mult, [RT, CONST], [BD])
                mm(BK[3][:], ONES4, BD[:].rearrange("p a b -> p (a b)"), True, False, [CONST, BD], [BK[3]])
                mm(BK[3][:], RT[:, 4, :], SELR, False, True, [RT, CONST], [BK[3]])
                tt("dve", EXPQ[:].rearrange("p (h c) -> p h c", h=4), BK[3][:].rearrange("p (h c) -> p h c", h=4),
                   bcm(MIT, 4), ALU.add, [BK[3], CONST], [EXPQ])
                act(EXPQ[:], EXPQ[:], AF.Exp, [EXPQ], [EXPQ])
                for h in range(4):
                    mm(BK[4][:, h * 128:(h + 1) * 128], GKT[:, h, :], GKT[:, h, :], True, True, [GKT], [BK[4]])
                for h in range(4):
                    mm(BK[5][:, h * 128:(h + 1) * 128], GKT[:, h, :], GQT[:, h, :], True, True, [GKT, GQT], [BK[5]])
                tt("dve", MB[0][:], BK[4][:], EXPA[:], ALU.mult, [BK[4], EXPA], [MB[0]])
                tt("dve", QKD[:].rearrange("p h c -> p (h c)"), BK[5][:], EXPQ[:], ALU.mult, [BK[5], EXPQ], [QKD])

                for h in range(4):
                    tr(TPB[:, h * 128:(h + 1) * 128], GKT[:, h, :], IDB[:], [GKT, IDB], [TPB])
                    tr(TPB[:, 512 + h * 128:512 + (h + 1) * 128], GVT[:, h, :], IDB[:], [GVT, IDB], [TPB])
                cp("act", GKt[:].rearrange("p h d -> p (h d)"), TPB[:, 0:512], [TPB], [GKt])
                EG, BEG, EGL, GEND = SC4[0], SC4[1], SC4[2], SC4[3]
                act(EG[:, 0:4], GF[:, 0:4], AF.Exp, [GF], [EG])
                tt("dve", BEG[:, 0:4], EG[:, 0:4], BETA[:], ALU.mult, [EG, BETA], [BEG])
                mm(BK[2][:, 0:8], SEL127, GF[:], True, True, [CONST, GF], [BK[2]])
                cp("dve", LASTG[:], BK[2][:, 0:8], [BK[2]], [LASTG])
                tt("dve", EGL[:, 0:4], LASTG[:, 0:4], GF[:, 0:4], ALU.subtract, [LASTG, GF], [EGL])
                act(EGL[:, 0:4], EGL[:, 0:4], AF.Exp, [EGL], [EGL])
                act(GEND[:, 0:4], LASTG[:, 0:4], AF.Exp, [LASTG], [GEND])
                tt("dve", XK[:], GKt[:], bc3(BEG[:, 0:4], 4, 128), ALU.mult, [GKt, BEG], [XK])
                tt("pool", KEND[:], GKt[:], bc3(EGL[:, 0:4], 4, 128), ALU.mult, [GKt, EGL], [KEND])
                tt("dve", GVb[:], TPB[:, 512:1024].rearrange("p (h e) -> p h e", h=4), bc3(BETA[:], 4, 128),
                   ALU.mult, [TPB, BETA], [GVb])
                def gen_EF():
                    for h in range(4):
                        tr(BK[5][:, h * 128:(h + 1) * 128], MB[0][:, h * 128:(h + 1) * 128], IDF, [MB[0], CONST], [BK[5]])
                    yield
                    cp("act", LB[0][:], BK[5][:], [BK[5]], [LB[0]])
                    yield
                    tt("dve", RR[:].rearrange("p (h c) -> p h c", h=4), bcm(IDF, 4),
                       MB[0][:].rearrange("p (h c) -> p h c", h=4), ALU.subtract, [CONST, MB[0]], [RR])
                    yield
                    NLEV = 6
                    yield
                    for k in range(NLEV):
                        a, b = k % 2, (k + 1) % 2
                        for h in range(4):
                            sl = slice(h * 128, (h + 1) * 128)
                            mm(BK[3][:, sl], MB[a][:, sl], LB[a][:, sl], True, True, [MB[a], LB[a]], [BK[3]])
                        yield
                        if k < NLEV - 1:
                            for h in range(4):
                                sl = slice(h * 128, (h + 1) * 128)
                                mm(BK[4][:, sl], LB[a][:, sl], MB[a][:, sl], True, True, [MB[a], LB[a]], [BK[4]])
                            yield
                        cp("act", LB[b][:], BK[3][:], [BK[3]], [LB[b]])
                        yield
                        if k < NLEV - 1:
                            cp("dve", MB[b][:], BK[4][:], [BK[4]], [MB[b]])
                            yield
                        for h in range(4):
                            sl = slice(h * 128, (h + 1) * 128)
                            mm(BK[5][:, sl], LB[b][:, sl], RR[:, sl], True, True, [LB[b], RR], [BK[5]])
                        yield
                        tt("dve", RR[:], RR[:], BK[5][:], ALU.add, [RR, BK[5]], [RR])
                        yield
                    cp("act", BINV[:].rearrange("p h c -> p (h c)"), RR[:], [RR], [BINV])
                    yield
                    for h in range(4):
                        sl = slice(h * 128, (h + 1) * 128)
                        mm(BK[3][:, sl], BINV[:, h, :], GVb[:, h, :], True, True, [BINV, GVb], [BK[3]])
                        mm(BK[4][:, sl], XK[:, h, :], BINV[:, h, :], True, True, [BINV, XK], [BK[4]])
                    yield
                    cp("act", WV[:], BK[3][:], [BK[3]], [WV])
                    yield
                    cp("dve", WKT[:].rearrange("p h c -> p (h c)"), BK[4][:], [BK[4]], [WKT])
                    yield
                    for h in range(4):
                        sl = slice(h * 128, (h + 1) * 128)
                        mm(BK[3][:, sl], WKT[:, h, :], SBh[:, h, :], True, True, [WKT, SBh], [BK[3]])
                        mm(BK[5][:, sl], GQT[:, h, :], SBh[:, h, :], True, True, [GQT, SBh], [BK[5]])
                    yield
                    tt("dve", U[:].rearrange("p h e -> p (h e)"), WV[:], BK[3][:], ALU.subtract, [WV, BK[3]], [U])
                    yield
                    for h in range(4):
                        sl = slice(h * 128, (h + 1) * 128)
                        mm(BK[4][:, sl], QKD[:, h, :], U[:, h, :], True, True, [QKD, U], [BK[4]])
                        mm(BK[3][:, sl], KEND[:, h, :], U[:, h, :], True, True, [KEND, U], [BK[3]])
                    yield
                    OG = MB[1]
                    yield
                    tt("dve", OG[:].rearrange("p (h e) -> p h e", h=4), BK[5][:].rearrange("p (h e) -> p h e", h=4),
                       bc3(EG[:, 0:4], 4, 128), ALU.mult, [BK[5], EG], [OG])
                    yield
                    tt("dve", OG[:], OG[:], BK[4][:], ALU.add, [OG, BK[4]], [OG])
                    yield
                    for h in range(4):
                        stt("dve", S32[:, h, :], S32[:, h, :], GEND[:, h:h + 1], BK[3][:, h * 128:(h + 1) * 128],
                            ALU.mult, ALU.add, [S32, GEND, BK[3]], [S32])
                    yield
                    cp("act", SBh[:], S32[:], [S32], [SBh])
                    yield
                    headnorm_gate(OG, ZSG, MIX[:, 0:512], LB[0], SC4[4], SC4[5])
                    yield
                def gen_G():
                    tt("dve", BD[:], bcm(RT[:, 3, :], 4), SELR3, ALU.mult, [RT, CONST], [BD])
                    yield
                    mm(BK[6][:], ONES4, BD[:].rearrange("p a b -> p (a b)"), True, False, [CONST, BD], [BK[6]])
                    yield
                    mm(BK[6][:], RT[:, 2, :], SELR, False, True, [RT, CONST], [BK[6]])
                    yield
                    PD = TMP[0]
                    yield
                    tt("dve", PD[:].rearrange("p (h s) -> p h s", h=4), BK[6][:].rearrange("p (h s) -> p h s", h=4),
                       bcm(MI, 4), ALU.add, [BK[6], CONST], [PD])
                    yield
                    red("dve", DMAX[:], PD[:].rearrange("p (h s) -> p h s", h=4), ALU.max, [PD], [DMAX])
                    yield
                    for h in range(4):
                        mm(BK[0][:, h * 128:(h + 1) * 128], MQT[:, h, :], MKT[:, h, :], True, True, [MQT, MKT], [BK[0]])
                    yield
                    for h in range(4):
                        tr(TPB[:, h * 64:(h + 1) * 64], MKT[:, h, :], IDB[0:64, 0:64], [MKT, IDB], [TPB])
                    yield
                    cp("act", MKt[:].rearrange("p h d -> p (h d)"), TPB[:, 0:256], [TPB], [MKt])
                    yield
                    Bv, MT, WP, EMT = SG4[0], SG4[1], SG4[2], SG4[3]
                    yield
                    tt("dve", Bv[:, 0:4], GF[:, 4:8], MBC[:], ALU.add, [GF, MBC], [Bv])
                    yield
                    tt("dve", MT[:, 0:4], Bv[:, 0:4], DMAX[:], ALU.max, [Bv, DMAX], [MT])
                    yield
                    tt("dve", WP[:, 0:4], Bv[:, 0:4], MT[:, 0:4], ALU.subtract, [Bv, MT], [WP])
                    yield
                    act(WP[:, 0:4], WP[:, 0:4], AF.Exp, [WP], [WP])
                    yield
                    tt("dve", PD[:].rearrange("p (h s) -> p h s", h=4), PD[:].rearrange("p (h s) -> p h s", h=4),
                       bc3(MT[:, 0:4], 4, 128), ALU.subtract, [PD, MT], [PD])
                    yield
                    act(PD[:], PD[:], AF.Exp, [PD], [PD])
                    yield
                    tt("dve", PD[:], PD[:], BK[0][:], ALU.mult, [PD, BK[0]], [PD])
                    yield
                    RS = SG4[4]
                    yield
                    red("dve", RS[:, 0:4], PD[:].rearrange("p (h s) -> p h s", h=4), ALU.add, [PD], [RS])
                    yield
                    cp("act", PQKb[:].rearrange("p h s -> p (h s)"), PD[:], [PD], [PQKb])
                    yield
                    for h in range(4):
                        tr(TPB[:, 512 + h * 128:512 + (h + 1) * 128], PQKb[:, h, :], IDB[:], [PQKb, IDB], [TPB])
                    yield
                    cp("act", PQKT[:].rearrange("p h s -> p (h s)"), TPB[:, 512:1024], [TPB], [PQKT])
                    yield
                    for h in range(4):
                        sl = slice(h * 128, (h + 1) * 128)
                        mm(BK[0][:, sl], MQT[:, h, :], CBh[:, h, :], True, True, [MQT, CBh], [BK[0]])
                        mm(BK[6][:, 2 * h:2 * h + 2], MQT[:, h, :], NBh[:, h, :], True, True, [MQT, NBh], [BK[6]])
                    yield
                    NUM = TMP[0]
                    yield
                    tt("dve", NUM[:].rearrange("p (h e) -> p h e", h=4), BK[0][:].rearrange("p (h e) -> p h e", h=4),
                       bc3(WP[:, 0:4], 4, 128), ALU.mult, [BK[0], WP], [NUM])
                    yield
                    for h in range(4):
                        sl = slice(h * 128, (h + 1) * 128)
                        mm(BK[0][:, sl], PQKT[:, h, :], MVt[:, h, :], True, True, [PQKT, MVt], [BK[0]])
                    yield
                    tt("dve", NUM[:], NUM[:], BK[0][:], ALU.add, [NUM, BK[0]], [NUM])
                    yield
                    DEN = SG4[5]
                    yield
                    tt("dve", DEN[:, 0:4], BK[6][:, 0:8].rearrange("p (h two) -> p h two", two=2)[:, :, 0], WP[:, 0:4], ALU.mult, [BK[6], WP], [DEN])
                    yield
                    tt("dve", DEN[:, 0:4], DEN[:, 0:4], RS[:, 0:4], ALU.add, [DEN, RS], [DEN])
                    yield
                    act(EMT[:, 0:4], MT[:, 0:4], AF.Exp, [MT], [EMT], scale=-1.0)
                    yield
                    ts("dve", DEN[:, 4:8], DEN[:, 0:4], -1.0, None, ALU.mult, None, [DEN], [DEN])
                    yield
                    tt("dve", DEN[:, 0:4], DEN[:, 0:4], DEN[:, 4:8], ALU.max, [DEN], [DEN])
                    yield
                    tt("dve", DEN[:, 0:4], DEN[:, 0:4], EMT[:, 0:4], ALU.max, [DEN, EMT], [DEN])
                    yield
                    P.op("dve", lambda e, d=DEN: e.reciprocal(out=d[:, 0:4], in_=d[:, 0:4]), [DEN], [DEN])
                    yield
                    tt("dve", NUM[:].rearrange("p (h e) -> p h e", h=4), NUM[:].rearrange("p (h e) -> p h e", h=4),
                       bc3(DEN[:, 0:4], 4, 128), ALU.mult, [NUM, DEN], [NUM])
                    yield
                    cp("pool", T12[:, 0:4], MT[:, 0:4], [MT], [T12])
                    yield
                    tt("dve", T12[:, 4:8], GF[:, 4:8], MT[:, 0:4], ALU.subtract, [GF, MT], [T12])
                    yield
                    cp("pool", T12[:, 8:12], WP[:, 0:4], [WP], [T12])
                    yield
                    mm(BK[6][:, 16:28], SEL127, T12[:], True, True, [CONST, T12], [BK[6]])
                    yield
                    cp("dve", MBC[:], BK[6][:, 16:20], [BK[6]], [MBC])
                    yield
                    PEND = SG4[4]
                    yield
                    tt("dve", PEND[:, 4:8], COLS[:, 12:16], BK[6][:, 20:24], ALU.add, [COLS, BK[6]], [PEND])
                    yield
                    act(PEND[:, 4:8], PEND[:, 4:8], AF.Exp, [PEND], [PEND])
                    yield
                    cp("dve", WLC[:], BK[6][0:64, 24:28], [BK[6]], [WLC])
                    yield
                    tt("dve", PK[:], MKt[:], bc3(PEND[:, 4:8], 4, 64), ALU.mult, [MKt, PEND], [PK])
                    yield
                    for h in range(4):
                        mm(BK[0][0:64, h * 128:(h + 1) * 128], PK[:, h, :], MVt[:, h, :], True, True, [PK, MVt], [BK[0]])
                    yield
                    for h in range(4):
                        mm(BK[6][0:64, 32 + 2 * h:34 + 2 * h], PK[:, h, :], ONEB[:], True, True, [PK, ONEB], [BK[6]])
                    yield
                    for h in range(4):
                        stt("dve", C32[:, h, :], C32[:, h, :], WLC[:, h:h + 1], BK[0][0:64, h * 128:(h + 1) * 128],
                            ALU.mult, ALU.add, [C32, WLC, BK[0]], [C32])
                    yield
                    tt("dve", N32[:], N32[:], WLC[:], ALU.mult, [N32, WLC], [N32])
                    yield
                    tt("dve", N32[:], N32[:], BK[6][0:64, 32:40].rearrange("p (a two) -> p a two", two=2)[:, :, 0],
                       ALU.add, [N32, BK[6]], [N32])
                    yield
                    cp("act", CBh[:], C32[:], [C32], [CBh])
                    yield
                    cp("act", NBh[:, :, 0], N32[:], [N32], [NBh])
                    yield

                if t + 1 < KNT:
                    interleave3([gen_EF(), gen_G(), gen_Bconv()], KRATIO)
                else:
                    interleave3([gen_EF(), gen_G()], [2, 1])
                headnorm_gate(TMP[0], MOSG, MIX[:, 512:1024], TMP[1], SG4[4], SG4[5])

            for _ in gen_H(KNT - 1):
                pass

            dma("sp", pS_d.rearrange("h d e -> d h e"), S32[:], [S32], [])
            dma("sp", pC_d.rearrange("h d e -> d h e"), C32[:], [C32], [])
            dma("sp", pn_d.rearrange("h d -> d h"), N32[:], [N32], [], slow=True)
            dma("sp", pm_d, MBC[0:1, :], [MBC], [])
            for ch in range(12):
                tr(BK[2 + ch // 4][0:3, (ch % 4) * 128:(ch % 4 + 1) * 128], TAIL[:, ch, :], IDF, [TAIL, CONST],
                   [BK[2 + ch // 4]])
            for g in range(3):
                cp("dve", TMP[g % 2][0:3, :], BK[2 + g][0:3, :], [BK[2 + g]], [TMP[g % 2]])
                dma("sp", pconv_d[:, g * 512:(g + 1) * 512], TMP[g % 2][0:3, :], [TMP[g % 2]], [])


            P.barrier(all_res_p1 + RX + [XS1T.r, HNS.r, IDF2.r, IDB.r, GPREMLP.r])
            p1a.close()
            p1b = ExitStack()
            cur[0] = p1b
            R_ = NS
            GRP = 1
            XS = s1("XS", [R_, D])
            XNs = s1("XNs", [R_, D], BF16)
            HTs = s1("HTs", [128, 8, R_], BF16)
            SCV = s1("SCV", [R_, 512])
            XPT = s1("XPT", [128, 12, 4, R_])
            ACs = s1("ACs", [128, 12, R_])
            TM12 = s1("TM12", [128, 12, R_])
            SQs = s1("SQs", [128, 8, R_])
            GQs = s1("GQs", [128, 4, R_])
            GKs = s1("GKs", [128, 4, R_])
            GVs = s1("GVs", [128, 4, R_])
            MQs = s1("MQs", [64, 4, R_])
            MKs = s1("MKs", [64, 4, R_])
            ZSGs = s1("ZSGs", [R_, 512])
            MOSGs = s1("MOSGs", [R_, 512])
            MVs = s1("MVs", [R_, 4, 128])
            GTs = s1("GTs", [R_, 16])
            ARGs = s1("ARGs", [R_, 12])
            GLs = s1("GLs", [R_, 12])
            BETAs = s1("BETAs", [R_, 4])
            IGs = s1("IGs", [R_, 4])
            EGs = s1("EGs", [R_, 4])
            Qt = s1("Qt", [R_, 4, 128])
            Kt = s1("Kt", [R_, 4, 128])
            Vt = s1("Vt", [R_, 4, 128])
            MQt = s1("MQt", [R_, 4, 64])
            MKtt = s1("MKtt", [R_, 4, 64])
            PKt = s1("PKt", [R_, 4, 64])
            DIAGI = s1("DIAGI", [R_, 16, 16])
            M16 = s1("M16", [128, 16, 16])
            KD = [s1("KD%d" % i, [128, 4, 16]) for i in range(2)]
            QD = [s1("QD%d" % i, [128, 4, 16]) for i in range(2)]
            QDm = [s1("QDm%d" % i, [64, 4, 16]) for i in range(2)]
            DG4 = s1("DG4", [R_, 16, 4])
            EGBC = s1("EGBC", [128, 64])
            WPBC = s1("WPBC", [128, 64])
            S0b = [s1("S0b%d" % i, [128, 4, 128]) for i in range(2)]
            C0b = [s1("C0b%d" % i, [64, 4, 128]) for i in range(2)]
            Ug = s1("Ug", [R_, 4, 128])
            KROW = s1("KROW", [R_, 512])
            PKROW = View(DIAGI[:].rearrange("p a b -> p (a b)"), DIAGI.r)
            N0 = s1("N0", [R_, 4, 64])
            SMs = [s1("SMs%d" % i, [R_, 8]) for i in range(10)]
            T1 = s1("T1s", [R_, 512])
            T2 = KROW

            def hn_gate16(src, gate, dst, wres):
                ss, rs = SMs[8], SMs[9]
                tt("pool", T2[:], src[:], src[:], ALU.mult, [src], [T2])
                red("dve", ss[:, 0:4], T2[:].rearrange("p (h e) -> p h e", h=4), ALU.add, [T2], [ss])
                rsqrt_small(rs[:, 0:4], ss[:, 0:4], 1.0 / 128, [ss], [rs], rs[:, 4:8])
                tt("dve", T2[:].rearrange("p (h e) -> p h e", h=4), src[:].rearrange("p (h e) -> p h e", h=4),
                   bc3(rs[:, 0:4], 4, 128), ALU.mult, [src, rs], [T2])
                tt("dve", dst, T2[:], gate[:], ALU.mult, [T2, gate], [wres])

            IDF16 = CONST[0:R_, C_IDF:C_IDF + R_]
            ONES16 = CONST[0:R_, C_ONES:C_ONES + 128]

            dma("sp", XS[:], xs_d, [], [XS])
            dma("sp", N0[:].rearrange("p h d -> p (h d)"), sn_d, [], [N0])
            dma("sp", SMs[0][:, 0:4], sm_d, [], [SMs[0]])
            dma("sp", oconv_d[:, 0:2, :], sconv_d[:, 1:3, :], [], [])
            ss, rs = SMs[8], SMs[9]
            act(XNs[:], XS[:], AF.Square, [XS], [XNs, ss], accum=ss[:, 0:1])
            rsqrt_small(rs[:, 0:1], ss[:, 0:1], 1.0 / D, [ss], [rs], rs[:, 1:2])
            ts("dve", XNs[:], XS[:], rs[:, 0:1], None, ALU.mult, None, [XS, rs], [XNs])
            for k in range(8):
                tr(TPB[:, k * 128:k * 128 + R_], XNs[:, k * 128:(k + 1) * 128], IDB[0:R_, 0:R_], [XNs, IDB], [TPB])
            tt("dve", HTs[:], TPB[:].rearrange("p (k t) -> p k t", k=8)[:, :, 0:R_], bc3(GPRE[:], 8, R_), ALU.mult,
               [TPB, GPRE], [HTs])

            for ch in range(12):
                bk = BK[ch % 2]
                c0 = ch * 128
                for k in range(8):
                    mm(bk[:, 0:R_], WIN[:, k, c0:c0 + 128], HTs[:, k, :], k == 0, k == 7, [HTs] + rwin(c0, c0 + 128), [bk])
                cp("act", XPT[:, ch, 3, :], bk[:, 0:R_], [bk], [XPT])
            for hh in range(8):
                bk = BK[hh % 2]
                c0 = W_MQK + hh * 64
                for k in range(8):
                    mm(bk[0:64, 0:R_], WIN[:, k, c0:c0 + 64], HTs[:, k, :], k == 0, k == 7, [HTs] + rwin(c0, c0 + 64), [bk])
                if hh < 4:
                    cp("act", MQs[:, hh, :], bk[0:64, 0:R_], [bk], [MQs])
                else:
                    act(MKs[:, hh - 4, :], bk[0:64, 0:R_], AF.Copy, [bk], [MKs], scale=0.125)
            for g in range(3):
                for j in range(3):
                    dma("sp", SCV[:], sconv_d[:, j, g * 512:(g + 1) * 512], [], [SCV])
                    for c4 in range(4):
                        o0 = (j * 4 + c4) * R_
                        tr(BK[2][:, o0:o0 + R_], SCV[:, c4 * 128:(c4 + 1) * 128], IDF16, [SCV, CONST], [BK[2]])
                cp("dve", XPT[:, 4 * g:4 * g + 4, 0:3, :],
                   BK[2][:, 0:12 * R_].rearrange("p (j c r) -> p c j r", j=3, c=4), [BK[2]], [XPT])
            for g in range(3):
                bk = BK[3 + g % 2]
                for k in range(8):
                    mm(bk[0:R_, :], HTs[:, k, :], WIN[:, k, g * 512:(g + 1) * 512], k == 0, k == 7,
                       [HTs] + rwin(g * 512, (g + 1) * 512), [bk])
                cp("act", SCV[:], bk[0:R_, :], [bk], [SCV])
                dma("sp", oconv_d[:, 2, g * 512:(g + 1) * 512], SCV[:], [SCV], [])
            tt("dve", ACs[:], XPT[:, :, 0, :], bc3(CW[:, 0, :], 12, R_), ALU.mult, [XPT, CW], [ACs])
            for j in range(1, 4):
                tt("dve", TM12[:], XPT[:, :, j, :], bc3(CW[:, j, :], 12, R_), ALU.mult, [XPT, CW], [TM12])
                tt("dve", ACs[:], ACs[:], TM12[:], ALU.add, [ACs, TM12], [ACs])
            QKFs = TM12
            act(QKFs[:, 0:8, :], ACs[:, 0:8, :], AF.Silu, [ACs], [QKFs])
            act(GVs[:], ACs[:, 8:12, :], AF.Silu, [ACs], [GVs])
            act(SQs[:], QKFs[:, 0:8, :], AF.Square, [QKFs], [SQs])
            mm(BK[2][:, 0:8 * R_], ONES, SQs[:].rearrange("p a b -> p (a b)"), True, True, [CONST, SQs], [BK[2]])
            ts("dve", SQs[:].rearrange("p a b -> p (a b)"), BK[2][:, 0:8 * R_], 1.0, EPS, ALU.mult, ALU.add, [BK[2]], [SQs])
            act(SQs[:], SQs[:], AF.Ln, [SQs], [SQs])
            act(SQs[:], SQs[:], AF.Exp, [SQs], [SQs], scale=-0.5)
            stt("dve", GQs[:], QKFs[:, 0:4, :], 128.0 ** -0.5, SQs[:, 0:4, :], ALU.mult, ALU.mult, [QKFs, SQs], [GQs])
            tt("dve", GKs[:], QKFs[:, 4:8, :], SQs[:, 4:8, :], ALU.mult, [QKFs, SQs], [GKs])

            def tok16(bk, c0, n):
                for k in range(8):
                    mm(bk[0:R_, 0:n], HTs[:, k, :], WIN[:, k, c0:c0 + n], k == 0, k == 7, [HTs] + rwin(c0, c0 + n), [bk])
            tok16(BK[0], W_GZ, 512)
            act(T1[:], BK[0][0:R_, :], AF.Silu, [BK[0]], [T1])
            tt("dve", ZSGs[:].rearrange("p (h e) -> p h e", h=4), T1[:].rearrange("p (h e) -> p h e", h=4),
               bcm(GNG[0:R_, :], 4), ALU.mult, [T1, GNG], [ZSGs])
            tok16(BK[1], W_MV, 512)
            cp("act", MVs[:].rearrange("p h e -> p (h e)"), BK[1][0:R_, :], [BK[1]], [MVs])
            tok16(BK[0], W_MO, 512)
            act(T1[:], BK[0][0:R_, :], AF.Exp, [BK[0]], [T1], scale=-1.0)
            ts("dve", T1[:], T1[:], 1.0, None, ALU.add, None, [T1], [T1])
            P.op("dve", lambda e: e.reciprocal(out=T1[:], in_=T1[:]), [T1], [T1])
            tt("dve", MOSGs[:].rearrange("p (h e) -> p h e", h=4), T1[:].rearrange("p (h e) -> p h e", h=4),
               bcm(MNG[0:R_, :], 4), ALU.mult, [T1, MNG], [MOSGs])
            tok16(BK[1], W_GATE, 16)
            cp("dve", GTs[:], BK[1][0:R_, 0:16], [BK[1]], [GTs])
            tt("dve", ARGs[:], GTs[:, 0:12], SGN[0:R_, :], ALU.mult, [GTs, SGN], [ARGs])
            tt("dve", ARGs[:], ARGs[:], BIA[0:R_, :], ALU.add, [ARGs, BIA], [ARGs])
            act(ARGs[:], ARGs[:], AF.Exp, [ARGs], [ARGs])
            act(ARGs[:], ARGs[:], AF.Ln, [ARGs], [ARGs], bias=1.0)
            tt("dve", GLs[:], ARGs[:], NEGC[0:R_, :], ALU.mult, [ARGs, NEGC], [GLs])
            act(BETAs[:], GLs[:, 0:4], AF.Exp, [GLs], [BETAs])
            tt("dve", IGs[:], GTs[:, 12:16], SMB[0:R_, 8:12], ALU.add, [GTs, SMB], [IGs])
            act(EGs[:], GLs[:, 4:8], AF.Exp, [GLs], [EGs])
            M0s, Bs, MTs, WPs, Ps, QKG, QKMs, QNs = SMs[0], SMs[1], SMs[2], SMs[3], SMs[4], SMs[5], SMs[6], SMs[7]
            tt("dve", Bs[:, 0:4], GLs[:, 8:12], M0s[:, 0:4], ALU.add, [GLs, M0s], [Bs])
            tt("dve", MTs[:, 0:4], Bs[:, 0:4], IGs[:], ALU.max, [Bs, IGs], [MTs])
            tt("dve", WPs[:, 0:4], Bs[:, 0:4], MTs[:, 0:4], ALU.subtract, [Bs, MTs], [WPs])
            act(WPs[:, 0:4], WPs[:, 0:4], AF.Exp, [WPs], [WPs])
            tt("dve", Ps[:, 0:4], IGs[:], MTs[:, 0:4], ALU.subtract, [IGs, MTs], [Ps])
            act(Ps[:, 0:4], Ps[:, 0:4], AF.Exp, [Ps], [Ps])
            dma("sp", om_d, MTs[:, 0:4], [MTs], [])

            for h in range(4):
                tr(BK[2][0:R_, h * 128:(h + 1) * 128], GQs[:, h, :], IDF, [GQs, CONST], [BK[2]])
                tr(BK[3][0:R_, h * 128:(h + 1) * 128], GKs[:, h, :], IDF, [GKs, CONST], [BK[3]])
                tr(BK[4][0:R_, h * 128:(h + 1) * 128], GVs[:, h, :], IDF, [GVs, CONST], [BK[4]])
                tr(BK[5][0:R_, h * 64:(h + 1) * 64], MQs[:, h, :], CONST[0:64, C_IDF:C_IDF + 64], [MQs, CONST], [BK[5]])
                tr(BK[5][0:R_, 256 + h * 64:256 + (h + 1) * 64], MKs[:, h, :], CONST[0:64, C_IDF:C_IDF + 64],
                   [MKs, CONST], [BK[5]])
            cp("act", Qt[:].rearrange("p h d -> p (h d)"), BK[2][0:R_, :], [BK[2]], [Qt])
            cp("dve", Kt[:].rearrange("p h d -> p (h d)"), BK[3][0:R_, :], [BK[3]], [Kt])
            cp("act", Vt[:].rearrange("p h d -> p (h d)"), BK[4][0:R_, :], [BK[4]], [Vt])
            cp("dve", MQt[:].rearrange("p h d -> p (h d)"), BK[5][0:R_, 0:256], [BK[5]], [MQt])
            cp("act", MKtt[:].rearrange("p h d -> p (h d)"), BK[5][0:R_, 256:512], [BK[5]], [MKtt])
            tt("dve", T1[:], Qt[:].rearrange("p h d -> p (h d)"), Kt[:].rearrange("p h d -> p (h d)"), ALU.mult,
               [Qt, Kt], [T1])
            red("dve", QKG[:, 0:4], T1[:].rearrange("p (h d) -> p h d", h=4), ALU.add, [T1], [QKG])
            tt("dve", T1[:, 0:256], MQt[:].rearrange("p h d -> p (h d)"), MKtt[:].rearrange("p h d -> p (h d)"),
               ALU.mult, [MQt, MKtt], [T1])
            red("dve", QKMs[:, 0:4], T1[:, 0:256].rearrange("p (h d) -> p h d", h=4), ALU.add, [T1], [QKMs])
            tt("dve", T1[:, 0:256], MQt[:].rearrange("p h d -> p (h d)"), N0[:].rearrange("p h d -> p (h d)"),
               ALU.mult, [MQt, N0], [T1])
            red("dve", QNs[:, 0:4], T1[:, 0:256].rearrange("p (h d) -> p h d", h=4), ALU.add, [T1], [QNs])
            tt("dve", PKt[:], MKtt[:], bc3(Ps[:, 0:4], 4, 64), ALU.mult, [MKtt, Ps], [PKt])
            tt("dve", N0[:], N0[:], bc3(WPs[:, 0:4], 4, 64), ALU.mult, [N0, WPs], [N0])
            tt("dve", N0[:], N0[:], PKt[:], ALU.add, [N0, PKt], [N0])
            dma("sp", on_d, N0[:].rearrange("p h d -> p (h d)"), [N0], [])

            tt("dve", DIAGI[:], bc3(IDF16, R_, R_), bcm(IDF16, R_), ALU.mult, [CONST], [DIAGI])
            mm(BK[2][:, 0:R_ * R_], ONES16, DIAGI[:].rearrange("p a b -> p (a b)"), True, True, [CONST, DIAGI], [BK[2]])
            cp("dve", M16[:].rearrange("p a b -> p (a b)"), BK[2][:, 0:R_ * R_], [BK[2]], [M16])
            tt("dve", DG4[:], bcm(EGs[:], R_), bc3(IDF16, R_, 4), ALU.mult, [EGs, CONST], [DG4])
            mm(BK[3][:, 0:64], ONES16, DG4[:].rearrange("p a b -> p (a b)"), True, True, [CONST, DG4], [BK[3]])
            cp("dve", EGBC[:], BK[3][:, 0:64], [BK[3]], [EGBC])
            tt("dve", DG4[:], bcm(WPs[:, 0:4], R_), bc3(IDF16, R_, 4), ALU.mult, [WPs, CONST], [DG4])
            mm(BK[3][:, 0:64], ONES16, DG4[:].rearrange("p a b -> p (a b)"), True, True, [CONST, DG4], [BK[3]])
            cp("dve", WPBC[:], BK[3][:, 0:64], [BK[3]], [WPBC])

            def ld(r, i):
                dma("sp", S0b[i][:], sS_d[r].rearrange("h d e -> d h e"), [], [S0b[i]])
                dma("act", C0b[i][:], sC_d[r].rearrange("h d e -> d h e"), [], [C0b[i]])
            ld(0, 0)
            for r in range(R_):
                i = r % 2
                if r + 1 < R_:
                    ld(r + 1, 1 - i)
                tt("dve", KD[i][:], GKs[:], bcm(M16[:, r, :], 4), ALU.mult, [GKs, M16], [KD[i]])
                tt("pool", QD[i][:], GQs[:], bcm(M16[:, r, :], 4), ALU.mult, [GQs, M16], [QD[i]])
                tt("dve", QDm[i][:], MQs[:], bcm(M16[0:64, r, :], 4), ALU.mult, [MQs, M16], [QDm[i]])
                for h in range(4):
                    sl = slice(h * 128, (h + 1) * 128)
                    st_, sp_ = (r == 0 and h == 0), (r == R_ - 1 and h == 3)
                    mm(BK[2][0:R_, sl], KD[i][:, h, :], S0b[i][:, h, :], st_, sp_, [KD[i], S0b[i]], [BK[2]], skip=True)
                    mm(BK[3][0:R_, sl], QD[i][:, h, :], S0b[i][:, h, :], st_, sp_, [QD[i], S0b[i]], [BK[3]], skip=True)
                    mm(BK[5][0:R_, sl], QDm[i][:, h, :], C0b[i][:, h, :], st_, sp_, [QDm[i], C0b[i]], [BK[5]], skip=True)
            KSp, QSp, QCp = BK[2], BK[3], BK[5]
            tt("dve", Ug[:], KSp[0:R_, :].rearrange("p (h e) -> p h e", h=4), bc3(EGs[:], 4, 128), ALU.mult,
               [KSp, EGs], [Ug])
            tt("dve", Ug[:], Vt[:], Ug[:], ALU.subtract, [Vt, Ug], [Ug])
            tt("dve", Ug[:], Ug[:], bc3(BETAs[:], 4, 128), ALU.mult, [Ug, BETAs], [Ug])
            ld(0, 0)
            for r in range(R_):
                i = r % 2
                if r + 1 < R_:
                    ld(r + 1, 1 - i)
                ts("dve", KROW[:], Kt[:].rearrange("p h d -> p (h d)"), IDF16[:, r:r + 1], None, ALU.mult, None,
                   [Kt, CONST], [KROW])
                ts("dve", PKROW[:], PKt[:].rearrange("p h d -> p (h d)"), IDF16[:, r:r + 1], None, ALU.mult, None,
                   [PKt, CONST], [PKROW])
                for h in range(4):
                    mm(BK[4][:, h * 128:(h + 1) * 128], KROW[:, h * 128:(h + 1) * 128], Ug[:, h, :], True, True,
                       [KROW, Ug], [BK[4]])
                for h in range(4):
                    mm(BK[6][0:64, h * 128:(h + 1) * 128], PKROW[:, h * 64:(h + 1) * 64], MVs[:, h, :], True, True,
                       [PKROW, MVs], [BK[6]])
                for h in range(4):
                    stt("dve", S0b[i][:, h, :], S0b[i][:, h, :], EGBC[:, r * 4 + h:r * 4 + h + 1],
                        BK[4][:, h * 128:(h + 1) * 128], ALU.mult, ALU.add, [S0b[i], EGBC, BK[4]], [S0b[i]])
                    stt("dve", C0b[i][:, h, :], C0b[i][:, h, :], WPBC[0:64, r * 4 + h:r * 4 + h + 1],
                        BK[6][0:64, h * 128:(h + 1) * 128], ALU.mult, ALU.add, [C0b[i], WPBC, BK[6]], [C0b[i]])
                dma("sp", oS_d[r].rearrange("h d e -> d h e"), S0b[i][:], [S0b[i]], [])
                dma("act", oC_d[r].rearrange("h d e -> d h e"), C0b[i][:], [C0b[i]], [])

            MIXs = XNs
            tt("dve", T1[:].rearrange("p (h e) -> p h e", h=4), QSp[0:R_, :].rearrange("p (h e) -> p h e", h=4),
               bc3(EGs[:], 4, 128), ALU.mult, [QSp, EGs], [T1])
            tt("dve", Ug[:], Ug[:], bc3(QKG[:, 0:4], 4, 128), ALU.mult, [Ug, QKG], [Ug])
            tt("dve", T1[:], T1[:], Ug[:].rearrange("p h e -> p (h e)"), ALU.add, [T1, Ug], [T1])
            hn_gate16(T1, ZSGs, MIXs[:, 0:512], MIXs)
            PQ = SMs[5]
            tt("dve", PQ[:, 4:8], Ps[:, 0:4], QKMs[:, 0:4], ALU.mult, [Ps, QKMs], [PQ])
            tt("dve", T1[:].rearrange("p (h e) -> p h e", h=4), QCp[0:R_, :].rearrange("p (h e) -> p h e", h=4),
               bc3(WPs[:, 0:4], 4, 128), ALU.mult, [QCp, WPs], [T1])
            tt("dve", Ug[:], MVs[:], bc3(PQ[:, 4:8], 4, 128), ALU.mult, [MVs, PQ], [Ug])
            tt("dve", T1[:], T1[:], Ug[:].rearrange("p h e -> p (h e)"), ALU.add, [T1, Ug], [T1])
            DENs, EMTs = SMs[6], SMs[7]
            tt("dve", DENs[:, 4:8], WPs[:, 0:4], QNs[:, 0:4], ALU.mult, [WPs, QNs], [DENs])
            tt("dve", DENs[:, 4:8], DENs[:, 4:8], PQ[:, 4:8], ALU.add, [DENs, PQ], [DENs])
            ts("dve", DENs[:, 0:4], DENs[:, 4:8], -1.0, None, ALU.mult, None, [DENs], [DENs])
            tt("dve", DENs[:, 4:8], DENs[:, 4:8], DENs[:, 0:4], ALU.max, [DENs], [DENs])
            act(EMTs[:, 4:8], MTs[:, 0:4], AF.Exp, [MTs], [EMTs], scale=-1.0)
            tt("dve", DENs[:, 4:8], DENs[:, 4:8], EMTs[:, 4:8], ALU.max, [DENs, EMTs], [DENs])
            P.op("dve", lambda e, d=DENs: e.reciprocal(out=d[:, 4:8], in_=d[:, 4:8]), [DENs], [DENs])
            tt("dve", T1[:].rearrange("p (h e) -> p h e", h=4), T1[:].rearrange("p (h e) -> p h e", h=4),
               bc3(DENs[:, 4:8], 4, 128), ALU.mult, [T1, DENs], [T1])
            hn_gate16(T1, MOSGs, MIXs[:, 512:1024], MIXs)
            for k in range(8):
                tr(TPB[:, k * 128:k * 128 + R_], MIXs[:, k * 128:(k + 1) * 128], IDB[0:R_, 0:R_], [MIXs, IDB], [TPB])
            cp("act", HTs[:], TPB[:].rearrange("p (k t) -> p k t", k=8)[:, :, 0:R_], [TPB], [HTs])
            for eh in range(2):
                for k in range(8):
                    mm(BK[eh][0:R_, :], HTs[:, k, :], WOUT[:, k, eh * 512:(eh + 1) * 512], k == 0, k == 7,
                       [HTs, WOUT], [BK[eh]])
            ss, rs = SMs[8], SMs[9]
            for eh in range(2):
                act(XNs[:, eh * 512:(eh + 1) * 512], BK[eh][0:R_, :], AF.Square, [BK[eh]], [XNs, ss],
                    accum=ss[:, eh:eh + 1])
            tt("dve", ss[:, 2:3], ss[:, 0:1], ss[:, 1:2], ALU.add, [ss], [ss])
            rsqrt_small(rs[:, 0:1], ss[:, 2:3], 1.0 / D, [ss], [rs], rs[:, 1:2])
            for eh in range(2):
                sl = slice(eh * 512, (eh + 1) * 512)
                stt("dve", T1[:], BK[eh][0:R_, :], rs[:, 0:1], GPM[0:R_, sl], ALU.mult, ALU.mult, [BK[eh], rs, GPM], [T1])
                tt("dve", XS[:, sl], XS[:, sl], T1[:], ALU.add, [XS, T1], [XS])
            for k in range(8):
                tr(BK[2][:, k * R_:(k + 1) * R_], XS[:, k * 128:(k + 1) * 128], IDF16, [XS, CONST], [BK[2]])
            cp("dve", XS1T[:].rearrange("p k r -> p (k r)"), BK[2][:, 0:8 * R_], [BK[2]], [XS1T])
            act(XNs[:], XS[:], AF.Square, [XS], [XNs, ss], accum=ss[:, 4:5])
            rsqrt_small(rs[:, 4:5], ss[:, 4:5], 1.0 / D, [ss], [rs], rs[:, 5:6])
            ts("dve", XNs[:], XS[:], rs[:, 4:5], None, ALU.mult, None, [XS, rs], [XNs])
            for k in range(8):
                tr(TPB[:, k * 128:k * 128 + R_], XNs[:, k * 128:(k + 1) * 128], IDB[0:R_, 0:R_], [XNs, IDB], [TPB])
            tt("dve", HNS[:], TPB[:].rearrange("p (k t) -> p k t", k=8)[:, :, 0:R_], bc3(GPREMLP[:], 8, R_), ALU.mult,
               [TPB, GPREMLP], [HNS])

            P.barrier(all_res_p1 + RX + [XS1T.r, HNS.r, IDF2.r, IDB.r, GPREMLP.r])
            p1b.close()

        with ExitStack() as p2:
            WUP = p2.enter_context(nc.sbuf_tensor("sb_WUP", [128, 8, DFF], BF16))
            WDN = p2.enter_context(nc.sbuf_tensor("sb_WDN", [128, 32, D], BF16))
            NWC = 8
            RWU = [Res("WUP%d" % i) for i in range(NWC)]
            RWD = [Res("WDN%d" % i) for i in range(NWC)]
            wup_v = wup_d.rearrange("(k p) c -> p k c", p=128)
            wdn_v = wdn_d.rearrange("(k p) c -> p k c", p=128)
            for i in range(NWC if KPH2 else 0):
                dma("pool", WUP[:, :, i * 512:(i + 1) * 512], wup_v[:, :, i * 512:(i + 1) * 512], [], [RWU[i]])
                dma("pool", WDN[:, i * 4:(i + 1) * 4, :], wdn_v[:, i * 4:(i + 1) * 4, :], [], [RWD[i]])
            XN2 = sb(p2, "XN2", [128, D], BF16)
            GPL = sb(p2, "GPL", [128, D])
            dma("sp", GPL[:], gpl_d.partition_broadcast(128), [], [GPL])
            HN = sb(p2, "HN", [128, 8, 256], BF16)
            UT = [sb(p2, "UT%d" % i, [128, 256], BF16) for i in range(2)]
            RL = [sb(p2, "RL%d" % i, [128, 256]) for i in range(2)]
            SS2 = sb(p2, "SS2", [128, 16])

            NB2 = NT // 2 if KPH2 else 0

            def xv_of(blk, j):
                return Xt[:, 2 * blk + j, :], RX[2 * blk + j]

            def prep(blk):
                for j in range(2):
                    xv, rx = xv_of(blk, j)
                    act(XN2[:], xv, AF.Square, [rx], [XN2, SS2], accum=SS2[:, 0:1])
                    rsqrt_small(SS2[:, 1:2], SS2[:, 0:1], 1.0 / D, [SS2], [SS2], SS2[:, 2:3])
                    ts("dve", XN2[:], xv, SS2[:, 1:2], None, ALU.mult, None, [rx, SS2], [XN2])
                    for k in range(8):
                        tr(TPB[:, k * 128:(k + 1) * 128], XN2[:, k * 128:(k + 1) * 128], IDB[:], [XN2, IDB], [TPB])
                    tt("dve", HN[:, :, j * 128:(j + 1) * 128], TPB[:].rearrange("p (k t) -> p k t", k=8),
                       bc3(GPREMLP[:], 8, 128), ALU.mult, [TPB, GPREMLP], [HN])

            def up(f, hn, n):
                bk = BK[f % 2]
                for k in range(8):
                    mm(bk[:, 0:n], WUP[:, k, f * 128:(f + 1) * 128], hn[:, k, 0:n], k == 0, k == 7,
                       [hn, RWU[f // 4]], [bk])
                rl, ut = RL[f % 2], UT[f % 2]
                act(rl[:, 0:n], bk[:, 0:n], AF.Relu, [bk], [rl])
                tt("dve", ut[:, 0:n], rl[:, 0:n], rl[:, 0:n], ALU.mult, [rl], [ut])

            def down(f, rows, ntl):
                ut = UT[f % 2]
                for j in range(ntl):
                    for eh in range(2):
                        ab = BK[2 + 2 * j + eh]
                        mm(ab[0:rows, :], ut[:, j * rows:(j + 1) * rows], WDN[:, f, eh * 512:(eh + 1) * 512],
                           f == 0, f == 31, [ut, RWD[f // 4]], [ab])

            def fin(blk):
                for j in range(2):
                    for eh in range(2):
                        ab = BK[2 + 2 * j + eh]
                        act(XN2[:, eh * 512:(eh + 1) * 512], ab[:], AF.Square, [ab], [XN2, SS2],
                            accum=SS2[:, 8 + 2 * j + eh:9 + 2 * j + eh])
                for j in range(2):
                    tt("dve", SS2[:, 12 + j:13 + j], SS2[:, 8 + 2 * j:9 + 2 * j], SS2[:, 9 + 2 * j:10 + 2 * j], ALU.add,
                       [SS2], [SS2])
                ts("dve", SS2[:, 14:16], SS2[:, 12:14], 1.0 / D, EPS, ALU.mult, ALU.add, [SS2], [SS2])
                act(SS2[:, 14:16], SS2[:, 14:16], AF.Ln, [SS2], [SS2])
                act(SS2[:, 14:16], SS2[:, 14:16], AF.Exp, [SS2], [SS2], scale=-0.5)
                for j in range(2):
                    xv, rx = xv_of(blk, j)
                    for eh in range(2):
                        ab = BK[2 + 2 * j + eh]
                        sl = slice(eh * 512, (eh + 1) * 512)
                        stt("dve", ab[:], ab[:], SS2[:, 14 + j:15 + j], GPL[:, sl], ALU.mult, ALU.mult,
                            [ab, SS2, GPL], [ab])
                        tt("dve", xv[:, sl], xv[:, sl], ab[:], ALU.add, [rx, ab], [rx])
                    r0 = (2 * blk + j) * 128
                    dma("sp", y_d[r0:r0 + 128, :], xv, [rx], [])

            if NB2:
                prep(0)
            for blk in range(NB2):
                up(0, HN, 256)
                for f in range(32):
                    if f + 1 < 32:
                        up(f + 1, HN, 256)
                    elif blk + 1 < NB2:
                        prep(blk + 1)
                    down(f, 128, 2)
                fin(blk)

            R_ = NS
            if KPH2:
                up(0, HNS, R_)
                for f in range(32):
                    if f + 1 < 32:
                        up(f + 1, HNS, R_)
                    down(f, R_, 1)
            if KPH2:
                for eh in range(2):
                    act(XN2[0:R_, eh * 512:(eh + 1) * 512], BK[2 + eh][0:R_, :], AF.Square, [BK[2 + eh]], [XN2, SS2],
                        accum=SS2[0:R_, 3 + eh:4 + eh])
                tt("dve", SS2[0:R_, 5:6], SS2[0:R_, 3:4], SS2[0:R_, 4:5], ALU.add, [SS2], [SS2])
                rsqrt_small(SS2[0:R_, 6:7], SS2[0:R_, 5:6], 1.0 / D, [SS2], [SS2], SS2[0:R_, 7:8])
                YSB = XN2.t.bitcast(F32)
                for eh in range(2):
                    sl = slice(eh * 512, (eh + 1) * 512)
                    stt("dve", BK[2 + eh][0:R_, :], BK[2 + eh][0:R_, :], SS2[0:R_, 6:7], GPL[0:R_, sl], ALU.mult, ALU.mult,
                        [BK[2 + eh], SS2, GPL], [BK[2 + eh]])
                    for j in range(4):
                        tr(BK[4 + eh][0:R_, j * 128:(j + 1) * 128], XS1T[:, 4 * eh + j, :], IDF2[:], [XS1T, IDF2],
                           [BK[4 + eh]])
                    cp("act", YSB[0:R_, :], BK[4 + eh][0:R_, :], [BK[4 + eh]], [XN2])
                    tt("dve", YSB[0:R_, :], YSB[0:R_, :], BK[2 + eh][0:R_, :], ALU.add, [XN2, BK[2 + eh]], [XN2])
                    dma("sp", ys_d[:, sl], YSB[0:R_, :], [XN2], [])

        n_ins = P.finalize(top)
    return nc, n_ins


_CACHE = {}


def kernel(x_prompt, x_sample, state_gdn_conv, state_gdn_S, state_mlstm_C, state_mlstm_n, state_mlstm_m,
           norm_pre_mix, w_in, conv_w, a_log, dt_bias, gdn_norm_g, b_igate, b_fgate, mlstm_norm_g, w_out,
           norm_post_mix, norm_pre_mlp, w_up, w_down, norm_post_mlp):
    f = lambda a: np.ascontiguousarray(np.asarray(a, dtype=np.float32))
    if "nc" not in _CACHE:
        _CACHE["nc"] = build_program()
    nc, _ = _CACHE["nc"]
    consts = make_consts()
    small = np.concatenate([f(a_log)[0], f(dt_bias)[0], f(b_igate)[0], f(b_fgate)[0]])[None, :]
    shared = {
        "w_in": f(w_in)[0], "w_out": f(w_out)[0], "w_up": f(w_up)[0], "w_down": f(w_down)[0],
        "consts": consts,
        "gpre_fm": f(f(norm_pre_mix)[0].reshape(8, 128).T),
        "gpremlp_fm": f(f(norm_pre_mlp)[0].reshape(8, 128).T),
        "cw_fm": f(f(conv_w)[0].reshape(4, 12, 128).transpose(2, 0, 1).reshape(128, 48)),
        "gpostmix": f(norm_post_mix)[0][None, :], "gpostmlp": f(norm_post_mlp)[0][None, :],
        "small": f(small), "gdn_norm_g": f(gdn_norm_g)[0][None, :], "mlstm_norm_g": f(mlstm_norm_g)[0][None, :],
    }
    xp, xs = f(x_prompt), f(x_sample)
    in_maps = []
    for c in range(NCORES):
        r = slice(c * NS, (c + 1) * NS)
        m = dict(shared)
        m.update({
            "x": xp[c], "xs": xs[r, 0, :],
            "sconv": f(state_gdn_conv)[0, r], "sS": f(state_gdn_S)[0, r], "sC": f(state_mlstm_C)[0, r],
            "sn": f(state_mlstm_n)[0, r].reshape(NS, 256), "sm": f(state_mlstm_m)[0, r],
        })
        in_maps.append(m)
    res = run_bass_kernel_spmd(nc, in_maps, core_ids=list(range(NCORES)))
    R = res.results
    g = lambda k: np.stack([np.asarray(R[c][k], dtype=np.float32) for c in range(NCORES)])
    gc = lambda k: np.concatenate([np.asarray(R[c][k], dtype=np.float32) for c in range(NCORES)], axis=0)
    y_prompt = g("y")
    y_sample = gc("ys")[:, None, :]
    p_conv = g("pconv")[None]
    p_S = g("pS")[None]
    p_C = g("pC")[None]
    p_n = g("pn")[None]
    p_m = g("pm").reshape(NCORES, 4)[None]
    s_conv = gc("oconv")[None]
    s_S = gc("oS")[None]
    s_C = gc("oC")[None]
    s_n = gc("on").reshape(NCORES * NS, 4, 64)[None]
    s_m = gc("om")[None]
    return (y_prompt, y_sample, p_conv, p_S, p_C, p_n, p_m, s_conv, s_S, s_C, s_n, s_m)
```

```python
from contextlib import ExitStack
import numpy as np
import concourse.bass as bass
import concourse.mybir as mybir
from concourse.bass_utils import run_bass_kernel_spmd

F32 = mybir.dt.float32
BF16 = mybir.dt.bfloat16
ALU = mybir.AluOpType
AF = mybir.ActivationFunctionType
AX = mybir.AxisListType

NCORES = 8
T = 2048
NT = T // 128
D = 1024
DFF = 4096
NS = 16
EPS = 1e-6
NEG = -30000.0
import os
KNT = int(os.environ.get('KNT', NT))
KPH2 = int(os.environ.get('KPH2', 1))
KSTAGE = int(os.environ.get('KSTAGE', 99))
KSUB = int(os.environ.get('KSUB', 99))
KRATIO = [int(v) for v in os.environ.get('KRATIO', '3,2,1').split(',')]
KSCHED = int(os.environ.get('KSCHED', 0))
KSEG = int(os.environ.get('KSEG', 7))
KLO = int(os.environ.get('KLO', 0))
KHI = int(os.environ.get('KHI', 99))


class Res:
    __slots__ = ("name", "w", "rd")

    def __init__(self, name):
        self.name = name
        self.w = None
        self.rd = []


class Op:
    __slots__ = ("eng", "fn", "reads", "writes", "dma", "deps", "signal", "cnt", "sem", "waits", "cost", "tab")

    def __init__(self, eng, fn, reads, writes, dma, cost=0.4, tab=None):
        self.cost = cost
        self.tab = tab
        self.eng = eng
        self.fn = fn
        self.reads = reads
        self.writes = writes
        self.dma = dma
        self.deps = []
        self.signal = False
        self.cnt = 0
        self.sem = None
        self.waits = []


def _res(lst):
    out = []
    for x in lst:
        if x is None:
            continue
        if isinstance(x, Res):
            out.append(x)
        elif isinstance(x, (list, tuple)):
            out.extend(_res(x))
        else:
            out.append(x.r)
    return out


class Prog:
    ENGS = ("pe", "act", "dve", "pool", "sp")

    def __init__(self, nc, n_dma_sems=56):
        self.nc = nc
        self.ops = []
        self.n_dma_sems = n_dma_sems
        self.n_sw_sems = 8
        self.fence = Res("fence")
        self.marks = []
        self.engobj = {"pe": nc.tensor, "act": nc.scalar, "dve": nc.vector,
                       "pool": nc.gpsimd, "sp": nc.sync}

    def op(self, eng, fn, r=(), w=(), cost=0.4, tab=None):
        self.ops.append(Op(eng, fn, _res(r) + ([self.fence] if KSCHED else []), _res(w), False, cost, tab))

    def dma(self, eng, fn, r=(), w=(), cost=2.5):
        self.ops.append(Op(eng, fn, _res(r) + ([self.fence] if KSCHED else []), _res(w), True, cost))

    def barrier(self, allres):
        for e in ("pe", "act", "dve", "pool", "sp"):
            self.ops.append(Op(e, (lambda en: en.nop(nofuse=True)), [], _res(allres) + [self.fence], False, 0.1))

    def _schedule(self):
        ops = self.ops
        n = len(ops)
        LAT = 0.2
        succ = [[] for _ in range(n)]
        for i, o in enumerate(ops):
            for j in o.deps:
                succ[j].append(i)
        prio = [0.0] * n
        for i in range(n - 1, -1, -1):
            m = 0.0
            for k in succ[i]:
                if prio[k] + LAT > m:
                    m = prio[k] + LAT
            prio[i] = ops[i].cost + m
        if KSCHED == 2:
            prio = [float(n - i) for i in range(n)]
        ndep = [len(o.deps) for o in ops]
        ready = [0.0] * n
        finish = [0.0] * n
        avail = {e: [] for e in self.ENGS}
        for i, o in enumerate(ops):
            if ndep[i] == 0:
                avail[o.eng].append(i)
        free = {e: 0.0 for e in self.ENGS}
        lasttab = None
        order = []
        start = [0.0] * n
        done = 0
        while done < n:
            best_e, best_i, best_t = None, None, None
            for e in self.ENGS:
                av = avail[e]
                if not av:
                    continue
                fe = free[e]
                cand = None
                cs = None
                tmin_i, tmin = None, None
                for i in av:
                    r = ready[i]
                    if tmin is None or r < tmin or (r == tmin and i < tmin_i):
                        tmin, tmin_i = r, i
                    if r <= fe + 1e-9:
                        sc = prio[i]
                        if e == "act" and ops[i].tab is not None and lasttab is not None and ops[i].tab != lasttab:
                            sc -= 4.0
                        if cs is None or sc > cs or (sc == cs and i < cand):
                            cand, cs = i, sc
                if cand is None:
                    cand = tmin_i
                t0 = max(fe, ready[cand])
                if best_t is None or t0 < best_t:
                    best_e, best_i, best_t = e, cand, t0
            i = best_i
            o = ops[i]
            avail[best_e].remove(i)
            c = o.cost
            if best_e == "act" and o.tab is not None:
                if lasttab is not None and o.tab != lasttab:
                    c += 1.3
                lasttab = o.tab
            start[i] = best_t
            if o.dma:
                free[best_e] = best_t + 0.1
                finish[i] = best_t + c
            else:
                free[best_e] = best_t + c
                finish[i] = best_t + c
            order.append(i)
            done += 1
            for k in succ[i]:
                ndep[k] -= 1
                if finish[i] + LAT > ready[k]:
                    ready[k] = finish[i] + LAT
                if ndep[k] == 0:
                    avail[ops[k].eng].append(k)
        newpos = {old: new for new, old in enumerate(order)}
        newops = [ops[i] for i in order]
        for o in newops:
            o.deps = sorted(newpos[j] for j in o.deps)
        self.ops = newops
        self.est_us = max(finish) if n else 0.0

    def finalize(self, stack):
        nc = self.nc
        ops = self.ops
        for i, o in enumerate(ops):
            deps = set()
            for r in o.reads:
                if r.w is not None:
                    deps.add(r.w)
            for r in o.writes:
                if r.w is not None:
                    deps.add(r.w)
                for j in r.rd:
                    deps.add(j)
            deps.discard(i)
            o.deps = sorted(deps)
            for r in o.reads:
                r.rd.append(i)
            for r in o.writes:
                r.w = i
                r.rd = []
        if KSCHED:
            seg = 0
            last = {}
            nbar = 0
            for i, o in enumerate(ops):
                if o.cost == 0.1 and not o.dma and o.writes and o.writes[-1] is self.fence:
                    nbar += 1
                    if nbar % 5 == 1:
                        seg += 1
                        last = {}
                free_ = (KSEG >> min(seg, 2)) & 1
                if seg == 0 and self.marks:
                    lo = self.marks[min(KLO, len(self.marks) - 1)]
                    hi = self.marks[min(KHI, len(self.marks) - 1)] if KHI < len(self.marks) else 10 ** 9
                    free_ = free_ and (lo <= i < hi)
                if not free_:
                    if o.eng in last and last[o.eng] not in o.deps:
                        o.deps = sorted(o.deps + [last[o.eng]])
                    last[o.eng] = i
            self._schedule()
            ops = self.ops
        dma_slot_last = [None] * self.n_dma_sems
        dma_i = 0
        sw_i = 0
        for i, o in enumerate(ops):
            if o.dma:
                if o.eng == "pool":
                    slot = sw_i % self.n_sw_sems
                    sw_i += 1
                else:
                    slot = self.n_sw_sems + dma_i % (self.n_dma_sems - self.n_sw_sems)
                    dma_i += 1
                o.sem = slot
                if dma_slot_last[slot] is not None and dma_slot_last[slot] not in o.deps:
                    o.deps = sorted(o.deps + [dma_slot_last[slot]])
                dma_slot_last[slot] = i
            for j in o.deps:
                pj = ops[j]
                if pj.dma:
                    continue
                if pj.eng == "pe" and o.eng == "pe" and not o.dma:
                    continue
                pj.signal = True
        cnt = {e: 0 for e in self.ENGS}
        dcnt = [0] * self.n_dma_sems
        for o in ops:
            if o.dma:
                dcnt[o.sem] += 16
                o.cnt = dcnt[o.sem]
            elif o.signal:
                cnt[o.eng] += 1
                o.cnt = cnt[o.eng]
        seen = {e: {} for e in self.ENGS}
        for o in ops:
            need = {}
            for j in o.deps:
                pj = ops[j]
                if pj.dma:
                    key = ("d", pj.sem)
                else:
                    if pj.eng == "pe" and o.eng == "pe" and not o.dma:
                        continue
                    key = ("e", pj.eng)
                if pj.cnt > need.get(key, 0):
                    need[key] = pj.cnt
            s = seen[o.eng]
            for key, v in need.items():
                if s.get(key, 0) >= v:
                    continue
                s[key] = v
                o.waits.append((key, v))
        final_waits = [(("d", k), dcnt[k]) for k in range(self.n_dma_sems) if dcnt[k] > 0]
        final_waits += [(("e", e), cnt[e]) for e in self.ENGS if cnt[e] > 0 and e != "sp"]
        esem = {e: stack.enter_context(nc.semaphore("s_" + e)) for e in self.ENGS}
        dsem = [stack.enter_context(nc.semaphore("d_%d" % k)) for k in range(self.n_dma_sems)]

        def semof(key):
            return dsem[key[1]] if key[0] == "d" else esem[key[1]]

        n_ins = 0
        for o in ops:
            e = self.engobj[o.eng]
            for key, v in o.waits:
                e.wait_ge(semof(key), v)
                n_ins += 1
            ins = o.fn(e)
            n_ins += 1
            if o.dma:
                ins.then_inc(dsem[o.sem], 16)
            elif o.signal:
                ins.then_inc(esem[o.eng], 1)
        sp = self.engobj["sp"]
        for key, v in final_waits:
            if seen["sp"].get(key, 0) >= v:
                continue
            sp.wait_ge(semof(key), v)
        return n_ins


class View:
    __slots__ = ("t", "r")

    def __init__(self, ap, r):
        self.t = ap
        self.r = r

    def __getitem__(self, k):
        return self.t[k]


class Tl:
    __slots__ = ("t", "r")

    def __init__(self, t, name):
        self.t = t
        self.r = Res(name)

    def __getitem__(self, k):
        return self.t[k]


C_IDF, C_TRI, C_ONES, C_SEL127, C_MST, C_MIT, C_MI, C_SELR = (
    0, 128, 256, 384, 512, 640, 768, 896)
NCONST = 1408


def make_consts():
    c = np.zeros((128, NCONST), np.float32)
    s = np.arange(128)[:, None]
    f = np.arange(128)[None, :]
    c[:, C_IDF:C_IDF + 128] = (s == f)
    c[:, C_TRI:C_TRI + 128] = (s <= f)
    c[:, C_ONES:C_ONES + 128] = 1.0
    c[:, C_SEL127:C_SEL127 + 128] = (s == 127)
    mst = np.where(s < f, 0.0, NEG)
    mit = np.where(s <= f, 0.0, NEG)
    mi = np.where(f <= s, 0.0, NEG)
    c[:, C_MST:C_MST + 128] = mst
    c[:, C_MIT:C_MIT + 128] = mit
    c[:, C_MI:C_MI + 128] = mi
    for h in range(4):
        c[h, C_SELR + h * 128:C_SELR + (h + 1) * 128] = 1.0
    return c


W_QKV, W_MQK, W_GZ, W_MV, W_MO, W_GATE = 0, 1536, 2048, 2560, 3072, 3584
WIN_MOVES = [(0, 0, 1536), (1536, 2056, 512), (2048, 1536, 512), (2560, 2568, 512),
             (3072, 3080, 512), (3584, 2048, 8), (3592, 3596, 4), (3596, 3592, 4)]


def build_program():
    nc = bass.Bass("TRN2", target_bir_lowering=False)
    P = Prog(nc)

    def din(name, shape):
        return nc.dram_tensor(name, list(shape), F32, kind="ExternalInput").ap()

    def dout(name, shape):
        return nc.dram_tensor(name, list(shape), F32, kind="ExternalOutput").ap()

    x_d = din("x", [T, D])
    xs_d = din("xs", [NS, D])
    sconv_d = din("sconv", [NS, 3, 1536])
    sS_d = din("sS", [NS, 4, 128, 128])
    sC_d = din("sC", [NS, 4, 64, 128])
    sn_d = din("sn", [NS, 256])
    sm_d = din("sm", [NS, 4])
    win_d = din("w_in", [D, 3600])
    wout_d = din("w_out", [D, D])
    wup_d = din("w_up", [D, DFF])
    wdn_d = din("w_down", [DFF, D])
    consts_d = din("consts", [128, NCONST])
    gpre_d = din("gpre_fm", [128, 8])
    gpremlp_d = din("gpremlp_fm", [128, 8])
    cw_d = din("cw_fm", [128, 48])
    gpm_d = din("gpostmix", [1, D])
    gpl_d = din("gpostmlp", [1, D])
    small_d = din("small", [1, 16])
    gng_d = din("gdn_norm_g", [1, 128])
    mng_d = din("mlstm_norm_g", [1, 128])

    y_d = dout("y", [T, D])
    ys_d = dout("ys", [NS, D])
    pconv_d = dout("pconv", [3, 1536])
    pS_d = dout("pS", [4, 128, 128])
    pC_d = dout("pC", [4, 64, 128])
    pn_d = dout("pn", [4, 64])
    pm_d = dout("pm", [1, 4])
    oconv_d = dout("oconv", [NS, 3, 1536])
    oS_d = dout("oS", [NS, 4, 128, 128])
    oC_d = dout("oC", [NS, 4, 64, 128])
    on_d = dout("on", [NS, 256])
    om_d = dout("om", [NS, 4])

    def fsz(ap):
        try:
            return int(ap.free_size())
        except Exception:
            return 128

    def ecost(eng, ap):
        n = fsz(ap)
        if eng == "act":
            return 0.23 + n / 1200.0
        if eng == "pool":
            return 0.12 + n / 480.0
        return 0.08 + n / 960.0

    def mm(out, lhsT, rhs, start, stop, r, w, skip=False):
        passes = 4 if lhsT.dtype == F32 else 1
        c = 0.035 + passes * fsz(out) / 2400.0
        if skip:
            P.op("pe", lambda e, o=out, l=lhsT, rr=rhs, s=start, t=stop:
                 e.matmul(o, lhsT=l, rhs=rr, start=s, stop=t, skip_group_check=True), r, w, cost=c)
        else:
            P.op("pe", lambda e, o=out, l=lhsT, rr=rhs, s=start, t=stop:
                 e.matmul(o, lhsT=l, rhs=rr, start=s, stop=t), r, w, cost=c)

    def tr(out, in_, ident, r, w):
        P.op("pe", lambda e, o=out, i=in_, d=ident: e.transpose(o, i, d), r, w, cost=0.04 + fsz(out) / 2400.0)

    def tt(eng, out, in0, in1, op, r, w):
        P.op(eng, lambda e, o=out, a=in0, b=in1, p=op: e.tensor_tensor(out=o, in0=a, in1=b, op=p), r, w,
             cost=ecost(eng, out))

    def ts(eng, out, in0, s1, s2, op0, op1, r, w, accum=None):
        c = ecost(eng, out)
        if op1 is None:
            P.op(eng, lambda e, o=out, a=in0, x=s1, p0=op0:
                 e.tensor_scalar(out=o, in0=a, scalar1=x, scalar2=None, op0=p0), r, w, cost=c)
        elif accum is None:
            P.op(eng, lambda e, o=out, a=in0, x=s1, y=s2, p0=op0, p1=op1:
                 e.tensor_scalar(out=o, in0=a, scalar1=x, scalar2=y, op0=p0, op1=p1), r, w, cost=c)
        else:
            P.op(eng, lambda e, o=out, a=in0, x=s1, y=s2, p0=op0, p1=op1, ac=accum:
                 e.tensor_scalar(out=o, in0=a, scalar1=x, scalar2=y, op0=p0, op1=p1, accum_out=ac), r, w, cost=c)

    def stt(eng, out, in0, scalar, in1, op0, op1, r, w):
        P.op(eng, lambda e, o=out, a=in0, s=scalar, b=in1, p0=op0, p1=op1:
             e.scalar_tensor_tensor(out=o, in0=a, scalar=s, in1=b, op0=p0, op1=p1), r, w, cost=ecost(eng, out))

    def act(out, in_, func, r, w, bias=None, scale=1.0, accum=None):
        kw = {}
        if bias is not None:
            kw["bias"] = bias
        if accum is not None:
            kw["accum_out"] = accum
        tab = None
        if func in (AF.Silu, AF.Sigmoid):
            tab = "S"
        elif func in (AF.Exp, AF.Ln):
            tab = "E"
        P.op("act", lambda e, o=out, i=in_, f=func, s=scale, k=kw:
             e.activation(out=o, in_=i, func=f, scale=s, **k), r, w, cost=ecost("act", out), tab=tab)

    def cp(eng, out, in_, r, w):
        if eng == "act":
            act(out, in_, AF.Copy, r, w)
        else:
            P.op(eng, lambda e, o=out, i=in_: e.tensor_copy(out=o, in_=i), r, w, cost=ecost(eng, out))

    def red(eng, out, in_, op, r, w):
        P.op(eng, lambda e, o=out, i=in_, p=op: e.tensor_reduce(out=o, in_=i, axis=AX.X, op=p), r, w,
             cost=ecost(eng, in_))

    def memset(eng, ap, val, w):
        P.op(eng, lambda e, a=ap, v=val: e.memset(a, v), [], w, cost=ecost(eng, ap))

    def dma(q, out, in_, r, w, slow=False):
        if slow:
            P.dma(q, lambda e, o=out, i=in_: e.dma_start(out=o, in_=i, allow_slow_non_contiguous=True), r, w)
        else:
            P.dma(q, lambda e, o=out, i=in_: e.dma_start(out=o, in_=i), r, w)

    def rsqrt_small(out, in_, scale, r_, w_, tmp):
        act(tmp, in_, AF.Ln, list(r_) + [w_[0], EPSB], [w_[0]], bias=EPSB[0:in_.shape[0], 0:1], scale=scale)
        act(out, tmp, AF.Exp, [w_[0]], w_, scale=-0.5)

    def bc3(ap, n_mid, n_in):
        return ap.unsqueeze(2).to_broadcast([ap.shape[0], n_mid, n_in])

    def bcm(ap, n_mid):
        return ap.unsqueeze(1).to_broadcast([ap.shape[0], n_mid, ap.shape[1]])

    with ExitStack() as top:
        def sb(stack, name, shape, dt=F32):
            return Tl(stack.enter_context(nc.sbuf_tensor("sb_" + name, list(shape), dt)), name)

        def ps(stack, name, shape, dt=F32):
            return Tl(stack.enter_context(nc.psum_tensor("ps_" + name, list(shape), dt)), name)

        Xt = top.enter_context(nc.sbuf_tensor("sb_X", [128, NT, D], F32))
        RX = [Res("X%d" % t) for t in range(NT)]
        XS1T = sb(top, "XS1T", [128, 8, NS])
        HNS = sb(top, "HNS", [128, 8, NS], BF16)
        IDF2 = sb(top, "IDF2", [128, 128])
        IDB = sb(top, "IDB", [128, 128], BF16)
        GPREMLP = sb(top, "GPREMLP", [128, 8])
        EPSB = sb(top, "EPSB", [128, 1])
        BK = [ps(top, "B%d" % i, [128, 512]) for i in range(7)]
        TPB = ps(top, "TPB", [128, 1024], BF16)

        dma("sp", GPREMLP[:], gpremlp_d, [], [GPREMLP])
        memset("pool", EPSB[:], EPS, [EPSB])

        all_res_p1 = []

        with ExitStack() as p1:
            cur = [p1]

            def s1(name, shape, dt=F32):
                tl = sb(cur[0], name, shape, dt)
                all_res_p1.append(tl.r)
                return tl

            WIN = p1.enter_context(nc.sbuf_tensor("sb_WIN", [128, 8, 3600], BF16))
            RWIN = [Res("WIN%d" % i) for i in range(len(WIN_MOVES))]
            WOUT = s1("WOUT", [128, 8, D], BF16)
            CONST = s1("CONST", [128, NCONST])
            GPRE = s1("GPRE", [128, 8])
            CW = s1("CW", [128, 4, 12])
            GPM = s1("GPM", [128, D])
            SMB = s1("SMB", [128, 16])
            GNG = s1("GNG", [128, 128])
            MNG = s1("MNG", [128, 128])
            all_res_p1.extend(RWIN)

            win_v = win_d.rearrange("(k p) c -> p k c", p=128)
            for i, (dst, src, n) in enumerate(WIN_MOVES):
                dma("pool", WIN[:, :, dst:dst + n], win_v[:, :, src:src + n], [], [RWIN[i]])
            dma("pool", WOUT[:], wout_d.rearrange("(k p) c -> p k c", p=128), [], [WOUT])
            dma("sp", CONST[:], consts_d, [], [CONST])
            dma("sp", GPRE[:], gpre_d, [], [GPRE])
            dma("sp", CW[:], cw_d.rearrange("p (j c) -> p j c", j=4), [], [CW])
            dma("sp", GPM[:], gpm_d.partition_broadcast(128), [], [GPM])
            dma("sp", SMB[:], small_d.partition_broadcast(128), [], [SMB])
            dma("sp", GNG[:], gng_d.partition_broadcast(128), [], [GNG])
            dma("sp", MNG[:], mng_d.partition_broadcast(128), [], [MNG])

            IDF = CONST[:, C_IDF:C_IDF + 128]
            TRI = CONST[:, C_TRI:C_TRI + 128]
            ONES = CONST[:, C_ONES:C_ONES + 128]
            SEL127 = CONST[:, C_SEL127:C_SEL127 + 128]
            MST = CONST[:, C_MST:C_MST + 128]
            MIT = CONST[:, C_MIT:C_MIT + 128]
            MI = CONST[:, C_MI:C_MI + 128]
            SELR = CONST[0:4, C_SELR:C_SELR + 512]
            ONES4 = CONST[0:4, C_ONES:C_ONES + 128]

            def rwin(c0, c1):
                out = []
                for i, (dst, src, n) in enumerate(WIN_MOVES):
                    if dst < c1 and c0 < dst + n:
                        out.append(RWIN[i])
                return out

            cp("dve", IDB[:], IDF, [CONST], [IDB])
            cp("pool", IDF2[:], IDF, [CONST], [IDF2])

            NEGC = s1("NEGC", [128, 12])
            SGN = s1("SGN", [128, 12])
            BIA = s1("BIA", [128, 12])
            memset("pool", NEGC[:], -1.0, [NEGC])
            act(NEGC[:, 4:8], SMB[:, 0:4], AF.Exp, [SMB, NEGC], [NEGC])
            ts("dve", NEGC[:, 4:8], NEGC[:, 4:8], -1.0, None, ALU.mult, None, [NEGC], [NEGC])
            memset("pool", SGN[:], -1.0, [SGN])
            memset("pool", SGN[:, 4:8], 1.0, [SGN])
            memset("pool", BIA[:], 0.0, [BIA])
            cp("dve", BIA[:, 4:8], SMB[:, 4:8], [SMB, BIA], [BIA])
            ts("dve", BIA[:, 8:12], SMB[:, 12:16], -1.0, None, ALU.mult, None, [SMB, BIA], [BIA])

            p1a = ExitStack()
            cur[0] = p1a
            XN = s1("XN", [128, D], BF16)
            HT = s1("HT", [128, 8, 128], BF16)
            PCC = [s1("PCC%d" % i, [128, 131]) for i in range(3)]
            ACC = [s1("ACC%d" % i, [128, 128]) for i in range(3)]
            SA4 = [s1("SA4_%d" % i, [128, 8]) for i in range(2)]
            TAIL = s1("TAIL", [128, 12, 3])
            QKF = s1("QKF", [128, 8, 128])
            GQT = s1("GQT", [128, 4, 128], BF16)
            GKT = s1("GKT", [128, 4, 128], BF16)
            GVT = s1("GVT", [128, 4, 128], BF16)
            MQT = s1("MQT", [64, 4, 128], BF16)
            MKT = s1("MKT", [64, 4, 128], BF16)
            ZSG = s1("ZSG", [128, 512])
            MOSG = s1("MOSG", [128, 512])
            MVt = s1("MVt", [128, 4, 128], BF16)
            GT = s1("GT", [128, 16])
            ARG = s1("ARG", [128, 12])
            GL = s1("GL", [128, 12])
            BETA = s1("BETA", [128, 4])
            IG = s1("IG", [128, 4])
            GF = s1("GF", [128, 8])
            COLS = s1("COLS", [128, 20])
            RT = s1("RT", [4, 5, 128])
            BD = s1("BD", [4, 4, 128])
            QKD = s1("QKD", [128, 4, 128], BF16)
            MB = [s1("MB%d" % i, [128, 512]) for i in range(2)]
            LB = [s1("LB%d" % i, [128, 512]) for i in range(2)]
            RR = s1("RR", [128, 512])
            BINV = s1("BINV", [128, 4, 128], BF16)
            GKt = s1("GKt", [128, 4, 128], BF16)
            XK = s1("XK", [128, 4, 128], BF16)
            KEND = s1("KEND", [128, 4, 128], BF16)
            GVb = s1("GVb", [128, 4, 128], BF16)
            SC4 = [s1("SC4_%d" % i, [128, 8]) for i in range(6)]
            LASTG = s1("LASTG", [128, 8])
            WKT = s1("WKT", [128, 4, 128], BF16)
            S32 = s1("S32", [128, 4, 128])
            SBh = s1("SBh", [128, 4, 128], BF16)
            U = s1("U", [128, 4, 128], BF16)
            TMP = [s1("TMP%d" % i, [128, 512]) for i in range(2)]
            WV = LB[1]
            MIX = View(MB[0].t.bitcast(BF16)[:, 0:1024], MB[0].r)
            MIXT = View(RR.t.bitcast(BF16)[:, 0:1024].rearrange("p (k t) -> p k t", k=8), RR.r)
            PK = s1("PK", [128, 4, 64], BF16)
            EXPQ = TMP[1]
            EXPA = MB[0]
            MKt = s1("MKt", [128, 4, 64], BF16)
            PQKb = s1("PQKb", [128, 4, 128], BF16)
            PQKT = s1("PQKT", [128, 4, 128], BF16)
            SG4 = [s1("SG4_%d" % i, [128, 8]) for i in range(6)]
            C32 = s1("C32", [64, 4, 128])
            CBh = s1("CBh", [64, 4, 128], BF16)
            N32 = s1("N32", [64, 4])
            NBh = s1("NBh", [64, 4, 2], BF16)
            MBC = s1("MBC", [128, 4])
            DMAX = s1("DMAX", [128, 4])
            T12 = s1("T12", [128, 12])
            WLC = s1("WLC", [64, 4])
            ONEB = s1("ONEB", [128, 2], BF16)
            for b in BK:
                all_res_p1.append(b.r)
            all_res_p1.append(TPB.r)

            memset("pool", TAIL[:], 0.0, [TAIL])
            memset("pool", S32[:], 0.0, [S32])
            memset("pool", SBh[:], 0.0, [SBh])
            memset("pool", C32[:], 0.0, [C32])
            memset("pool", CBh[:], 0.0, [CBh])
            memset("pool", N32[:], 0.0, [N32])
            memset("pool", NBh[:], 0.0, [NBh])
            memset("pool", MBC[:], 0.0, [MBC])
            memset("pool", ONEB[:], 1.0, [ONEB])

            def headnorm_gate(src, gate, dst, t0, ss, rs):
                tt("pool", t0[:], src[:], src[:], ALU.mult, [src], [t0])
                red("dve", ss[:, 0:4], t0[:].rearrange("p (h e) -> p h e", h=4), ALU.add, [t0], [ss])
                rsqrt_small(rs[:, 0:4], ss[:, 0:4], 1.0 / 128, [ss], [rs], rs[:, 4:8])
                tt("dve", t0[:].rearrange("p (h e) -> p h e", h=4), src[:].rearrange("p (h e) -> p h e", h=4),
                   bc3(rs[:, 0:4], 4, 128), ALU.mult, [src, rs], [t0])
                tt("dve", dst, t0[:], gate[:], ALU.mult, [t0, gate], [MIX])

            def gen_H(th):
                for k in range(8):
                    tr(TPB[:, k * 128:(k + 1) * 128], MIX[:, k * 128:(k + 1) * 128], IDB[:], [MIX, IDB], [TPB])
                yield
                cp("act", MIXT[:].rearrange("p k t -> p (k t)"), TPB[:], [TPB], [MIXT])
                yield
                for eh in range(2):
                    for k in range(8):
                        mm(BK[4 + eh][:], MIXT[:, k, :], WOUT[:, k, eh * 512:(eh + 1) * 512], k == 0, k == 7,
                           [MIXT, WOUT], [BK[4 + eh]])
                    yield
                ss, rs = SC4[0], SC4[1]
                for eh in range(2):
                    act(MIX[:, eh * 512:(eh + 1) * 512], BK[4 + eh][:], AF.Square, [BK[4 + eh]], [MIX, ss],
                        accum=ss[:, eh:eh + 1])
                    yield
                tt("dve", ss[:, 2:3], ss[:, 0:1], ss[:, 1:2], ALU.add, [ss], [ss])
                rsqrt_small(rs[:, 0:1], ss[:, 2:3], 1.0 / D, [ss], [rs], rs[:, 1:2])
                yield
                for eh in range(2):
                    sl = slice(eh * 512, (eh + 1) * 512)
                    stt("dve", BK[4 + eh][:], BK[4 + eh][:], rs[:, 0:1], GPM[:, sl], ALU.mult, ALU.mult,
                        [BK[4 + eh], rs, GPM], [BK[4 + eh]])
                    yield
                    tt("dve", Xt[:, th, sl], Xt[:, th, sl], BK[4 + eh][:], ALU.add, [RX[th], BK[4 + eh]], [RX[th]])
                    yield

            def interleave(ga, gb, na, nb):
                a = b = True
                while a or b:
                    for _ in range(na):
                        if a:
                            try:
                                next(ga)
                            except StopIteration:
                                a = False
                    for _ in range(nb):
                        if b:
                            try:
                                next(gb)
                            except StopIteration:
                                b = False

            RB1 = [Res("B1s%d" % i) for i in range(4)]
            ORDER = [8, 9, 10, 11, 0, 1, 2, 3, 4, 5, 6, 7]

            def gen_Bconv():
                def b_mm(i):
                    ch = ORDER[i]
                    c0 = ch * 128
                    bk = BK[1 + i % 2]
                    for k in range(8):
                        mm(bk[:, 0:128], WIN[:, k, c0:c0 + 128], HT[:, k, :], k == 0, k == 7,
                           [HT] + rwin(c0, c0 + 128), [bk])

                def b_copy(i):
                    ch = ORDER[i]
                    pc = PCC[i % 3]
                    bk = BK[1 + i % 2]
                    cp("act", pc[:, 3:131], bk[:, 0:128], [bk], [pc])
                    cp("pool", pc[:, 0:3], TAIL[:, ch, :], [TAIL], [pc])

                def b_conv(i):
                    ch = ORDER[i]
                    pc, ac = PCC[i % 3], ACC[i % 3]
                    act(ac[:], pc[:, 0:128], AF.Copy, [pc, CW], [ac], scale=CW[:, 0, ch:ch + 1])
                    for j in range(1, 3):
                        stt("dve", ac[:], pc[:, j:j + 128], CW[:, j, ch:ch + 1], ac[:], ALU.mult, ALU.add,
                            [pc, CW, ac], [ac])
                    if ch < 8:
                        stt("dve", QKF[:, ch, :], pc[:, 3:131], CW[:, 3, ch:ch + 1], ac[:], ALU.mult, ALU.add,
                            [pc, CW, ac], [QKF])
                    else:
                        stt("dve", ac[:], pc[:, 3:131], CW[:, 3, ch:ch + 1], ac[:], ALU.mult, ALU.add,
                            [pc, CW, ac], [ac])
                    cp("pool", TAIL[:, ch, :], pc[:, 128:131], [pc], [TAIL])

                def b_silu(i):
                    ch = ORDER[i]
                    if ch >= 8:
                        act(GVT[:, ch - 8, :], ACC[i % 3][:], AF.Silu, [ACC[i % 3]], [GVT])

                for i in range(12 + 3):
                    if i < 12:
                        b_mm(i)
                    if 0 <= i - 1 < 12:
                        b_copy(i - 1)
                    if 0 <= i - 2 < 12:
                        b_conv(i - 2)
                    if 0 <= i - 3 < 12:
                        b_silu(i - 3)
                    yield
                for hf in range(2):
                    act(QKF[:, hf * 4:(hf + 1) * 4, :], QKF[:, hf * 4:(hf + 1) * 4, :], AF.Silu, [QKF], [QKF])
                    yield

            def interleave3(gens, steps):
                alive = [True] * len(gens)
                while any(alive):
                    for gi, g in enumerate(gens):
                        for _ in range(steps[gi]):
                            if alive[gi]:
                                try:
                                    next(g)
                                except StopIteration:
                                    alive[gi] = False

            def stage_A(t):
                Xv = Xt[:, t, :]
                ss, rs = SA4[0], SA4[1]
                act(XN[:], Xv, AF.Square, [RX[t]], [XN, ss], accum=ss[:, 0:1])
                rsqrt_small(rs[:, 0:1], ss[:, 0:1], 1.0 / D, [ss], [rs], rs[:, 1:2])
                act(XN[:], Xv, AF.Copy, [RX[t], rs], [XN], scale=rs[:, 0:1])
                for k in range(8):
                    tr(TPB[:, k * 128:(k + 1) * 128], XN[:, k * 128:(k + 1) * 128], IDB[:], [XN, IDB], [TPB])
                tt("dve", HT[:], TPB[:].rearrange("p (k t) -> p k t", k=8), bc3(GPRE[:], 8, 128), ALU.mult,
                   [TPB, GPRE], [HT])

            P.marks.append(len(P.ops))
            for t in range(KNT):
                dma("sp", Xt[:, t, :], x_d[t * 128:(t + 1) * 128, :], [], [RX[t]])
            stage_A(0)
            for _ in gen_Bconv():
                pass

            for t in range(KNT):
                P.marks.append(len(P.ops))
                Xv = Xt[:, t, :]
                hgen = gen_H(t - 1) if t > 0 else None

                def hstep(n=2):
                    if hgen is not None:
                        for _ in range(n):
                            try:
                                next(hgen)
                            except StopIteration:
                                break

                def tok_proj(bk, c0, n):
                    for k in range(8):
                        mm(bk[:, 0:n], HT[:, k, :], WIN[:, k, c0:c0 + n], k == 0, k == 7,
                           [HT] + rwin(c0, c0 + n), [bk])
                hstep(2)
                tok_proj(BK[0], W_GZ, 512)
                hstep(1)
                act(TMP[0][:], BK[0][:], AF.Silu, [BK[0]], [TMP[0]])
                tt("pool", ZSG[:].rearrange("p (h e) -> p h e", h=4), TMP[0][:].rearrange("p (h e) -> p h e", h=4),
                   bcm(GNG[:], 4), ALU.mult, [TMP[0], GNG], [ZSG])
                hstep(1)
                tok_proj(BK[1], W_MO, 512)
                hstep(1)
                act(TMP[1][:], BK[1][:], AF.Sigmoid, [BK[1]], [TMP[1]])
                tt("pool", MOSG[:].rearrange("p (h e) -> p h e", h=4), TMP[1][:].rearrange("p (h e) -> p h e", h=4),
                   bcm(MNG[:], 4), ALU.mult, [TMP[1], MNG], [MOSG])
                for hh in range(8):
                    bk = BK[hh % 2]
                    c0 = W_MQK + hh * 64
                    for k in range(8):
                        mm(bk[0:64, 0:128], WIN[:, k, c0:c0 + 64], HT[:, k, :], k == 0, k == 7,
                           [HT] + rwin(c0, c0 + 64), [bk])
                    if hh < 4:
                        cp("act", MQT[:, hh, :], bk[0:64, 0:128], [bk], [MQT])
                    else:
                        act(MKT[:, hh - 4, :], bk[0:64, 0:128], AF.Copy, [bk], [MKT], scale=0.125)
                    hstep(1)
                hstep(20)
                tok_proj(BK[0], W_MV, 512)
                cp("act", MVt[:].rearrange("p h e -> p (h e)"), BK[0][:], [BK[0]], [MVt])
                tok_proj(BK[1], W_GATE, 16)
                cp("dve", GT[:], BK[1][:, 0:16], [BK[1]], [GT])
                tt("dve", ARG[:], GT[:, 0:12], SGN[:], ALU.mult, [GT, SGN], [ARG])
                tt("dve", ARG[:], ARG[:], BIA[:], ALU.add, [ARG, BIA], [ARG])
                for hf in range(2):
                    act(TMP[hf][:], QKF[:, hf * 4:(hf + 1) * 4, :].rearrange("p a b -> p (a b)"), AF.Square,
                        [QKF], [TMP[hf]])
                    mm(BK[2 + hf][:], ONES, TMP[hf][:], True, True, [CONST, TMP[hf]], [BK[2 + hf]])
                for hf in range(2):
                    act(TMP[hf][:], BK[2 + hf][:], AF.Ln, [BK[2 + hf], EPSB], [TMP[hf]], bias=EPSB[:, 0:1])
                    act(TMP[hf][:], TMP[hf][:], AF.Exp, [TMP[hf]], [TMP[hf]], scale=-0.5)
                stt("dve", GQT[:].rearrange("p a b -> p (a b)"), QKF[:, 0:4, :].rearrange("p a b -> p (a b)"),
                    128.0 ** -0.5, TMP[0][:], ALU.mult, ALU.mult, [QKF, TMP[0]], [GQT])
                tt("pool", GKT[:].rearrange("p a b -> p (a b)"), QKF[:, 4:8, :].rearrange("p a b -> p (a b)"),
                   TMP[1][:], ALU.mult, [QKF, TMP[1]], [GKT])
                act(ARG[:], ARG[:], AF.Exp, [ARG], [ARG])
                act(ARG[:], ARG[:], AF.Ln, [ARG], [ARG], bias=1.0)
                tt("dve", GL[:], ARG[:], NEGC[:], ALU.mult, [ARG, NEGC], [GL])
                act(BETA[:], GL[:, 0:4], AF.Exp, [GL], [BETA])
                tt("dve", IG[:], GT[:, 12:16], SMB[:, 8:12], ALU.add, [GT, SMB], [IG])

                if t + 1 < KNT:
                    stage_A(t + 1)

                mm(BK[2][:, 0:8], TRI, GL[:, 4:12], True, True, [CONST, GL], [BK[2]])
                cp("dve", GF[:], BK[2][:, 0:8], [BK[2]], [GF])
                cp("pool", COLS[:, 0:4], GF[:, 0:4], [GF], [COLS])
                tt("dve", COLS[:, 4:8], GF[:, 0:4], GL[:, 0:4], ALU.add, [GF, GL], [COLS])
                cp("pool", COLS[:, 8:12], GF[:, 4:8], [GF], [COLS])
                tt("dve", COLS[:, 12:16], IG[:], GF[:, 4:8], ALU.subtract, [IG, GF], [COLS])
                ts("dve", COLS[:, 16:20], GF[:, 0:4], -1.0, None, ALU.mult, None, [GF], [COLS])
                for j in range(4):
                    tr(BK[3][0:4, j * 128:(j + 1) * 128], COLS[:, 4 * j:4 * j + 4], IDF, [COLS, CONST], [BK[3]])
                tr(BK[4][0:4, 0:128], COLS[:, 16:20], IDF, [COLS, CONST], [BK[4]])
                cp("dve", RT[:, 0:4, :].rearrange("p a b -> p (a b)"), BK[3][0:4, :], [BK[3]], [RT])
                cp("dve", RT[:, 4, :], BK[4][0:4, 0:128], [BK[4]], [RT])
                SELR3 = SELR.rearrange("p (a b) -> p a b", a=4)
                tt("dve", BD[:], bcm(RT[:, 1, :], 4), SELR3, ALU.mult, [RT, CONST], [BD])
                mm(BK[2][:], ONES4, BD[:].rearrange("p a b -> p (a b)"), True, False, [CONST, BD], [BK[2]])
                mm(BK[2][:], RT[:, 4, :], SELR, False, True, [RT, CONST], [BK[2]])
                tt("dve", EXPA[:].rearrange("p (h c) -> p h c", h=4), BK[2][:].rearrange("p (h c) -> p h c", h=4),
                   bcm(MST, 4), ALU.add, [BK[2], CONST], [EXPA])
                act(EXPA[:], EXPA[:], AF.Exp, [EXPA], [EXPA])
                tt("dve", BD[:], bcm(RT[:, 0, :], 4), SELR3, ALU.mult, [RT, CONST], [BD])
                mm(BK[3][:], ONES4, BD[:].rearrange("p a b -> p (a b)"), True, False, [CONST, BD], [BK[3]])
                mm(BK[3][:], RT[:, 4, :], SELR, False, True, [RT, CONST], [BK[3]])
                tt("dve", EXPQ[:].rearrange("p (h c) -> p h c", h=4), BK[3][:].rearrange("p (h c) -> p h c", h=4),
                   bcm(MIT, 4), ALU.add, [BK[3], CONST], [EXPQ])
                act(EXPQ[:], EXPQ[:], AF.Exp, [EXPQ], [EXPQ])
                for h in range(4):
                    mm(BK[4][:, h * 128:(h + 1) * 128], GKT[:, h, :], GKT[:, h, :], True, True, [GKT], [BK[4]])
                for h in range(4):
                    mm(BK[5][:, h * 128:(h + 1) * 128], GKT[:, h, :], GQT[:, h, :], True, True, [GKT, GQT], [BK[5]])
                tt("dve", MB[0][:], BK[4][:], EXPA[:], ALU.mult, [BK[4], EXPA], [MB[0]])
                tt("dve", QKD[:].rearrange("p h c -> p (h c)"), BK[5][:], EXPQ[:], ALU.mult, [BK[5], EXPQ], [QKD])

                for h in range(4):
                    tr(TPB[:, h * 128:(h + 1) * 128], GKT[:, h, :], IDB[:], [GKT, IDB], [TPB])
                    tr(TPB[:, 512 + h * 128:512 + (h + 1) * 128], GVT[:, h, :], IDB[:], [GVT, IDB], [TPB])
                cp("act", GKt[:].rearrange("p h d -> p (h d)"), TPB[:, 0:512], [TPB], [GKt])
                EG, BEG, EGL, GEND = SC4[0], SC4[1], SC4[2], SC4[3]
                act(EG[:, 0:4], GF[:, 0:4], AF.Exp, [GF], [EG])
                tt("dve", BEG[:, 0:4], EG[:, 0:4], BETA[:], ALU.mult, [EG, BETA], [BEG])
                mm(BK[2][:, 0:8], SEL127, GF[:], True, True, [CONST, GF], [BK[2]])
                cp("dve", LASTG[:], BK[2][:, 0:8], [BK[2]], [LASTG])
                tt("dve", EGL[:, 0:4], LASTG[:, 0:4], GF[:, 0:4], ALU.subtract, [LASTG, GF], [EGL])
                act(EGL[:, 0:4], EGL[:, 0:4], AF.Exp, [EGL], [EGL])
                act(GEND[:, 0:4], LASTG[:, 0:4], AF.Exp, [LASTG], [GEND])
                tt("dve", XK[:], GKt[:], bc3(BEG[:, 0:4], 4, 128), ALU.mult, [GKt, BEG], [XK])
                tt("pool", KEND[:], GKt[:], bc3(EGL[:, 0:4], 4, 128), ALU.mult, [GKt, EGL], [KEND])
                tt("dve", GVb[:], TPB[:, 512:1024].rearrange("p (h e) -> p h e", h=4), bc3(BETA[:], 4, 128),
                   ALU.mult, [TPB, BETA], [GVb])
                def gen_EF():
                    for h in range(4):
                        tr(BK[5][:, h * 128:(h + 1) * 128], MB[0][:, h * 128:(h + 1) * 128], IDF, [MB[0], CONST], [BK[5]])
                    yield
                    cp("act", LB[0][:], BK[5][:], [BK[5]], [LB[0]])
                    yield
                    tt("dve", RR[:].rearrange("p (h c) -> p h c", h=4), bcm(IDF, 4),
                       MB[0][:].rearrange("p (h c) -> p h c", h=4), ALU.subtract, [CONST, MB[0]], [RR])
                    yield
                    NLEV = 6
                    yield
                    for k in range(NLEV):
                        a, b = k % 2, (k + 1) % 2
                        for h in range(4):
                            sl = slice(h * 128, (h + 1) * 128)
                            mm(BK[3][:, sl], MB[a][:, sl], LB[a][:, sl], True, True, [MB[a], LB[a]], [BK[3]])
                        yield
                        if k < NLEV - 1:
                            for h in range(4):
                                sl = slice(h * 128, (h + 1) * 128)
                                mm(BK[4][:, sl], LB[a][:, sl], MB[a][:, sl], True, True, [MB[a], LB[a]], [BK[4]])
                            yield
                        cp("act", LB[b][:], BK[3][:], [BK[3]], [LB[b]])
                        yield
                        if k < NLEV - 1:
                            cp("dve", MB[b][:], BK[4][:], [BK[4]], [MB[b]])
                            yield
                        for h in range(4):
                            sl = slice(h * 128, (h + 1) * 128)
                            mm(BK[5][:, sl], LB[b][:, sl], RR[:, sl], True, True, [LB[b], RR], [BK[5]])
                        yield
                        tt("dve", RR[:], RR[:], BK[5][:], ALU.add, [RR, BK[5]], [RR])
                        yield
                    cp("act", BINV[:].rearrange("p h c -> p (h c)"), RR[:], [RR], [BINV])
                    yield
                    for h in range(4):
                        sl = slice(h * 128, (h + 1) * 128)
                        mm(BK[3][:, sl], BINV[:, h, :], GVb[:, h, :], True, True, [BINV, GVb], [BK[3]])
                        mm(BK[4][:, sl], XK[:, h, :], BINV[:, h, :], True, True, [BINV, XK], [BK[4]])
                    yield
                    cp("act", WV[:], BK[3][:], [BK[3]], [WV])
                    yield
                    cp("dve", WKT[:].rearrange("p h c -> p (h c)"), BK[4][:], [BK[4]], [WKT])
                    yield
                    for h in range(4):
                        sl = slice(h * 128, (h + 1) * 128)
                        mm(BK[3][:, sl], WKT[:, h, :], SBh[:, h, :], True, True, [WKT, SBh], [BK[3]])
                        mm(BK[5][:, sl], GQT[:, h, :], SBh[:, h, :], True, True, [GQT, SBh], [BK[5]])
                    yield
                    tt("dve", U[:].rearrange("p h e -> p (h e)"), WV[:], BK[3][:], ALU.subtract, [WV, BK[3]], [U])
                    yield
                    for h in range(4):
                        sl = slice(h * 128, (h + 1) * 128)
                        mm(BK[4][:, sl], QKD[:, h, :], U[:, h, :], True, True, [QKD, U], [BK[4]])
                        mm(BK[3][:, sl], KEND[:, h, :], U[:, h, :], True, True, [KEND, U], [BK[3]])
                    yield
                    OG = MB[1]
                    yield
                    tt("dve", OG[:].rearrange("p (h e) -> p h e", h=4), BK[5][:].rearrange("p (h e) -> p h e", h=4),
                       bc3(EG[:, 0:4], 4, 128), ALU.mult, [BK[5], EG], [OG])
                    yield
                    tt("dve", OG[:], OG[:], BK[4][:], ALU.add, [OG, BK[4]], [OG])
                    yield
                    for h in range(4):
                        stt("dve", S32[:, h, :], S32[:, h, :], GEND[:, h:h + 1], BK[3][:, h * 128:(h + 1) * 128],
                            ALU.mult, ALU.add, [S32, GEND, BK[3]], [S32])
                    yield
                    cp("act", SBh[:], S32[:], [S32], [SBh])
                    yield
                    headnorm_gate(OG, ZSG, MIX[:, 0:512], LB[0], SC4[4], SC4[5])
                    yield
                def gen_G():
                    tt("dve", BD[:], bcm(RT[:, 3, :], 4), SELR3, ALU.mult, [RT, CONST], [BD])
                    yield
                    mm(BK[6][:], ONES4, BD[:].rearrange("p a b -> p (a b)"), True, False, [CONST, BD], [BK[6]])
                    yield
                    mm(BK[6][:], RT[:, 2, :], SELR, False, True, [RT, CONST], [BK[6]])
                    yield
                    PD = TMP[0]
                    yield
                    tt("dve", PD[:].rearrange("p (h s) -> p h s", h=4), BK[6][:].rearrange("p (h s) -> p h s", h=4),
                       bcm(MI, 4), ALU.add, [BK[6], CONST], [PD])
                    yield
                    red("dve", DMAX[:], PD[:].rearrange("p (h s) -> p h s", h=4), ALU.max, [PD], [DMAX])
                    yield
                    for h in range(4):
                        mm(BK[0][:, h * 128:(h + 1) * 128], MQT[:, h, :], MKT[:, h, :], True, True, [MQT, MKT], [BK[0]])
                    yield
                    for h in range(4):
                        tr(TPB[:, h * 64:(h + 1) * 64], MKT[:, h, :], IDB[0:64, 0:64], [MKT, IDB], [TPB])
                    yield
                    cp("act", MKt[:].rearrange("p h d -> p (h d)"), TPB[:, 0:256], [TPB], [MKt])
                    yield
                    Bv, MT, WP, EMT = SG4[0], SG4[1], SG4[2], SG4[3]
                    yield
                    tt("dve", Bv[:, 0:4], GF[:, 4:8], MBC[:], ALU.add, [GF, MBC], [Bv])
                    yield
                    tt("dve", MT[:, 0:4], Bv[:, 0:4], DMAX[:], ALU.max, [Bv, DMAX], [MT])
                    yield
                    tt("dve", WP[:, 0:4], Bv[:, 0:4], MT[:, 0:4], ALU.subtract, [Bv, MT], [WP])
                    yield
                    act(WP[:, 0:4], WP[:, 0:4], AF.Exp, [WP], [WP])
                    yield
                    tt("dve", PD[:].rearrange("p (h s) -> p h s", h=4), PD[:].rearrange("p (h s) -> p h s", h=4),
                       bc3(MT[:, 0:4], 4, 128), ALU.subtract, [PD, MT], [PD])
                    yield
                    act(PD[:], PD[:], AF.Exp, [PD], [PD])
                    yield
                    tt("dve", PD[:], PD[:], BK[0][:], ALU.mult, [PD, BK[0]], [PD])
                    yield
                    RS = SG4[4]
                    yield
                    red("dve", RS[:, 0:4], PD[:].rearrange("p (h s) -> p h s", h=4), ALU.add, [PD], [RS])
                    yield
                    cp("act", PQKb[:].rearrange("p h s -> p (h s)"), PD[:], [PD], [PQKb])
                    yield
                    for h in range(4):
                        tr(TPB[:, 512 + h * 128:512 + (h + 1) * 128], PQKb[:, h, :], IDB[:], [PQKb, IDB], [TPB])
                    yield
                    cp("act", PQKT[:].rearrange("p h s -> p (h s)"), TPB[:, 512:1024], [TPB], [PQKT])
                    yield
                    for h in range(4):
                        sl = slice(h * 128, (h + 1) * 128)
                        mm(BK[0][:, sl], MQT[:, h, :], CBh[:, h, :], True, True, [MQT, CBh], [BK[0]])
                        mm(BK[6][:, 2 * h:2 * h + 2], MQT[:, h, :], NBh[:, h, :], True, True, [MQT, NBh], [BK[6]])
                    yield
                    NUM = TMP[0]
                    yield
                    tt("dve", NUM[:].rearrange("p (h e) -> p h e", h=4), BK[0][:].rearrange("p (h e) -> p h e", h=4),
                       bc3(WP[:, 0:4], 4, 128), ALU.mult, [BK[0], WP], [NUM])
                    yield
                    for h in range(4):
                        sl = slice(h * 128, (h + 1) * 128)
                        mm(BK[0][:, sl], PQKT[:, h, :], MVt[:, h, :], True, True, [PQKT, MVt], [BK[0]])
                    yield
                    tt("dve", NUM[:], NUM[:], BK[0][:], ALU.add, [NUM, BK[0]], [NUM])
                    yield
                    DEN = SG4[5]
                    yield
                    tt("dve", DEN[:, 0:4], BK[6][:, 0:8].rearrange("p (h two) -> p h two", two=2)[:, :, 0], WP[:, 0:4], ALU.mult, [BK[6], WP], [DEN])
                    yield
                    tt("dve", DEN[:, 0:4], DEN[:, 0:4], RS[:, 0:4], ALU.add, [DEN, RS], [DEN])
                    yield
                    act(EMT[:, 0:4], MT[:, 0:4], AF.Exp, [MT], [EMT], scale=-1.0)
                    yield
                    ts("dve", DEN[:, 4:8], DEN[:, 0:4], -1.0, None, ALU.mult, None, [DEN], [DEN])
                    yield
                    tt("dve", DEN[:, 0:4], DEN[:, 0:4], DEN[:, 4:8], ALU.max, [DEN], [DEN])
                    yield
                    tt("dve", DEN[:, 0:4], DEN[:, 0:4], EMT[:, 0:4], ALU.max, [DEN, EMT], [DEN])
                    yield
                    P.op("dve", lambda e, d=DEN: e.reciprocal(out=d[:, 0:4], in_=d[:, 0:4]), [DEN], [DEN])
                    yield
                    tt("dve", NUM[:].rearrange("p (h e) -> p h e", h=4), NUM[:].rearrange("p (h e) -> p h e", h=4),
                       bc3(DEN[:, 0:4], 4, 128), ALU.mult, [NUM, DEN], [NUM])
                    yield
                    cp("pool", T12[:, 0:4], MT[:, 0:4], [MT], [T12])
                    yield
                    tt("dve", T12[:, 4:8], GF[:, 4:8], MT[:, 0:4], ALU.subtract, [GF, MT], [T12])
                    yield
                    cp("pool", T12[:, 8:12], WP[:, 0:4], [WP], [T12])
                    yield
                    mm(BK[6][:, 16:28], SEL127, T12[:], True, True, [CONST, T12], [BK[6]])
                    yield
                    cp("dve", MBC[:], BK[6][:, 16:20], [BK[6]], [MBC])
                    yield
                    PEND = SG4[4]
                    yield
                    tt("dve", PEND[:, 4:8], COLS[:, 12:16], BK[6][:, 20:24], ALU.add, [COLS, BK[6]], [PEND])
                    yield
                    act(PEND[:, 4:8], PEND[:, 4:8], AF.Exp, [PEND], [PEND])
                    yield
                    cp("dve", WLC[:], BK[6][0:64, 24:28], [BK[6]], [WLC])
                    yield
                    tt("dve", PK[:], MKt[:], bc3(PEND[:, 4:8], 4, 64), ALU.mult, [MKt, PEND], [PK])
                    yield
                    for h in range(4):
                        mm(BK[0][0:64, h * 128:(h + 1) * 128], PK[:, h, :], MVt[:, h, :], True, True, [PK, MVt], [BK[0]])
                    yield
                    for h in range(4):
                        mm(BK[6][0:64, 32 + 2 * h:34 + 2 * h], PK[:, h, :], ONEB[:], True, True, [PK, ONEB], [BK[6]])
                    yield
                    for h in range(4):
                        stt("dve", C32[:, h, :], C32[:, h, :], WLC[:, h:h + 1], BK[0][0:64, h * 128:(h + 1) * 128],
                            ALU.mult, ALU.add, [C32, WLC, BK[0]], [C32])
                    yield
                    tt("dve", N32[:], N32[:], WLC[:], ALU.mult, [N32, WLC], [N32])
                    yield
                    tt("dve", N32[:], N32[:], BK[6][0:64, 32:40].rearrange("p (a two) -> p a two", two=2)[:, :, 0],
                       ALU.add, [N32, BK[6]], [N32])
                    yield
                    cp("act", CBh[:], C32[:], [C32], [CBh])
                    yield
                    cp("act", NBh[:, :, 0], N32[:], [N32], [NBh])
                    yield

                if t + 1 < KNT:
                    interleave3([gen_EF(), gen_G(), gen_Bconv()], KRATIO)
                else:
                    interleave3([gen_EF(), gen_G()], [2, 1])
                headnorm_gate(TMP[0], MOSG, MIX[:, 512:1024], TMP[1], SG4[4], SG4[5])

            for _ in gen_H(KNT - 1):
                pass

            dma("sp", pS_d.rearrange("h d e -> d h e"), S32[:], [S32], [])
            dma("sp", pC_d.rearrange("h d e -> d h e"), C32[:], [C32], [])
            dma("sp", pn_d.rearrange("h d -> d h"), N32[:], [N32], [], slow=True)
            dma("sp", pm_d, MBC[0:1, :], [MBC], [])
            for ch in range(12):
                tr(BK[2 + ch // 4][0:3, (ch % 4) * 128:(ch % 4 + 1) * 128], TAIL[:, ch, :], IDF, [TAIL, CONST],
                   [BK[2 + ch // 4]])
            for g in range(3):
                cp("dve", TMP[g % 2][0:3, :], BK[2 + g][0:3, :], [BK[2 + g]], [TMP[g % 2]])
                dma("sp", pconv_d[:, g * 512:(g + 1) * 512], TMP[g % 2][0:3, :], [TMP[g % 2]], [])


            P.barrier(all_res_p1 + RX + [XS1T.r, HNS.r, IDF2.r, IDB.r, GPREMLP.r])
            p1a.close()
            p1b = ExitStack()
            cur[0] = p1b
            R_ = NS
            GRP = 1
            XS = s1("XS", [R_, D])
            XNs = s1("XNs", [R_, D], BF16)
            HTs = s1("HTs", [128, 8, R_], BF16)
            SCV = s1("SCV", [R_, 512])
            XPT = s1("XPT", [128, 12, 4, R_])
            ACs = s1("ACs", [128, 12, R_])
            TM12 = s1("TM12", [128, 12, R_])
            SQs = s1("SQs", [128, 8, R_])
            GQs = s1("GQs", [128, 4, R_])
            GKs = s1("GKs", [128, 4, R_])
            GVs = s1("GVs", [128, 4, R_])
            MQs = s1("MQs", [64, 4, R_])
            MKs = s1("MKs", [64, 4, R_])
            ZSGs = s1("ZSGs", [R_, 512])
            MOSGs = s1("MOSGs", [R_, 512])
            MVs = s1("MVs", [R_, 4, 128])
            GTs = s1("GTs", [R_, 16])
            ARGs = s1("ARGs", [R_, 12])
            GLs = s1("GLs", [R_, 12])
            BETAs = s1("BETAs", [R_, 4])
            IGs = s1("IGs", [R_, 4])
            EGs = s1("EGs", [R_, 4])
            Qt = s1("Qt", [R_, 4, 128])
            Kt = s1("Kt", [R_, 4, 128])
            Vt = s1("Vt", [R_, 4, 128])
            MQt = s1("MQt", [R_, 4, 64])
            MKtt = s1("MKtt", [R_, 4, 64])
            PKt = s1("PKt", [R_, 4, 64])
            DIAGI = s1("DIAGI", [R_, 16, 16])
            M16 = s1("M16", [128, 16, 16])
            KD = [s1("KD%d" % i, [128, 4, 16]) for i in range(2)]
            QD = [s1("QD%d" % i, [128, 4, 16]) for i in range(2)]
            QDm = [s1("QDm%d" % i, [64, 4, 16]) for i in range(2)]
            DG4 = s1("DG4", [R_, 16, 4])
            EGBC = s1("EGBC", [128, 64])
            WPBC = s1("WPBC", [128, 64])
            S0b = [s1("S0b%d" % i, [128, 4, 128]) for i in range(2)]
            C0b = [s1("C0b%d" % i, [64, 4, 128]) for i in range(2)]
            Ug = s1("Ug", [R_, 4, 128])
            KROW = s1("KROW", [R_, 512])
            PKROW = View(DIAGI[:].rearrange("p a b -> p (a b)"), DIAGI.r)
            N0 = s1("N0", [R_, 4, 64])
            SMs = [s1("SMs%d" % i, [R_, 8]) for i in range(10)]
            T1 = s1("T1s", [R_, 512])
            T2 = KROW

            def hn_gate16(src, gate, dst, wres):
                ss, rs = SMs[8], SMs[9]
                tt("pool", T2[:], src[:], src[:], ALU.mult, [src], [T2])
                red("dve", ss[:, 0:4], T2[:].rearrange("p (h e) -> p h e", h=4), ALU.add, [T2], [ss])
                rsqrt_small(rs[:, 0:4], ss[:, 0:4], 1.0 / 128, [ss], [rs], rs[:, 4:8])
                tt("dve", T2[:].rearrange("p (h e) -> p h e", h=4), src[:].rearrange("p (h e) -> p h e", h=4),
                   bc3(rs[:, 0:4], 4, 128), ALU.mult, [src, rs], [T2])
                tt("dve", dst, T2[:], gate[:], ALU.mult, [T2, gate], [wres])

            IDF16 = CONST[0:R_, C_IDF:C_IDF + R_]
            ONES16 = CONST[0:R_, C_ONES:C_ONES + 128]

            dma("sp", XS[:], xs_d, [], [XS])
            dma("sp", N0[:].rearrange("p h d -> p (h d)"), sn_d, [], [N0])
            dma("sp", SMs[0][:, 0:4], sm_d, [], [SMs[0]])
            dma("sp", oconv_d[:, 0:2, :], sconv_d[:, 1:3, :], [], [])
            ss, rs = SMs[8], SMs[9]
            act(XNs[:], XS[:], AF.Square, [XS], [XNs, ss], accum=ss[:, 0:1])
            rsqrt_small(rs[:, 0:1], ss[:, 0:1], 1.0 / D, [ss], [rs], rs[:, 1:2])
            ts("dve", XNs[:], XS[:], rs[:, 0:1], None, ALU.mult, None, [XS, rs], [XNs])
            for k in range(8):
                tr(TPB[:, k * 128:k * 128 + R_], XNs[:, k * 128:(k + 1) * 128], IDB[0:R_, 0:R_], [XNs, IDB], [TPB])
            tt("dve", HTs[:], TPB[:].rearrange("p (k t) -> p k t", k=8)[:, :, 0:R_], bc3(GPRE[:], 8, R_), ALU.mult,
               [TPB, GPRE], [HTs])

            for ch in range(12):
                bk = BK[ch % 2]
                c0 = ch * 128
                for k in range(8):
                    mm(bk[:, 0:R_], WIN[:, k, c0:c0 + 128], HTs[:, k, :], k == 0, k == 7, [HTs] + rwin(c0, c0 + 128), [bk])
                cp("act", XPT[:, ch, 3, :], bk[:, 0:R_], [bk], [XPT])
            for hh in range(8):
                bk = BK[hh % 2]
                c0 = W_MQK + hh * 64
                for k in range(8):
                    mm(bk[0:64, 0:R_], WIN[:, k, c0:c0 + 64], HTs[:, k, :], k == 0, k == 7, [HTs] + rwin(c0, c0 + 64), [bk])
                if hh < 4:
                    cp("act", MQs[:, hh, :], bk[0:64, 0:R_], [bk], [MQs])
                else:
                    act(MKs[:, hh - 4, :], bk[0:64, 0:R_], AF.Copy, [bk], [MKs], scale=0.125)
            for g in range(3):
                for j in range(3):
                    dma("sp", SCV[:], sconv_d[:, j, g * 512:(g + 1) * 512], [], [SCV])
                    for c4 in range(4):
                        o0 = (j * 4 + c4) * R_
                        tr(BK[2][:, o0:o0 + R_], SCV[:, c4 * 128:(c4 + 1) * 128], IDF16, [SCV, CONST], [BK[2]])
                cp("dve", XPT[:, 4 * g:4 * g + 4, 0:3, :],
                   BK[2][:, 0:12 * R_].rearrange("p (j c r) -> p c j r", j=3, c=4), [BK[2]], [XPT])
            for g in range(3):
                bk = BK[3 + g % 2]
                for k in range(8):
                    mm(bk[0:R_, :], HTs[:, k, :], WIN[:, k, g * 512:(g + 1) * 512], k == 0, k == 7,
                       [HTs] + rwin(g * 512, (g + 1) * 512), [bk])
                cp("act", SCV[:], bk[0:R_, :], [bk], [SCV])
                dma("sp", oconv_d[:, 2, g * 512:(g + 1) * 512], SCV[:], [SCV], [])
            tt("dve", ACs[:], XPT[:, :, 0, :], bc3(CW[:, 0, :], 12, R_), ALU.mult, [XPT, CW], [ACs])
            for j in range(1, 4):
                tt("dve", TM12[:], XPT[:, :, j, :], bc3(CW[:, j, :], 12, R_), ALU.mult, [XPT, CW], [TM12])
                tt("dve", ACs[:], ACs[:], TM12[:], ALU.add, [ACs, TM12], [ACs])
            QKFs = TM12
            act(QKFs[:, 0:8, :], ACs[:, 0:8, :], AF.Silu, [ACs], [QKFs])
            act(GVs[:], ACs[:, 8:12, :], AF.Silu, [ACs], [GVs])
            act(SQs[:], QKFs[:, 0:8, :], AF.Square, [QKFs], [SQs])
            mm(BK[2][:, 0:8 * R_], ONES, SQs[:].rearrange("p a b -> p (a b)"), True, True, [CONST, SQs], [BK[2]])
            ts("dve", SQs[:].rearrange("p a b -> p (a b)"), BK[2][:, 0:8 * R_], 1.0, EPS, ALU.mult, ALU.add, [BK[2]], [SQs])
            act(SQs[:], SQs[:], AF.Ln, [SQs], [SQs])
            act(SQs[:], SQs[:], AF.Exp, [SQs], [SQs], scale=-0.5)
            stt("dve", GQs[:], QKFs[:, 0:4, :], 128.0 ** -0.5, SQs[:, 0:4, :], ALU.mult, ALU.mult, [QKFs, SQs], [GQs])
            tt("dve", GKs[:], QKFs[:, 4:8, :], SQs[:, 4:8, :], ALU.mult, [QKFs, SQs], [GKs])

            def tok16(bk, c0, n):
                for k in range(8):
                    mm(bk[0:R_, 0:n], HTs[:, k, :], WIN[:, k, c0:c0 + n], k == 0, k == 7, [HTs] + rwin(c0, c0 + n), [bk])
            tok16(BK[0], W_GZ, 512)
            act(T1[:], BK[0][0:R_, :], AF.Silu, [BK[0]], [T1])
            tt("dve", ZSGs[:].rearrange("p (h e) -> p h e", h=4), T1[:].rearrange("p (h e) -> p h e", h=4),
               bcm(GNG[0:R_, :], 4), ALU.mult, [T1, GNG], [ZSGs])
            tok16(BK[1], W_MV, 512)
            cp("act", MVs[:].rearrange("p h e -> p (h e)"), BK[1][0:R_, :], [BK[1]], [MVs])
            tok16(BK[0], W_MO, 512)
            act(T1[:], BK[0][0:R_, :], AF.Exp, [BK[0]], [T1], scale=-1.0)
            ts("dve", T1[:], T1[:], 1.0, None, ALU.add, None, [T1], [T1])
            P.op("dve", lambda e: e.reciprocal(out=T1[:], in_=T1[:]), [T1], [T1])
            tt("dve", MOSGs[:].rearrange("p (h e) -> p h e", h=4), T1[:].rearrange("p (h e) -> p h e", h=4),
               bcm(MNG[0:R_, :], 4), ALU.mult, [T1, MNG], [MOSGs])
            tok16(BK[1], W_GATE, 16)
            cp("dve", GTs[:], BK[1][0:R_, 0:16], [BK[1]], [GTs])
            tt("dve", ARGs[:], GTs[:, 0:12], SGN[0:R_, :], ALU.mult, [GTs, SGN], [ARGs])
            tt("dve", ARGs[:], ARGs[:], BIA[0:R_, :], ALU.add, [ARGs, BIA], [ARGs])
            act(ARGs[:], ARGs[:], AF.Exp, [ARGs], [ARGs])
            act(ARGs[:], ARGs[:], AF.Ln, [ARGs], [ARGs], bias=1.0)
            tt("dve", GLs[:], ARGs[:], NEGC[0:R_, :], ALU.mult, [ARGs, NEGC], [GLs])
            act(BETAs[:], GLs[:, 0:4], AF.Exp, [GLs], [BETAs])
            tt("dve", IGs[:], GTs[:, 12:16], SMB[0:R_, 8:12], ALU.add, [GTs, SMB], [IGs])
            act(EGs[:], GLs[:, 4:8], AF.Exp, [GLs], [EGs])
            M0s, Bs, MTs, WPs, Ps, QKG, QKMs, QNs = SMs[0], SMs[1], SMs[2], SMs[3], SMs[4], SMs[5], SMs[6], SMs[7]
            tt("dve", Bs[:, 0:4], GLs[:, 8:12], M0s[:, 0:4], ALU.add, [GLs, M0s], [Bs])
            tt("dve", MTs[:, 0:4], Bs[:, 0:4], IGs[:], ALU.max, [Bs, IGs], [MTs])
            tt("dve", WPs[:, 0:4], Bs[:, 0:4], MTs[:, 0:4], ALU.subtract, [Bs, MTs], [WPs])
            act(WPs[:, 0:4], WPs[:, 0:4], AF.Exp, [WPs], [WPs])
            tt("dve", Ps[:, 0:4], IGs[:], MTs[:, 0:4], ALU.subtract, [IGs, MTs], [Ps])
            act(Ps[:, 0:4], Ps[:, 0:4], AF.Exp, [Ps], [Ps])
            dma("sp", om_d, MTs[:, 0:4], [MTs], [])

            for h in range(4):
                tr(BK[2][0:R_, h * 128:(h + 1) * 128], GQs[:, h, :], IDF, [GQs, CONST], [BK[2]])
                tr(BK[3][0:R_, h * 128:(h + 1) * 128], GKs[:, h, :], IDF, [GKs, CONST], [BK[3]])
                tr(BK[4][0:R_, h * 128:(h + 1) * 128], GVs[:, h, :], IDF, [GVs, CONST], [BK[4]])
                tr(BK[5][0:R_, h * 64:(h + 1) * 64], MQs[:, h, :], CONST[0:64, C_IDF:C_IDF + 64], [MQs, CONST], [BK[5]])
                tr(BK[5][0:R_, 256 + h * 64:256 + (h + 1) * 64], MKs[:, h, :], CONST[0:64, C_IDF:C_IDF + 64],
                   [MKs, CONST], [BK[5]])
            cp("act", Qt[:].rearrange("p h d -> p (h d)"), BK[2][0:R_, :], [BK[2]], [Qt])
            cp("dve", Kt[:].rearrange("p h d -> p (h d)"), BK[3][0:R_, :], [BK[3]], [Kt])
            cp("act", Vt[:].rearrange("p h d -> p (h d)"), BK[4][0:R_, :], [BK[4]], [Vt])
            cp("dve", MQt[:].rearrange("p h d -> p (h d)"), BK[5][0:R_, 0:256], [BK[5]], [MQt])
            cp("act", MKtt[:].rearrange("p h d -> p (h d)"), BK[5][0:R_, 256:512], [BK[5]], [MKtt])
            tt("dve", T1[:], Qt[:].rearrange("p h d -> p (h d)"), Kt[:].rearrange("p h d -> p (h d)"), ALU.mult,
               [Qt, Kt], [T1])
            red("dve", QKG[:, 0:4], T1[:].rearrange("p (h d) -> p h d", h=4), ALU.add, [T1], [QKG])
            tt("dve", T1[:, 0:256], MQt[:].rearrange("p h d -> p (h d)"), MKtt[:].rearrange("p h d -> p (h d)"),
               ALU.mult, [MQt, MKtt], [T1])
            red("dve", QKMs[:, 0:4], T1[:, 0:256].rearrange("p (h d) -> p h d", h=4), ALU.add, [T1], [QKMs])
            tt("dve", T1[:, 0:256], MQt[:].rearrange("p h d -> p (h d)"), N0[:].rearrange("p h d -> p (h d)"),
               ALU.mult, [MQt, N0], [T1])
            red("dve", QNs[:, 0:4], T1[:, 0:256].rearrange("p (h d) -> p h d", h=4), ALU.add, [T1], [QNs])
            tt("dve", PKt[:], MKtt[:], bc3(Ps[:, 0:4], 4, 64), ALU.mult, [MKtt, Ps], [PKt])
            tt("dve", N0[:], N0[:], bc3(WPs[:, 0:4], 4, 64), ALU.mult, [N0, WPs], [N0])
            tt("dve", N0[:], N0[:], PKt[:], ALU.add, [N0, PKt], [N0])
            dma("sp", on_d, N0[:].rearrange("p h d -> p (h d)"), [N0], [])

            tt("dve", DIAGI[:], bc3(IDF16, R_, R_), bcm(IDF16, R_), ALU.mult, [CONST], [DIAGI])
            mm(BK[2][:, 0:R_ * R_], ONES16, DIAGI[:].rearrange("p a b -> p (a b)"), True, True, [CONST, DIAGI], [BK[2]])
            cp("dve", M16[:].rearrange("p a b -> p (a b)"), BK[2][:, 0:R_ * R_], [BK[2]], [M16])
            tt("dve", DG4[:], bcm(EGs[:], R_), bc3(IDF16, R_, 4), ALU.mult, [EGs, CONST], [DG4])
            mm(BK[3][:, 0:64], ONES16, DG4[:].rearrange("p a b -> p (a b)"), True, True, [CONST, DG4], [BK[3]])
            cp("dve", EGBC[:], BK[3][:, 0:64], [BK[3]], [EGBC])
            tt("dve", DG4[:], bcm(WPs[:, 0:4], R_), bc3(IDF16, R_, 4), ALU.mult, [WPs, CONST], [DG4])
            mm(BK[3][:, 0:64], ONES16, DG4[:].rearrange("p a b -> p (a b)"), True, True, [CONST, DG4], [BK[3]])
            cp("dve", WPBC[:], BK[3][:, 0:64], [BK[3]], [WPBC])

            def ld(r, i):
                dma("sp", S0b[i][:], sS_d[r].rearrange("h d e -> d h e"), [], [S0b[i]])
                dma("act", C0b[i][:], sC_d[r].rearrange("h d e -> d h e"), [], [C0b[i]])
            ld(0, 0)
            for r in range(R_):
                i = r % 2
                if r + 1 < R_:
                    ld(r + 1, 1 - i)
                tt("dve", KD[i][:], GKs[:], bcm(M16[:, r, :], 4), ALU.mult, [GKs, M16], [KD[i]])
                tt("pool", QD[i][:], GQs[:], bcm(M16[:, r, :], 4), ALU.mult, [GQs, M16], [QD[i]])
                tt("dve", QDm[i][:], MQs[:], bcm(M16[0:64, r, :], 4), ALU.mult, [MQs, M16], [QDm[i]])
                for h in range(4):
                    sl = slice(h * 128, (h + 1) * 128)
                    st_, sp_ = (r == 0 and h == 0), (r == R_ - 1 and h == 3)
                    mm(BK[2][0:R_, sl], KD[i][:, h, :], S0b[i][:, h, :], st_, sp_, [KD[i], S0b[i]], [BK[2]], skip=True)
                    mm(BK[3][0:R_, sl], QD[i][:, h, :], S0b[i][:, h, :], st_, sp_, [QD[i], S0b[i]], [BK[3]], skip=True)
                    mm(BK[5][0:R_, sl], QDm[i][:, h, :], C0b[i][:, h, :], st_, sp_, [QDm[i], C0b[i]], [BK[5]], skip=True)
            KSp, QSp, QCp = BK[2], BK[3], BK[5]
            tt("dve", Ug[:], KSp[0:R_, :].rearrange("p (h e) -> p h e", h=4), bc3(EGs[:], 4, 128), ALU.mult,
               [KSp, EGs], [Ug])
            tt("dve", Ug[:], Vt[:], Ug[:], ALU.subtract, [Vt, Ug], [Ug])
            tt("dve", Ug[:], Ug[:], bc3(BETAs[:], 4, 128), ALU.mult, [Ug, BETAs], [Ug])
            ld(0, 0)
            for r in range(R_):
                i = r % 2
                if r + 1 < R_:
                    ld(r + 1, 1 - i)
                ts("dve", KROW[:], Kt[:].rearrange("p h d -> p (h d)"), IDF16[:, r:r + 1], None, ALU.mult, None,
                   [Kt, CONST], [KROW])
                ts("dve", PKROW[:], PKt[:].rearrange("p h d -> p (h d)"), IDF16[:, r:r + 1], None, ALU.mult, None,
                   [PKt, CONST], [PKROW])
                for h in range(4):
                    mm(BK[4][:, h * 128:(h + 1) * 128], KROW[:, h * 128:(h + 1) * 128], Ug[:, h, :], True, True,
                       [KROW, Ug], [BK[4]])
                for h in range(4):
                    mm(BK[6][0:64, h * 128:(h + 1) * 128], PKROW[:, h * 64:(h + 1) * 64], MVs[:, h, :], True, True,
                       [PKROW, MVs], [BK[6]])
                for h in range(4):
                    stt("dve", S0b[i][:, h, :], S0b[i][:, h, :], EGBC[:, r * 4 + h:r * 4 + h + 1],
                        BK[4][:, h * 128:(h + 1) * 128], ALU.mult, ALU.add, [S0b[i], EGBC, BK[4]], [S0b[i]])
                    stt("dve", C0b[i][:, h, :], C0b[i][:, h, :], WPBC[0:64, r * 4 + h:r * 4 + h + 1],
                        BK[6][0:64, h * 128:(h + 1) * 128], ALU.mult, ALU.add, [C0b[i], WPBC, BK[6]], [C0b[i]])
                dma("sp", oS_d[r].rearrange("h d e -> d h e"), S0b[i][:], [S0b[i]], [])
                dma("act", oC_d[r].rearrange("h d e -> d h e"), C0b[i][:], [C0b[i]], [])

            MIXs = XNs
            tt("dve", T1[:].rearrange("p (h e) -> p h e", h=4), QSp[0:R_, :].rearrange("p (h e) -> p h e", h=4),
               bc3(EGs[:], 4, 128), ALU.mult, [QSp, EGs], [T1])
            tt("dve", Ug[:], Ug[:], bc3(QKG[:, 0:4], 4, 128), ALU.mult, [Ug, QKG], [Ug])
            tt("dve", T1[:], T1[:], Ug[:].rearrange("p h e -> p (h e)"), ALU.add, [T1, Ug], [T1])
            hn_gate16(T1, ZSGs, MIXs[:, 0:512], MIXs)
            PQ = SMs[5]
            tt("dve", PQ[:, 4:8], Ps[:, 0:4], QKMs[:, 0:4], ALU.mult, [Ps, QKMs], [PQ])
            tt("dve", T1[:].rearrange("p (h e) -> p h e", h=4), QCp[0:R_, :].rearrange("p (h e) -> p h e", h=4),
               bc3(WPs[:, 0:4], 4, 128), ALU.mult, [QCp, WPs], [T1])
            tt("dve", Ug[:], MVs[:], bc3(PQ[:, 4:8], 4, 128), ALU.mult, [MVs, PQ], [Ug])
            tt("dve", T1[:], T1[:], Ug[:].rearrange("p h e -> p (h e)"), ALU.add, [T1, Ug], [T1])
            DENs, EMTs = SMs[6], SMs[7]
            tt("dve", DENs[:, 4:8], WPs[:, 0:4], QNs[:, 0:4], ALU.mult, [WPs, QNs], [DENs])
            tt("dve", DENs[:, 4:8], DENs[:, 4:8], PQ[:, 4:8], ALU.add, [DENs, PQ], [DENs])
            ts("dve", DENs[:, 0:4], DENs[:, 4:8], -1.0, None, ALU.mult, None, [DENs], [DENs])
            tt("dve", DENs[:, 4:8], DENs[:, 4:8], DENs[:, 0:4], ALU.max, [DENs], [DENs])
            act(EMTs[:, 4:8], MTs[:, 0:4], AF.Exp, [MTs], [EMTs], scale=-1.0)
            tt("dve", DENs[:, 4:8], DENs[:, 4:8], EMTs[:, 4:8], ALU.max, [DENs, EMTs], [DENs])
            P.op("dve", lambda e, d=DENs: e.reciprocal(out=d[:, 4:8], in_=d[:, 4:8]), [DENs], [DENs])
            tt("dve", T1[:].rearrange("p (h e) -> p h e", h=4), T1[:].rearrange("p (h e) -> p h e", h=4),
               bc3(DENs[:, 4:8], 4, 128), ALU.mult, [T1, DENs], [T1])
            hn_gate16(T1, MOSGs, MIXs[:, 512:1024], MIXs)
            for k in range(8):
                tr(TPB[:, k * 128:k * 128 + R_], MIXs[:, k * 128:(k + 1) * 128], IDB[0:R_, 0:R_], [MIXs, IDB], [TPB])
            cp("act", HTs[:], TPB[:].rearrange("p (k t) -> p k t", k=8)[:, :, 0:R_], [TPB], [HTs])
            for eh in range(2):
                for k in range(8):
                    mm(BK[eh][0:R_, :], HTs[:, k, :], WOUT[:, k, eh * 512:(eh + 1) * 512], k == 0, k == 7,
                       [HTs, WOUT], [BK[eh]])
            ss, rs = SMs[8], SMs[9]
            for eh in range(2):
                act(XNs[:, eh * 512:(eh + 1) * 512], BK[eh][0:R_, :], AF.Square, [BK[eh]], [XNs, ss],
                    accum=ss[:, eh:eh + 1])
            tt("dve", ss[:, 2:3], ss[:, 0:1], ss[:, 1:2], ALU.add, [ss], [ss])
            rsqrt_small(rs[:, 0:1], ss[:, 2:3], 1.0 / D, [ss], [rs], rs[:, 1:2])
            for eh in range(2):
                sl = slice(eh * 512, (eh + 1) * 512)
                stt("dve", T1[:], BK[eh][0:R_, :], rs[:, 0:1], GPM[0:R_, sl], ALU.mult, ALU.mult, [BK[eh], rs, GPM], [T1])
                tt("dve", XS[:, sl], XS[:, sl], T1[:], ALU.add, [XS, T1], [XS])
            for k in range(8):
                tr(BK[2][:, k * R_:(k + 1) * R_], XS[:, k * 128:(k + 1) * 128], IDF16, [XS, CONST], [BK[2]])
            cp("dve", XS1T[:].rearrange("p k r -> p (k r)"), BK[2][:, 0:8 * R_], [BK[2]], [XS1T])
            act(XNs[:], XS[:], AF.Square, [XS], [XNs, ss], accum=ss[:, 4:5])
            rsqrt_small(rs[:, 4:5], ss[:, 4:5], 1.0 / D, [ss], [rs], rs[:, 5:6])
            ts("dve", XNs[:], XS[:], rs[:, 4:5], None, ALU.mult, None, [XS, rs], [XNs])
            for k in range(8):
                tr(TPB[:, k * 128:k * 128 + R_], XNs[:, k * 128:(k + 1) * 128], IDB[0:R_, 0:R_], [XNs, IDB], [TPB])
            tt("dve", HNS[:], TPB[:].rearrange("p (k t) -> p k t", k=8)[:, :, 0:R_], bc3(GPREMLP[:], 8, R_), ALU.mult,
               [TPB, GPREMLP], [HNS])

            P.barrier(all_res_p1 + RX + [XS1T.r, HNS.r, IDF2.r, IDB.r, GPREMLP.r])
            p1b.close()

        with ExitStack() as p2:
            WUP = p2.enter_context(nc.sbuf_tensor("sb_WUP", [128, 8, DFF], BF16))
            WDN = p2.enter_context(nc.sbuf_tensor("sb_WDN", [128, 32, D], BF16))
            NWC = 8
            RWU = [Res("WUP%d" % i) for i in range(NWC)]
            RWD = [Res("WDN%d" % i) for i in range(NWC)]
            wup_v = wup_d.rearrange("(k p) c -> p k c", p=128)
            wdn_v = wdn_d.rearrange("(k p) c -> p k c", p=128)
            for i in range(NWC if KPH2 else 0):
                dma("pool", WUP[:, :, i * 512:(i + 1) * 512], wup_v[:, :, i * 512:(i + 1) * 512], [], [RWU[i]])
                dma("pool", WDN[:, i * 4:(i + 1) * 4, :], wdn_v[:, i * 4:(i + 1) * 4, :], [], [RWD[i]])
            XN2 = sb(p2, "XN2", [128, D], BF16)
            GPL = sb(p2, "GPL", [128, D])
            dma("sp", GPL[:], gpl_d.partition_broadcast(128), [], [GPL])
            HN = sb(p2, "HN", [128, 8, 256], BF16)
            UT = [sb(p2, "UT%d" % i, [128, 256], BF16) for i in range(2)]
            RL = [sb(p2, "RL%d" % i, [128, 256]) for i in range(2)]
            SS2 = sb(p2, "SS2", [128, 16])

            NB2 = NT // 2 if KPH2 else 0

            def xv_of(blk, j):
                return Xt[:, 2 * blk + j, :], RX[2 * blk + j]

            def prep(blk):
                for j in range(2):
                    xv, rx = xv_of(blk, j)
                    act(XN2[:], xv, AF.Square, [rx], [XN2, SS2], accum=SS2[:, 0:1])
                    rsqrt_small(SS2[:, 1:2], SS2[:, 0:1], 1.0 / D, [SS2], [SS2], SS2[:, 2:3])
                    ts("dve", XN2[:], xv, SS2[:, 1:2], None, ALU.mult, None, [rx, SS2], [XN2])
                    for k in range(8):
                        tr(TPB[:, k * 128:(k + 1) * 128], XN2[:, k * 128:(k + 1) * 128], IDB[:], [XN2, IDB], [TPB])
                    tt("dve", HN[:, :, j * 128:(j + 1) * 128], TPB[:].rearrange("p (k t) -> p k t", k=8),
                       bc3(GPREMLP[:], 8, 128), ALU.mult, [TPB, GPREMLP], [HN])

            def up(f, hn, n):
                bk = BK[f % 2]
                for k in range(8):
                    mm(bk[:, 0:n], WUP[:, k, f * 128:(f + 1) * 128], hn[:, k, 0:n], k == 0, k == 7,
                       [hn, RWU[f // 4]], [bk])
                rl, ut = RL[f % 2], UT[f % 2]
                act(rl[:, 0:n], bk[:, 0:n], AF.Relu, [bk], [rl])
                tt("dve", ut[:, 0:n], rl[:, 0:n], rl[:, 0:n], ALU.mult, [rl], [ut])

            def down(f, rows, ntl):
                ut = UT[f % 2]
                for j in range(ntl):
                    for eh in range(2):
                        ab = BK[2 + 2 * j + eh]
                        mm(ab[0:rows, :], ut[:, j * rows:(j + 1) * rows], WDN[:, f, eh * 512:(eh + 1) * 512],
                           f == 0, f == 31, [ut, RWD[f // 4]], [ab])

            def fin(blk):
                for j in range(2):
                    for eh in range(2):
                        ab = BK[2 + 2 * j + eh]
                        act(XN2[:, eh * 512:(eh + 1) * 512], ab[:], AF.Square, [ab], [XN2, SS2],
                            accum=SS2[:, 8 + 2 * j + eh:9 + 2 * j + eh])
                for j in range(2):
                    tt("dve", SS2[:, 12 + j:13 + j], SS2[:, 8 + 2 * j:9 + 2 * j], SS2[:, 9 + 2 * j:10 + 2 * j], ALU.add,
                       [SS2], [SS2])
                ts("dve", SS2[:, 14:16], SS2[:, 12:14], 1.0 / D, EPS, ALU.mult, ALU.add, [SS2], [SS2])
                act(SS2[:, 14:16], SS2[:, 14:16], AF.Ln, [SS2], [SS2])
                act(SS2[:, 14:16], SS2[:, 14:16], AF.Exp, [SS2], [SS2], scale=-0.5)
                for j in range(2):
                    xv, rx = xv_of(blk, j)
                    for eh in range(2):
                        ab = BK[2 + 2 * j + eh]
                        sl = slice(eh * 512, (eh + 1) * 512)
                        stt("dve", ab[:], ab[:], SS2[:, 14 + j:15 + j], GPL[:, sl], ALU.mult, ALU.mult,
                            [ab, SS2, GPL], [ab])
                        tt("dve", xv[:, sl], xv[:, sl], ab[:], ALU.add, [rx, ab], [rx])
                    r0 = (2 * blk + j) * 128
                    dma("sp", y_d[r0:r0 + 128, :], xv, [rx], [])

            if NB2:
                prep(0)
            for blk in range(NB2):
                up(0, HN, 256)
                for f in range(32):
                    if f + 1 < 32:
                        up(f + 1, HN, 256)
                    elif blk + 1 < NB2:
                        prep(blk + 1)
                    down(f, 128, 2)
                fin(blk)

            R_ = NS
            if KPH2:
                up(0, HNS, R_)
                for f in range(32):
                    if f + 1 < 32:
                        up(f + 1, HNS, R_)
                    down(f, R_, 1)
            if KPH2:
                for eh in range(2):
                    act(XN2[0:R_, eh * 512:(eh + 1) * 512], BK[2 + eh][0:R_, :], AF.Square, [BK[2 + eh]], [XN2, SS2],
                        accum=SS2[0:R_, 3 + eh:4 + eh])
                tt("dve", SS2[0:R_, 5:6], SS2[0:R_, 3:4], SS2[0:R_, 4:5], ALU.add, [SS2], [SS2])
                rsqrt_small(SS2[0:R_, 6:7], SS2[0:R_, 5:6], 1.0 / D, [SS2], [SS2], SS2[0:R_, 7:8])
                YSB = XN2.t.bitcast(F32)
                for eh in range(2):
                    sl = slice(eh * 512, (eh + 1) * 512)
                    stt("dve", BK[2 + eh][0:R_, :], BK[2 + eh][0:R_, :], SS2[0:R_, 6:7], GPL[0:R_, sl], ALU.mult, ALU.mult,
                        [BK[2 + eh], SS2, GPL], [BK[2 + eh]])
                    for j in range(4):
                        tr(BK[4 + eh][0:R_, j * 128:(j + 1) * 128], XS1T[:, 4 * eh + j, :], IDF2[:], [XS1T, IDF2],
                           [BK[4 + eh]])
                    cp("act", YSB[0:R_, :], BK[4 + eh][0:R_, :], [BK[4 + eh]], [XN2])
                    tt("dve", YSB[0:R_, :], YSB[0:R_, :], BK[2 + eh][0:R_, :], ALU.add, [XN2, BK[2 + eh]], [XN2])
                    dma("sp", ys_d[:, sl], YSB[0:R_, :], [XN2], [])

        n_ins = P.finalize(top)
    return nc, n_ins


_CACHE = {}


def kernel(x_prompt, x_sample, state_gdn_conv, state_gdn_S, state_mlstm_C, state_mlstm_n, state_mlstm_m,
           norm_pre_mix, w_in, conv_w, a_log, dt_bias, gdn_norm_g, b_igate, b_fgate, mlstm_norm_g, w_out,
           norm_post_mix, norm_pre_mlp, w_up, w_down, norm_post_mlp):
    f = lambda a: np.ascontiguousarray(np.asarray(a, dtype=np.float32))
    if "nc" not in _CACHE:
        _CACHE["nc"] = build_program()
    nc, _ = _CACHE["nc"]
    consts = make_consts()
    small = np.concatenate([f(a_log)[0], f(dt_bias)[0], f(b_igate)[0], f(b_fgate)[0]])[None, :]
    shared = {
        "w_in": f(w_in)[0], "w_out": f(w_out)[0], "w_up": f(w_up)[0], "w_down": f(w_down)[0],
        "consts": consts,
        "gpre_fm": f(f(norm_pre_mix)[0].reshape(8, 128).T),
        "gpremlp_fm": f(f(norm_pre_mlp)[0].reshape(8, 128).T),
        "cw_fm": f(f(conv_w)[0].reshape(4, 12, 128).transpose(2, 0, 1).reshape(128, 48)),
        "gpostmix": f(norm_post_mix)[0][None, :], "gpostmlp": f(norm_post_mlp)[0][None, :],
        "small": f(small), "gdn_norm_g": f(gdn_norm_g)[0][None, :], "mlstm_norm_g": f(mlstm_norm_g)[0][None, :],
    }
    xp, xs = f(x_prompt), f(x_sample)
    in_maps = []
    for c in range(NCORES):
        r = slice(c * NS, (c + 1) * NS)
        m = dict(shared)
        m.update({
            "x": xp[c], "xs": xs[r, 0, :],
            "sconv": f(state_gdn_conv)[0, r], "sS": f(state_gdn_S)[0, r], "sC": f(state_mlstm_C)[0, r],
            "sn": f(state_mlstm_n)[0, r].reshape(NS, 256), "sm": f(state_mlstm_m)[0, r],
        })
        in_maps.append(m)
    res = run_bass_kernel_spmd(nc, in_maps, core_ids=list(range(NCORES)))
    R = res.results
    g = lambda k: np.stack([np.asarray(R[c][k], dtype=np.float32) for c in range(NCORES)])
    gc = lambda k: np.concatenate([np.asarray(R[c][k], dtype=np.float32) for c in range(NCORES)], axis=0)
    y_prompt = g("y")
    y_sample = gc("ys")[:, None, :]
    p_conv = g("pconv")[None]
    p_S = g("pS")[None]
    p_C = g("pC")[None]
    p_n = g("pn")[None]
    p_m = g("pm").reshape(NCORES, 4)[None]
    s_conv = gc("oconv")[None]
    s_S = gc("oS")[None]
    s_C = gc("oC")[None]
    s_n = gc("on").reshape(NCORES * NS, 4, 64)[None]
    s_m = gc("om")[None]
    return (y_prompt, y_sample, p_conv, p_S, p_C, p_n, p_m, s_conv, s_S, s_C, s_n, s_m)
```

```python
from contextlib import ExitStack
import numpy as np
import concourse.bass as bass
import concourse.mybir as mybir
from concourse.bass_utils import run_bass_kernel_spmd

F32 = mybir.dt.float32
BF16 = mybir.dt.bfloat16
ALU = mybir.AluOpType
AF = mybir.ActivationFunctionType
AX = mybir.AxisListType

NCORES = 8
T = 2048
NT = T // 128
D = 1024
DFF = 4096
NS = 16
EPS = 1e-6
NEG = -30000.0
import os
KNT = int(os.environ.get('KNT', NT))
KPH2 = int(os.environ.get('KPH2', 1))
KSTAGE = int(os.environ.get('KSTAGE', 99))
KSUB = int(os.environ.get('KSUB', 99))
KRATIO = [int(v) for v in os.environ.get('KRATIO', '1,1,2').split(',')]
KSCHED = int(os.environ.get('KSCHED', 0))
KSEG = int(os.environ.get('KSEG', 7))
KLO = int(os.environ.get('KLO', 0))
KHI = int(os.environ.get('KHI', 99))


class Res:
    __slots__ = ("name", "w", "rd")

    def __init__(self, name):
        self.name = name
        self.w = None
        self.rd = []


class Op:
    __slots__ = ("eng", "fn", "reads", "writes", "dma", "deps", "signal", "cnt", "sem", "waits", "cost", "tab")

    def __init__(self, eng, fn, reads, writes, dma, cost=0.4, tab=None):
        self.cost = cost
        self.tab = tab
        self.eng = eng
        self.fn = fn
        self.reads = reads
        self.writes = writes
        self.dma = dma
        self.deps = []
        self.signal = False
        self.cnt = 0
        self.sem = None
        self.waits = []


def _res(lst):
    out = []
    for x in lst:
        if x is None:
            continue
        if isinstance(x, Res):
            out.append(x)
        elif isinstance(x, (list, tuple)):
            out.extend(_res(x))
        else:
            out.append(x.r)
    return out


class Prog:
    ENGS = ("pe", "act", "dve", "pool", "sp")

    def __init__(self, nc, n_dma_sems=56):
        self.nc = nc
        self.ops = []
        self.n_dma_sems = n_dma_sems
        self.n_sw_sems = 8
        self.fence = Res("fence")
        self.marks = []
        self.engobj = {"pe": nc.tensor, "act": nc.scalar, "dve": nc.vector,
                       "pool": nc.gpsimd, "sp": nc.sync}

    def op(self, eng, fn, r=(), w=(), cost=0.4, tab=None):
        self.ops.append(Op(eng, fn, _res(r) + ([self.fence] if KSCHED else []), _res(w), False, cost, tab))

    def dma(self, eng, fn, r=(), w=(), cost=2.5):
        self.ops.append(Op(eng, fn, _res(r) + ([self.fence] if KSCHED else []), _res(w), True, cost))

    def barrier(self, allres):
        for e in ("pe", "act", "dve", "pool", "sp"):
            self.ops.append(Op(e, (lambda en: en.nop(nofuse=True)), [], _res(allres) + [self.fence], False, 0.1))

    def _schedule(self):
        ops = self.ops
        n = len(ops)
        LAT = 0.2
        succ = [[] for _ in range(n)]
        for i, o in enumerate(ops):
            for j in o.deps:
                succ[j].append(i)
        prio = [0.0] * n
        for i in range(n - 1, -1, -1):
            m = 0.0
            for k in succ[i]:
                if prio[k] + LAT > m:
                    m = prio[k] + LAT
            prio[i] = ops[i].cost + m
        if KSCHED == 2:
            prio = [float(n - i) for i in range(n)]
        ndep = [len(o.deps) for o in ops]
        ready = [0.0] * n
        finish = [0.0] * n
        avail = {e: [] for e in self.ENGS}
        for i, o in enumerate(ops):
            if ndep[i] == 0:
                avail[o.eng].append(i)
        free = {e: 0.0 for e in self.ENGS}
        lasttab = None
        order = []
        start = [0.0] * n
        done = 0
        while done < n:
            best_e, best_i, best_t = None, None, None
            for e in self.ENGS:
                av = avail[e]
                if not av:
                    continue
                fe = free[e]
                cand = None
                cs = None
                tmin_i, tmin = None, None
                for i in av:
                    r = ready[i]
                    if tmin is None or r < tmin or (r == tmin and i < tmin_i):
                        tmin, tmin_i = r, i
                    if r <= fe + 1e-9:
                        sc = prio[i]
                        if e == "act" and ops[i].tab is not None and lasttab is not None and ops[i].tab != lasttab:
                            sc -= 4.0
                        if cs is None or sc > cs or (sc == cs and i < cand):
                            cand, cs = i, sc
                if cand is None:
                    cand = tmin_i
                t0 = max(fe, ready[cand])
                if best_t is None or t0 < best_t:
                    best_e, best_i, best_t = e, cand, t0
            i = best_i
            o = ops[i]
            avail[best_e].remove(i)
            c = o.cost
            if best_e == "act" and o.tab is not None:
                if lasttab is not None and o.tab != lasttab:
                    c += 1.3
                lasttab = o.tab
            start[i] = best_t
            if o.dma:
                free[best_e] = best_t + 0.1
                finish[i] = best_t + c
            else:
                free[best_e] = best_t + c
                finish[i] = best_t + c
            order.append(i)
            done += 1
            for k in succ[i]:
                ndep[k] -= 1
                if finish[i] + LAT > ready[k]:
                    ready[k] = finish[i] + LAT
                if ndep[k] == 0:
                    avail[ops[k].eng].append(k)
        newpos = {old: new for new, old in enumerate(order)}
        newops = [ops[i] for i in order]
        for o in newops:
            o.deps = sorted(newpos[j] for j in o.deps)
        self.ops = newops
        self.est_us = max(finish) if n else 0.0

    def finalize(self, stack):
        nc = self.nc
        ops = self.ops
        for i, o in enumerate(ops):
            deps = set()
            for r in o.reads:
                if r.w is not None:
                    deps.add(r.w)
            for r in o.writes:
                if r.w is not None:
                    deps.add(r.w)
                for j in r.rd:
                    deps.add(j)
            deps.discard(i)
            o.deps = sorted(deps)
            for r in o.reads:
                r.rd.append(i)
            for r in o.writes:
                r.w = i
                r.rd = []
        if KSCHED:
            seg = 0
            last = {}
            nbar = 0
            for i, o in enumerate(ops):
                if o.cost == 0.1 and not o.dma and o.writes and o.writes[-1] is self.fence:
                    nbar += 1
                    if nbar % 5 == 1:
                        seg += 1
                        last = {}
                free_ = (KSEG >> min(seg, 2)) & 1
                if seg == 0 and self.marks:
                    lo = self.marks[min(KLO, len(self.marks) - 1)]
                    hi = self.marks[min(KHI, len(self.marks) - 1)] if KHI < len(self.marks) else 10 ** 9
                    free_ = free_ and (lo <= i < hi)
                if not free_:
                    if o.eng in last and last[o.eng] not in o.deps:
                        o.deps = sorted(o.deps + [last[o.eng]])
                    last[o.eng] = i
            self._schedule()
            ops = self.ops
        dma_slot_last = [None] * self.n_dma_sems
        dma_i = 0
        sw_i = 0
        for i, o in enumerate(ops):
            if o.dma:
                if o.eng == "pool":
                    slot = sw_i % self.n_sw_sems
                    sw_i += 1
                else:
                    slot = self.n_sw_sems + dma_i % (self.n_dma_sems - self.n_sw_sems)
                    dma_i += 1
                o.sem = slot
                if dma_slot_last[slot] is not None and dma_slot_last[slot] not in o.deps:
                    o.deps = sorted(o.deps + [dma_slot_last[slot]])
                dma_slot_last[slot] = i
            for j in o.deps:
                pj = ops[j]
                if pj.dma:
                    continue
                if pj.eng == "pe" and o.eng == "pe" and not o.dma:
                    continue
                pj.signal = True
        cnt = {e: 0 for e in self.ENGS}
        dcnt = [0] * self.n_dma_sems
        for o in ops:
            if o.dma:
                dcnt[o.sem] += 16
                o.cnt = dcnt[o.sem]
            elif o.signal:
                cnt[o.eng] += 1
                o.cnt = cnt[o.eng]
        seen = {e: {} for e in self.ENGS}
        for o in ops:
            need = {}
            for j in o.deps:
                pj = ops[j]
                if pj.dma:
                    key = ("d", pj.sem)
                else:
                    if pj.eng == "pe" and o.eng == "pe" and not o.dma:
                        continue
                    key = ("e", pj.eng)
                if pj.cnt > need.get(key, 0):
                    need[key] = pj.cnt
            s = seen[o.eng]
            for key, v in need.items():
                if s.get(key, 0) >= v:
                    continue
                s[key] = v
                o.waits.append((key, v))
        final_waits = [(("d", k), dcnt[k]) for k in range(self.n_dma_sems) if dcnt[k] > 0]
        final_waits += [(("e", e), cnt[e]) for e in self.ENGS if cnt[e] > 0 and e != "sp"]
        esem = {e: stack.enter_context(nc.semaphore("s_" + e)) for e in self.ENGS}
        dsem = [stack.enter_context(nc.semaphore("d_%d" % k)) for k in range(self.n_dma_sems)]

        def semof(key):
            return dsem[key[1]] if key[0] == "d" else esem[key[1]]

        n_ins = 0
        for o in ops:
            e = self.engobj[o.eng]
            for key, v in o.waits:
                e.wait_ge(semof(key), v)
                n_ins += 1
            ins = o.fn(e)
            n_ins += 1
            if o.dma:
                ins.then_inc(dsem[o.sem], 16)
            elif o.signal:
                ins.then_inc(esem[o.eng], 1)
        sp = self.engobj["sp"]
        for key, v in final_waits:
            if seen["sp"].get(key, 0) >= v:
                continue
            sp.wait_ge(semof(key), v)
        return n_ins


class View:
    __slots__ = ("t", "r")

    def __init__(self, ap, r):
        self.t = ap
        self.r = r

    def __getitem__(self, k):
        return self.t[k]


class Tl:
    __slots__ = ("t", "r")

    def __init__(self, t, name):
        self.t = t
        self.r = Res(name)

    def __getitem__(self, k):
        return self.t[k]


C_IDF, C_TRI, C_ONES, C_SEL127, C_MST, C_MIT, C_MI, C_SELR = (
    0, 128, 256, 384, 512, 640, 768, 896)
NCONST = 1408


def make_consts():
    c = np.zeros((128, NCONST), np.float32)
    s = np.arange(128)[:, None]
    f = np.arange(128)[None, :]
    c[:, C_IDF:C_IDF + 128] = (s == f)
    c[:, C_TRI:C_TRI + 128] = (s <= f)
    c[:, C_ONES:C_ONES + 128] = 1.0
    c[:, C_SEL127:C_SEL127 + 128] = (s == 127)
    mst = np.where(s < f, 0.0, NEG)
    mit = np.where(s <= f, 0.0, NEG)
    mi = np.where(f <= s, 0.0, NEG)
    c[:, C_MST:C_MST + 128] = mst
    c[:, C_MIT:C_MIT + 128] = mit
    c[:, C_MI:C_MI + 128] = mi
    for h in range(4):
        c[h, C_SELR + h * 128:C_SELR + (h + 1) * 128] = 1.0
    return c


W_QKV, W_MQK, W_GZ, W_MV, W_MO, W_GATE = 0, 1536, 2048, 2560, 3072, 3584
WIN_MOVES = [(0, 0, 1536), (1536, 2056, 512), (2048, 1536, 512), (2560, 2568, 512),
             (3072, 3080, 512), (3584, 2048, 8), (3592, 3596, 4), (3596, 3592, 4)]


def build_program():
    nc = bass.Bass("TRN2", target_bir_lowering=False)
    P = Prog(nc)

    def din(name, shape):
        return nc.dram_tensor(name, list(shape), F32, kind="ExternalInput").ap()

    def dout(name, shape):
        return nc.dram_tensor(name, list(shape), F32, kind="ExternalOutput").ap()

    x_d = din("x", [T, D])
    xs_d = din("xs", [NS, D])
    sconv_d = din("sconv", [NS, 3, 1536])
    sS_d = din("sS", [NS, 4, 128, 128])
    sC_d = din("sC", [NS, 4, 64, 128])
    sn_d = din("sn", [NS, 256])
    sm_d = din("sm", [NS, 4])
    win_d = din("w_in", [D, 3600])
    wout_d = din("w_out", [D, D])
    wup_d = din("w_up", [D, DFF])
    wdn_d = din("w_down", [DFF, D])
    consts_d = din("consts", [128, NCONST])
    gpre_d = din("gpre_fm", [128, 8])
    gpremlp_d = din("gpremlp_fm", [128, 8])
    cw_d = din("cw_fm", [128, 48])
    gpm_d = din("gpostmix", [1, D])
    gpl_d = din("gpostmlp", [1, D])
    small_d = din("small", [1, 16])
    gng_d = din("gdn_norm_g", [1, 128])
    mng_d = din("mlstm_norm_g", [1, 128])

    y_d = dout("y", [T, D])
    ys_d = dout("ys", [NS, D])
    pconv_d = dout("pconv", [3, 1536])
    pS_d = dout("pS", [4, 128, 128])
    pC_d = dout("pC", [4, 64, 128])
    pn_d = dout("pn", [4, 64])
    pm_d = dout("pm", [1, 4])
    oconv_d = dout("oconv", [NS, 3, 1536])
    oS_d = dout("oS", [NS, 4, 128, 128])
    oC_d = dout("oC", [NS, 4, 64, 128])
    on_d = dout("on", [NS, 256])
    om_d = dout("om", [NS, 4])

    def fsz(ap):
        try:
            return int(ap.free_size())
        except Exception:
            return 128

    def ecost(eng, ap):
        n = fsz(ap)
        if eng == "act":
            return 0.23 + n / 1200.0
        if eng == "pool":
            return 0.12 + n / 480.0
        return 0.08 + n / 960.0

    def mm(out, lhsT, rhs, start, stop, r, w, skip=False):
        passes = 4 if lhsT.dtype == F32 else 1
        c = 0.035 + passes * fsz(out) / 2400.0
        if skip:
            P.op("pe", lambda e, o=out, l=lhsT, rr=rhs, s=start, t=stop:
                 e.matmul(o, lhsT=l, rhs=rr, start=s, stop=t, skip_group_check=True), r, w, cost=c)
        else:
            P.op("pe", lambda e, o=out, l=lhsT, rr=rhs, s=start, t=stop:
                 e.matmul(o, lhsT=l, rhs=rr, start=s, stop=t), r, w, cost=c)

    def tr(out, in_, ident, r, w):
        P.op("pe", lambda e, o=out, i=in_, d=ident: e.transpose(o, i, d), r, w, cost=0.04 + fsz(out) / 2400.0)

    def tt(eng, out, in0, in1, op, r, w):
        P.op(eng, lambda e, o=out, a=in0, b=in1, p=op: e.tensor_tensor(out=o, in0=a, in1=b, op=p), r, w,
             cost=ecost(eng, out))

    def ts(eng, out, in0, s1, s2, op0, op1, r, w, accum=None):
        c = ecost(eng, out)
        if op1 is None:
            P.op(eng, lambda e, o=out, a=in0, x=s1, p0=op0:
                 e.tensor_scalar(out=o, in0=a, scalar1=x, scalar2=None, op0=p0), r, w, cost=c)
        elif accum is None:
            P.op(eng, lambda e, o=out, a=in0, x=s1, y=s2, p0=op0, p1=op1:
                 e.tensor_scalar(out=o, in0=a, scalar1=x, scalar2=y, op0=p0, op1=p1), r, w, cost=c)
        else:
            P.op(eng, lambda e, o=out, a=in0, x=s1, y=s2, p0=op0, p1=op1, ac=accum:
                 e.tensor_scalar(out=o, in0=a, scalar1=x, scalar2=y, op0=p0, op1=p1, accum_out=ac), r, w, cost=c)

    def stt(eng, out, in0, scalar, in1, op0, op1, r, w):
        P.op(eng, lambda e, o=out, a=in0, s=scalar, b=in1, p0=op0, p1=op1:
             e.scalar_tensor_tensor(out=o, in0=a, scalar=s, in1=b, op0=p0, op1=p1), r, w, cost=ecost(eng, out))

    def act(out, in_, func, r, w, bias=None, scale=1.0, accum=None):
        kw = {}
        if bias is not None:
            kw["bias"] = bias
        if accum is not None:
            kw["accum_out"] = accum
        tab = None
        if func in (AF.Silu, AF.Sigmoid):
            tab = "S"
        elif func in (AF.Exp, AF.Ln):
            tab = "E"
        P.op("act", lambda e, o=out, i=in_, f=func, s=scale, k=kw:
             e.activation(out=o, in_=i, func=f, scale=s, **k), r, w, cost=ecost("act", out), tab=tab)

    def cp(eng, out, in_, r, w):
        if eng == "act":
            act(out, in_, AF.Copy, r, w)
        else:
            P.op(eng, lambda e, o=out, i=in_: e.tensor_copy(out=o, in_=i), r, w, cost=ecost(eng, out))

    def red(eng, out, in_, op, r, w):
        P.op(eng, lambda e, o=out, i=in_, p=op: e.tensor_reduce(out=o, in_=i, axis=AX.X, op=p), r, w,
             cost=ecost(eng, in_))

    def memset(eng, ap, val, w):
        P.op(eng, lambda e, a=ap, v=val: e.memset(a, v), [], w, cost=ecost(eng, ap))

    def dma(q, out, in_, r, w, slow=False):
        if slow:
            P.dma(q, lambda e, o=out, i=in_: e.dma_start(out=o, in_=i, allow_slow_non_contiguous=True), r, w)
        else:
            P.dma(q, lambda e, o=out, i=in_: e.dma_start(out=o, in_=i), r, w)

    def rsqrt_small(out, in_, scale, r_, w_, tmp):
        act(tmp, in_, AF.Ln, list(r_) + [w_[0], EPSB], [w_[0]], bias=EPSB[0:in_.shape[0], 0:1], scale=scale)
        act(out, tmp, AF.Exp, [w_[0]], w_, scale=-0.5)

    def bc3(ap, n_mid, n_in):
        return ap.unsqueeze(2).to_broadcast([ap.shape[0], n_mid, n_in])

    def bcm(ap, n_mid):
        return ap.unsqueeze(1).to_broadcast([ap.shape[0], n_mid, ap.shape[1]])

    with ExitStack() as top:
        def sb(stack, name, shape, dt=F32):
            return Tl(stack.enter_context(nc.sbuf_tensor("sb_" + name, list(shape), dt)), name)

        def ps(stack, name, shape, dt=F32):
            return Tl(stack.enter_context(nc.psum_tensor("ps_" + name, list(shape), dt)), name)

        Xt = top.enter_context(nc.sbuf_tensor("sb_X", [128, NT, D], F32))
        RX = [Res("X%d" % t) for t in range(NT)]
        XS1T = sb(top, "XS1T", [128, 8, NS])
        HNS = sb(top, "HNS", [128, 8, NS], BF16)
        IDF2 = sb(top, "IDF2", [128, 128])
        IDB = sb(top, "IDB", [128, 128], BF16)
        GPREMLP = sb(top, "GPREMLP", [128, 8])
        EPSB = sb(top, "EPSB", [128, 1])
        BK = [ps(top, "B%d" % i, [128, 512]) for i in range(7)]
        TPB = ps(top, "TPB", [128, 1024], BF16)

        dma("sp", GPREMLP[:], gpremlp_d, [], [GPREMLP])
        memset("pool", EPSB[:], EPS, [EPSB])

        all_res_p1 = []

        with ExitStack() as p1:
            cur = [p1]

            def s1(name, shape, dt=F32):
                tl = sb(cur[0], name, shape, dt)
                all_res_p1.append(tl.r)
                return tl

            WIN = p1.enter_context(nc.sbuf_tensor("sb_WIN", [128, 8, 3600], BF16))
            RWIN = [Res("WIN%d" % i) for i in range(len(WIN_MOVES))]
            WOUT = s1("WOUT", [128, 8, D], BF16)
            CONST = s1("CONST", [128, NCONST])
            GPRE = s1("GPRE", [128, 8])
            CW = s1("CW", [128, 4, 12])
            GPM = s1("GPM", [128, D])
            SMB = s1("SMB", [128, 16])
            GNG = s1("GNG", [128, 128])
            MNG = s1("MNG", [128, 128])
            all_res_p1.extend(RWIN)

            win_v = win_d.rearrange("(k p) c -> p k c", p=128)
            for i, (dst, src, n) in enumerate(WIN_MOVES):
                dma("pool", WIN[:, :, dst:dst + n], win_v[:, :, src:src + n], [], [RWIN[i]])
            dma("pool", WOUT[:], wout_d.rearrange("(k p) c -> p k c", p=128), [], [WOUT])
            dma("sp", CONST[:], consts_d, [], [CONST])
            dma("sp", GPRE[:], gpre_d, [], [GPRE])
            dma("sp", CW[:], cw_d.rearrange("p (j c) -> p j c", j=4), [], [CW])
            dma("sp", GPM[:], gpm_d.partition_broadcast(128), [], [GPM])
            dma("sp", SMB[:], small_d.partition_broadcast(128), [], [SMB])
            dma("sp", GNG[:], gng_d.partition_broadcast(128), [], [GNG])
            dma("sp", MNG[:], mng_d.partition_broadcast(128), [], [MNG])

            IDF = CONST[:, C_IDF:C_IDF + 128]
            TRI = CONST[:, C_TRI:C_TRI + 128]
            ONES = CONST[:, C_ONES:C_ONES + 128]
            SEL127 = CONST[:, C_SEL127:C_SEL127 + 128]
            MST = CONST[:, C_MST:C_MST + 128]
            MIT = CONST[:, C_MIT:C_MIT + 128]
            MI = CONST[:, C_MI:C_MI + 128]
            SELR = CONST[0:4, C_SELR:C_SELR + 512]
            ONES4 = CONST[0:4, C_ONES:C_ONES + 128]

            def rwin(c0, c1):
                out = []
                for i, (dst, src, n) in enumerate(WIN_MOVES):
                    if dst < c1 and c0 < dst + n:
                        out.append(RWIN[i])
                return out

            cp("dve", IDB[:], IDF, [CONST], [IDB])
            cp("pool", IDF2[:], IDF, [CONST], [IDF2])

            NEGC = s1("NEGC", [128, 12])
            SGN = s1("SGN", [128, 12])
            BIA = s1("BIA", [128, 12])
            memset("pool", NEGC[:], -1.0, [NEGC])
            act(NEGC[:, 4:8], SMB[:, 0:4], AF.Exp, [SMB, NEGC], [NEGC])
            ts("dve", NEGC[:, 4:8], NEGC[:, 4:8], -1.0, None, ALU.mult, None, [NEGC], [NEGC])
            memset("pool", SGN[:], -1.0, [SGN])
            memset("pool", SGN[:, 4:8], 1.0, [SGN])
            memset("pool", BIA[:], 0.0, [BIA])
            cp("dve", BIA[:, 4:8], SMB[:, 4:8], [SMB, BIA], [BIA])
            ts("dve", BIA[:, 8:12], SMB[:, 12:16], -1.0, None, ALU.mult, None, [SMB, BIA], [BIA])

            p1a = ExitStack()
            cur[0] = p1a
            XN = s1("XN", [128, D], BF16)
            HT = s1("HT", [128, 8, 128], BF16)
            PCC = [s1("PCC%d" % i, [128, 131]) for i in range(3)]
            ACC = [s1("ACC%d" % i, [128, 128]) for i in range(3)]
            SA4 = [s1("SA4_%d" % i, [128, 8]) for i in range(2)]
            TAIL = s1("TAIL", [128, 12, 3])
            QKF = s1("QKF", [128, 8, 128])
            GQT = s1("GQT", [128, 4, 128], BF16)
            GKT = s1("GKT", [128, 4, 128], BF16)
            GVT = s1("GVT", [128, 4, 128], BF16)
            MQT = s1("MQT", [64, 4, 128], BF16)
            MKT = s1("MKT", [64, 4, 128], BF16)
            ZSG = s1("ZSG", [128, 512])
            MOSG = s1("MOSG", [128, 512])
            MVt = s1("MVt", [128, 4, 128], BF16)
            GT = s1("GT", [128, 16])
            ARG = s1("ARG", [128, 12])
            GL = s1("GL", [128, 12])
            BETA = s1("BETA", [128, 4])
            IG = s1("IG", [128, 4])
            GF = s1("GF", [128, 8])
            COLS = s1("COLS", [128, 20])
            RT = s1("RT", [4, 5, 128])
            BD = s1("BD", [4, 4, 128])
            QKD = s1("QKD", [128, 4, 128], BF16)
            MB = [s1("MB%d" % i, [128, 512]) for i in range(2)]
            LB = [s1("LB%d" % i, [128, 512]) for i in range(2)]
            RR = s1("RR", [128, 512])
            BINV = s1("BINV", [128, 4, 128], BF16)
            GKt = s1("GKt", [128, 4, 128], BF16)
            XK = s1("XK", [128, 4, 128], BF16)
            KEND = s1("KEND", [128, 4, 128], BF16)
            GVb = s1("GVb", [128, 4, 128], BF16)
            SC4 = [s1("SC4_%d" % i, [128, 8]) for i in range(6)]
            LASTG = s1("LASTG", [128, 8])
            WKT = s1("WKT", [128, 4, 128], BF16)
            S32 = s1("S32", [128, 4, 128])
            SBh = s1("SBh", [128, 4, 128], BF16)
            U = s1("U", [128, 4, 128], BF16)
            TMP = [s1("TMP%d" % i, [128, 512]) for i in range(2)]
            WV = LB[1]
            MIX = View(MB[0].t.bitcast(BF16)[:, 0:1024], MB[0].r)
            MIXT = View(RR.t.bitcast(BF16)[:, 0:1024].rearrange("p (k t) -> p k t", k=8), RR.r)
            PK = s1("PK", [128, 4, 64], BF16)
            EXPQ = TMP[1]
            EXPA = MB[0]
            MKt = s1("MKt", [128, 4, 64], BF16)
            PQKb = s1("PQKb", [128, 4, 128], BF16)
            PQKT = s1("PQKT", [128, 4, 128], BF16)
            SG4 = [s1("SG4_%d" % i, [128, 8]) for i in range(6)]
            C32 = s1("C32", [64, 4, 128])
            CBh = s1("CBh", [64, 4, 128], BF16)
            N32 = s1("N32", [64, 4])
            NBh = s1("NBh", [64, 4, 2], BF16)
            MBC = s1("MBC", [128, 4])
            DMAX = s1("DMAX", [128, 4])
            T12 = s1("T12", [128, 12])
            WLC = s1("WLC", [64, 4])
            ONEB = s1("ONEB", [128, 2], BF16)
            for b in BK:
                all_res_p1.append(b.r)
            all_res_p1.append(TPB.r)

            memset("pool", TAIL[:], 0.0, [TAIL])
            memset("pool", S32[:], 0.0, [S32])
            memset("pool", SBh[:], 0.0, [SBh])
            memset("pool", C32[:], 0.0, [C32])
            memset("pool", CBh[:], 0.0, [CBh])
            memset("pool", N32[:], 0.0, [N32])
            memset("pool", NBh[:], 0.0, [NBh])
            memset("pool", MBC[:], 0.0, [MBC])
            memset("pool", ONEB[:], 1.0, [ONEB])

            def headnorm_gate(src, gate, dst, t0, ss, rs):
                tt("pool", t0[:], src[:], src[:], ALU.mult, [src], [t0])
                red("dve", ss[:, 0:4], t0[:].rearrange("p (h e) -> p h e", h=4), ALU.add, [t0], [ss])
                rsqrt_small(rs[:, 0:4], ss[:, 0:4], 1.0 / 128, [ss], [rs], rs[:, 4:8])
                tt("dve", t0[:].rearrange("p (h e) -> p h e", h=4), src[:].rearrange("p (h e) -> p h e", h=4),
                   bc3(rs[:, 0:4], 4, 128), ALU.mult, [src, rs], [t0])
                tt("dve", dst, t0[:], gate[:], ALU.mult, [t0, gate], [MIX])

            def gen_H(th):
                for k in range(8):
                    tr(TPB[:, k * 128:(k + 1) * 128], MIX[:, k * 128:(k + 1) * 128], IDB[:], [MIX, IDB], [TPB])
                yield
                cp("act", MIXT[:].rearrange("p k t -> p (k t)"), TPB[:], [TPB], [MIXT])
                yield
                for eh in range(2):
                    for k in range(8):
                        mm(BK[4 + eh][:], MIXT[:, k, :], WOUT[:, k, eh * 512:(eh + 1) * 512], k == 0, k == 7,
                           [MIXT, WOUT], [BK[4 + eh]])
                    yield
                ss, rs = SC4[0], SC4[1]
                for eh in range(2):
                    act(MIX[:, eh * 512:(eh + 1) * 512], BK[4 + eh][:], AF.Square, [BK[4 + eh]], [MIX, ss],
                        accum=ss[:, eh:eh + 1])
                    yield
                tt("dve", ss[:, 2:3], ss[:, 0:1], ss[:, 1:2], ALU.add, [ss], [ss])
                rsqrt_small(rs[:, 0:1], ss[:, 2:3], 1.0 / D, [ss], [rs], rs[:, 1:2])
                yield
                for eh in range(2):
                    sl = slice(eh * 512, (eh + 1) * 512)
                    stt("dve", BK[4 + eh][:], BK[4 + eh][:], rs[:, 0:1], GPM[:, sl], ALU.mult, ALU.mult,
                        [BK[4 + eh], rs, GPM], [BK[4 + eh]])
                    yield
                    tt("dve", Xt[:, th, sl], Xt[:, th, sl], BK[4 + eh][:], ALU.add, [RX[th], BK[4 + eh]], [RX[th]])
                    yield

            def interleave(ga, gb, na, nb):
                a = b = True
                while a or b:
                    for _ in range(na):
                        if a:
                            try:
                                next(ga)
                            except StopIteration:
                                a = False
                    for _ in range(nb):
                        if b:
                            try:
                                next(gb)
                            except StopIteration:
                                b = False

            RB1 = [Res("B1s%d" % i) for i in range(4)]
            ORDER = [8, 9, 10, 11, 0, 1, 2, 3, 4, 5, 6, 7]

            def gen_Bconv():
                def b_mm(i):
                    ch = ORDER[i]
                    c0 = ch * 128
                    bk = BK[1 + i % 2]
                    for k in range(8):
                        mm(bk[:, 0:128], WIN[:, k, c0:c0 + 128], HT[:, k, :], k == 0, k == 7,
                           [HT] + rwin(c0, c0 + 128), [bk])

                def b_copy(i):
                    ch = ORDER[i]
                    pc = PCC[i % 3]
                    bk = BK[1 + i % 2]
                    cp("act", pc[:, 3:131], bk[:, 0:128], [bk], [pc])
                    cp("pool", pc[:, 0:3], TAIL[:, ch, :], [TAIL], [pc])

                def b_conv(i):
                    ch = ORDER[i]
                    pc, ac = PCC[i % 3], ACC[i % 3]
                    act(ac[:], pc[:, 0:128], AF.Copy, [pc, CW], [ac], scale=CW[:, 0, ch:ch + 1])
                    for j in range(1, 3):
                        stt("dve", ac[:], pc[:, j:j + 128], CW[:, j, ch:ch + 1], ac[:], ALU.mult, ALU.add,
                            [pc, CW, ac], [ac])
                    if ch < 8:
                        stt("dve", QKF[:, ch, :], pc[:, 3:131], CW[:, 3, ch:ch + 1], ac[:], ALU.mult, ALU.add,
                            [pc, CW, ac], [QKF])
                    else:
                        stt("dve", ac[:], pc[:, 3:131], CW[:, 3, ch:ch + 1], ac[:], ALU.mult, ALU.add,
                            [pc, CW, ac], [ac])
                    cp("pool", TAIL[:, ch, :], pc[:, 128:131], [pc], [TAIL])

                def b_silu(i):
                    ch = ORDER[i]
                    if ch >= 8:
                        act(GVT[:, ch - 8, :], ACC[i % 3][:], AF.Silu, [ACC[i % 3]], [GVT])

                for i in range(12 + 3):
                    if i < 12:
                        b_mm(i)
                        yield
                    if 0 <= i - 1 < 12:
                        b_copy(i - 1)
                        yield
                    if 0 <= i - 2 < 12:
                        b_conv(i - 2)
                        yield
                    if 0 <= i - 3 < 12:
                        b_silu(i - 3)
                        yield
                for hf in range(2):
                    act(QKF[:, hf * 4:(hf + 1) * 4, :], QKF[:, hf * 4:(hf + 1) * 4, :], AF.Silu, [QKF], [QKF])
                    yield

            def interleave3(gens, steps):
                alive = [True] * len(gens)
                while any(alive):
                    for gi, g in enumerate(gens):
                        for _ in range(steps[gi]):
                            if alive[gi]:
                                try:
                                    next(g)
                                except StopIteration:
                                    alive[gi] = False

            def stage_A(t):
                Xv = Xt[:, t, :]
                ss, rs = SA4[0], SA4[1]
                act(XN[:], Xv, AF.Square, [RX[t]], [XN, ss], accum=ss[:, 0:1])
                rsqrt_small(rs[:, 0:1], ss[:, 0:1], 1.0 / D, [ss], [rs], rs[:, 1:2])
                act(XN[:], Xv, AF.Copy, [RX[t], rs], [XN], scale=rs[:, 0:1])
                for k in range(8):
                    tr(TPB[:, k * 128:(k + 1) * 128], XN[:, k * 128:(k + 1) * 128], IDB[:], [XN, IDB], [TPB])
                tt("dve", HT[:], TPB[:].rearrange("p (k t) -> p k t", k=8), bc3(GPRE[:], 8, 128), ALU.mult,
                   [TPB, GPRE], [HT])

            P.marks.append(len(P.ops))
            for t in range(KNT):
                dma("sp", Xt[:, t, :], x_d[t * 128:(t + 1) * 128, :], [], [RX[t]])
            stage_A(0)
            for _ in gen_Bconv():
                pass

            for t in range(KNT):
                P.marks.append(len(P.ops))
                Xv = Xt[:, t, :]
                hgen = gen_H(t - 1) if t > 0 else None

                def hstep(n=2):
                    if hgen is not None:
                        for _ in range(n):
                            try:
                                next(hgen)
                            except StopIteration:
                                break

                def tok_proj(bk, c0, n):
                    for k in range(8):
                        mm(bk[:, 0:n], HT[:, k, :], WIN[:, k, c0:c0 + n], k == 0, k == 7,
                           [HT] + rwin(c0, c0 + n), [bk])
                hstep(2)
                tok_proj(BK[0], W_GZ, 512)
                hstep(1)
                act(TMP[0][:], BK[0][:], AF.Silu, [BK[0]], [TMP[0]])
                tt("pool", ZSG[:].rearrange("p (h e) -> p h e", h=4), TMP[0][:].rearrange("p (h e) -> p h e", h=4),
                   bcm(GNG[:], 4), ALU.mult, [TMP[0], GNG], [ZSG])
                hstep(1)
                tok_proj(BK[1], W_MO, 512)
                hstep(1)
                act(TMP[1][:], BK[1][:], AF.Sigmoid, [BK[1]], [TMP[1]])
                tt("pool", MOSG[:].rearrange("p (h e) -> p h e", h=4), TMP[1][:].rearrange("p (h e) -> p h e", h=4),
                   bcm(MNG[:], 4), ALU.mult, [TMP[1], MNG], [MOSG])
                for hh in range(8):
                    bk = BK[hh % 2]
                    c0 = W_MQK + hh * 64
                    for k in range(8):
                        mm(bk[0:64, 0:128], WIN[:, k, c0:c0 + 64], HT[:, k, :], k == 0, k == 7,
                           [HT] + rwin(c0, c0 + 64), [bk])
                    if hh < 4:
                        cp("act", MQT[:, hh, :], bk[0:64, 0:128], [bk], [MQT])
                    else:
                        act(MKT[:, hh - 4, :], bk[0:64, 0:128], AF.Copy, [bk], [MKT], scale=0.125)
                    hstep(1)
                hstep(20)
                tok_proj(BK[0], W_MV, 512)
                cp("act", MVt[:].rearrange("p h e -> p (h e)"), BK[0][:], [BK[0]], [MVt])
                tok_proj(BK[1], W_GATE, 16)
                cp("dve", GT[:], BK[1][:, 0:16], [BK[1]], [GT])
                tt("dve", ARG[:], GT[:, 0:12], SGN[:], ALU.mult, [GT, SGN], [ARG])
                tt("dve", ARG[:], ARG[:], BIA[:], ALU.add, [ARG, BIA], [ARG])
                for hf in range(2):
                    act(TMP[hf][:], QKF[:, hf * 4:(hf + 1) * 4, :].rearrange("p a b -> p (a b)"), AF.Square,
                        [QKF], [TMP[hf]])
                    mm(BK[2 + hf][:], ONES, TMP[hf][:], True, True, [CONST, TMP[hf]], [BK[2 + hf]])
                for hf in range(2):
                    act(TMP[hf][:], BK[2 + hf][:], AF.Ln, [BK[2 + hf], EPSB], [TMP[hf]], bias=EPSB[:, 0:1])
                    act(TMP[hf][:], TMP[hf][:], AF.Exp, [TMP[hf]], [TMP[hf]], scale=-0.5)
                stt("dve", GQT[:].rearrange("p a b -> p (a b)"), QKF[:, 0:4, :].rearrange("p a b -> p (a b)"),
                    128.0 ** -0.5, TMP[0][:], ALU.mult, ALU.mult, [QKF, TMP[0]], [GQT])
                tt("pool", GKT[:].rearrange("p a b -> p (a b)"), QKF[:, 4:8, :].rearrange("p a b -> p (a b)"),
                   TMP[1][:], ALU.mult, [QKF, TMP[1]], [GKT])
                act(ARG[:], ARG[:], AF.Exp, [ARG], [ARG])
                act(ARG[:], ARG[:], AF.Ln, [ARG], [ARG], bias=1.0)
                tt("dve", GL[:], ARG[:], NEGC[:], ALU.mult, [ARG, NEGC], [GL])
                act(BETA[:], GL[:, 0:4], AF.Exp, [GL], [BETA])
                tt("dve", IG[:], GT[:, 12:16], SMB[:, 8:12], ALU.add, [GT, SMB], [IG])

                if t + 1 < KNT:
                    stage_A(t + 1)

                mm(BK[2][:, 0:8], TRI, GL[:, 4:12], True, True, [CONST, GL], [BK[2]])
                cp("dve", GF[:], BK[2][:, 0:8], [BK[2]], [GF])
                cp("pool", COLS[:, 0:4], GF[:, 0:4], [GF], [COLS])
                tt("dve", COLS[:, 4:8], GF[:, 0:4], GL[:, 0:4], ALU.add, [GF, GL], [COLS])
                cp("pool", COLS[:, 8:12], GF[:, 4:8], [GF], [COLS])
                tt("dve", COLS[:, 12:16], IG[:], GF[:, 4:8], ALU.subtract, [IG, GF], [COLS])
                ts("dve", COLS[:, 16:20], GF[:, 0:4], -1.0, None, ALU.mult, None, [GF], [COLS])
                for j in range(4):
                    tr(BK[3][0:4, j * 128:(j + 1) * 128], COLS[:, 4 * j:4 * j + 4], IDF, [COLS, CONST], [BK[3]])
                tr(BK[4][0:4, 0:128], COLS[:, 16:20], IDF, [COLS, CONST], [BK[4]])
                cp("dve", RT[:, 0:4, :].rearrange("p a b -> p (a b)"), BK[3][0:4, :], [BK[3]], [RT])
                cp("dve", RT[:, 4, :], BK[4][0:4, 0:128], [BK[4]], [RT])
                SELR3 = SELR.rearrange("p (a b) -> p a b", a=4)
                tt("dve", BD[:], bcm(RT[:, 1, :], 4), SELR3, ALU.mult, [RT, CONST], [BD])
                mm(BK[2][:], ONES4, BD[:].rearrange("p a b -> p (a b)"), True, False, [CONST, BD], [BK[2]])
                mm(BK[2][:], RT[:, 4, :], SELR, False, True, [RT, CONST], [BK[2]])
                tt("dve", EXPA[:].rearrange("p (h c) -> p h c", h=4), BK[2][:].rearrange("p (h c) -> p h c", h=4),
                   bcm(MST, 4), ALU.add, [BK[2], CONST], [EXPA])
                act(EXPA[:], EXPA[:], AF.Exp, [EXPA], [EXPA])
                tt("dve", BD[:], bcm(RT[:, 0, :], 4), SELR3, ALU.mult, [RT, CONST], [BD])
                mm(BK[3][:], ONES4, BD[:].rearrange("p a b -> p (a b)"), True, False, [CONST, BD], [BK[3]])
                mm(BK[3][:], RT[:, 4, :], SELR, False, True, [RT, CONST], [BK[3]])
                tt("dve", EXPQ[:].rearrange("p (h c) -> p h c", h=4), BK[3][:].rearrange("p (h c) -> p h c", h=4),
                   bcm(MIT, 4), ALU.add, [BK[3], CONST], [EXPQ])
                act(EXPQ[:], EXPQ[:], AF.Exp, [EXPQ], [EXPQ])
                for h in range(4):
                    mm(BK[4][:, h * 128:(h + 1) * 128], GKT[:, h, :], GKT[:, h, :], True, True, [GKT], [BK[4]])
                for h in range(4):
                    mm(BK[5][:, h * 128:(h + 1) * 128], GKT[:, h, :], GQT[:, h, :], True, True, [GKT, GQT], [BK[5]])
                tt("dve", MB[0][:], BK[4][:], EXPA[:], ALU.mult, [BK[4], EXPA], [MB[0]])
                tt("dve", QKD[:].rearrange("p h c -> p (h c)"), BK[5][:], EXPQ[:], ALU.mult, [BK[5], EXPQ], [QKD])

                for h in range(4):
                    tr(TPB[:, h * 128:(h + 1) * 128], GKT[:, h, :], IDB[:], [GKT, IDB], [TPB])
                    tr(TPB[:, 512 + h * 128:512 + (h + 1) * 128], GVT[:, h, :], IDB[:], [GVT, IDB], [TPB])
                cp("act", GKt[:].rearrange("p h d -> p (h d)"), TPB[:, 0:512], [TPB], [GKt])
                EG, BEG, EGL, GEND = SC4[0], SC4[1], SC4[2], SC4[3]
                act(EG[:, 0:4], GF[:, 0:4], AF.Exp, [GF], [EG])
                tt("dve", BEG[:, 0:4], EG[:, 0:4], BETA[:], ALU.mult, [EG, BETA], [BEG])
                mm(BK[2][:, 0:8], SEL127, GF[:], True, True, [CONST, GF], [BK[2]])
                cp("dve", LASTG[:], BK[2][:, 0:8], [BK[2]], [LASTG])
                tt("dve", EGL[:, 0:4], LASTG[:, 0:4], GF[:, 0:4], ALU.subtract, [LASTG, GF], [EGL])
                act(EGL[:, 0:4], EGL[:, 0:4], AF.Exp, [EGL], [EGL])
                act(GEND[:, 0:4], LASTG[:, 0:4], AF.Exp, [LASTG], [GEND])
                tt("dve", XK[:], GKt[:], bc3(BEG[:, 0:4], 4, 128), ALU.mult, [GKt, BEG], [XK])
                tt("pool", KEND[:], GKt[:], bc3(EGL[:, 0:4], 4, 128), ALU.mult, [GKt, EGL], [KEND])
                tt("dve", GVb[:], TPB[:, 512:1024].rearrange("p (h e) -> p h e", h=4), bc3(BETA[:], 4, 128),
                   ALU.mult, [TPB, BETA], [GVb])
                def gen_EF():
                    for h in range(4):
                        tr(BK[5][:, h * 128:(h + 1) * 128], MB[0][:, h * 128:(h + 1) * 128], IDF, [MB[0], CONST], [BK[5]])
                    yield
                    cp("act", LB[0][:], BK[5][:], [BK[5]], [LB[0]])
                    yield
                    tt("dve", RR[:].rearrange("p (h c) -> p h c", h=4), bcm(IDF, 4),
                       MB[0][:].rearrange("p (h c) -> p h c", h=4), ALU.subtract, [CONST, MB[0]], [RR])
                    yield
                    NLEV = 6
                    yield
                    for k in range(NLEV):
                        a, b = k % 2, (k + 1) % 2
                        for h in range(4):
                            sl = slice(h * 128, (h + 1) * 128)
                            mm(BK[3][:, sl], MB[a][:, sl], LB[a][:, sl], True, True, [MB[a], LB[a]], [BK[3]])
                        yield
                        if k < NLEV - 1:
                            for h in range(4):
                                sl = slice(h * 128, (h + 1) * 128)
                                mm(BK[4][:, sl], LB[a][:, sl], MB[a][:, sl], True, True, [MB[a], LB[a]], [BK[4]])
                            yield
                        cp("act", LB[b][:], BK[3][:], [BK[3]], [LB[b]])
                        yield
                        if k < NLEV - 1:
                            cp("dve", MB[b][:], BK[4][:], [BK[4]], [MB[b]])
                            yield
                        for h in range(4):
                            sl = slice(h * 128, (h + 1) * 128)
                            mm(BK[5][:, sl], LB[b][:, sl], RR[:, sl], True, True, [LB[b], RR], [BK[5]])
                        yield
                        tt("dve", RR[:], RR[:], BK[5][:], ALU.add, [RR, BK[5]], [RR])
                        yield
                    cp("act", BINV[:].rearrange("p h c -> p (h c)"), RR[:], [RR], [BINV])
                    yield
                    for h in range(4):
                        sl = slice(h * 128, (h + 1) * 128)
                        mm(BK[3][:, sl], BINV[:, h, :], GVb[:, h, :], True, True, [BINV, GVb], [BK[3]])
                        mm(BK[4][:, sl], XK[:, h, :], BINV[:, h, :], True, True, [BINV, XK], [BK[4]])
                    yield
                    cp("act", WV[:], BK[3][:], [BK[3]], [WV])
                    yield
                    cp("dve", WKT[:].rearrange("p h c -> p (h c)"), BK[4][:], [BK[4]], [WKT])
                    yield
                    for h in range(4):
                        sl = slice(h * 128, (h + 1) * 128)
                        mm(BK[3][:, sl], WKT[:, h, :], SBh[:, h, :], True, True, [WKT, SBh], [BK[3]])
                        mm(BK[5][:, sl], GQT[:, h, :], SBh[:, h, :], True, True, [GQT, SBh], [BK[5]])
                    yield
                    tt("dve", U[:].rearrange("p h e -> p (h e)"), WV[:], BK[3][:], ALU.subtract, [WV, BK[3]], [U])
                    yield
                    for h in range(4):
                        sl = slice(h * 128, (h + 1) * 128)
                        mm(BK[4][:, sl], QKD[:, h, :], U[:, h, :], True, True, [QKD, U], [BK[4]])
                        mm(BK[3][:, sl], KEND[:, h, :], U[:, h, :], True, True, [KEND, U], [BK[3]])
                    yield
                    OG = MB[1]
                    yield
                    tt("dve", OG[:].rearrange("p (h e) -> p h e", h=4), BK[5][:].rearrange("p (h e) -> p h e", h=4),
                       bc3(EG[:, 0:4], 4, 128), ALU.mult, [BK[5], EG], [OG])
                    yield
                    tt("dve", OG[:], OG[:], BK[4][:], ALU.add, [OG, BK[4]], [OG])
                    yield
                    for h in range(4):
                        stt("dve", S32[:, h, :], S32[:, h, :], GEND[:, h:h + 1], BK[3][:, h * 128:(h + 1) * 128],
                            ALU.mult, ALU.add, [S32, GEND, BK[3]], [S32])
                    yield
                    cp("act", SBh[:], S32[:], [S32], [SBh])
                    yield
                    headnorm_gate(OG, ZSG, MIX[:, 0:512], LB[0], SC4[4], SC4[5])
                    yield
                def gen_G():
                    tt("dve", BD[:], bcm(RT[:, 3, :], 4), SELR3, ALU.mult, [RT, CONST], [BD])
                    yield
                    mm(BK[6][:], ONES4, BD[:].rearrange("p a b -> p (a b)"), True, False, [CONST, BD], [BK[6]])
                    yield
                    mm(BK[6][:], RT[:, 2, :], SELR, False, True, [RT, CONST], [BK[6]])
                    yield
                    PD = TMP[0]
                    yield
                    tt("dve", PD[:].rearrange("p (h s) -> p h s", h=4), BK[6][:].rearrange("p (h s) -> p h s", h=4),
                       bcm(MI, 4), ALU.add, [BK[6], CONST], [PD])
                    yield
                    red("dve", DMAX[:], PD[:].rearrange("p (h s) -> p h s", h=4), ALU.max, [PD], [DMAX])
                    yield
                    for h in range(4):
                        mm(BK[0][:, h * 128:(h + 1) * 128], MQT[:, h, :], MKT[:, h, :], True, True, [MQT, MKT], [BK[0]])
                    yield
                    for h in range(4):
                        tr(TPB[:, h * 64:(h + 1) * 64], MKT[:, h, :], IDB[0:64, 0:64], [MKT, IDB], [TPB])
                    yield
                    cp("act", MKt[:].rearrange("p h d -> p (h d)"), TPB[:, 0:256], [TPB], [MKt])
                    yield
                    Bv, MT, WP, EMT = SG4[0], SG4[1], SG4[2], SG4[3]
                    yield
                    tt("dve", Bv[:, 0:4], GF[:, 4:8], MBC[:], ALU.add, [GF, MBC], [Bv])
                    yield
                    tt("dve", MT[:, 0:4], Bv[:, 0:4], DMAX[:], ALU.max, [Bv, DMAX], [MT])
                    yield
                    tt("dve", WP[:, 0:4], Bv[:, 0:4], MT[:, 0:4], ALU.subtract, [Bv, MT], [WP])
                    yield
                    act(WP[:, 0:4], WP[:, 0:4], AF.Exp, [WP], [WP])
                    yield
                    tt("dve", PD[:].rearrange("p (h s) -> p h s", h=4), PD[:].rearrange("p (h s) -> p h s", h=4),
                       bc3(MT[:, 0:4], 4, 128), ALU.subtract, [PD, MT], [PD])
                    yield
                    act(PD[:], PD[:], AF.Exp, [PD], [PD])
                    yield
                    tt("dve", PD[:], PD[:], BK[0][:], ALU.mult, [PD, BK[0]], [PD])
                    yield
                    RS = SG4[4]
                    yield
                    red("dve", RS[:, 0:4], PD[:].rearrange("p (h s) -> p h s", h=4), ALU.add, [PD], [RS])
                    yield
                    cp("act", PQKb[:].rearrange("p h s -> p (h s)"), PD[:], [PD], [PQKb])
                    yield
                    for h in range(4):
                        tr(TPB[:, 512 + h * 128:512 + (h + 1) * 128], PQKb[:, h, :], IDB[:], [PQKb, IDB], [TPB])
                    yield
                    cp("act", PQKT[:].rearrange("p h s -> p (h s)"), TPB[:, 512:1024], [TPB], [PQKT])
                    yield
                    for h in range(4):
                        sl = slice(h * 128, (h + 1) * 128)
                        mm(BK[0][:, sl], MQT[:, h, :], CBh[:, h, :], True, True, [MQT, CBh], [BK[0]])
                        mm(BK[6][:, 2 * h:2 * h + 2], MQT[:, h, :], NBh[:, h, :], True, True, [MQT, NBh], [BK[6]])
                    yield
                    NUM = TMP[0]
                    yield
                    tt("dve", NUM[:].rearrange("p (h e) -> p h e", h=4), BK[0][:].rearrange("p (h e) -> p h e", h=4),
                       bc3(WP[:, 0:4], 4, 128), ALU.mult, [BK[0], WP], [NUM])
                    yield
                    for h in range(4):
                        sl = slice(h * 128, (h + 1) * 128)
                        mm(BK[0][:, sl], PQKT[:, h, :], MVt[:, h, :], True, True, [PQKT, MVt], [BK[0]])
                    yield
                    tt("dve", NUM[:], NUM[:], BK[0][:], ALU.add, [NUM, BK[0]], [NUM])
                    yield
                    DEN = SG4[5]
                    yield
                    tt("dve", DEN[:, 0:4], BK[6][:, 0:8].rearrange("p (h two) -> p h two", two=2)[:, :, 0], WP[:, 0:4], ALU.mult, [BK[6], WP], [DEN])
                    yield
                    tt("dve", DEN[:, 0:4], DEN[:, 0:4], RS[:, 0:4], ALU.add, [DEN, RS], [DEN])
                    yield
                    act(EMT[:, 0:4], MT[:, 0:4], AF.Exp, [MT], [EMT], scale=-1.0)
                    yield
                    ts("dve", DEN[:, 4:8], DEN[:, 0:4], -1.0, None, ALU.mult, None, [DEN], [DEN])
                    yield
                    tt("dve", DEN[:, 0:4], DEN[:, 0:4], DEN[:, 4:8], ALU.max, [DEN], [DEN])
                    yield
                    tt("dve", DEN[:, 0:4], DEN[:, 0:4], EMT[:, 0:4], ALU.max, [DEN, EMT], [DEN])
                    yield
                    P.op("dve", lambda e, d=DEN: e.reciprocal(out=d[:, 0:4], in_=d[:, 0:4]), [DEN], [DEN])
                    yield
                    tt("dve", NUM[:].rearrange("p (h e) -> p h e", h=4), NUM[:].rearrange("p (h e) -> p h e", h=4),
                       bc3(DEN[:, 0:4], 4, 128), ALU.mult, [NUM, DEN], [NUM])
                    yield
                    cp("pool", T12[:, 0:4], MT[:, 0:4], [MT], [T12])
                    yield
                    tt("dve", T12[:, 4:8], GF[:, 4:8], MT[:, 0:4], ALU.subtract, [GF, MT], [T12])
                    yield
                    cp("pool", T12[:, 8:12], WP[:, 0:4], [WP], [T12])
                    yield
                    mm(BK[6][:, 16:28], SEL127, T12[:], True, True, [CONST, T12], [BK[6]])
                    yield
                    cp("dve", MBC[:], BK[6][:, 16:20], [BK[6]], [MBC])
                    yield
                    PEND = SG4[4]
                    yield
                    tt("dve", PEND[:, 4:8], COLS[:, 12:16], BK[6][:, 20:24], ALU.add, [COLS, BK[6]], [PEND])
                    yield
                    act(PEND[:, 4:8], PEND[:, 4:8], AF.Exp, [PEND], [PEND])
                    yield
                    cp("dve", WLC[:], BK[6][0:64, 24:28], [BK[6]], [WLC])
                    yield
                    tt("dve", PK[:], MKt[:], bc3(PEND[:, 4:8], 4, 64), ALU.mult, [MKt, PEND], [PK])
                    yield
                    for h in range(4):
                        mm(BK[0][0:64, h * 128:(h + 1) * 128], PK[:, h, :], MVt[:, h, :], True, True, [PK, MVt], [BK[0]])
                    yield
                    for h in range(4):
                        mm(BK[6][0:64, 32 + 2 * h:34 + 2 * h], PK[:, h, :], ONEB[:], True, True, [PK, ONEB], [BK[6]])
                    yield
                    for h in range(4):
                        stt("dve", C32[:, h, :], C32[:, h, :], WLC[:, h:h + 1], BK[0][0:64, h * 128:(h + 1) * 128],
                            ALU.mult, ALU.add, [C32, WLC, BK[0]], [C32])
                    yield
                    tt("dve", N32[:], N32[:], WLC[:], ALU.mult, [N32, WLC], [N32])
                    yield
                    tt("dve", N32[:], N32[:], BK[6][0:64, 32:40].rearrange("p (a two) -> p a two", two=2)[:, :, 0],
                       ALU.add, [N32, BK[6]], [N32])
                    yield
                    cp("act", CBh[:], C32[:], [C32], [CBh])
                    yield
                    cp("act", NBh[:, :, 0], N32[:], [N32], [NBh])
                    yield

                if t + 1 < KNT:
                    interleave3([gen_EF(), gen_G(), gen_Bconv()], KRATIO)
                else:
                    interleave3([gen_EF(), gen_G()], [2, 1])
                headnorm_gate(TMP[0], MOSG, MIX[:, 512:1024], TMP[1], SG4[4], SG4[5])

            for _ in gen_H(KNT - 1):
                pass

            dma("sp", pS_d.rearrange("h d e -> d h e"), S32[:], [S32], [])
            dma("sp", pC_d.rearrange("h d e -> d h e"), C32[:], [C32], [])
            dma("sp", pn_d.rearrange("h d -> d h"), N32[:], [N32], [], slow=True)
            dma("sp", pm_d, MBC[0:1, :], [MBC], [])
            for ch in range(12):
                tr(BK[2 + ch // 4][0:3, (ch % 4) * 128:(ch % 4 + 1) * 128], TAIL[:, ch, :], IDF, [TAIL, CONST],
                   [BK[2 + ch // 4]])
            for g in range(3):
                cp("dve", TMP[g % 2][0:3, :], BK[2 + g][0:3, :], [BK[2 + g]], [TMP[g % 2]])
                dma("sp", pconv_d[:, g * 512:(g + 1) * 512], TMP[g % 2][0:3, :], [TMP[g % 2]], [])


            P.barrier(all_res_p1 + RX + [XS1T.r, HNS.r, IDF2.r, IDB.r, GPREMLP.r])
            p1a.close()
            p1b = ExitStack()
            cur[0] = p1b
            R_ = NS
            GRP = 1
            XS = s1("XS", [R_, D])
            XNs = s1("XNs", [R_, D], BF16)
            HTs = s1("HTs", [128, 8, R_], BF16)
            SCV = s1("SCV", [R_, 512])
            XPT = s1("XPT", [128, 12, 4, R_])
            ACs = s1("ACs", [128, 12, R_])
            TM12 = s1("TM12", [128, 12, R_])
            SQs = s1("SQs", [128, 8, R_])
            GQs = s1("GQs", [128, 4, R_])
            GKs = s1("GKs", [128, 4, R_])
            GVs = s1("GVs", [128, 4, R_])
            MQs = s1("MQs", [64, 4, R_])
            MKs = s1("MKs", [64, 4, R_])
            ZSGs = s1("ZSGs", [R_, 512])
            MOSGs = s1("MOSGs", [R_, 512])
            MVs = s1("MVs", [R_, 4, 128])
            GTs = s1("GTs", [R_, 16])
            ARGs = s1("ARGs", [R_, 12])
            GLs = s1("GLs", [R_, 12])
            BETAs = s1("BETAs", [R_, 4])
            IGs = s1("IGs", [R_, 4])
            EGs = s1("EGs", [R_, 4])
            Qt = s1("Qt", [R_, 4, 128])
            Kt = s1("Kt", [R_, 4, 128])
            Vt = s1("Vt", [R_, 4, 128])
            MQt = s1("MQt", [R_, 4, 64])
            MKtt = s1("MKtt", [R_, 4, 64])
            PKt = s1("PKt", [R_, 4, 64])
            DIAGI = s1("DIAGI", [R_, 16, 16])
            M16 = s1("M16", [128, 16, 16])
            KD = [s1("KD%d" % i, [128, 4, 16]) for i in range(2)]
            QD = [s1("QD%d" % i, [128, 4, 16]) for i in range(2)]
            QDm = [s1("QDm%d" % i, [64, 4, 16]) for i in range(2)]
            DG4 = s1("DG4", [R_, 16, 4])
            EGBC = s1("EGBC", [128, 64])
            WPBC = s1("WPBC", [128, 64])
            S0b = [s1("S0b%d" % i, [128, 4, 128]) for i in range(2)]
            C0b = [s1("C0b%d" % i, [64, 4, 128]) for i in range(2)]
            Ug = s1("Ug", [R_, 4, 128])
            KROW = s1("KROW", [R_, 512])
            PKROW = View(DIAGI[:].rearrange("p a b -> p (a b)"), DIAGI.r)
            N0 = s1("N0", [R_, 4, 64])
            SMs = [s1("SMs%d" % i, [R_, 8]) for i in range(10)]
            T1 = s1("T1s", [R_, 512])
            T2 = KROW

            def hn_gate16(src, gate, dst, wres):
                ss, rs = SMs[8], SMs[9]
                tt("pool", T2[:], src[:], src[:], ALU.mult, [src], [T2])
                red("dve", ss[:, 0:4], T2[:].rearrange("p (h e) -> p h e", h=4), ALU.add, [T2], [ss])
                rsqrt_small(rs[:, 0:4], ss[:, 0:4], 1.0 / 128, [ss], [rs], rs[:, 4:8])
                tt("dve", T2[:].rearrange("p (h e) -> p h e", h=4), src[:].rearrange("p (h e) -> p h e", h=4),
                   bc3(rs[:, 0:4], 4, 128), ALU.mult, [src, rs], [T2])
                tt("dve", dst, T2[:], gate[:], ALU.mult, [T2, gate], [wres])

            IDF16 = CONST[0:R_, C_IDF:C_IDF + R_]
            ONES16 = CONST[0:R_, C_ONES:C_ONES + 128]

            dma("sp", XS[:], xs_d, [], [XS])
            dma("sp", N0[:].rearrange("p h d -> p (h d)"), sn_d, [], [N0])
            dma("sp", SMs[0][:, 0:4], sm_d, [], [SMs[0]])
            dma("sp", oconv_d[:, 0:2, :], sconv_d[:, 1:3, :], [], [])
            ss, rs = SMs[8], SMs[9]
            act(XNs[:], XS[:], AF.Square, [XS], [XNs, ss], accum=ss[:, 0:1])
            rsqrt_small(rs[:, 0:1], ss[:, 0:1], 1.0 / D, [ss], [rs], rs[:, 1:2])
            ts("dve", XNs[:], XS[:], rs[:, 0:1], None, ALU.mult, None, [XS, rs], [XNs])
            for k in range(8):
                tr(TPB[:, k * 128:k * 128 + R_], XNs[:, k * 128:(k + 1) * 128], IDB[0:R_, 0:R_], [XNs, IDB], [TPB])
            tt("dve", HTs[:], TPB[:].rearrange("p (k t) -> p k t", k=8)[:, :, 0:R_], bc3(GPRE[:], 8, R_), ALU.mult,
               [TPB, GPRE], [HTs])

            for ch in range(12):
                bk = BK[ch % 2]
                c0 = ch * 128
                for k in range(8):
                    mm(bk[:, 0:R_], WIN[:, k, c0:c0 + 128], HTs[:, k, :], k == 0, k == 7, [HTs] + rwin(c0, c0 + 128), [bk])
                cp("act", XPT[:, ch, 3, :], bk[:, 0:R_], [bk], [XPT])
            for hh in range(8):
                bk = BK[hh % 2]
                c0 = W_MQK + hh * 64
                for k in range(8):
                    mm(bk[0:64, 0:R_], WIN[:, k, c0:c0 + 64], HTs[:, k, :], k == 0, k == 7, [HTs] + rwin(c0, c0 + 64), [bk])
                if hh < 4:
                    cp("act", MQs[:, hh, :], bk[0:64, 0:R_], [bk], [MQs])
                else:
                    act(MKs[:, hh - 4, :], bk[0:64, 0:R_], AF.Copy, [bk], [MKs], scale=0.125)
            for g in range(3):
                for j in range(3):
                    dma("sp", SCV[:], sconv_d[:, j, g * 512:(g + 1) * 512], [], [SCV])
                    for c4 in range(4):
                        o0 = (j * 4 + c4) * R_
                        tr(BK[2][:, o0:o0 + R_], SCV[:, c4 * 128:(c4 + 1) * 128], IDF16, [SCV, CONST], [BK[2]])
                cp("dve", XPT[:, 4 * g:4 * g + 4, 0:3, :],
                   BK[2][:, 0:12 * R_].rearrange("p (j c r) -> p c j r", j=3, c=4), [BK[2]], [XPT])
            for g in range(3):
                bk = BK[3 + g % 2]
                for k in range(8):
                    mm(bk[0:R_, :], HTs[:, k, :], WIN[:, k, g * 512:(g + 1) * 512], k == 0, k == 7,
                       [HTs] + rwin(g * 512, (g + 1) * 512), [bk])
                cp("act", SCV[:], bk[0:R_, :], [bk], [SCV])
                dma("sp", oconv_d[:, 2, g * 512:(g + 1) * 512], SCV[:], [SCV], [])
            tt("dve", ACs[:], XPT[:, :, 0, :], bc3(CW[:, 0, :], 12, R_), ALU.mult, [XPT, CW], [ACs])
            for j in range(1, 4):
                tt("dve", TM12[:], XPT[:, :, j, :], bc3(CW[:, j, :], 12, R_), ALU.mult, [XPT, CW], [TM12])
                tt("dve", ACs[:], ACs[:], TM12[:], ALU.add, [ACs, TM12], [ACs])
            QKFs = TM12
            act(QKFs[:, 0:8, :], ACs[:, 0:8, :], AF.Silu, [ACs], [QKFs])
            act(GVs[:], ACs[:, 8:12, :], AF.Silu, [ACs], [GVs])
            act(SQs[:], QKFs[:, 0:8, :], AF.Square, [QKFs], [SQs])
            mm(BK[2][:, 0:8 * R_], ONES, SQs[:].rearrange("p a b -> p (a b)"), True, True, [CONST, SQs], [BK[2]])
            ts("dve", SQs[:].rearrange("p a b -> p (a b)"), BK[2][:, 0:8 * R_], 1.0, EPS, ALU.mult, ALU.add, [BK[2]], [SQs])
            act(SQs[:], SQs[:], AF.Ln, [SQs], [SQs])
            act(SQs[:], SQs[:], AF.Exp, [SQs], [SQs], scale=-0.5)
            stt("dve", GQs[:], QKFs[:, 0:4, :], 128.0 ** -0.5, SQs[:, 0:4, :], ALU.mult, ALU.mult, [QKFs, SQs], [GQs])
            tt("dve", GKs[:], QKFs[:, 4:8, :], SQs[:, 4:8, :], ALU.mult, [QKFs, SQs], [GKs])

            def tok16(bk, c0, n):
                for k in range(8):
                    mm(bk[0:R_, 0:n], HTs[:, k, :], WIN[:, k, c0:c0 + n], k == 0, k == 7, [HTs] + rwin(c0, c0 + n), [bk])
            tok16(BK[0], W_GZ, 512)
            act(T1[:], BK[0][0:R_, :], AF.Silu, [BK[0]], [T1])
            tt("dve", ZSGs[:].rearrange("p (h e) -> p h e", h=4), T1[:].rearrange("p (h e) -> p h e", h=4),
               bcm(GNG[0:R_, :], 4), ALU.mult, [T1, GNG], [ZSGs])
            tok16(BK[1], W_MV, 512)
            cp("act", MVs[:].rearrange("p h e -> p (h e)"), BK[1][0:R_, :], [BK[1]], [MVs])
            tok16(BK[0], W_MO, 512)
            act(T1[:], BK[0][0:R_, :], AF.Exp, [BK[0]], [T1], scale=-1.0)
            ts("dve", T1[:], T1[:], 1.0, None, ALU.add, None, [T1], [T1])
            P.op("dve", lambda e: e.reciprocal(out=T1[:], in_=T1[:]), [T1], [T1])
            tt("dve", MOSGs[:].rearrange("p (h e) -> p h e", h=4), T1[:].rearrange("p (h e) -> p h e", h=4),
               bcm(MNG[0:R_, :], 4), ALU.mult, [T1, MNG], [MOSGs])
            tok16(BK[1], W_GATE, 16)
            cp("dve", GTs[:], BK[1][0:R_, 0:16], [BK[1]], [GTs])
            tt("dve", ARGs[:], GTs[:, 0:12], SGN[0:R_, :], ALU.mult, [GTs, SGN], [ARGs])
            tt("dve", ARGs[:], ARGs[:], BIA[0:R_, :], ALU.add, [ARGs, BIA], [ARGs])
            act(ARGs[:], ARGs[:], AF.Exp, [ARGs], [ARGs])
            act(ARGs[:], ARGs[:], AF.Ln, [ARGs], [ARGs], bias=1.0)
            tt("dve", GLs[:], ARGs[:], NEGC[0:R_, :], ALU.mult, [ARGs, NEGC], [GLs])
            act(BETAs[:], GLs[:, 0:4], AF.Exp, [GLs], [BETAs])
            tt("dve", IGs[:], GTs[:, 12:16], SMB[0:R_, 8:12], ALU.add, [GTs, SMB], [IGs])
            act(EGs[:], GLs[:, 4:8], AF.Exp, [GLs], [EGs])
            M0s, Bs, MTs, WPs, Ps, QKG, QKMs, QNs = SMs[0], SMs[1], SMs[2], SMs[3], SMs[4], SMs[5], SMs[6], SMs[7]
            tt("dve", Bs[:, 0:4], GLs[:, 8:12], M0s[:, 0:4], ALU.add, [GLs, M0s], [Bs])
            tt("dve", MTs[:, 0:4], Bs[:, 0:4], IGs[:], ALU.max, [Bs, IGs], [MTs])
            tt("dve", WPs[:, 0:4], Bs[:, 0:4], MTs[:, 0:4], ALU.subtract, [Bs, MTs], [WPs])
            act(WPs[:, 0:4], WPs[:, 0:4], AF.Exp, [WPs], [WPs])
            tt("dve", Ps[:, 0:4], IGs[:], MTs[:, 0:4], ALU.subtract, [IGs, MTs], [Ps])
            act(Ps[:, 0:4], Ps[:, 0:4], AF.Exp, [Ps], [Ps])
            dma("sp", om_d, MTs[:, 0:4], [MTs], [])

            for h in range(4):
                tr(BK[2][0:R_, h * 128:(h + 1) * 128], GQs[:, h, :], IDF, [GQs, CONST], [BK[2]])
                tr(BK[3][0:R_, h * 128:(h + 1) * 128], GKs[:, h, :], IDF, [GKs, CONST], [BK[3]])
                tr(BK[4][0:R_, h * 128:(h + 1) * 128], GVs[:, h, :], IDF, [GVs, CONST], [BK[4]])
                tr(BK[5][0:R_, h * 64:(h + 1) * 64], MQs[:, h, :], CONST[0:64, C_IDF:C_IDF + 64], [MQs, CONST], [BK[5]])
                tr(BK[5][0:R_, 256 + h * 64:256 + (h + 1) * 64], MKs[:, h, :], CONST[0:64, C_IDF:C_IDF + 64],
                   [MKs, CONST], [BK[5]])
            cp("act", Qt[:].rearrange("p h d -> p (h d)"), BK[2][0:R_, :], [BK[2]], [Qt])
            cp("dve", Kt[:].rearrange("p h d -> p (h d)"), BK[3][0:R_, :], [BK[3]], [Kt])
            cp("act", Vt[:].rearrange("p h d -> p (h d)"), BK[4][0:R_, :], [BK[4]], [Vt])
            cp("dve", MQt[:].rearrange("p h d -> p (h d)"), BK[5][0:R_, 0:256], [BK[5]], [MQt])
            cp("act", MKtt[:].rearrange("p h d -> p (h d)"), BK[5][0:R_, 256:512], [BK[5]], [MKtt])
            tt("dve", T1[:], Qt[:].rearrange("p h d -> p (h d)"), Kt[:].rearrange("p h d -> p (h d)"), ALU.mult,
               [Qt, Kt], [T1])
            red("dve", QKG[:, 0:4], T1[:].rearrange("p (h d) -> p h d", h=4), ALU.add, [T1], [QKG])
            tt("dve", T1[:, 0:256], MQt[:].rearrange("p h d -> p (h d)"), MKtt[:].rearrange("p h d -> p (h d)"),
               ALU.mult, [MQt, MKtt], [T1])
            red("dve", QKMs[:, 0:4], T1[:, 0:256].rearrange("p (h d) -> p h d", h=4), ALU.add, [T1], [QKMs])
            tt("dve", T1[:, 0:256], MQt[:].rearrange("p h d -> p (h d)"), N0[:].rearrange("p h d -> p (h d)"),
               ALU.mult, [MQt, N0], [T1])
            red("dve", QNs[:, 0:4], T1[:, 0:256].rearrange("p (h d) -> p h d", h=4), ALU.add, [T1], [QNs])
            tt("dve", PKt[:], MKtt[:], bc3(Ps[:, 0:4], 4, 64), ALU.mult, [MKtt, Ps], [PKt])
            tt("dve", N0[:], N0[:], bc3(WPs[:, 0:4], 4, 64), ALU.mult, [N0, WPs], [N0])
            tt("dve", N0[:], N0[:], PKt[:], ALU.add, [N0, PKt], [N0])
            dma("sp", on_d, N0[:].rearrange("p h d -> p (h d)"), [N0], [])

            tt("dve", DIAGI[:], bc3(IDF16, R_, R_), bcm(IDF16, R_), ALU.mult, [CONST], [DIAGI])
            mm(BK[2][:, 0:R_ * R_], ONES16, DIAGI[:].rearrange("p a b -> p (a b)"), True, True, [CONST, DIAGI], [BK[2]])
            cp("dve", M16[:].rearrange("p a b -> p (a b)"), BK[2][:, 0:R_ * R_], [BK[2]], [M16])
            tt("dve", DG4[:], bcm(EGs[:], R_), bc3(IDF16, R_, 4), ALU.mult, [EGs, CONST], [DG4])
            mm(BK[3][:, 0:64], ONES16, DG4[:].rearrange("p a b -> p (a b)"), True, True, [CONST, DG4], [BK[3]])
            cp("dve", EGBC[:], BK[3][:, 0:64], [BK[3]], [EGBC])
            tt("dve", DG4[:], bcm(WPs[:, 0:4], R_), bc3(IDF16, R_, 4), ALU.mult, [WPs, CONST], [DG4])
            mm(BK[3][:, 0:64], ONES16, DG4[:].rearrange("p a b -> p (a b)"), True, True, [CONST, DG4], [BK[3]])
            cp("dve", WPBC[:], BK[3][:, 0:64], [BK[3]], [WPBC])

            def ld(r, i):
                dma("sp", S0b[i][:], sS_d[r].rearrange("h d e -> d h e"), [], [S0b[i]])
                dma("act", C0b[i][:], sC_d[r].rearrange("h d e -> d h e"), [], [C0b[i]])
            ld(0, 0)
            for r in range(R_):
                i = r % 2
                if r + 1 < R_:
                    ld(r + 1, 1 - i)
                tt("dve", KD[i][:], GKs[:], bcm(M16[:, r, :], 4), ALU.mult, [GKs, M16], [KD[i]])
                tt("pool", QD[i][:], GQs[:], bcm(M16[:, r, :], 4), ALU.mult, [GQs, M16], [QD[i]])
                tt("dve", QDm[i][:], MQs[:], bcm(M16[0:64, r, :], 4), ALU.mult, [MQs, M16], [QDm[i]])
                for h in range(4):
                    sl = slice(h * 128, (h + 1) * 128)
                    st_, sp_ = (r == 0 and h == 0), (r == R_ - 1 and h == 3)
                    mm(BK[2][0:R_, sl], KD[i][:, h, :], S0b[i][:, h, :], st_, sp_, [KD[i], S0b[i]], [BK[2]], skip=True)
                    mm(BK[3][0:R_, sl], QD[i][:, h, :], S0b[i][:, h, :], st_, sp_, [QD[i], S0b[i]], [BK[3]], skip=True)
                    mm(BK[5][0:R_, sl], QDm[i][:, h, :], C0b[i][:, h, :], st_, sp_, [QDm[i], C0b[i]], [BK[5]], skip=True)
            KSp, QSp, QCp = BK[2], BK[3], BK[5]
            tt("dve", Ug[:], KSp[0:R_, :].rearrange("p (h e) -> p h e", h=4), bc3(EGs[:], 4, 128), ALU.mult,
               [KSp, EGs], [Ug])
            tt("dve", Ug[:], Vt[:], Ug[:], ALU.subtract, [Vt, Ug], [Ug])
            tt("dve", Ug[:], Ug[:], bc3(BETAs[:], 4, 128), ALU.mult, [Ug, BETAs], [Ug])
            ld(0, 0)
            for r in range(R_):
                i = r % 2
                if r + 1 < R_:
                    ld(r + 1, 1 - i)
                ts("dve", KROW[:], Kt[:].rearrange("p h d -> p (h d)"), IDF16[:, r:r + 1], None, ALU.mult, None,
                   [Kt, CONST], [KROW])
                ts("dve", PKROW[:], PKt[:].rearrange("p h d -> p (h d)"), IDF16[:, r:r + 1], None, ALU.mult, None,
                   [PKt, CONST], [PKROW])
                for h in range(4):
                    mm(BK[4][:, h * 128:(h + 1) * 128], KROW[:, h * 128:(h + 1) * 128], Ug[:, h, :], True, True,
                       [KROW, Ug], [BK[4]])
                for h in range(4):
                    mm(BK[6][0:64, h * 128:(h + 1) * 128], PKROW[:, h * 64:(h + 1) * 64], MVs[:, h, :], True, True,
                       [PKROW, MVs], [BK[6]])
                for h in range(4):
                    stt("dve", S0b[i][:, h, :], S0b[i][:, h, :], EGBC[:, r * 4 + h:r * 4 + h + 1],
                        BK[4][:, h * 128:(h + 1) * 128], ALU.mult, ALU.add, [S0b[i], EGBC, BK[4]], [S0b[i]])
                    stt("dve", C0b[i][:, h, :], C0b[i][:, h, :], WPBC[0:64, r * 4 + h:r * 4 + h + 1],
                        BK[6][0:64, h * 128:(h + 1) * 128], ALU.mult, ALU.add, [C0b[i], WPBC, BK[6]], [C0b[i]])
                dma("sp", oS_d[r].rearrange("h d e -> d h e"), S0b[i][:], [S0b[i]], [])
                dma("act", oC_d[r].rearrange("h d e -> d h e"), C0b[i][:], [C0b[i]], [])

            MIXs = XNs
            tt("dve", T1[:].rearrange("p (h e) -> p h e", h=4), QSp[0:R_, :].rearrange("p (h e) -> p h e", h=4),
               bc3(EGs[:], 4, 128), ALU.mult, [QSp, EGs], [T1])
            tt("dve", Ug[:], Ug[:], bc3(QKG[:, 0:4], 4, 128), ALU.mult, [Ug, QKG], [Ug])
            tt("dve", T1[:], T1[:], Ug[:].rearrange("p h e -> p (h e)"), ALU.add, [T1, Ug], [T1])
            hn_gate16(T1, ZSGs, MIXs[:, 0:512], MIXs)
            PQ = SMs[5]
            tt("dve", PQ[:, 4:8], Ps[:, 0:4], QKMs[:, 0:4], ALU.mult, [Ps, QKMs], [PQ])
            tt("dve", T1[:].rearrange("p (h e) -> p h e", h=4), QCp[0:R_, :].rearrange("p (h e) -> p h e", h=4),
               bc3(WPs[:, 0:4], 4, 128), ALU.mult, [QCp, WPs], [T1])
            tt("dve", Ug[:], MVs[:], bc3(PQ[:, 4:8], 4, 128), ALU.mult, [MVs, PQ], [Ug])
            tt("dve", T1[:], T1[:], Ug[:].rearrange("p h e -> p (h e)"), ALU.add, [T1, Ug], [T1])
            DENs, EMTs = SMs[6], SMs[7]
            tt("dve", DENs[:, 4:8], WPs[:, 0:4], QNs[:, 0:4], ALU.mult, [WPs, QNs], [DENs])
            tt("dve", DENs[:, 4:8], DENs[:, 4:8], PQ[:, 4:8], ALU.add, [DENs, PQ], [DENs])
            ts("dve", DENs[:, 0:4], DENs[:, 4:8], -1.0, None, ALU.mult, None, [DENs], [DENs])
            tt("dve", DENs[:, 4:8], DENs[:, 4:8], DENs[:, 0:4], ALU.max, [DENs], [DENs])
            act(EMTs[:, 4:8], MTs[:, 0:4], AF.Exp, [MTs], [EMTs], scale=-1.0)
            tt("dve", DENs[:, 4:8], DENs[:, 4:8], EMTs[:, 4:8], ALU.max, [DENs, EMTs], [DENs])
            P.op("dve", lambda e, d=DENs: e.reciprocal(out=d[:, 4:8], in_=d[:, 4:8]), [DENs], [DENs])
            tt("dve", T1[:].rearrange("p (h e) -> p h e", h=4), T1[:].rearrange("p (h e) -> p h e", h=4),
               bc3(DENs[:, 4:8], 4, 128), ALU.mult, [T1, DENs], [T1])
            hn_gate16(T1, MOSGs, MIXs[:, 512:1024], MIXs)
            for k in range(8):
                tr(TPB[:, k * 128:k * 128 + R_], MIXs[:, k * 128:(k + 1) * 128], IDB[0:R_, 0:R_], [MIXs, IDB], [TPB])
            cp("act", HTs[:], TPB[:].rearrange("p (k t) -> p k t", k=8)[:, :, 0:R_], [TPB], [HTs])
            for eh in range(2):
                for k in range(8):
                    mm(BK[eh][0:R_, :], HTs[:, k, :], WOUT[:, k, eh * 512:(eh + 1) * 512], k == 0, k == 7,
                       [HTs, WOUT], [BK[eh]])
            ss, rs = SMs[8], SMs[9]
            for eh in range(2):
                act(XNs[:, eh * 512:(eh + 1) * 512], BK[eh][0:R_, :], AF.Square, [BK[eh]], [XNs, ss],
                    accum=ss[:, eh:eh + 1])
            tt("dve", ss[:, 2:3], ss[:, 0:1], ss[:, 1:2], ALU.add, [ss], [ss])
            rsqrt_small(rs[:, 0:1], ss[:, 2:3], 1.0 / D, [ss], [rs], rs[:, 1:2])
            for eh in range(2):
                sl = slice(eh * 512, (eh + 1) * 512)
                stt("dve", T1[:], BK[eh][0:R_, :], rs[:, 0:1], GPM[0:R_, sl], ALU.mult, ALU.mult, [BK[eh], rs, GPM], [T1])
                tt("dve", XS[:, sl], XS[:, sl], T1[:], ALU.add, [XS, T1], [XS])
            for k in range(8):
                tr(BK[2][:, k * R_:(k + 1) * R_], XS[:, k * 128:(k + 1) * 128], IDF16, [XS, CONST], [BK[2]])
            cp("dve", XS1T[:].rearrange("p k r -> p (k r)"), BK[2][:, 0:8 * R_], [BK[2]], [XS1T])
            act(XNs[:], XS[:], AF.Square, [XS], [XNs, ss], accum=ss[:, 4:5])
            rsqrt_small(rs[:, 4:5], ss[:, 4:5], 1.0 / D, [ss], [rs], rs[:, 5:6])
            ts("dve", XNs[:], XS[:], rs[:, 4:5], None, ALU.mult, None, [XS, rs], [XNs])
            for k in range(8):
                tr(TPB[:, k * 128:k * 128 + R_], XNs[:, k * 128:(k + 1) * 128], IDB[0:R_, 0:R_], [XNs, IDB], [TPB])
            tt("dve", HNS[:], TPB[:].rearrange("p (k t) -> p k t", k=8)[:, :, 0:R_], bc3(GPREMLP[:], 8, R_), ALU.mult,
               [TPB, GPREMLP], [HNS])

            P.barrier(all_res_p1 + RX + [XS1T.r, HNS.r, IDF2.r, IDB.r, GPREMLP.r])
            p1b.close()

        with ExitStack() as p2:
            WUP = p2.enter_context(nc.sbuf_tensor("sb_WUP", [128, 8, DFF], BF16))
            WDN = p2.enter_context(nc.sbuf_tensor("sb_WDN", [128, 32, D], BF16))
            NWC = 8
            RWU = [Res("WUP%d" % i) for i in range(NWC)]
            RWD = [Res("WDN%d" % i) for i in range(NWC)]
            wup_v = wup_d.rearrange("(k p) c -> p k c", p=128)
            wdn_v = wdn_d.rearrange("(k p) c -> p k c", p=128)
            for i in range(NWC if KPH2 else 0):
                dma("pool", WUP[:, :, i * 512:(i + 1) * 512], wup_v[:, :, i * 512:(i + 1) * 512], [], [RWU[i]])
                dma("pool", WDN[:, i * 4:(i + 1) * 4, :], wdn_v[:, i * 4:(i + 1) * 4, :], [], [RWD[i]])
            XN2 = sb(p2, "XN2", [128, D], BF16)
            GPL = sb(p2, "GPL", [128, D])
            dma("sp", GPL[:], gpl_d.partition_broadcast(128), [], [GPL])
            HN = sb(p2, "HN", [128, 8, 256], BF16)
            UT = [sb(p2, "UT%d" % i, [128, 256], BF16) for i in range(2)]
            RL = [sb(p2, "RL%d" % i, [128, 256]) for i in range(2)]
            SS2 = sb(p2, "SS2", [128, 16])

            NB2 = NT // 2 if KPH2 else 0

            def xv_of(blk, j):
                return Xt[:, 2 * blk + j, :], RX[2 * blk + j]

            def prep(blk):
                for j in range(2):
                    xv, rx = xv_of(blk, j)
                    act(XN2[:], xv, AF.Square, [rx], [XN2, SS2], accum=SS2[:, 0:1])
                    rsqrt_small(SS2[:, 1:2], SS2[:, 0:1], 1.0 / D, [SS2], [SS2], SS2[:, 2:3])
                    ts("dve", XN2[:], xv, SS2[:, 1:2], None, ALU.mult, None, [rx, SS2], [XN2])
                    for k in range(8):
                        tr(TPB[:, k * 128:(k + 1) * 128], XN2[:, k * 128:(k + 1) * 128], IDB[:], [XN2, IDB], [TPB])
                    tt("dve", HN[:, :, j * 128:(j + 1) * 128], TPB[:].rearrange("p (k t) -> p k t", k=8),
                       bc3(GPREMLP[:], 8, 128), ALU.mult, [TPB, GPREMLP], [HN])

            def up(f, hn, n):
                bk = BK[f % 2]
                for k in range(8):
                    mm(bk[:, 0:n], WUP[:, k, f * 128:(f + 1) * 128], hn[:, k, 0:n], k == 0, k == 7,
                       [hn, RWU[f // 4]], [bk])
                rl, ut = RL[f % 2], UT[f % 2]
                act(rl[:, 0:n], bk[:, 0:n], AF.Relu, [bk], [rl])
                tt("dve", ut[:, 0:n], rl[:, 0:n], rl[:, 0:n], ALU.mult, [rl], [ut])

            def down(f, rows, ntl):
                ut = UT[f % 2]
                for j in range(ntl):
                    for eh in range(2):
                        ab = BK[2 + 2 * j + eh]
                        mm(ab[0:rows, :], ut[:, j * rows:(j + 1) * rows], WDN[:, f, eh * 512:(eh + 1) * 512],
                           f == 0, f == 31, [ut, RWD[f // 4]], [ab])

            def fin(blk):
                for j in range(2):
                    for eh in range(2):
                        ab = BK[2 + 2 * j + eh]
                        act(XN2[:, eh * 512:(eh + 1) * 512], ab[:], AF.Square, [ab], [XN2, SS2],
                            accum=SS2[:, 8 + 2 * j + eh:9 + 2 * j + eh])
                for j in range(2):
                    tt("dve", SS2[:, 12 + j:13 + j], SS2[:, 8 + 2 * j:9 + 2 * j], SS2[:, 9 + 2 * j:10 + 2 * j], ALU.add,
                       [SS2], [SS2])
                ts("dve", SS2[:, 14:16], SS2[:, 12:14], 1.0 / D, EPS, ALU.mult, ALU.add, [SS2], [SS2])
                act(SS2[:, 14:16], SS2[:, 14:16], AF.Ln, [SS2], [SS2])
                act(SS2[:, 14:16], SS2[:, 14:16], AF.Exp, [SS2], [SS2], scale=-0.5)
                for j in range(2):
                    xv, rx = xv_of(blk, j)
                    for eh in range(2):
                        ab = BK[2 + 2 * j + eh]
                        sl = slice(eh * 512, (eh + 1) * 512)
                        stt("dve", ab[:], ab[:], SS2[:, 14 + j:15 + j], GPL[:, sl], ALU.mult, ALU.mult,
                            [ab, SS2, GPL], [ab])
                        tt("dve", xv[:, sl], xv[:, sl], ab[:], ALU.add, [rx, ab], [rx])
                    r0 = (2 * blk + j) * 128
                    dma("sp", y_d[r0:r0 + 128, :], xv, [rx], [])

            if NB2:
                prep(0)
            for blk in range(NB2):
                up(0, HN, 256)
                for f in range(32):
                    if f + 1 < 32:
                        up(f + 1, HN, 256)
                    elif blk + 1 < NB2:
                        prep(blk + 1)
                    down(f, 128, 2)
                fin(blk)

            R_ = NS
            if KPH2:
                up(0, HNS, R_)
                for f in range(32):
                    if f + 1 < 32:
                        up(f + 1, HNS, R_)
                    down(f, R_, 1)
            if KPH2:
                for eh in range(2):
                    act(XN2[0:R_, eh * 512:(eh + 1) * 512], BK[2 + eh][0:R_, :], AF.Square, [BK[2 + eh]], [XN2, SS2],
                        accum=SS2[0:R_, 3 + eh:4 + eh])
                tt("dve", SS2[0:R_, 5:6], SS2[0:R_, 3:4], SS2[0:R_, 4:5], ALU.add, [SS2], [SS2])
                rsqrt_small(SS2[0:R_, 6:7], SS2[0:R_, 5:6], 1.0 / D, [SS2], [SS2], SS2[0:R_, 7:8])
                YSB = XN2.t.bitcast(F32)
                for eh in range(2):
                    sl = slice(eh * 512, (eh + 1) * 512)
                    stt("dve", BK[2 + eh][0:R_, :], BK[2 + eh][0:R_, :], SS2[0:R_, 6:7], GPL[0:R_, sl], ALU.mult, ALU.mult,
                        [BK[2 + eh], SS2, GPL], [BK[2 + eh]])
                    for j in range(4):
                        tr(BK[4 + eh][0:R_, j * 128:(j + 1) * 128], XS1T[:, 4 * eh + j, :], IDF2[:], [XS1T, IDF2],
                           [BK[4 + eh]])
                    cp("act", YSB[0:R_, :], BK[4 + eh][0:R_, :], [BK[4 + eh]], [XN2])
                    tt("dve", YSB[0:R_, :], YSB[0:R_, :], BK[2 + eh][0:R_, :], ALU.add, [XN2, BK[2 + eh]], [XN2])
                    dma("sp", ys_d[:, sl], YSB[0:R_, :], [XN2], [])

        n_ins = P.finalize(top)
    return nc, n_ins


_CACHE = {}


def kernel(x_prompt, x_sample, state_gdn_conv, state_gdn_S, state_mlstm_C, state_mlstm_n, state_mlstm_m,
           norm_pre_mix, w_in, conv_w, a_log, dt_bias, gdn_norm_g, b_igate, b_fgate, mlstm_norm_g, w_out,
           norm_post_mix, norm_pre_mlp, w_up, w_down, norm_post_mlp):
    f = lambda a: np.ascontiguousarray(np.asarray(a, dtype=np.float32))
    if "nc" not in _CACHE:
        _CACHE["nc"] = build_program()
    nc, _ = _CACHE["nc"]
    consts = make_consts()
    small = np.concatenate([f(a_log)[0], f(dt_bias)[0], f(b_igate)[0], f(b_fgate)[0]])[None, :]
    shared = {
        "w_in": f(w_in)[0], "w_out": f(w_out)[0], "w_up": f(w_up)[0], "w_down": f(w_down)[0],
        "consts": consts,
        "gpre_fm": f(f(norm_pre_mix)[0].reshape(8, 128).T),
        "gpremlp_fm": f(f(norm_pre_mlp)[0].reshape(8, 128).T),
        "cw_fm": f(f(conv_w)[0].reshape(4, 12, 128).transpose(2, 0, 1).reshape(128, 48)),
        "gpostmix": f(norm_post_mix)[0][None, :], "gpostmlp": f(norm_post_mlp)[0][None, :],
        "small": f(small), "gdn_norm_g": f(gdn_norm_g)[0][None, :], "mlstm_norm_g": f(mlstm_norm_g)[0][None, :],
    }
    xp, xs = f(x_prompt), f(x_sample)
    in_maps = []
    for c in range(NCORES):
        r = slice(c * NS, (c + 1) * NS)
        m = dict(shared)
        m.update({
            "x": xp[c], "xs": xs[r, 0, :],
            "sconv": f(state_gdn_conv)[0, r], "sS": f(state_gdn_S)[0, r], "sC": f(state_mlstm_C)[0, r],
            "sn": f(state_mlstm_n)[0, r].reshape(NS, 256), "sm": f(state_mlstm_m)[0, r],
        })
        in_maps.append(m)
    res = run_bass_kernel_spmd(nc, in_maps, core_ids=list(range(NCORES)))
    R = res.results
    g = lambda k: np.stack([np.asarray(R[c][k], dtype=np.float32) for c in range(NCORES)])
    gc = lambda k: np.concatenate([np.asarray(R[c][k], dtype=np.float32) for c in range(NCORES)], axis=0)
    y_prompt = g("y")
    y_sample = gc("ys")[:, None, :]
    p_conv = g("pconv")[None]
    p_S = g("pS")[None]
    p_C = g("pC")[None]
    p_n = g("pn")[None]
    p_m = g("pm").reshape(NCORES, 4)[None]
    s_conv = gc("oconv")[None]
    s_S = gc("oS")[None]
    s_C = gc("oC")[None]
    s_n = gc("on").reshape(NCORES * NS, 4, 64)[None]
    s_m = gc("om")[None]
    return (y_prompt, y_sample, p_conv, p_S, p_C, p_n, p_m, s_conv, s_S, s_C, s_n, s_m)
```
